# Optimizing a Trainium2 kernel written in Bass

```python
import math
import jax
import jax.numpy as jnp
from jax import lax
import numpy as np

D_MODEL = 1024
BATCH = 8
SEQ = 2048
DEPTH = 4

GRID_W = 64
CTX_LEN = 256

MIX_W = D_MODEL
S5_W = D_MODEL // 4
S5_GH = 16
S5_G = S5_W // S5_GH
S5_P = 64
HEAD_DIM = 64
ATT_W = D_MODEL // 2
N_Q = ATT_W // HEAD_DIM
GQA_REP = 4
N_KV = N_Q // GQA_REP
ATT_KV = N_KV * HEAD_DIM
ROPE_PAIRS = HEAD_DIM // 4
ROPE_THETA = 10000.0
Q_BLOCK = 128
RW_W = D_MODEL // 4
RW_HD = 64
RW_H = RW_W // RW_HD
LORA_W = 32
LORA_A = 32
LORA_G = 64
RW_IN = 3 * RW_W + 2 * LORA_W + 2 * LORA_A + LORA_G
RW_LN_EPS = 64e-5
N_IN = S5_W + ATT_W + 2 * ATT_KV + RW_IN
IN_SPLITS = (S5_W, S5_W + ATT_W, S5_W + ATT_W + ATT_KV, S5_W + ATT_W + 2 * ATT_KV)
RW_SPLITS = (RW_W, 2 * RW_W, 3 * RW_W, 3 * RW_W + LORA_W, 3 * RW_W + 2 * LORA_W,
             3 * RW_W + 2 * LORA_W + LORA_A, 3 * RW_W + 2 * LORA_W + 2 * LORA_A)
D_FF = 2816
CONV_W = 3
N_MOD = 6
RMS_EPS = 1e-6

kernel_name = "hymba_s5_gqa_rwkv7_convffn_prefix_dit"

F32 = jnp.float32


def rms_norm(x, g):
    xf = x.astype(F32)
    y = xf * lax.rsqrt(jnp.mean(xf * xf, axis=-1, keepdims=True) + RMS_EPS)
    return y.astype(x.dtype) * g


def modulate(h, shift, scale):
    return h * (1.0 + scale) + shift


def _lin_rec_combine(left, right):
    a_l, b_l = left
    a_r, b_r = right
    return a_r * a_l, a_r * b_l + b_r


def s5_discretize(a_re, a_im, log_step, b_re, b_im):
    lam = lax.complex(a_re.astype(F32), a_im.astype(F32))
    step = jnp.exp(log_step.astype(F32))[:, None]
    lam_bar = jnp.exp(lam * step)
    b = lax.complex(b_re.astype(F32), b_im.astype(F32))
    b_bar = ((lam_bar - 1.0) / lam)[..., None] * b
    return lam_bar, b_bar


def s5_scan(u, lam_bar, b_bar, h0, reverse):
    bu = jnp.einsum('gph,blgh->blgp', b_bar, u.astype(jnp.complex64))
    if h0 is not None:
        first = -1 if reverse else 0
        bu = bu.at[:, first].add(lam_bar * h0)
    lam = jnp.broadcast_to(lam_bar, bu.shape)
    _, states = lax.associative_scan(_lin_rec_combine, (lam, bu), reverse=reverse, axis=1)
    return states


def s5_readout(c_mat, states):
    return jnp.real(jnp.einsum('ghp,blgp->blgh', c_mat, states))


def s5_glu(y, glu_w, glu_b, out_g):
    a = jax.nn.gelu(y)
    return rms_norm(a * jax.nn.sigmoid(a @ glu_w + glu_b), out_g)


def s5_mixer(u_lat, u_ctx, a_re, a_im, log_step, b_re, b_im, c_re, c_im, d, glu_w, glu_b,
             out_g, need_ctx_out):
    B, L, _ = u_lat.shape
    ul = u_lat.astype(F32).reshape(B, L, S5_G, S5_GH)
    uc = u_ctx.astype(F32).reshape(B, u_ctx.shape[1], S5_G, S5_GH)
    dd = d.astype(F32).reshape(S5_G, S5_GH)
    y_lat = ul * dd
    y_ctx = uc * dd if need_ctx_out else None
    for di, reverse in enumerate((False, True)):
        lam_bar, b_bar = s5_discretize(a_re[di], a_im[di], log_step[di], b_re[di], b_im[di])
        c_mat = lax.complex(c_re[di].astype(F32), c_im[di].astype(F32))
        st_c = s5_scan(uc, lam_bar, b_bar, None, reverse)
        h0 = st_c[:, 0] if reverse else st_c[:, -1]
        st_l = s5_scan(ul, lam_bar, b_bar, h0, reverse)
        y_lat = y_lat + s5_readout(c_mat, st_l)
        if need_ctx_out:
            y_ctx = y_ctx + s5_readout(c_mat, st_c)
    out_lat = s5_glu(y_lat.reshape(B, L, S5_W).astype(u_lat.dtype), glu_w, glu_b, out_g)
    out_ctx = None
    if need_ctx_out:
        out_ctx = s5_glu(y_ctx.reshape(B, u_ctx.shape[1], S5_W).astype(u_ctx.dtype),
                         glu_w, glu_b, out_g)
    return out_lat, out_ctx


def axial_rope(t, pos_row, pos_col):
    B, L, H, _ = t.shape
    inv = ROPE_THETA ** (-jnp.arange(ROPE_PAIRS, dtype=F32) / ROPE_PAIRS)
    ang = jnp.stack([pos_row[:, None] * inv, pos_col[:, None] * inv], axis=1)
    cos = jnp.cos(ang)[None, :, None]
    sin = jnp.sin(ang)[None, :, None]
    tf = t.astype(F32).reshape(B, L, H, 2, 2, ROPE_PAIRS)
    t1, t2 = tf[..., 0, :], tf[..., 1, :]
    out = jnp.stack([t1 * cos - t2 * sin, t2 * cos + t1 * sin], axis=-2)
    return out.reshape(B, L, H, HEAD_DIM).astype(t.dtype)


def block_attention(q, k, v):
    B, Lq = q.shape[0], q.shape[1]
    nb = Lq // Q_BLOCK
    qb = q.reshape(B, nb, Q_BLOCK, N_KV, GQA_REP, HEAD_DIM).transpose(1, 0, 2, 3, 4, 5)
    scale = HEAD_DIM ** -0.5

    def one_block(qi):
        s = jnp.einsum('bqgrd,bkgd->bgrqk', qi, k).astype(F32) * scale
        p = jax.nn.softmax(s, axis=-1).astype(v.dtype)
        return jnp.einsum('bgrqk,bkgd->bqgrd', p, v)

    o = lax.map(one_block, qb)
    return o.transpose(1, 0, 2, 3, 4, 5).reshape(B, Lq, N_Q * HEAD_DIM)


def attn_mixer(q_lat, k_lat, v_lat, q_ctx, k_ctx, v_ctx, qn_g, kn_g, out_g, pos_row, pos_col,
               need_ctx_out):
    def heads(t, n):
        return t.reshape(t.shape[0], t.shape[1], n, HEAD_DIM)
    ql = axial_rope(rms_norm(heads(q_lat, N_Q), qn_g), pos_row, pos_col)
    kl = axial_rope(rms_norm(heads(k_lat, N_KV), kn_g), pos_row, pos_col)
    kc = rms_norm(heads(k_ctx, N_KV), kn_g)
    vc = heads(v_ctx, N_KV)
    k_all = jnp.concatenate([kc, kl], axis=1)
    v_all = jnp.concatenate([vc, heads(v_lat, N_KV)], axis=1)
    out_lat = rms_norm(block_attention(ql, k_all, v_all), out_g)
    out_ctx = None
    if need_ctx_out:
        qc = rms_norm(heads(q_ctx, N_Q), qn_g)
        out_ctx = rms_norm(block_attention(qc, kc, vc), out_g)
    return out_lat, out_ctx


def centred_token_shift(z, mu):
    zp = jnp.pad(z, ((0, 0), (1, 1), (0, 0)))
    return z + mu * (0.5 * (zp[:, :-2] + zp[:, 2:]) - z)


def rwkv_inputs(z, w0, w2, a0, a2, g2, k_k, k_a):
    zf = z.astype(F32)
    B, L, _ = zf.shape
    r, k, v, wl_f, wl_b, al_f, al_b, gl = jnp.split(zf, RW_SPLITS, axis=-1)

    def hd(t):
        return t.reshape(B, L, RW_H, RW_HD)
    kk = hd(k * k_k)
    kk = kk / jnp.maximum(jnp.linalg.norm(kk, axis=-1, keepdims=True), 1e-12)
    dirs = []
    for di, (wl, al) in enumerate(((wl_f, al_f), (wl_b, al_b))):
        w_log = -jax.nn.softplus(-(w0[di] + jnp.tanh(wl) @ w2[di])) - 0.5
        decay = jnp.exp(-jnp.exp(w_log))
        iclr = jax.nn.sigmoid(a0[di] + al @ a2[di])
        k_dir = k * (1.0 + (iclr - 1.0) * k_a)
        dirs.append((hd(decay), hd(k_dir), hd(iclr)))
    g = jax.nn.sigmoid(gl) @ g2
    return hd(r), hd(v), kk, g, dirs


def wkv7_scan(r, decay, k, v, alpha, beta, s0, reverse):
    seq = tuple(jnp.swapaxes(t, 0, 1) for t in (r, decay, k, v, alpha, beta))

    def step(state, inp):
        r_t, w_t, k_t, v_t, a_t, b_t = inp
        sa = jnp.einsum('bhvk,bhk->bhv', state, a_t)
        state = (state * w_t[:, :, None, :] + sa[..., None] * b_t[:, :, None, :]
                 + v_t[..., None] * k_t[:, :, None, :])
        return state, jnp.einsum('bhvk,bhk->bhv', state, r_t)

    s_final, ys = lax.scan(step, s0, seq, reverse=reverse)
    return jnp.swapaxes(ys, 0, 1), s_final


def current_token_bonus(r, k, v, r_k):
    return jnp.sum(r * k * r_k, axis=-1, keepdims=True) * v


def rwkv_output(wkv, bonus, g, ln_g, ln_b, dtype):
    B, L = wkv.shape[0], wkv.shape[1]
    mean = jnp.mean(wkv, axis=-1, keepdims=True)
    var = jnp.mean(jnp.square(wkv - mean), axis=-1, keepdims=True)
    y = ((wkv - mean) * lax.rsqrt(var + RW_LN_EPS)).reshape(B, L, RW_W) * ln_g + ln_b
    y = y + bonus.reshape(B, L, RW_W)
    return (y * g).astype(dtype)


def rwkv_mixer(z_lat, z_ctx, mu, w0, w2, a0, a2, g2, k_k, k_a, r_k, ln_g, ln_b, need_ctx_out):
    r_l, v_l, kk_l, g_l, dirs_l = rwkv_inputs(centred_token_shift(z_lat, mu), w0, w2, a0, a2, g2, k_k, k_a)
    r_c, v_c, kk_c, g_c, dirs_c = rwkv_inputs(centred_token_shift(z_ctx, mu), w0, w2, a0, a2, g2, k_k, k_a)
    r_k = r_k.astype(F32)
    s0 = jnp.zeros((z_lat.shape[0], RW_H, RW_HD, RW_HD), F32)
    wkv_l, bon_l, wkv_c, bon_c = [], [], [], []
    for di, reverse in enumerate((False, True)):
        dec_c, kd_c, ic_c = dirs_c[di]
        y_c, s_c = wkv7_scan(r_c, dec_c, kd_c, v_c, -kk_c, kk_c * ic_c, s0, reverse)
        dec_l, kd_l, ic_l = dirs_l[di]
        y_l, _ = wkv7_scan(r_l, dec_l, kd_l, v_l, -kk_l, kk_l * ic_l, s_c, reverse)
        wkv_l.append(y_l)
        bon_l.append(current_token_bonus(r_l, kd_l, v_l, r_k))
        if need_ctx_out:
            wkv_c.append(y_c)
            bon_c.append(current_token_bonus(r_c, kd_c, v_c, r_k))
    out_lat = rwkv_output(wkv_l[0] + wkv_l[1], bon_l[0] + bon_l[1], g_l, ln_g, ln_b, z_lat.dtype)
    out_ctx = None
    if need_ctx_out:
        out_ctx = rwkv_output(wkv_c[0] + wkv_c[1], bon_c[0] + bon_c[1], g_c, ln_g, ln_b, z_ctx.dtype)
    return out_lat, out_ctx


def token_mixer(h_lat, h_ctx, w_in, w_out, s5_p, att_p, rw_p, pos_row, pos_col, need_ctx_out):
    zl = jnp.split(h_lat @ w_in, IN_SPLITS, axis=-1)
    zc = jnp.split(h_ctx @ w_in, IN_SPLITS, axis=-1)
    s5_l, s5_c = s5_mixer(zl[0], zc[0], *s5_p, need_ctx_out)
    at_l, at_c = attn_mixer(zl[1], zl[2], zl[3], zc[1], zc[2], zc[3], *att_p, pos_row, pos_col,
                            need_ctx_out)
    rw_l, rw_c = rwkv_mixer(zl[4], zc[4], *rw_p, need_ctx_out)
    o_lat = jnp.concatenate([s5_l, at_l, rw_l], axis=-1) @ w_out
    o_ctx = None
    if need_ctx_out:
        o_ctx = jnp.concatenate([s5_c, at_c, rw_c], axis=-1) @ w_out
    return o_lat, o_ctx


def centred_dwconv(x, w, b):
    C = x.shape[-1]
    y = lax.conv_general_dilated(x, w[:, None, :].astype(x.dtype), window_strides=(1,),
                                 padding=((CONV_W // 2, CONV_W // 2),),
                                 dimension_numbers=('NWC', 'WIO', 'NWC'),
                                 feature_group_count=C)
    return y + b


def conv_ffn(h, up, conv_w, conv_b, down):
    u = centred_dwconv(h @ up, conv_w, conv_b)
    gate, val = jnp.split(u, 2, axis=-1)
    return (jax.nn.silu(gate) * val) @ down


def setup_inputs(seed: int = 0) -> dict:
    key = jax.random.key(seed)
    ks = iter(jax.random.split(key, 48))
    L = DEPTH

    def nrm(shape, scale):
        return jax.random.normal(next(ks), shape, F32) * scale

    def unif(shape, lo, hi):
        return jax.random.uniform(next(ks), shape, F32, lo, hi)

    def gain(shape):
        return 1.0 + nrm(shape, 0.02)

    return {
        "x": nrm((BATCH, SEQ, D_MODEL), 1.0),
        "c": nrm((BATCH, D_MODEL), 1.0),
        "ctx": nrm((BATCH, CTX_LEN, D_MODEL), 1.0),
        "c_ctx": nrm((D_MODEL,), 1.0),
        "norm1_g": gain((L, D_MODEL)),
        "norm2_g": gain((L, D_MODEL)),
        "mod_w": nrm((L, D_MODEL, N_MOD * D_MODEL), 0.5 * D_MODEL ** -0.5),
        "mod_b": nrm((L, N_MOD * D_MODEL), 0.02),
        "w_in": nrm((L, D_MODEL, N_IN), D_MODEL ** -0.5),
        "w_out": nrm((L, MIX_W, D_MODEL), MIX_W ** -0.5),
        "s5_a_re": -0.5 + nrm((L, 2, S5_G, S5_P), 0.01),
        "s5_a_im": math.pi * jnp.arange(S5_P, dtype=F32) + nrm((L, 2, S5_G, S5_P), 0.01),
        "s5_log_step": unif((L, 2, S5_G), math.log(1e-3), math.log(1e-1)),
        "s5_b_re": nrm((L, 2, S5_G, S5_P, S5_GH), (2 * S5_GH) ** -0.5),
        "s5_b_im": nrm((L, 2, S5_G, S5_P, S5_GH), (2 * S5_GH) ** -0.5),
        "s5_c_re": nrm((L, 2, S5_G, S5_GH, S5_P), (2 * S5_P) ** -0.5),
        "s5_c_im": nrm((L, 2, S5_G, S5_GH, S5_P), (2 * S5_P) ** -0.5),
        "s5_d": nrm((L, S5_W), 1.0),
        "s5_glu_w": nrm((L, S5_W, S5_W), S5_W ** -0.5),
        "s5_glu_b": nrm((L, S5_W), 0.02),
        "s5_out_g": gain((L, S5_W)),
        "att_qn_g": gain((L, HEAD_DIM)),
        "att_kn_g": gain((L, HEAD_DIM)),
        "att_out_g": gain((L, ATT_W)),
        "rw_mu": unif((L, RW_IN), 0.0, 1.0),
        "rw_w0": unif((L, 2, RW_W), -5.0, -0.5),
        "rw_w2": nrm((L, 2, LORA_W, RW_W), 0.5 * LORA_W ** -0.5),
        "rw_a0": nrm((L, 2, RW_W), 0.1),
        "rw_a2": nrm((L, 2, LORA_A, RW_W), 0.5 * LORA_A ** -0.5),
        "rw_g2": nrm((L, LORA_G, RW_W), LORA_G ** -0.5),
        "rw_k_k": 0.85 + nrm((L, RW_W), 0.02),
        "rw_k_a": 1.0 + nrm((L, RW_W), 0.02),
        "rw_r_k": nrm((L, RW_H, RW_HD), 0.1),
        "rw_ln_g": gain((L, RW_W)),
        "rw_ln_b": nrm((L, RW_W), 0.02),
        "ffn_up": nrm((L, D_MODEL, 2 * D_FF), D_MODEL ** -0.5),
        "ffn_conv_w": nrm((L, CONV_W, 2 * D_FF), CONV_W ** -0.5),
        "ffn_conv_b": nrm((L, 2 * D_FF), 0.02),
        "ffn_down": nrm((L, D_FF, D_MODEL), D_FF ** -0.5),
        "final_g": gain((D_MODEL,)),
    }


def reference(x, c, ctx, c_ctx, norm1_g, norm2_g, mod_w, mod_b, w_in, w_out,
              s5_a_re, s5_a_im, s5_log_step, s5_b_re, s5_b_im, s5_c_re, s5_c_im, s5_d,
              s5_glu_w, s5_glu_b, s5_out_g, att_qn_g, att_kn_g, att_out_g,
              rw_mu, rw_w0, rw_w2, rw_a0, rw_a2, rw_g2, rw_k_k, rw_k_a, rw_r_k, rw_ln_g, rw_ln_b,
              ffn_up, ffn_conv_w, ffn_conv_b, ffn_down, final_g):
    n_tok = x.shape[1]
    rows = n_tok // GRID_W
    tok = jnp.arange(rows * GRID_W, dtype=jnp.int32)
    pos_row = (tok // GRID_W).astype(F32)
    pos_col = (tok % GRID_W).astype(F32)
    silu_c = jax.nn.silu(c)
    silu_cc = jax.nn.silu(c_ctx)
    xl, xc = x, ctx
    for l in range(DEPTH):
        ctx_continues = l < DEPTH - 1
        sh1, sc1, gt1, sh2, sc2, gt2 = jnp.split((silu_c @ mod_w[l] + mod_b[l])[:, None, :], N_MOD, axis=-1)
        csh1, csc1, cgt1, csh2, csc2, cgt2 = jnp.split(silu_cc @ mod_w[l] + mod_b[l], N_MOD, axis=-1)
        s5_p = (s5_a_re[l], s5_a_im[l], s5_log_step[l], s5_b_re[l], s5_b_im[l], s5_c_re[l],
                s5_c_im[l], s5_d[l], s5_glu_w[l], s5_glu_b[l], s5_out_g[l])
        att_p = (att_qn_g[l], att_kn_g[l], att_out_g[l])
        rw_p = (rw_mu[l], rw_w0[l], rw_w2[l], rw_a0[l], rw_a2[l], rw_g2[l], rw_k_k[l], rw_k_a[l],
                rw_r_k[l], rw_ln_g[l], rw_ln_b[l])
        hl = modulate(rms_norm(xl, norm1_g[l]), sh1, sc1)
        hc = modulate(rms_norm(xc, norm1_g[l]), csh1, csc1)
        ol, oc = token_mixer(hl, hc, w_in[l], w_out[l], s5_p, att_p, rw_p, pos_row, pos_col,
                             ctx_continues)
        xl = xl + gt1 * ol
        hl = modulate(rms_norm(xl, norm2_g[l]), sh2, sc2)
        xl = xl + gt2 * conv_ffn(hl, ffn_up[l], ffn_conv_w[l], ffn_conv_b[l], ffn_down[l])
        if ctx_continues:
            xc = xc + cgt1 * oc
            hc = modulate(rms_norm(xc, norm2_g[l]), csh2, csc2)
            xc = xc + cgt2 * conv_ffn(hc, ffn_up[l], ffn_conv_w[l], ffn_conv_b[l], ffn_down[l])
    return rms_norm(xl, final_g)
```

```python
import math
import numpy as np
from contextlib import ExitStack
import concourse.bass as bass
import concourse.mybir as mybir
from concourse.bass_utils import run_bass_kernel_spmd

F32 = mybir.dt.float32
BF16 = mybir.dt.bfloat16
I32 = mybir.dt.int32
ALU = mybir.AluOpType
AF = mybir.ActivationFunctionType
AX = mybir.AxisListType

T = 2304
TE = 2560
NCTX = 256
NLAT = 2048
DM = 1024
SEGS = [(0, 256, 1), (256, 768, 0), (768, 1280, 0), (1280, 1792, 0), (1792, 2304, 0)]
RMS_EPS = 1e-6


class Tok:
    __slots__ = ("w", "r")

    def __init__(self):
        self.w = None
        self.r = {}


class FW:
    ENG = ("pe", "dve", "act", "pool", "sp")
    NDMA = 8

    def __init__(self, nc, es):
        self.nc = nc
        self.es = es
        self.eng = {"pe": nc.tensor, "dve": nc.vector, "act": nc.scalar,
                    "pool": nc.gpsimd, "sp": nc.sync}
        self.sem = {}
        self.cnt = {}
        for e in self.ENG:
            self.sem[e] = es.enter_context(nc.semaphore("s_" + e))
            self.cnt[e] = 0
        self.dq = {}
        for q in ("sp", "pool", "act"):
            ring = []
            for i in range(self.NDMA):
                k = "d_%s_%d" % (q, i)
                self.sem[k] = es.enter_context(nc.semaphore(k))
                self.cnt[k] = 0
                ring.append(k)
            self.dq[q] = [ring, 0]
        self.seen = {e: {} for e in self.ENG}
        self.ninst = 0
        self.uid = 0

    def name(self, p):
        self.uid += 1
        return "%s_%d" % (p, self.uid)

    def _deps(self, reads, writes):
        deps = {}
        for t in reads:
            if t.w is not None and deps.get(t.w[0], 0) < t.w[1]:
                deps[t.w[0]] = t.w[1]
        for t in writes:
            if t.w is not None and deps.get(t.w[0], 0) < t.w[1]:
                deps[t.w[0]] = t.w[1]
            for k, v in t.r.items():
                if deps.get(k, 0) < v:
                    deps[k] = v
        return deps

    def _wait(self, e, deps):
        seen = self.seen[e]
        for k, v in deps.items():
            if seen.get(k, 0) < v:
                self.eng[e].wait_ge(self.sem[k], v)
                seen[k] = v

    def op(self, e, fn, reads=(), writes=()):
        self._wait(e, self._deps(reads, writes))
        inst = fn(self.eng[e])
        self.cnt[e] += 1
        inst.then_inc(self.sem[e], 1)
        v = self.cnt[e]
        for t in reads:
            t.r[e] = v
        for t in writes:
            t.w = (e, v)
            t.r = {}
        self.ninst += 1
        return inst

    def dma(self, q, out, in_, reads=(), writes=(), **kw):
        ring, idx = self.dq[q]
        k = ring[idx % len(ring)]
        self.dq[q][1] = idx + 1
        deps = self._deps(reads, writes)
        if self.cnt[k] > 0:
            deps[k] = max(deps.get(k, 0), self.cnt[k])
        self._wait(q, deps)
        inst = self.eng[q].dma_start(out=out, in_=in_, **kw)
        self.cnt[k] += 16
        inst.then_inc(self.sem[k], 16)
        v = self.cnt[k]
        for t in reads:
            t.r[k] = v
        for t in writes:
            t.w = (k, v)
            t.r = {}
        self.ninst += 1
        return inst

    def barrier(self, engines=None):
        allv = {k: v for k, v in self.cnt.items() if v > 0}
        for e in (engines or self.ENG):
            self._wait(e, allv)


class Pool_:
    def __init__(self, fw, es):
        self.fw = fw
        self.es = es
        self.nc = fw.nc

    def sb(self, shape, dt, name="t"):
        return self.es.enter_context(self.nc.sbuf_tensor(self.fw.name(name), list(shape), dt))

    def ps(self, shape=(128, 512), dt=F32, name="ps"):
        return self.es.enter_context(self.nc.psum_tensor(self.fw.name(name), list(shape), dt))


class Rot:
    def __init__(self, bufs):
        self.bufs = bufs
        self.toks = [Tok() for _ in bufs]
        self.i = 0

    def next(self):
        j = self.i % len(self.bufs)
        self.i += 1
        return self.bufs[j], self.toks[j]


def load_w_bf16(fw, P, W, rows, cols, tok, name="w", q="pool"):
    kt = rows // 128
    wb = P.sb([128, kt, cols], BF16, name)
    for k in range(kt):
        fw.dma(q, wb[:, k, :], W[k * 128:(k + 1) * 128, :], writes=[tok])
    return wb


def stage_p0(fw, io):
    nc = fw.nc
    xb = io("x_b", [NLAT, DM], F32, "in")
    cb = io("ctx_b", [NCTX, DM], F32, "in")
    ident = io("ident", [128, 128], F32, "in")
    XT = io("XT", [DM, T], F32, "out")
    with ExitStack() as es:
        P = Pool_(fw, es)
        idt = P.sb([128, 128], F32)
        tid = Tok()
        fw.dma("sp", idt[:], ident, writes=[tid])
        xt = P.sb([128, 8, T], F32)
        txt = Tok()
        xin = Rot([P.sb([128, DM], F32) for _ in range(3)])
        pss = Rot([P.ps() for _ in range(4)])
        for tt in range(18):
            src = cb[tt * 128:(tt + 1) * 128, :] if tt < 2 else xb[(tt - 2) * 128:(tt - 1) * 128, :]
            xi, txi = xin.next()
            fw.dma("sp", xi[:], src, writes=[txi])
            for half in range(2):
                ps, tps = pss.next()
                for k in range(4):
                    kk = half * 4 + k
                    fw.op("pe", lambda e, ps=ps, k=k, kk=kk, xi=xi: e.transpose(
                        ps[:, k * 128:(k + 1) * 128], xi[:, kk * 128:(kk + 1) * 128], idt[:]),
                        reads=[txi, tid], writes=[tps])
                eng = "dve" if half == 0 else "act"
                outap = xt[:, half * 4:half * 4 + 4, tt * 128:(tt + 1) * 128]
                inap = ps[:].rearrange("p (k t) -> p k t", k=4)
                if eng == "dve":
                    fw.op("dve", lambda e, o=outap, i=inap: e.tensor_copy(out=o, in_=i), reads=[tps], writes=[txt])
                else:
                    fw.op("act", lambda e, o=outap, i=inap: e.copy(out=o, in_=i), reads=[tps], writes=[txt])
        tout = Tok()
        for k in range(8):
            fw.dma("sp", XT[k * 128:(k + 1) * 128, :], xt[:, k, :], reads=[txt], writes=[tout])
        fw.barrier()


def make_AB(fw, P, MODs, tmod, g_ap, sh_base, sc_base):
    g = P.sb([128, 8], F32)
    tg = Tok()
    fw.dma("sp", g[:], g_ap.rearrange("(k p) -> p k", p=128), writes=[tg], allow_slow_non_contiguous=True)
    AB = P.sb([128, 2, 2, 8], F32)
    tab = Tok()
    for ic in range(2):
        fw.op("dve", lambda e, ic=ic: e.tensor_scalar(out=AB[:, ic, 0, :], in0=MODs[:, sc_base:sc_base + 8, ic],
                                                      scalar1=1.0, scalar2=None, op0=ALU.add),
              reads=[tmod], writes=[tab])
        fw.op("dve", lambda e, ic=ic: e.tensor_tensor(out=AB[:, ic, 0, :], in0=AB[:, ic, 0, :], in1=g[:], op=ALU.mult),
              reads=[tg, tab], writes=[tab])
        fw.op("dve", lambda e, ic=ic: e.tensor_copy(out=AB[:, ic, 1, :], in_=MODs[:, sh_base:sh_base + 8, ic]),
              reads=[tmod], writes=[tab])
    return AB, tab


def norm_mod_seg(fw, P, st, xs, txs, n, ic, AB, tab, outs, touts):
    sq, ones, tones, psr, rs, tmpr = st["sq"], st["ones"], st["tones"], st["psr"], st["rs"], st["tmpr"]
    tsq, trs = st["tsq"], st["trs"]
    fw.op("act", lambda e: e.activation(out=sq[:, :, :n], in_=xs[:, :, :n], func=AF.Square), reads=[txs], writes=[tsq])
    ps, tps = psr.next()
    for k in range(8):
        fw.op("pe", lambda e, k=k: e.matmul(ps[:, :n], lhsT=ones[:], rhs=sq[:, k, :n], start=(k == 0), stop=(k == 7)),
              reads=[tsq, tones], writes=[tps])
    fw.op("act", lambda e: e.activation(out=rs[:, :n], in_=ps[:, :n], func=AF.Ln, scale=1.0 / DM, bias=st["eps"][:, 0:1]),
          reads=[tps, st["teps"]], writes=[trs])
    fw.op("act", lambda e: e.activation(out=rs[:, :n], in_=rs[:, :n], func=AF.Exp, scale=-0.5), reads=[trs], writes=[trs])
    for k in range(8):
        tmp, ttmp = tmpr.next()
        fw.op("dve", lambda e, k=k, tmp=tmp: e.tensor_tensor(out=tmp[:, :n], in0=xs[:, k, :n], in1=rs[:, :n], op=ALU.mult),
              reads=[txs, trs], writes=[ttmp])
        for o in outs(k):
            fw.op("act", lambda e, k=k, tmp=tmp, o=o: e.activation(out=o, in_=tmp[:, :n], func=AF.Identity,
                                                                   scale=AB[:, ic, 0, k:k + 1], bias=AB[:, ic, 1, k:k + 1]),
                  reads=[ttmp, tab], writes=touts)


def norm_state(fw, P):
    st = {}
    st["sq"] = P.sb([128, 8, 512], BF16)
    st["tsq"] = Tok()
    st["ones"] = P.sb([128, 128], BF16)
    st["tones"] = Tok()
    fw.op("pool", lambda e: e.memset(st["ones"][:], 1.0), writes=[st["tones"]])
    st["eps"] = P.sb([128, 1], F32)
    st["teps"] = Tok()
    fw.op("pool", lambda e: e.memset(st["eps"][:], RMS_EPS), writes=[st["teps"]])
    st["psr"] = Rot([P.ps() for _ in range(2)])
    st["rs"] = P.sb([128, 512], F32)
    st["trs"] = Tok()
    st["tmpr"] = Rot([P.sb([128, 512], F32) for _ in range(2)])
    return st


def stage_p1(fw, io):
    XT = io("XT", [DM, T], F32, "in")
    c_b = io("c_b", [DM], F32, "in")
    c_ctx = io("c_ctx", [DM], F32, "in")
    mod_w = io("mod_w", [DM, 6 * DM], F32, "in")
    mod_b = io("mod_b", [6 * DM], F32, "in")
    n1g = io("norm1_g", [DM], F32, "in")
    w_in = io("w_in", [DM, 1984], F32, "in")
    MOD = io("MOD", [128, 48, 2], F32, "out")
    ZT = io("ZT", [2048, TE], F32, "out")
    VT = io("VT", [T, 128], F32, "out")
    XTv = XT.rearrange("(k p) t -> p k t", p=128)
    with ExitStack() as es:
        P = Pool_(fw, es)
        tmw = Tok()
        cc = P.sb([128, 8, 2], F32)
        tcc = Tok()
        fw.dma("sp", cc[:, :, 0], c_b.rearrange("(k p) -> p k", p=128), writes=[tcc], allow_slow_non_contiguous=True)
        fw.dma("sp", cc[:, :, 1], c_ctx.rearrange("(k p) -> p k", p=128), writes=[tcc], allow_slow_non_contiguous=True)
        scb = P.sb([128, 8, 2], BF16)
        tscb = Tok()
        fw.op("act", lambda e: e.activation(out=scb[:], in_=cc[:], func=AF.Silu), reads=[tcc], writes=[tscb])
        mb = P.sb([128, 48], F32)
        tmb = Tok()
        fw.dma("sp", mb[:], mod_b.rearrange("(j p) -> p j", p=128), writes=[tmb], allow_slow_non_contiguous=True)
        MODs = P.sb([128, 48, 2], F32)
        tmod = Tok()
        with ExitStack() as es2:
            P2 = Pool_(fw, es2)
            mwb = load_w_bf16(fw, P2, mod_w, DM, 6 * DM, tmw, "modw")
            psm = P2.ps([128, 512])
            tpsm = Tok()
            for j in range(48):
                for k in range(8):
                    fw.op("pe", lambda e, j=j, k=k: e.matmul(psm[:, 2 * j:2 * j + 2], lhsT=mwb[:, k, j * 128:(j + 1) * 128],
                                                             rhs=scb[:, k, :], start=(k == 0), stop=(k == 7)),
                          reads=[tmw, tscb], writes=[tpsm])
            for ic in range(2):
                fw.op("dve", lambda e, ic=ic: e.tensor_tensor(
                    out=MODs[:, :, ic], in0=psm[:, 0:96].rearrange("p (j c) -> p j c", c=2)[:, :, ic], in1=mb[:], op=ALU.add),
                    reads=[tpsm, tmb], writes=[tmod])
            fw.barrier()
        tmo = Tok()
        fw.dma("sp", MOD, MODs[:], reads=[tmod], writes=[tmo])
        AB, tab = make_AB(fw, P, MODs, tmod, n1g, 0, 8)
        tw = Tok()
        wb = load_w_bf16(fw, P, w_in, DM, 1984, tw, "win")
        hT = P.sb([128, 8, TE], BF16)
        thT = Tok()
        st = norm_state(fw, P)
        xr = Rot([P.sb([128, 8, 512], F32) for _ in range(2)])
        for (lo, hi, ic) in SEGS:
            n = hi - lo
            xs, txs = xr.next()
            fw.dma("sp", xs[:, :, :n], XTv[:, :, lo:hi], writes=[txs])

            def outs(k, lo=lo, hi=hi, ic=ic):
                o = [hT[:, k, lo:hi]]
                if ic:
                    o.append(hT[:, k, T + lo:T + hi])
                return o
            norm_mod_seg(fw, P, st, xs, txs, n, ic, AB, tab, outs, [thT])
        psr = Rot([P.ps() for _ in range(4)])
        stg = Rot([P.sb([128, 512], F32) for _ in range(4)])
        tz = Tok()
        cnt = 0
        for nt in range(16):
            if nt == 7:
                continue
            M = 64 if nt == 15 else 128
            for cc_ in range(5):
                c0 = cc_ * 512
                ps, tps = psr.next()
                for k in range(8):
                    fw.op("pe", lambda e, ps=ps, k=k, nt=nt, M=M, c0=c0: e.matmul(
                        ps[0:M, :], lhsT=wb[:, k, nt * 128:nt * 128 + M], rhs=hT[:, k, c0:c0 + 512],
                        start=(k == 0), stop=(k == 7)), reads=[tw, thT], writes=[tps])
                sg, tsg = stg.next()
                if cnt % 2 == 0:
                    fw.op("dve", lambda e, sg=sg, ps=ps, M=M: e.tensor_copy(out=sg[0:M, :], in_=ps[0:M, :]), reads=[tps], writes=[tsg])
                else:
                    fw.op("act", lambda e, sg=sg, ps=ps, M=M: e.copy(out=sg[0:M, :], in_=ps[0:M, :]), reads=[tps], writes=[tsg])
                cnt += 1
                fw.dma("sp", ZT[nt * 128:nt * 128 + M, c0:c0 + 512], sg[0:M, :], reads=[tsg], writes=[tz])
        for tt in range(18):
            ps, tps = psr.next()
            for k in range(8):
                fw.op("pe", lambda e, ps=ps, k=k, tt=tt: e.matmul(
                    ps[:, 0:128], lhsT=hT[:, k, tt * 128:(tt + 1) * 128], rhs=wb[:, k, 896:1024],
                    start=(k == 0), stop=(k == 7)), reads=[tw, thT], writes=[tps])
            sg, tsg = stg.next()
            fw.op("dve", lambda e, sg=sg, ps=ps: e.tensor_copy(out=sg[:, 0:128], in_=ps[:, 0:128]), reads=[tps], writes=[tsg])
            fw.dma("sp", VT[tt * 128:(tt + 1) * 128, :], sg[:, 0:128], reads=[tsg], writes=[tz])
        fw.barrier()


def build_program(stage_fns):
    nc = bass.Bass("TRN2", target_bir_lowering=False)
    decl = {}

    def io(name, shape, dt, role):
        if name in decl:
            return decl[name][0]
        kind = "ExternalInput" if role == "in" else "ExternalOutput"
        ap = nc.dram_tensor(name, list(shape), dt, kind=kind).ap()
        decl[name] = (ap, role, shape)
        return ap
    with ExitStack() as es:
        fw = FW(nc, es)
        for fn in stage_fns:
            fn(fw, io)
        fw.barrier()
    return nc, decl, fw


_PROG_CACHE = {}


def run_stage(key, stage_fns, in_maps, ncores):
    if key not in _PROG_CACHE:
        _PROG_CACHE[key] = build_program(stage_fns)
    nc, decl, fw = _PROG_CACHE[key]
    res = run_bass_kernel_spmd(nc, in_maps, core_ids=list(range(ncores)))
    return res.results


def rstd_from_ps(fw, rs, trs, ps, tps, n, scale, epsap, teps, rows=128):
    fw.op("act", lambda e: e.activation(out=rs[0:rows, :n], in_=ps[0:rows, :n], func=AF.Ln, scale=scale, bias=epsap),
          reads=[tps, teps], writes=[trs])
    fw.op("act", lambda e: e.activation(out=rs[0:rows, :n], in_=rs[0:rows, :n], func=AF.Exp, scale=-0.5), reads=[trs], writes=[trs])


def stage_p3(fw, io):
    ZT = io("ZT", [2048, TE], F32, "in")
    VT = io("VT", [T, 128], F32, "in")
    qn_g = io("att_qn_g", [64], F32, "in")
    kn_g = io("att_kn_g", [64], F32, "in")
    og = io("att_out_g", [512], F32, "in")
    POS = io("POS", [128, NLAT], F32, "in")
    CST = io("CST", [128, 260], F32, "in")
    ATT = io("ATT_T", [512, T], F32, "out")
    with ExitStack() as es:
        P = Pool_(fw, es)
        cst = P.sb([128, 260], F32)
        tc = Tok()
        fw.dma("sp", cst[:], CST, writes=[tc])
        permb = P.sb([128, 128], BF16)
        bdb = P.sb([128, 128], BF16)
        fw.op("dve", lambda e: e.tensor_copy(out=permb[:], in_=cst[:, 1:129]), reads=[tc], writes=[tc])
        fw.op("dve", lambda e: e.tensor_copy(out=bdb[:], in_=cst[:, 129:257]), reads=[tc], writes=[tc])
        eps = P.sb([128, 2], F32)
        teps = Tok()
        fw.op("pool", lambda e: e.memset(eps[:, 0:1], RMS_EPS), writes=[teps])
        fw.op("pool", lambda e: e.memset(eps[:, 1:2], 0.0), writes=[teps])
        gq = P.sb([128, 2], F32)
        tg = Tok()
        for h in range(2):
            fw.dma("sp", gq[h * 64:(h + 1) * 64, 0:1], qn_g.rearrange("(p o) -> p o", o=1), writes=[tg])
            fw.dma("sp", gq[h * 64:(h + 1) * 64, 1:2], kn_g.rearrange("(p o) -> p o", o=1), writes=[tg])
        cos = P.sb([128, NLAT], F32)
        sin = P.sb([128, NLAT], F32)
        ttab = Tok()
        with ExitStack() as es2:
            P2 = Pool_(fw, es2)
            ang = P2.sb([128, NLAT], F32)
            tmpf = P2.sb([128, NLAT], F32)
            tmpi = P2.sb([128, NLAT], I32)
            ta = Tok()
            fw.dma("sp", ang[:], POS, writes=[ta])
            fw.op("dve", lambda e: e.tensor_scalar(out=ang[:], in0=ang[:], scalar1=cst[:, 0:1], scalar2=None, op0=ALU.mult),
                  reads=[ta, tc], writes=[ta])
            for (tab, off) in ((sin, 0.0), (cos, math.pi / 2)):
                fw.op("dve", lambda e, off=off: e.tensor_scalar(out=tmpi[:], in0=ang[:], scalar1=off, scalar2=1.0 / (2 * math.pi),
                                                                op0=ALU.add, op1=ALU.mult), reads=[ta], writes=[ta])
                fw.op("dve", lambda e: e.tensor_copy(out=tmpf[:], in_=tmpi[:]), reads=[ta], writes=[ta])
                fw.op("dve", lambda e: e.scalar_tensor_tensor(out=tmpf[:], in0=tmpf[:], scalar=-2 * math.pi, in1=ang[:],
                                                              op0=ALU.mult, op1=ALU.add), reads=[ta], writes=[ta])
                fw.op("dve", lambda e, off=off: e.tensor_scalar(out=tmpf[:], in0=tmpf[:], scalar1=off, scalar2=math.pi,
                                                                op0=ALU.add, op1=ALU.min), reads=[ta], writes=[ta])
                fw.op("dve", lambda e: e.tensor_scalar(out=tmpf[:], in0=tmpf[:], scalar1=-math.pi, scalar2=None, op0=ALU.max),
                      reads=[ta], writes=[ta])
                fw.op("act", lambda e, tab=tab: e.activation(out=tab[:], in_=tmpf[:], func=AF.Sin), reads=[ta], writes=[ttab])
            fw.barrier()
        qb = P.sb([128, 4, T], BF16)
        kd = P.sb([128, 2, T], BF16)
        tq = Tok()
        with ExitStack() as es2:
            P2 = Pool_(fw, es2)
            raw = Rot([P2.sb([128, 512], F32) for _ in range(2)])
            sqr = Rot([P2.sb([128, 512], BF16) for _ in range(2)])
            psr = Rot([P2.ps() for _ in range(2)])
            psr2 = Rot([P2.ps() for _ in range(2)])
            rsr = Rot([P2.sb([128, 512], F32) for _ in range(2)])
            nbr = Rot([P2.sb([128, 512], BF16) for _ in range(2)])
            t1r = Rot([P2.sb([128, 512], F32) for _ in range(2)])
            t2r = Rot([P2.sb([128, 512], F32) for _ in range(2)])
            items = [("q", j) for j in range(4)] + [("k", g) for g in range(2)]
            for (kind, j) in items:
                for (lo, hi, ic) in SEGS:
                    n = hi - lo
                    rw, trw = raw.next()
                    if kind == "q":
                        fw.dma("sp", rw[:, :n], ZT[256 + j * 128:256 + (j + 1) * 128, lo:hi], writes=[trw])
                        gcol = 0
                        dst = qb[:, j, lo:hi]
                    else:
                        for h in range(2):
                            fw.dma("sp", rw[h * 64:(h + 1) * 64, :n], ZT[768 + j * 64:768 + (j + 1) * 64, lo:hi], writes=[trw])
                        gcol = 1
                        dst = kd[:, j, lo:hi]
                    sq, tsq = sqr.next()
                    fw.op("act", lambda e, sq=sq, rw=rw, n=n: e.activation(out=sq[:, :n], in_=rw[:, :n], func=AF.Square), reads=[trw], writes=[tsq])
                    ps, tps = psr.next()
                    fw.op("pe", lambda e, ps=ps, sq=sq, n=n: e.matmul(ps[:, :n], lhsT=bdb[:], rhs=sq[:, :n], start=True, stop=True),
                          reads=[tsq, tc], writes=[tps])
                    rs, trs = rsr.next()
                    rstd_from_ps(fw, rs, trs, ps, tps, n, 1.0, eps[:, 0:1], teps)
                    t1, tt1 = t1r.next()
                    fw.op("dve", lambda e, t1=t1, rw=rw, rs=rs, n=n, gcol=gcol: e.scalar_tensor_tensor(
                        out=t1[:, :n], in0=rw[:, :n], scalar=gq[:, gcol:gcol + 1], in1=rs[:, :n], op0=ALU.mult, op1=ALU.mult),
                        reads=[trw, trs, tg], writes=[tt1])
                    if ic:
                        fw.op("act", lambda e, dst=dst, t1=t1, n=n: e.copy(out=dst, in_=t1[:, :n]), reads=[tt1], writes=[tq])
                        continue
                    nb, tnb = nbr.next()
                    fw.op("act", lambda e, nb=nb, t1=t1, n=n: e.copy(out=nb[:, :n], in_=t1[:, :n]), reads=[tt1], writes=[tnb])
                    ps2, tps2 = psr2.next()
                    fw.op("pe", lambda e, ps2=ps2, nb=nb, n=n: e.matmul(ps2[:, :n], lhsT=permb[:], rhs=nb[:, :n], start=True, stop=True),
                          reads=[tnb, tc], writes=[tps2])
                    p0 = lo - NCTX
                    t2, tt2 = t2r.next()
                    fw.op("dve", lambda e, t2=t2, ps2=ps2, n=n, p0=p0: e.tensor_tensor(out=t2[:, :n], in0=ps2[:, :n], in1=sin[:, p0:p0 + n], op=ALU.mult),
                          reads=[tps2, ttab], writes=[tt2])
                    fw.op("pool", lambda e, t1=t1, nb=nb, n=n, p0=p0: e.tensor_tensor(out=t1[:, :n], in0=nb[:, :n], in1=cos[:, p0:p0 + n], op=ALU.mult),
                          reads=[tnb, ttab, tt1], writes=[tt1])
                    fw.op("pool", lambda e, dst=dst, t1=t1, t2=t2, n=n: e.tensor_tensor(out=dst, in0=t1[:, :n], in1=t2[:, :n], op=ALU.add),
                          reads=[tt1, tt2], writes=[tq])
            fw.barrier()
        va = P.sb([128, 18, 2, 128], BF16)
        tva = Tok()
        fw.op("pool", lambda e: e.memset(va[:], 1.0), writes=[tva])
        for g in range(2):
            fw.dma("pool", va[:, :, g, 0:64], VT.rearrange("(t p) c -> p t c", p=128)[:, :, g * 64:(g + 1) * 64], writes=[tva])
        att = P.sb([128, 4, T], F32)
        tatt = Tok()
        pss = Rot([P.ps() for _ in range(3)])
        pso = Rot([P.ps() for _ in range(2)])
        ptr = Rot([P.sb([128, 512], BF16) for _ in range(3)])
        rcr = Rot([P.sb([64, 512], F32) for _ in range(2)])
        jobs = [(0, 256, 0, 2)] + [(256 + 512 * i, 768 + 512 * i, 0, 18) for i in range(4)]
        for h in range(8):
            g = h // 4
            jt, r0 = h // 2, (h % 2) * 64
            for (qlo, qhi, k0, k1) in jobs:
                n = qhi - qlo
                po, tpo = pso.next()
                for kt in range(k0, k1):
                    ps, tps = pss.next()
                    fw.op("pe", lambda e, ps=ps, kt=kt, g=g, jt=jt, r0=r0, qlo=qlo, qhi=qhi, n=n: e.matmul(
                        ps[:, :n], lhsT=kd[r0:r0 + 64, g, kt * 128:(kt + 1) * 128], rhs=qb[r0:r0 + 64, jt, qlo:qhi],
                        start=True, stop=True), reads=[tq], writes=[tps])
                    pt, tpt = ptr.next()
                    fw.op("act", lambda e, pt=pt, ps=ps, n=n: e.activation(out=pt[:, :n], in_=ps[:, :n], func=AF.Exp, scale=0.125),
                          reads=[tps], writes=[tpt])
                    fw.op("pe", lambda e, po=po, pt=pt, kt=kt, g=g, n=n, k0=k0, k1=k1: e.matmul(
                        po[:, :n], lhsT=va[:, kt, g, :], rhs=pt[:, :n], start=(kt == k0), stop=(kt == k1 - 1)),
                        reads=[tpt, tva], writes=[tpo])
                rc, trc = rcr.next()
                fw.op("dve", lambda e, rc=rc, po=po, n=n: e.reciprocal(out=rc[:, :n], in_=po[64:128, :n]), reads=[tpo], writes=[trc])
                fw.op("dve", lambda e, rc=rc, po=po, n=n, jt=jt, r0=r0, qlo=qlo, qhi=qhi: e.tensor_tensor(
                    out=att[r0:r0 + 64, jt, qlo:qhi], in0=po[0:64, :n], in1=rc[:, :n], op=ALU.mult),
                    reads=[tpo, trc], writes=[tatt])
        ogs = P.sb([128, 4], F32)
        tog = Tok()
        fw.dma("sp", ogs[:], og.rearrange("(k p) -> p k", p=128), writes=[tog], allow_slow_non_contiguous=True)
        ones = P.sb([128, 128], BF16)
        fw.op("pool", lambda e: e.memset(ones[:], 1.0), writes=[tog])
        sq4 = P.sb([128, 4, 512], BF16)
        tsq4 = Tok()
        rs = P.sb([128, 512], F32)
        trs = Tok()
        stg = Rot([P.sb([128, 512], F32) for _ in range(3)])
        tout = Tok()
        for (lo, hi, ic) in SEGS:
            n = hi - lo
            fw.op("act", lambda e, lo=lo, hi=hi, n=n: e.activation(out=sq4[:, :, :n], in_=att[:, :, lo:hi], func=AF.Square), reads=[tatt], writes=[tsq4])
            ps, tps = pss.next()
            for k in range(4):
                fw.op("pe", lambda e, ps=ps, k=k, n=n: e.matmul(ps[:, :n], lhsT=ones[:], rhs=sq4[:, k, :n], start=(k == 0), stop=(k == 3)),
                      reads=[tsq4, tog], writes=[tps])
            rstd_from_ps(fw, rs, trs, ps, tps, n, 1.0 / 512, eps[:, 0:1], teps)
            for k in range(4):
                sg, tsg = stg.next()
                fw.op("dve", lambda e, sg=sg, k=k, lo=lo, hi=hi, n=n: e.scalar_tensor_tensor(
                    out=sg[:, :n], in0=att[:, k, lo:hi], scalar=ogs[:, k:k + 1], in1=rs[:, :n], op0=ALU.mult, op1=ALU.mult),
                    reads=[tatt, trs, tog], writes=[tsg])
                fw.dma("sp", ATT[k * 128:(k + 1) * 128, lo:hi], sg[:, :n], reads=[tsg], writes=[tout])
        fw.barrier()


def host_consts():
    pos = np.zeros((128, NLAT), np.float32)
    inv = np.zeros((128,), np.float32)
    tok = np.arange(NLAT)
    for p in range(128):
        d = p % 64
        pos[p] = (tok // 64) if d < 32 else (tok % 64)
        inv[p] = 10000.0 ** (-(d % 16) / 16.0)
    cst = np.zeros((128, 260), np.float32)
    cst[:, 0] = inv
    perm = np.zeros((128, 128), np.float32)
    for m in range(128):
        d = m % 32
        if d < 16:
            perm[m + 16, m] = -1.0
        else:
            perm[m - 16, m] = 1.0
    cst[:, 1:129] = perm
    bd = np.zeros((128, 128), np.float32)
    bd[:64, :64] = 1.0 / 64
    bd[64:, 64:] = 1.0 / 64
    cst[:, 129:257] = bd
    return pos, cst


def stage_p5a(fw, io):
    XT = io("XT", [DM, T], F32, "in")
    S5T = io("S5T", [256, T], F32, "in")
    ATT = io("ATT_T", [512, T], F32, "in")
    RWT = io("RWT", [256, T], F32, "in")
    MOD = io("MOD", [128, 48, 2], F32, "in")
    w_out = io("w_out", [DM, DM], F32, "in")
    XT1 = io("XT1", [DM, T], F32, "out")
    XTv = XT.rearrange("(k p) t -> p k t", p=128)
    with ExitStack() as es:
        P = Pool_(fw, es)
        MODs = P.sb([128, 48, 2], F32)
        tmod = Tok()
        fw.dma("sp", MODs[:], MOD, writes=[tmod])
        tw = Tok()
        wb = load_w_bf16(fw, P, w_out, DM, DM, tw, "wout")
        cat = P.sb([128, 8, T], BF16)
        tcat = Tok()
        for k in range(8):
            src = S5T[k * 128:(k + 1) * 128, :] if k < 2 else (ATT[(k - 2) * 128:(k - 1) * 128, :] if k < 6 else RWT[(k - 6) * 128:(k - 5) * 128, :])
            fw.dma("pool", cat[:, k, :], src, writes=[tcat])
        xr = Rot([P.sb([128, 8, 512], F32) for _ in range(2)])
        x1r = Rot([P.sb([128, 8, 512], F32) for _ in range(2)])
        psr = Rot([P.ps() for _ in range(4)])
        to1, to2 = Tok(), Tok()
        for (lo, hi, ic) in SEGS:
            n = hi - lo
            xs, txs = xr.next()
            fw.dma("sp", xs[:, :, :n], XTv[:, :, lo:hi], writes=[txs])
            x1, tx1 = x1r.next()
            for d in range(8):
                ps, tps = psr.next()
                for k in range(8):
                    fw.op("pe", lambda e, ps=ps, k=k, d=d, lo=lo, hi=hi, n=n: e.matmul(
                        ps[:, :n], lhsT=wb[:, k, d * 128:(d + 1) * 128], rhs=cat[:, k, lo:hi], start=(k == 0), stop=(k == 7)),
                        reads=[tw, tcat], writes=[tps])
                fw.op("dve", lambda e, ps=ps, d=d, n=n, ic=ic, x1=x1, xs=xs: e.scalar_tensor_tensor(
                    out=x1[:, d, :n], in0=ps[:, :n], scalar=MODs[:, 16 + d, ic:ic + 1], in1=xs[:, d, :n], op0=ALU.mult, op1=ALU.add),
                    reads=[tps, txs, tmod], writes=[tx1])
            for k in range(8):
                fw.dma("sp", XT1[k * 128:(k + 1) * 128, lo:hi], x1[:, k, :n], reads=[tx1], writes=[to1])
        fw.barrier()


def stage_p5b(fw, io):
    XT1 = io("XT1", [DM, T], F32, "in")
    n2g = io("norm2_g", [DM], F32, "in")
    MOD = io("MOD", [128, 48, 2], F32, "in")
    up = io("ffn_up", [DM, 5632], F32, "in")
    cw = io("ffn_conv_w", [3, 5632], F32, "in")
    cb = io("ffn_conv_b", [5632], F32, "in")
    down = io("ffn_down", [2816, DM], F32, "in")
    XT2 = io("XT2", [DM, T], F32, "out")
    X1v = XT1.rearrange("(k p) t -> p k t", p=128)
    X2v = XT2.rearrange("(k p) t -> p k t", p=128)
    with ExitStack() as es:
        P = Pool_(fw, es)
        MODs = P.sb([128, 48, 2], F32)
        tmod = Tok()
        fw.dma("sp", MODs[:], MOD, writes=[tmod])
        cws = P.sb([128, 44, 3], F32)
        cbs = P.sb([128, 44], F32)
        tcw = Tok()
        for w in range(3):
            fw.dma("sp", cws[:, :, w], cw[w].rearrange("(j p) -> p j", p=128), writes=[tcw], allow_slow_non_contiguous=True)
        fw.dma("sp", cbs[:], cb.rearrange("(j p) -> p j", p=128), writes=[tcw], allow_slow_non_contiguous=True)
        h2 = P.sb([128, 8, T], BF16)
        th2 = Tok()
        AB, tab = make_AB(fw, P, MODs, tmod, n2g, 24, 32)
        with ExitStack() as es2:
            P2 = Pool_(fw, es2)
            st = norm_state(fw, P2)
            xr0 = Rot([P2.sb([128, 8, 512], F32) for _ in range(2)])
            for (lo, hi, ic) in SEGS:
                n = hi - lo
                xs, txs = xr0.next()
                fw.dma("sp", xs[:, :, :n], X1v[:, :, lo:hi], writes=[txs])
                norm_mod_seg(fw, P2, st, xs, txs, n, ic, AB, tab, lambda k, lo=lo, hi=hi: [h2[:, k, lo:hi]], [th2])
            fw.barrier()
        hid = P.sb([128, 11, T], BF16)
        thid = Tok()
        dwb = P.sb([128, 11, DM], BF16)
        tdw = Tok()
        ubr = Rot([P.sb([128, 8, 2, 128], BF16) for _ in range(2)])
        u = [P.sb([128, T], F32) for _ in range(2)]
        y = [P.sb([128, T], F32) for _ in range(2)]
        tu = [Tok(), Tok()]
        ty = [Tok(), Tok()]
        psr = Rot([P.ps() for _ in range(4)])
        xr = Rot([P.sb([128, 8, 512], F32) for _ in range(2)])
        tx2 = Tok()
        RANGES = [(0, NCTX), (NCTX, T)]
        for half in range(2):
            for jj in range(11):
                r0 = (half * 11 + jj) * 128
                fw.dma("pool", dwb[:, jj, :], down[r0:r0 + 128, :], reads=[], writes=[tdw])
            for jj in range(11):
                j = half * 11 + jj
                ub, tub = ubr.next()
                for k in range(8):
                    for wh in range(2):
                        c0 = wh * 2816 + j * 128
                        fw.dma("pool", ub[:, k, wh, :], up[k * 128:(k + 1) * 128, c0:c0 + 128], writes=[tub])
                for wh in range(2):
                    jc = wh * 22 + j
                    for si, (lo, hi, ic) in enumerate(SEGS):
                        n = hi - lo
                        ps, tps = psr.next()
                        for k in range(8):
                            fw.op("pe", lambda e, ps=ps, k=k, wh=wh, ub=ub, lo=lo, hi=hi, n=n: e.matmul(
                                ps[:, :n], lhsT=ub[:, k, wh, :], rhs=h2[:, k, lo:hi], start=(k == 0), stop=(k == 7)),
                                reads=[tub, th2], writes=[tps])
                        if si % 2 == 0:
                            fw.op("dve", lambda e, ps=ps, wh=wh, lo=lo, hi=hi, n=n: e.tensor_copy(out=u[wh][:, lo:hi], in_=ps[:, :n]),
                                  reads=[tps], writes=[tu[wh]])
                        else:
                            fw.op("act", lambda e, ps=ps, wh=wh, lo=lo, hi=hi, n=n: e.copy(out=u[wh][:, lo:hi], in_=ps[:, :n]),
                                  reads=[tps], writes=[tu[wh]])
                    fw.op("act", lambda e, wh=wh, jc=jc: e.activation(out=y[wh][:], in_=u[wh][:], func=AF.Identity,
                                                                     scale=cws[:, jc, 1:2], bias=cbs[:, jc:jc + 1]),
                          reads=[tu[wh], tcw], writes=[ty[wh]])
                    for (lo, hi) in RANGES:
                        fw.op("dve", lambda e, wh=wh, jc=jc, lo=lo, hi=hi: e.scalar_tensor_tensor(
                            out=y[wh][:, lo + 1:hi], in0=u[wh][:, lo:hi - 1], scalar=cws[:, jc, 0:1], in1=y[wh][:, lo + 1:hi],
                            op0=ALU.mult, op1=ALU.add), reads=[tu[wh], tcw, ty[wh]], writes=[ty[wh]])
                        fw.op("dve", lambda e, wh=wh, jc=jc, lo=lo, hi=hi: e.scalar_tensor_tensor(
                            out=y[wh][:, lo:hi - 1], in0=u[wh][:, lo + 1:hi], scalar=cws[:, jc, 2:3], in1=y[wh][:, lo:hi - 1],
                            op0=ALU.mult, op1=ALU.add), reads=[tu[wh], tcw, ty[wh]], writes=[ty[wh]])
                fw.op("act", lambda e: e.activation(out=y[0][:], in_=y[0][:], func=AF.Silu), reads=[ty[0]], writes=[ty[0]])
                fw.op("pool", lambda e, jj=jj: e.tensor_tensor(out=hid[:, jj, :], in0=y[0][:], in1=y[1][:], op=ALU.mult),
                      reads=[ty[0], ty[1]], writes=[thid])
            for (lo, hi, ic) in SEGS:
                n = hi - lo
                xs, txs = xr.next()
                src = X1v if half == 0 else X2v
                fw.dma("sp", xs[:, :, :n], src[:, :, lo:hi], reads=([tx2] if half else []), writes=[txs])
                for d in range(8):
                    ps, tps = psr.next()
                    for jj in range(11):
                        fw.op("pe", lambda e, ps=ps, jj=jj, d=d, lo=lo, hi=hi, n=n: e.matmul(
                            ps[:, :n], lhsT=dwb[:, jj, d * 128:(d + 1) * 128], rhs=hid[:, jj, lo:hi], start=(jj == 0), stop=(jj == 10)),
                            reads=[tdw, thid], writes=[tps])
                    fw.op("dve", lambda e, ps=ps, d=d, n=n, ic=ic, xs=xs: e.scalar_tensor_tensor(
                        out=xs[:, d, :n], in0=ps[:, :n], scalar=MODs[:, 40 + d, ic:ic + 1], in1=xs[:, d, :n], op0=ALU.mult, op1=ALU.add),
                        reads=[tps, tmod, txs], writes=[txs])
                for k in range(8):
                    fw.dma("sp", XT2[k * 128:(k + 1) * 128, lo:hi], xs[:, k, :n], reads=[txs], writes=[tx2])
            fw.barrier()


def stage_p6(fw, io):
    XT = io("XT", [DM, T], F32, "in")
    fg = io("final_g", [DM], F32, "in")
    ident = io("ident", [128, 128], F32, "in")
    OUT = io("OUT", [NLAT, DM], F32, "out")
    XTv = XT.rearrange("(k p) t -> p k t", p=128)
    with ExitStack() as es:
        P = Pool_(fw, es)
        idt = P.sb([128, 128], F32)
        tid = Tok()
        fw.dma("sp", idt[:], ident, writes=[tid])
        g = P.sb([128, 8], F32)
        fw.dma("sp", g[:], fg.rearrange("(k p) -> p k", p=128), writes=[tid], allow_slow_non_contiguous=True)
        st = norm_state(fw, P)
        xr = Rot([P.sb([128, 8, 512], F32) for _ in range(2)])
        yr = Rot([P.sb([128, 8, 512], F32) for _ in range(2)])
        psr = Rot([P.ps() for _ in range(4)])
        orr = Rot([P.sb([128, DM], F32) for _ in range(3)])
        tout = Tok()
        for (lo, hi, ic) in SEGS[1:]:
            n = hi - lo
            xs, txs = xr.next()
            fw.dma("sp", xs[:, :, :n], XTv[:, :, lo:hi], writes=[txs])
            sq, ones = st["sq"], st["ones"]
            fw.op("act", lambda e, xs=xs: e.activation(out=sq[:], in_=xs[:], func=AF.Square), reads=[txs], writes=[st["tsq"]])
            ps, tps = st["psr"].next()
            for k in range(8):
                fw.op("pe", lambda e, ps=ps, k=k: e.matmul(ps[:], lhsT=ones[:], rhs=sq[:, k, :], start=(k == 0), stop=(k == 7)),
                      reads=[st["tsq"], st["tones"]], writes=[tps])
            rstd_from_ps(fw, st["rs"], st["trs"], ps, tps, n, 1.0 / DM, st["eps"][:, 0:1], st["teps"])
            ys, tys = yr.next()
            for k in range(8):
                fw.op("dve", lambda e, ys=ys, xs=xs, k=k: e.scalar_tensor_tensor(
                    out=ys[:, k, :], in0=xs[:, k, :], scalar=g[:, k:k + 1], in1=st["rs"][:], op0=ALU.mult, op1=ALU.mult),
                    reads=[txs, st["trs"], tid], writes=[tys])
            for blk in range(4):
                ot, tot = orr.next()
                for half in range(2):
                    ps2, tps2 = psr.next()
                    for k in range(4):
                        kk = half * 4 + k
                        fw.op("pe", lambda e, ps2=ps2, k=k, kk=kk, ys=ys, blk=blk: e.transpose(
                            ps2[:, k * 128:(k + 1) * 128], ys[:, kk, blk * 128:(blk + 1) * 128], idt[:]),
                            reads=[tys, tid], writes=[tps2])
                    if half == 0:
                        fw.op("dve", lambda e, ot=ot, ps2=ps2: e.tensor_copy(out=ot[:, 0:512], in_=ps2[:]), reads=[tps2], writes=[tot])
                    else:
                        fw.op("act", lambda e, ot=ot, ps2=ps2: e.copy(out=ot[:, 512:1024], in_=ps2[:]), reads=[tps2], writes=[tot])
                r0 = lo - NCTX + blk * 128
                fw.dma("sp", OUT[r0:r0 + 128, :], ot[:], reads=[tot], writes=[tout])
        fw.barrier()


def sin_reduced(fw, P, out, src, tsrc, shape, off, tout):
    ti = P.sb(shape, I32)
    tf = P.sb(shape, F32)
    tt = Tok()
    fw.op("dve", lambda e: e.tensor_scalar(out=ti[:], in0=src, scalar1=off, scalar2=1.0 / (2 * math.pi), op0=ALU.add, op1=ALU.mult),
          reads=[tsrc], writes=[tt])
    fw.op("dve", lambda e: e.tensor_copy(out=tf[:], in_=ti[:]), reads=[tt], writes=[tt])
    fw.op("dve", lambda e: e.scalar_tensor_tensor(out=tf[:], in0=tf[:], scalar=-2 * math.pi, in1=src, op0=ALU.mult, op1=ALU.add),
          reads=[tt, tsrc], writes=[tt])
    fw.op("dve", lambda e: e.tensor_scalar(out=tf[:], in0=tf[:], scalar1=off, scalar2=math.pi, op0=ALU.add, op1=ALU.min), reads=[tt], writes=[tt])
    fw.op("dve", lambda e: e.tensor_scalar(out=tf[:], in0=tf[:], scalar1=-math.pi, scalar2=None, op0=ALU.max), reads=[tt], writes=[tt])
    fw.op("act", lambda e: e.activation(out=out, in_=tf[:], func=AF.Sin), reads=[tt], writes=[tout])


def stage_p2(fw, io):
    ZT = io("ZT", [2048, TE], F32, "in")
    a_re = io("s5_a_re", [2, 16, 64], F32, "in")
    a_im = io("s5_a_im", [2, 16, 64], F32, "in")
    lstep = io("s5_log_step", [2, 16], F32, "in")
    b_re = io("s5_b_re", [2, 16, 64, 16], F32, "in")
    b_im = io("s5_b_im", [2, 16, 64, 16], F32, "in")
    c_re = io("s5_c_re", [2, 16, 16, 64], F32, "in")
    c_im = io("s5_c_im", [2, 16, 16, 64], F32, "in")
    dsk = io("s5_d", [256], F32, "in")
    glu_w = io("s5_glu_w", [256, 256], F32, "in")
    glu_b = io("s5_glu_b", [256], F32, "in")
    out_g = io("s5_out_g", [256], F32, "in")
    S5T = io("S5T", [256, T], F32, "out")
    N = T
    with ExitStack() as es:
        P = Pool_(fw, es)
        are = P.sb([128, 2, 8], F32)
        aim = P.sb([128, 2, 8], F32)
        lst = P.sb([128, 2, 8], F32)
        tpar = Tok()
        for di in range(2):
            fw.dma("sp", are[:, di, :], a_re[di].rearrange("(s g) p -> (g p) s", g=2), writes=[tpar], allow_slow_non_contiguous=True)
            fw.dma("sp", aim[:, di, :], a_im[di].rearrange("(s g) p -> (g p) s", g=2), writes=[tpar], allow_slow_non_contiguous=True)
            for g2 in range(2):
                fw.dma("sp", lst[g2 * 64:(g2 + 1) * 64, di:di + 1, :],
                       lstep[di].rearrange("(s g) -> g s", g=2)[g2:g2 + 1, :].partition_broadcast(64), writes=[tpar],
                       allow_slow_non_contiguous=True)
        sh = [128, 2, 8]
        step = P.sb(sh, F32)
        fw.op("act", lambda e: e.activation(out=step[:], in_=lst[:], func=AF.Exp), reads=[tpar], writes=[tpar])
        er = P.sb(sh, F32)
        th = P.sb(sh, F32)
        fw.op("dve", lambda e: e.tensor_tensor(out=er[:], in0=are[:], in1=step[:], op=ALU.mult), reads=[tpar], writes=[tpar])
        fw.op("act", lambda e: e.activation(out=er[:], in_=er[:], func=AF.Exp), reads=[tpar], writes=[tpar])
        fw.op("dve", lambda e: e.tensor_tensor(out=th[:], in0=aim[:], in1=step[:], op=ALU.mult), reads=[tpar], writes=[tpar])
        sn = P.sb(sh, F32)
        cs = P.sb(sh, F32)
        ttrig = Tok()
        sin_reduced(fw, P, sn[:], th[:], tpar, sh, 0.0, ttrig)
        sin_reduced(fw, P, cs[:], th[:], tpar, sh, math.pi / 2, ttrig)
        PW = P.sb([128, 9, 3, 2, 8], F32)
        tpw = Tok()
        fw.op("dve", lambda e: e.tensor_tensor(out=PW[:, 0, 0], in0=er[:], in1=cs[:], op=ALU.mult), reads=[tpar, ttrig], writes=[tpw])
        fw.op("dve", lambda e: e.tensor_tensor(out=PW[:, 0, 1], in0=er[:], in1=sn[:], op=ALU.mult), reads=[tpar, ttrig], writes=[tpw])
        t1 = P.sb(sh, F32)
        t2 = P.sb(sh, F32)
        for k in range(9):
            fw.op("dve", lambda e, k=k: e.tensor_scalar(out=PW[:, k, 2], in0=PW[:, k, 1], scalar1=-1.0, scalar2=None, op0=ALU.mult),
                  reads=[tpw], writes=[tpw])
            if k == 8:
                break
            fw.op("dve", lambda e, k=k: e.tensor_tensor(out=t1[:], in0=PW[:, k, 0], in1=PW[:, k, 0], op=ALU.mult), reads=[tpw], writes=[tpw])
            fw.op("dve", lambda e, k=k: e.tensor_tensor(out=t2[:], in0=PW[:, k, 1], in1=PW[:, k, 1], op=ALU.mult), reads=[tpw], writes=[tpw])
            fw.op("dve", lambda e, k=k: e.tensor_tensor(out=PW[:, k + 1, 0], in0=t1[:], in1=t2[:], op=ALU.subtract), reads=[tpw], writes=[tpw])
            fw.op("dve", lambda e, k=k: e.scalar_tensor_tensor(out=PW[:, k + 1, 1], in0=PW[:, k, 0], scalar=2.0, in1=PW[:, k, 1],
                                                               op0=ALU.mult, op1=ALU.mult), reads=[tpw], writes=[tpw])
        br = P.sb(sh, F32)
        bi = P.sb(sh, F32)
        nbi = P.sb(sh, F32)
        den = P.sb(sh, F32)
        nr = P.sb(sh, F32)
        tb = Tok()
        fw.op("dve", lambda e: e.tensor_tensor(out=den[:], in0=are[:], in1=are[:], op=ALU.mult), reads=[tpar], writes=[tb])
        fw.op("dve", lambda e: e.tensor_tensor(out=t1[:], in0=aim[:], in1=aim[:], op=ALU.mult), reads=[tpar, tpw], writes=[tpw])
        fw.op("dve", lambda e: e.tensor_tensor(out=den[:], in0=den[:], in1=t1[:], op=ALU.add), reads=[tb, tpw], writes=[tb])
        fw.op("dve", lambda e: e.reciprocal(out=den[:], in_=den[:]), reads=[tb], writes=[tb])
        fw.op("dve", lambda e: e.tensor_scalar(out=nr[:], in0=PW[:, 0, 0], scalar1=-1.0, scalar2=None, op0=ALU.add), reads=[tpw], writes=[tb])
        fw.op("dve", lambda e: e.tensor_tensor(out=t1[:], in0=nr[:], in1=are[:], op=ALU.mult), reads=[tb, tpar, tpw], writes=[tpw])
        fw.op("dve", lambda e: e.tensor_tensor(out=t2[:], in0=PW[:, 0, 1], in1=aim[:], op=ALU.mult), reads=[tpw, tpar], writes=[tpw])
        fw.op("dve", lambda e: e.tensor_tensor(out=t1[:], in0=t1[:], in1=t2[:], op=ALU.add), reads=[tpw], writes=[tpw])
        fw.op("dve", lambda e: e.tensor_tensor(out=br[:], in0=t1[:], in1=den[:], op=ALU.mult), reads=[tpw, tb], writes=[tb])
        fw.op("dve", lambda e: e.tensor_tensor(out=t1[:], in0=PW[:, 0, 1], in1=are[:], op=ALU.mult), reads=[tpw, tpar, tb], writes=[tpw])
        fw.op("dve", lambda e: e.tensor_tensor(out=t2[:], in0=nr[:], in1=aim[:], op=ALU.mult), reads=[tb, tpar, tpw], writes=[tpw])
        fw.op("dve", lambda e: e.tensor_tensor(out=t1[:], in0=t1[:], in1=t2[:], op=ALU.subtract), reads=[tpw], writes=[tpw])
        fw.op("dve", lambda e: e.tensor_tensor(out=bi[:], in0=t1[:], in1=den[:], op=ALU.mult), reads=[tpw, tb], writes=[tb])
        fw.op("dve", lambda e: e.tensor_scalar(out=nbi[:], in0=bi[:], scalar1=-1.0, scalar2=None, op0=ALU.mult), reads=[tb], writes=[tb])
        BTb = P.sb([128, 2, 2, 8, 128], BF16)
        CTb = P.sb([128, 2, 2, 8, 128], BF16)
        tBT = Tok()
        tCT = Tok()
        with ExitStack() as es2:
            P2 = Pool_(fw, es2)
            BTf = P2.sb([128, 2, 2, 8, 128], F32)
            CTf = P2.sb([128, 2, 2, 8, 128], F32)
            CT2 = P2.sb([128, 2, 2, 8, 128], F32)
            tf1, tf2 = Tok(), Tok()
            fw.op("pool", lambda e: e.memset(BTf[:], 0.0), writes=[tf1])
            fw.op("pool", lambda e: e.memset(CTf[:], 0.0), writes=[tf2])
            for di in range(2):
                for ri, (bsrc, csrc) in enumerate(((b_re, c_re), (b_im, c_im))):
                    for g in range(16):
                        s, g2 = g // 2, g % 2
                        r0 = (g % 8) * 16
                        fw.dma("sp", BTf[r0:r0 + 16, di, ri, s, g2 * 64:(g2 + 1) * 64], bsrc[di, g].rearrange("p h -> h p"),
                               writes=[tf1], allow_slow_non_contiguous=True)
                        fw.dma("sp", CTf[g2 * 64:(g2 + 1) * 64, di, ri, s, r0:r0 + 16], csrc[di, g].rearrange("h p -> p h"),
                               writes=[tf2], allow_slow_non_contiguous=True)
            fw.op("act", lambda e: e.copy(out=BTb[:], in_=BTf[:]), reads=[tf1], writes=[tBT])
            bsh = [128, 2, 8, 128]
            brb = br[:].unsqueeze(3).to_broadcast(bsh)
            nbib = nbi[:].unsqueeze(3).to_broadcast(bsh)
            tc2 = Tok()
            fw.op("dve", lambda e: e.tensor_tensor(out=CT2[:, :, 0], in0=CTf[:, :, 0], in1=brb, op=ALU.mult), reads=[tf2, tb], writes=[tc2])
            fw.op("pool", lambda e: e.tensor_tensor(out=CT2[:, :, 1], in0=CTf[:, :, 1], in1=nbib, op=ALU.mult), reads=[tf2, tb], writes=[tc2])
            fw.op("dve", lambda e: e.tensor_tensor(out=CT2[:, :, 0], in0=CT2[:, :, 0], in1=CT2[:, :, 1], op=ALU.add), reads=[tc2], writes=[tc2])
            fw.op("act", lambda e: e.copy(out=CTb[:, :, 0], in_=CT2[:, :, 0]), reads=[tc2], writes=[tCT])
            fw.op("dve", lambda e: e.tensor_tensor(out=CT2[:, :, 0], in0=CTf[:, :, 0], in1=nbib, op=ALU.mult), reads=[tf2, tb, tCT, tc2], writes=[tc2])
            fw.op("pool", lambda e: e.tensor_tensor(out=CT2[:, :, 1], in0=CTf[:, :, 1], in1=brb, op=ALU.mult), reads=[tf2, tb, tc2], writes=[tc2])
            fw.op("dve", lambda e: e.tensor_tensor(out=CT2[:, :, 0], in0=CT2[:, :, 0], in1=CT2[:, :, 1], op=ALU.subtract), reads=[tc2], writes=[tc2])
            fw.op("act", lambda e: e.copy(out=CTb[:, :, 1], in_=CT2[:, :, 0]), reads=[tc2], writes=[tCT])
            fw.barrier()
        ub = P.sb([128, 2, TE], BF16)
        tub = Tok()
        for ct in range(2):
            fw.dma("pool", ub[:, ct, :], ZT[ct * 128:(ct + 1) * 128, :], writes=[tub])
        dk = P.sb([128, 2], F32)
        tdk = Tok()
        fw.dma("sp", dk[:], dsk.rearrange("(c p) -> p c", p=128), writes=[tdk], allow_slow_non_contiguous=True)
        y = P.sb([128, 2, T], F32)
        ty = Tok()
        for ct in range(2):
            fw.dma("sp", y[:, ct, :], ZT[ct * 128:(ct + 1) * 128, 0:T], writes=[ty])
        for ct in range(2):
            fw.op("pool", lambda e, ct=ct: e.tensor_scalar(out=y[:, ct, :], in0=y[:, ct, :], scalar1=dk[:, ct:ct + 1], scalar2=None, op0=ALU.mult),
                  reads=[ty, tdk], writes=[ty])
        Xs = Rot([P.sb([128, 2, N], F32) for _ in range(2)])
        xbs = Rot([P.sb([128, 2, N], BF16) for _ in range(2)])
        psr = Rot([P.ps() for _ in range(4)])
        psy = Rot([P.ps() for _ in range(2)])
        tmpy = Rot([P.sb([128, 512], F32) for _ in range(2)])
        CH = [(0, 512), (512, 1024), (1024, 1536), (1536, 2048), (2048, 2304)]
        for di in range(2):
            off = 0 if di == 0 else 256
            for s in range(8):
                ct = s // 4
                X, tX = Xs.next()
                for ri in range(2):
                    for ci, (c0, c1) in enumerate(CH):
                        n = c1 - c0
                        ps, tps = psr.next()
                        fw.op("pe", lambda e, ps=ps, di=di, ri=ri, s=s, ct=ct, c0=c0, c1=c1, n=n, off=off: e.matmul(
                            ps[:, :n], lhsT=BTb[:, di, ri, s, :], rhs=ub[:, ct, off + c0:off + c1], start=True, stop=True),
                            reads=[tBT, tub], writes=[tps])
                        fw.op("act", lambda e, ps=ps, X=X, ri=ri, c0=c0, c1=c1, n=n: e.copy(out=X[:, ri, c0:c1], in_=ps[:, :n]),
                              reads=[tps], writes=[tX])

                def cstep(w_re, w_im, r_re, r_im, k, di=di, s=s, tX=tX):
                    pr = PW[:, k, 0, di, s:s + 1]
                    pi = PW[:, k, 1, di, s:s + 1]
                    npi = PW[:, k, 2, di, s:s + 1]
                    for (o, i0, sc) in ((w_re, r_re, pr), (w_re, r_im, npi), (w_im, r_im, pr), (w_im, r_re, pi)):
                        fw.op("dve", lambda e, o=o, i0=i0, sc=sc: e.scalar_tensor_tensor(out=o, in0=i0, scalar=sc, in1=o, op0=ALU.mult, op1=ALU.add),
                              reads=[tX, tpw], writes=[tX])
                for k in range(8):
                    st_ = 1 << k
                    Xv = [X[:, ri, :].rearrange("p (m c) -> p m c", c=2 * st_) for ri in range(2)]
                    if di == 0:
                        cstep(Xv[0][:, :, 2 * st_ - 1], Xv[1][:, :, 2 * st_ - 1], Xv[0][:, :, st_ - 1], Xv[1][:, :, st_ - 1], k)
                    else:
                        cstep(Xv[0][:, :, 0], Xv[1][:, :, 0], Xv[0][:, :, st_], Xv[1][:, :, st_], k)
                for i in (range(1, 9) if di == 0 else range(7, -1, -1)):
                    if di == 0:
                        w, r = 256 * i + 255, 256 * (i - 1) + 255
                    else:
                        w, r = 256 * i, 256 * (i + 1)
                    cstep(X[:, 0, w:w + 1], X[:, 1, w:w + 1], X[:, 0, r:r + 1], X[:, 1, r:r + 1], 8)
                for k in range(7, -1, -1):
                    st_ = 1 << k
                    Xv = [X[:, ri, :].rearrange("p (m c) -> p m c", c=2 * st_) for ri in range(2)]
                    if di == 0:
                        cstep(Xv[0][:, 1:, st_ - 1], Xv[1][:, 1:, st_ - 1], Xv[0][:, :-1, 2 * st_ - 1], Xv[1][:, :-1, 2 * st_ - 1], k)
                    else:
                        cstep(Xv[0][:, :-1, st_], Xv[1][:, :-1, st_], Xv[0][:, 1:, 0], Xv[1][:, 1:, 0], k)
                xb, txb = xbs.next()
                fw.op("act", lambda e, xb=xb, X=X: e.copy(out=xb[:], in_=X[:]), reads=[tX], writes=[txb])
                for (c0, c1) in CH:
                    n = c1 - c0
                    if di == 0:
                        y0 = c0
                    else:
                        y0 = c0 + 256 if c0 < 2048 else 0
                    ps, tps = psy.next()
                    for ri in range(2):
                        fw.op("pe", lambda e, ps=ps, xb=xb, di=di, ri=ri, s=s, c0=c0, c1=c1, n=n: e.matmul(
                            ps[:, :n], lhsT=CTb[:, di, ri, s, :], rhs=xb[:, ri, c0:c1], start=(ri == 0), stop=(ri == 1)),
                            reads=[tCT, txb], writes=[tps])
                    tm, ttm = tmpy.next()
                    fw.op("act", lambda e, tm=tm, ps=ps, n=n: e.copy(out=tm[:, :n], in_=ps[:, :n]), reads=[tps], writes=[ttm])
                    fw.op("pool", lambda e, tm=tm, ct=ct, y0=y0, n=n: e.tensor_tensor(out=y[:, ct, y0:y0 + n], in0=y[:, ct, y0:y0 + n], in1=tm[:, :n], op=ALU.add),
                          reads=[ttm, ty], writes=[ty])
        tg = Tok()
        gw = load_w_bf16(fw, P, glu_w, 256, 256, tg, "gluw")
        gb = P.sb([128, 2], F32)
        og = P.sb([128, 2], F32)
        fw.dma("sp", gb[:], glu_b.rearrange("(c p) -> p c", p=128), writes=[tg], allow_slow_non_contiguous=True)
        fw.dma("sp", og[:], out_g.rearrange("(c p) -> p c", p=128), writes=[tg], allow_slow_non_contiguous=True)
        ones = P.sb([128, 128], BF16)
        eps = P.sb([128, 1], F32)
        fw.op("pool", lambda e: e.memset(ones[:], 1.0), writes=[tg])
        fw.op("pool", lambda e: e.memset(eps[:], RMS_EPS), writes=[tg])
        a32 = P.sb([128, 2, 512], F32)
        ab = P.sb([128, 2, 512], BF16)
        w1 = P.sb([128, 2, 512], F32)
        w2 = P.sb([128, 2, 512], F32)
        sqb = P.sb([128, 2, 512], BF16)
        rs = P.sb([128, 512], F32)
        ta, tw1, trs = Tok(), Tok(), Tok()
        stg = Rot([P.sb([128, 512], F32) for _ in range(2)])
        tout = Tok()
        C1 = math.sqrt(2.0 / math.pi)
        for (lo, hi, ic) in SEGS:
            n = hi - lo
            yv = y[:, :, lo:hi]
            fw.op("act", lambda e, yv=yv, n=n: e.activation(out=w1[:, :, :n], in_=yv, func=AF.Square), reads=[ty], writes=[tw1])
            fw.op("dve", lambda e, n=n: e.tensor_scalar(out=w1[:, :, :n], in0=w1[:, :, :n], scalar1=0.044715 * C1, scalar2=C1, op0=ALU.mult, op1=ALU.add),
                  reads=[tw1], writes=[tw1])
            fw.op("dve", lambda e, yv=yv, n=n: e.tensor_tensor(out=w1[:, :, :n], in0=w1[:, :, :n], in1=yv, op=ALU.mult), reads=[tw1, ty], writes=[tw1])
            fw.op("act", lambda e, n=n: e.activation(out=w1[:, :, :n], in_=w1[:, :, :n], func=AF.Tanh), reads=[tw1], writes=[tw1])
            fw.op("dve", lambda e, n=n: e.tensor_scalar(out=w1[:, :, :n], in0=w1[:, :, :n], scalar1=0.5, scalar2=0.5, op0=ALU.mult, op1=ALU.add),
                  reads=[tw1], writes=[tw1])
            fw.op("dve", lambda e, yv=yv, n=n: e.tensor_tensor(out=a32[:, :, :n], in0=w1[:, :, :n], in1=yv, op=ALU.mult), reads=[tw1, ty], writes=[ta])
            fw.op("act", lambda e, n=n: e.copy(out=ab[:, :, :n], in_=a32[:, :, :n]), reads=[ta], writes=[ta])
            for nt in range(2):
                ps, tps = psr.next()
                for k in range(2):
                    fw.op("pe", lambda e, ps=ps, k=k, nt=nt, n=n: e.matmul(ps[:, :n], lhsT=gw[:, k, nt * 128:(nt + 1) * 128], rhs=ab[:, k, :n],
                                                                      start=(k == 0), stop=(k == 1)), reads=[tg, ta], writes=[tps])
                fw.op("act", lambda e, ps=ps, nt=nt, n=n: e.activation(out=w2[:, nt, :n], in_=ps[:, :n], func=AF.Sigmoid, bias=gb[:, nt:nt + 1]),
                      reads=[tps, tg], writes=[tw1])
            fw.op("dve", lambda e, n=n: e.tensor_tensor(out=w2[:, :, :n], in0=w2[:, :, :n], in1=a32[:, :, :n], op=ALU.mult), reads=[tw1, ta], writes=[tw1])
            fw.op("act", lambda e, n=n: e.activation(out=sqb[:, :, :n], in_=w2[:, :, :n], func=AF.Square), reads=[tw1], writes=[tw1])
            ps, tps = psr.next()
            for k in range(2):
                fw.op("pe", lambda e, ps=ps, k=k, n=n: e.matmul(ps[:, :n], lhsT=ones[:], rhs=sqb[:, k, :n], start=(k == 0), stop=(k == 1)),
                      reads=[tw1, tg], writes=[tps])
            rstd_from_ps(fw, rs, trs, ps, tps, n, 1.0 / 256, eps[:, 0:1], tg)
            for k in range(2):
                sg, tsg = stg.next()
                fw.op("dve", lambda e, sg=sg, k=k, n=n: e.scalar_tensor_tensor(out=sg[:, :n], in0=w2[:, k, :n], scalar=og[:, k:k + 1], in1=rs[:, :n],
                                                                             op0=ALU.mult, op1=ALU.mult), reads=[tw1, trs, tg], writes=[tsg])
                fw.dma("sp", S5T[k * 128:(k + 1) * 128, lo:hi], sg[:, :n], reads=[tsg], writes=[tout])
        fw.barrier()


RW_BASE = 1024
CHK = 64
NCH = T // CHK
LN_EPS_RW = 64e-5


def rw_consts():
    idx = np.arange(64)
    m = np.zeros((64, 4, 64), np.float32)
    m[:, 0, :] = (idx[:, None] < idx[None, :])
    m[:, 1, :] = (idx[:, None] > idx[None, :])
    m[:, 2, :] = (idx[:, None] <= idx[None, :])
    m[:, 3, :] = (idx[:, None] >= idx[None, :])
    bd = np.zeros((128, 128), np.float32)
    bd[:64, :64] = 1.0
    bd[64:, 64:] = 1.0
    return m, bd


def stage_p4(fw, io):
    ZT = io("ZT", [2048, TE], F32, "in")
    mu = io("rw_mu", [960], F32, "in")
    w0 = io("rw_w0", [2, 256], F32, "in")
    w2 = io("rw_w2", [2, 32, 256], F32, "in")
    a0 = io("rw_a0", [2, 256], F32, "in")
    a2 = io("rw_a2", [2, 32, 256], F32, "in")
    g2 = io("rw_g2", [64, 256], F32, "in")
    k_k = io("rw_k_k", [256], F32, "in")
    k_a = io("rw_k_a", [256], F32, "in")
    r_k = io("rw_r_k", [256], F32, "in")
    ln_g = io("rw_ln_g", [256], F32, "in")
    ln_b = io("rw_ln_b", [256], F32, "in")
    ident = io("ident", [128, 128], F32, "in")
    MASKS = io("RWMASK", [64, 4, 64], F32, "in")
    BD = io("RWBD", [128, 128], F32, "in")
    RWT = io("RWT", [256, T], F32, "out")
    N = T
    CH5 = [(0, 512), (512, 1024), (1024, 1536), (1536, 2048), (2048, 2560)]
    with ExitStack() as es:
        P = Pool_(fw, es)
        tc = Tok()
        idt = P.sb([128, 128], F32)
        idb = P.sb([128, 128], BF16)
        msk = P.sb([64, 4, 64], F32)
        bd1 = P.sb([128, 128], F32)
        fw.dma("sp", idt[:], ident, writes=[tc])
        fw.dma("sp", msk[:], MASKS, writes=[tc])
        fw.dma("sp", bd1[:], BD, writes=[tc])
        fw.op("dve", lambda e: e.tensor_copy(out=idb[:], in_=idt[:]), reads=[tc], writes=[tc])
        mrep = P.sb([64, 4, 4, 64], F32)
        for rep in range(4):
            fw.op("dve", lambda e, rep=rep: e.tensor_copy(out=mrep[:, :, rep, :], in_=msk[:]), reads=[tc], writes=[tc])
        pp = P.sb([128, 12, 2], F32)
        tpp = Tok()
        srcs = [w0[0], w0[1], a0[0], a0[1], k_k, k_a, k_a, r_k, ln_g, ln_b]
        for i, sap in enumerate(srcs):
            fw.dma("sp", pp[:, i, :], sap.rearrange("(c p) -> p c", p=128), writes=[tpp], allow_slow_non_contiguous=True)
        fw.op("dve", lambda e: e.tensor_scalar(out=pp[:, 6, :], in0=pp[:, 6, :], scalar1=-1.0, scalar2=1.0, op0=ALU.mult, op1=ALU.add),
              reads=[tpp], writes=[tpp])
        epsl = P.sb([128, 2], F32)
        fw.op("pool", lambda e: e.memset(epsl[:, 0:1], LN_EPS_RW), writes=[tpp])
        fw.op("pool", lambda e: e.memset(epsl[:, 1:2], 1e-24), writes=[tpp])
        wA = P.sb([128, 256], BF16)
        wB = P.sb([128, 256], BF16)
        tlw = Tok()
        fw.dma("pool", wA[0:32, :], w2[0], writes=[tlw])
        fw.dma("pool", wA[32:64, :], w2[1], writes=[tlw])
        fw.dma("pool", wA[64:96, :], a2[0], writes=[tlw])
        fw.dma("pool", wB[0:32, :], a2[1], writes=[tlw])
        fw.dma("pool", wB[64:128, :], g2, writes=[tlw])
        smask = P.sb([128, N], BF16)
        tsm = Tok()
        fw.op("pool", lambda e: e.memset(smask[:], 1.0), writes=[tsm])
        fw.op("pool", lambda e: e.memset(smask[:].rearrange("p (c j) -> p c j", j=CHK)[:, :, 0], 0.0), writes=[tsm])

        def load_shift(dst, tdst, row0, rows, P2, post=None, pb=0):
            zt_ = P2.sb([128, TE], F32)
            nbt_ = P2.sb([128, TE], F32)
            mtt_ = P2.sb([128, 2], F32)
            tz, tnb, tm = Tok(), Tok(), Tok()
            ps_ = slice(pb, pb + rows)
            z = zt_[ps_, :]
            fw.dma("sp", z, ZT[RW_BASE + row0:RW_BASE + row0 + rows, :], writes=[tz])
            fw.dma("sp", mtt_[ps_, 0:1], mu[row0:row0 + rows].rearrange("(p o) -> p o", o=1), writes=[tm])
            fw.op("dve", lambda e: e.tensor_scalar(out=mtt_[ps_, 1:2], in0=mtt_[ps_, 0:1], scalar1=0.5, scalar2=None, op0=ALU.mult), reads=[tm], writes=[tm])
            fw.op("dve", lambda e: e.tensor_scalar(out=mtt_[ps_, 0:1], in0=mtt_[ps_, 0:1], scalar1=-1.0, scalar2=1.0, op0=ALU.mult, op1=ALU.add),
                  reads=[tm], writes=[tm])
            fw.op("pool", lambda e: e.memset(nbt_[ps_, 0:1], 0.0), writes=[tnb])
            fw.op("pool", lambda e: e.tensor_copy(out=nbt_[ps_, 1:TE], in_=zt_[ps_, 0:TE - 1]), reads=[tz], writes=[tnb])
            fw.op("pool", lambda e: e.tensor_tensor(out=nbt_[ps_, 0:TE - 1], in0=nbt_[ps_, 0:TE - 1], in1=zt_[ps_, 1:TE], op=ALU.add),
                  reads=[tz, tnb], writes=[tnb])
            for cb in (256, 2304):
                fw.op("pool", lambda e, cb=cb: e.tensor_tensor(out=nbt_[ps_, cb:cb + 1], in0=nbt_[ps_, cb:cb + 1], in1=zt_[ps_, cb - 1:cb], op=ALU.subtract),
                      reads=[tz, tnb], writes=[tnb])
                fw.op("pool", lambda e, cb=cb: e.tensor_tensor(out=nbt_[ps_, cb - 1:cb], in0=nbt_[ps_, cb - 1:cb], in1=zt_[ps_, cb:cb + 1], op=ALU.subtract),
                      reads=[tz, tnb], writes=[tnb])
            fw.op("act", lambda e: e.activation(out=z, in_=z, func=AF.Identity, scale=mtt_[ps_, 0:1]), reads=[tz, tm], writes=[tz])
            if post is None:
                fw.op("dve", lambda e: e.scalar_tensor_tensor(out=dst, in0=nbt_[ps_, :], scalar=mtt_[ps_, 1:2], in1=z, op0=ALU.mult, op1=ALU.add),
                      reads=[tz, tnb, tm], writes=[tdst])
            else:
                fw.op("dve", lambda e: e.scalar_tensor_tensor(out=z, in0=nbt_[ps_, :], scalar=mtt_[ps_, 1:2], in1=z, op0=ALU.mult, op1=ALU.add),
                      reads=[tz, tnb, tm], writes=[tz])
                fw.op("act", lambda e: e.activation(out=dst, in_=z, func=post), reads=[tz], writes=[tdst])

        lorA = P.sb([128, TE], BF16)
        lorB = P.sb([128, TE], BF16)
        tlor = Tok()
        for i in range(4):
            with ExitStack() as es2:
                dstt = lorA[32 * i:32 * i + 32, :] if i < 3 else lorB[0:32, :]
                load_shift(dstt, tlor, 768 + 32 * i, 32, Pool_(fw, es2), post=(AF.Tanh if i < 2 else AF.Copy), pb=(32 * i if i < 3 else 0))
                fw.barrier()
        with ExitStack() as es2:
            load_shift(lorB[64:128, :], tlor, 896, 64, Pool_(fw, es2), post=AF.Sigmoid, pb=64)
            fw.barrier()

        for c in range(2):
            with ExitStack() as esc:
                Pc = Pool_(fw, esc)
                rr = Pc.sb([128, TE], F32)
                kx = Pc.sb([128, TE], F32)
                vv = Pc.sb([128, TE], F32)
                kk = Pc.sb([128, TE], F32)
                trr, tkx, tvv, tkk = Tok(), Tok(), Tok(), Tok()
                vtok = Pc.sb([64, TE // CHK, 128], BF16)
                tvt = Tok()
                yacc = Pc.sb([128, T], F32)
                bon = Pc.sb([128, T], F32)
                tya, tbon = Tok(), Tok()
                fw.op("pool", lambda e: e.memset(yacc[:], 0.0), writes=[tya])
                fw.op("pool", lambda e: e.memset(bon[:], 0.0), writes=[tbon])
                for (dst, tdst, r0) in ((rr, trr, 0), (kx, tkx, 256), (vv, tvv, 512)):
                    with ExitStack() as es2:
                        load_shift(dst[:], tdst, r0 + c * 128, 128, Pool_(fw, es2))
                        fw.barrier()
                with ExitStack() as es2:
                    P2 = Pool_(fw, es2)
                    sq = P2.sb([128, 512], F32)
                    rn = P2.sb([128, 512], F32)
                    tsq, trn = Tok(), Tok()
                    ps1 = P2.ps()
                    tps1 = Tok()
                    fw.op("dve", lambda e: e.tensor_scalar(out=kk[:], in0=kx[:], scalar1=pp[:, 4, c:c + 1], scalar2=None, op0=ALU.mult),
                          reads=[tkx, tpp], writes=[tkk])
                    for (c0, c1) in CH5:
                        fw.op("act", lambda e, c0=c0, c1=c1: e.activation(out=sq[:], in_=kk[:, c0:c1], func=AF.Square), reads=[tkk], writes=[tsq])
                        fw.op("pe", lambda e: e.matmul(ps1[:], lhsT=bd1[:], rhs=sq[:], start=True, stop=True), reads=[tsq, tc], writes=[tps1])
                        rstd_from_ps(fw, rn, trn, ps1, tps1, 512, 1.0, epsl[:, 1:2], tpp)
                        fw.op("dve", lambda e, c0=c0, c1=c1: e.tensor_tensor(out=kk[:, c0:c1], in0=kk[:, c0:c1], in1=rn[:], op=ALU.mult),
                              reads=[tkk, trn], writes=[tkk])
                    vb = P2.sb([128, TE], BF16)
                    tvb = Tok()
                    fw.op("act", lambda e: e.copy(out=vb[:], in_=vv[:]), reads=[tvv], writes=[tvb])
                    pst = P2.ps([128, 1024], BF16)
                    tpst = Tok()
                    for q in range(TE // CHK // 4):
                        for j in range(4):
                            ch = q * 4 + j
                            fw.op("pe", lambda e, j=j, ch=ch: e.transpose(pst[0:64, j * 128:(j + 1) * 128], vb[:, ch * CHK:(ch + 1) * CHK], idb[:]),
                                  reads=[tvb, tc], writes=[tpst])
                        fw.op("dve", lambda e, q=q: e.tensor_copy(out=vtok[:, q * 4:(q + 1) * 4, :], in_=pst[0:64, 0:512].rearrange("p (j c) -> p j c", j=4)),
                              reads=[tpst], writes=[tvt])
                    fw.barrier()

                for di in range(2):
                    off = 0 if di == 0 else 256
                    with ExitStack() as esd:
                        Pd = Pool_(fw, esd)
                        aT = Pd.sb([128, N], BF16)
                        bT = Pd.sb([128, N], BF16)
                        kT = Pd.sb([128, N], BF16)
                        rT = Pd.sb([128, N], BF16)
                        btok = Pd.sb([64, NCH, 128], BF16)
                        ktok = Pd.sb([64, NCH, 128], BF16)
                        pC = Pd.sb([128, NCH], F32)
                        tops = Tok()
                        ttok = Tok()
                        with ExitStack() as es2:
                            P2 = Pool_(fw, es2)
                            ld = P2.sb([128, TE], F32)
                            kd = P2.sb([128, TE], F32)
                            bb = P2.sb([128, TE], F32)
                            tld, tkd, tbb = Tok(), Tok(), Tok()
                            psr = Rot([P2.ps() for _ in range(3)])
                            tm5 = Rot([P2.sb([128, 512], F32) for _ in range(2)])
                            for (c0, c1) in CH5:
                                ps, tps = psr.next()
                                fw.op("pe", lambda e, ps=ps, c0=c0, c1=c1: e.matmul(ps[:], lhsT=wA[32 * di:32 * di + 32, c * 128:(c + 1) * 128], rhs=lorA[32 * di:32 * di + 32, c0:c1],
                                                                                  start=True, stop=True), reads=[tlw, tlor], writes=[tps])
                                fw.op("act", lambda e, ps=ps, c0=c0, c1=c1: e.activation(out=ld[:, c0:c1], in_=ps[:], func=AF.Sigmoid, bias=pp[:, di, c:c + 1]),
                                      reads=[tps, tpp], writes=[tld])
                                ps, tps = psr.next()
                                fw.op("pe", lambda e, ps=ps, c0=c0, c1=c1: e.matmul(ps[:], lhsT=(wA[64:96, c * 128:(c + 1) * 128] if di == 0 else wB[0:32, c * 128:(c + 1) * 128]),
                                                                                  rhs=(lorA[64:96, c0:c1] if di == 0 else lorB[0:32, c0:c1]),
                                                                                  start=True, stop=True), reads=[tlw, tlor], writes=[tps])
                                fw.op("act", lambda e, ps=ps, c0=c0, c1=c1: e.activation(out=bb[:, c0:c1], in_=ps[:], func=AF.Sigmoid, bias=pp[:, 2 + di, c:c + 1]),
                                      reads=[tps, tpp], writes=[tbb])
                            fw.op("pool", lambda e: e.tensor_scalar(out=ld[:], in0=ld[:], scalar1=-math.exp(-0.5), scalar2=None, op0=ALU.mult), reads=[tld], writes=[tld])
                            fw.op("act", lambda e: e.activation(out=kd[:], in_=bb[:], func=AF.Identity, scale=pp[:, 5, c:c + 1], bias=pp[:, 6, c:c + 1]),
                                  reads=[tbb, tpp], writes=[tkd])
                            fw.op("dve", lambda e: e.tensor_tensor(out=kd[:], in0=kd[:], in1=kx[:], op=ALU.mult), reads=[tkd, tkx], writes=[tkd])
                            fw.op("pool", lambda e: e.tensor_tensor(out=bb[:], in0=bb[:], in1=kk[:], op=ALU.mult), reads=[tbb, tkk], writes=[tbb])
                            for (c0, c1) in CH5:
                                c1 = min(c1, T)
                                n = c1 - c0
                                tm, ttm = tm5.next()
                                fw.op("dve", lambda e, tm=tm, c0=c0, c1=c1, n=n: e.scalar_tensor_tensor(out=tm[:, :n], in0=rr[:, c0:c1], scalar=pp[:, 7, c:c + 1], in1=kd[:, c0:c1],
                                                                                                 op0=ALU.mult, op1=ALU.mult), reads=[trr, tkd, tpp], writes=[ttm])
                                ps, tps = psr.next()
                                fw.op("pe", lambda e, ps=ps, tm=tm, n=n: e.matmul(ps[:, :n], lhsT=bd1[:], rhs=tm[:, :n], start=True, stop=True), reads=[ttm, tc], writes=[tps])
                                tm2, ttm2 = tm5.next()
                                fw.op("dve", lambda e, tm2=tm2, ps=ps, c0=c0, c1=c1, n=n: e.tensor_tensor(out=tm2[:, :n], in0=ps[:, :n], in1=vv[:, c0:c1], op=ALU.mult),
                                      reads=[tps, tvv], writes=[ttm2])
                                fw.op("pool", lambda e, tm2=tm2, c0=c0, c1=c1, n=n: e.tensor_tensor(out=bon[:, c0:c1], in0=bon[:, c0:c1], in1=tm2[:, :n], op=ALU.add),
                                      reads=[ttm2, tbon], writes=[tbon])
                            cs = P2.sb([128, N], F32)
                            ex = P2.sb([128, N], F32)
                            tcs, tex = Tok(), Tok()
                            ldw = ld[:, off:off + N]
                            fw.op("dve", lambda e: e.tensor_tensor_scan(out=cs[:], data0=smask[:], data1=ldw, initial=0.0, op0=ALU.mult, op1=ALU.add),
                                  reads=[tsm, tld], writes=[tcs])
                            csv = cs[:].rearrange("p (c j) -> p c j", j=CHK)
                            tot = P2.sb([128, NCH, 1], F32)
                            ttot = Tok()
                            fw.op("dve", lambda e: e.tensor_copy(out=tot[:], in_=csv[:, :, CHK - 1:CHK]), reads=[tcs], writes=[ttot])
                            totb = tot[:].to_broadcast([128, NCH, CHK])
                            if di == 1:
                                fw.op("dve", lambda e: e.tensor_tensor(out=csv, in0=totb, in1=csv, op=ALU.subtract), reads=[ttot, tcs], writes=[tcs])
                                fw.op("dve", lambda e: e.tensor_tensor(out=cs[:], in0=cs[:], in1=ldw, op=ALU.add), reads=[tcs, tld], writes=[tcs])
                            fw.op("act", lambda e: e.activation(out=pC[:], in_=tot[:, :, 0], func=AF.Exp), reads=[ttot], writes=[tops])
                            kdw, bbw = kd[:, off:off + N], bb[:, off:off + N]
                            rrw, kkw = rr[:, off:off + N], kk[:, off:off + N]
                            fw.op("act", lambda e: e.activation(out=ex[:], in_=cs[:], func=AF.Exp), reads=[tcs], writes=[tex])
                            fw.op("dve", lambda e: e.tensor_tensor(out=rT[:], in0=rrw, in1=ex[:], op=ALU.mult), reads=[trr, tex], writes=[tops])
                            fw.op("act", lambda e: e.activation(out=ex[:], in_=cs[:], func=AF.Exp, scale=-1.0), reads=[tcs, tops], writes=[tex])
                            fw.op("dve", lambda e: e.tensor_tensor(out=bT[:], in0=bbw, in1=ex[:], op=ALU.mult), reads=[tbb, tex], writes=[tops])
                            fw.op("pool", lambda e: e.tensor_tensor(out=kT[:], in0=kdw, in1=ex[:], op=ALU.mult), reads=[tkd, tex], writes=[tops])
                            e3 = ex
                            te3 = tex
                            fw.op("dve", lambda e: e.tensor_tensor(out=e3[:], in0=cs[:], in1=ldw, op=ALU.subtract), reads=[tcs, tld], writes=[te3])
                            fw.op("act", lambda e: e.activation(out=e3[:], in_=e3[:], func=AF.Exp), reads=[te3], writes=[te3])
                            fw.op("dve", lambda e: e.scalar_tensor_tensor(out=aT[:], in0=kkw, scalar=-1.0, in1=e3[:], op0=ALU.mult, op1=ALU.mult),
                                  reads=[tkk, te3], writes=[tops])
                            e3v = e3[:].rearrange("p (c j) -> p c j", j=CHK)
                            fw.op("dve", lambda e: e.tensor_tensor(out=e3v, in0=totb, in1=csv, op=ALU.subtract), reads=[ttot, tcs, tops, te3], writes=[te3])
                            fw.op("act", lambda e: e.activation(out=e3[:], in_=e3[:], func=AF.Exp), reads=[te3], writes=[te3])
                            bh = P2.sb([128, N], BF16)
                            kh = P2.sb([128, N], BF16)
                            tbh = Tok()
                            fw.op("dve", lambda e: e.tensor_tensor(out=bh[:], in0=bbw, in1=e3[:], op=ALU.mult), reads=[tbb, te3], writes=[tbh])
                            fw.op("pool", lambda e: e.tensor_tensor(out=kh[:], in0=kdw, in1=e3[:], op=ALU.mult), reads=[tkd, te3], writes=[tbh])
                            pst = P2.ps([128, 1024], BF16)
                            tpst = Tok()
                            for (src, dstt) in ((bh, btok), (kh, ktok)):
                                for q in range(NCH // 4):
                                    for j in range(4):
                                        ch = q * 4 + j
                                        fw.op("pe", lambda e, j=j, ch=ch, src=src: e.transpose(pst[0:64, j * 128:(j + 1) * 128], src[:, ch * CHK:(ch + 1) * CHK], idb[:]),
                                              reads=[tbh, tc], writes=[tpst])
                                    fw.op("act", lambda e, q=q, dstt=dstt: e.copy(out=dstt[:, q * 4:(q + 1) * 4, :], in_=pst[0:64, 0:512].rearrange("p (j c) -> p j c", j=4)),
                                          reads=[tpst], writes=[ttok])
                            fw.barrier()
                        with ExitStack() as es3:
                            P3 = Pool_(fw, es3)
                            Hs = P3.sb([128, 64], F32)
                            Hb = P3.sb([128, 64], BF16)
                            tH = Tok()
                            fw.op("pool", lambda e: e.memset(Hs[:], 0.0), writes=[tH])
                            fw.op("pool", lambda e: e.memset(Hb[:], 0.0), writes=[tH])
                            psA = Rot([P3.ps([64, 512]) for _ in range(1)])
                            psB = Rot([P3.ps([64, 512]) for _ in range(1)])
                            psI = Rot([P3.ps([64, 512]) for _ in range(1)])
                            psG = Rot([P3.ps([64, 512]) for _ in range(1)])
                            psH = Rot([P3.ps([128, 512]) for _ in range(1)])
                            psY = Rot([P3.ps([128, 512]) for _ in range(1)])
                            g1b = Rot([P3.sb([64, 2, 64], BF16) for _ in range(2)])
                            nl = Rot([P3.sb([64, 2, 2, 64], F32) for _ in range(2)])
                            nl2 = Rot([P3.sb([64, 2, 2, 64], F32) for _ in range(2)])
                            g2b_ = Rot([P3.sb([64, 4, 64], BF16) for _ in range(2)])
                            Pm = Rot([P3.sb([64, 2, 64], F32) for _ in range(2)])
                            Gs = Rot([P3.sb([64, 128], F32) for _ in range(2)])
                            Ub = Rot([P3.sb([64, 128], BF16) for _ in range(2)])
                            Ys = Rot([P3.sb([64, 128], F32) for _ in range(2)])
                            ms, ml, mi = (0, 1, 2) if di == 0 else (1, 0, 3)
                            order = range(NCH) if di == 0 else range(NCH - 1, -1, -1)
                            for i in order:
                                cc0, cc1 = i * CHK, (i + 1) * CHK
                                gch = i + off // CHK
                                pa, tpa = psA.next()
                                pb, tpb = psB.next()
                                for h in range(2):
                                    hp = slice(h * 64, (h + 1) * 64)
                                    for (dst, lt, rt_) in ((pa[:, h * 64:(h + 1) * 64], kT, aT), (pa[:, 128 + h * 64:128 + (h + 1) * 64], bT, aT),
                                                           (pa[:, 256 + h * 64:256 + (h + 1) * 64], aT, bT)):
                                        fw.op("pe", lambda e, dst=dst, lt=lt, rt_=rt_, hp=hp, cc0=cc0, cc1=cc1: e.matmul(dst, lhsT=lt[hp, cc0:cc1], rhs=rt_[hp, cc0:cc1], start=True, stop=True),
                                              reads=[tops], writes=[tpa])
                                    for (dst, lt, rt_) in ((pb[:, h * 64:(h + 1) * 64], bT, rT), (pb[:, 128 + h * 64:128 + (h + 1) * 64], kT, rT)):
                                        fw.op("pe", lambda e, dst=dst, lt=lt, rt_=rt_, hp=hp, cc0=cc0, cc1=cc1: e.matmul(dst, lhsT=lt[hp, cc0:cc1], rhs=rt_[hp, cc0:cc1], start=True, stop=True),
                                              reads=[tops], writes=[tpb])
                                a1, ta1 = g1b.next()
                                nlt, tnl = nl.next()
                                a45, ta45 = g2b_.next()
                                fw.op("dve", lambda e, a1=a1, pa=pa: e.tensor_tensor(out=a1[:], in0=pa[:, 0:128].rearrange("p (h t) -> p h t", h=2), in1=mrep[:, ms, 0:2, :], op=ALU.mult),
                                      reads=[tpa, tc], writes=[ta1])
                                fw.op("dve", lambda e, nlt=nlt, pa=pa: e.tensor_tensor(out=nlt[:, 0], in0=pa[:, 128:256].rearrange("p (h t) -> p h t", h=2), in1=mrep[:, ms, 0:2, :], op=ALU.mult),
                                      reads=[tpa, tc], writes=[tnl])
                                fw.op("dve", lambda e, nlt=nlt, pa=pa: e.tensor_tensor(out=nlt[:, 1], in0=pa[:, 256:384].rearrange("p (h t) -> p h t", h=2), in1=mrep[:, ml, 0:2, :], op=ALU.mult),
                                      reads=[tpa, tc], writes=[tnl])
                                fw.op("dve", lambda e, a45=a45, pb=pb: e.tensor_tensor(out=a45[:], in0=pb[:, 0:256].rearrange("p (h t) -> p h t", h=4), in1=mrep[:, mi, :, :], op=ALU.mult),
                                      reads=[tpb, tc], writes=[ta45])
                                pm, tpm = Pm.next()
                                fw.op("dve", lambda e, pm=pm, nlt=nlt: e.tensor_tensor(out=pm[:], in0=nlt[:, 0], in1=idt[0:64, 0:64].unsqueeze(1).to_broadcast([64, 2, 64]), op=ALU.add),
                                      reads=[tnl, tc], writes=[tpm])
                                cur, tcur = nlt, tnl
                                for lev in range(5):
                                    pi_, tpi = psI.next()
                                    for h in range(2):
                                        fw.op("pe", lambda e, pi_=pi_, h=h, cur=cur: e.matmul(pi_[:, h * 64:(h + 1) * 64], lhsT=cur[:, 0, h, :], rhs=cur[:, 1, h, :], start=True, stop=True),
                                              reads=[tcur], writes=[tpi])
                                        fw.op("pe", lambda e, pi_=pi_, h=h, cur=cur: e.matmul(pi_[:, 128 + h * 64:128 + (h + 1) * 64], lhsT=cur[:, 1, h, :], rhs=cur[:, 0, h, :], start=True, stop=True),
                                              reads=[tcur], writes=[tpi])
                                    nxt, tnxt = (nl2.next() if lev % 2 == 0 else nl.next())
                                    fw.op("act", lambda e, nxt=nxt, pi_=pi_: e.copy(out=nxt[:, 1], in_=pi_[:, 0:128].rearrange("p (h t) -> p h t", h=2)), reads=[tpi], writes=[tnxt])
                                    fw.op("act", lambda e, nxt=nxt, pi_=pi_: e.copy(out=nxt[:, 0], in_=pi_[:, 128:256].rearrange("p (h t) -> p h t", h=2)), reads=[tpi], writes=[tnxt])
                                    for h in range(2):
                                        fw.op("pe", lambda e, pi_=pi_, h=h, nxt=nxt, pm=pm: e.matmul(pi_[:, 256 + h * 64:256 + (h + 1) * 64], lhsT=nxt[:, 1, h, :], rhs=pm[:, h, :], start=True, stop=True),
                                              reads=[tnxt, tpm], writes=[tpi])
                                    fw.op("dve", lambda e, pm=pm, pi_=pi_: e.tensor_tensor(out=pm[:], in0=pm[:], in1=pi_[:, 256:384].rearrange("p (h t) -> p h t", h=2), op=ALU.add),
                                          reads=[tpi, tpm], writes=[tpm])
                                    cur, tcur = nxt, tnxt
                                pg, tpg = psG.next()
                                for h in range(2):
                                    hp = slice(h * 64, (h + 1) * 64)
                                    fw.op("pe", lambda e, pg=pg, h=h, hp=hp, cc0=cc0, cc1=cc1: e.matmul(pg[:, h * 64:(h + 1) * 64], lhsT=aT[hp, cc0:cc1], rhs=Hb[hp, :], start=True, stop=False),
                                          reads=[tops, tH], writes=[tpg])
                                    fw.op("pe", lambda e, pg=pg, h=h, hp=hp, a1=a1, gch=gch: e.matmul(pg[:, h * 64:(h + 1) * 64], lhsT=a1[:, h, :], rhs=vtok[:, gch, hp], start=False, stop=True),
                                          reads=[ta1, tvt], writes=[tpg])
                                gs, tgs = Gs.next()
                                fw.op("act", lambda e, gs=gs, pg=pg: e.copy(out=gs[:], in_=pg[:, 0:128]), reads=[tpg], writes=[tgs])
                                for h in range(2):
                                    fw.op("pe", lambda e, pg=pg, h=h, pm=pm, gs=gs: e.matmul(pg[:, 128 + h * 64:128 + (h + 1) * 64], lhsT=pm[:, h, :], rhs=gs[:, h * 64:(h + 1) * 64], start=True, stop=True),
                                          reads=[tpm, tgs], writes=[tpg])
                                ub, tub = Ub.next()
                                fw.op("dve", lambda e, ub=ub, pg=pg: e.tensor_copy(out=ub[:], in_=pg[:, 128:256]), reads=[tpg], writes=[tub])
                                ph, tph = psH.next()
                                py, tpy = psY.next()
                                for h in range(2):
                                    hp = slice(h * 64, (h + 1) * 64)
                                    fw.op("pe", lambda e, ph=ph, h=h, hp=hp, ub=ub, i=i: e.matmul(ph[hp, 0:64], lhsT=btok[:, i, hp], rhs=ub[:, hp], start=True, stop=False),
                                          reads=[ttok, tub], writes=[tph])
                                    fw.op("pe", lambda e, ph=ph, h=h, hp=hp, i=i, gch=gch: e.matmul(ph[hp, 0:64], lhsT=ktok[:, i, hp], rhs=vtok[:, gch, hp], start=False, stop=True),
                                          reads=[ttok, tvt], writes=[tph])
                                    fw.op("pe", lambda e, py=py, h=h, hp=hp, cc0=cc0, cc1=cc1: e.matmul(py[0:64, hp], lhsT=rT[hp, cc0:cc1], rhs=Hb[hp, :], start=True, stop=False),
                                          reads=[tops, tH], writes=[tpy])
                                    fw.op("pe", lambda e, py=py, h=h, hp=hp, a45=a45, ub=ub: e.matmul(py[0:64, hp], lhsT=a45[:, h, :], rhs=ub[:, hp], start=False, stop=False),
                                          reads=[ta45, tub], writes=[tpy])
                                    fw.op("pe", lambda e, py=py, h=h, hp=hp, a45=a45, gch=gch: e.matmul(py[0:64, hp], lhsT=a45[:, 2 + h, :], rhs=vtok[:, gch, hp], start=False, stop=True),
                                          reads=[ta45, tvt], writes=[tpy])
                                fw.op("dve", lambda e, ph=ph, i=i: e.scalar_tensor_tensor(out=Hs[:], in0=Hs[:], scalar=pC[:, i:i + 1], in1=ph[:, 0:64], op0=ALU.mult, op1=ALU.add),
                                      reads=[tph, tH, tops, tpy], writes=[tH])
                                fw.op("act", lambda e: e.copy(out=Hb[:], in_=Hs[:]), reads=[tH, tpy, tpg], writes=[tH])
                                ys, tys = Ys.next()
                                fw.op("act", lambda e, ys=ys, py=py: e.copy(out=ys[:], in_=py[0:64, 0:128]), reads=[tpy], writes=[tys])
                                fw.op("pe", lambda e, py=py, ys=ys: e.transpose(py[:, 256:320], ys[:], idt[0:64, 0:64]), reads=[tys, tc], writes=[tpy])
                                y0 = off + cc0
                                if y0 >= T:
                                    y0 -= T
                                fw.op("dve", lambda e, py=py, y0=y0: e.tensor_tensor(out=yacc[:, y0:y0 + CHK], in0=yacc[:, y0:y0 + CHK], in1=py[:, 256:320], op=ALU.add),
                                      reads=[tpy, tya], writes=[tya])
                            fw.barrier()
                with ExitStack() as es4:
                    P4 = Pool_(fw, es4)
                    psr = Rot([P4.ps() for _ in range(3)])
                    xc = P4.sb([128, 512], F32)
                    sq = P4.sb([128, 512], F32)
                    rs = P4.sb([128, 512], F32)
                    txc, tsq, trs = Tok(), Tok(), Tok()
                    stg = Rot([P4.sb([128, 512], F32) for _ in range(2)])
                    tout = Tok()
                    for (lo, hi, ic) in SEGS:
                        n = hi - lo
                        ps, tps = psr.next()
                        fw.op("pe", lambda e, ps=ps, lo=lo, hi=hi, n=n: e.matmul(ps[:, :n], lhsT=bd1[:], rhs=yacc[:, lo:hi], start=True, stop=True), reads=[tya, tc], writes=[tps])
                        fw.op("dve", lambda e, ps=ps, lo=lo, hi=hi, n=n: e.scalar_tensor_tensor(out=xc[:, :n], in0=ps[:, :n], scalar=-1.0 / 64, in1=yacc[:, lo:hi], op0=ALU.mult, op1=ALU.add),
                              reads=[tps, tya], writes=[txc])
                        fw.op("act", lambda e, n=n: e.activation(out=sq[:, :n], in_=xc[:, :n], func=AF.Square), reads=[txc], writes=[tsq])
                        ps2, tps2 = psr.next()
                        fw.op("pe", lambda e, ps2=ps2, n=n: e.matmul(ps2[:, :n], lhsT=bd1[:], rhs=sq[:, :n], start=True, stop=True), reads=[tsq, tc], writes=[tps2])
                        rstd_from_ps(fw, rs, trs, ps2, tps2, n, 1.0 / 64, epsl[:, 0:1], tpp)
                        fw.op("dve", lambda e, n=n: e.tensor_tensor(out=xc[:, :n], in0=xc[:, :n], in1=rs[:, :n], op=ALU.mult), reads=[txc, trs], writes=[txc])
                        fw.op("act", lambda e, n=n: e.activation(out=xc[:, :n], in_=xc[:, :n], func=AF.Identity, scale=pp[:, 8, c:c + 1], bias=pp[:, 9, c:c + 1]),
                              reads=[txc, tpp], writes=[txc])
                        fw.op("pool", lambda e, lo=lo, hi=hi, n=n: e.tensor_tensor(out=xc[:, :n], in0=xc[:, :n], in1=bon[:, lo:hi], op=ALU.add), reads=[txc, tbon], writes=[txc])
                        ps3, tps3 = psr.next()
                        fw.op("pe", lambda e, ps3=ps3, lo=lo, hi=hi, n=n: e.matmul(ps3[:, :n], lhsT=wB[64:128, c * 128:(c + 1) * 128], rhs=lorB[64:128, lo:hi], start=True, stop=True),
                              reads=[tlw, tlor], writes=[tps3])
                        sg, tsg = stg.next()
                        fw.op("dve", lambda e, sg=sg, ps3=ps3, n=n: e.tensor_tensor(out=sg[:, :n], in0=ps3[:, :n], in1=xc[:, :n], op=ALU.mult), reads=[tps3, txc], writes=[tsg])
                        fw.dma("sp", RWT[c * 128:(c + 1) * 128, lo:hi], sg[:, :n], reads=[tsg], writes=[tout])
                    fw.barrier()
        fw.barrier()
NCORES = 8
DEPTH = 4
S5_KEYS = ["s5_a_re", "s5_a_im", "s5_log_step", "s5_b_re", "s5_b_im", "s5_c_re", "s5_c_im", "s5_d", "s5_glu_w", "s5_glu_b", "s5_out_g"]
RW_KEYS = ["rw_mu", "rw_w0", "rw_w2", "rw_a0", "rw_a2", "rw_g2", "rw_k_k", "rw_k_a", "rw_r_k", "rw_ln_g", "rw_ln_b"]
LAYER_KEYS = (["norm1_g", "norm2_g", "mod_w", "mod_b", "w_in", "w_out", "att_qn_g", "att_kn_g", "att_out_g",
               "ffn_up", "ffn_conv_w", "ffn_conv_b", "ffn_down"] + S5_KEYS + RW_KEYS)
SHARED_KEYS = ["c_ctx", "final_g", "ident", "POS", "CST", "RWMASK", "RWBD"]
PERCORE_KEYS = ["x_b", "ctx_b", "c_b"]
SCRATCH = {"ZT": [2048, TE], "VT": [T, 128], "MOD": [128, 48, 2], "S5T": [256, T], "ATT_T": [512, T], "RWT": [256, T],
           "XT1": [DM, T], "XA": [DM, T], "XB": [DM, T]}


def build_fused(depth=DEPTH):
    nc = bass.Bass("TRN2", target_bir_lowering=False)
    decl = {}

    def ext(name, shape, dt, kind):
        if name not in decl:
            decl[name] = nc.dram_tensor(name, list(shape), dt, kind=kind).ap()
        return decl[name]

    def make_io(l):
        xin = "XA" if l % 2 == 0 else "XB"
        xout = "XB" if l % 2 == 0 else "XA"

        def io(name, shape, dt, role):
            if name in LAYER_KEYS:
                full = ext(name, [DEPTH] + list(shape), dt, "ExternalInput")
                return full[l]
            if name in SHARED_KEYS or name in PERCORE_KEYS:
                return ext(name, shape, dt, "ExternalInput")
            if name == "OUT":
                return ext(name, shape, dt, "ExternalOutput")
            if name == "XT":
                name = xin
            elif name == "XT2":
                name = xout
            return ext(name, SCRATCH[name], dt, "Internal")
        return io
    with ExitStack() as es:
        fw = FW(nc, es)
        stage_p0(fw, make_io(0))
        for l in range(depth):
            io = make_io(l)
            for st in (stage_p1, stage_p2, stage_p3, stage_p4, stage_p5a, stage_p5b):
                st(fw, io)
        stage_p6(fw, make_io(depth))
        fw.barrier()
    return nc, fw


_FUSED = {}


def kernel(**inp):
    inp = {k: np.ascontiguousarray(np.asarray(v)) for k, v in inp.items()}
    if "nc" not in _FUSED:
        _FUSED["nc"], _FUSED["fw"] = build_fused()
    nc = _FUSED["nc"]
    pos, cst = host_consts()
    rwm, rwbd = rw_consts()
    shared = {k: inp[k] for k in LAYER_KEYS if k != "rw_r_k"}
    shared["rw_r_k"] = inp["rw_r_k"].reshape(DEPTH, 256)
    shared.update(c_ctx=inp["c_ctx"], final_g=inp["final_g"], ident=np.eye(128, dtype=np.float32),
                  POS=pos, CST=cst, RWMASK=rwm, RWBD=rwbd)
    in_maps = [dict(shared, x_b=inp["x"][b], ctx_b=inp["ctx"][b], c_b=inp["c"][b]) for b in range(NCORES)]
    res = run_bass_kernel_spmd(nc, in_maps, core_ids=list(range(NCORES)))
    return np.stack([res.results[b]["OUT"] for b in range(NCORES)], 0).astype(np.float32)
```

```python
import math
import numpy as np
from contextlib import ExitStack
import concourse.bass as bass
import concourse.mybir as mybir
from concourse.bass_utils import run_bass_kernel_spmd

F32 = mybir.dt.float32
F32R = mybir.dt.float32r
BF16 = mybir.dt.bfloat16
I32 = mybir.dt.int32
ALU = mybir.AluOpType
AF = mybir.ActivationFunctionType
AX = mybir.AxisListType

T = 2304
TE = 2560
NCTX = 256
NLAT = 2048
DM = 1024
SEGS = [(0, 256, 1), (256, 768, 0), (768, 1280, 0), (1280, 1792, 0), (1792, 2304, 0)]
RMS_EPS = 1e-6


class Tok:
    __slots__ = ("w", "r")

    def __init__(self):
        self.w = None
        self.r = {}


class FW:
    ENG = ("pe", "dve", "act", "pool", "sp")
    NDMA = 8

    def __init__(self, nc, es):
        self.nc = nc
        self.es = es
        self.eng = {"pe": nc.tensor, "dve": nc.vector, "act": nc.scalar,
                    "pool": nc.gpsimd, "sp": nc.sync}
        self.sem = {}
        self.cnt = {}
        for e in self.ENG:
            self.sem[e] = es.enter_context(nc.semaphore("s_" + e))
            self.cnt[e] = 0
        self.dq = {}
        for q in ("sp", "pool", "act"):
            ring = []
            for i in range(self.NDMA):
                k = "d_%s_%d" % (q, i)
                self.sem[k] = es.enter_context(nc.semaphore(k))
                self.cnt[k] = 0
                ring.append(k)
            self.dq[q] = [ring, 0]
        self.seen = {e: {} for e in self.ENG}
        self.ninst = 0
        self.uid = 0

    def name(self, p):
        self.uid += 1
        return "%s_%d" % (p, self.uid)

    def _deps(self, reads, writes):
        deps = {}
        for t in reads:
            if t.w is not None and deps.get(t.w[0], 0) < t.w[1]:
                deps[t.w[0]] = t.w[1]
        for t in writes:
            if t.w is not None and deps.get(t.w[0], 0) < t.w[1]:
                deps[t.w[0]] = t.w[1]
            for k, v in t.r.items():
                if deps.get(k, 0) < v:
                    deps[k] = v
        return deps

    def _wait(self, e, deps):
        seen = self.seen[e]
        for k, v in deps.items():
            if seen.get(k, 0) < v:
                self.eng[e].wait_ge(self.sem[k], v)
                seen[k] = v

    def op(self, e, fn, reads=(), writes=()):
        self._wait(e, self._deps(reads, writes))
        inst = fn(self.eng[e])
        self.cnt[e] += 1
        inst.then_inc(self.sem[e], 1)
        v = self.cnt[e]
        for t in reads:
            t.r[e] = v
        for t in writes:
            t.w = (e, v)
            t.r = {}
        self.ninst += 1
        return inst

    def dma(self, q, out, in_, reads=(), writes=(), **kw):
        ring, idx = self.dq[q]
        k = ring[idx % len(ring)]
        self.dq[q][1] = idx + 1
        deps = self._deps(reads, writes)
        if self.cnt[k] > 0:
            deps[k] = max(deps.get(k, 0), self.cnt[k])
        self._wait(q, deps)
        inst = self.eng[q].dma_start(out=out, in_=in_, **kw)
        self.cnt[k] += 16
        inst.then_inc(self.sem[k], 16)
        v = self.cnt[k]
        for t in reads:
            t.r[k] = v
        for t in writes:
            t.w = (k, v)
            t.r = {}
        self.ninst += 1
        return inst

    def barrier(self, engines=None):
        allv = {k: v for k, v in self.cnt.items() if v > 0}
        for e in (engines or self.ENG):
            self._wait(e, allv)


class Pool_:
    def __init__(self, fw, es):
        self.fw = fw
        self.es = es
        self.nc = fw.nc

    def sb(self, shape, dt, name="t"):
        return self.es.enter_context(self.nc.sbuf_tensor(self.fw.name(name), list(shape), dt))

    def ps(self, shape=(128, 512), dt=F32, name="ps"):
        return self.es.enter_context(self.nc.psum_tensor(self.fw.name(name), list(shape), dt))


class Rot:
    def __init__(self, bufs):
        self.bufs = bufs
        self.toks = [Tok() for _ in bufs]
        self.i = 0

    def next(self):
        j = self.i % len(self.bufs)
        self.i += 1
        return self.bufs[j], self.toks[j]


def load_w_bf16(fw, P, W, rows, cols, tok, name="w", q="pool", chunk=2048, stage=None):
    kt = rows // 128
    wb = P.sb([128, kt, cols], BF16, name)
    chunk = min(chunk, cols)
    st = stage or Rot([P.sb([128, chunk], F32, "wstg") for _ in range(3)])
    for k in range(kt):
        for c0 in range(0, cols, chunk):
            n = min(chunk, cols - c0)
            sg, tsg = st.next()
            fw.dma("sp", sg[:, :n], W[k * 128:(k + 1) * 128, c0:c0 + n], writes=[tsg])
            fw.op("pool", lambda e, sg=sg, k=k, c0=c0, n=n: e.tensor_copy(out=wb[:, k, c0:c0 + n], in_=sg[:, :n]), reads=[tsg], writes=[tok])
    return wb


def stage_p0(fw, io):
    nc = fw.nc
    xb = io("x_b", [NLAT, DM], F32, "in")
    cb = io("ctx_b", [NCTX, DM], F32, "in")
    ident = io("ident", [128, 128], F32, "in")
    XT = io("XT", [DM, T], F32, "out")
    with ExitStack() as es:
        P = Pool_(fw, es)
        idt = P.sb([128, 128], F32)
        tid = Tok()
        fw.dma("sp", idt[:], ident, writes=[tid])
        xt = P.sb([128, 8, T], F32)
        txt = Tok()
        xin = Rot([P.sb([128, DM], F32) for _ in range(3)])
        pss = Rot([P.ps() for _ in range(4)])
        for tt in range(18):
            src = cb[tt * 128:(tt + 1) * 128, :] if tt < 2 else xb[(tt - 2) * 128:(tt - 1) * 128, :]
            xi, txi = xin.next()
            fw.dma("sp", xi[:], src, writes=[txi])
            for half in range(2):
                ps, tps = pss.next()
                for k in range(4):
                    kk = half * 4 + k
                    fw.op("pe", lambda e, ps=ps, k=k, kk=kk, xi=xi: e.transpose(
                        ps[:, k * 128:(k + 1) * 128], xi[:, kk * 128:(kk + 1) * 128], idt[:]),
                        reads=[txi, tid], writes=[tps])
                eng = "dve" if half == 0 else "act"
                outap = xt[:, half * 4:half * 4 + 4, tt * 128:(tt + 1) * 128]
                inap = ps[:].rearrange("p (k t) -> p k t", k=4)
                if eng == "dve":
                    fw.op("dve", lambda e, o=outap, i=inap: e.tensor_copy(out=o, in_=i), reads=[tps], writes=[txt])
                else:
                    fw.op("act", lambda e, o=outap, i=inap: e.copy(out=o, in_=i), reads=[tps], writes=[txt])
        tout = Tok()
        for k in range(8):
            fw.dma("sp", XT[k * 128:(k + 1) * 128, :], xt[:, k, :], reads=[txt], writes=[tout])
        fw.barrier()


def make_AB(fw, P, MODs, tmod, g_ap, sh_base, sc_base):
    g = P.sb([128, 8], F32)
    tg = Tok()
    fw.dma("sp", g[:], g_ap.rearrange("(k p) -> p k", p=128), writes=[tg], allow_slow_non_contiguous=True)
    AB = P.sb([128, 2, 2, 8], F32)
    tab = Tok()
    for ic in range(2):
        fw.op("dve", lambda e, ic=ic: e.tensor_scalar(out=AB[:, ic, 0, :], in0=MODs[:, sc_base:sc_base + 8, ic],
                                                      scalar1=1.0, scalar2=None, op0=ALU.add),
              reads=[tmod], writes=[tab])
        fw.op("dve", lambda e, ic=ic: e.tensor_tensor(out=AB[:, ic, 0, :], in0=AB[:, ic, 0, :], in1=g[:], op=ALU.mult),
              reads=[tg, tab], writes=[tab])
        fw.op("dve", lambda e, ic=ic: e.tensor_copy(out=AB[:, ic, 1, :], in_=MODs[:, sh_base:sh_base + 8, ic]),
              reads=[tmod], writes=[tab])
    return AB, tab


def norm_mod_seg(fw, P, st, xs, txs, n, ic, AB, tab, outs, touts):
    sq, ones, tones, psr, rs, tmpr = st["sq"], st["ones"], st["tones"], st["psr"], st["rs"], st["tmpr"]
    tsq, trs = st["tsq"], st["trs"]
    fw.op("act", lambda e: e.activation(out=sq[:, :, :n], in_=xs[:, :, :n], func=AF.Square), reads=[txs], writes=[tsq])
    ps, tps = psr.next()
    for k in range(8):
        fw.op("pe", lambda e, k=k: e.matmul(ps[:, :n], lhsT=ones[:], rhs=sq[:, k, :n], start=(k == 0), stop=(k == 7)),
              reads=[tsq, tones], writes=[tps])
    fw.op("act", lambda e: e.activation(out=rs[:, :n], in_=ps[:, :n], func=AF.Ln, scale=1.0 / DM, bias=st["eps"][:, 0:1]),
          reads=[tps, st["teps"]], writes=[trs])
    fw.op("act", lambda e: e.activation(out=rs[:, :n], in_=rs[:, :n], func=AF.Exp, scale=-0.5), reads=[trs], writes=[trs])
    for k in range(8):
        tmp, ttmp = tmpr.next()
        fw.op("dve", lambda e, k=k, tmp=tmp: e.tensor_tensor(out=tmp[:, :n], in0=xs[:, k, :n], in1=rs[:, :n], op=ALU.mult),
              reads=[txs, trs], writes=[ttmp])
        for o in outs(k):
            fw.op("act", lambda e, k=k, tmp=tmp, o=o: e.activation(out=o, in_=tmp[:, :n], func=AF.Identity,
                                                                   scale=AB[:, ic, 0, k:k + 1], bias=AB[:, ic, 1, k:k + 1]),
                  reads=[ttmp, tab], writes=touts)


def norm_state(fw, P):
    st = {}
    st["sq"] = P.sb([128, 8, 512], BF16)
    st["tsq"] = Tok()
    st["ones"] = P.sb([128, 128], BF16)
    st["tones"] = Tok()
    fw.op("pool", lambda e: e.memset(st["ones"][:], 1.0), writes=[st["tones"]])
    st["eps"] = P.sb([128, 1], F32)
    st["teps"] = Tok()
    fw.op("pool", lambda e: e.memset(st["eps"][:], RMS_EPS), writes=[st["teps"]])
    st["psr"] = Rot([P.ps() for _ in range(2)])
    st["rs"] = P.sb([128, 512], F32)
    st["trs"] = Tok()
    st["tmpr"] = Rot([P.sb([128, 512], F32) for _ in range(2)])
    return st


def stage_p1(fw, io):
    XT = io("XT", [DM, T], F32, "in")
    c_b = io("c_b", [DM], F32, "in")
    c_ctx = io("c_ctx", [DM], F32, "in")
    mod_w = io("mod_w", [DM, 6 * DM], F32, "in")
    mod_b = io("mod_b", [6 * DM], F32, "in")
    n1g = io("norm1_g", [DM], F32, "in")
    w_in = io("w_in", [DM, 1984], F32, "in")
    MOD = io("MOD", [128, 48, 2], F32, "out")
    ZT = io("ZT", [2048, TE], F32, "out")
    VT = io("VT", [T, 128], F32, "out")
    XTv = XT.rearrange("(k p) t -> p k t", p=128)
    with ExitStack() as es:
        P = Pool_(fw, es)
        wstage = Rot([P.sb([128, 2048], F32, "wstg") for _ in range(3)])
        tmw = Tok()
        cc = P.sb([128, 8, 2], F32)
        tcc = Tok()
        fw.dma("sp", cc[:, :, 0], c_b.rearrange("(k p) -> p k", p=128), writes=[tcc], allow_slow_non_contiguous=True)
        fw.dma("sp", cc[:, :, 1], c_ctx.rearrange("(k p) -> p k", p=128), writes=[tcc], allow_slow_non_contiguous=True)
        scb = P.sb([128, 8, 2], BF16)
        tscb = Tok()
        fw.op("act", lambda e: e.activation(out=scb[:], in_=cc[:], func=AF.Silu), reads=[tcc], writes=[tscb])
        mb = P.sb([128, 48], F32)
        tmb = Tok()
        fw.dma("sp", mb[:], mod_b.rearrange("(j p) -> p j", p=128), writes=[tmb], allow_slow_non_contiguous=True)
        MODs = P.sb([128, 48, 2], F32)
        tmod = Tok()
        with ExitStack() as es2:
            P2 = Pool_(fw, es2)
            mwb = load_w_bf16(fw, P2, mod_w, DM, 6 * DM, tmw, "modw", stage=wstage)
            psm = P2.ps([128, 512])
            tpsm = Tok()
            for j in range(48):
                for k in range(8):
                    fw.op("pe", lambda e, j=j, k=k: e.matmul(psm[:, 2 * j:2 * j + 2], lhsT=mwb[:, k, j * 128:(j + 1) * 128],
                                                             rhs=scb[:, k, :], start=(k == 0), stop=(k == 7)),
                          reads=[tmw, tscb], writes=[tpsm])
            for ic in range(2):
                fw.op("dve", lambda e, ic=ic: e.tensor_tensor(
                    out=MODs[:, :, ic], in0=psm[:, 0:96].rearrange("p (j c) -> p j c", c=2)[:, :, ic], in1=mb[:], op=ALU.add),
                    reads=[tpsm, tmb], writes=[tmod])
            fw.barrier()
        tmo = Tok()
        fw.dma("sp", MOD, MODs[:], reads=[tmod], writes=[tmo])
        AB, tab = make_AB(fw, P, MODs, tmod, n1g, 0, 8)
        tw = Tok()
        wb = load_w_bf16(fw, P, w_in, DM, 1984, tw, "win", stage=wstage)
        hT = P.sb([128, 8, TE], BF16)
        thT = Tok()
        st = norm_state(fw, P)
        xr = Rot([P.sb([128, 8, 512], F32) for _ in range(2)])
        for (lo, hi, ic) in SEGS:
            n = hi - lo
            xs, txs = xr.next()
            fw.dma("sp", xs[:, :, :n], XTv[:, :, lo:hi], writes=[txs])

            def outs(k, lo=lo, hi=hi, ic=ic):
                o = [hT[:, k, lo:hi]]
                if ic:
                    o.append(hT[:, k, T + lo:T + hi])
                return o
            norm_mod_seg(fw, P, st, xs, txs, n, ic, AB, tab, outs, [thT])
        psr = Rot([P.ps() for _ in range(4)])
        stg = Rot([P.sb([128, 512], F32) for _ in range(4)])
        tz = Tok()
        cnt = 0
        for nt in range(16):
            if nt == 7:
                continue
            M = 64 if nt == 15 else 128
            for cc_ in range(5):
                c0 = cc_ * 512
                ps, tps = psr.next()
                for k in range(8):
                    fw.op("pe", lambda e, ps=ps, k=k, nt=nt, M=M, c0=c0: e.matmul(
                        ps[0:M, :], lhsT=wb[:, k, nt * 128:nt * 128 + M], rhs=hT[:, k, c0:c0 + 512],
                        start=(k == 0), stop=(k == 7)), reads=[tw, thT], writes=[tps])
                sg, tsg = stg.next()
                if cnt % 2 == 0:
                    fw.op("dve", lambda e, sg=sg, ps=ps, M=M: e.tensor_copy(out=sg[0:M, :], in_=ps[0:M, :]), reads=[tps], writes=[tsg])
                else:
                    fw.op("act", lambda e, sg=sg, ps=ps, M=M: e.copy(out=sg[0:M, :], in_=ps[0:M, :]), reads=[tps], writes=[tsg])
                cnt += 1
                fw.dma("sp", ZT[nt * 128:nt * 128 + M, c0:c0 + 512], sg[0:M, :], reads=[tsg], writes=[tz])
        for tt in range(18):
            ps, tps = psr.next()
            for k in range(8):
                fw.op("pe", lambda e, ps=ps, k=k, tt=tt: e.matmul(
                    ps[:, 0:128], lhsT=hT[:, k, tt * 128:(tt + 1) * 128], rhs=wb[:, k, 896:1024],
                    start=(k == 0), stop=(k == 7)), reads=[tw, thT], writes=[tps])
            sg, tsg = stg.next()
            fw.op("dve", lambda e, sg=sg, ps=ps: e.tensor_copy(out=sg[:, 0:128], in_=ps[:, 0:128]), reads=[tps], writes=[tsg])
            fw.dma("sp", VT[tt * 128:(tt + 1) * 128, :], sg[:, 0:128], reads=[tsg], writes=[tz])
        fw.barrier()


def build_program(stage_fns):
    nc = bass.Bass("TRN2", target_bir_lowering=False)
    decl = {}

    def io(name, shape, dt, role):
        if name in decl:
            return decl[name][0]
        kind = "ExternalInput" if role == "in" else "ExternalOutput"
        ap = nc.dram_tensor(name, list(shape), dt, kind=kind).ap()
        decl[name] = (ap, role, shape)
        return ap
    with ExitStack() as es:
        fw = FW(nc, es)
        for fn in stage_fns:
            fn(fw, io)
        fw.barrier()
    return nc, decl, fw


_PROG_CACHE = {}


def run_stage(key, stage_fns, in_maps, ncores):
    if key not in _PROG_CACHE:
        _PROG_CACHE[key] = build_program(stage_fns)
    nc, decl, fw = _PROG_CACHE[key]
    res = run_bass_kernel_spmd(nc, in_maps, core_ids=list(range(ncores)))
    return res.results


def rstd_from_ps(fw, rs, trs, ps, tps, n, scale, epsap, teps, rows=128):
    fw.op("act", lambda e: e.activation(out=rs[0:rows, :n], in_=ps[0:rows, :n], func=AF.Ln, scale=scale, bias=epsap),
          reads=[tps, teps], writes=[trs])
    fw.op("act", lambda e: e.activation(out=rs[0:rows, :n], in_=rs[0:rows, :n], func=AF.Exp, scale=-0.5), reads=[trs], writes=[trs])


def stage_p3(fw, io):
    ZT = io("ZT", [2048, TE], F32, "in")
    VT = io("VT", [T, 128], F32, "in")
    qn_g = io("att_qn_g", [64], F32, "in")
    kn_g = io("att_kn_g", [64], F32, "in")
    og = io("att_out_g", [512], F32, "in")
    POS = io("POS", [128, NLAT], F32, "in")
    CST = io("CST", [128, 260], F32, "in")
    ATT = io("ATT_T", [512, T], F32, "out")
    with ExitStack() as es:
        P = Pool_(fw, es)
        cst = P.sb([128, 260], F32)
        tc = Tok()
        fw.dma("sp", cst[:], CST, writes=[tc])
        permb = P.sb([128, 128], BF16)
        bdb = P.sb([128, 128], BF16)
        fw.op("dve", lambda e: e.tensor_copy(out=permb[:], in_=cst[:, 1:129]), reads=[tc], writes=[tc])
        fw.op("dve", lambda e: e.tensor_copy(out=bdb[:], in_=cst[:, 129:257]), reads=[tc], writes=[tc])
        eps = P.sb([128, 2], F32)
        teps = Tok()
        fw.op("pool", lambda e: e.memset(eps[:, 0:1], RMS_EPS), writes=[teps])
        fw.op("pool", lambda e: e.memset(eps[:, 1:2], 0.0), writes=[teps])
        gq = P.sb([128, 2], F32)
        tg = Tok()
        for h in range(2):
            fw.dma("sp", gq[h * 64:(h + 1) * 64, 0:1], qn_g.rearrange("(p o) -> p o", o=1), writes=[tg])
            fw.dma("sp", gq[h * 64:(h + 1) * 64, 1:2], kn_g.rearrange("(p o) -> p o", o=1), writes=[tg])
        cos = P.sb([128, NLAT], F32)
        sin = P.sb([128, NLAT], F32)
        ttab = Tok()
        with ExitStack() as es2:
            P2 = Pool_(fw, es2)
            ang = P2.sb([128, NLAT], F32)
            tmpf = P2.sb([128, NLAT], F32)
            tmpi = P2.sb([128, NLAT], I32)
            ta = Tok()
            fw.dma("sp", ang[:], POS, writes=[ta])
            fw.op("dve", lambda e: e.tensor_scalar(out=ang[:], in0=ang[:], scalar1=cst[:, 0:1], scalar2=None, op0=ALU.mult),
                  reads=[ta, tc], writes=[ta])
            for (tab, off) in ((sin, 0.0), (cos, math.pi / 2)):
                fw.op("dve", lambda e, off=off: e.tensor_scalar(out=tmpi[:], in0=ang[:], scalar1=off, scalar2=1.0 / (2 * math.pi),
                                                                op0=ALU.add, op1=ALU.mult), reads=[ta], writes=[ta])
                fw.op("dve", lambda e: e.tensor_copy(out=tmpf[:], in_=tmpi[:]), reads=[ta], writes=[ta])
                fw.op("dve", lambda e: e.scalar_tensor_tensor(out=tmpf[:], in0=tmpf[:], scalar=-2 * math.pi, in1=ang[:],
                                                              op0=ALU.mult, op1=ALU.add), reads=[ta], writes=[ta])
                fw.op("dve", lambda e, off=off: e.tensor_scalar(out=tmpf[:], in0=tmpf[:], scalar1=off, scalar2=math.pi,
                                                                op0=ALU.add, op1=ALU.min), reads=[ta], writes=[ta])
                fw.op("dve", lambda e: e.tensor_scalar(out=tmpf[:], in0=tmpf[:], scalar1=-math.pi, scalar2=None, op0=ALU.max),
                      reads=[ta], writes=[ta])
                fw.op("act", lambda e, tab=tab: e.activation(out=tab[:], in_=tmpf[:], func=AF.Sin), reads=[ta], writes=[ttab])
            fw.barrier()
        qb = P.sb([128, 4, T], BF16)
        kd = P.sb([128, 2, T], BF16)
        tq = Tok()
        with ExitStack() as es2:
            P2 = Pool_(fw, es2)
            raw = Rot([P2.sb([128, 512], F32) for _ in range(2)])
            sqr = Rot([P2.sb([128, 512], BF16) for _ in range(2)])
            psr = Rot([P2.ps() for _ in range(2)])
            psr2 = Rot([P2.ps() for _ in range(2)])
            rsr = Rot([P2.sb([128, 512], F32) for _ in range(2)])
            nbr = Rot([P2.sb([128, 512], BF16) for _ in range(2)])
            t1r = Rot([P2.sb([128, 512], F32) for _ in range(2)])
            t2r = Rot([P2.sb([128, 512], F32) for _ in range(2)])
            items = [("q", j) for j in range(4)] + [("k", g) for g in range(2)]
            for (kind, j) in items:
                for (lo, hi, ic) in SEGS:
                    n = hi - lo
                    rw, trw = raw.next()
                    if kind == "q":
                        fw.dma("sp", rw[:, :n], ZT[256 + j * 128:256 + (j + 1) * 128, lo:hi], writes=[trw])
                        gcol = 0
                        dst = qb[:, j, lo:hi]
                    else:
                        for h in range(2):
                            fw.dma("sp", rw[h * 64:(h + 1) * 64, :n], ZT[768 + j * 64:768 + (j + 1) * 64, lo:hi], writes=[trw])
                        gcol = 1
                        dst = kd[:, j, lo:hi]
                    sq, tsq = sqr.next()
                    fw.op("act", lambda e, sq=sq, rw=rw, n=n: e.activation(out=sq[:, :n], in_=rw[:, :n], func=AF.Square), reads=[trw], writes=[tsq])
                    ps, tps = psr.next()
                    fw.op("pe", lambda e, ps=ps, sq=sq, n=n: e.matmul(ps[:, :n], lhsT=bdb[:], rhs=sq[:, :n], start=True, stop=True),
                          reads=[tsq, tc], writes=[tps])
                    rs, trs = rsr.next()
                    rstd_from_ps(fw, rs, trs, ps, tps, n, 1.0, eps[:, 0:1], teps)
                    t1, tt1 = t1r.next()
                    fw.op("dve", lambda e, t1=t1, rw=rw, rs=rs, n=n, gcol=gcol: e.scalar_tensor_tensor(
                        out=t1[:, :n], in0=rw[:, :n], scalar=gq[:, gcol:gcol + 1], in1=rs[:, :n], op0=ALU.mult, op1=ALU.mult),
                        reads=[trw, trs, tg], writes=[tt1])
                    if ic:
                        fw.op("act", lambda e, dst=dst, t1=t1, n=n: e.copy(out=dst, in_=t1[:, :n]), reads=[tt1], writes=[tq])
                        continue
                    nb, tnb = nbr.next()
                    fw.op("act", lambda e, nb=nb, t1=t1, n=n: e.copy(out=nb[:, :n], in_=t1[:, :n]), reads=[tt1], writes=[tnb])
                    ps2, tps2 = psr2.next()
                    fw.op("pe", lambda e, ps2=ps2, nb=nb, n=n: e.matmul(ps2[:, :n], lhsT=permb[:], rhs=nb[:, :n], start=True, stop=True),
                          reads=[tnb, tc], writes=[tps2])
                    p0 = lo - NCTX
                    t2, tt2 = t2r.next()
                    fw.op("dve", lambda e, t2=t2, ps2=ps2, n=n, p0=p0: e.tensor_tensor(out=t2[:, :n], in0=ps2[:, :n], in1=sin[:, p0:p0 + n], op=ALU.mult),
                          reads=[tps2, ttab], writes=[tt2])
                    fw.op("pool", lambda e, t1=t1, nb=nb, n=n, p0=p0: e.tensor_tensor(out=t1[:, :n], in0=nb[:, :n], in1=cos[:, p0:p0 + n], op=ALU.mult),
                          reads=[tnb, ttab, tt1], writes=[tt1])
                    fw.op("pool", lambda e, dst=dst, t1=t1, t2=t2, n=n: e.tensor_tensor(out=dst, in0=t1[:, :n], in1=t2[:, :n], op=ALU.add),
                          reads=[tt1, tt2], writes=[tq])
            fw.barrier()
        va = P.sb([128, 18, 2, 128], BF16)
        tva = Tok()
        fw.op("pool", lambda e: e.memset(va[:], 1.0), writes=[tva])
        for g in range(2):
            fw.dma("pool", va[:, :, g, 0:64], VT.rearrange("(t p) c -> p t c", p=128)[:, :, g * 64:(g + 1) * 64], writes=[tva])
        att = P.sb([128, 4, T], F32)
        tatt = Tok()
        pss = Rot([P.ps() for _ in range(3)])
        pso = Rot([P.ps() for _ in range(2)])
        ptr = Rot([P.sb([128, 512], BF16) for _ in range(3)])
        rcr = Rot([P.sb([64, 512], F32) for _ in range(2)])
        jobs = [(0, 256, 0, 2)] + [(256 + 512 * i, 768 + 512 * i, 0, 18) for i in range(4)]
        for h in range(8):
            g = h // 4
            jt, r0 = h // 2, (h % 2) * 64
            for (qlo, qhi, k0, k1) in jobs:
                n = qhi - qlo
                po, tpo = pso.next()
                for kt in range(k0, k1):
                    ps, tps = pss.next()
                    fw.op("pe", lambda e, ps=ps, kt=kt, g=g, jt=jt, r0=r0, qlo=qlo, qhi=qhi, n=n: e.matmul(
                        ps[:, :n], lhsT=kd[r0:r0 + 64, g, kt * 128:(kt + 1) * 128], rhs=qb[r0:r0 + 64, jt, qlo:qhi],
                        start=True, stop=True), reads=[tq], writes=[tps])
                    pt, tpt = ptr.next()
                    fw.op("act", lambda e, pt=pt, ps=ps, n=n: e.activation(out=pt[:, :n], in_=ps[:, :n], func=AF.Exp, scale=0.125),
                          reads=[tps], writes=[tpt])
                    fw.op("pe", lambda e, po=po, pt=pt, kt=kt, g=g, n=n, k0=k0, k1=k1: e.matmul(
                        po[:, :n], lhsT=va[:, kt, g, :], rhs=pt[:, :n], start=(kt == k0), stop=(kt == k1 - 1)),
                        reads=[tpt, tva], writes=[tpo])
                rc, trc = rcr.next()
                fw.op("dve", lambda e, rc=rc, po=po, n=n: e.reciprocal(out=rc[:, :n], in_=po[64:128, :n]), reads=[tpo], writes=[trc])
                fw.op("dve", lambda e, rc=rc, po=po, n=n, jt=jt, r0=r0, qlo=qlo, qhi=qhi: e.tensor_tensor(
                    out=att[r0:r0 + 64, jt, qlo:qhi], in0=po[0:64, :n], in1=rc[:, :n], op=ALU.mult),
                    reads=[tpo, trc], writes=[tatt])
        ogs = P.sb([128, 4], F32)
        tog = Tok()
        fw.dma("sp", ogs[:], og.rearrange("(k p) -> p k", p=128), writes=[tog], allow_slow_non_contiguous=True)
        ones = P.sb([128, 128], BF16)
        fw.op("pool", lambda e: e.memset(ones[:], 1.0), writes=[tog])
        sq4 = P.sb([128, 4, 512], BF16)
        tsq4 = Tok()
        rs = P.sb([128, 512], F32)
        trs = Tok()
        stg = Rot([P.sb([128, 512], F32) for _ in range(3)])
        tout = Tok()
        for (lo, hi, ic) in SEGS:
            n = hi - lo
            fw.op("act", lambda e, lo=lo, hi=hi, n=n: e.activation(out=sq4[:, :, :n], in_=att[:, :, lo:hi], func=AF.Square), reads=[tatt], writes=[tsq4])
            ps, tps = pss.next()
            for k in range(4):
                fw.op("pe", lambda e, ps=ps, k=k, n=n: e.matmul(ps[:, :n], lhsT=ones[:], rhs=sq4[:, k, :n], start=(k == 0), stop=(k == 3)),
                      reads=[tsq4, tog], writes=[tps])
            rstd_from_ps(fw, rs, trs, ps, tps, n, 1.0 / 512, eps[:, 0:1], teps)
            for k in range(4):
                sg, tsg = stg.next()
                fw.op("dve", lambda e, sg=sg, k=k, lo=lo, hi=hi, n=n: e.scalar_tensor_tensor(
                    out=sg[:, :n], in0=att[:, k, lo:hi], scalar=ogs[:, k:k + 1], in1=rs[:, :n], op0=ALU.mult, op1=ALU.mult),
                    reads=[tatt, trs, tog], writes=[tsg])
                fw.dma("sp", ATT[k * 128:(k + 1) * 128, lo:hi], sg[:, :n], reads=[tsg], writes=[tout])
        fw.barrier()


def host_consts():
    pos = np.zeros((128, NLAT), np.float32)
    inv = np.zeros((128,), np.float32)
    tok = np.arange(NLAT)
    for p in range(128):
        d = p % 64
        pos[p] = (tok // 64) if d < 32 else (tok % 64)
        inv[p] = 10000.0 ** (-(d % 16) / 16.0)
    cst = np.zeros((128, 260), np.float32)
    cst[:, 0] = inv
    perm = np.zeros((128, 128), np.float32)
    for m in range(128):
        d = m % 32
        if d < 16:
            perm[m + 16, m] = -1.0
        else:
            perm[m - 16, m] = 1.0
    cst[:, 1:129] = perm
    bd = np.zeros((128, 128), np.float32)
    bd[:64, :64] = 1.0 / 64
    bd[64:, 64:] = 1.0 / 64
    cst[:, 129:257] = bd
    return pos, cst


def stage_p5a(fw, io):
    XT = io("XT", [DM, T], F32, "in")
    S5T = io("S5T", [256, T], F32, "in")
    ATT = io("ATT_T", [512, T], F32, "in")
    RWT = io("RWT", [256, T], F32, "in")
    MOD = io("MOD", [128, 48, 2], F32, "in")
    w_out = io("w_out", [DM, DM], F32, "in")
    XT1 = io("XT1", [DM, T], F32, "out")
    XTv = XT.rearrange("(k p) t -> p k t", p=128)
    with ExitStack() as es:
        P = Pool_(fw, es)
        MODs = P.sb([128, 48, 2], F32)
        tmod = Tok()
        fw.dma("sp", MODs[:], MOD, writes=[tmod])
        tw = Tok()
        wb = load_w_bf16(fw, P, w_out, DM, DM, tw, "wout")
        cat = P.sb([128, 8, T], BF16)
        tcat = Tok()
        cstg = Rot([P.sb([128, T], F32, "cstg") for _ in range(2)])
        for k in range(8):
            src = S5T[k * 128:(k + 1) * 128, :] if k < 2 else (ATT[(k - 2) * 128:(k - 1) * 128, :] if k < 6 else RWT[(k - 6) * 128:(k - 5) * 128, :])
            sg, tsg = cstg.next()
            fw.dma("sp", sg[:], src, writes=[tsg])
            fw.op("pool", lambda e, sg=sg, k=k: e.tensor_copy(out=cat[:, k, :], in_=sg[:]), reads=[tsg], writes=[tcat])
        xr = Rot([P.sb([128, 8, 512], F32) for _ in range(2)])
        x1r = Rot([P.sb([128, 8, 512], F32) for _ in range(2)])
        psr = Rot([P.ps() for _ in range(4)])
        to1, to2 = Tok(), Tok()
        for (lo, hi, ic) in SEGS:
            n = hi - lo
            xs, txs = xr.next()
            fw.dma("sp", xs[:, :, :n], XTv[:, :, lo:hi], writes=[txs])
            x1, tx1 = x1r.next()
            for d in range(8):
                ps, tps = psr.next()
                for k in range(8):
                    fw.op("pe", lambda e, ps=ps, k=k, d=d, lo=lo, hi=hi, n=n: e.matmul(
                        ps[:, :n], lhsT=wb[:, k, d * 128:(d + 1) * 128], rhs=cat[:, k, lo:hi], start=(k == 0), stop=(k == 7)),
                        reads=[tw, tcat], writes=[tps])
                fw.op("dve", lambda e, ps=ps, d=d, n=n, ic=ic, x1=x1, xs=xs: e.scalar_tensor_tensor(
                    out=x1[:, d, :n], in0=ps[:, :n], scalar=MODs[:, 16 + d, ic:ic + 1], in1=xs[:, d, :n], op0=ALU.mult, op1=ALU.add),
                    reads=[tps, txs, tmod], writes=[tx1])
            for k in range(8):
                fw.dma("sp", XT1[k * 128:(k + 1) * 128, lo:hi], x1[:, k, :n], reads=[tx1], writes=[to1])
        fw.barrier()


def stage_p5b(fw, io):
    XT1 = io("XT1", [DM, T], F32, "in")
    n2g = io("norm2_g", [DM], F32, "in")
    MOD = io("MOD", [128, 48, 2], F32, "in")
    up = io("ffn_up", [DM, 5632], F32, "in")
    cw = io("ffn_conv_w", [3, 5632], F32, "in")
    cb = io("ffn_conv_b", [5632], F32, "in")
    down = io("ffn_down", [2816, DM], F32, "in")
    XT2 = io("XT2", [DM, T], F32, "out")
    X1v = XT1.rearrange("(k p) t -> p k t", p=128)
    X2v = XT2.rearrange("(k p) t -> p k t", p=128)
    with ExitStack() as es:
        P = Pool_(fw, es)
        MODs = P.sb([128, 48, 2], F32)
        tmod = Tok()
        fw.dma("sp", MODs[:], MOD, writes=[tmod])
        cws = P.sb([128, 44, 3], F32)
        cbs = P.sb([128, 44], F32)
        tcw = Tok()
        for w in range(3):
            fw.dma("sp", cws[:, :, w], cw[w].rearrange("(j p) -> p j", p=128), writes=[tcw], allow_slow_non_contiguous=True)
        fw.dma("sp", cbs[:], cb.rearrange("(j p) -> p j", p=128), writes=[tcw], allow_slow_non_contiguous=True)
        h2 = P.sb([128, 8, T], BF16)
        th2 = Tok()
        AB, tab = make_AB(fw, P, MODs, tmod, n2g, 24, 32)
        with ExitStack() as es2:
            P2 = Pool_(fw, es2)
            st = norm_state(fw, P2)
            xr0 = Rot([P2.sb([128, 8, 512], F32) for _ in range(2)])
            for (lo, hi, ic) in SEGS:
                n = hi - lo
                xs, txs = xr0.next()
                fw.dma("sp", xs[:, :, :n], X1v[:, :, lo:hi], writes=[txs])
                norm_mod_seg(fw, P2, st, xs, txs, n, ic, AB, tab, lambda k, lo=lo, hi=hi: [h2[:, k, lo:hi]], [th2])
            fw.barrier()
        hid = P.sb([128, 11, T], BF16)
        thid = Tok()
        dwb = P.sb([128, 11, DM], BF16)
        tdw = Tok()
        u = [P.sb([128, T], F32) for _ in range(2)]
        y = [P.sb([128, T], F32) for _ in range(2)]
        tu = [Tok(), Tok()]
        ty = [Tok(), Tok()]
        psr = Rot([P.ps() for _ in range(4)])
        tx2 = Tok()
        RANGES = [(0, NCTX), (NCTX, T)]
        upv = up.rearrange("(k p) n -> p k n", p=128)
        GROUPS = [(0, 2), (2, 2), (4, 2), (6, 2), (8, 2), (10, 1)]
        for half in range(2):
            with ExitStack() as esu:
                Pu = Pool_(fw, esu)
                dstg = Rot([Pu.sb([128, DM], F32, "dstg") for _ in range(2)])
                for jj in range(11):
                    r0 = (half * 11 + jj) * 128
                    sg, tsg = dstg.next()
                    fw.dma("sp", sg[:], down[r0:r0 + 128, :], writes=[tsg])
                    fw.op("pool", lambda e, sg=sg, jj=jj: e.tensor_copy(out=dwb[:, jj, :], in_=sg[:]), reads=[tsg], writes=[tdw])
                ustg = Rot([Pu.sb([128, 8, 256], F32, "ustg") for _ in range(4)])
                ubr = Rot([Pu.sb([128, 8, 2, 256], BF16, "ub") for _ in range(2)])

                def issue_load(g, half=half):
                    jj0, ng = GROUPS[g]
                    ub, tub = ubr.next()
                    for wh in range(2):
                        sg, tsg = ustg.next()
                        c0 = wh * 2816 + (half * 11 + jj0) * 128
                        fw.dma("sp", sg[:, :, :ng * 128], upv[:, :, c0:c0 + ng * 128], writes=[tsg])
                        fw.op("pool", lambda e, sg=sg, ub=ub, wh=wh, ng=ng: e.tensor_copy(out=ub[:, :, wh, :ng * 128], in_=sg[:, :, :ng * 128]),
                              reads=[tsg], writes=[tub])
                    return ub, tub
                loaded = issue_load(0)
                for g, (jj0, ng) in enumerate(GROUPS):
                    ub, tub = loaded
                    if g + 1 < len(GROUPS):
                        loaded = issue_load(g + 1)
                    for jl in range(ng):
                        jj = jj0 + jl
                        j = half * 11 + jj
                        for wh in range(2):
                            jc = wh * 22 + j
                            for si, (lo, hi, ic) in enumerate(SEGS):
                                n = hi - lo
                                ps, tps = psr.next()
                                for k in range(8):
                                    fw.op("pe", lambda e, ps=ps, k=k, wh=wh, ub=ub, jl=jl, lo=lo, hi=hi, n=n: e.matmul(
                                        ps[:, :n], lhsT=ub[:, k, wh, jl * 128:(jl + 1) * 128], rhs=h2[:, k, lo:hi], start=(k == 0), stop=(k == 7)),
                                        reads=[tub, th2], writes=[tps])
                                fw.op("act", lambda e, ps=ps, wh=wh, lo=lo, hi=hi, n=n: e.copy(out=u[wh][:, lo:hi], in_=ps[:, :n]),
                                      reads=[tps], writes=[tu[wh]])
                            fw.op("act", lambda e, wh=wh, jc=jc: e.activation(out=y[wh][:], in_=u[wh][:], func=AF.Identity,
                                                                             scale=cws[:, jc, 1:2], bias=cbs[:, jc:jc + 1]),
                                  reads=[tu[wh], tcw], writes=[ty[wh]])
                            for (lo, hi) in RANGES:
                                fw.op("dve", lambda e, wh=wh, jc=jc, lo=lo, hi=hi: e.scalar_tensor_tensor(
                                    out=y[wh][:, lo + 1:hi], in0=u[wh][:, lo:hi - 1], scalar=cws[:, jc, 0:1], in1=y[wh][:, lo + 1:hi],
                                    op0=ALU.mult, op1=ALU.add), reads=[tu[wh], tcw, ty[wh]], writes=[ty[wh]])
                                fw.op("dve", lambda e, wh=wh, jc=jc, lo=lo, hi=hi: e.scalar_tensor_tensor(
                                    out=y[wh][:, lo:hi - 1], in0=u[wh][:, lo + 1:hi], scalar=cws[:, jc, 2:3], in1=y[wh][:, lo:hi - 1],
                                    op0=ALU.mult, op1=ALU.add), reads=[tu[wh], tcw, ty[wh]], writes=[ty[wh]])
                        fw.op("act", lambda e: e.activation(out=y[0][:], in_=y[0][:], func=AF.Silu), reads=[ty[0]], writes=[ty[0]])
                        fw.op("dve", lambda e, jj=jj: e.tensor_tensor(out=hid[:, jj, :], in0=y[0][:], in1=y[1][:], op=ALU.mult),
                              reads=[ty[0], ty[1]], writes=[thid])
                fw.barrier()
            with ExitStack() as esd:
                Pd = Pool_(fw, esd)
                xr = Rot([Pd.sb([128, 8, 512], F32) for _ in range(2)])
                for (lo, hi, ic) in SEGS:
                    n = hi - lo
                    xs, txs = xr.next()
                    src = X1v if half == 0 else X2v
                    fw.dma("sp", xs[:, :, :n], src[:, :, lo:hi], reads=([tx2] if half else []), writes=[txs])
                    for d in range(8):
                        ps, tps = psr.next()
                        for jj in range(11):
                            fw.op("pe", lambda e, ps=ps, jj=jj, d=d, lo=lo, hi=hi, n=n: e.matmul(
                                ps[:, :n], lhsT=dwb[:, jj, d * 128:(d + 1) * 128], rhs=hid[:, jj, lo:hi], start=(jj == 0), stop=(jj == 10)),
                                reads=[tdw, thid], writes=[tps])
                        fw.op("dve", lambda e, ps=ps, d=d, n=n, ic=ic, xs=xs: e.scalar_tensor_tensor(
                            out=xs[:, d, :n], in0=ps[:, :n], scalar=MODs[:, 40 + d, ic:ic + 1], in1=xs[:, d, :n], op0=ALU.mult, op1=ALU.add),
                            reads=[tps, tmod, txs], writes=[txs])
                    for k in range(8):
                        fw.dma("sp", XT2[k * 128:(k + 1) * 128, lo:hi], xs[:, k, :n], reads=[txs], writes=[tx2])
                fw.barrier()


def stage_p6(fw, io):
    XT = io("XT", [DM, T], F32, "in")
    fg = io("final_g", [DM], F32, "in")
    ident = io("ident", [128, 128], F32, "in")
    OUT = io("OUT", [NLAT, DM], F32, "out")
    XTv = XT.rearrange("(k p) t -> p k t", p=128)
    with ExitStack() as es:
        P = Pool_(fw, es)
        idt = P.sb([128, 128], F32)
        tid = Tok()
        fw.dma("sp", idt[:], ident, writes=[tid])
        g = P.sb([128, 8], F32)
        fw.dma("sp", g[:], fg.rearrange("(k p) -> p k", p=128), writes=[tid], allow_slow_non_contiguous=True)
        st = norm_state(fw, P)
        xr = Rot([P.sb([128, 8, 512], F32) for _ in range(2)])
        yr = Rot([P.sb([128, 8, 512], F32) for _ in range(2)])
        psr = Rot([P.ps() for _ in range(4)])
        orr = Rot([P.sb([128, DM], F32) for _ in range(3)])
        tout = Tok()
        for (lo, hi, ic) in SEGS[1:]:
            n = hi - lo
            xs, txs = xr.next()
            fw.dma("sp", xs[:, :, :n], XTv[:, :, lo:hi], writes=[txs])
            sq, ones = st["sq"], st["ones"]
            fw.op("act", lambda e, xs=xs: e.activation(out=sq[:], in_=xs[:], func=AF.Square), reads=[txs], writes=[st["tsq"]])
            ps, tps = st["psr"].next()
            for k in range(8):
                fw.op("pe", lambda e, ps=ps, k=k: e.matmul(ps[:], lhsT=ones[:], rhs=sq[:, k, :], start=(k == 0), stop=(k == 7)),
                      reads=[st["tsq"], st["tones"]], writes=[tps])
            rstd_from_ps(fw, st["rs"], st["trs"], ps, tps, n, 1.0 / DM, st["eps"][:, 0:1], st["teps"])
            ys, tys = yr.next()
            for k in range(8):
                fw.op("dve", lambda e, ys=ys, xs=xs, k=k: e.scalar_tensor_tensor(
                    out=ys[:, k, :], in0=xs[:, k, :], scalar=g[:, k:k + 1], in1=st["rs"][:], op0=ALU.mult, op1=ALU.mult),
                    reads=[txs, st["trs"], tid], writes=[tys])
            for blk in range(4):
                ot, tot = orr.next()
                for half in range(2):
                    ps2, tps2 = psr.next()
                    for k in range(4):
                        kk = half * 4 + k
                        fw.op("pe", lambda e, ps2=ps2, k=k, kk=kk, ys=ys, blk=blk: e.transpose(
                            ps2[:, k * 128:(k + 1) * 128], ys[:, kk, blk * 128:(blk + 1) * 128], idt[:]),
                            reads=[tys, tid], writes=[tps2])
                    if half == 0:
                        fw.op("dve", lambda e, ot=ot, ps2=ps2: e.tensor_copy(out=ot[:, 0:512], in_=ps2[:]), reads=[tps2], writes=[tot])
                    else:
                        fw.op("act", lambda e, ot=ot, ps2=ps2: e.copy(out=ot[:, 512:1024], in_=ps2[:]), reads=[tps2], writes=[tot])
                r0 = lo - NCTX + blk * 128
                fw.dma("sp", OUT[r0:r0 + 128, :], ot[:], reads=[tot], writes=[tout])
        fw.barrier()


def sin_reduced(fw, P, out, src, tsrc, shape, off, tout):
    ti = P.sb(shape, I32)
    tf = P.sb(shape, F32)
    tt = Tok()
    fw.op("dve", lambda e: e.tensor_scalar(out=ti[:], in0=src, scalar1=off, scalar2=1.0 / (2 * math.pi), op0=ALU.add, op1=ALU.mult),
          reads=[tsrc], writes=[tt])
    fw.op("dve", lambda e: e.tensor_copy(out=tf[:], in_=ti[:]), reads=[tt], writes=[tt])
    fw.op("dve", lambda e: e.scalar_tensor_tensor(out=tf[:], in0=tf[:], scalar=-2 * math.pi, in1=src, op0=ALU.mult, op1=ALU.add),
          reads=[tt, tsrc], writes=[tt])
    fw.op("dve", lambda e: e.tensor_scalar(out=tf[:], in0=tf[:], scalar1=off, scalar2=math.pi, op0=ALU.add, op1=ALU.min), reads=[tt], writes=[tt])
    fw.op("dve", lambda e: e.tensor_scalar(out=tf[:], in0=tf[:], scalar1=-math.pi, scalar2=None, op0=ALU.max), reads=[tt], writes=[tt])
    fw.op("act", lambda e: e.activation(out=out, in_=tf[:], func=AF.Sin), reads=[tt], writes=[tout])


def stage_p2(fw, io):
    ZT = io("ZT", [2048, TE], F32, "in")
    a_re = io("s5_a_re", [2, 16, 64], F32, "in")
    a_im = io("s5_a_im", [2, 16, 64], F32, "in")
    lstep = io("s5_log_step", [2, 16], F32, "in")
    b_re = io("s5_b_re", [2, 16, 64, 16], F32, "in")
    b_im = io("s5_b_im", [2, 16, 64, 16], F32, "in")
    c_re = io("s5_c_re", [2, 16, 16, 64], F32, "in")
    c_im = io("s5_c_im", [2, 16, 16, 64], F32, "in")
    dsk = io("s5_d", [256], F32, "in")
    glu_w = io("s5_glu_w", [256, 256], F32, "in")
    glu_b = io("s5_glu_b", [256], F32, "in")
    out_g = io("s5_out_g", [256], F32, "in")
    S5T = io("S5T", [256, T], F32, "out")
    N = T
    with ExitStack() as es:
        P = Pool_(fw, es)
        are = P.sb([128, 2, 8], F32)
        aim = P.sb([128, 2, 8], F32)
        lst = P.sb([128, 2, 8], F32)
        tpar = Tok()
        for di in range(2):
            fw.dma("sp", are[:, di, :], a_re[di].rearrange("(s g) p -> (g p) s", g=2), writes=[tpar], allow_slow_non_contiguous=True)
            fw.dma("sp", aim[:, di, :], a_im[di].rearrange("(s g) p -> (g p) s", g=2), writes=[tpar], allow_slow_non_contiguous=True)
            for g2 in range(2):
                fw.dma("sp", lst[g2 * 64:(g2 + 1) * 64, di:di + 1, :],
                       lstep[di].rearrange("(s g) -> g s", g=2)[g2:g2 + 1, :].partition_broadcast(64), writes=[tpar],
                       allow_slow_non_contiguous=True)
        sh = [128, 2, 8]
        step = P.sb(sh, F32)
        fw.op("act", lambda e: e.activation(out=step[:], in_=lst[:], func=AF.Exp), reads=[tpar], writes=[tpar])
        er = P.sb(sh, F32)
        th = P.sb(sh, F32)
        fw.op("dve", lambda e: e.tensor_tensor(out=er[:], in0=are[:], in1=step[:], op=ALU.mult), reads=[tpar], writes=[tpar])
        fw.op("act", lambda e: e.activation(out=er[:], in_=er[:], func=AF.Exp), reads=[tpar], writes=[tpar])
        fw.op("dve", lambda e: e.tensor_tensor(out=th[:], in0=aim[:], in1=step[:], op=ALU.mult), reads=[tpar], writes=[tpar])
        sn = P.sb(sh, F32)
        cs = P.sb(sh, F32)
        ttrig = Tok()
        sin_reduced(fw, P, sn[:], th[:], tpar, sh, 0.0, ttrig)
        sin_reduced(fw, P, cs[:], th[:], tpar, sh, math.pi / 2, ttrig)
        PW = P.sb([128, 9, 3, 2, 8], F32)
        tpw = Tok()
        fw.op("dve", lambda e: e.tensor_tensor(out=PW[:, 0, 0], in0=er[:], in1=cs[:], op=ALU.mult), reads=[tpar, ttrig], writes=[tpw])
        fw.op("dve", lambda e: e.tensor_tensor(out=PW[:, 0, 1], in0=er[:], in1=sn[:], op=ALU.mult), reads=[tpar, ttrig], writes=[tpw])
        t1 = P.sb(sh, F32)
        t2 = P.sb(sh, F32)
        for k in range(9):
            fw.op("dve", lambda e, k=k: e.tensor_scalar(out=PW[:, k, 2], in0=PW[:, k, 1], scalar1=-1.0, scalar2=None, op0=ALU.mult),
                  reads=[tpw], writes=[tpw])
            if k == 8:
                break
            fw.op("dve", lambda e, k=k: e.tensor_tensor(out=t1[:], in0=PW[:, k, 0], in1=PW[:, k, 0], op=ALU.mult), reads=[tpw], writes=[tpw])
            fw.op("dve", lambda e, k=k: e.tensor_tensor(out=t2[:], in0=PW[:, k, 1], in1=PW[:, k, 1], op=ALU.mult), reads=[tpw], writes=[tpw])
            fw.op("dve", lambda e, k=k: e.tensor_tensor(out=PW[:, k + 1, 0], in0=t1[:], in1=t2[:], op=ALU.subtract), reads=[tpw], writes=[tpw])
            fw.op("dve", lambda e, k=k: e.scalar_tensor_tensor(out=PW[:, k + 1, 1], in0=PW[:, k, 0], scalar=2.0, in1=PW[:, k, 1],
                                                               op0=ALU.mult, op1=ALU.mult), reads=[tpw], writes=[tpw])
        br = P.sb(sh, F32)
        bi = P.sb(sh, F32)
        nbi = P.sb(sh, F32)
        den = P.sb(sh, F32)
        nr = P.sb(sh, F32)
        tb = Tok()
        fw.op("dve", lambda e: e.tensor_tensor(out=den[:], in0=are[:], in1=are[:], op=ALU.mult), reads=[tpar], writes=[tb])
        fw.op("dve", lambda e: e.tensor_tensor(out=t1[:], in0=aim[:], in1=aim[:], op=ALU.mult), reads=[tpar, tpw], writes=[tpw])
        fw.op("dve", lambda e: e.tensor_tensor(out=den[:], in0=den[:], in1=t1[:], op=ALU.add), reads=[tb, tpw], writes=[tb])
        fw.op("dve", lambda e: e.reciprocal(out=den[:], in_=den[:]), reads=[tb], writes=[tb])
        fw.op("dve", lambda e: e.tensor_scalar(out=nr[:], in0=PW[:, 0, 0], scalar1=-1.0, scalar2=None, op0=ALU.add), reads=[tpw], writes=[tb])
        fw.op("dve", lambda e: e.tensor_tensor(out=t1[:], in0=nr[:], in1=are[:], op=ALU.mult), reads=[tb, tpar, tpw], writes=[tpw])
        fw.op("dve", lambda e: e.tensor_tensor(out=t2[:], in0=PW[:, 0, 1], in1=aim[:], op=ALU.mult), reads=[tpw, tpar], writes=[tpw])
        fw.op("dve", lambda e: e.tensor_tensor(out=t1[:], in0=t1[:], in1=t2[:], op=ALU.add), reads=[tpw], writes=[tpw])
        fw.op("dve", lambda e: e.tensor_tensor(out=br[:], in0=t1[:], in1=den[:], op=ALU.mult), reads=[tpw, tb], writes=[tb])
        fw.op("dve", lambda e: e.tensor_tensor(out=t1[:], in0=PW[:, 0, 1], in1=are[:], op=ALU.mult), reads=[tpw, tpar, tb], writes=[tpw])
        fw.op("dve", lambda e: e.tensor_tensor(out=t2[:], in0=nr[:], in1=aim[:], op=ALU.mult), reads=[tb, tpar, tpw], writes=[tpw])
        fw.op("dve", lambda e: e.tensor_tensor(out=t1[:], in0=t1[:], in1=t2[:], op=ALU.subtract), reads=[tpw], writes=[tpw])
        fw.op("dve", lambda e: e.tensor_tensor(out=bi[:], in0=t1[:], in1=den[:], op=ALU.mult), reads=[tpw, tb], writes=[tb])
        fw.op("dve", lambda e: e.tensor_scalar(out=nbi[:], in0=bi[:], scalar1=-1.0, scalar2=None, op0=ALU.mult), reads=[tb], writes=[tb])
        BTb = P.sb([128, 2, 2, 8, 128], BF16)
        CTb = P.sb([128, 2, 2, 8, 128], BF16)
        tBT = Tok()
        tCT = Tok()
        with ExitStack() as es2:
            P2 = Pool_(fw, es2)
            BTf = P2.sb([128, 2, 2, 8, 128], F32)
            CTf = P2.sb([128, 2, 2, 8, 128], F32)
            CT2 = P2.sb([128, 2, 2, 8, 128], F32)
            tf1, tf2 = Tok(), Tok()
            fw.op("pool", lambda e: e.memset(BTf[:], 0.0), writes=[tf1])
            fw.op("pool", lambda e: e.memset(CTf[:], 0.0), writes=[tf2])
            for di in range(2):
                for ri, (bsrc, csrc) in enumerate(((b_re, c_re), (b_im, c_im))):
                    for g in range(16):
                        s, g2 = g // 2, g % 2
                        r0 = (g % 8) * 16
                        fw.dma("sp", BTf[r0:r0 + 16, di, ri, s, g2 * 64:(g2 + 1) * 64], bsrc[di, g].rearrange("p h -> h p"),
                               writes=[tf1], allow_slow_non_contiguous=True)
                        fw.dma("sp", CTf[g2 * 64:(g2 + 1) * 64, di, ri, s, r0:r0 + 16], csrc[di, g].rearrange("h p -> p h"),
                               writes=[tf2], allow_slow_non_contiguous=True)
            fw.op("act", lambda e: e.copy(out=BTb[:], in_=BTf[:]), reads=[tf1], writes=[tBT])
            bsh = [128, 2, 8, 128]
            brb = br[:].unsqueeze(3).to_broadcast(bsh)
            nbib = nbi[:].unsqueeze(3).to_broadcast(bsh)
            tc2 = Tok()
            fw.op("dve", lambda e: e.tensor_tensor(out=CT2[:, :, 0], in0=CTf[:, :, 0], in1=brb, op=ALU.mult), reads=[tf2, tb], writes=[tc2])
            fw.op("pool", lambda e: e.tensor_tensor(out=CT2[:, :, 1], in0=CTf[:, :, 1], in1=nbib, op=ALU.mult), reads=[tf2, tb], writes=[tc2])
            fw.op("dve", lambda e: e.tensor_tensor(out=CT2[:, :, 0], in0=CT2[:, :, 0], in1=CT2[:, :, 1], op=ALU.add), reads=[tc2], writes=[tc2])
            fw.op("act", lambda e: e.copy(out=CTb[:, :, 0], in_=CT2[:, :, 0]), reads=[tc2], writes=[tCT])
            fw.op("dve", lambda e: e.tensor_tensor(out=CT2[:, :, 0], in0=CTf[:, :, 0], in1=nbib, op=ALU.mult), reads=[tf2, tb, tCT, tc2], writes=[tc2])
            fw.op("pool", lambda e: e.tensor_tensor(out=CT2[:, :, 1], in0=CTf[:, :, 1], in1=brb, op=ALU.mult), reads=[tf2, tb, tc2], writes=[tc2])
            fw.op("dve", lambda e: e.tensor_tensor(out=CT2[:, :, 0], in0=CT2[:, :, 0], in1=CT2[:, :, 1], op=ALU.subtract), reads=[tc2], writes=[tc2])
            fw.op("act", lambda e: e.copy(out=CTb[:, :, 1], in_=CT2[:, :, 0]), reads=[tc2], writes=[tCT])
            fw.barrier()
        ub = P.sb([128, 2, TE], BF16)
        tub = Tok()
        for ct in range(2):
            fw.dma("pool", ub[:, ct, :], ZT[ct * 128:(ct + 1) * 128, :], writes=[tub])
        dk = P.sb([128, 2], F32)
        tdk = Tok()
        fw.dma("sp", dk[:], dsk.rearrange("(c p) -> p c", p=128), writes=[tdk], allow_slow_non_contiguous=True)
        y = P.sb([128, 2, T], F32)
        ty = Tok()
        for ct in range(2):
            fw.dma("sp", y[:, ct, :], ZT[ct * 128:(ct + 1) * 128, 0:T], writes=[ty])
        for ct in range(2):
            fw.op("pool", lambda e, ct=ct: e.tensor_scalar(out=y[:, ct, :], in0=y[:, ct, :], scalar1=dk[:, ct:ct + 1], scalar2=None, op0=ALU.mult),
                  reads=[ty, tdk], writes=[ty])
        Xs = Rot([P.sb([128, 2, N], F32) for _ in range(2)])
        xbs = Rot([P.sb([128, 2, N], BF16) for _ in range(2)])
        psr = Rot([P.ps() for _ in range(4)])
        psy = Rot([P.ps() for _ in range(2)])
        tmpy = Rot([P.sb([128, 512], F32) for _ in range(2)])
        CH = [(0, 512), (512, 1024), (1024, 1536), (1536, 2048), (2048, 2304)]
        for di in range(2):
            off = 0 if di == 0 else 256
            for s in range(8):
                ct = s // 4
                X, tX = Xs.next()
                for ri in range(2):
                    for ci, (c0, c1) in enumerate(CH):
                        n = c1 - c0
                        ps, tps = psr.next()
                        fw.op("pe", lambda e, ps=ps, di=di, ri=ri, s=s, ct=ct, c0=c0, c1=c1, n=n, off=off: e.matmul(
                            ps[:, :n], lhsT=BTb[:, di, ri, s, :], rhs=ub[:, ct, off + c0:off + c1], start=True, stop=True),
                            reads=[tBT, tub], writes=[tps])
                        fw.op("act", lambda e, ps=ps, X=X, ri=ri, c0=c0, c1=c1, n=n: e.copy(out=X[:, ri, c0:c1], in_=ps[:, :n]),
                              reads=[tps], writes=[tX])

                def cstep(w_re, w_im, r_re, r_im, k, di=di, s=s, tX=tX):
                    pr = PW[:, k, 0, di, s:s + 1]
                    pi = PW[:, k, 1, di, s:s + 1]
                    npi = PW[:, k, 2, di, s:s + 1]
                    for (o, i0, sc) in ((w_re, r_re, pr), (w_re, r_im, npi), (w_im, r_im, pr), (w_im, r_re, pi)):
                        fw.op("dve", lambda e, o=o, i0=i0, sc=sc: e.scalar_tensor_tensor(out=o, in0=i0, scalar=sc, in1=o, op0=ALU.mult, op1=ALU.add),
                              reads=[tX, tpw], writes=[tX])
                for k in range(8):
                    st_ = 1 << k
                    Xv = [X[:, ri, :].rearrange("p (m c) -> p m c", c=2 * st_) for ri in range(2)]
                    if di == 0:
                        cstep(Xv[0][:, :, 2 * st_ - 1], Xv[1][:, :, 2 * st_ - 1], Xv[0][:, :, st_ - 1], Xv[1][:, :, st_ - 1], k)
                    else:
                        cstep(Xv[0][:, :, 0], Xv[1][:, :, 0], Xv[0][:, :, st_], Xv[1][:, :, st_], k)
                for i in (range(1, 9) if di == 0 else range(7, -1, -1)):
                    if di == 0:
                        w, r = 256 * i + 255, 256 * (i - 1) + 255
                    else:
                        w, r = 256 * i, 256 * (i + 1)
                    cstep(X[:, 0, w:w + 1], X[:, 1, w:w + 1], X[:, 0, r:r + 1], X[:, 1, r:r + 1], 8)
                for k in range(7, -1, -1):
                    st_ = 1 << k
                    Xv = [X[:, ri, :].rearrange("p (m c) -> p m c", c=2 * st_) for ri in range(2)]
                    if di == 0:
                        cstep(Xv[0][:, 1:, st_ - 1], Xv[1][:, 1:, st_ - 1], Xv[0][:, :-1, 2 * st_ - 1], Xv[1][:, :-1, 2 * st_ - 1], k)
                    else:
                        cstep(Xv[0][:, :-1, st_], Xv[1][:, :-1, st_], Xv[0][:, 1:, 0], Xv[1][:, 1:, 0], k)
                xb, txb = xbs.next()
                fw.op("act", lambda e, xb=xb, X=X: e.copy(out=xb[:], in_=X[:]), reads=[tX], writes=[txb])
                for (c0, c1) in CH:
                    n = c1 - c0
                    if di == 0:
                        y0 = c0
                    else:
                        y0 = c0 + 256 if c0 < 2048 else 0
                    ps, tps = psy.next()
                    for ri in range(2):
                        fw.op("pe", lambda e, ps=ps, xb=xb, di=di, ri=ri, s=s, c0=c0, c1=c1, n=n: e.matmul(
                            ps[:, :n], lhsT=CTb[:, di, ri, s, :], rhs=xb[:, ri, c0:c1], start=(ri == 0), stop=(ri == 1)),
                            reads=[tCT, txb], writes=[tps])
                    tm, ttm = tmpy.next()
                    fw.op("act", lambda e, tm=tm, ps=ps, n=n: e.copy(out=tm[:, :n], in_=ps[:, :n]), reads=[tps], writes=[ttm])
                    fw.op("pool", lambda e, tm=tm, ct=ct, y0=y0, n=n: e.tensor_tensor(out=y[:, ct, y0:y0 + n], in0=y[:, ct, y0:y0 + n], in1=tm[:, :n], op=ALU.add),
                          reads=[ttm, ty], writes=[ty])
        tg = Tok()
        gw = load_w_bf16(fw, P, glu_w, 256, 256, tg, "gluw")
        gb = P.sb([128, 2], F32)
        og = P.sb([128, 2], F32)
        fw.dma("sp", gb[:], glu_b.rearrange("(c p) -> p c", p=128), writes=[tg], allow_slow_non_contiguous=True)
        fw.dma("sp", og[:], out_g.rearrange("(c p) -> p c", p=128), writes=[tg], allow_slow_non_contiguous=True)
        ones = P.sb([128, 128], BF16)
        eps = P.sb([128, 1], F32)
        fw.op("pool", lambda e: e.memset(ones[:], 1.0), writes=[tg])
        fw.op("pool", lambda e: e.memset(eps[:], RMS_EPS), writes=[tg])
        a32 = P.sb([128, 2, 512], F32)
        ab = P.sb([128, 2, 512], BF16)
        w1 = P.sb([128, 2, 512], F32)
        w2 = P.sb([128, 2, 512], F32)
        sqb = P.sb([128, 2, 512], BF16)
        rs = P.sb([128, 512], F32)
        ta, tw1, trs = Tok(), Tok(), Tok()
        stg = Rot([P.sb([128, 512], F32) for _ in range(2)])
        tout = Tok()
        C1 = math.sqrt(2.0 / math.pi)
        for (lo, hi, ic) in SEGS:
            n = hi - lo
            yv = y[:, :, lo:hi]
            fw.op("act", lambda e, yv=yv, n=n: e.activation(out=w1[:, :, :n], in_=yv, func=AF.Square), reads=[ty], writes=[tw1])
            fw.op("dve", lambda e, n=n: e.tensor_scalar(out=w1[:, :, :n], in0=w1[:, :, :n], scalar1=0.044715 * C1, scalar2=C1, op0=ALU.mult, op1=ALU.add),
                  reads=[tw1], writes=[tw1])
            fw.op("dve", lambda e, yv=yv, n=n: e.tensor_tensor(out=w1[:, :, :n], in0=w1[:, :, :n], in1=yv, op=ALU.mult), reads=[tw1, ty], writes=[tw1])
            fw.op("act", lambda e, n=n: e.activation(out=w1[:, :, :n], in_=w1[:, :, :n], func=AF.Tanh), reads=[tw1], writes=[tw1])
            fw.op("dve", lambda e, n=n: e.tensor_scalar(out=w1[:, :, :n], in0=w1[:, :, :n], scalar1=0.5, scalar2=0.5, op0=ALU.mult, op1=ALU.add),
                  reads=[tw1], writes=[tw1])
            fw.op("dve", lambda e, yv=yv, n=n: e.tensor_tensor(out=a32[:, :, :n], in0=w1[:, :, :n], in1=yv, op=ALU.mult), reads=[tw1, ty], writes=[ta])
            fw.op("act", lambda e, n=n: e.copy(out=ab[:, :, :n], in_=a32[:, :, :n]), reads=[ta], writes=[ta])
            for nt in range(2):
                ps, tps = psr.next()
                for k in range(2):
                    fw.op("pe", lambda e, ps=ps, k=k, nt=nt, n=n: e.matmul(ps[:, :n], lhsT=gw[:, k, nt * 128:(nt + 1) * 128], rhs=ab[:, k, :n],
                                                                      start=(k == 0), stop=(k == 1)), reads=[tg, ta], writes=[tps])
                fw.op("act", lambda e, ps=ps, nt=nt, n=n: e.activation(out=w2[:, nt, :n], in_=ps[:, :n], func=AF.Sigmoid, bias=gb[:, nt:nt + 1]),
                      reads=[tps, tg], writes=[tw1])
            fw.op("dve", lambda e, n=n: e.tensor_tensor(out=w2[:, :, :n], in0=w2[:, :, :n], in1=a32[:, :, :n], op=ALU.mult), reads=[tw1, ta], writes=[tw1])
            fw.op("act", lambda e, n=n: e.activation(out=sqb[:, :, :n], in_=w2[:, :, :n], func=AF.Square), reads=[tw1], writes=[tw1])
            ps, tps = psr.next()
            for k in range(2):
                fw.op("pe", lambda e, ps=ps, k=k, n=n: e.matmul(ps[:, :n], lhsT=ones[:], rhs=sqb[:, k, :n], start=(k == 0), stop=(k == 1)),
                      reads=[tw1, tg], writes=[tps])
            rstd_from_ps(fw, rs, trs, ps, tps, n, 1.0 / 256, eps[:, 0:1], tg)
            for k in range(2):
                sg, tsg = stg.next()
                fw.op("dve", lambda e, sg=sg, k=k, n=n: e.scalar_tensor_tensor(out=sg[:, :n], in0=w2[:, k, :n], scalar=og[:, k:k + 1], in1=rs[:, :n],
                                                                             op0=ALU.mult, op1=ALU.mult), reads=[tw1, trs, tg], writes=[tsg])
                fw.dma("sp", S5T[k * 128:(k + 1) * 128, lo:hi], sg[:, :n], reads=[tsg], writes=[tout])
        fw.barrier()


RW_BASE = 1024
CHK = 64
NCH = T // CHK
LN_EPS_RW = 64e-5


def rw_consts():
    idx = np.arange(64)
    m = np.zeros((64, 4, 64), np.float32)
    m[:, 0, :] = (idx[:, None] < idx[None, :])
    m[:, 1, :] = (idx[:, None] > idx[None, :])
    m[:, 2, :] = (idx[:, None] <= idx[None, :])
    m[:, 3, :] = (idx[:, None] >= idx[None, :])
    bd = np.zeros((128, 128), np.float32)
    bd[:64, :64] = 1.0
    bd[64:, 64:] = 1.0
    return m, bd


def stage_p4(fw, io):
    ZT = io("ZT", [2048, TE], F32, "in")
    mu = io("rw_mu", [960], F32, "in")
    w0 = io("rw_w0", [2, 256], F32, "in")
    w2 = io("rw_w2", [2, 32, 256], F32, "in")
    a0 = io("rw_a0", [2, 256], F32, "in")
    a2 = io("rw_a2", [2, 32, 256], F32, "in")
    g2 = io("rw_g2", [64, 256], F32, "in")
    k_k = io("rw_k_k", [256], F32, "in")
    k_a = io("rw_k_a", [256], F32, "in")
    r_k = io("rw_r_k", [256], F32, "in")
    ln_g = io("rw_ln_g", [256], F32, "in")
    ln_b = io("rw_ln_b", [256], F32, "in")
    ident = io("ident", [128, 128], F32, "in")
    MASKS = io("RWMASK", [64, 4, 64], F32, "in")
    BD = io("RWBD", [128, 128], F32, "in")
    RWT = io("RWT", [256, T], F32, "out")
    N = T
    CH5 = [(0, 512), (512, 1024), (1024, 1536), (1536, 2048), (2048, 2560)]
    with ExitStack() as es:
        P = Pool_(fw, es)
        tc = Tok()
        idt = P.sb([128, 128], F32)
        idb = P.sb([128, 128], BF16)
        msk = P.sb([64, 4, 64], F32)
        bd1 = P.sb([128, 128], F32)
        fw.dma("sp", idt[:], ident, writes=[tc])
        fw.dma("sp", msk[:], MASKS, writes=[tc])
        fw.dma("sp", bd1[:], BD, writes=[tc])
        fw.op("dve", lambda e: e.tensor_copy(out=idb[:], in_=idt[:]), reads=[tc], writes=[tc])
        mrep = P.sb([64, 4, 4, 64], F32)
        for rep in range(4):
            fw.op("dve", lambda e, rep=rep: e.tensor_copy(out=mrep[:, :, rep, :], in_=msk[:]), reads=[tc], writes=[tc])
        pp = P.sb([128, 12, 2], F32)
        tpp = Tok()
        srcs = [w0[0], w0[1], a0[0], a0[1], k_k, k_a, k_a, r_k, ln_g, ln_b]
        for i, sap in enumerate(srcs):
            fw.dma("sp", pp[:, i, :], sap.rearrange("(c p) -> p c", p=128), writes=[tpp], allow_slow_non_contiguous=True)
        fw.op("dve", lambda e: e.tensor_scalar(out=pp[:, 6, :], in0=pp[:, 6, :], scalar1=-1.0, scalar2=1.0, op0=ALU.mult, op1=ALU.add),
              reads=[tpp], writes=[tpp])
        epsl = P.sb([128, 2], F32)
        fw.op("pool", lambda e: e.memset(epsl[:, 0:1], LN_EPS_RW), writes=[tpp])
        fw.op("pool", lambda e: e.memset(epsl[:, 1:2], 1e-24), writes=[tpp])
        wA = P.sb([128, 256], BF16)
        wB = P.sb([128, 256], BF16)
        tlw = Tok()
        fw.dma("pool", wA[0:32, :], w2[0], writes=[tlw])
        fw.dma("pool", wA[32:64, :], w2[1], writes=[tlw])
        fw.dma("pool", wA[64:96, :], a2[0], writes=[tlw])
        fw.dma("pool", wB[0:32, :], a2[1], writes=[tlw])
        fw.dma("pool", wB[64:128, :], g2, writes=[tlw])
        smask = P.sb([128, N], BF16)
        tsm = Tok()
        fw.op("pool", lambda e: e.memset(smask[:], 1.0), writes=[tsm])
        fw.op("pool", lambda e: e.memset(smask[:].rearrange("p (c j) -> p c j", j=CHK)[:, :, 0], 0.0), writes=[tsm])

        def load_shift(dst, tdst, row0, rows, P2, post=None, pb=0):
            zt_ = P2.sb([128, TE], F32)
            nbt_ = P2.sb([128, TE], F32)
            mtt_ = P2.sb([128, 2], F32)
            tz, tnb, tm = Tok(), Tok(), Tok()
            ps_ = slice(pb, pb + rows)
            z = zt_[ps_, :]
            fw.dma("sp", z, ZT[RW_BASE + row0:RW_BASE + row0 + rows, :], writes=[tz])
            fw.dma("sp", mtt_[ps_, 0:1], mu[row0:row0 + rows].rearrange("(p o) -> p o", o=1), writes=[tm])
            fw.op("dve", lambda e: e.tensor_scalar(out=mtt_[ps_, 1:2], in0=mtt_[ps_, 0:1], scalar1=0.5, scalar2=None, op0=ALU.mult), reads=[tm], writes=[tm])
            fw.op("dve", lambda e: e.tensor_scalar(out=mtt_[ps_, 0:1], in0=mtt_[ps_, 0:1], scalar1=-1.0, scalar2=1.0, op0=ALU.mult, op1=ALU.add),
                  reads=[tm], writes=[tm])
            fw.op("pool", lambda e: e.memset(nbt_[ps_, 0:1], 0.0), writes=[tnb])
            fw.op("pool", lambda e: e.tensor_copy(out=nbt_[ps_, 1:TE], in_=zt_[ps_, 0:TE - 1]), reads=[tz], writes=[tnb])
            fw.op("pool", lambda e: e.tensor_tensor(out=nbt_[ps_, 0:TE - 1], in0=nbt_[ps_, 0:TE - 1], in1=zt_[ps_, 1:TE], op=ALU.add),
                  reads=[tz, tnb], writes=[tnb])
            for cb in (256, 2304):
                fw.op("pool", lambda e, cb=cb: e.tensor_tensor(out=nbt_[ps_, cb:cb + 1], in0=nbt_[ps_, cb:cb + 1], in1=zt_[ps_, cb - 1:cb], op=ALU.subtract),
                      reads=[tz, tnb], writes=[tnb])
                fw.op("pool", lambda e, cb=cb: e.tensor_tensor(out=nbt_[ps_, cb - 1:cb], in0=nbt_[ps_, cb - 1:cb], in1=zt_[ps_, cb:cb + 1], op=ALU.subtract),
                      reads=[tz, tnb], writes=[tnb])
            fw.op("act", lambda e: e.activation(out=z, in_=z, func=AF.Identity, scale=mtt_[ps_, 0:1]), reads=[tz, tm], writes=[tz])
            if post is None:
                fw.op("dve", lambda e: e.scalar_tensor_tensor(out=dst, in0=nbt_[ps_, :], scalar=mtt_[ps_, 1:2], in1=z, op0=ALU.mult, op1=ALU.add),
                      reads=[tz, tnb, tm], writes=[tdst])
            else:
                fw.op("dve", lambda e: e.scalar_tensor_tensor(out=z, in0=nbt_[ps_, :], scalar=mtt_[ps_, 1:2], in1=z, op0=ALU.mult, op1=ALU.add),
                      reads=[tz, tnb, tm], writes=[tz])
                fw.op("act", lambda e: e.activation(out=dst, in_=z, func=post), reads=[tz], writes=[tdst])

        lorA = P.sb([128, TE], BF16)
        lorB = P.sb([128, TE], BF16)
        tlor = Tok()
        for i in range(4):
            with ExitStack() as es2:
                dstt = lorA[32 * i:32 * i + 32, :] if i < 3 else lorB[0:32, :]
                load_shift(dstt, tlor, 768 + 32 * i, 32, Pool_(fw, es2), post=(AF.Tanh if i < 2 else AF.Copy), pb=(32 * i if i < 3 else 0))
                fw.barrier()
        with ExitStack() as es2:
            load_shift(lorB[64:128, :], tlor, 896, 64, Pool_(fw, es2), post=AF.Sigmoid, pb=64)
            fw.barrier()

        for c in range(2):
            with ExitStack() as esc:
                Pc = Pool_(fw, esc)
                rr = Pc.sb([128, TE], F32)
                kx = Pc.sb([128, TE], F32)
                vv = Pc.sb([128, TE], F32)
                kk = Pc.sb([128, TE], F32)
                trr, tkx, tvv, tkk = Tok(), Tok(), Tok(), Tok()
                vtok = Pc.sb([64, TE // CHK, 128], BF16)
                tvt = Tok()
                yacc = Pc.sb([128, T], F32)
                bon = Pc.sb([128, T], F32)
                tya, tbon = Tok(), Tok()
                fw.op("pool", lambda e: e.memset(yacc[:], 0.0), writes=[tya])
                fw.op("pool", lambda e: e.memset(bon[:], 0.0), writes=[tbon])
                for (dst, tdst, r0) in ((rr, trr, 0), (kx, tkx, 256), (vv, tvv, 512)):
                    with ExitStack() as es2:
                        load_shift(dst[:], tdst, r0 + c * 128, 128, Pool_(fw, es2))
                        fw.barrier()
                with ExitStack() as es2:
                    P2 = Pool_(fw, es2)
                    sq = P2.sb([128, 512], F32)
                    rn = P2.sb([128, 512], F32)
                    tsq, trn = Tok(), Tok()
                    ps1 = P2.ps()
                    tps1 = Tok()
                    fw.op("dve", lambda e: e.tensor_scalar(out=kk[:], in0=kx[:], scalar1=pp[:, 4, c:c + 1], scalar2=None, op0=ALU.mult),
                          reads=[tkx, tpp], writes=[tkk])
                    for (c0, c1) in CH5:
                        fw.op("act", lambda e, c0=c0, c1=c1: e.activation(out=sq[:], in_=kk[:, c0:c1], func=AF.Square), reads=[tkk], writes=[tsq])
                        fw.op("pe", lambda e: e.matmul(ps1[:], lhsT=bd1[:], rhs=sq[:], start=True, stop=True), reads=[tsq, tc], writes=[tps1])
                        rstd_from_ps(fw, rn, trn, ps1, tps1, 512, 1.0, epsl[:, 1:2], tpp)
                        fw.op("dve", lambda e, c0=c0, c1=c1: e.tensor_tensor(out=kk[:, c0:c1], in0=kk[:, c0:c1], in1=rn[:], op=ALU.mult),
                              reads=[tkk, trn], writes=[tkk])
                    vb = P2.sb([128, TE], BF16)
                    tvb = Tok()
                    fw.op("act", lambda e: e.copy(out=vb[:], in_=vv[:]), reads=[tvv], writes=[tvb])
                    pst = P2.ps([128, 1024], BF16)
                    tpst = Tok()
                    for q in range(TE // CHK // 4):
                        for j in range(4):
                            ch = q * 4 + j
                            fw.op("pe", lambda e, j=j, ch=ch: e.transpose(pst[0:64, j * 128:(j + 1) * 128], vb[:, ch * CHK:(ch + 1) * CHK], idb[:]),
                                  reads=[tvb, tc], writes=[tpst])
                        fw.op("dve", lambda e, q=q: e.tensor_copy(out=vtok[:, q * 4:(q + 1) * 4, :], in_=pst[0:64, 0:512].rearrange("p (j c) -> p j c", j=4)),
                              reads=[tpst], writes=[tvt])
                    fw.barrier()

                for di in range(2):
                    off = 0 if di == 0 else 256
                    with ExitStack() as esd:
                        Pd = Pool_(fw, esd)
                        aT = Pd.sb([128, N], BF16)
                        bT = Pd.sb([128, N], BF16)
                        kT = Pd.sb([128, N], BF16)
                        rT = Pd.sb([128, N], BF16)
                        btok = Pd.sb([64, NCH, 128], BF16)
                        ktok = Pd.sb([64, NCH, 128], BF16)
                        pC = Pd.sb([128, NCH], F32)
                        tops = Tok()
                        ttok = Tok()
                        with ExitStack() as es2:
                            P2 = Pool_(fw, es2)
                            ld = P2.sb([128, TE], F32)
                            kd = P2.sb([128, TE], F32)
                            bb = P2.sb([128, TE], F32)
                            tld, tkd, tbb = Tok(), Tok(), Tok()
                            psr = Rot([P2.ps() for _ in range(3)])
                            tm5 = Rot([P2.sb([128, 512], F32) for _ in range(2)])
                            for (c0, c1) in CH5:
                                ps, tps = psr.next()
                                fw.op("pe", lambda e, ps=ps, c0=c0, c1=c1: e.matmul(ps[:], lhsT=wA[32 * di:32 * di + 32, c * 128:(c + 1) * 128], rhs=lorA[32 * di:32 * di + 32, c0:c1],
                                                                                  start=True, stop=True), reads=[tlw, tlor], writes=[tps])
                                fw.op("act", lambda e, ps=ps, c0=c0, c1=c1: e.activation(out=ld[:, c0:c1], in_=ps[:], func=AF.Sigmoid, bias=pp[:, di, c:c + 1]),
                                      reads=[tps, tpp], writes=[tld])
                                ps, tps = psr.next()
                                fw.op("pe", lambda e, ps=ps, c0=c0, c1=c1: e.matmul(ps[:], lhsT=(wA[64:96, c * 128:(c + 1) * 128] if di == 0 else wB[0:32, c * 128:(c + 1) * 128]),
                                                                                  rhs=(lorA[64:96, c0:c1] if di == 0 else lorB[0:32, c0:c1]),
                                                                                  start=True, stop=True), reads=[tlw, tlor], writes=[tps])
                                fw.op("act", lambda e, ps=ps, c0=c0, c1=c1: e.activation(out=bb[:, c0:c1], in_=ps[:], func=AF.Sigmoid, bias=pp[:, 2 + di, c:c + 1]),
                                      reads=[tps, tpp], writes=[tbb])
                            fw.op("pool", lambda e: e.tensor_scalar(out=ld[:], in0=ld[:], scalar1=-math.exp(-0.5), scalar2=None, op0=ALU.mult), reads=[tld], writes=[tld])
                            fw.op("act", lambda e: e.activation(out=kd[:], in_=bb[:], func=AF.Identity, scale=pp[:, 5, c:c + 1], bias=pp[:, 6, c:c + 1]),
                                  reads=[tbb, tpp], writes=[tkd])
                            fw.op("dve", lambda e: e.tensor_tensor(out=kd[:], in0=kd[:], in1=kx[:], op=ALU.mult), reads=[tkd, tkx], writes=[tkd])
                            fw.op("pool", lambda e: e.tensor_tensor(out=bb[:], in0=bb[:], in1=kk[:], op=ALU.mult), reads=[tbb, tkk], writes=[tbb])
                            for (c0, c1) in CH5:
                                c1 = min(c1, T)
                                n = c1 - c0
                                tm, ttm = tm5.next()
                                fw.op("dve", lambda e, tm=tm, c0=c0, c1=c1, n=n: e.scalar_tensor_tensor(out=tm[:, :n], in0=rr[:, c0:c1], scalar=pp[:, 7, c:c + 1], in1=kd[:, c0:c1],
                                                                                                 op0=ALU.mult, op1=ALU.mult), reads=[trr, tkd, tpp], writes=[ttm])
                                ps, tps = psr.next()
                                fw.op("pe", lambda e, ps=ps, tm=tm, n=n: e.matmul(ps[:, :n], lhsT=bd1[:], rhs=tm[:, :n], start=True, stop=True), reads=[ttm, tc], writes=[tps])
                                tm2, ttm2 = tm5.next()
                                fw.op("dve", lambda e, tm2=tm2, ps=ps, c0=c0, c1=c1, n=n: e.tensor_tensor(out=tm2[:, :n], in0=ps[:, :n], in1=vv[:, c0:c1], op=ALU.mult),
                                      reads=[tps, tvv], writes=[ttm2])
                                fw.op("pool", lambda e, tm2=tm2, c0=c0, c1=c1, n=n: e.tensor_tensor(out=bon[:, c0:c1], in0=bon[:, c0:c1], in1=tm2[:, :n], op=ALU.add),
                                      reads=[ttm2, tbon], writes=[tbon])
                            cs = P2.sb([128, N], F32)
                            ex = P2.sb([128, N], F32)
                            tcs, tex = Tok(), Tok()
                            ldw = ld[:, off:off + N]
                            fw.op("dve", lambda e: e.tensor_tensor_scan(out=cs[:], data0=smask[:], data1=ldw, initial=0.0, op0=ALU.mult, op1=ALU.add),
                                  reads=[tsm, tld], writes=[tcs])
                            csv = cs[:].rearrange("p (c j) -> p c j", j=CHK)
                            tot = P2.sb([128, NCH, 1], F32)
                            ttot = Tok()
                            fw.op("dve", lambda e: e.tensor_copy(out=tot[:], in_=csv[:, :, CHK - 1:CHK]), reads=[tcs], writes=[ttot])
                            totb = tot[:].to_broadcast([128, NCH, CHK])
                            if di == 1:
                                fw.op("dve", lambda e: e.tensor_tensor(out=csv, in0=totb, in1=csv, op=ALU.subtract), reads=[ttot, tcs], writes=[tcs])
                                fw.op("dve", lambda e: e.tensor_tensor(out=cs[:], in0=cs[:], in1=ldw, op=ALU.add), reads=[tcs, tld], writes=[tcs])
                            fw.op("act", lambda e: e.activation(out=pC[:], in_=tot[:, :, 0], func=AF.Exp), reads=[ttot], writes=[tops])
                            kdw, bbw = kd[:, off:off + N], bb[:, off:off + N]
                            rrw, kkw = rr[:, off:off + N], kk[:, off:off + N]
                            fw.op("act", lambda e: e.activation(out=ex[:], in_=cs[:], func=AF.Exp), reads=[tcs], writes=[tex])
                            fw.op("dve", lambda e: e.tensor_tensor(out=rT[:], in0=rrw, in1=ex[:], op=ALU.mult), reads=[trr, tex], writes=[tops])
                            fw.op("act", lambda e: e.activation(out=ex[:], in_=cs[:], func=AF.Exp, scale=-1.0), reads=[tcs, tops], writes=[tex])
                            fw.op("dve", lambda e: e.tensor_tensor(out=bT[:], in0=bbw, in1=ex[:], op=ALU.mult), reads=[tbb, tex], writes=[tops])
                            fw.op("pool", lambda e: e.tensor_tensor(out=kT[:], in0=kdw, in1=ex[:], op=ALU.mult), reads=[tkd, tex], writes=[tops])
                            e3 = ex
                            te3 = tex
                            fw.op("dve", lambda e: e.tensor_tensor(out=e3[:], in0=cs[:], in1=ldw, op=ALU.subtract), reads=[tcs, tld], writes=[te3])
                            fw.op("act", lambda e: e.activation(out=e3[:], in_=e3[:], func=AF.Exp), reads=[te3], writes=[te3])
                            fw.op("dve", lambda e: e.scalar_tensor_tensor(out=aT[:], in0=kkw, scalar=-1.0, in1=e3[:], op0=ALU.mult, op1=ALU.mult),
                                  reads=[tkk, te3], writes=[tops])
                            e3v = e3[:].rearrange("p (c j) -> p c j", j=CHK)
                            fw.op("dve", lambda e: e.tensor_tensor(out=e3v, in0=totb, in1=csv, op=ALU.subtract), reads=[ttot, tcs, tops, te3], writes=[te3])
                            fw.op("act", lambda e: e.activation(out=e3[:], in_=e3[:], func=AF.Exp), reads=[te3], writes=[te3])
                            bh = P2.sb([128, N], BF16)
                            kh = P2.sb([128, N], BF16)
                            tbh = Tok()
                            fw.op("dve", lambda e: e.tensor_tensor(out=bh[:], in0=bbw, in1=e3[:], op=ALU.mult), reads=[tbb, te3], writes=[tbh])
                            fw.op("pool", lambda e: e.tensor_tensor(out=kh[:], in0=kdw, in1=e3[:], op=ALU.mult), reads=[tkd, te3], writes=[tbh])
                            pst = P2.ps([128, 1024], BF16)
                            tpst = Tok()
                            for (src, dstt) in ((bh, btok), (kh, ktok)):
                                for q in range(NCH // 4):
                                    for j in range(4):
                                        ch = q * 4 + j
                                        fw.op("pe", lambda e, j=j, ch=ch, src=src: e.transpose(pst[0:64, j * 128:(j + 1) * 128], src[:, ch * CHK:(ch + 1) * CHK], idb[:]),
                                              reads=[tbh, tc], writes=[tpst])
                                    fw.op("act", lambda e, q=q, dstt=dstt: e.copy(out=dstt[:, q * 4:(q + 1) * 4, :], in_=pst[0:64, 0:512].rearrange("p (j c) -> p j c", j=4)),
                                          reads=[tpst], writes=[ttok])
                            fw.barrier()
                        with ExitStack() as es3:
                            P3 = Pool_(fw, es3)
                            Hs = P3.sb([128, 64], F32)
                            Hb = P3.sb([128, 64], BF16)
                            tH = Tok()
                            fw.op("pool", lambda e: e.memset(Hs[:], 0.0), writes=[tH])
                            fw.op("pool", lambda e: e.memset(Hb[:], 0.0), writes=[tH])
                            psA = Rot([P3.ps([64, 512]) for _ in range(1)])
                            psB = Rot([P3.ps([64, 512]) for _ in range(1)])
                            psI = Rot([P3.ps([64, 512]) for _ in range(1)])
                            psG = Rot([P3.ps([64, 512]) for _ in range(1)])
                            psH = Rot([P3.ps([128, 512]) for _ in range(1)])
                            psY = Rot([P3.ps([128, 512]) for _ in range(1)])
                            g1b = Rot([P3.sb([64, 2, 64], BF16) for _ in range(2)])
                            nl = Rot([P3.sb([64, 2, 2, 64], F32) for _ in range(2)])
                            nl2 = Rot([P3.sb([64, 2, 2, 64], F32) for _ in range(2)])
                            g2b_ = Rot([P3.sb([64, 4, 64], BF16) for _ in range(2)])
                            Pm = Rot([P3.sb([64, 2, 64], F32) for _ in range(2)])
                            Gs = Rot([P3.sb([64, 128], F32) for _ in range(2)])
                            Ub = Rot([P3.sb([64, 128], BF16) for _ in range(2)])
                            Ys = Rot([P3.sb([64, 128], F32) for _ in range(2)])
                            ms, ml, mi = (0, 1, 2) if di == 0 else (1, 0, 3)
                            order = range(NCH) if di == 0 else range(NCH - 1, -1, -1)
                            for i in order:
                                cc0, cc1 = i * CHK, (i + 1) * CHK
                                gch = i + off // CHK
                                pa, tpa = psA.next()
                                pb, tpb = psB.next()
                                for h in range(2):
                                    hp = slice(h * 64, (h + 1) * 64)
                                    for (dst, lt, rt_) in ((pa[:, h * 64:(h + 1) * 64], kT, aT), (pa[:, 128 + h * 64:128 + (h + 1) * 64], bT, aT),
                                                           (pa[:, 256 + h * 64:256 + (h + 1) * 64], aT, bT)):
                                        fw.op("pe", lambda e, dst=dst, lt=lt, rt_=rt_, hp=hp, cc0=cc0, cc1=cc1: e.matmul(dst, lhsT=lt[hp, cc0:cc1], rhs=rt_[hp, cc0:cc1], start=True, stop=True),
                                              reads=[tops], writes=[tpa])
                                    for (dst, lt, rt_) in ((pb[:, h * 64:(h + 1) * 64], bT, rT), (pb[:, 128 + h * 64:128 + (h + 1) * 64], kT, rT)):
                                        fw.op("pe", lambda e, dst=dst, lt=lt, rt_=rt_, hp=hp, cc0=cc0, cc1=cc1: e.matmul(dst, lhsT=lt[hp, cc0:cc1], rhs=rt_[hp, cc0:cc1], start=True, stop=True),
                                              reads=[tops], writes=[tpb])
                                a1, ta1 = g1b.next()
                                nlt, tnl = nl.next()
                                a45, ta45 = g2b_.next()
                                fw.op("dve", lambda e, a1=a1, pa=pa: e.tensor_tensor(out=a1[:], in0=pa[:, 0:128].rearrange("p (h t) -> p h t", h=2), in1=mrep[:, ms, 0:2, :], op=ALU.mult),
                                      reads=[tpa, tc], writes=[ta1])
                                fw.op("dve", lambda e, nlt=nlt, pa=pa: e.tensor_tensor(out=nlt[:, 0], in0=pa[:, 128:256].rearrange("p (h t) -> p h t", h=2), in1=mrep[:, ms, 0:2, :], op=ALU.mult),
                                      reads=[tpa, tc], writes=[tnl])
                                fw.op("dve", lambda e, nlt=nlt, pa=pa: e.tensor_tensor(out=nlt[:, 1], in0=pa[:, 256:384].rearrange("p (h t) -> p h t", h=2), in1=mrep[:, ml, 0:2, :], op=ALU.mult),
                                      reads=[tpa, tc], writes=[tnl])
                                fw.op("dve", lambda e, a45=a45, pb=pb: e.tensor_tensor(out=a45[:], in0=pb[:, 0:256].rearrange("p (h t) -> p h t", h=4), in1=mrep[:, mi, :, :], op=ALU.mult),
                                      reads=[tpb, tc], writes=[ta45])
                                pm, tpm = Pm.next()
                                fw.op("dve", lambda e, pm=pm, nlt=nlt: e.tensor_tensor(out=pm[:], in0=nlt[:, 0], in1=idt[0:64, 0:64].unsqueeze(1).to_broadcast([64, 2, 64]), op=ALU.add),
                                      reads=[tnl, tc], writes=[tpm])
                                cur, tcur = nlt, tnl
                                for lev in range(5):
                                    pi_, tpi = psI.next()
                                    for h in range(2):
                                        fw.op("pe", lambda e, pi_=pi_, h=h, cur=cur: e.matmul(pi_[:, h * 64:(h + 1) * 64], lhsT=cur[:, 0, h, :], rhs=cur[:, 1, h, :], start=True, stop=True),
                                              reads=[tcur], writes=[tpi])
                                        fw.op("pe", lambda e, pi_=pi_, h=h, cur=cur: e.matmul(pi_[:, 128 + h * 64:128 + (h + 1) * 64], lhsT=cur[:, 1, h, :], rhs=cur[:, 0, h, :], start=True, stop=True),
                                              reads=[tcur], writes=[tpi])
                                    nxt, tnxt = (nl2.next() if lev % 2 == 0 else nl.next())
                                    fw.op("act", lambda e, nxt=nxt, pi_=pi_: e.copy(out=nxt[:, 1], in_=pi_[:, 0:128].rearrange("p (h t) -> p h t", h=2)), reads=[tpi], writes=[tnxt])
                                    fw.op("act", lambda e, nxt=nxt, pi_=pi_: e.copy(out=nxt[:, 0], in_=pi_[:, 128:256].rearrange("p (h t) -> p h t", h=2)), reads=[tpi], writes=[tnxt])
                                    for h in range(2):
                                        fw.op("pe", lambda e, pi_=pi_, h=h, nxt=nxt, pm=pm: e.matmul(pi_[:, 256 + h * 64:256 + (h + 1) * 64], lhsT=nxt[:, 1, h, :], rhs=pm[:, h, :], start=True, stop=True),
                                              reads=[tnxt, tpm], writes=[tpi])
                                    fw.op("dve", lambda e, pm=pm, pi_=pi_: e.tensor_tensor(out=pm[:], in0=pm[:], in1=pi_[:, 256:384].rearrange("p (h t) -> p h t", h=2), op=ALU.add),
                                          reads=[tpi, tpm], writes=[tpm])
                                    cur, tcur = nxt, tnxt
                                pg, tpg = psG.next()
                                for h in range(2):
                                    hp = slice(h * 64, (h + 1) * 64)
                                    fw.op("pe", lambda e, pg=pg, h=h, hp=hp, cc0=cc0, cc1=cc1: e.matmul(pg[:, h * 64:(h + 1) * 64], lhsT=aT[hp, cc0:cc1], rhs=Hb[hp, :], start=True, stop=False),
                                          reads=[tops, tH], writes=[tpg])
                                    fw.op("pe", lambda e, pg=pg, h=h, hp=hp, a1=a1, gch=gch: e.matmul(pg[:, h * 64:(h + 1) * 64], lhsT=a1[:, h, :], rhs=vtok[:, gch, hp], start=False, stop=True),
                                          reads=[ta1, tvt], writes=[tpg])
                                gs, tgs = Gs.next()
                                fw.op("act", lambda e, gs=gs, pg=pg: e.copy(out=gs[:], in_=pg[:, 0:128]), reads=[tpg], writes=[tgs])
                                for h in range(2):
                                    fw.op("pe", lambda e, pg=pg, h=h, pm=pm, gs=gs: e.matmul(pg[:, 128 + h * 64:128 + (h + 1) * 64], lhsT=pm[:, h, :], rhs=gs[:, h * 64:(h + 1) * 64], start=True, stop=True),
                                          reads=[tpm, tgs], writes=[tpg])
                                ub, tub = Ub.next()
                                fw.op("dve", lambda e, ub=ub, pg=pg: e.tensor_copy(out=ub[:], in_=pg[:, 128:256]), reads=[tpg], writes=[tub])
                                ph, tph = psH.next()
                                py, tpy = psY.next()
                                for h in range(2):
                                    hp = slice(h * 64, (h + 1) * 64)
                                    fw.op("pe", lambda e, ph=ph, h=h, hp=hp, ub=ub, i=i: e.matmul(ph[hp, 0:64], lhsT=btok[:, i, hp], rhs=ub[:, hp], start=True, stop=False),
                                          reads=[ttok, tub], writes=[tph])
                                    fw.op("pe", lambda e, ph=ph, h=h, hp=hp, i=i, gch=gch: e.matmul(ph[hp, 0:64], lhsT=ktok[:, i, hp], rhs=vtok[:, gch, hp], start=False, stop=True),
                                          reads=[ttok, tvt], writes=[tph])
                                    fw.op("pe", lambda e, py=py, h=h, hp=hp, cc0=cc0, cc1=cc1: e.matmul(py[0:64, hp], lhsT=rT[hp, cc0:cc1], rhs=Hb[hp, :], start=True, stop=False),
                                          reads=[tops, tH], writes=[tpy])
                                    fw.op("pe", lambda e, py=py, h=h, hp=hp, a45=a45, ub=ub: e.matmul(py[0:64, hp], lhsT=a45[:, h, :], rhs=ub[:, hp], start=False, stop=False),
                                          reads=[ta45, tub], writes=[tpy])
                                    fw.op("pe", lambda e, py=py, h=h, hp=hp, a45=a45, gch=gch: e.matmul(py[0:64, hp], lhsT=a45[:, 2 + h, :], rhs=vtok[:, gch, hp], start=False, stop=True),
                                          reads=[ta45, tvt], writes=[tpy])
                                fw.op("dve", lambda e, ph=ph, i=i: e.scalar_tensor_tensor(out=Hs[:], in0=Hs[:], scalar=pC[:, i:i + 1], in1=ph[:, 0:64], op0=ALU.mult, op1=ALU.add),
                                      reads=[tph, tH, tops, tpy], writes=[tH])
                                fw.op("act", lambda e: e.copy(out=Hb[:], in_=Hs[:]), reads=[tH, tpy, tpg], writes=[tH])
                                ys, tys = Ys.next()
                                fw.op("act", lambda e, ys=ys, py=py: e.copy(out=ys[:], in_=py[0:64, 0:128]), reads=[tpy], writes=[tys])
                                fw.op("pe", lambda e, py=py, ys=ys: e.transpose(py[:, 256:320], ys[:], idt[0:64, 0:64]), reads=[tys, tc], writes=[tpy])
                                y0 = off + cc0
                                if y0 >= T:
                                    y0 -= T
                                fw.op("dve", lambda e, py=py, y0=y0: e.tensor_tensor(out=yacc[:, y0:y0 + CHK], in0=yacc[:, y0:y0 + CHK], in1=py[:, 256:320], op=ALU.add),
                                      reads=[tpy, tya], writes=[tya])
                            fw.barrier()
                with ExitStack() as es4:
                    P4 = Pool_(fw, es4)
                    psr = Rot([P4.ps() for _ in range(3)])
                    xc = P4.sb([128, 512], F32)
                    sq = P4.sb([128, 512], F32)
                    rs = P4.sb([128, 512], F32)
                    txc, tsq, trs = Tok(), Tok(), Tok()
                    stg = Rot([P4.sb([128, 512], F32) for _ in range(2)])
                    tout = Tok()
                    for (lo, hi, ic) in SEGS:
                        n = hi - lo
                        ps, tps = psr.next()
                        fw.op("pe", lambda e, ps=ps, lo=lo, hi=hi, n=n: e.matmul(ps[:, :n], lhsT=bd1[:], rhs=yacc[:, lo:hi], start=True, stop=True), reads=[tya, tc], writes=[tps])
                        fw.op("dve", lambda e, ps=ps, lo=lo, hi=hi, n=n: e.scalar_tensor_tensor(out=xc[:, :n], in0=ps[:, :n], scalar=-1.0 / 64, in1=yacc[:, lo:hi], op0=ALU.mult, op1=ALU.add),
                              reads=[tps, tya], writes=[txc])
                        fw.op("act", lambda e, n=n: e.activation(out=sq[:, :n], in_=xc[:, :n], func=AF.Square), reads=[txc], writes=[tsq])
                        ps2, tps2 = psr.next()
                        fw.op("pe", lambda e, ps2=ps2, n=n: e.matmul(ps2[:, :n], lhsT=bd1[:], rhs=sq[:, :n], start=True, stop=True), reads=[tsq, tc], writes=[tps2])
                        rstd_from_ps(fw, rs, trs, ps2, tps2, n, 1.0 / 64, epsl[:, 0:1], tpp)
                        fw.op("dve", lambda e, n=n: e.tensor_tensor(out=xc[:, :n], in0=xc[:, :n], in1=rs[:, :n], op=ALU.mult), reads=[txc, trs], writes=[txc])
                        fw.op("act", lambda e, n=n: e.activation(out=xc[:, :n], in_=xc[:, :n], func=AF.Identity, scale=pp[:, 8, c:c + 1], bias=pp[:, 9, c:c + 1]),
                              reads=[txc, tpp], writes=[txc])
                        fw.op("pool", lambda e, lo=lo, hi=hi, n=n: e.tensor_tensor(out=xc[:, :n], in0=xc[:, :n], in1=bon[:, lo:hi], op=ALU.add), reads=[txc, tbon], writes=[txc])
                        ps3, tps3 = psr.next()
                        fw.op("pe", lambda e, ps3=ps3, lo=lo, hi=hi, n=n: e.matmul(ps3[:, :n], lhsT=wB[64:128, c * 128:(c + 1) * 128], rhs=lorB[64:128, lo:hi], start=True, stop=True),
                              reads=[tlw, tlor], writes=[tps3])
                        sg, tsg = stg.next()
                        fw.op("dve", lambda e, sg=sg, ps3=ps3, n=n: e.tensor_tensor(out=sg[:, :n], in0=ps3[:, :n], in1=xc[:, :n], op=ALU.mult), reads=[tps3, txc], writes=[tsg])
                        fw.dma("sp", RWT[c * 128:(c + 1) * 128, lo:hi], sg[:, :n], reads=[tsg], writes=[tout])
                    fw.barrier()
        fw.barrier()
NCORES = 8
DEPTH = 4
S5_KEYS = ["s5_a_re", "s5_a_im", "s5_log_step", "s5_b_re", "s5_b_im", "s5_c_re", "s5_c_im", "s5_d", "s5_glu_w", "s5_glu_b", "s5_out_g"]
RW_KEYS = ["rw_mu", "rw_w0", "rw_w2", "rw_a0", "rw_a2", "rw_g2", "rw_k_k", "rw_k_a", "rw_r_k", "rw_ln_g", "rw_ln_b"]
LAYER_KEYS = (["norm1_g", "norm2_g", "mod_w", "mod_b", "w_in", "w_out", "att_qn_g", "att_kn_g", "att_out_g",
               "ffn_up", "ffn_conv_w", "ffn_conv_b", "ffn_down"] + S5_KEYS + RW_KEYS)
SHARED_KEYS = ["c_ctx", "final_g", "ident", "POS", "CST", "RWMASK", "RWBD"]
PERCORE_KEYS = ["x_b", "ctx_b", "c_b"]
SCRATCH = {"ZT": [2048, TE], "VT": [T, 128], "MOD": [128, 48, 2], "S5T": [256, T], "ATT_T": [512, T], "RWT": [256, T],
           "XT1": [DM, T], "XA": [DM, T], "XB": [DM, T]}


def build_fused(depth=DEPTH):
    nc = bass.Bass("TRN2", target_bir_lowering=False)
    decl = {}

    def ext(name, shape, dt, kind):
        if name not in decl:
            decl[name] = nc.dram_tensor(name, list(shape), dt, kind=kind).ap()
        return decl[name]

    def make_io(l):
        xin = "XA" if l % 2 == 0 else "XB"
        xout = "XB" if l % 2 == 0 else "XA"

        def io(name, shape, dt, role):
            if name in LAYER_KEYS:
                full = ext(name, [DEPTH] + list(shape), dt, "ExternalInput")
                return full[l]
            if name in SHARED_KEYS or name in PERCORE_KEYS:
                return ext(name, shape, dt, "ExternalInput")
            if name == "OUT":
                return ext(name, shape, dt, "ExternalOutput")
            if name == "XT":
                name = xin
            elif name == "XT2":
                name = xout
            return ext(name, SCRATCH[name], dt, "Internal")
        return io
    with ExitStack() as es:
        fw = FW(nc, es)
        stage_p0(fw, make_io(0))
        for l in range(depth):
            io = make_io(l)
            for st in (stage_p1, stage_p2, stage_p3, stage_p4, stage_p5a, stage_p5b):
                st(fw, io)
        stage_p6(fw, make_io(depth))
        fw.barrier()
    return nc, fw


_FUSED = {}


def kernel(**inp):
    inp = {k: np.ascontiguousarray(np.asarray(v)) for k, v in inp.items()}
    if "nc" not in _FUSED:
        _FUSED["nc"], _FUSED["fw"] = build_fused()
    nc = _FUSED["nc"]
    pos, cst = host_consts()
    rwm, rwbd = rw_consts()
    shared = {k: inp[k] for k in LAYER_KEYS if k != "rw_r_k"}
    shared["rw_r_k"] = inp["rw_r_k"].reshape(DEPTH, 256)
    shared.update(c_ctx=inp["c_ctx"], final_g=inp["final_g"], ident=np.eye(128, dtype=np.float32),
                  POS=pos, CST=cst, RWMASK=rwm, RWBD=rwbd)
    in_maps = [dict(shared, x_b=inp["x"][b], ctx_b=inp["ctx"][b], c_b=inp["c"][b]) for b in range(NCORES)]
    res = run_bass_kernel_spmd(nc, in_maps, core_ids=list(range(NCORES)))
    return np.stack([res.results[b]["OUT"] for b in range(NCORES)], 0).astype(np.float32)
```

```python
import math
import numpy as np
from contextlib import ExitStack
import concourse.bass as bass
import concourse.mybir as mybir
from concourse.bass_utils import run_bass_kernel_spmd

F32 = mybir.dt.float32
F32R = mybir.dt.float32r
BF16 = mybir.dt.bfloat16
I32 = mybir.dt.int32
ALU = mybir.AluOpType
AF = mybir.ActivationFunctionType
AX = mybir.AxisListType

T = 2304
TE = 2560
NCTX = 256
NLAT = 2048
DM = 1024
SEGS = [(0, 256, 1), (256, 768, 0), (768, 1280, 0), (1280, 1792, 0), (1792, 2304, 0)]
RMS_EPS = 1e-6


class Tok:
    __slots__ = ("w", "r")

    def __init__(self):
        self.w = None
        self.r = {}


class FW:
    ENG = ("pe", "dve", "act", "pool", "sp")
    NDMA = 8

    def __init__(self, nc, es):
        self.nc = nc
        self.es = es
        self.eng = {"pe": nc.tensor, "dve": nc.vector, "act": nc.scalar,
                    "pool": nc.gpsimd, "sp": nc.sync}
        self.sem = {}
        self.cnt = {}
        for e in self.ENG:
            self.sem[e] = es.enter_context(nc.semaphore("s_" + e))
            self.cnt[e] = 0
        self.dq = {}
        for q in ("sp", "pool", "act"):
            ring = []
            for i in range(self.NDMA):
                k = "d_%s_%d" % (q, i)
                self.sem[k] = es.enter_context(nc.semaphore(k))
                self.cnt[k] = 0
                ring.append(k)
            self.dq[q] = [ring, 0]
        self.seen = {e: {} for e in self.ENG}
        self.ninst = 0
        self.uid = 0

    def name(self, p):
        self.uid += 1
        return "%s_%d" % (p, self.uid)

    def _deps(self, reads, writes):
        deps = {}
        for t in reads:
            if t.w is not None and deps.get(t.w[0], 0) < t.w[1]:
                deps[t.w[0]] = t.w[1]
        for t in writes:
            if t.w is not None and deps.get(t.w[0], 0) < t.w[1]:
                deps[t.w[0]] = t.w[1]
            for k, v in t.r.items():
                if deps.get(k, 0) < v:
                    deps[k] = v
        return deps

    def _wait(self, e, deps):
        seen = self.seen[e]
        for k, v in deps.items():
            if seen.get(k, 0) < v:
                self.eng[e].wait_ge(self.sem[k], v)
                seen[k] = v

    def op(self, e, fn, reads=(), writes=()):
        self._wait(e, self._deps(reads, writes))
        inst = fn(self.eng[e])
        self.cnt[e] += 1
        inst.then_inc(self.sem[e], 1)
        v = self.cnt[e]
        for t in reads:
            t.r[e] = v
        for t in writes:
            t.w = (e, v)
            t.r = {}
        self.ninst += 1
        return inst

    def dma(self, q, out, in_, reads=(), writes=(), **kw):
        ring, idx = self.dq[q]
        k = ring[idx % len(ring)]
        self.dq[q][1] = idx + 1
        deps = self._deps(reads, writes)
        if self.cnt[k] > 0:
            deps[k] = max(deps.get(k, 0), self.cnt[k])
        self._wait(q, deps)
        inst = self.eng[q].dma_start(out=out, in_=in_, **kw)
        self.cnt[k] += 16
        inst.then_inc(self.sem[k], 16)
        v = self.cnt[k]
        for t in reads:
            t.r[k] = v
        for t in writes:
            t.w = (k, v)
            t.r = {}
        self.ninst += 1
        return inst

    def barrier(self, engines=None):
        allv = {k: v for k, v in self.cnt.items() if v > 0}
        for e in (engines or self.ENG):
            self._wait(e, allv)


class Pool_:
    def __init__(self, fw, es):
        self.fw = fw
        self.es = es
        self.nc = fw.nc

    def sb(self, shape, dt, name="t"):
        return self.es.enter_context(self.nc.sbuf_tensor(self.fw.name(name), list(shape), dt))

    def ps(self, shape=(128, 512), dt=F32, name="ps"):
        return self.es.enter_context(self.nc.psum_tensor(self.fw.name(name), list(shape), dt))


class Rot:
    def __init__(self, bufs):
        self.bufs = bufs
        self.toks = [Tok() for _ in bufs]
        self.i = 0

    def next(self):
        j = self.i % len(self.bufs)
        self.i += 1
        return self.bufs[j], self.toks[j]


def load_w_bf16(fw, P, W, rows, cols, tok, name="w", q="pool", chunk=2048, stage=None):
    kt = rows // 128
    wb = P.sb([128, kt, cols], BF16, name)
    chunk = min(chunk, cols)
    st = stage or Rot([P.sb([128, chunk], F32, "wstg") for _ in range(3)])
    for k in range(kt):
        for c0 in range(0, cols, chunk):
            n = min(chunk, cols - c0)
            sg, tsg = st.next()
            fw.dma("sp", sg[:, :n], W[k * 128:(k + 1) * 128, c0:c0 + n], writes=[tsg])
            fw.op("pool", lambda e, sg=sg, k=k, c0=c0, n=n: e.tensor_copy(out=wb[:, k, c0:c0 + n], in_=sg[:, :n]), reads=[tsg], writes=[tok])
    return wb


def stage_p0(fw, io):
    nc = fw.nc
    xb = io("x_b", [NLAT, DM], F32, "in")
    cb = io("ctx_b", [NCTX, DM], F32, "in")
    ident = io("ident", [128, 128], F32, "in")
    XT = io("XT", [DM, T], F32, "out")
    with ExitStack() as es:
        P = Pool_(fw, es)
        idt = P.sb([128, 128], F32)
        tid = Tok()
        fw.dma("sp", idt[:], ident, writes=[tid])
        xt = P.sb([128, 8, T], F32)
        txt = Tok()
        xin = Rot([P.sb([128, DM], F32) for _ in range(3)])
        pss = Rot([P.ps() for _ in range(4)])
        for tt in range(18):
            src = cb[tt * 128:(tt + 1) * 128, :] if tt < 2 else xb[(tt - 2) * 128:(tt - 1) * 128, :]
            xi, txi = xin.next()
            fw.dma("sp", xi[:], src, writes=[txi])
            for half in range(2):
                ps, tps = pss.next()
                for k in range(4):
                    kk = half * 4 + k
                    fw.op("pe", lambda e, ps=ps, k=k, kk=kk, xi=xi: e.transpose(
                        ps[:, k * 128:(k + 1) * 128], xi[:, kk * 128:(kk + 1) * 128], idt[:]),
                        reads=[txi, tid], writes=[tps])
                eng = "dve" if half == 0 else "act"
                outap = xt[:, half * 4:half * 4 + 4, tt * 128:(tt + 1) * 128]
                inap = ps[:].rearrange("p (k t) -> p k t", k=4)
                if eng == "dve":
                    fw.op("dve", lambda e, o=outap, i=inap: e.tensor_copy(out=o, in_=i), reads=[tps], writes=[txt])
                else:
                    fw.op("act", lambda e, o=outap, i=inap: e.copy(out=o, in_=i), reads=[tps], writes=[txt])
        tout = Tok()
        for k in range(8):
            fw.dma("sp", XT[k * 128:(k + 1) * 128, :], xt[:, k, :], reads=[txt], writes=[tout])
        fw.barrier()


def make_AB(fw, P, MODs, tmod, g_ap, sh_base, sc_base):
    g = P.sb([128, 8], F32)
    tg = Tok()
    fw.dma("sp", g[:], g_ap.rearrange("(k p) -> p k", p=128), writes=[tg], allow_slow_non_contiguous=True)
    AB = P.sb([128, 2, 2, 8], F32)
    tab = Tok()
    for ic in range(2):
        fw.op("dve", lambda e, ic=ic: e.tensor_scalar(out=AB[:, ic, 0, :], in0=MODs[:, sc_base:sc_base + 8, ic],
                                                      scalar1=1.0, scalar2=None, op0=ALU.add),
              reads=[tmod], writes=[tab])
        fw.op("dve", lambda e, ic=ic: e.tensor_tensor(out=AB[:, ic, 0, :], in0=AB[:, ic, 0, :], in1=g[:], op=ALU.mult),
              reads=[tg, tab], writes=[tab])
        fw.op("dve", lambda e, ic=ic: e.tensor_copy(out=AB[:, ic, 1, :], in_=MODs[:, sh_base:sh_base + 8, ic]),
              reads=[tmod], writes=[tab])
    return AB, tab


def norm_mod_seg(fw, P, st, xs, txs, n, ic, AB, tab, outs, touts):
    sq, ones, tones, psr, rs, tmpr = st["sq"], st["ones"], st["tones"], st["psr"], st["rs"], st["tmpr"]
    tsq, trs = st["tsq"], st["trs"]
    fw.op("act", lambda e: e.activation(out=sq[:, :, :n], in_=xs[:, :, :n], func=AF.Square), reads=[txs], writes=[tsq])
    ps, tps = psr.next()
    for k in range(8):
        fw.op("pe", lambda e, k=k: e.matmul(ps[:, :n], lhsT=ones[:], rhs=sq[:, k, :n], start=(k == 0), stop=(k == 7)),
              reads=[tsq, tones], writes=[tps])
    fw.op("act", lambda e: e.activation(out=rs[:, :n], in_=ps[:, :n], func=AF.Ln, scale=1.0 / DM, bias=st["eps"][:, 0:1]),
          reads=[tps, st["teps"]], writes=[trs])
    fw.op("act", lambda e: e.activation(out=rs[:, :n], in_=rs[:, :n], func=AF.Exp, scale=-0.5), reads=[trs], writes=[trs])
    for k in range(8):
        tmp, ttmp = tmpr.next()
        fw.op("dve", lambda e, k=k, tmp=tmp: e.tensor_tensor(out=tmp[:, :n], in0=xs[:, k, :n], in1=rs[:, :n], op=ALU.mult),
              reads=[txs, trs], writes=[ttmp])
        for o in outs(k):
            fw.op("act", lambda e, k=k, tmp=tmp, o=o: e.activation(out=o, in_=tmp[:, :n], func=AF.Identity,
                                                                   scale=AB[:, ic, 0, k:k + 1], bias=AB[:, ic, 1, k:k + 1]),
                  reads=[ttmp, tab], writes=touts)


def norm_state(fw, P):
    st = {}
    st["sq"] = P.sb([128, 8, 512], BF16)
    st["tsq"] = Tok()
    st["ones"] = P.sb([128, 128], BF16)
    st["tones"] = Tok()
    fw.op("pool", lambda e: e.memset(st["ones"][:], 1.0), writes=[st["tones"]])
    st["eps"] = P.sb([128, 1], F32)
    st["teps"] = Tok()
    fw.op("pool", lambda e: e.memset(st["eps"][:], RMS_EPS), writes=[st["teps"]])
    st["psr"] = Rot([P.ps() for _ in range(2)])
    st["rs"] = P.sb([128, 512], F32)
    st["trs"] = Tok()
    st["tmpr"] = Rot([P.sb([128, 512], F32) for _ in range(2)])
    return st


def stage_p1(fw, io):
    XT = io("XT", [DM, T], F32, "in")
    c_b = io("c_b", [DM], F32, "in")
    c_ctx = io("c_ctx", [DM], F32, "in")
    mod_w = io("mod_w", [DM, 6 * DM], F32, "in")
    mod_b = io("mod_b", [6 * DM], F32, "in")
    n1g = io("norm1_g", [DM], F32, "in")
    w_in = io("w_in", [DM, 1984], F32, "in")
    MOD = io("MOD", [128, 48, 2], F32, "out")
    ZT = io("ZT", [2048, TE], F32, "out")
    VT = io("VT", [T, 128], F32, "out")
    XTv = XT.rearrange("(k p) t -> p k t", p=128)
    with ExitStack() as es:
        P = Pool_(fw, es)
        wstage = Rot([P.sb([128, 2048], F32, "wstg") for _ in range(3)])
        tmw = Tok()
        cc = P.sb([128, 8, 2], F32)
        tcc = Tok()
        fw.dma("sp", cc[:, :, 0], c_b.rearrange("(k p) -> p k", p=128), writes=[tcc], allow_slow_non_contiguous=True)
        fw.dma("sp", cc[:, :, 1], c_ctx.rearrange("(k p) -> p k", p=128), writes=[tcc], allow_slow_non_contiguous=True)
        scb = P.sb([128, 8, 2], BF16)
        tscb = Tok()
        fw.op("act", lambda e: e.activation(out=scb[:], in_=cc[:], func=AF.Silu), reads=[tcc], writes=[tscb])
        mb = P.sb([128, 48], F32)
        tmb = Tok()
        fw.dma("sp", mb[:], mod_b.rearrange("(j p) -> p j", p=128), writes=[tmb], allow_slow_non_contiguous=True)
        MODs = P.sb([128, 48, 2], F32)
        tmod = Tok()
        with ExitStack() as es2:
            P2 = Pool_(fw, es2)
            mwb = load_w_bf16(fw, P2, mod_w, DM, 6 * DM, tmw, "modw", stage=wstage)
            psm = P2.ps([128, 512])
            tpsm = Tok()
            for j in range(48):
                for k in range(8):
                    fw.op("pe", lambda e, j=j, k=k: e.matmul(psm[:, 2 * j:2 * j + 2], lhsT=mwb[:, k, j * 128:(j + 1) * 128],
                                                             rhs=scb[:, k, :], start=(k == 0), stop=(k == 7)),
                          reads=[tmw, tscb], writes=[tpsm])
            for ic in range(2):
                fw.op("dve", lambda e, ic=ic: e.tensor_tensor(
                    out=MODs[:, :, ic], in0=psm[:, 0:96].rearrange("p (j c) -> p j c", c=2)[:, :, ic], in1=mb[:], op=ALU.add),
                    reads=[tpsm, tmb], writes=[tmod])
            fw.barrier()
        tmo = Tok()
        fw.dma("sp", MOD, MODs[:], reads=[tmod], writes=[tmo])
        AB, tab = make_AB(fw, P, MODs, tmod, n1g, 0, 8)
        tw = Tok()
        wb = load_w_bf16(fw, P, w_in, DM, 1984, tw, "win", stage=wstage)
        hT = P.sb([128, 8, TE], BF16)
        thT = Tok()
        st = norm_state(fw, P)
        xr = Rot([P.sb([128, 8, 512], F32) for _ in range(2)])
        for (lo, hi, ic) in SEGS:
            n = hi - lo
            xs, txs = xr.next()
            fw.dma("sp", xs[:, :, :n], XTv[:, :, lo:hi], writes=[txs])

            def outs(k, lo=lo, hi=hi, ic=ic):
                o = [hT[:, k, lo:hi]]
                if ic:
                    o.append(hT[:, k, T + lo:T + hi])
                return o
            norm_mod_seg(fw, P, st, xs, txs, n, ic, AB, tab, outs, [thT])
        psr = Rot([P.ps() for _ in range(4)])
        stg = Rot([P.sb([128, 512], F32) for _ in range(4)])
        tz = Tok()
        cnt = 0
        for nt in range(16):
            if nt == 7:
                continue
            M = 64 if nt == 15 else 128
            for cc_ in range(5):
                c0 = cc_ * 512
                ps, tps = psr.next()
                for k in range(8):
                    fw.op("pe", lambda e, ps=ps, k=k, nt=nt, M=M, c0=c0: e.matmul(
                        ps[0:M, :], lhsT=wb[:, k, nt * 128:nt * 128 + M], rhs=hT[:, k, c0:c0 + 512],
                        start=(k == 0), stop=(k == 7)), reads=[tw, thT], writes=[tps])
                sg, tsg = stg.next()
                if cnt % 2 == 0:
                    fw.op("dve", lambda e, sg=sg, ps=ps, M=M: e.tensor_copy(out=sg[0:M, :], in_=ps[0:M, :]), reads=[tps], writes=[tsg])
                else:
                    fw.op("act", lambda e, sg=sg, ps=ps, M=M: e.copy(out=sg[0:M, :], in_=ps[0:M, :]), reads=[tps], writes=[tsg])
                cnt += 1
                fw.dma("sp", ZT[nt * 128:nt * 128 + M, c0:c0 + 512], sg[0:M, :], reads=[tsg], writes=[tz])
        for tt in range(18):
            ps, tps = psr.next()
            for k in range(8):
                fw.op("pe", lambda e, ps=ps, k=k, tt=tt: e.matmul(
                    ps[:, 0:128], lhsT=hT[:, k, tt * 128:(tt + 1) * 128], rhs=wb[:, k, 896:1024],
                    start=(k == 0), stop=(k == 7)), reads=[tw, thT], writes=[tps])
            sg, tsg = stg.next()
            fw.op("dve", lambda e, sg=sg, ps=ps: e.tensor_copy(out=sg[:, 0:128], in_=ps[:, 0:128]), reads=[tps], writes=[tsg])
            fw.dma("sp", VT[tt * 128:(tt + 1) * 128, :], sg[:, 0:128], reads=[tsg], writes=[tz])
        fw.barrier()


def build_program(stage_fns):
    nc = bass.Bass("TRN2", target_bir_lowering=False)
    decl = {}

    def io(name, shape, dt, role):
        if name in decl:
            return decl[name][0]
        kind = "ExternalInput" if role == "in" else "ExternalOutput"
        ap = nc.dram_tensor(name, list(shape), dt, kind=kind).ap()
        decl[name] = (ap, role, shape)
        return ap
    with ExitStack() as es:
        fw = FW(nc, es)
        for fn in stage_fns:
            fn(fw, io)
        fw.barrier()
    return nc, decl, fw


_PROG_CACHE = {}


def run_stage(key, stage_fns, in_maps, ncores):
    if key not in _PROG_CACHE:
        _PROG_CACHE[key] = build_program(stage_fns)
    nc, decl, fw = _PROG_CACHE[key]
    res = run_bass_kernel_spmd(nc, in_maps, core_ids=list(range(ncores)))
    return res.results


def rstd_from_ps(fw, rs, trs, ps, tps, n, scale, epsap, teps, rows=128):
    fw.op("act", lambda e: e.activation(out=rs[0:rows, :n], in_=ps[0:rows, :n], func=AF.Ln, scale=scale, bias=epsap),
          reads=[tps, teps], writes=[trs])
    fw.op("act", lambda e: e.activation(out=rs[0:rows, :n], in_=rs[0:rows, :n], func=AF.Exp, scale=-0.5), reads=[trs], writes=[trs])


def stage_p3(fw, io):
    ZT = io("ZT", [2048, TE], F32, "in")
    VT = io("VT", [T, 128], F32, "in")
    qn_g = io("att_qn_g", [64], F32, "in")
    kn_g = io("att_kn_g", [64], F32, "in")
    og = io("att_out_g", [512], F32, "in")
    POS = io("POS", [128, NLAT], F32, "in")
    CST = io("CST", [128, 260], F32, "in")
    ATT = io("ATT_T", [512, T], F32, "out")
    with ExitStack() as es:
        P = Pool_(fw, es)
        cst = P.sb([128, 260], F32)
        tc = Tok()
        fw.dma("sp", cst[:], CST, writes=[tc])
        permb = P.sb([128, 128], BF16)
        bdb = P.sb([128, 128], BF16)
        fw.op("dve", lambda e: e.tensor_copy(out=permb[:], in_=cst[:, 1:129]), reads=[tc], writes=[tc])
        fw.op("dve", lambda e: e.tensor_copy(out=bdb[:], in_=cst[:, 129:257]), reads=[tc], writes=[tc])
        eps = P.sb([128, 2], F32)
        teps = Tok()
        fw.op("pool", lambda e: e.memset(eps[:, 0:1], RMS_EPS), writes=[teps])
        fw.op("pool", lambda e: e.memset(eps[:, 1:2], 0.0), writes=[teps])
        gq = P.sb([128, 2], F32)
        tg = Tok()
        for h in range(2):
            fw.dma("sp", gq[h * 64:(h + 1) * 64, 0:1], qn_g.rearrange("(p o) -> p o", o=1), writes=[tg])
            fw.dma("sp", gq[h * 64:(h + 1) * 64, 1:2], kn_g.rearrange("(p o) -> p o", o=1), writes=[tg])
        cos = P.sb([128, NLAT], F32)
        sin = P.sb([128, NLAT], F32)
        ttab = Tok()
        with ExitStack() as es2:
            P2 = Pool_(fw, es2)
            ang = P2.sb([128, NLAT], F32)
            tmpf = P2.sb([128, NLAT], F32)
            tmpi = P2.sb([128, NLAT], I32)
            ta = Tok()
            fw.dma("sp", ang[:], POS, writes=[ta])
            fw.op("dve", lambda e: e.tensor_scalar(out=ang[:], in0=ang[:], scalar1=cst[:, 0:1], scalar2=None, op0=ALU.mult),
                  reads=[ta, tc], writes=[ta])
            for (tab, off) in ((sin, 0.0), (cos, math.pi / 2)):
                fw.op("dve", lambda e, off=off: e.tensor_scalar(out=tmpi[:], in0=ang[:], scalar1=off, scalar2=1.0 / (2 * math.pi),
                                                                op0=ALU.add, op1=ALU.mult), reads=[ta], writes=[ta])
                fw.op("dve", lambda e: e.tensor_copy(out=tmpf[:], in_=tmpi[:]), reads=[ta], writes=[ta])
                fw.op("dve", lambda e: e.scalar_tensor_tensor(out=tmpf[:], in0=tmpf[:], scalar=-2 * math.pi, in1=ang[:],
                                                              op0=ALU.mult, op1=ALU.add), reads=[ta], writes=[ta])
                fw.op("dve", lambda e, off=off: e.tensor_scalar(out=tmpf[:], in0=tmpf[:], scalar1=off, scalar2=math.pi,
                                                                op0=ALU.add, op1=ALU.min), reads=[ta], writes=[ta])
                fw.op("dve", lambda e: e.tensor_scalar(out=tmpf[:], in0=tmpf[:], scalar1=-math.pi, scalar2=None, op0=ALU.max),
                      reads=[ta], writes=[ta])
                fw.op("act", lambda e, tab=tab: e.activation(out=tab[:], in_=tmpf[:], func=AF.Sin), reads=[ta], writes=[ttab])
            fw.barrier()
        qb = P.sb([128, 4, T], BF16)
        kd = P.sb([128, 2, T], BF16)
        tq = Tok()
        with ExitStack() as es2:
            P2 = Pool_(fw, es2)
            raw = Rot([P2.sb([128, 512], F32) for _ in range(2)])
            sqr = Rot([P2.sb([128, 512], BF16) for _ in range(2)])
            psr = Rot([P2.ps() for _ in range(2)])
            psr2 = Rot([P2.ps() for _ in range(2)])
            rsr = Rot([P2.sb([128, 512], F32) for _ in range(2)])
            nbr = Rot([P2.sb([128, 512], BF16) for _ in range(2)])
            t1r = Rot([P2.sb([128, 512], F32) for _ in range(2)])
            t2r = Rot([P2.sb([128, 512], F32) for _ in range(2)])
            items = [("q", j) for j in range(4)] + [("k", g) for g in range(2)]
            for (kind, j) in items:
                for (lo, hi, ic) in SEGS:
                    n = hi - lo
                    rw, trw = raw.next()
                    if kind == "q":
                        fw.dma("sp", rw[:, :n], ZT[256 + j * 128:256 + (j + 1) * 128, lo:hi], writes=[trw])
                        gcol = 0
                        dst = qb[:, j, lo:hi]
                    else:
                        for h in range(2):
                            fw.dma("sp", rw[h * 64:(h + 1) * 64, :n], ZT[768 + j * 64:768 + (j + 1) * 64, lo:hi], writes=[trw])
                        gcol = 1
                        dst = kd[:, j, lo:hi]
                    sq, tsq = sqr.next()
                    fw.op("act", lambda e, sq=sq, rw=rw, n=n: e.activation(out=sq[:, :n], in_=rw[:, :n], func=AF.Square), reads=[trw], writes=[tsq])
                    ps, tps = psr.next()
                    fw.op("pe", lambda e, ps=ps, sq=sq, n=n: e.matmul(ps[:, :n], lhsT=bdb[:], rhs=sq[:, :n], start=True, stop=True),
                          reads=[tsq, tc], writes=[tps])
                    rs, trs = rsr.next()
                    rstd_from_ps(fw, rs, trs, ps, tps, n, 1.0, eps[:, 0:1], teps)
                    t1, tt1 = t1r.next()
                    fw.op("dve", lambda e, t1=t1, rw=rw, rs=rs, n=n, gcol=gcol: e.scalar_tensor_tensor(
                        out=t1[:, :n], in0=rw[:, :n], scalar=gq[:, gcol:gcol + 1], in1=rs[:, :n], op0=ALU.mult, op1=ALU.mult),
                        reads=[trw, trs, tg], writes=[tt1])
                    if ic:
                        fw.op("act", lambda e, dst=dst, t1=t1, n=n: e.copy(out=dst, in_=t1[:, :n]), reads=[tt1], writes=[tq])
                        continue
                    nb, tnb = nbr.next()
                    fw.op("act", lambda e, nb=nb, t1=t1, n=n: e.copy(out=nb[:, :n], in_=t1[:, :n]), reads=[tt1], writes=[tnb])
                    ps2, tps2 = psr2.next()
                    fw.op("pe", lambda e, ps2=ps2, nb=nb, n=n: e.matmul(ps2[:, :n], lhsT=permb[:], rhs=nb[:, :n], start=True, stop=True),
                          reads=[tnb, tc], writes=[tps2])
                    p0 = lo - NCTX
                    t2, tt2 = t2r.next()
                    fw.op("dve", lambda e, t2=t2, ps2=ps2, n=n, p0=p0: e.tensor_tensor(out=t2[:, :n], in0=ps2[:, :n], in1=sin[:, p0:p0 + n], op=ALU.mult),
                          reads=[tps2, ttab], writes=[tt2])
                    fw.op("pool", lambda e, t1=t1, nb=nb, n=n, p0=p0: e.tensor_tensor(out=t1[:, :n], in0=nb[:, :n], in1=cos[:, p0:p0 + n], op=ALU.mult),
                          reads=[tnb, ttab, tt1], writes=[tt1])
                    fw.op("pool", lambda e, dst=dst, t1=t1, t2=t2, n=n: e.tensor_tensor(out=dst, in0=t1[:, :n], in1=t2[:, :n], op=ALU.add),
                          reads=[tt1, tt2], writes=[tq])
            fw.barrier()
        va = P.sb([128, 18, 2, 128], BF16)
        tva = Tok()
        fw.op("pool", lambda e: e.memset(va[:], 1.0), writes=[tva])
        for g in range(2):
            fw.dma("pool", va[:, :, g, 0:64], VT.rearrange("(t p) c -> p t c", p=128)[:, :, g * 64:(g + 1) * 64], writes=[tva])
        att = P.sb([128, 4, T], F32)
        tatt = Tok()
        pss = Rot([P.ps() for _ in range(3)])
        pso = Rot([P.ps() for _ in range(2)])
        ptr = Rot([P.sb([128, 512], BF16) for _ in range(3)])
        rcr = Rot([P.sb([64, 512], F32) for _ in range(2)])
        jobs = [(0, 256, 0, 2)] + [(256 + 512 * i, 768 + 512 * i, 0, 18) for i in range(4)]
        for h in range(8):
            g = h // 4
            jt, r0 = h // 2, (h % 2) * 64
            for (qlo, qhi, k0, k1) in jobs:
                n = qhi - qlo
                po, tpo = pso.next()
                for kt in range(k0, k1):
                    ps, tps = pss.next()
                    fw.op("pe", lambda e, ps=ps, kt=kt, g=g, jt=jt, r0=r0, qlo=qlo, qhi=qhi, n=n: e.matmul(
                        ps[:, :n], lhsT=kd[r0:r0 + 64, g, kt * 128:(kt + 1) * 128], rhs=qb[r0:r0 + 64, jt, qlo:qhi],
                        start=True, stop=True), reads=[tq], writes=[tps])
                    pt, tpt = ptr.next()
                    fw.op("act", lambda e, pt=pt, ps=ps, n=n: e.activation(out=pt[:, :n], in_=ps[:, :n], func=AF.Exp, scale=0.125),
                          reads=[tps], writes=[tpt])
                    fw.op("pe", lambda e, po=po, pt=pt, kt=kt, g=g, n=n, k0=k0, k1=k1: e.matmul(
                        po[:, :n], lhsT=va[:, kt, g, :], rhs=pt[:, :n], start=(kt == k0), stop=(kt == k1 - 1)),
                        reads=[tpt, tva], writes=[tpo])
                rc, trc = rcr.next()
                fw.op("dve", lambda e, rc=rc, po=po, n=n: e.reciprocal(out=rc[:, :n], in_=po[64:128, :n]), reads=[tpo], writes=[trc])
                fw.op("dve", lambda e, rc=rc, po=po, n=n, jt=jt, r0=r0, qlo=qlo, qhi=qhi: e.tensor_tensor(
                    out=att[r0:r0 + 64, jt, qlo:qhi], in0=po[0:64, :n], in1=rc[:, :n], op=ALU.mult),
                    reads=[tpo, trc], writes=[tatt])
        ogs = P.sb([128, 4], F32)
        tog = Tok()
        fw.dma("sp", ogs[:], og.rearrange("(k p) -> p k", p=128), writes=[tog], allow_slow_non_contiguous=True)
        ones = P.sb([128, 128], BF16)
        fw.op("pool", lambda e: e.memset(ones[:], 1.0), writes=[tog])
        sq4 = P.sb([128, 4, 512], BF16)
        tsq4 = Tok()
        rs = P.sb([128, 512], F32)
        trs = Tok()
        stg = Rot([P.sb([128, 512], F32) for _ in range(3)])
        tout = Tok()
        for (lo, hi, ic) in SEGS:
            n = hi - lo
            fw.op("act", lambda e, lo=lo, hi=hi, n=n: e.activation(out=sq4[:, :, :n], in_=att[:, :, lo:hi], func=AF.Square), reads=[tatt], writes=[tsq4])
            ps, tps = pss.next()
            for k in range(4):
                fw.op("pe", lambda e, ps=ps, k=k, n=n: e.matmul(ps[:, :n], lhsT=ones[:], rhs=sq4[:, k, :n], start=(k == 0), stop=(k == 3)),
                      reads=[tsq4, tog], writes=[tps])
            rstd_from_ps(fw, rs, trs, ps, tps, n, 1.0 / 512, eps[:, 0:1], teps)
            for k in range(4):
                sg, tsg = stg.next()
                fw.op("dve", lambda e, sg=sg, k=k, lo=lo, hi=hi, n=n: e.scalar_tensor_tensor(
                    out=sg[:, :n], in0=att[:, k, lo:hi], scalar=ogs[:, k:k + 1], in1=rs[:, :n], op0=ALU.mult, op1=ALU.mult),
                    reads=[tatt, trs, tog], writes=[tsg])
                fw.dma("sp", ATT[k * 128:(k + 1) * 128, lo:hi], sg[:, :n], reads=[tsg], writes=[tout])
        fw.barrier()


def host_consts():
    pos = np.zeros((128, NLAT), np.float32)
    inv = np.zeros((128,), np.float32)
    tok = np.arange(NLAT)
    for p in range(128):
        d = p % 64
        pos[p] = (tok // 64) if d < 32 else (tok % 64)
        inv[p] = 10000.0 ** (-(d % 16) / 16.0)
    cst = np.zeros((128, 260), np.float32)
    cst[:, 0] = inv
    perm = np.zeros((128, 128), np.float32)
    for m in range(128):
        d = m % 32
        if d < 16:
            perm[m + 16, m] = -1.0
        else:
            perm[m - 16, m] = 1.0
    cst[:, 1:129] = perm
    bd = np.zeros((128, 128), np.float32)
    bd[:64, :64] = 1.0 / 64
    bd[64:, 64:] = 1.0 / 64
    cst[:, 129:257] = bd
    return pos, cst


def stage_p5a(fw, io):
    XT = io("XT", [DM, T], F32, "in")
    S5T = io("S5T", [256, T], F32, "in")
    ATT = io("ATT_T", [512, T], F32, "in")
    RWT = io("RWT", [256, T], F32, "in")
    MOD = io("MOD", [128, 48, 2], F32, "in")
    w_out = io("w_out", [DM, DM], F32, "in")
    XT1 = io("XT1", [DM, T], F32, "out")
    XTv = XT.rearrange("(k p) t -> p k t", p=128)
    with ExitStack() as es:
        P = Pool_(fw, es)
        MODs = P.sb([128, 48, 2], F32)
        tmod = Tok()
        fw.dma("sp", MODs[:], MOD, writes=[tmod])
        tw = Tok()
        wb = load_w_bf16(fw, P, w_out, DM, DM, tw, "wout")
        cat = P.sb([128, 8, T], BF16)
        tcat = Tok()
        cstg = Rot([P.sb([128, T], F32, "cstg") for _ in range(2)])
        for k in range(8):
            src = S5T[k * 128:(k + 1) * 128, :] if k < 2 else (ATT[(k - 2) * 128:(k - 1) * 128, :] if k < 6 else RWT[(k - 6) * 128:(k - 5) * 128, :])
            sg, tsg = cstg.next()
            fw.dma("sp", sg[:], src, writes=[tsg])
            fw.op("pool", lambda e, sg=sg, k=k: e.tensor_copy(out=cat[:, k, :], in_=sg[:]), reads=[tsg], writes=[tcat])
        xr = Rot([P.sb([128, 8, 512], F32) for _ in range(2)])
        x1r = Rot([P.sb([128, 8, 512], F32) for _ in range(2)])
        psr = Rot([P.ps() for _ in range(4)])
        to1, to2 = Tok(), Tok()
        for (lo, hi, ic) in SEGS:
            n = hi - lo
            xs, txs = xr.next()
            fw.dma("sp", xs[:, :, :n], XTv[:, :, lo:hi], writes=[txs])
            x1, tx1 = x1r.next()
            for d in range(8):
                ps, tps = psr.next()
                for k in range(8):
                    fw.op("pe", lambda e, ps=ps, k=k, d=d, lo=lo, hi=hi, n=n: e.matmul(
                        ps[:, :n], lhsT=wb[:, k, d * 128:(d + 1) * 128], rhs=cat[:, k, lo:hi], start=(k == 0), stop=(k == 7)),
                        reads=[tw, tcat], writes=[tps])
                fw.op("dve", lambda e, ps=ps, d=d, n=n, ic=ic, x1=x1, xs=xs: e.scalar_tensor_tensor(
                    out=x1[:, d, :n], in0=ps[:, :n], scalar=MODs[:, 16 + d, ic:ic + 1], in1=xs[:, d, :n], op0=ALU.mult, op1=ALU.add),
                    reads=[tps, txs, tmod], writes=[tx1])
            for k in range(8):
                fw.dma("sp", XT1[k * 128:(k + 1) * 128, lo:hi], x1[:, k, :n], reads=[tx1], writes=[to1])
        fw.barrier()


def stage_p5b(fw, io):
    XT1 = io("XT1", [DM, T], F32, "in")
    n2g = io("norm2_g", [DM], F32, "in")
    MOD = io("MOD", [128, 48, 2], F32, "in")
    up = io("ffn_up", [DM, 5632], F32, "in")
    cw = io("ffn_conv_w", [3, 5632], F32, "in")
    cb = io("ffn_conv_b", [5632], F32, "in")
    down = io("ffn_down", [2816, DM], F32, "in")
    XT2 = io("XT2", [DM, T], F32, "out")
    X1v = XT1.rearrange("(k p) t -> p k t", p=128)
    X2v = XT2.rearrange("(k p) t -> p k t", p=128)
    with ExitStack() as es:
        P = Pool_(fw, es)
        MODs = P.sb([128, 48, 2], F32)
        tmod = Tok()
        fw.dma("sp", MODs[:], MOD, writes=[tmod])
        cws = P.sb([128, 44, 3], F32)
        cbs = P.sb([128, 44], F32)
        tcw = Tok()
        for w in range(3):
            fw.dma("sp", cws[:, :, w], cw[w].rearrange("(j p) -> p j", p=128), writes=[tcw], allow_slow_non_contiguous=True)
        fw.dma("sp", cbs[:], cb.rearrange("(j p) -> p j", p=128), writes=[tcw], allow_slow_non_contiguous=True)
        h2 = P.sb([128, 8, T], BF16)
        th2 = Tok()
        AB, tab = make_AB(fw, P, MODs, tmod, n2g, 24, 32)
        with ExitStack() as es2:
            P2 = Pool_(fw, es2)
            st = norm_state(fw, P2)
            xr0 = Rot([P2.sb([128, 8, 512], F32) for _ in range(2)])
            for (lo, hi, ic) in SEGS:
                n = hi - lo
                xs, txs = xr0.next()
                fw.dma("sp", xs[:, :, :n], X1v[:, :, lo:hi], writes=[txs])
                norm_mod_seg(fw, P2, st, xs, txs, n, ic, AB, tab, lambda k, lo=lo, hi=hi: [h2[:, k, lo:hi]], [th2])
            fw.barrier()
        hid = P.sb([128, 11, T], BF16)
        thid = Tok()
        dwb = P.sb([128, 11, DM], BF16)
        tdw = Tok()
        urot = [Rot([P.sb([128, T], F32) for _ in range(2)]) for _ in range(2)]
        y = [P.sb([128, T], F32) for _ in range(2)]
        ty = [Tok(), Tok()]
        psr = Rot([P.ps() for _ in range(4)])
        tx2 = Tok()
        RANGES = [(0, NCTX), (NCTX, T)]
        upv = up.rearrange("(k p) n -> p k n", p=128)
        GROUPS = [(0, 2), (2, 2), (4, 2), (6, 2), (8, 2), (10, 1)]
        for half in range(2):
            with ExitStack() as esu:
                Pu = Pool_(fw, esu)
                ustg = Rot([Pu.sb([128, 8, 256], F32, "ustg") for _ in range(2)])
                for jj in range(11):
                    r0 = (half * 11 + jj) * 128
                    sg, tsg = ustg.next()
                    sgv = sg[:].rearrange("p k n -> p (k n)")[:, 0:DM]
                    fw.dma("sp", sgv, down[r0:r0 + 128, :], writes=[tsg])
                    fw.op("pool", lambda e, sgv=sgv, jj=jj: e.tensor_copy(out=dwb[:, jj, :], in_=sgv), reads=[tsg], writes=[tdw])
                ubr = Rot([Pu.sb([128, 8, 2, 256], BF16, "ub") for _ in range(2)])

                def issue_load(g, half=half):
                    jj0, ng = GROUPS[g]
                    ub, tub = ubr.next()
                    for wh in range(2):
                        sg, tsg = ustg.next()
                        c0 = wh * 2816 + (half * 11 + jj0) * 128
                        fw.dma("sp", sg[:, :, :ng * 128], upv[:, :, c0:c0 + ng * 128], writes=[tsg])
                        fw.op("pool", lambda e, sg=sg, ub=ub, wh=wh, ng=ng: e.tensor_copy(out=ub[:, :, wh, :ng * 128], in_=sg[:, :, :ng * 128]),
                              reads=[tsg], writes=[tub])
                    return ub, tub
                loaded = issue_load(0)
                for g, (jj0, ng) in enumerate(GROUPS):
                    ub, tub = loaded
                    if g + 1 < len(GROUPS):
                        loaded = issue_load(g + 1)
                    for jl in range(ng):
                        jj = jj0 + jl
                        j = half * 11 + jj
                        for wh in range(2):
                            jc = wh * 22 + j
                            ucur, tucur = urot[wh].next()
                            for si, (lo, hi, ic) in enumerate(SEGS):
                                n = hi - lo
                                ps, tps = psr.next()
                                for k in range(8):
                                    fw.op("pe", lambda e, ps=ps, k=k, wh=wh, ub=ub, jl=jl, lo=lo, hi=hi, n=n: e.matmul(
                                        ps[:, :n], lhsT=ub[:, k, wh, jl * 128:(jl + 1) * 128], rhs=h2[:, k, lo:hi], start=(k == 0), stop=(k == 7)),
                                        reads=[tub, th2], writes=[tps])
                                fw.op("act", lambda e, ps=ps, ucur=ucur, lo=lo, hi=hi, n=n: e.copy(out=ucur[:, lo:hi], in_=ps[:, :n]),
                                      reads=[tps], writes=[tucur])
                            fw.op("act", lambda e, wh=wh, jc=jc, ucur=ucur: e.activation(out=y[wh][:], in_=ucur[:], func=AF.Identity,
                                                                                        scale=cws[:, jc, 1:2], bias=cbs[:, jc:jc + 1]),
                                  reads=[tucur, tcw], writes=[ty[wh]])
                            for (lo, hi) in RANGES:
                                fw.op("dve", lambda e, wh=wh, jc=jc, lo=lo, hi=hi, ucur=ucur: e.scalar_tensor_tensor(
                                    out=y[wh][:, lo + 1:hi], in0=ucur[:, lo:hi - 1], scalar=cws[:, jc, 0:1], in1=y[wh][:, lo + 1:hi],
                                    op0=ALU.mult, op1=ALU.add), reads=[tucur, tcw, ty[wh]], writes=[ty[wh]])
                                fw.op("dve", lambda e, wh=wh, jc=jc, lo=lo, hi=hi, ucur=ucur: e.scalar_tensor_tensor(
                                    out=y[wh][:, lo:hi - 1], in0=ucur[:, lo + 1:hi], scalar=cws[:, jc, 2:3], in1=y[wh][:, lo:hi - 1],
                                    op0=ALU.mult, op1=ALU.add), reads=[tucur, tcw, ty[wh]], writes=[ty[wh]])
                        fw.op("act", lambda e: e.activation(out=y[0][:], in_=y[0][:], func=AF.Silu), reads=[ty[0]], writes=[ty[0]])
                        fw.op("dve", lambda e, jj=jj: e.tensor_tensor(out=hid[:, jj, :], in0=y[0][:], in1=y[1][:], op=ALU.mult),
                              reads=[ty[0], ty[1]], writes=[thid])
                fw.barrier()
            with ExitStack() as esd:
                Pd = Pool_(fw, esd)
                xr = Rot([Pd.sb([128, 8, 512], F32) for _ in range(2)])
                for (lo, hi, ic) in SEGS:
                    n = hi - lo
                    xs, txs = xr.next()
                    src = X1v if half == 0 else X2v
                    fw.dma("sp", xs[:, :, :n], src[:, :, lo:hi], reads=([tx2] if half else []), writes=[txs])
                    for d in range(8):
                        ps, tps = psr.next()
                        for jj in range(11):
                            fw.op("pe", lambda e, ps=ps, jj=jj, d=d, lo=lo, hi=hi, n=n: e.matmul(
                                ps[:, :n], lhsT=dwb[:, jj, d * 128:(d + 1) * 128], rhs=hid[:, jj, lo:hi], start=(jj == 0), stop=(jj == 10)),
                                reads=[tdw, thid], writes=[tps])
                        fw.op("dve", lambda e, ps=ps, d=d, n=n, ic=ic, xs=xs: e.scalar_tensor_tensor(
                            out=xs[:, d, :n], in0=ps[:, :n], scalar=MODs[:, 40 + d, ic:ic + 1], in1=xs[:, d, :n], op0=ALU.mult, op1=ALU.add),
                            reads=[tps, tmod, txs], writes=[txs])
                    for k in range(8):
                        fw.dma("sp", XT2[k * 128:(k + 1) * 128, lo:hi], xs[:, k, :n], reads=[txs], writes=[tx2])
                fw.barrier()


def stage_p6(fw, io):
    XT = io("XT", [DM, T], F32, "in")
    fg = io("final_g", [DM], F32, "in")
    ident = io("ident", [128, 128], F32, "in")
    OUT = io("OUT", [NLAT, DM], F32, "out")
    XTv = XT.rearrange("(k p) t -> p k t", p=128)
    with ExitStack() as es:
        P = Pool_(fw, es)
        idt = P.sb([128, 128], F32)
        tid = Tok()
        fw.dma("sp", idt[:], ident, writes=[tid])
        g = P.sb([128, 8], F32)
        fw.dma("sp", g[:], fg.rearrange("(k p) -> p k", p=128), writes=[tid], allow_slow_non_contiguous=True)
        st = norm_state(fw, P)
        xr = Rot([P.sb([128, 8, 512], F32) for _ in range(2)])
        yr = Rot([P.sb([128, 8, 512], F32) for _ in range(2)])
        psr = Rot([P.ps() for _ in range(4)])
        orr = Rot([P.sb([128, DM], F32) for _ in range(3)])
        tout = Tok()
        for (lo, hi, ic) in SEGS[1:]:
            n = hi - lo
            xs, txs = xr.next()
            fw.dma("sp", xs[:, :, :n], XTv[:, :, lo:hi], writes=[txs])
            sq, ones = st["sq"], st["ones"]
            fw.op("act", lambda e, xs=xs: e.activation(out=sq[:], in_=xs[:], func=AF.Square), reads=[txs], writes=[st["tsq"]])
            ps, tps = st["psr"].next()
            for k in range(8):
                fw.op("pe", lambda e, ps=ps, k=k: e.matmul(ps[:], lhsT=ones[:], rhs=sq[:, k, :], start=(k == 0), stop=(k == 7)),
                      reads=[st["tsq"], st["tones"]], writes=[tps])
            rstd_from_ps(fw, st["rs"], st["trs"], ps, tps, n, 1.0 / DM, st["eps"][:, 0:1], st["teps"])
            ys, tys = yr.next()
            for k in range(8):
                fw.op("dve", lambda e, ys=ys, xs=xs, k=k: e.scalar_tensor_tensor(
                    out=ys[:, k, :], in0=xs[:, k, :], scalar=g[:, k:k + 1], in1=st["rs"][:], op0=ALU.mult, op1=ALU.mult),
                    reads=[txs, st["trs"], tid], writes=[tys])
            for blk in range(4):
                ot, tot = orr.next()
                for half in range(2):
                    ps2, tps2 = psr.next()
                    for k in range(4):
                        kk = half * 4 + k
                        fw.op("pe", lambda e, ps2=ps2, k=k, kk=kk, ys=ys, blk=blk: e.transpose(
                            ps2[:, k * 128:(k + 1) * 128], ys[:, kk, blk * 128:(blk + 1) * 128], idt[:]),
                            reads=[tys, tid], writes=[tps2])
                    if half == 0:
                        fw.op("dve", lambda e, ot=ot, ps2=ps2: e.tensor_copy(out=ot[:, 0:512], in_=ps2[:]), reads=[tps2], writes=[tot])
                    else:
                        fw.op("act", lambda e, ot=ot, ps2=ps2: e.copy(out=ot[:, 512:1024], in_=ps2[:]), reads=[tps2], writes=[tot])
                r0 = lo - NCTX + blk * 128
                fw.dma("sp", OUT[r0:r0 + 128, :], ot[:], reads=[tot], writes=[tout])
        fw.barrier()


def sin_reduced(fw, P, out, src, tsrc, shape, off, tout):
    ti = P.sb(shape, I32)
    tf = P.sb(shape, F32)
    tt = Tok()
    fw.op("dve", lambda e: e.tensor_scalar(out=ti[:], in0=src, scalar1=off, scalar2=1.0 / (2 * math.pi), op0=ALU.add, op1=ALU.mult),
          reads=[tsrc], writes=[tt])
    fw.op("dve", lambda e: e.tensor_copy(out=tf[:], in_=ti[:]), reads=[tt], writes=[tt])
    fw.op("dve", lambda e: e.scalar_tensor_tensor(out=tf[:], in0=tf[:], scalar=-2 * math.pi, in1=src, op0=ALU.mult, op1=ALU.add),
          reads=[tt, tsrc], writes=[tt])
    fw.op("dve", lambda e: e.tensor_scalar(out=tf[:], in0=tf[:], scalar1=off, scalar2=math.pi, op0=ALU.add, op1=ALU.min), reads=[tt], writes=[tt])
    fw.op("dve", lambda e: e.tensor_scalar(out=tf[:], in0=tf[:], scalar1=-math.pi, scalar2=None, op0=ALU.max), reads=[tt], writes=[tt])
    fw.op("act", lambda e: e.activation(out=out, in_=tf[:], func=AF.Sin), reads=[tt], writes=[tout])


def stage_p2(fw, io):
    ZT = io("ZT", [2048, TE], F32, "in")
    a_re = io("s5_a_re", [2, 16, 64], F32, "in")
    a_im = io("s5_a_im", [2, 16, 64], F32, "in")
    lstep = io("s5_log_step", [2, 16], F32, "in")
    b_re = io("s5_b_re", [2, 16, 64, 16], F32, "in")
    b_im = io("s5_b_im", [2, 16, 64, 16], F32, "in")
    c_re = io("s5_c_re", [2, 16, 16, 64], F32, "in")
    c_im = io("s5_c_im", [2, 16, 16, 64], F32, "in")
    dsk = io("s5_d", [256], F32, "in")
    glu_w = io("s5_glu_w", [256, 256], F32, "in")
    glu_b = io("s5_glu_b", [256], F32, "in")
    out_g = io("s5_out_g", [256], F32, "in")
    S5T = io("S5T", [256, T], F32, "out")
    N = T
    with ExitStack() as es:
        P = Pool_(fw, es)
        are = P.sb([128, 2, 8], F32)
        aim = P.sb([128, 2, 8], F32)
        lst = P.sb([128, 2, 8], F32)
        tpar = Tok()
        for di in range(2):
            fw.dma("sp", are[:, di, :], a_re[di].rearrange("(s g) p -> (g p) s", g=2), writes=[tpar], allow_slow_non_contiguous=True)
            fw.dma("sp", aim[:, di, :], a_im[di].rearrange("(s g) p -> (g p) s", g=2), writes=[tpar], allow_slow_non_contiguous=True)
            for g2 in range(2):
                fw.dma("sp", lst[g2 * 64:(g2 + 1) * 64, di:di + 1, :],
                       lstep[di].rearrange("(s g) -> g s", g=2)[g2:g2 + 1, :].partition_broadcast(64), writes=[tpar],
                       allow_slow_non_contiguous=True)
        sh = [128, 2, 8]
        step = P.sb(sh, F32)
        fw.op("act", lambda e: e.activation(out=step[:], in_=lst[:], func=AF.Exp), reads=[tpar], writes=[tpar])
        er = P.sb(sh, F32)
        th = P.sb(sh, F32)
        fw.op("dve", lambda e: e.tensor_tensor(out=er[:], in0=are[:], in1=step[:], op=ALU.mult), reads=[tpar], writes=[tpar])
        fw.op("act", lambda e: e.activation(out=er[:], in_=er[:], func=AF.Exp), reads=[tpar], writes=[tpar])
        fw.op("dve", lambda e: e.tensor_tensor(out=th[:], in0=aim[:], in1=step[:], op=ALU.mult), reads=[tpar], writes=[tpar])
        sn = P.sb(sh, F32)
        cs = P.sb(sh, F32)
        ttrig = Tok()
        sin_reduced(fw, P, sn[:], th[:], tpar, sh, 0.0, ttrig)
        sin_reduced(fw, P, cs[:], th[:], tpar, sh, math.pi / 2, ttrig)
        PW = P.sb([128, 9, 3, 2, 8], F32)
        tpw = Tok()
        fw.op("dve", lambda e: e.tensor_tensor(out=PW[:, 0, 0], in0=er[:], in1=cs[:], op=ALU.mult), reads=[tpar, ttrig], writes=[tpw])
        fw.op("dve", lambda e: e.tensor_tensor(out=PW[:, 0, 1], in0=er[:], in1=sn[:], op=ALU.mult), reads=[tpar, ttrig], writes=[tpw])
        t1 = P.sb(sh, F32)
        t2 = P.sb(sh, F32)
        for k in range(9):
            fw.op("dve", lambda e, k=k: e.tensor_scalar(out=PW[:, k, 2], in0=PW[:, k, 1], scalar1=-1.0, scalar2=None, op0=ALU.mult),
                  reads=[tpw], writes=[tpw])
            if k == 8:
                break
            fw.op("dve", lambda e, k=k: e.tensor_tensor(out=t1[:], in0=PW[:, k, 0], in1=PW[:, k, 0], op=ALU.mult), reads=[tpw], writes=[tpw])
            fw.op("dve", lambda e, k=k: e.tensor_tensor(out=t2[:], in0=PW[:, k, 1], in1=PW[:, k, 1], op=ALU.mult), reads=[tpw], writes=[tpw])
            fw.op("dve", lambda e, k=k: e.tensor_tensor(out=PW[:, k + 1, 0], in0=t1[:], in1=t2[:], op=ALU.subtract), reads=[tpw], writes=[tpw])
            fw.op("dve", lambda e, k=k: e.scalar_tensor_tensor(out=PW[:, k + 1, 1], in0=PW[:, k, 0], scalar=2.0, in1=PW[:, k, 1],
                                                               op0=ALU.mult, op1=ALU.mult), reads=[tpw], writes=[tpw])
        br = P.sb(sh, F32)
        bi = P.sb(sh, F32)
        nbi = P.sb(sh, F32)
        den = P.sb(sh, F32)
        nr = P.sb(sh, F32)
        tb = Tok()
        fw.op("dve", lambda e: e.tensor_tensor(out=den[:], in0=are[:], in1=are[:], op=ALU.mult), reads=[tpar], writes=[tb])
        fw.op("dve", lambda e: e.tensor_tensor(out=t1[:], in0=aim[:], in1=aim[:], op=ALU.mult), reads=[tpar, tpw], writes=[tpw])
        fw.op("dve", lambda e: e.tensor_tensor(out=den[:], in0=den[:], in1=t1[:], op=ALU.add), reads=[tb, tpw], writes=[tb])
        fw.op("dve", lambda e: e.reciprocal(out=den[:], in_=den[:]), reads=[tb], writes=[tb])
        fw.op("dve", lambda e: e.tensor_scalar(out=nr[:], in0=PW[:, 0, 0], scalar1=-1.0, scalar2=None, op0=ALU.add), reads=[tpw], writes=[tb])
        fw.op("dve", lambda e: e.tensor_tensor(out=t1[:], in0=nr[:], in1=are[:], op=ALU.mult), reads=[tb, tpar, tpw], writes=[tpw])
        fw.op("dve", lambda e: e.tensor_tensor(out=t2[:], in0=PW[:, 0, 1], in1=aim[:], op=ALU.mult), reads=[tpw, tpar], writes=[tpw])
        fw.op("dve", lambda e: e.tensor_tensor(out=t1[:], in0=t1[:], in1=t2[:], op=ALU.add), reads=[tpw], writes=[tpw])
        fw.op("dve", lambda e: e.tensor_tensor(out=br[:], in0=t1[:], in1=den[:], op=ALU.mult), reads=[tpw, tb], writes=[tb])
        fw.op("dve", lambda e: e.tensor_tensor(out=t1[:], in0=PW[:, 0, 1], in1=are[:], op=ALU.mult), reads=[tpw, tpar, tb], writes=[tpw])
        fw.op("dve", lambda e: e.tensor_tensor(out=t2[:], in0=nr[:], in1=aim[:], op=ALU.mult), reads=[tb, tpar, tpw], writes=[tpw])
        fw.op("dve", lambda e: e.tensor_tensor(out=t1[:], in0=t1[:], in1=t2[:], op=ALU.subtract), reads=[tpw], writes=[tpw])
        fw.op("dve", lambda e: e.tensor_tensor(out=bi[:], in0=t1[:], in1=den[:], op=ALU.mult), reads=[tpw, tb], writes=[tb])
        fw.op("dve", lambda e: e.tensor_scalar(out=nbi[:], in0=bi[:], scalar1=-1.0, scalar2=None, op0=ALU.mult), reads=[tb], writes=[tb])
        BTb = P.sb([128, 2, 2, 8, 128], BF16)
        CTb = P.sb([128, 2, 2, 8, 128], BF16)
        tBT = Tok()
        tCT = Tok()
        with ExitStack() as es2:
            P2 = Pool_(fw, es2)
            BTf = P2.sb([128, 2, 2, 8, 128], F32)
            CTf = P2.sb([128, 2, 2, 8, 128], F32)
            CT2 = P2.sb([128, 2, 2, 8, 128], F32)
            tf1, tf2 = Tok(), Tok()
            fw.op("pool", lambda e: e.memset(BTf[:], 0.0), writes=[tf1])
            fw.op("pool", lambda e: e.memset(CTf[:], 0.0), writes=[tf2])
            for di in range(2):
                for ri, (bsrc, csrc) in enumerate(((b_re, c_re), (b_im, c_im))):
                    for g in range(16):
                        s, g2 = g // 2, g % 2
                        r0 = (g % 8) * 16
                        fw.dma("sp", BTf[r0:r0 + 16, di, ri, s, g2 * 64:(g2 + 1) * 64], bsrc[di, g].rearrange("p h -> h p"),
                               writes=[tf1], allow_slow_non_contiguous=True)
                        fw.dma("sp", CTf[g2 * 64:(g2 + 1) * 64, di, ri, s, r0:r0 + 16], csrc[di, g].rearrange("h p -> p h"),
                               writes=[tf2], allow_slow_non_contiguous=True)
            fw.op("act", lambda e: e.copy(out=BTb[:], in_=BTf[:]), reads=[tf1], writes=[tBT])
            bsh = [128, 2, 8, 128]
            brb = br[:].unsqueeze(3).to_broadcast(bsh)
            nbib = nbi[:].unsqueeze(3).to_broadcast(bsh)
            tc2 = Tok()
            fw.op("dve", lambda e: e.tensor_tensor(out=CT2[:, :, 0], in0=CTf[:, :, 0], in1=brb, op=ALU.mult), reads=[tf2, tb], writes=[tc2])
            fw.op("pool", lambda e: e.tensor_tensor(out=CT2[:, :, 1], in0=CTf[:, :, 1], in1=nbib, op=ALU.mult), reads=[tf2, tb], writes=[tc2])
            fw.op("dve", lambda e: e.tensor_tensor(out=CT2[:, :, 0], in0=CT2[:, :, 0], in1=CT2[:, :, 1], op=ALU.add), reads=[tc2], writes=[tc2])
            fw.op("act", lambda e: e.copy(out=CTb[:, :, 0], in_=CT2[:, :, 0]), reads=[tc2], writes=[tCT])
            fw.op("dve", lambda e: e.tensor_tensor(out=CT2[:, :, 0], in0=CTf[:, :, 0], in1=nbib, op=ALU.mult), reads=[tf2, tb, tCT, tc2], writes=[tc2])
            fw.op("pool", lambda e: e.tensor_tensor(out=CT2[:, :, 1], in0=CTf[:, :, 1], in1=brb, op=ALU.mult), reads=[tf2, tb, tc2], writes=[tc2])
            fw.op("dve", lambda e: e.tensor_tensor(out=CT2[:, :, 0], in0=CT2[:, :, 0], in1=CT2[:, :, 1], op=ALU.subtract), reads=[tc2], writes=[tc2])
            fw.op("act", lambda e: e.copy(out=CTb[:, :, 1], in_=CT2[:, :, 0]), reads=[tc2], writes=[tCT])
            fw.barrier()
        ub = P.sb([128, 2, TE], BF16)
        tub = Tok()
        for ct in range(2):
            fw.dma("pool", ub[:, ct, :], ZT[ct * 128:(ct + 1) * 128, :], writes=[tub])
        dk = P.sb([128, 2], F32)
        tdk = Tok()
        fw.dma("sp", dk[:], dsk.rearrange("(c p) -> p c", p=128), writes=[tdk], allow_slow_non_contiguous=True)
        y = P.sb([128, 2, T], F32)
        ty = Tok()
        for ct in range(2):
            fw.dma("sp", y[:, ct, :], ZT[ct * 128:(ct + 1) * 128, 0:T], writes=[ty])
        for ct in range(2):
            fw.op("pool", lambda e, ct=ct: e.tensor_scalar(out=y[:, ct, :], in0=y[:, ct, :], scalar1=dk[:, ct:ct + 1], scalar2=None, op0=ALU.mult),
                  reads=[ty, tdk], writes=[ty])
        Xs = Rot([P.sb([128, 2, N], F32) for _ in range(2)])
        xbs = Rot([P.sb([128, 2, N], BF16) for _ in range(2)])
        psr = Rot([P.ps() for _ in range(4)])
        psy = Rot([P.ps() for _ in range(2)])
        tmpy = Rot([P.sb([128, 512], F32) for _ in range(2)])
        CH = [(0, 512), (512, 1024), (1024, 1536), (1536, 2048), (2048, 2304)]
        for di in range(2):
            off = 0 if di == 0 else 256
            for s in range(8):
                ct = s // 4
                X, tX = Xs.next()
                for ri in range(2):
                    for ci, (c0, c1) in enumerate(CH):
                        n = c1 - c0
                        ps, tps = psr.next()
                        fw.op("pe", lambda e, ps=ps, di=di, ri=ri, s=s, ct=ct, c0=c0, c1=c1, n=n, off=off: e.matmul(
                            ps[:, :n], lhsT=BTb[:, di, ri, s, :], rhs=ub[:, ct, off + c0:off + c1], start=True, stop=True),
                            reads=[tBT, tub], writes=[tps])
                        fw.op("act", lambda e, ps=ps, X=X, ri=ri, c0=c0, c1=c1, n=n: e.copy(out=X[:, ri, c0:c1], in_=ps[:, :n]),
                              reads=[tps], writes=[tX])

                def cstep(w_re, w_im, r_re, r_im, k, di=di, s=s, tX=tX):
                    pr = PW[:, k, 0, di, s:s + 1]
                    pi = PW[:, k, 1, di, s:s + 1]
                    npi = PW[:, k, 2, di, s:s + 1]
                    for (o, i0, sc) in ((w_re, r_re, pr), (w_re, r_im, npi), (w_im, r_im, pr), (w_im, r_re, pi)):
                        fw.op("dve", lambda e, o=o, i0=i0, sc=sc: e.scalar_tensor_tensor(out=o, in0=i0, scalar=sc, in1=o, op0=ALU.mult, op1=ALU.add),
                              reads=[tX, tpw], writes=[tX])
                for k in range(8):
                    st_ = 1 << k
                    Xv = [X[:, ri, :].rearrange("p (m c) -> p m c", c=2 * st_) for ri in range(2)]
                    if di == 0:
                        cstep(Xv[0][:, :, 2 * st_ - 1], Xv[1][:, :, 2 * st_ - 1], Xv[0][:, :, st_ - 1], Xv[1][:, :, st_ - 1], k)
                    else:
                        cstep(Xv[0][:, :, 0], Xv[1][:, :, 0], Xv[0][:, :, st_], Xv[1][:, :, st_], k)
                for i in (range(1, 9) if di == 0 else range(7, -1, -1)):
                    if di == 0:
                        w, r = 256 * i + 255, 256 * (i - 1) + 255
                    else:
                        w, r = 256 * i, 256 * (i + 1)
                    cstep(X[:, 0, w:w + 1], X[:, 1, w:w + 1], X[:, 0, r:r + 1], X[:, 1, r:r + 1], 8)
                for k in range(7, -1, -1):
                    st_ = 1 << k
                    Xv = [X[:, ri, :].rearrange("p (m c) -> p m c", c=2 * st_) for ri in range(2)]
                    if di == 0:
                        cstep(Xv[0][:, 1:, st_ - 1], Xv[1][:, 1:, st_ - 1], Xv[0][:, :-1, 2 * st_ - 1], Xv[1][:, :-1, 2 * st_ - 1], k)
                    else:
                        cstep(Xv[0][:, :-1, st_], Xv[1][:, :-1, st_], Xv[0][:, 1:, 0], Xv[1][:, 1:, 0], k)
                xb, txb = xbs.next()
                fw.op("act", lambda e, xb=xb, X=X: e.copy(out=xb[:], in_=X[:]), reads=[tX], writes=[txb])
                for (c0, c1) in CH:
                    n = c1 - c0
                    if di == 0:
                        y0 = c0
                    else:
                        y0 = c0 + 256 if c0 < 2048 else 0
                    ps, tps = psy.next()
                    for ri in range(2):
                        fw.op("pe", lambda e, ps=ps, xb=xb, di=di, ri=ri, s=s, c0=c0, c1=c1, n=n: e.matmul(
                            ps[:, :n], lhsT=CTb[:, di, ri, s, :], rhs=xb[:, ri, c0:c1], start=(ri == 0), stop=(ri == 1)),
                            reads=[tCT, txb], writes=[tps])
                    tm, ttm = tmpy.next()
                    fw.op("act", lambda e, tm=tm, ps=ps, n=n: e.copy(out=tm[:, :n], in_=ps[:, :n]), reads=[tps], writes=[ttm])
                    fw.op("pool", lambda e, tm=tm, ct=ct, y0=y0, n=n: e.tensor_tensor(out=y[:, ct, y0:y0 + n], in0=y[:, ct, y0:y0 + n], in1=tm[:, :n], op=ALU.add),
                          reads=[ttm, ty], writes=[ty])
        tg = Tok()
        gw = load_w_bf16(fw, P, glu_w, 256, 256, tg, "gluw")
        gb = P.sb([128, 2], F32)
        og = P.sb([128, 2], F32)
        fw.dma("sp", gb[:], glu_b.rearrange("(c p) -> p c", p=128), writes=[tg], allow_slow_non_contiguous=True)
        fw.dma("sp", og[:], out_g.rearrange("(c p) -> p c", p=128), writes=[tg], allow_slow_non_contiguous=True)
        ones = P.sb([128, 128], BF16)
        eps = P.sb([128, 1], F32)
        fw.op("pool", lambda e: e.memset(ones[:], 1.0), writes=[tg])
        fw.op("pool", lambda e: e.memset(eps[:], RMS_EPS), writes=[tg])
        a32 = P.sb([128, 2, 512], F32)
        ab = P.sb([128, 2, 512], BF16)
        w1 = P.sb([128, 2, 512], F32)
        w2 = P.sb([128, 2, 512], F32)
        sqb = P.sb([128, 2, 512], BF16)
        rs = P.sb([128, 512], F32)
        ta, tw1, trs = Tok(), Tok(), Tok()
        stg = Rot([P.sb([128, 512], F32) for _ in range(2)])
        tout = Tok()
        C1 = math.sqrt(2.0 / math.pi)
        for (lo, hi, ic) in SEGS:
            n = hi - lo
            yv = y[:, :, lo:hi]
            fw.op("act", lambda e, yv=yv, n=n: e.activation(out=w1[:, :, :n], in_=yv, func=AF.Square), reads=[ty], writes=[tw1])
            fw.op("dve", lambda e, n=n: e.tensor_scalar(out=w1[:, :, :n], in0=w1[:, :, :n], scalar1=0.044715 * C1, scalar2=C1, op0=ALU.mult, op1=ALU.add),
                  reads=[tw1], writes=[tw1])
            fw.op("dve", lambda e, yv=yv, n=n: e.tensor_tensor(out=w1[:, :, :n], in0=w1[:, :, :n], in1=yv, op=ALU.mult), reads=[tw1, ty], writes=[tw1])
            fw.op("act", lambda e, n=n: e.activation(out=w1[:, :, :n], in_=w1[:, :, :n], func=AF.Tanh), reads=[tw1], writes=[tw1])
            fw.op("dve", lambda e, n=n: e.tensor_scalar(out=w1[:, :, :n], in0=w1[:, :, :n], scalar1=0.5, scalar2=0.5, op0=ALU.mult, op1=ALU.add),
                  reads=[tw1], writes=[tw1])
            fw.op("dve", lambda e, yv=yv, n=n: e.tensor_tensor(out=a32[:, :, :n], in0=w1[:, :, :n], in1=yv, op=ALU.mult), reads=[tw1, ty], writes=[ta])
            fw.op("act", lambda e, n=n: e.copy(out=ab[:, :, :n], in_=a32[:, :, :n]), reads=[ta], writes=[ta])
            for nt in range(2):
                ps, tps = psr.next()
                for k in range(2):
                    fw.op("pe", lambda e, ps=ps, k=k, nt=nt, n=n: e.matmul(ps[:, :n], lhsT=gw[:, k, nt * 128:(nt + 1) * 128], rhs=ab[:, k, :n],
                                                                      start=(k == 0), stop=(k == 1)), reads=[tg, ta], writes=[tps])
                fw.op("act", lambda e, ps=ps, nt=nt, n=n: e.activation(out=w2[:, nt, :n], in_=ps[:, :n], func=AF.Sigmoid, bias=gb[:, nt:nt + 1]),
                      reads=[tps, tg], writes=[tw1])
            fw.op("dve", lambda e, n=n: e.tensor_tensor(out=w2[:, :, :n], in0=w2[:, :, :n], in1=a32[:, :, :n], op=ALU.mult), reads=[tw1, ta], writes=[tw1])
            fw.op("act", lambda e, n=n: e.activation(out=sqb[:, :, :n], in_=w2[:, :, :n], func=AF.Square), reads=[tw1], writes=[tw1])
            ps, tps = psr.next()
            for k in range(2):
                fw.op("pe", lambda e, ps=ps, k=k, n=n: e.matmul(ps[:, :n], lhsT=ones[:], rhs=sqb[:, k, :n], start=(k == 0), stop=(k == 1)),
                      reads=[tw1, tg], writes=[tps])
            rstd_from_ps(fw, rs, trs, ps, tps, n, 1.0 / 256, eps[:, 0:1], tg)
            for k in range(2):
                sg, tsg = stg.next()
                fw.op("dve", lambda e, sg=sg, k=k, n=n: e.scalar_tensor_tensor(out=sg[:, :n], in0=w2[:, k, :n], scalar=og[:, k:k + 1], in1=rs[:, :n],
                                                                             op0=ALU.mult, op1=ALU.mult), reads=[tw1, trs, tg], writes=[tsg])
                fw.dma("sp", S5T[k * 128:(k + 1) * 128, lo:hi], sg[:, :n], reads=[tsg], writes=[tout])
        fw.barrier()


RW_BASE = 1024
CHK = 64
NCH = T // CHK
LN_EPS_RW = 64e-5


def rw_consts():
    idx = np.arange(64)
    m = np.zeros((64, 4, 64), np.float32)
    m[:, 0, :] = (idx[:, None] < idx[None, :])
    m[:, 1, :] = (idx[:, None] > idx[None, :])
    m[:, 2, :] = (idx[:, None] <= idx[None, :])
    m[:, 3, :] = (idx[:, None] >= idx[None, :])
    bd = np.zeros((128, 128), np.float32)
    bd[:64, :64] = 1.0
    bd[64:, 64:] = 1.0
    return m, bd


def stage_p4(fw, io):
    ZT = io("ZT", [2048, TE], F32, "in")
    mu = io("rw_mu", [960], F32, "in")
    w0 = io("rw_w0", [2, 256], F32, "in")
    w2 = io("rw_w2", [2, 32, 256], F32, "in")
    a0 = io("rw_a0", [2, 256], F32, "in")
    a2 = io("rw_a2", [2, 32, 256], F32, "in")
    g2 = io("rw_g2", [64, 256], F32, "in")
    k_k = io("rw_k_k", [256], F32, "in")
    k_a = io("rw_k_a", [256], F32, "in")
    r_k = io("rw_r_k", [256], F32, "in")
    ln_g = io("rw_ln_g", [256], F32, "in")
    ln_b = io("rw_ln_b", [256], F32, "in")
    ident = io("ident", [128, 128], F32, "in")
    MASKS = io("RWMASK", [64, 4, 64], F32, "in")
    BD = io("RWBD", [128, 128], F32, "in")
    RWT = io("RWT", [256, T], F32, "out")
    N = T
    CH5 = [(0, 512), (512, 1024), (1024, 1536), (1536, 2048), (2048, 2560)]
    with ExitStack() as es:
        P = Pool_(fw, es)
        tc = Tok()
        idt = P.sb([128, 128], F32)
        idb = P.sb([128, 128], BF16)
        msk = P.sb([64, 4, 64], F32)
        bd1 = P.sb([128, 128], F32)
        fw.dma("sp", idt[:], ident, writes=[tc])
        fw.dma("sp", msk[:], MASKS, writes=[tc])
        fw.dma("sp", bd1[:], BD, writes=[tc])
        fw.op("dve", lambda e: e.tensor_copy(out=idb[:], in_=idt[:]), reads=[tc], writes=[tc])
        mrep = P.sb([64, 4, 4, 64], F32)
        for rep in range(4):
            fw.op("dve", lambda e, rep=rep: e.tensor_copy(out=mrep[:, :, rep, :], in_=msk[:]), reads=[tc], writes=[tc])
        pp = P.sb([128, 12, 2], F32)
        tpp = Tok()
        srcs = [w0[0], w0[1], a0[0], a0[1], k_k, k_a, k_a, r_k, ln_g, ln_b]
        for i, sap in enumerate(srcs):
            fw.dma("sp", pp[:, i, :], sap.rearrange("(c p) -> p c", p=128), writes=[tpp], allow_slow_non_contiguous=True)
        fw.op("dve", lambda e: e.tensor_scalar(out=pp[:, 6, :], in0=pp[:, 6, :], scalar1=-1.0, scalar2=1.0, op0=ALU.mult, op1=ALU.add),
              reads=[tpp], writes=[tpp])
        epsl = P.sb([128, 2], F32)
        fw.op("pool", lambda e: e.memset(epsl[:, 0:1], LN_EPS_RW), writes=[tpp])
        fw.op("pool", lambda e: e.memset(epsl[:, 1:2], 1e-24), writes=[tpp])
        wA = P.sb([128, 256], BF16)
        wB = P.sb([128, 256], BF16)
        tlw = Tok()
        fw.dma("pool", wA[0:32, :], w2[0], writes=[tlw])
        fw.dma("pool", wA[32:64, :], w2[1], writes=[tlw])
        fw.dma("pool", wA[64:96, :], a2[0], writes=[tlw])
        fw.dma("pool", wB[0:32, :], a2[1], writes=[tlw])
        fw.dma("pool", wB[64:128, :], g2, writes=[tlw])
        smask = P.sb([128, N], BF16)
        tsm = Tok()
        fw.op("pool", lambda e: e.memset(smask[:], 1.0), writes=[tsm])
        fw.op("pool", lambda e: e.memset(smask[:].rearrange("p (c j) -> p c j", j=CHK)[:, :, 0], 0.0), writes=[tsm])

        def load_shift(dst, tdst, row0, rows, P2, post=None, pb=0):
            zt_ = P2.sb([128, TE], F32)
            nbt_ = P2.sb([128, TE], F32)
            mtt_ = P2.sb([128, 2], F32)
            tz, tnb, tm = Tok(), Tok(), Tok()
            ps_ = slice(pb, pb + rows)
            z = zt_[ps_, :]
            fw.dma("sp", z, ZT[RW_BASE + row0:RW_BASE + row0 + rows, :], writes=[tz])
            fw.dma("sp", mtt_[ps_, 0:1], mu[row0:row0 + rows].rearrange("(p o) -> p o", o=1), writes=[tm])
            fw.op("dve", lambda e: e.tensor_scalar(out=mtt_[ps_, 1:2], in0=mtt_[ps_, 0:1], scalar1=0.5, scalar2=None, op0=ALU.mult), reads=[tm], writes=[tm])
            fw.op("dve", lambda e: e.tensor_scalar(out=mtt_[ps_, 0:1], in0=mtt_[ps_, 0:1], scalar1=-1.0, scalar2=1.0, op0=ALU.mult, op1=ALU.add),
                  reads=[tm], writes=[tm])
            fw.op("pool", lambda e: e.memset(nbt_[ps_, 0:1], 0.0), writes=[tnb])
            fw.op("pool", lambda e: e.tensor_copy(out=nbt_[ps_, 1:TE], in_=zt_[ps_, 0:TE - 1]), reads=[tz], writes=[tnb])
            fw.op("pool", lambda e: e.tensor_tensor(out=nbt_[ps_, 0:TE - 1], in0=nbt_[ps_, 0:TE - 1], in1=zt_[ps_, 1:TE], op=ALU.add),
                  reads=[tz, tnb], writes=[tnb])
            for cb in (256, 2304):
                fw.op("pool", lambda e, cb=cb: e.tensor_tensor(out=nbt_[ps_, cb:cb + 1], in0=nbt_[ps_, cb:cb + 1], in1=zt_[ps_, cb - 1:cb], op=ALU.subtract),
                      reads=[tz, tnb], writes=[tnb])
                fw.op("pool", lambda e, cb=cb: e.tensor_tensor(out=nbt_[ps_, cb - 1:cb], in0=nbt_[ps_, cb - 1:cb], in1=zt_[ps_, cb:cb + 1], op=ALU.subtract),
                      reads=[tz, tnb], writes=[tnb])
            fw.op("act", lambda e: e.activation(out=z, in_=z, func=AF.Identity, scale=mtt_[ps_, 0:1]), reads=[tz, tm], writes=[tz])
            if post is None:
                fw.op("dve", lambda e: e.scalar_tensor_tensor(out=dst, in0=nbt_[ps_, :], scalar=mtt_[ps_, 1:2], in1=z, op0=ALU.mult, op1=ALU.add),
                      reads=[tz, tnb, tm], writes=[tdst])
            else:
                fw.op("dve", lambda e: e.scalar_tensor_tensor(out=z, in0=nbt_[ps_, :], scalar=mtt_[ps_, 1:2], in1=z, op0=ALU.mult, op1=ALU.add),
                      reads=[tz, tnb, tm], writes=[tz])
                fw.op("act", lambda e: e.activation(out=dst, in_=z, func=post), reads=[tz], writes=[tdst])

        lorA = P.sb([128, TE], BF16)
        lorB = P.sb([128, TE], BF16)
        tlor = Tok()
        for i in range(4):
            with ExitStack() as es2:
                dstt = lorA[32 * i:32 * i + 32, :] if i < 3 else lorB[0:32, :]
                load_shift(dstt, tlor, 768 + 32 * i, 32, Pool_(fw, es2), post=(AF.Tanh if i < 2 else AF.Copy), pb=(32 * i if i < 3 else 0))
                fw.barrier()
        with ExitStack() as es2:
            load_shift(lorB[64:128, :], tlor, 896, 64, Pool_(fw, es2), post=AF.Sigmoid, pb=64)
            fw.barrier()

        for c in range(2):
            with ExitStack() as esc:
                Pc = Pool_(fw, esc)
                rr = Pc.sb([128, TE], F32)
                kx = Pc.sb([128, TE], F32)
                vv = Pc.sb([128, TE], F32)
                kk = Pc.sb([128, TE], F32)
                trr, tkx, tvv, tkk = Tok(), Tok(), Tok(), Tok()
                vtok = Pc.sb([64, TE // CHK, 128], BF16)
                tvt = Tok()
                yacc = Pc.sb([128, T], F32)
                bon = Pc.sb([128, T], F32)
                tya, tbon = Tok(), Tok()
                fw.op("pool", lambda e: e.memset(yacc[:], 0.0), writes=[tya])
                fw.op("pool", lambda e: e.memset(bon[:], 0.0), writes=[tbon])
                for (dst, tdst, r0) in ((rr, trr, 0), (kx, tkx, 256), (vv, tvv, 512)):
                    with ExitStack() as es2:
                        load_shift(dst[:], tdst, r0 + c * 128, 128, Pool_(fw, es2))
                        fw.barrier()
                with ExitStack() as es2:
                    P2 = Pool_(fw, es2)
                    sq = P2.sb([128, 512], F32)
                    rn = P2.sb([128, 512], F32)
                    tsq, trn = Tok(), Tok()
                    ps1 = P2.ps()
                    tps1 = Tok()
                    fw.op("dve", lambda e: e.tensor_scalar(out=kk[:], in0=kx[:], scalar1=pp[:, 4, c:c + 1], scalar2=None, op0=ALU.mult),
                          reads=[tkx, tpp], writes=[tkk])
                    for (c0, c1) in CH5:
                        fw.op("act", lambda e, c0=c0, c1=c1: e.activation(out=sq[:], in_=kk[:, c0:c1], func=AF.Square), reads=[tkk], writes=[tsq])
                        fw.op("pe", lambda e: e.matmul(ps1[:], lhsT=bd1[:], rhs=sq[:], start=True, stop=True), reads=[tsq, tc], writes=[tps1])
                        rstd_from_ps(fw, rn, trn, ps1, tps1, 512, 1.0, epsl[:, 1:2], tpp)
                        fw.op("dve", lambda e, c0=c0, c1=c1: e.tensor_tensor(out=kk[:, c0:c1], in0=kk[:, c0:c1], in1=rn[:], op=ALU.mult),
                              reads=[tkk, trn], writes=[tkk])
                    vb = P2.sb([128, TE], BF16)
                    tvb = Tok()
                    fw.op("act", lambda e: e.copy(out=vb[:], in_=vv[:]), reads=[tvv], writes=[tvb])
                    pst = P2.ps([128, 1024], BF16)
                    tpst = Tok()
                    for q in range(TE // CHK // 4):
                        for j in range(4):
                            ch = q * 4 + j
                            fw.op("pe", lambda e, j=j, ch=ch: e.transpose(pst[0:64, j * 128:(j + 1) * 128], vb[:, ch * CHK:(ch + 1) * CHK], idb[:]),
                                  reads=[tvb, tc], writes=[tpst])
                        fw.op("dve", lambda e, q=q: e.tensor_copy(out=vtok[:, q * 4:(q + 1) * 4, :], in_=pst[0:64, 0:512].rearrange("p (j c) -> p j c", j=4)),
                              reads=[tpst], writes=[tvt])
                    fw.barrier()

                for di in range(2):
                    off = 0 if di == 0 else 256
                    with ExitStack() as esd:
                        Pd = Pool_(fw, esd)
                        aT = Pd.sb([128, N], BF16)
                        bT = Pd.sb([128, N], BF16)
                        kT = Pd.sb([128, N], BF16)
                        rT = Pd.sb([128, N], BF16)
                        btok = Pd.sb([64, NCH, 128], BF16)
                        ktok = Pd.sb([64, NCH, 128], BF16)
                        pC = Pd.sb([128, NCH], F32)
                        tops = Tok()
                        ttok = Tok()
                        with ExitStack() as es2:
                            P2 = Pool_(fw, es2)
                            ld = P2.sb([128, TE], F32)
                            kd = P2.sb([128, TE], F32)
                            bb = P2.sb([128, TE], F32)
                            tld, tkd, tbb = Tok(), Tok(), Tok()
                            psr = Rot([P2.ps() for _ in range(3)])
                            tm5 = Rot([P2.sb([128, 512], F32) for _ in range(2)])
                            for (c0, c1) in CH5:
                                ps, tps = psr.next()
                                fw.op("pe", lambda e, ps=ps, c0=c0, c1=c1: e.matmul(ps[:], lhsT=wA[32 * di:32 * di + 32, c * 128:(c + 1) * 128], rhs=lorA[32 * di:32 * di + 32, c0:c1],
                                                                                  start=True, stop=True), reads=[tlw, tlor], writes=[tps])
                                fw.op("act", lambda e, ps=ps, c0=c0, c1=c1: e.activation(out=ld[:, c0:c1], in_=ps[:], func=AF.Sigmoid, bias=pp[:, di, c:c + 1]),
                                      reads=[tps, tpp], writes=[tld])
                                ps, tps = psr.next()
                                fw.op("pe", lambda e, ps=ps, c0=c0, c1=c1: e.matmul(ps[:], lhsT=(wA[64:96, c * 128:(c + 1) * 128] if di == 0 else wB[0:32, c * 128:(c + 1) * 128]),
                                                                                  rhs=(lorA[64:96, c0:c1] if di == 0 else lorB[0:32, c0:c1]),
                                                                                  start=True, stop=True), reads=[tlw, tlor], writes=[tps])
                                fw.op("act", lambda e, ps=ps, c0=c0, c1=c1: e.activation(out=bb[:, c0:c1], in_=ps[:], func=AF.Sigmoid, bias=pp[:, 2 + di, c:c + 1]),
                                      reads=[tps, tpp], writes=[tbb])
                            fw.op("pool", lambda e: e.tensor_scalar(out=ld[:], in0=ld[:], scalar1=-math.exp(-0.5), scalar2=None, op0=ALU.mult), reads=[tld], writes=[tld])
                            fw.op("act", lambda e: e.activation(out=kd[:], in_=bb[:], func=AF.Identity, scale=pp[:, 5, c:c + 1], bias=pp[:, 6, c:c + 1]),
                                  reads=[tbb, tpp], writes=[tkd])
                            fw.op("dve", lambda e: e.tensor_tensor(out=kd[:], in0=kd[:], in1=kx[:], op=ALU.mult), reads=[tkd, tkx], writes=[tkd])
                            fw.op("pool", lambda e: e.tensor_tensor(out=bb[:], in0=bb[:], in1=kk[:], op=ALU.mult), reads=[tbb, tkk], writes=[tbb])
                            for (c0, c1) in CH5:
                                c1 = min(c1, T)
                                n = c1 - c0
                                tm, ttm = tm5.next()
                                fw.op("dve", lambda e, tm=tm, c0=c0, c1=c1, n=n: e.scalar_tensor_tensor(out=tm[:, :n], in0=rr[:, c0:c1], scalar=pp[:, 7, c:c + 1], in1=kd[:, c0:c1],
                                                                                                 op0=ALU.mult, op1=ALU.mult), reads=[trr, tkd, tpp], writes=[ttm])
                                ps, tps = psr.next()
                                fw.op("pe", lambda e, ps=ps, tm=tm, n=n: e.matmul(ps[:, :n], lhsT=bd1[:], rhs=tm[:, :n], start=True, stop=True), reads=[ttm, tc], writes=[tps])
                                tm2, ttm2 = tm5.next()
                                fw.op("dve", lambda e, tm2=tm2, ps=ps, c0=c0, c1=c1, n=n: e.tensor_tensor(out=tm2[:, :n], in0=ps[:, :n], in1=vv[:, c0:c1], op=ALU.mult),
                                      reads=[tps, tvv], writes=[ttm2])
                                fw.op("pool", lambda e, tm2=tm2, c0=c0, c1=c1, n=n: e.tensor_tensor(out=bon[:, c0:c1], in0=bon[:, c0:c1], in1=tm2[:, :n], op=ALU.add),
                                      reads=[ttm2, tbon], writes=[tbon])
                            cs = P2.sb([128, N], F32)
                            ex = P2.sb([128, N], F32)
                            tcs, tex = Tok(), Tok()
                            ldw = ld[:, off:off + N]
                            fw.op("dve", lambda e: e.tensor_tensor_scan(out=cs[:], data0=smask[:], data1=ldw, initial=0.0, op0=ALU.mult, op1=ALU.add),
                                  reads=[tsm, tld], writes=[tcs])
                            csv = cs[:].rearrange("p (c j) -> p c j", j=CHK)
                            tot = P2.sb([128, NCH, 1], F32)
                            ttot = Tok()
                            fw.op("dve", lambda e: e.tensor_copy(out=tot[:], in_=csv[:, :, CHK - 1:CHK]), reads=[tcs], writes=[ttot])
                            totb = tot[:].to_broadcast([128, NCH, CHK])
                            if di == 1:
                                fw.op("dve", lambda e: e.tensor_tensor(out=csv, in0=totb, in1=csv, op=ALU.subtract), reads=[ttot, tcs], writes=[tcs])
                                fw.op("dve", lambda e: e.tensor_tensor(out=cs[:], in0=cs[:], in1=ldw, op=ALU.add), reads=[tcs, tld], writes=[tcs])
                            fw.op("act", lambda e: e.activation(out=pC[:], in_=tot[:, :, 0], func=AF.Exp), reads=[ttot], writes=[tops])
                            kdw, bbw = kd[:, off:off + N], bb[:, off:off + N]
                            rrw, kkw = rr[:, off:off + N], kk[:, off:off + N]
                            fw.op("act", lambda e: e.activation(out=ex[:], in_=cs[:], func=AF.Exp), reads=[tcs], writes=[tex])
                            fw.op("dve", lambda e: e.tensor_tensor(out=rT[:], in0=rrw, in1=ex[:], op=ALU.mult), reads=[trr, tex], writes=[tops])
                            fw.op("act", lambda e: e.activation(out=ex[:], in_=cs[:], func=AF.Exp, scale=-1.0), reads=[tcs, tops], writes=[tex])
                            fw.op("dve", lambda e: e.tensor_tensor(out=bT[:], in0=bbw, in1=ex[:], op=ALU.mult), reads=[tbb, tex], writes=[tops])
                            fw.op("pool", lambda e: e.tensor_tensor(out=kT[:], in0=kdw, in1=ex[:], op=ALU.mult), reads=[tkd, tex], writes=[tops])
                            e3 = ex
                            te3 = tex
                            fw.op("dve", lambda e: e.tensor_tensor(out=e3[:], in0=cs[:], in1=ldw, op=ALU.subtract), reads=[tcs, tld], writes=[te3])
                            fw.op("act", lambda e: e.activation(out=e3[:], in_=e3[:], func=AF.Exp), reads=[te3], writes=[te3])
                            fw.op("dve", lambda e: e.scalar_tensor_tensor(out=aT[:], in0=kkw, scalar=-1.0, in1=e3[:], op0=ALU.mult, op1=ALU.mult),
                                  reads=[tkk, te3], writes=[tops])
                            e3v = e3[:].rearrange("p (c j) -> p c j", j=CHK)
                            fw.op("dve", lambda e: e.tensor_tensor(out=e3v, in0=totb, in1=csv, op=ALU.subtract), reads=[ttot, tcs, tops, te3], writes=[te3])
                            fw.op("act", lambda e: e.activation(out=e3[:], in_=e3[:], func=AF.Exp), reads=[te3], writes=[te3])
                            bh = P2.sb([128, N], BF16)
                            kh = P2.sb([128, N], BF16)
                            tbh = Tok()
                            fw.op("dve", lambda e: e.tensor_tensor(out=bh[:], in0=bbw, in1=e3[:], op=ALU.mult), reads=[tbb, te3], writes=[tbh])
                            fw.op("pool", lambda e: e.tensor_tensor(out=kh[:], in0=kdw, in1=e3[:], op=ALU.mult), reads=[tkd, te3], writes=[tbh])
                            pst = P2.ps([128, 1024], BF16)
                            tpst = Tok()
                            for (src, dstt) in ((bh, btok), (kh, ktok)):
                                for q in range(NCH // 4):
                                    for j in range(4):
                                        ch = q * 4 + j
                                        fw.op("pe", lambda e, j=j, ch=ch, src=src: e.transpose(pst[0:64, j * 128:(j + 1) * 128], src[:, ch * CHK:(ch + 1) * CHK], idb[:]),
                                              reads=[tbh, tc], writes=[tpst])
                                    fw.op("act", lambda e, q=q, dstt=dstt: e.copy(out=dstt[:, q * 4:(q + 1) * 4, :], in_=pst[0:64, 0:512].rearrange("p (j c) -> p j c", j=4)),
                                          reads=[tpst], writes=[ttok])
                            fw.barrier()
                        with ExitStack() as es3:
                            P3 = Pool_(fw, es3)
                            Hs = P3.sb([128, 64], F32)
                            Hb = P3.sb([128, 64], BF16)
                            tH = Tok()
                            fw.op("pool", lambda e: e.memset(Hs[:], 0.0), writes=[tH])
                            fw.op("pool", lambda e: e.memset(Hb[:], 0.0), writes=[tH])
                            psA = Rot([P3.ps([64, 512]) for _ in range(1)])
                            psB = Rot([P3.ps([64, 512]) for _ in range(1)])
                            psI = Rot([P3.ps([64, 512]) for _ in range(1)])
                            psG = Rot([P3.ps([64, 512]) for _ in range(1)])
                            psH = Rot([P3.ps([128, 512]) for _ in range(1)])
                            psY = Rot([P3.ps([128, 512]) for _ in range(1)])
                            g1b = Rot([P3.sb([64, 2, 64], BF16) for _ in range(2)])
                            nl = Rot([P3.sb([64, 2, 2, 64], F32) for _ in range(2)])
                            nl2 = Rot([P3.sb([64, 2, 2, 64], F32) for _ in range(2)])
                            g2b_ = Rot([P3.sb([64, 4, 64], BF16) for _ in range(2)])
                            Pm = Rot([P3.sb([64, 2, 64], F32) for _ in range(2)])
                            Gs = Rot([P3.sb([64, 128], F32) for _ in range(2)])
                            Ub = Rot([P3.sb([64, 128], BF16) for _ in range(2)])
                            Ys = Rot([P3.sb([64, 128], F32) for _ in range(2)])
                            ms, ml, mi = (0, 1, 2) if di == 0 else (1, 0, 3)
                            order = list(range(NCH)) if di == 0 else list(range(NCH - 1, -1, -1))
                            res = {}

                            def par_gen(i):
                                cc0, cc1 = i * CHK, (i + 1) * CHK
                                pa, tpa = psA.next()
                                pb, tpb = psB.next()
                                for h in range(2):
                                    hp = slice(h * 64, (h + 1) * 64)
                                    for (dst, lt, rt_) in ((pa[:, h * 64:(h + 1) * 64], kT, aT), (pa[:, 128 + h * 64:128 + (h + 1) * 64], bT, aT),
                                                           (pa[:, 256 + h * 64:256 + (h + 1) * 64], aT, bT)):
                                        fw.op("pe", lambda e, dst=dst, lt=lt, rt_=rt_, hp=hp: e.matmul(dst, lhsT=lt[hp, cc0:cc1], rhs=rt_[hp, cc0:cc1], start=True, stop=True),
                                              reads=[tops], writes=[tpa])
                                        yield
                                    for (dst, lt, rt_) in ((pb[:, h * 64:(h + 1) * 64], bT, rT), (pb[:, 128 + h * 64:128 + (h + 1) * 64], kT, rT)):
                                        fw.op("pe", lambda e, dst=dst, lt=lt, rt_=rt_, hp=hp: e.matmul(dst, lhsT=lt[hp, cc0:cc1], rhs=rt_[hp, cc0:cc1], start=True, stop=True),
                                              reads=[tops], writes=[tpb])
                                        yield
                                a1, ta1 = g1b.next()
                                nlt, tnl = nl.next()
                                a45, ta45 = g2b_.next()
                                fw.op("dve", lambda e: e.tensor_tensor(out=a1[:], in0=pa[:, 0:128].rearrange("p (h t) -> p h t", h=2), in1=mrep[:, ms, 0:2, :], op=ALU.mult),
                                      reads=[tpa, tc], writes=[ta1])
                                yield
                                fw.op("dve", lambda e: e.tensor_tensor(out=nlt[:, 0], in0=pa[:, 128:256].rearrange("p (h t) -> p h t", h=2), in1=mrep[:, ms, 0:2, :], op=ALU.mult),
                                      reads=[tpa, tc], writes=[tnl])
                                yield
                                fw.op("dve", lambda e: e.tensor_tensor(out=nlt[:, 1], in0=pa[:, 256:384].rearrange("p (h t) -> p h t", h=2), in1=mrep[:, ml, 0:2, :], op=ALU.mult),
                                      reads=[tpa, tc], writes=[tnl])
                                yield
                                fw.op("dve", lambda e: e.tensor_tensor(out=a45[:], in0=pb[:, 0:256].rearrange("p (h t) -> p h t", h=4), in1=mrep[:, mi, :, :], op=ALU.mult),
                                      reads=[tpb, tc], writes=[ta45])
                                yield
                                pm, tpm = Pm.next()
                                fw.op("dve", lambda e: e.tensor_tensor(out=pm[:], in0=nlt[:, 0], in1=idt[0:64, 0:64].unsqueeze(1).to_broadcast([64, 2, 64]), op=ALU.add),
                                      reads=[tnl, tc], writes=[tpm])
                                yield
                                cur, tcur = nlt, tnl
                                for lev in range(5):
                                    pi_, tpi = psI.next()
                                    for h in range(2):
                                        fw.op("pe", lambda e, pi_=pi_, h=h, cur=cur: e.matmul(pi_[:, h * 64:(h + 1) * 64], lhsT=cur[:, 0, h, :], rhs=cur[:, 1, h, :], start=True, stop=True),
                                              reads=[tcur], writes=[tpi])
                                        yield
                                        fw.op("pe", lambda e, pi_=pi_, h=h, cur=cur: e.matmul(pi_[:, 128 + h * 64:128 + (h + 1) * 64], lhsT=cur[:, 1, h, :], rhs=cur[:, 0, h, :], start=True, stop=True),
                                              reads=[tcur], writes=[tpi])
                                        yield
                                    nxt, tnxt = (nl2.next() if lev % 2 == 0 else nl.next())
                                    fw.op("act", lambda e, nxt=nxt, pi_=pi_: e.copy(out=nxt[:, 1], in_=pi_[:, 0:128].rearrange("p (h t) -> p h t", h=2)), reads=[tpi], writes=[tnxt])
                                    yield
                                    fw.op("act", lambda e, nxt=nxt, pi_=pi_: e.copy(out=nxt[:, 0], in_=pi_[:, 128:256].rearrange("p (h t) -> p h t", h=2)), reads=[tpi], writes=[tnxt])
                                    yield
                                    for h in range(2):
                                        fw.op("pe", lambda e, pi_=pi_, h=h, nxt=nxt: e.matmul(pi_[:, 256 + h * 64:256 + (h + 1) * 64], lhsT=nxt[:, 1, h, :], rhs=pm[:, h, :], start=True, stop=True),
                                              reads=[tnxt, tpm], writes=[tpi])
                                        yield
                                    fw.op("dve", lambda e, pi_=pi_: e.tensor_tensor(out=pm[:], in0=pm[:], in1=pi_[:, 256:384].rearrange("p (h t) -> p h t", h=2), op=ALU.add),
                                          reads=[tpi, tpm], writes=[tpm])
                                    yield
                                    cur, tcur = nxt, tnxt
                                res[i] = (a1, ta1, a45, ta45, pm, tpm)

                            def chain_gen(i):
                                cc0, cc1 = i * CHK, (i + 1) * CHK
                                gch = i + off // CHK
                                a1, ta1, a45, ta45, pm, tpm = res.pop(i)
                                pg, tpg = psG.next()
                                for h in range(2):
                                    hp = slice(h * 64, (h + 1) * 64)
                                    fw.op("pe", lambda e, h=h, hp=hp: e.matmul(pg[:, h * 64:(h + 1) * 64], lhsT=aT[hp, cc0:cc1], rhs=Hb[hp, :], start=True, stop=False),
                                          reads=[tops, tH], writes=[tpg])
                                    yield
                                    fw.op("pe", lambda e, h=h, hp=hp: e.matmul(pg[:, h * 64:(h + 1) * 64], lhsT=a1[:, h, :], rhs=vtok[:, gch, hp], start=False, stop=True),
                                          reads=[ta1, tvt], writes=[tpg])
                                    yield
                                gs, tgs = Gs.next()
                                fw.op("act", lambda e: e.copy(out=gs[:], in_=pg[:, 0:128]), reads=[tpg], writes=[tgs])
                                yield
                                for h in range(2):
                                    fw.op("pe", lambda e, h=h: e.matmul(pg[:, 128 + h * 64:128 + (h + 1) * 64], lhsT=pm[:, h, :], rhs=gs[:, h * 64:(h + 1) * 64], start=True, stop=True),
                                          reads=[tpm, tgs], writes=[tpg])
                                    yield
                                ub, tub = Ub.next()
                                fw.op("dve", lambda e: e.tensor_copy(out=ub[:], in_=pg[:, 128:256]), reads=[tpg], writes=[tub])
                                yield
                                ph, tph = psH.next()
                                py, tpy = psY.next()
                                for h in range(2):
                                    hp = slice(h * 64, (h + 1) * 64)
                                    fw.op("pe", lambda e, h=h, hp=hp: e.matmul(ph[hp, 0:64], lhsT=btok[:, i, hp], rhs=ub[:, hp], start=True, stop=False),
                                          reads=[ttok, tub], writes=[tph])
                                    yield
                                    fw.op("pe", lambda e, h=h, hp=hp: e.matmul(ph[hp, 0:64], lhsT=ktok[:, i, hp], rhs=vtok[:, gch, hp], start=False, stop=True),
                                          reads=[ttok, tvt], writes=[tph])
                                    yield
                                for h in range(2):
                                    hp = slice(h * 64, (h + 1) * 64)
                                    fw.op("pe", lambda e, h=h, hp=hp: e.matmul(py[0:64, hp], lhsT=rT[hp, cc0:cc1], rhs=Hb[hp, :], start=True, stop=False),
                                          reads=[tops, tH], writes=[tpy])
                                    yield
                                    fw.op("pe", lambda e, h=h, hp=hp: e.matmul(py[0:64, hp], lhsT=a45[:, h, :], rhs=ub[:, hp], start=False, stop=False),
                                          reads=[ta45, tub], writes=[tpy])
                                    yield
                                    fw.op("pe", lambda e, h=h, hp=hp: e.matmul(py[0:64, hp], lhsT=a45[:, 2 + h, :], rhs=vtok[:, gch, hp], start=False, stop=True),
                                          reads=[ta45, tvt], writes=[tpy])
                                    yield
                                fw.op("dve", lambda e: e.scalar_tensor_tensor(out=Hs[:], in0=Hs[:], scalar=pC[:, i:i + 1], in1=ph[:, 0:64], op0=ALU.mult, op1=ALU.add),
                                      reads=[tph, tH, tops, tpy], writes=[tH])
                                yield
                                fw.op("act", lambda e: e.copy(out=Hb[:], in_=Hs[:]), reads=[tH, tpy, tpg], writes=[tH])
                                yield
                                ys, tys = Ys.next()
                                fw.op("act", lambda e: e.copy(out=ys[:], in_=py[0:64, 0:128]), reads=[tpy], writes=[tys])
                                yield
                                fw.op("pe", lambda e: e.transpose(py[:, 256:320], ys[:], idt[0:64, 0:64]), reads=[tys, tc], writes=[tpy])
                                yield
                                y0 = off + cc0
                                if y0 >= T:
                                    y0 -= T
                                fw.op("dve", lambda e: e.tensor_tensor(out=yacc[:, y0:y0 + CHK], in0=yacc[:, y0:y0 + CHK], in1=py[:, 256:320], op=ALU.add),
                                      reads=[tpy, tya], writes=[tya])
                                yield

                            for _ in par_gen(order[0]):
                                pass
                            for idx, i in enumerate(order):
                                gp = par_gen(order[idx + 1]) if idx + 1 < len(order) else iter(())
                                gc = chain_gen(i)
                                done_p = done_c = False
                                while not (done_p and done_c):
                                    for _ in range(2):
                                        if not done_p:
                                            try:
                                                next(gp)
                                            except StopIteration:
                                                done_p = True
                                    if not done_c:
                                        try:
                                            next(gc)
                                        except StopIteration:
                                            done_c = True
                            fw.barrier()
                with ExitStack() as es4:
                    P4 = Pool_(fw, es4)
                    psr = Rot([P4.ps() for _ in range(3)])
                    xc = P4.sb([128, 512], F32)
                    sq = P4.sb([128, 512], F32)
                    rs = P4.sb([128, 512], F32)
                    txc, tsq, trs = Tok(), Tok(), Tok()
                    stg = Rot([P4.sb([128, 512], F32) for _ in range(2)])
                    tout = Tok()
                    for (lo, hi, ic) in SEGS:
                        n = hi - lo
                        ps, tps = psr.next()
                        fw.op("pe", lambda e, ps=ps, lo=lo, hi=hi, n=n: e.matmul(ps[:, :n], lhsT=bd1[:], rhs=yacc[:, lo:hi], start=True, stop=True), reads=[tya, tc], writes=[tps])
                        fw.op("dve", lambda e, ps=ps, lo=lo, hi=hi, n=n: e.scalar_tensor_tensor(out=xc[:, :n], in0=ps[:, :n], scalar=-1.0 / 64, in1=yacc[:, lo:hi], op0=ALU.mult, op1=ALU.add),
                              reads=[tps, tya], writes=[txc])
                        fw.op("act", lambda e, n=n: e.activation(out=sq[:, :n], in_=xc[:, :n], func=AF.Square), reads=[txc], writes=[tsq])
                        ps2, tps2 = psr.next()
                        fw.op("pe", lambda e, ps2=ps2, n=n: e.matmul(ps2[:, :n], lhsT=bd1[:], rhs=sq[:, :n], start=True, stop=True), reads=[tsq, tc], writes=[tps2])
                        rstd_from_ps(fw, rs, trs, ps2, tps2, n, 1.0 / 64, epsl[:, 0:1], tpp)
                        fw.op("dve", lambda e, n=n: e.tensor_tensor(out=xc[:, :n], in0=xc[:, :n], in1=rs[:, :n], op=ALU.mult), reads=[txc, trs], writes=[txc])
                        fw.op("act", lambda e, n=n: e.activation(out=xc[:, :n], in_=xc[:, :n], func=AF.Identity, scale=pp[:, 8, c:c + 1], bias=pp[:, 9, c:c + 1]),
                              reads=[txc, tpp], writes=[txc])
                        fw.op("pool", lambda e, lo=lo, hi=hi, n=n: e.tensor_tensor(out=xc[:, :n], in0=xc[:, :n], in1=bon[:, lo:hi], op=ALU.add), reads=[txc, tbon], writes=[txc])
                        ps3, tps3 = psr.next()
                        fw.op("pe", lambda e, ps3=ps3, lo=lo, hi=hi, n=n: e.matmul(ps3[:, :n], lhsT=wB[64:128, c * 128:(c + 1) * 128], rhs=lorB[64:128, lo:hi], start=True, stop=True),
                              reads=[tlw, tlor], writes=[tps3])
                        sg, tsg = stg.next()
                        fw.op("dve", lambda e, sg=sg, ps3=ps3, n=n: e.tensor_tensor(out=sg[:, :n], in0=ps3[:, :n], in1=xc[:, :n], op=ALU.mult), reads=[tps3, txc], writes=[tsg])
                        fw.dma("sp", RWT[c * 128:(c + 1) * 128, lo:hi], sg[:, :n], reads=[tsg], writes=[tout])
                    fw.barrier()
        fw.barrier()
NCORES = 8
DEPTH = 4
S5_KEYS = ["s5_a_re", "s5_a_im", "s5_log_step", "s5_b_re", "s5_b_im", "s5_c_re", "s5_c_im", "s5_d", "s5_glu_w", "s5_glu_b", "s5_out_g"]
RW_KEYS = ["rw_mu", "rw_w0", "rw_w2", "rw_a0", "rw_a2", "rw_g2", "rw_k_k", "rw_k_a", "rw_r_k", "rw_ln_g", "rw_ln_b"]
LAYER_KEYS = (["norm1_g", "norm2_g", "mod_w", "mod_b", "w_in", "w_out", "att_qn_g", "att_kn_g", "att_out_g",
               "ffn_up", "ffn_conv_w", "ffn_conv_b", "ffn_down"] + S5_KEYS + RW_KEYS)
SHARED_KEYS = ["c_ctx", "final_g", "ident", "POS", "CST", "RWMASK", "RWBD"]
PERCORE_KEYS = ["x_b", "ctx_b", "c_b"]
SCRATCH = {"ZT": [2048, TE], "VT": [T, 128], "MOD": [128, 48, 2], "S5T": [256, T], "ATT_T": [512, T], "RWT": [256, T],
           "XT1": [DM, T], "XA": [DM, T], "XB": [DM, T]}


def build_fused(depth=DEPTH):
    nc = bass.Bass("TRN2", target_bir_lowering=False)
    decl = {}

    def ext(name, shape, dt, kind):
        if name not in decl:
            decl[name] = nc.dram_tensor(name, list(shape), dt, kind=kind).ap()
        return decl[name]

    def make_io(l):
        xin = "XA" if l % 2 == 0 else "XB"
        xout = "XB" if l % 2 == 0 else "XA"

        def io(name, shape, dt, role):
            if name in LAYER_KEYS:
                full = ext(name, [DEPTH] + list(shape), dt, "ExternalInput")
                return full[l]
            if name in SHARED_KEYS or name in PERCORE_KEYS:
                return ext(name, shape, dt, "ExternalInput")
            if name == "OUT":
                return ext(name, shape, dt, "ExternalOutput")
            if name == "XT":
                name = xin
            elif name == "XT2":
                name = xout
            return ext(name, SCRATCH[name], dt, "Internal")
        return io
    with ExitStack() as es:
        fw = FW(nc, es)
        stage_p0(fw, make_io(0))
        for l in range(depth):
            io = make_io(l)
            for st in (stage_p1, stage_p2, stage_p3, stage_p4, stage_p5a, stage_p5b):
                st(fw, io)
        stage_p6(fw, make_io(depth))
        fw.barrier()
    return nc, fw


_FUSED = {}


def kernel(**inp):
    inp = {k: np.ascontiguousarray(np.asarray(v)) for k, v in inp.items()}
    if "nc" not in _FUSED:
        _FUSED["nc"], _FUSED["fw"] = build_fused()
    nc = _FUSED["nc"]
    pos, cst = host_consts()
    rwm, rwbd = rw_consts()
    shared = {k: inp[k] for k in LAYER_KEYS if k != "rw_r_k"}
    shared["rw_r_k"] = inp["rw_r_k"].reshape(DEPTH, 256)
    shared.update(c_ctx=inp["c_ctx"], final_g=inp["final_g"], ident=np.eye(128, dtype=np.float32),
                  POS=pos, CST=cst, RWMASK=rwm, RWBD=rwbd)
    in_maps = [dict(shared, x_b=inp["x"][b], ctx_b=inp["ctx"][b], c_b=inp["c"][b]) for b in range(NCORES)]
    res = run_bass_kernel_spmd(nc, in_maps, core_ids=list(range(NCORES)))
    return np.stack([res.results[b]["OUT"] for b in range(NCORES)], 0).astype(np.float32)
```

```python
import math
import numpy as np
from contextlib import ExitStack
import concourse.bass as bass
import concourse.mybir as mybir
from concourse.bass_utils import run_bass_kernel_spmd

F32 = mybir.dt.float32
F32R = mybir.dt.float32r
BF16 = mybir.dt.bfloat16
I32 = mybir.dt.int32
ALU = mybir.AluOpType
AF = mybir.ActivationFunctionType
AX = mybir.AxisListType

T = 2304
TE = 2560
NCTX = 256
NLAT = 2048
DM = 1024
SEGS = [(0, 256, 1), (256, 768, 0), (768, 1280, 0), (1280, 1792, 0), (1792, 2304, 0)]
RMS_EPS = 1e-6


class Tok:
    __slots__ = ("w", "r")

    def __init__(self):
        self.w = None
        self.r = {}


class FW:
    ENG = ("pe", "dve", "act", "pool", "sp")
    NDMA = 8

    def __init__(self, nc, es):
        self.nc = nc
        self.es = es
        self.eng = {"pe": nc.tensor, "dve": nc.vector, "act": nc.scalar,
                    "pool": nc.gpsimd, "sp": nc.sync}
        self.sem = {}
        self.cnt = {}
        for e in self.ENG:
            self.sem[e] = es.enter_context(nc.semaphore("s_" + e))
            self.cnt[e] = 0
        self.dq = {}
        for q in ("sp", "pool", "act"):
            ring = []
            for i in range(self.NDMA):
                k = "d_%s_%d" % (q, i)
                self.sem[k] = es.enter_context(nc.semaphore(k))
                self.cnt[k] = 0
                ring.append(k)
            self.dq[q] = [ring, 0]
        self.seen = {e: {} for e in self.ENG}
        self.ninst = 0
        self.uid = 0

    def name(self, p):
        self.uid += 1
        return "%s_%d" % (p, self.uid)

    def _deps(self, reads, writes):
        deps = {}
        for t in reads:
            if t.w is not None and deps.get(t.w[0], 0) < t.w[1]:
                deps[t.w[0]] = t.w[1]
        for t in writes:
            if t.w is not None and deps.get(t.w[0], 0) < t.w[1]:
                deps[t.w[0]] = t.w[1]
            for k, v in t.r.items():
                if deps.get(k, 0) < v:
                    deps[k] = v
        return deps

    def _wait(self, e, deps):
        seen = self.seen[e]
        for k, v in deps.items():
            if seen.get(k, 0) < v:
                self.eng[e].wait_ge(self.sem[k], v)
                seen[k] = v

    def op(self, e, fn, reads=(), writes=()):
        self._wait(e, self._deps(reads, writes))
        inst = fn(self.eng[e])
        self.cnt[e] += 1
        inst.then_inc(self.sem[e], 1)
        v = self.cnt[e]
        for t in reads:
            t.r[e] = v
        for t in writes:
            t.w = (e, v)
            t.r = {}
        self.ninst += 1
        return inst

    def dma(self, q, out, in_, reads=(), writes=(), **kw):
        ring, idx = self.dq[q]
        k = ring[idx % len(ring)]
        self.dq[q][1] = idx + 1
        deps = self._deps(reads, writes)
        if self.cnt[k] > 0:
            deps[k] = max(deps.get(k, 0), self.cnt[k])
        self._wait(q, deps)
        inst = self.eng[q].dma_start(out=out, in_=in_, **kw)
        self.cnt[k] += 16
        inst.then_inc(self.sem[k], 16)
        v = self.cnt[k]
        for t in reads:
            t.r[k] = v
        for t in writes:
            t.w = (k, v)
            t.r = {}
        self.ninst += 1
        return inst

    def barrier(self, engines=None):
        allv = {k: v for k, v in self.cnt.items() if v > 0}
        for e in (engines or self.ENG):
            self._wait(e, allv)


class Pool_:
    def __init__(self, fw, es):
        self.fw = fw
        self.es = es
        self.nc = fw.nc

    def sb(self, shape, dt, name="t"):
        return self.es.enter_context(self.nc.sbuf_tensor(self.fw.name(name), list(shape), dt))

    def ps(self, shape=(128, 512), dt=F32, name="ps"):
        return self.es.enter_context(self.nc.psum_tensor(self.fw.name(name), list(shape), dt))


class Rot:
    def __init__(self, bufs):
        self.bufs = bufs
        self.toks = [Tok() for _ in bufs]
        self.i = 0

    def next(self):
        j = self.i % len(self.bufs)
        self.i += 1
        return self.bufs[j], self.toks[j]


def load_w_bf16(fw, P, W, rows, cols, tok, name="w", q="pool", chunk=2048, stage=None):
    kt = rows // 128
    wb = P.sb([128, kt, cols], BF16, name)
    chunk = min(chunk, cols)
    st = stage or Rot([P.sb([128, chunk], F32, "wstg") for _ in range(3)])
    for k in range(kt):
        for c0 in range(0, cols, chunk):
            n = min(chunk, cols - c0)
            sg, tsg = st.next()
            fw.dma("sp", sg[:, :n], W[k * 128:(k + 1) * 128, c0:c0 + n], writes=[tsg])
            fw.op("pool", lambda e, sg=sg, k=k, c0=c0, n=n: e.tensor_copy(out=wb[:, k, c0:c0 + n], in_=sg[:, :n]), reads=[tsg], writes=[tok])
    return wb


def stage_p0(fw, io):
    nc = fw.nc
    xb = io("x_b", [NLAT, DM], F32, "in")
    cb = io("ctx_b", [NCTX, DM], F32, "in")
    ident = io("ident", [128, 128], F32, "in")
    XT = io("XT", [DM, T], F32, "out")
    with ExitStack() as es:
        P = Pool_(fw, es)
        idt = P.sb([128, 128], F32)
        tid = Tok()
        fw.dma("sp", idt[:], ident, writes=[tid])
        xt = P.sb([128, 8, T], F32)
        txt = Tok()
        xin = Rot([P.sb([128, DM], F32) for _ in range(3)])
        pss = Rot([P.ps() for _ in range(4)])
        for tt in range(18):
            src = cb[tt * 128:(tt + 1) * 128, :] if tt < 2 else xb[(tt - 2) * 128:(tt - 1) * 128, :]
            xi, txi = xin.next()
            fw.dma("sp", xi[:], src, writes=[txi])
            for half in range(2):
                ps, tps = pss.next()
                for k in range(4):
                    kk = half * 4 + k
                    fw.op("pe", lambda e, ps=ps, k=k, kk=kk, xi=xi: e.transpose(
                        ps[:, k * 128:(k + 1) * 128], xi[:, kk * 128:(kk + 1) * 128], idt[:]),
                        reads=[txi, tid], writes=[tps])
                eng = "dve" if half == 0 else "act"
                outap = xt[:, half * 4:half * 4 + 4, tt * 128:(tt + 1) * 128]
                inap = ps[:].rearrange("p (k t) -> p k t", k=4)
                if eng == "dve":
                    fw.op("dve", lambda e, o=outap, i=inap: e.tensor_copy(out=o, in_=i), reads=[tps], writes=[txt])
                else:
                    fw.op("act", lambda e, o=outap, i=inap: e.copy(out=o, in_=i), reads=[tps], writes=[txt])
        tout = Tok()
        for k in range(8):
            fw.dma("sp", XT[k * 128:(k + 1) * 128, :], xt[:, k, :], reads=[txt], writes=[tout])
        fw.barrier()


def make_AB(fw, P, MODs, tmod, g_ap, sh_base, sc_base):
    g = P.sb([128, 8], F32)
    tg = Tok()
    fw.dma("sp", g[:], g_ap.rearrange("(k p) -> p k", p=128), writes=[tg], allow_slow_non_contiguous=True)
    AB = P.sb([128, 2, 2, 8], F32)
    tab = Tok()
    for ic in range(2):
        fw.op("dve", lambda e, ic=ic: e.tensor_scalar(out=AB[:, ic, 0, :], in0=MODs[:, sc_base:sc_base + 8, ic],
                                                      scalar1=1.0, scalar2=None, op0=ALU.add),
              reads=[tmod], writes=[tab])
        fw.op("dve", lambda e, ic=ic: e.tensor_tensor(out=AB[:, ic, 0, :], in0=AB[:, ic, 0, :], in1=g[:], op=ALU.mult),
              reads=[tg, tab], writes=[tab])
        fw.op("dve", lambda e, ic=ic: e.tensor_copy(out=AB[:, ic, 1, :], in_=MODs[:, sh_base:sh_base + 8, ic]),
              reads=[tmod], writes=[tab])
    return AB, tab


def norm_mod_seg(fw, P, st, xs, txs, n, ic, AB, tab, outs, touts):
    sq, ones, tones, psr, rs, tmpr = st["sq"], st["ones"], st["tones"], st["psr"], st["rs"], st["tmpr"]
    tsq, trs = st["tsq"], st["trs"]
    fw.op("act", lambda e: e.activation(out=sq[:, :, :n], in_=xs[:, :, :n], func=AF.Square), reads=[txs], writes=[tsq])
    ps, tps = psr.next()
    for k in range(8):
        fw.op("pe", lambda e, k=k: e.matmul(ps[:, :n], lhsT=ones[:], rhs=sq[:, k, :n], start=(k == 0), stop=(k == 7)),
              reads=[tsq, tones], writes=[tps])
    fw.op("act", lambda e: e.activation(out=rs[:, :n], in_=ps[:, :n], func=AF.Ln, scale=1.0 / DM, bias=st["eps"][:, 0:1]),
          reads=[tps, st["teps"]], writes=[trs])
    fw.op("act", lambda e: e.activation(out=rs[:, :n], in_=rs[:, :n], func=AF.Exp, scale=-0.5), reads=[trs], writes=[trs])
    for k in range(8):
        tmp, ttmp = tmpr.next()
        fw.op("dve", lambda e, k=k, tmp=tmp: e.tensor_tensor(out=tmp[:, :n], in0=xs[:, k, :n], in1=rs[:, :n], op=ALU.mult),
              reads=[txs, trs], writes=[ttmp])
        for o in outs(k):
            fw.op("act", lambda e, k=k, tmp=tmp, o=o: e.activation(out=o, in_=tmp[:, :n], func=AF.Identity,
                                                                   scale=AB[:, ic, 0, k:k + 1], bias=AB[:, ic, 1, k:k + 1]),
                  reads=[ttmp, tab], writes=touts)


def norm_state(fw, P):
    st = {}
    st["sq"] = P.sb([128, 8, 512], BF16)
    st["tsq"] = Tok()
    st["ones"] = P.sb([128, 128], BF16)
    st["tones"] = Tok()
    fw.op("pool", lambda e: e.memset(st["ones"][:], 1.0), writes=[st["tones"]])
    st["eps"] = P.sb([128, 1], F32)
    st["teps"] = Tok()
    fw.op("pool", lambda e: e.memset(st["eps"][:], RMS_EPS), writes=[st["teps"]])
    st["psr"] = Rot([P.ps() for _ in range(2)])
    st["rs"] = P.sb([128, 512], F32)
    st["trs"] = Tok()
    st["tmpr"] = Rot([P.sb([128, 512], F32) for _ in range(2)])
    return st


def stage_p1(fw, io):
    XT = io("XT", [DM, T], F32, "in")
    c_b = io("c_b", [DM], F32, "in")
    c_ctx = io("c_ctx", [DM], F32, "in")
    mod_w = io("mod_w", [DM, 6 * DM], F32, "in")
    mod_b = io("mod_b", [6 * DM], F32, "in")
    n1g = io("norm1_g", [DM], F32, "in")
    w_in = io("w_in", [DM, 1984], F32, "in")
    MOD = io("MOD", [128, 48, 2], F32, "out")
    ZT = io("ZT", [2048, TE], F32, "out")
    VT = io("VT", [T, 128], F32, "out")
    XTv = XT.rearrange("(k p) t -> p k t", p=128)
    with ExitStack() as es:
        P = Pool_(fw, es)
        wstage = Rot([P.sb([128, 2048], F32, "wstg") for _ in range(3)])
        tmw = Tok()
        cc = P.sb([128, 8, 2], F32)
        tcc = Tok()
        fw.dma("sp", cc[:, :, 0], c_b.rearrange("(k p) -> p k", p=128), writes=[tcc], allow_slow_non_contiguous=True)
        fw.dma("sp", cc[:, :, 1], c_ctx.rearrange("(k p) -> p k", p=128), writes=[tcc], allow_slow_non_contiguous=True)
        scb = P.sb([128, 8, 2], BF16)
        tscb = Tok()
        fw.op("act", lambda e: e.activation(out=scb[:], in_=cc[:], func=AF.Silu), reads=[tcc], writes=[tscb])
        mb = P.sb([128, 48], F32)
        tmb = Tok()
        fw.dma("sp", mb[:], mod_b.rearrange("(j p) -> p j", p=128), writes=[tmb], allow_slow_non_contiguous=True)
        MODs = P.sb([128, 48, 2], F32)
        tmod = Tok()
        with ExitStack() as es2:
            P2 = Pool_(fw, es2)
            mwb = load_w_bf16(fw, P2, mod_w, DM, 6 * DM, tmw, "modw", stage=wstage)
            psm = P2.ps([128, 512])
            tpsm = Tok()
            for j in range(48):
                for k in range(8):
                    fw.op("pe", lambda e, j=j, k=k: e.matmul(psm[:, 2 * j:2 * j + 2], lhsT=mwb[:, k, j * 128:(j + 1) * 128],
                                                             rhs=scb[:, k, :], start=(k == 0), stop=(k == 7)),
                          reads=[tmw, tscb], writes=[tpsm])
            for ic in range(2):
                fw.op("dve", lambda e, ic=ic: e.tensor_tensor(
                    out=MODs[:, :, ic], in0=psm[:, 0:96].rearrange("p (j c) -> p j c", c=2)[:, :, ic], in1=mb[:], op=ALU.add),
                    reads=[tpsm, tmb], writes=[tmod])
            fw.barrier()
        tmo = Tok()
        fw.dma("sp", MOD, MODs[:], reads=[tmod], writes=[tmo])
        AB, tab = make_AB(fw, P, MODs, tmod, n1g, 0, 8)
        tw = Tok()
        wb = load_w_bf16(fw, P, w_in, DM, 1984, tw, "win", stage=wstage)
        hT = P.sb([128, 8, TE], BF16)
        thT = Tok()
        st = norm_state(fw, P)
        xr = Rot([P.sb([128, 8, 512], F32) for _ in range(2)])
        for (lo, hi, ic) in SEGS:
            n = hi - lo
            xs, txs = xr.next()
            fw.dma("sp", xs[:, :, :n], XTv[:, :, lo:hi], writes=[txs])

            def outs(k, lo=lo, hi=hi, ic=ic):
                o = [hT[:, k, lo:hi]]
                if ic:
                    o.append(hT[:, k, T + lo:T + hi])
                return o
            norm_mod_seg(fw, P, st, xs, txs, n, ic, AB, tab, outs, [thT])
        psr = Rot([P.ps() for _ in range(4)])
        stg = Rot([P.sb([128, 512], F32) for _ in range(4)])
        tz = Tok()
        cnt = 0
        for nt in range(16):
            if nt == 7:
                continue
            M = 64 if nt == 15 else 128
            for cc_ in range(5):
                c0 = cc_ * 512
                ps, tps = psr.next()
                for k in range(8):
                    fw.op("pe", lambda e, ps=ps, k=k, nt=nt, M=M, c0=c0: e.matmul(
                        ps[0:M, :], lhsT=wb[:, k, nt * 128:nt * 128 + M], rhs=hT[:, k, c0:c0 + 512],
                        start=(k == 0), stop=(k == 7)), reads=[tw, thT], writes=[tps])
                sg, tsg = stg.next()
                if cnt % 2 == 0:
                    fw.op("dve", lambda e, sg=sg, ps=ps, M=M: e.tensor_copy(out=sg[0:M, :], in_=ps[0:M, :]), reads=[tps], writes=[tsg])
                else:
                    fw.op("act", lambda e, sg=sg, ps=ps, M=M: e.copy(out=sg[0:M, :], in_=ps[0:M, :]), reads=[tps], writes=[tsg])
                cnt += 1
                fw.dma("sp", ZT[nt * 128:nt * 128 + M, c0:c0 + 512], sg[0:M, :], reads=[tsg], writes=[tz])
        for tt in range(18):
            ps, tps = psr.next()
            for k in range(8):
                fw.op("pe", lambda e, ps=ps, k=k, tt=tt: e.matmul(
                    ps[:, 0:128], lhsT=hT[:, k, tt * 128:(tt + 1) * 128], rhs=wb[:, k, 896:1024],
                    start=(k == 0), stop=(k == 7)), reads=[tw, thT], writes=[tps])
            sg, tsg = stg.next()
            fw.op("dve", lambda e, sg=sg, ps=ps: e.tensor_copy(out=sg[:, 0:128], in_=ps[:, 0:128]), reads=[tps], writes=[tsg])
            fw.dma("sp", VT[tt * 128:(tt + 1) * 128, :], sg[:, 0:128], reads=[tsg], writes=[tz])
        fw.barrier()


def build_program(stage_fns):
    nc = bass.Bass("TRN2", target_bir_lowering=False)
    decl = {}

    def io(name, shape, dt, role):
        if name in decl:
            return decl[name][0]
        kind = "ExternalInput" if role == "in" else "ExternalOutput"
        ap = nc.dram_tensor(name, list(shape), dt, kind=kind).ap()
        decl[name] = (ap, role, shape)
        return ap
    with ExitStack() as es:
        fw = FW(nc, es)
        for fn in stage_fns:
            fn(fw, io)
        fw.barrier()
    return nc, decl, fw


_PROG_CACHE = {}


def run_stage(key, stage_fns, in_maps, ncores):
    if key not in _PROG_CACHE:
        _PROG_CACHE[key] = build_program(stage_fns)
    nc, decl, fw = _PROG_CACHE[key]
    res = run_bass_kernel_spmd(nc, in_maps, core_ids=list(range(ncores)))
    return res.results


def rstd_from_ps(fw, rs, trs, ps, tps, n, scale, epsap, teps, rows=128):
    fw.op("act", lambda e: e.activation(out=rs[0:rows, :n], in_=ps[0:rows, :n], func=AF.Ln, scale=scale, bias=epsap),
          reads=[tps, teps], writes=[trs])
    fw.op("act", lambda e: e.activation(out=rs[0:rows, :n], in_=rs[0:rows, :n], func=AF.Exp, scale=-0.5), reads=[trs], writes=[trs])


def stage_p3(fw, io):
    ZT = io("ZT", [2048, TE], F32, "in")
    VT = io("VT", [T, 128], F32, "in")
    qn_g = io("att_qn_g", [64], F32, "in")
    kn_g = io("att_kn_g", [64], F32, "in")
    og = io("att_out_g", [512], F32, "in")
    POS = io("POS", [128, NLAT], F32, "in")
    CST = io("CST", [128, 260], F32, "in")
    ATT = io("ATT_T", [512, T], F32, "out")
    with ExitStack() as es:
        P = Pool_(fw, es)
        cst = P.sb([128, 260], F32)
        tc = Tok()
        fw.dma("sp", cst[:], CST, writes=[tc])
        permb = P.sb([128, 128], BF16)
        bdb = P.sb([128, 128], BF16)
        fw.op("dve", lambda e: e.tensor_copy(out=permb[:], in_=cst[:, 1:129]), reads=[tc], writes=[tc])
        fw.op("dve", lambda e: e.tensor_copy(out=bdb[:], in_=cst[:, 129:257]), reads=[tc], writes=[tc])
        eps = P.sb([128, 2], F32)
        teps = Tok()
        fw.op("pool", lambda e: e.memset(eps[:, 0:1], RMS_EPS), writes=[teps])
        fw.op("pool", lambda e: e.memset(eps[:, 1:2], 0.0), writes=[teps])
        gq = P.sb([128, 2], F32)
        tg = Tok()
        for h in range(2):
            fw.dma("sp", gq[h * 64:(h + 1) * 64, 0:1], qn_g.rearrange("(p o) -> p o", o=1), writes=[tg])
            fw.dma("sp", gq[h * 64:(h + 1) * 64, 1:2], kn_g.rearrange("(p o) -> p o", o=1), writes=[tg])
        cos = P.sb([128, NLAT], F32)
        sin = P.sb([128, NLAT], F32)
        ttab = Tok()
        with ExitStack() as es2:
            P2 = Pool_(fw, es2)
            ang = P2.sb([128, NLAT], F32)
            tmpf = P2.sb([128, NLAT], F32)
            tmpi = P2.sb([128, NLAT], I32)
            ta = Tok()
            fw.dma("sp", ang[:], POS, writes=[ta])
            fw.op("dve", lambda e: e.tensor_scalar(out=ang[:], in0=ang[:], scalar1=cst[:, 0:1], scalar2=None, op0=ALU.mult),
                  reads=[ta, tc], writes=[ta])
            for (tab, off) in ((sin, 0.0), (cos, math.pi / 2)):
                fw.op("dve", lambda e, off=off: e.tensor_scalar(out=tmpi[:], in0=ang[:], scalar1=off, scalar2=1.0 / (2 * math.pi),
                                                                op0=ALU.add, op1=ALU.mult), reads=[ta], writes=[ta])
                fw.op("dve", lambda e: e.tensor_copy(out=tmpf[:], in_=tmpi[:]), reads=[ta], writes=[ta])
                fw.op("dve", lambda e: e.scalar_tensor_tensor(out=tmpf[:], in0=tmpf[:], scalar=-2 * math.pi, in1=ang[:],
                                                              op0=ALU.mult, op1=ALU.add), reads=[ta], writes=[ta])
                fw.op("dve", lambda e, off=off: e.tensor_scalar(out=tmpf[:], in0=tmpf[:], scalar1=off, scalar2=math.pi,
                                                                op0=ALU.add, op1=ALU.min), reads=[ta], writes=[ta])
                fw.op("dve", lambda e: e.tensor_scalar(out=tmpf[:], in0=tmpf[:], scalar1=-math.pi, scalar2=None, op0=ALU.max),
                      reads=[ta], writes=[ta])
                fw.op("act", lambda e, tab=tab: e.activation(out=tab[:], in_=tmpf[:], func=AF.Sin), reads=[ta], writes=[ttab])
            fw.barrier()
        qb = P.sb([128, 4, T], BF16)
        kd = P.sb([128, 2, T], BF16)
        tq = Tok()
        with ExitStack() as es2:
            P2 = Pool_(fw, es2)
            raw = Rot([P2.sb([128, 512], F32) for _ in range(2)])
            sqr = Rot([P2.sb([128, 512], BF16) for _ in range(2)])
            psr = Rot([P2.ps() for _ in range(2)])
            psr2 = Rot([P2.ps() for _ in range(2)])
            rsr = Rot([P2.sb([128, 512], F32) for _ in range(2)])
            nbr = Rot([P2.sb([128, 512], BF16) for _ in range(2)])
            t1r = Rot([P2.sb([128, 512], F32) for _ in range(2)])
            t2r = Rot([P2.sb([128, 512], F32) for _ in range(2)])
            items = [("q", j) for j in range(4)] + [("k", g) for g in range(2)]
            for (kind, j) in items:
                for (lo, hi, ic) in SEGS:
                    n = hi - lo
                    rw, trw = raw.next()
                    if kind == "q":
                        fw.dma("sp", rw[:, :n], ZT[256 + j * 128:256 + (j + 1) * 128, lo:hi], writes=[trw])
                        gcol = 0
                        dst = qb[:, j, lo:hi]
                    else:
                        for h in range(2):
                            fw.dma("sp", rw[h * 64:(h + 1) * 64, :n], ZT[768 + j * 64:768 + (j + 1) * 64, lo:hi], writes=[trw])
                        gcol = 1
                        dst = kd[:, j, lo:hi]
                    sq, tsq = sqr.next()
                    fw.op("act", lambda e, sq=sq, rw=rw, n=n: e.activation(out=sq[:, :n], in_=rw[:, :n], func=AF.Square), reads=[trw], writes=[tsq])
                    ps, tps = psr.next()
                    fw.op("pe", lambda e, ps=ps, sq=sq, n=n: e.matmul(ps[:, :n], lhsT=bdb[:], rhs=sq[:, :n], start=True, stop=True),
                          reads=[tsq, tc], writes=[tps])
                    rs, trs = rsr.next()
                    rstd_from_ps(fw, rs, trs, ps, tps, n, 1.0, eps[:, 0:1], teps)
                    t1, tt1 = t1r.next()
                    fw.op("dve", lambda e, t1=t1, rw=rw, rs=rs, n=n, gcol=gcol: e.scalar_tensor_tensor(
                        out=t1[:, :n], in0=rw[:, :n], scalar=gq[:, gcol:gcol + 1], in1=rs[:, :n], op0=ALU.mult, op1=ALU.mult),
                        reads=[trw, trs, tg], writes=[tt1])
                    if ic:
                        fw.op("act", lambda e, dst=dst, t1=t1, n=n: e.copy(out=dst, in_=t1[:, :n]), reads=[tt1], writes=[tq])
                        continue
                    nb, tnb = nbr.next()
                    fw.op("act", lambda e, nb=nb, t1=t1, n=n: e.copy(out=nb[:, :n], in_=t1[:, :n]), reads=[tt1], writes=[tnb])
                    ps2, tps2 = psr2.next()
                    fw.op("pe", lambda e, ps2=ps2, nb=nb, n=n: e.matmul(ps2[:, :n], lhsT=permb[:], rhs=nb[:, :n], start=True, stop=True),
                          reads=[tnb, tc], writes=[tps2])
                    p0 = lo - NCTX
                    t2, tt2 = t2r.next()
                    fw.op("dve", lambda e, t2=t2, ps2=ps2, n=n, p0=p0: e.tensor_tensor(out=t2[:, :n], in0=ps2[:, :n], in1=sin[:, p0:p0 + n], op=ALU.mult),
                          reads=[tps2, ttab], writes=[tt2])
                    fw.op("pool", lambda e, t1=t1, nb=nb, n=n, p0=p0: e.tensor_tensor(out=t1[:, :n], in0=nb[:, :n], in1=cos[:, p0:p0 + n], op=ALU.mult),
                          reads=[tnb, ttab, tt1], writes=[tt1])
                    fw.op("pool", lambda e, dst=dst, t1=t1, t2=t2, n=n: e.tensor_tensor(out=dst, in0=t1[:, :n], in1=t2[:, :n], op=ALU.add),
                          reads=[tt1, tt2], writes=[tq])
            fw.barrier()
        va = P.sb([128, 18, 2, 128], BF16)
        tva = Tok()
        fw.op("pool", lambda e: e.memset(va[:], 1.0), writes=[tva])
        for g in range(2):
            fw.dma("pool", va[:, :, g, 0:64], VT.rearrange("(t p) c -> p t c", p=128)[:, :, g * 64:(g + 1) * 64], writes=[tva])
        att = P.sb([128, 4, T], F32)
        tatt = Tok()
        pss = Rot([P.ps() for _ in range(3)])
        pso = Rot([P.ps() for _ in range(2)])
        ptr = Rot([P.sb([128, 512], BF16) for _ in range(3)])
        rcr = Rot([P.sb([64, 512], F32) for _ in range(2)])
        jobs = [(0, 256, 0, 2)] + [(256 + 512 * i, 768 + 512 * i, 0, 18) for i in range(4)]
        for h in range(8):
            g = h // 4
            jt, r0 = h // 2, (h % 2) * 64
            for (qlo, qhi, k0, k1) in jobs:
                n = qhi - qlo
                po, tpo = pso.next()
                for kt in range(k0, k1):
                    ps, tps = pss.next()
                    fw.op("pe", lambda e, ps=ps, kt=kt, g=g, jt=jt, r0=r0, qlo=qlo, qhi=qhi, n=n: e.matmul(
                        ps[:, :n], lhsT=kd[r0:r0 + 64, g, kt * 128:(kt + 1) * 128], rhs=qb[r0:r0 + 64, jt, qlo:qhi],
                        start=True, stop=True), reads=[tq], writes=[tps])
                    pt, tpt = ptr.next()
                    fw.op("act", lambda e, pt=pt, ps=ps, n=n: e.activation(out=pt[:, :n], in_=ps[:, :n], func=AF.Exp, scale=0.125),
                          reads=[tps], writes=[tpt])
                    fw.op("pe", lambda e, po=po, pt=pt, kt=kt, g=g, n=n, k0=k0, k1=k1: e.matmul(
                        po[:, :n], lhsT=va[:, kt, g, :], rhs=pt[:, :n], start=(kt == k0), stop=(kt == k1 - 1)),
                        reads=[tpt, tva], writes=[tpo])
                rc, trc = rcr.next()
                fw.op("dve", lambda e, rc=rc, po=po, n=n: e.reciprocal(out=rc[:, :n], in_=po[64:128, :n]), reads=[tpo], writes=[trc])
                fw.op("dve", lambda e, rc=rc, po=po, n=n, jt=jt, r0=r0, qlo=qlo, qhi=qhi: e.tensor_tensor(
                    out=att[r0:r0 + 64, jt, qlo:qhi], in0=po[0:64, :n], in1=rc[:, :n], op=ALU.mult),
                    reads=[tpo, trc], writes=[tatt])
        ogs = P.sb([128, 4], F32)
        tog = Tok()
        fw.dma("sp", ogs[:], og.rearrange("(k p) -> p k", p=128), writes=[tog], allow_slow_non_contiguous=True)
        ones = P.sb([128, 128], BF16)
        fw.op("pool", lambda e: e.memset(ones[:], 1.0), writes=[tog])
        sq4 = P.sb([128, 4, 512], BF16)
        tsq4 = Tok()
        rs = P.sb([128, 512], F32)
        trs = Tok()
        stg = Rot([P.sb([128, 512], F32) for _ in range(3)])
        tout = Tok()
        for (lo, hi, ic) in SEGS:
            n = hi - lo
            fw.op("act", lambda e, lo=lo, hi=hi, n=n: e.activation(out=sq4[:, :, :n], in_=att[:, :, lo:hi], func=AF.Square), reads=[tatt], writes=[tsq4])
            ps, tps = pss.next()
            for k in range(4):
                fw.op("pe", lambda e, ps=ps, k=k, n=n: e.matmul(ps[:, :n], lhsT=ones[:], rhs=sq4[:, k, :n], start=(k == 0), stop=(k == 3)),
                      reads=[tsq4, tog], writes=[tps])
            rstd_from_ps(fw, rs, trs, ps, tps, n, 1.0 / 512, eps[:, 0:1], teps)
            for k in range(4):
                sg, tsg = stg.next()
                fw.op("dve", lambda e, sg=sg, k=k, lo=lo, hi=hi, n=n: e.scalar_tensor_tensor(
                    out=sg[:, :n], in0=att[:, k, lo:hi], scalar=ogs[:, k:k + 1], in1=rs[:, :n], op0=ALU.mult, op1=ALU.mult),
                    reads=[tatt, trs, tog], writes=[tsg])
                fw.dma("sp", ATT[k * 128:(k + 1) * 128, lo:hi], sg[:, :n], reads=[tsg], writes=[tout])
        fw.barrier()


def host_consts():
    pos = np.zeros((128, NLAT), np.float32)
    inv = np.zeros((128,), np.float32)
    tok = np.arange(NLAT)
    for p in range(128):
        d = p % 64
        pos[p] = (tok // 64) if d < 32 else (tok % 64)
        inv[p] = 10000.0 ** (-(d % 16) / 16.0)
    cst = np.zeros((128, 260), np.float32)
    cst[:, 0] = inv
    perm = np.zeros((128, 128), np.float32)
    for m in range(128):
        d = m % 32
        if d < 16:
            perm[m + 16, m] = -1.0
        else:
            perm[m - 16, m] = 1.0
    cst[:, 1:129] = perm
    bd = np.zeros((128, 128), np.float32)
    bd[:64, :64] = 1.0 / 64
    bd[64:, 64:] = 1.0 / 64
    cst[:, 129:257] = bd
    return pos, cst


def stage_p5a(fw, io):
    XT = io("XT", [DM, T], F32, "in")
    S5T = io("S5T", [256, T], F32, "in")
    ATT = io("ATT_T", [512, T], F32, "in")
    RWT = io("RWT", [256, T], F32, "in")
    MOD = io("MOD", [128, 48, 2], F32, "in")
    w_out = io("w_out", [DM, DM], F32, "in")
    XT1 = io("XT1", [DM, T], F32, "out")
    XTv = XT.rearrange("(k p) t -> p k t", p=128)
    with ExitStack() as es:
        P = Pool_(fw, es)
        MODs = P.sb([128, 48, 2], F32)
        tmod = Tok()
        fw.dma("sp", MODs[:], MOD, writes=[tmod])
        tw = Tok()
        wb = load_w_bf16(fw, P, w_out, DM, DM, tw, "wout")
        cat = P.sb([128, 8, T], BF16)
        tcat = Tok()
        cstg = Rot([P.sb([128, T], F32, "cstg") for _ in range(2)])
        for k in range(8):
            src = S5T[k * 128:(k + 1) * 128, :] if k < 2 else (ATT[(k - 2) * 128:(k - 1) * 128, :] if k < 6 else RWT[(k - 6) * 128:(k - 5) * 128, :])
            sg, tsg = cstg.next()
            fw.dma("sp", sg[:], src, writes=[tsg])
            fw.op("pool", lambda e, sg=sg, k=k: e.tensor_copy(out=cat[:, k, :], in_=sg[:]), reads=[tsg], writes=[tcat])
        xr = Rot([P.sb([128, 8, 512], F32) for _ in range(2)])
        x1r = Rot([P.sb([128, 8, 512], F32) for _ in range(2)])
        psr = Rot([P.ps() for _ in range(4)])
        to1, to2 = Tok(), Tok()
        for (lo, hi, ic) in SEGS:
            n = hi - lo
            xs, txs = xr.next()
            fw.dma("sp", xs[:, :, :n], XTv[:, :, lo:hi], writes=[txs])
            x1, tx1 = x1r.next()
            for d in range(8):
                ps, tps = psr.next()
                for k in range(8):
                    fw.op("pe", lambda e, ps=ps, k=k, d=d, lo=lo, hi=hi, n=n: e.matmul(
                        ps[:, :n], lhsT=wb[:, k, d * 128:(d + 1) * 128], rhs=cat[:, k, lo:hi], start=(k == 0), stop=(k == 7)),
                        reads=[tw, tcat], writes=[tps])
                fw.op("dve", lambda e, ps=ps, d=d, n=n, ic=ic, x1=x1, xs=xs: e.scalar_tensor_tensor(
                    out=x1[:, d, :n], in0=ps[:, :n], scalar=MODs[:, 16 + d, ic:ic + 1], in1=xs[:, d, :n], op0=ALU.mult, op1=ALU.add),
                    reads=[tps, txs, tmod], writes=[tx1])
            for k in range(8):
                fw.dma("sp", XT1[k * 128:(k + 1) * 128, lo:hi], x1[:, k, :n], reads=[tx1], writes=[to1])
        fw.barrier()


def stage_p5b(fw, io):
    XT1 = io("XT1", [DM, T], F32, "in")
    n2g = io("norm2_g", [DM], F32, "in")
    MOD = io("MOD", [128, 48, 2], F32, "in")
    up = io("ffn_up", [44, 128, DM], F32, "in")
    cw = io("ffn_conv_w", [3, 5632], F32, "in")
    cb = io("ffn_conv_b", [5632], F32, "in")
    down = io("ffn_down", [2816, DM], F32, "in")
    XT2 = io("XT2", [DM, T], F32, "out")
    X1v = XT1.rearrange("(k p) t -> p k t", p=128)
    X2v = XT2.rearrange("(k p) t -> p k t", p=128)
    with ExitStack() as es:
        P = Pool_(fw, es)
        MODs = P.sb([128, 48, 2], F32)
        tmod = Tok()
        fw.dma("sp", MODs[:], MOD, writes=[tmod])
        cws = P.sb([128, 44, 3], F32)
        cbs = P.sb([128, 44], F32)
        tcw = Tok()
        for w in range(3):
            fw.dma("sp", cws[:, :, w], cw[w].rearrange("(j p) -> p j", p=128), writes=[tcw], allow_slow_non_contiguous=True)
        fw.dma("sp", cbs[:], cb.rearrange("(j p) -> p j", p=128), writes=[tcw], allow_slow_non_contiguous=True)
        h2 = P.sb([128, 8, T], BF16)
        th2 = Tok()
        AB, tab = make_AB(fw, P, MODs, tmod, n2g, 24, 32)
        with ExitStack() as es2:
            P2 = Pool_(fw, es2)
            st = norm_state(fw, P2)
            xr0 = Rot([P2.sb([128, 8, 512], F32) for _ in range(2)])
            for (lo, hi, ic) in SEGS:
                n = hi - lo
                xs, txs = xr0.next()
                fw.dma("sp", xs[:, :, :n], X1v[:, :, lo:hi], writes=[txs])
                norm_mod_seg(fw, P2, st, xs, txs, n, ic, AB, tab, lambda k, lo=lo, hi=hi: [h2[:, k, lo:hi]], [th2])
            fw.barrier()
        hid = P.sb([128, 11, T], BF16)
        thid = Tok()
        dwb = P.sb([128, 11, DM], BF16)
        tdw = Tok()
        urot = [Rot([P.sb([128, T], F32) for _ in range(2)]) for _ in range(2)]
        y = [P.sb([128, T], F32) for _ in range(2)]
        ty = [Tok(), Tok()]
        psr = Rot([P.ps() for _ in range(4)])
        tx2 = Tok()
        RANGES = [(0, NCTX), (NCTX, T)]
        GROUPS = [(0, 2), (2, 2), (4, 2), (6, 2), (8, 2), (10, 1)]
        for half in range(2):
            with ExitStack() as esu:
                Pu = Pool_(fw, esu)
                ustg = Rot([Pu.sb([128, 8, 256], F32, "ustg") for _ in range(2)])
                for jj in range(11):
                    r0 = (half * 11 + jj) * 128
                    sg, tsg = ustg.next()
                    sgv = sg[:].rearrange("p k n -> p (k n)")[:, 0:DM]
                    fw.dma("sp", sgv, down[r0:r0 + 128, :], writes=[tsg])
                    fw.op("pool", lambda e, sgv=sgv, jj=jj: e.tensor_copy(out=dwb[:, jj, :], in_=sgv), reads=[tsg], writes=[tdw])
                ubr = Rot([Pu.sb([128, 8, 2, 256], BF16, "ub") for _ in range(2)])

                def issue_load(g, half=half):
                    jj0, ng = GROUPS[g]
                    ub, tub = ubr.next()
                    for wh in range(2):
                        sg, tsg = ustg.next()
                        sgv = sg[:].rearrange("p k (j c) -> p (k j c)", j=2).rearrange("p (j k c) -> p j k c", j=2, k=8)
                        for jl in range(ng):
                            jt = wh * 22 + half * 11 + jj0 + jl
                            fw.dma("sp", sgv[:, jl].rearrange("p k c -> p (k c)"), up[jt], writes=[tsg])
                            fw.op("pool", lambda e, sgv=sgv, ub=ub, wh=wh, jl=jl: e.tensor_copy(out=ub[:, :, wh, jl * 128:(jl + 1) * 128], in_=sgv[:, jl]),
                                  reads=[tsg], writes=[tub])
                    return ub, tub
                loaded = issue_load(0)
                for g, (jj0, ng) in enumerate(GROUPS):
                    ub, tub = loaded
                    if g + 1 < len(GROUPS):
                        loaded = issue_load(g + 1)
                    for jl in range(ng):
                        jj = jj0 + jl
                        j = half * 11 + jj
                        for wh in range(2):
                            jc = wh * 22 + j
                            ucur, tucur = urot[wh].next()
                            for si, (lo, hi, ic) in enumerate(SEGS):
                                n = hi - lo
                                ps, tps = psr.next()
                                for k in range(8):
                                    fw.op("pe", lambda e, ps=ps, k=k, wh=wh, ub=ub, jl=jl, lo=lo, hi=hi, n=n: e.matmul(
                                        ps[:, :n], lhsT=ub[:, k, wh, jl * 128:(jl + 1) * 128], rhs=h2[:, k, lo:hi], start=(k == 0), stop=(k == 7)),
                                        reads=[tub, th2], writes=[tps])
                                fw.op("act", lambda e, ps=ps, ucur=ucur, lo=lo, hi=hi, n=n: e.copy(out=ucur[:, lo:hi], in_=ps[:, :n]),
                                      reads=[tps], writes=[tucur])
                            fw.op("act", lambda e, wh=wh, jc=jc, ucur=ucur: e.activation(out=y[wh][:], in_=ucur[:], func=AF.Identity,
                                                                                        scale=cws[:, jc, 1:2], bias=cbs[:, jc:jc + 1]),
                                  reads=[tucur, tcw], writes=[ty[wh]])
                            for (lo, hi) in RANGES:
                                fw.op("dve", lambda e, wh=wh, jc=jc, lo=lo, hi=hi, ucur=ucur: e.scalar_tensor_tensor(
                                    out=y[wh][:, lo + 1:hi], in0=ucur[:, lo:hi - 1], scalar=cws[:, jc, 0:1], in1=y[wh][:, lo + 1:hi],
                                    op0=ALU.mult, op1=ALU.add), reads=[tucur, tcw, ty[wh]], writes=[ty[wh]])
                                fw.op("dve", lambda e, wh=wh, jc=jc, lo=lo, hi=hi, ucur=ucur: e.scalar_tensor_tensor(
                                    out=y[wh][:, lo:hi - 1], in0=ucur[:, lo + 1:hi], scalar=cws[:, jc, 2:3], in1=y[wh][:, lo:hi - 1],
                                    op0=ALU.mult, op1=ALU.add), reads=[tucur, tcw, ty[wh]], writes=[ty[wh]])
                        fw.op("act", lambda e: e.activation(out=y[0][:], in_=y[0][:], func=AF.Silu), reads=[ty[0]], writes=[ty[0]])
                        fw.op("dve", lambda e, jj=jj: e.tensor_tensor(out=hid[:, jj, :], in0=y[0][:], in1=y[1][:], op=ALU.mult),
                              reads=[ty[0], ty[1]], writes=[thid])
                fw.barrier()
            with ExitStack() as esd:
                Pd = Pool_(fw, esd)
                xr = Rot([Pd.sb([128, 8, 512], F32) for _ in range(2)])
                for (lo, hi, ic) in SEGS:
                    n = hi - lo
                    xs, txs = xr.next()
                    src = X1v if half == 0 else X2v
                    fw.dma("sp", xs[:, :, :n], src[:, :, lo:hi], reads=([tx2] if half else []), writes=[txs])
                    for d in range(8):
                        ps, tps = psr.next()
                        for jj in range(11):
                            fw.op("pe", lambda e, ps=ps, jj=jj, d=d, lo=lo, hi=hi, n=n: e.matmul(
                                ps[:, :n], lhsT=dwb[:, jj, d * 128:(d + 1) * 128], rhs=hid[:, jj, lo:hi], start=(jj == 0), stop=(jj == 10)),
                                reads=[tdw, thid], writes=[tps])
                        fw.op("dve", lambda e, ps=ps, d=d, n=n, ic=ic, xs=xs: e.scalar_tensor_tensor(
                            out=xs[:, d, :n], in0=ps[:, :n], scalar=MODs[:, 40 + d, ic:ic + 1], in1=xs[:, d, :n], op0=ALU.mult, op1=ALU.add),
                            reads=[tps, tmod, txs], writes=[txs])
                    for k in range(8):
                        fw.dma("sp", XT2[k * 128:(k + 1) * 128, lo:hi], xs[:, k, :n], reads=[txs], writes=[tx2])
                fw.barrier()


def stage_p6(fw, io):
    XT = io("XT", [DM, T], F32, "in")
    fg = io("final_g", [DM], F32, "in")
    ident = io("ident", [128, 128], F32, "in")
    OUT = io("OUT", [NLAT, DM], F32, "out")
    XTv = XT.rearrange("(k p) t -> p k t", p=128)
    with ExitStack() as es:
        P = Pool_(fw, es)
        idt = P.sb([128, 128], F32)
        tid = Tok()
        fw.dma("sp", idt[:], ident, writes=[tid])
        g = P.sb([128, 8], F32)
        fw.dma("sp", g[:], fg.rearrange("(k p) -> p k", p=128), writes=[tid], allow_slow_non_contiguous=True)
        st = norm_state(fw, P)
        xr = Rot([P.sb([128, 8, 512], F32) for _ in range(2)])
        yr = Rot([P.sb([128, 8, 512], F32) for _ in range(2)])
        psr = Rot([P.ps() for _ in range(4)])
        orr = Rot([P.sb([128, DM], F32) for _ in range(3)])
        tout = Tok()
        for (lo, hi, ic) in SEGS[1:]:
            n = hi - lo
            xs, txs = xr.next()
            fw.dma("sp", xs[:, :, :n], XTv[:, :, lo:hi], writes=[txs])
            sq, ones = st["sq"], st["ones"]
            fw.op("act", lambda e, xs=xs: e.activation(out=sq[:], in_=xs[:], func=AF.Square), reads=[txs], writes=[st["tsq"]])
            ps, tps = st["psr"].next()
            for k in range(8):
                fw.op("pe", lambda e, ps=ps, k=k: e.matmul(ps[:], lhsT=ones[:], rhs=sq[:, k, :], start=(k == 0), stop=(k == 7)),
                      reads=[st["tsq"], st["tones"]], writes=[tps])
            rstd_from_ps(fw, st["rs"], st["trs"], ps, tps, n, 1.0 / DM, st["eps"][:, 0:1], st["teps"])
            ys, tys = yr.next()
            for k in range(8):
                fw.op("dve", lambda e, ys=ys, xs=xs, k=k: e.scalar_tensor_tensor(
                    out=ys[:, k, :], in0=xs[:, k, :], scalar=g[:, k:k + 1], in1=st["rs"][:], op0=ALU.mult, op1=ALU.mult),
                    reads=[txs, st["trs"], tid], writes=[tys])
            for blk in range(4):
                ot, tot = orr.next()
                for half in range(2):
                    ps2, tps2 = psr.next()
                    for k in range(4):
                        kk = half * 4 + k
                        fw.op("pe", lambda e, ps2=ps2, k=k, kk=kk, ys=ys, blk=blk: e.transpose(
                            ps2[:, k * 128:(k + 1) * 128], ys[:, kk, blk * 128:(blk + 1) * 128], idt[:]),
                            reads=[tys, tid], writes=[tps2])
                    if half == 0:
                        fw.op("dve", lambda e, ot=ot, ps2=ps2: e.tensor_copy(out=ot[:, 0:512], in_=ps2[:]), reads=[tps2], writes=[tot])
                    else:
                        fw.op("act", lambda e, ot=ot, ps2=ps2: e.copy(out=ot[:, 512:1024], in_=ps2[:]), reads=[tps2], writes=[tot])
                r0 = lo - NCTX + blk * 128
                fw.dma("sp", OUT[r0:r0 + 128, :], ot[:], reads=[tot], writes=[tout])
        fw.barrier()


def sin_reduced(fw, P, out, src, tsrc, shape, off, tout):
    ti = P.sb(shape, I32)
    tf = P.sb(shape, F32)
    tt = Tok()
    fw.op("dve", lambda e: e.tensor_scalar(out=ti[:], in0=src, scalar1=off, scalar2=1.0 / (2 * math.pi), op0=ALU.add, op1=ALU.mult),
          reads=[tsrc], writes=[tt])
    fw.op("dve", lambda e: e.tensor_copy(out=tf[:], in_=ti[:]), reads=[tt], writes=[tt])
    fw.op("dve", lambda e: e.scalar_tensor_tensor(out=tf[:], in0=tf[:], scalar=-2 * math.pi, in1=src, op0=ALU.mult, op1=ALU.add),
          reads=[tt, tsrc], writes=[tt])
    fw.op("dve", lambda e: e.tensor_scalar(out=tf[:], in0=tf[:], scalar1=off, scalar2=math.pi, op0=ALU.add, op1=ALU.min), reads=[tt], writes=[tt])
    fw.op("dve", lambda e: e.tensor_scalar(out=tf[:], in0=tf[:], scalar1=-math.pi, scalar2=None, op0=ALU.max), reads=[tt], writes=[tt])
    fw.op("act", lambda e: e.activation(out=out, in_=tf[:], func=AF.Sin), reads=[tt], writes=[tout])


def stage_p2(fw, io):
    ZT = io("ZT", [2048, TE], F32, "in")
    a_re = io("s5_a_re", [2, 16, 64], F32, "in")
    a_im = io("s5_a_im", [2, 16, 64], F32, "in")
    lstep = io("s5_log_step", [2, 16], F32, "in")
    b_re = io("s5_b_re", [2, 16, 64, 16], F32, "in")
    b_im = io("s5_b_im", [2, 16, 64, 16], F32, "in")
    c_re = io("s5_c_re", [2, 16, 16, 64], F32, "in")
    c_im = io("s5_c_im", [2, 16, 16, 64], F32, "in")
    dsk = io("s5_d", [256], F32, "in")
    glu_w = io("s5_glu_w", [256, 256], F32, "in")
    glu_b = io("s5_glu_b", [256], F32, "in")
    out_g = io("s5_out_g", [256], F32, "in")
    S5T = io("S5T", [256, T], F32, "out")
    N = T
    with ExitStack() as es:
        P = Pool_(fw, es)
        are = P.sb([128, 2, 8], F32)
        aim = P.sb([128, 2, 8], F32)
        lst = P.sb([128, 2, 8], F32)
        tpar = Tok()
        for di in range(2):
            fw.dma("sp", are[:, di, :], a_re[di].rearrange("(s g) p -> (g p) s", g=2), writes=[tpar], allow_slow_non_contiguous=True)
            fw.dma("sp", aim[:, di, :], a_im[di].rearrange("(s g) p -> (g p) s", g=2), writes=[tpar], allow_slow_non_contiguous=True)
            for g2 in range(2):
                fw.dma("sp", lst[g2 * 64:(g2 + 1) * 64, di:di + 1, :],
                       lstep[di].rearrange("(s g) -> g s", g=2)[g2:g2 + 1, :].partition_broadcast(64), writes=[tpar],
                       allow_slow_non_contiguous=True)
        sh = [128, 2, 8]
        step = P.sb(sh, F32)
        fw.op("act", lambda e: e.activation(out=step[:], in_=lst[:], func=AF.Exp), reads=[tpar], writes=[tpar])
        er = P.sb(sh, F32)
        th = P.sb(sh, F32)
        fw.op("dve", lambda e: e.tensor_tensor(out=er[:], in0=are[:], in1=step[:], op=ALU.mult), reads=[tpar], writes=[tpar])
        fw.op("act", lambda e: e.activation(out=er[:], in_=er[:], func=AF.Exp), reads=[tpar], writes=[tpar])
        fw.op("dve", lambda e: e.tensor_tensor(out=th[:], in0=aim[:], in1=step[:], op=ALU.mult), reads=[tpar], writes=[tpar])
        sn = P.sb(sh, F32)
        cs = P.sb(sh, F32)
        ttrig = Tok()
        sin_reduced(fw, P, sn[:], th[:], tpar, sh, 0.0, ttrig)
        sin_reduced(fw, P, cs[:], th[:], tpar, sh, math.pi / 2, ttrig)
        PW = P.sb([128, 9, 3, 2, 8], F32)
        tpw = Tok()
        fw.op("dve", lambda e: e.tensor_tensor(out=PW[:, 0, 0], in0=er[:], in1=cs[:], op=ALU.mult), reads=[tpar, ttrig], writes=[tpw])
        fw.op("dve", lambda e: e.tensor_tensor(out=PW[:, 0, 1], in0=er[:], in1=sn[:], op=ALU.mult), reads=[tpar, ttrig], writes=[tpw])
        t1 = P.sb(sh, F32)
        t2 = P.sb(sh, F32)
        for k in range(9):
            fw.op("dve", lambda e, k=k: e.tensor_scalar(out=PW[:, k, 2], in0=PW[:, k, 1], scalar1=-1.0, scalar2=None, op0=ALU.mult),
                  reads=[tpw], writes=[tpw])
            if k == 8:
                break
            fw.op("dve", lambda e, k=k: e.tensor_tensor(out=t1[:], in0=PW[:, k, 0], in1=PW[:, k, 0], op=ALU.mult), reads=[tpw], writes=[tpw])
            fw.op("dve", lambda e, k=k: e.tensor_tensor(out=t2[:], in0=PW[:, k, 1], in1=PW[:, k, 1], op=ALU.mult), reads=[tpw], writes=[tpw])
            fw.op("dve", lambda e, k=k: e.tensor_tensor(out=PW[:, k + 1, 0], in0=t1[:], in1=t2[:], op=ALU.subtract), reads=[tpw], writes=[tpw])
            fw.op("dve", lambda e, k=k: e.scalar_tensor_tensor(out=PW[:, k + 1, 1], in0=PW[:, k, 0], scalar=2.0, in1=PW[:, k, 1],
                                                               op0=ALU.mult, op1=ALU.mult), reads=[tpw], writes=[tpw])
        br = P.sb(sh, F32)
        bi = P.sb(sh, F32)
        nbi = P.sb(sh, F32)
        den = P.sb(sh, F32)
        nr = P.sb(sh, F32)
        tb = Tok()
        fw.op("dve", lambda e: e.tensor_tensor(out=den[:], in0=are[:], in1=are[:], op=ALU.mult), reads=[tpar], writes=[tb])
        fw.op("dve", lambda e: e.tensor_tensor(out=t1[:], in0=aim[:], in1=aim[:], op=ALU.mult), reads=[tpar, tpw], writes=[tpw])
        fw.op("dve", lambda e: e.tensor_tensor(out=den[:], in0=den[:], in1=t1[:], op=ALU.add), reads=[tb, tpw], writes=[tb])
        fw.op("dve", lambda e: e.reciprocal(out=den[:], in_=den[:]), reads=[tb], writes=[tb])
        fw.op("dve", lambda e: e.tensor_scalar(out=nr[:], in0=PW[:, 0, 0], scalar1=-1.0, scalar2=None, op0=ALU.add), reads=[tpw], writes=[tb])
        fw.op("dve", lambda e: e.tensor_tensor(out=t1[:], in0=nr[:], in1=are[:], op=ALU.mult), reads=[tb, tpar, tpw], writes=[tpw])
        fw.op("dve", lambda e: e.tensor_tensor(out=t2[:], in0=PW[:, 0, 1], in1=aim[:], op=ALU.mult), reads=[tpw, tpar], writes=[tpw])
        fw.op("dve", lambda e: e.tensor_tensor(out=t1[:], in0=t1[:], in1=t2[:], op=ALU.add), reads=[tpw], writes=[tpw])
        fw.op("dve", lambda e: e.tensor_tensor(out=br[:], in0=t1[:], in1=den[:], op=ALU.mult), reads=[tpw, tb], writes=[tb])
        fw.op("dve", lambda e: e.tensor_tensor(out=t1[:], in0=PW[:, 0, 1], in1=are[:], op=ALU.mult), reads=[tpw, tpar, tb], writes=[tpw])
        fw.op("dve", lambda e: e.tensor_tensor(out=t2[:], in0=nr[:], in1=aim[:], op=ALU.mult), reads=[tb, tpar, tpw], writes=[tpw])
        fw.op("dve", lambda e: e.tensor_tensor(out=t1[:], in0=t1[:], in1=t2[:], op=ALU.subtract), reads=[tpw], writes=[tpw])
        fw.op("dve", lambda e: e.tensor_tensor(out=bi[:], in0=t1[:], in1=den[:], op=ALU.mult), reads=[tpw, tb], writes=[tb])
        fw.op("dve", lambda e: e.tensor_scalar(out=nbi[:], in0=bi[:], scalar1=-1.0, scalar2=None, op0=ALU.mult), reads=[tb], writes=[tb])
        BTb = P.sb([128, 2, 2, 8, 128], BF16)
        CTb = P.sb([128, 2, 2, 8, 128], BF16)
        tBT = Tok()
        tCT = Tok()
        with ExitStack() as es2:
            P2 = Pool_(fw, es2)
            BTf = P2.sb([128, 2, 2, 8, 128], F32)
            CTf = P2.sb([128, 2, 2, 8, 128], F32)
            CT2 = P2.sb([128, 2, 2, 8, 128], F32)
            tf1, tf2 = Tok(), Tok()
            fw.op("pool", lambda e: e.memset(BTf[:], 0.0), writes=[tf1])
            fw.op("pool", lambda e: e.memset(CTf[:], 0.0), writes=[tf2])
            for di in range(2):
                for ri, (bsrc, csrc) in enumerate(((b_re, c_re), (b_im, c_im))):
                    for g in range(16):
                        s, g2 = g // 2, g % 2
                        r0 = (g % 8) * 16
                        fw.dma("sp", BTf[r0:r0 + 16, di, ri, s, g2 * 64:(g2 + 1) * 64], bsrc[di, g].rearrange("p h -> h p"),
                               writes=[tf1], allow_slow_non_contiguous=True)
                        fw.dma("sp", CTf[g2 * 64:(g2 + 1) * 64, di, ri, s, r0:r0 + 16], csrc[di, g].rearrange("h p -> p h"),
                               writes=[tf2], allow_slow_non_contiguous=True)
            fw.op("act", lambda e: e.copy(out=BTb[:], in_=BTf[:]), reads=[tf1], writes=[tBT])
            bsh = [128, 2, 8, 128]
            brb = br[:].unsqueeze(3).to_broadcast(bsh)
            nbib = nbi[:].unsqueeze(3).to_broadcast(bsh)
            tc2 = Tok()
            fw.op("dve", lambda e: e.tensor_tensor(out=CT2[:, :, 0], in0=CTf[:, :, 0], in1=brb, op=ALU.mult), reads=[tf2, tb], writes=[tc2])
            fw.op("pool", lambda e: e.tensor_tensor(out=CT2[:, :, 1], in0=CTf[:, :, 1], in1=nbib, op=ALU.mult), reads=[tf2, tb], writes=[tc2])
            fw.op("dve", lambda e: e.tensor_tensor(out=CT2[:, :, 0], in0=CT2[:, :, 0], in1=CT2[:, :, 1], op=ALU.add), reads=[tc2], writes=[tc2])
            fw.op("act", lambda e: e.copy(out=CTb[:, :, 0], in_=CT2[:, :, 0]), reads=[tc2], writes=[tCT])
            fw.op("dve", lambda e: e.tensor_tensor(out=CT2[:, :, 0], in0=CTf[:, :, 0], in1=nbib, op=ALU.mult), reads=[tf2, tb, tCT, tc2], writes=[tc2])
            fw.op("pool", lambda e: e.tensor_tensor(out=CT2[:, :, 1], in0=CTf[:, :, 1], in1=brb, op=ALU.mult), reads=[tf2, tb, tc2], writes=[tc2])
            fw.op("dve", lambda e: e.tensor_tensor(out=CT2[:, :, 0], in0=CT2[:, :, 0], in1=CT2[:, :, 1], op=ALU.subtract), reads=[tc2], writes=[tc2])
            fw.op("act", lambda e: e.copy(out=CTb[:, :, 1], in_=CT2[:, :, 0]), reads=[tc2], writes=[tCT])
            fw.barrier()
        ub = P.sb([128, 2, TE], BF16)
        tub = Tok()
        for ct in range(2):
            fw.dma("pool", ub[:, ct, :], ZT[ct * 128:(ct + 1) * 128, :], writes=[tub])
        dk = P.sb([128, 2], F32)
        tdk = Tok()
        fw.dma("sp", dk[:], dsk.rearrange("(c p) -> p c", p=128), writes=[tdk], allow_slow_non_contiguous=True)
        y = P.sb([128, 2, T], F32)
        ty = Tok()
        for ct in range(2):
            fw.dma("sp", y[:, ct, :], ZT[ct * 128:(ct + 1) * 128, 0:T], writes=[ty])
        for ct in range(2):
            fw.op("pool", lambda e, ct=ct: e.tensor_scalar(out=y[:, ct, :], in0=y[:, ct, :], scalar1=dk[:, ct:ct + 1], scalar2=None, op0=ALU.mult),
                  reads=[ty, tdk], writes=[ty])
        Xs = Rot([P.sb([128, 2, N], F32) for _ in range(4)])
        xbs = Rot([P.sb([128, 2, N], BF16) for _ in range(2)])
        psr = Rot([P.ps() for _ in range(4)])
        psy = Rot([P.ps() for _ in range(2)])
        tmpy = Rot([P.sb([128, 512], F32) for _ in range(2)])
        CH = [(0, 512), (512, 1024), (1024, 1536), (1536, 2048), (2048, 2304)]

        def prep(di, s):
            off = 0 if di == 0 else 256
            ct = s // 4
            X, tX = Xs.next()
            for ri in range(2):
                for ci, (c0, c1) in enumerate(CH):
                    n = c1 - c0
                    ps, tps = psr.next()
                    fw.op("pe", lambda e, ps=ps, ri=ri, c0=c0, c1=c1, n=n: e.matmul(
                        ps[:, :n], lhsT=BTb[:, di, ri, s, :], rhs=ub[:, ct, off + c0:off + c1], start=True, stop=True),
                        reads=[tBT, tub], writes=[tps])
                    fw.op("act", lambda e, ps=ps, ri=ri, c0=c0, c1=c1, n=n: e.copy(out=X[:, ri, c0:c1], in_=ps[:, :n]),
                          reads=[tps], writes=[tX])
            return X, tX

        def scan_gen(di, s, X, tX):
            def cstep(w_re, w_im, r_re, r_im, k):
                pr = PW[:, k, 0, di, s:s + 1]
                pi = PW[:, k, 1, di, s:s + 1]
                npi = PW[:, k, 2, di, s:s + 1]
                for (o, i0, sc) in ((w_re, r_re, pr), (w_re, r_im, npi), (w_im, r_im, pr), (w_im, r_re, pi)):
                    fw.op("dve", lambda e, o=o, i0=i0, sc=sc: e.scalar_tensor_tensor(out=o, in0=i0, scalar=sc, in1=o, op0=ALU.mult, op1=ALU.add),
                          reads=[tX, tpw], writes=[tX])
                    yield
            for k in range(8):
                st_ = 1 << k
                Xv = [X[:, ri, :].rearrange("p (m c) -> p m c", c=2 * st_) for ri in range(2)]
                if di == 0:
                    yield from cstep(Xv[0][:, :, 2 * st_ - 1], Xv[1][:, :, 2 * st_ - 1], Xv[0][:, :, st_ - 1], Xv[1][:, :, st_ - 1], k)
                else:
                    yield from cstep(Xv[0][:, :, 0], Xv[1][:, :, 0], Xv[0][:, :, st_], Xv[1][:, :, st_], k)
            for i in (range(1, 9) if di == 0 else range(7, -1, -1)):
                if di == 0:
                    w, r = 256 * i + 255, 256 * (i - 1) + 255
                else:
                    w, r = 256 * i, 256 * (i + 1)
                yield from cstep(X[:, 0, w:w + 1], X[:, 1, w:w + 1], X[:, 0, r:r + 1], X[:, 1, r:r + 1], 8)
            for k in range(7, -1, -1):
                st_ = 1 << k
                Xv = [X[:, ri, :].rearrange("p (m c) -> p m c", c=2 * st_) for ri in range(2)]
                if di == 0:
                    yield from cstep(Xv[0][:, 1:, st_ - 1], Xv[1][:, 1:, st_ - 1], Xv[0][:, :-1, 2 * st_ - 1], Xv[1][:, :-1, 2 * st_ - 1], k)
                else:
                    yield from cstep(Xv[0][:, :-1, st_], Xv[1][:, :-1, st_], Xv[0][:, 1:, 0], Xv[1][:, 1:, 0], k)

        def fin(di, s, X, tX):
            ct = s // 4
            xb, txb = xbs.next()
            fw.op("act", lambda e: e.copy(out=xb[:], in_=X[:]), reads=[tX], writes=[txb])
            for (c0, c1) in CH:
                n = c1 - c0
                if di == 0:
                    y0 = c0
                else:
                    y0 = c0 + 256 if c0 < 2048 else 0
                ps, tps = psy.next()
                for ri in range(2):
                    fw.op("pe", lambda e, ps=ps, ri=ri, c0=c0, c1=c1, n=n: e.matmul(
                        ps[:, :n], lhsT=CTb[:, di, ri, s, :], rhs=xb[:, ri, c0:c1], start=(ri == 0), stop=(ri == 1)),
                        reads=[tCT, txb], writes=[tps])
                tm, ttm = tmpy.next()
                fw.op("act", lambda e, tm=tm, ps=ps, n=n: e.copy(out=tm[:, :n], in_=ps[:, :n]), reads=[tps], writes=[ttm])
                fw.op("pool", lambda e, tm=tm, y0=y0, n=n: e.tensor_tensor(out=y[:, ct, y0:y0 + n], in0=y[:, ct, y0:y0 + n], in1=tm[:, :n], op=ALU.add),
                      reads=[ttm, ty], writes=[ty])

        pairs = [(di, s0, s0 + 1) for di in range(2) for s0 in range(0, 8, 2)]
        nxt = [prep(pairs[0][0], pairs[0][1]), prep(pairs[0][0], pairs[0][2])]
        for pi_, (di, s0, s1) in enumerate(pairs):
            cur = nxt
            if pi_ + 1 < len(pairs):
                d2, a0, a1 = pairs[pi_ + 1]
                nxt = [prep(d2, a0), prep(d2, a1)]
            gens = [scan_gen(di, s0, *cur[0]), scan_gen(di, s1, *cur[1])]
            alive = [True, True]
            while any(alive):
                for gi in range(2):
                    if alive[gi]:
                        try:
                            next(gens[gi])
                        except StopIteration:
                            alive[gi] = False
            fin(di, s0, *cur[0])
            fin(di, s1, *cur[1])
        tg = Tok()
        gw = load_w_bf16(fw, P, glu_w, 256, 256, tg, "gluw")
        gb = P.sb([128, 2], F32)
        og = P.sb([128, 2], F32)
        fw.dma("sp", gb[:], glu_b.rearrange("(c p) -> p c", p=128), writes=[tg], allow_slow_non_contiguous=True)
        fw.dma("sp", og[:], out_g.rearrange("(c p) -> p c", p=128), writes=[tg], allow_slow_non_contiguous=True)
        ones = P.sb([128, 128], BF16)
        eps = P.sb([128, 1], F32)
        fw.op("pool", lambda e: e.memset(ones[:], 1.0), writes=[tg])
        fw.op("pool", lambda e: e.memset(eps[:], RMS_EPS), writes=[tg])
        a32 = P.sb([128, 2, 512], F32)
        ab = P.sb([128, 2, 512], BF16)
        w1 = P.sb([128, 2, 512], F32)
        w2 = P.sb([128, 2, 512], F32)
        sqb = P.sb([128, 2, 512], BF16)
        rs = P.sb([128, 512], F32)
        ta, tw1, trs = Tok(), Tok(), Tok()
        stg = Rot([P.sb([128, 512], F32) for _ in range(2)])
        tout = Tok()
        C1 = math.sqrt(2.0 / math.pi)
        for (lo, hi, ic) in SEGS:
            n = hi - lo
            yv = y[:, :, lo:hi]
            fw.op("act", lambda e, yv=yv, n=n: e.activation(out=w1[:, :, :n], in_=yv, func=AF.Square), reads=[ty], writes=[tw1])
            fw.op("dve", lambda e, n=n: e.tensor_scalar(out=w1[:, :, :n], in0=w1[:, :, :n], scalar1=0.044715 * C1, scalar2=C1, op0=ALU.mult, op1=ALU.add),
                  reads=[tw1], writes=[tw1])
            fw.op("dve", lambda e, yv=yv, n=n: e.tensor_tensor(out=w1[:, :, :n], in0=w1[:, :, :n], in1=yv, op=ALU.mult), reads=[tw1, ty], writes=[tw1])
            fw.op("act", lambda e, n=n: e.activation(out=w1[:, :, :n], in_=w1[:, :, :n], func=AF.Tanh), reads=[tw1], writes=[tw1])
            fw.op("dve", lambda e, n=n: e.tensor_scalar(out=w1[:, :, :n], in0=w1[:, :, :n], scalar1=0.5, scalar2=0.5, op0=ALU.mult, op1=ALU.add),
                  reads=[tw1], writes=[tw1])
            fw.op("dve", lambda e, yv=yv, n=n: e.tensor_tensor(out=a32[:, :, :n], in0=w1[:, :, :n], in1=yv, op=ALU.mult), reads=[tw1, ty], writes=[ta])
            fw.op("act", lambda e, n=n: e.copy(out=ab[:, :, :n], in_=a32[:, :, :n]), reads=[ta], writes=[ta])
            for nt in range(2):
                ps, tps = psr.next()
                for k in range(2):
                    fw.op("pe", lambda e, ps=ps, k=k, nt=nt, n=n: e.matmul(ps[:, :n], lhsT=gw[:, k, nt * 128:(nt + 1) * 128], rhs=ab[:, k, :n],
                                                                      start=(k == 0), stop=(k == 1)), reads=[tg, ta], writes=[tps])
                fw.op("act", lambda e, ps=ps, nt=nt, n=n: e.activation(out=w2[:, nt, :n], in_=ps[:, :n], func=AF.Sigmoid, bias=gb[:, nt:nt + 1]),
                      reads=[tps, tg], writes=[tw1])
            fw.op("dve", lambda e, n=n: e.tensor_tensor(out=w2[:, :, :n], in0=w2[:, :, :n], in1=a32[:, :, :n], op=ALU.mult), reads=[tw1, ta], writes=[tw1])
            fw.op("act", lambda e, n=n: e.activation(out=sqb[:, :, :n], in_=w2[:, :, :n], func=AF.Square), reads=[tw1], writes=[tw1])
            ps, tps = psr.next()
            for k in range(2):
                fw.op("pe", lambda e, ps=ps, k=k, n=n: e.matmul(ps[:, :n], lhsT=ones[:], rhs=sqb[:, k, :n], start=(k == 0), stop=(k == 1)),
                      reads=[tw1, tg], writes=[tps])
            rstd_from_ps(fw, rs, trs, ps, tps, n, 1.0 / 256, eps[:, 0:1], tg)
            for k in range(2):
                sg, tsg = stg.next()
                fw.op("dve", lambda e, sg=sg, k=k, n=n: e.scalar_tensor_tensor(out=sg[:, :n], in0=w2[:, k, :n], scalar=og[:, k:k + 1], in1=rs[:, :n],
                                                                             op0=ALU.mult, op1=ALU.mult), reads=[tw1, trs, tg], writes=[tsg])
                fw.dma("sp", S5T[k * 128:(k + 1) * 128, lo:hi], sg[:, :n], reads=[tsg], writes=[tout])
        fw.barrier()


RW_BASE = 1024
CHK = 64
NCH = T // CHK
LN_EPS_RW = 64e-5


def rw_consts():
    idx = np.arange(64)
    m = np.zeros((64, 4, 64), np.float32)
    m[:, 0, :] = (idx[:, None] < idx[None, :])
    m[:, 1, :] = (idx[:, None] > idx[None, :])
    m[:, 2, :] = (idx[:, None] <= idx[None, :])
    m[:, 3, :] = (idx[:, None] >= idx[None, :])
    bd = np.zeros((128, 128), np.float32)
    bd[:64, :64] = 1.0
    bd[64:, 64:] = 1.0
    return m, bd


def stage_p4(fw, io):
    ZT = io("ZT", [2048, TE], F32, "in")
    mu = io("rw_mu", [960], F32, "in")
    w0 = io("rw_w0", [2, 256], F32, "in")
    w2 = io("rw_w2", [2, 32, 256], F32, "in")
    a0 = io("rw_a0", [2, 256], F32, "in")
    a2 = io("rw_a2", [2, 32, 256], F32, "in")
    g2 = io("rw_g2", [64, 256], F32, "in")
    k_k = io("rw_k_k", [256], F32, "in")
    k_a = io("rw_k_a", [256], F32, "in")
    r_k = io("rw_r_k", [256], F32, "in")
    ln_g = io("rw_ln_g", [256], F32, "in")
    ln_b = io("rw_ln_b", [256], F32, "in")
    ident = io("ident", [128, 128], F32, "in")
    MASKS = io("RWMASK", [64, 4, 64], F32, "in")
    BD = io("RWBD", [128, 128], F32, "in")
    RWT = io("RWT", [256, T], F32, "out")
    N = T
    CH5 = [(0, 512), (512, 1024), (1024, 1536), (1536, 2048), (2048, 2560)]
    with ExitStack() as es:
        P = Pool_(fw, es)
        tc = Tok()
        idt = P.sb([128, 128], F32)
        idb = P.sb([128, 128], BF16)
        msk = P.sb([64, 4, 64], F32)
        bd1 = P.sb([128, 128], F32)
        fw.dma("sp", idt[:], ident, writes=[tc])
        fw.dma("sp", msk[:], MASKS, writes=[tc])
        fw.dma("sp", bd1[:], BD, writes=[tc])
        fw.op("dve", lambda e: e.tensor_copy(out=idb[:], in_=idt[:]), reads=[tc], writes=[tc])
        mrep = P.sb([64, 4, 4, 64], F32)
        for rep in range(4):
            fw.op("dve", lambda e, rep=rep: e.tensor_copy(out=mrep[:, :, rep, :], in_=msk[:]), reads=[tc], writes=[tc])
        pp = P.sb([128, 12, 2], F32)
        tpp = Tok()
        srcs = [w0[0], w0[1], a0[0], a0[1], k_k, k_a, k_a, r_k, ln_g, ln_b]
        for i, sap in enumerate(srcs):
            fw.dma("sp", pp[:, i, :], sap.rearrange("(c p) -> p c", p=128), writes=[tpp], allow_slow_non_contiguous=True)
        fw.op("dve", lambda e: e.tensor_scalar(out=pp[:, 6, :], in0=pp[:, 6, :], scalar1=-1.0, scalar2=1.0, op0=ALU.mult, op1=ALU.add),
              reads=[tpp], writes=[tpp])
        epsl = P.sb([128, 2], F32)
        fw.op("pool", lambda e: e.memset(epsl[:, 0:1], LN_EPS_RW), writes=[tpp])
        fw.op("pool", lambda e: e.memset(epsl[:, 1:2], 1e-24), writes=[tpp])
        wA = P.sb([128, 256], BF16)
        wB = P.sb([128, 256], BF16)
        tlw = Tok()
        fw.dma("pool", wA[0:32, :], w2[0], writes=[tlw])
        fw.dma("pool", wA[32:64, :], w2[1], writes=[tlw])
        fw.dma("pool", wA[64:96, :], a2[0], writes=[tlw])
        fw.dma("pool", wB[0:32, :], a2[1], writes=[tlw])
        fw.dma("pool", wB[64:128, :], g2, writes=[tlw])
        smask = P.sb([128, N], BF16)
        tsm = Tok()
        fw.op("pool", lambda e: e.memset(smask[:], 1.0), writes=[tsm])
        fw.op("pool", lambda e: e.memset(smask[:].rearrange("p (c j) -> p c j", j=CHK)[:, :, 0], 0.0), writes=[tsm])

        def load_shift(dst, tdst, row0, rows, P2, post=None, pb=0):
            zt_ = P2.sb([128, TE], F32)
            nbt_ = P2.sb([128, TE], F32)
            mtt_ = P2.sb([128, 2], F32)
            tz, tnb, tm = Tok(), Tok(), Tok()
            ps_ = slice(pb, pb + rows)
            z = zt_[ps_, :]
            fw.dma("sp", z, ZT[RW_BASE + row0:RW_BASE + row0 + rows, :], writes=[tz])
            fw.dma("sp", mtt_[ps_, 0:1], mu[row0:row0 + rows].rearrange("(p o) -> p o", o=1), writes=[tm])
            fw.op("dve", lambda e: e.tensor_scalar(out=mtt_[ps_, 1:2], in0=mtt_[ps_, 0:1], scalar1=0.5, scalar2=None, op0=ALU.mult), reads=[tm], writes=[tm])
            fw.op("dve", lambda e: e.tensor_scalar(out=mtt_[ps_, 0:1], in0=mtt_[ps_, 0:1], scalar1=-1.0, scalar2=1.0, op0=ALU.mult, op1=ALU.add),
                  reads=[tm], writes=[tm])
            fw.op("pool", lambda e: e.memset(nbt_[ps_, 0:1], 0.0), writes=[tnb])
            fw.op("pool", lambda e: e.tensor_copy(out=nbt_[ps_, 1:TE], in_=zt_[ps_, 0:TE - 1]), reads=[tz], writes=[tnb])
            fw.op("pool", lambda e: e.tensor_tensor(out=nbt_[ps_, 0:TE - 1], in0=nbt_[ps_, 0:TE - 1], in1=zt_[ps_, 1:TE], op=ALU.add),
                  reads=[tz, tnb], writes=[tnb])
            for cb in (256, 2304):
                fw.op("pool", lambda e, cb=cb: e.tensor_tensor(out=nbt_[ps_, cb:cb + 1], in0=nbt_[ps_, cb:cb + 1], in1=zt_[ps_, cb - 1:cb], op=ALU.subtract),
                      reads=[tz, tnb], writes=[tnb])
                fw.op("pool", lambda e, cb=cb: e.tensor_tensor(out=nbt_[ps_, cb - 1:cb], in0=nbt_[ps_, cb - 1:cb], in1=zt_[ps_, cb:cb + 1], op=ALU.subtract),
                      reads=[tz, tnb], writes=[tnb])
            fw.op("act", lambda e: e.activation(out=z, in_=z, func=AF.Identity, scale=mtt_[ps_, 0:1]), reads=[tz, tm], writes=[tz])
            if post is None:
                fw.op("dve", lambda e: e.scalar_tensor_tensor(out=dst, in0=nbt_[ps_, :], scalar=mtt_[ps_, 1:2], in1=z, op0=ALU.mult, op1=ALU.add),
                      reads=[tz, tnb, tm], writes=[tdst])
            else:
                fw.op("dve", lambda e: e.scalar_tensor_tensor(out=z, in0=nbt_[ps_, :], scalar=mtt_[ps_, 1:2], in1=z, op0=ALU.mult, op1=ALU.add),
                      reads=[tz, tnb, tm], writes=[tz])
                fw.op("act", lambda e: e.activation(out=dst, in_=z, func=post), reads=[tz], writes=[tdst])

        lorA = P.sb([128, TE], BF16)
        lorB = P.sb([128, TE], BF16)
        tlor = Tok()
        for i in range(4):
            with ExitStack() as es2:
                dstt = lorA[32 * i:32 * i + 32, :] if i < 3 else lorB[0:32, :]
                load_shift(dstt, tlor, 768 + 32 * i, 32, Pool_(fw, es2), post=(AF.Tanh if i < 2 else AF.Copy), pb=(32 * i if i < 3 else 0))
                fw.barrier()
        with ExitStack() as es2:
            load_shift(lorB[64:128, :], tlor, 896, 64, Pool_(fw, es2), post=AF.Sigmoid, pb=64)
            fw.barrier()

        for c in range(2):
            with ExitStack() as esc:
                Pc = Pool_(fw, esc)
                rr = Pc.sb([128, TE], F32)
                kx = Pc.sb([128, TE], F32)
                vv = Pc.sb([128, TE], F32)
                kk = Pc.sb([128, TE], F32)
                trr, tkx, tvv, tkk = Tok(), Tok(), Tok(), Tok()
                vtok = Pc.sb([64, TE // CHK, 128], BF16)
                tvt = Tok()
                yacc = Pc.sb([128, T], F32)
                bon = Pc.sb([128, T], F32)
                tya, tbon = Tok(), Tok()
                fw.op("pool", lambda e: e.memset(yacc[:], 0.0), writes=[tya])
                fw.op("pool", lambda e: e.memset(bon[:], 0.0), writes=[tbon])
                for (dst, tdst, r0) in ((rr, trr, 0), (kx, tkx, 256), (vv, tvv, 512)):
                    with ExitStack() as es2:
                        load_shift(dst[:], tdst, r0 + c * 128, 128, Pool_(fw, es2))
                        fw.barrier()
                with ExitStack() as es2:
                    P2 = Pool_(fw, es2)
                    sq = P2.sb([128, 512], F32)
                    rn = P2.sb([128, 512], F32)
                    tsq, trn = Tok(), Tok()
                    ps1 = P2.ps()
                    tps1 = Tok()
                    fw.op("dve", lambda e: e.tensor_scalar(out=kk[:], in0=kx[:], scalar1=pp[:, 4, c:c + 1], scalar2=None, op0=ALU.mult),
                          reads=[tkx, tpp], writes=[tkk])
                    for (c0, c1) in CH5:
                        fw.op("act", lambda e, c0=c0, c1=c1: e.activation(out=sq[:], in_=kk[:, c0:c1], func=AF.Square), reads=[tkk], writes=[tsq])
                        fw.op("pe", lambda e: e.matmul(ps1[:], lhsT=bd1[:], rhs=sq[:], start=True, stop=True), reads=[tsq, tc], writes=[tps1])
                        rstd_from_ps(fw, rn, trn, ps1, tps1, 512, 1.0, epsl[:, 1:2], tpp)
                        fw.op("dve", lambda e, c0=c0, c1=c1: e.tensor_tensor(out=kk[:, c0:c1], in0=kk[:, c0:c1], in1=rn[:], op=ALU.mult),
                              reads=[tkk, trn], writes=[tkk])
                    vb = P2.sb([128, TE], BF16)
                    tvb = Tok()
                    fw.op("act", lambda e: e.copy(out=vb[:], in_=vv[:]), reads=[tvv], writes=[tvb])
                    pst = P2.ps([128, 1024], BF16)
                    tpst = Tok()
                    for q in range(TE // CHK // 4):
                        for j in range(4):
                            ch = q * 4 + j
                            fw.op("pe", lambda e, j=j, ch=ch: e.transpose(pst[0:64, j * 128:(j + 1) * 128], vb[:, ch * CHK:(ch + 1) * CHK], idb[:]),
                                  reads=[tvb, tc], writes=[tpst])
                        fw.op("dve", lambda e, q=q: e.tensor_copy(out=vtok[:, q * 4:(q + 1) * 4, :], in_=pst[0:64, 0:512].rearrange("p (j c) -> p j c", j=4)),
                              reads=[tpst], writes=[tvt])
                    fw.barrier()

                for di in range(2):
                    off = 0 if di == 0 else 256
                    with ExitStack() as esd:
                        Pd = Pool_(fw, esd)
                        aT = Pd.sb([128, N], BF16)
                        bT = Pd.sb([128, N], BF16)
                        kT = Pd.sb([128, N], BF16)
                        rT = Pd.sb([128, N], BF16)
                        btok = Pd.sb([64, NCH, 128], BF16)
                        ktok = Pd.sb([64, NCH, 128], BF16)
                        pC = Pd.sb([128, NCH], F32)
                        tops = Tok()
                        ttok = Tok()
                        with ExitStack() as es2:
                            P2 = Pool_(fw, es2)
                            ld = P2.sb([128, TE], F32)
                            kd = P2.sb([128, TE], F32)
                            bb = P2.sb([128, TE], F32)
                            tld, tkd, tbb = Tok(), Tok(), Tok()
                            psr = Rot([P2.ps() for _ in range(3)])
                            tm5 = Rot([P2.sb([128, 512], F32) for _ in range(2)])
                            for (c0, c1) in CH5:
                                ps, tps = psr.next()
                                fw.op("pe", lambda e, ps=ps, c0=c0, c1=c1: e.matmul(ps[:], lhsT=wA[32 * di:32 * di + 32, c * 128:(c + 1) * 128], rhs=lorA[32 * di:32 * di + 32, c0:c1],
                                                                                  start=True, stop=True), reads=[tlw, tlor], writes=[tps])
                                fw.op("act", lambda e, ps=ps, c0=c0, c1=c1: e.activation(out=ld[:, c0:c1], in_=ps[:], func=AF.Sigmoid, bias=pp[:, di, c:c + 1]),
                                      reads=[tps, tpp], writes=[tld])
                                ps, tps = psr.next()
                                fw.op("pe", lambda e, ps=ps, c0=c0, c1=c1: e.matmul(ps[:], lhsT=(wA[64:96, c * 128:(c + 1) * 128] if di == 0 else wB[0:32, c * 128:(c + 1) * 128]),
                                                                                  rhs=(lorA[64:96, c0:c1] if di == 0 else lorB[0:32, c0:c1]),
                                                                                  start=True, stop=True), reads=[tlw, tlor], writes=[tps])
                                fw.op("act", lambda e, ps=ps, c0=c0, c1=c1: e.activation(out=bb[:, c0:c1], in_=ps[:], func=AF.Sigmoid, bias=pp[:, 2 + di, c:c + 1]),
                                      reads=[tps, tpp], writes=[tbb])
                            fw.op("pool", lambda e: e.tensor_scalar(out=ld[:], in0=ld[:], scalar1=-math.exp(-0.5), scalar2=None, op0=ALU.mult), reads=[tld], writes=[tld])
                            fw.op("act", lambda e: e.activation(out=kd[:], in_=bb[:], func=AF.Identity, scale=pp[:, 5, c:c + 1], bias=pp[:, 6, c:c + 1]),
                                  reads=[tbb, tpp], writes=[tkd])
                            fw.op("dve", lambda e: e.tensor_tensor(out=kd[:], in0=kd[:], in1=kx[:], op=ALU.mult), reads=[tkd, tkx], writes=[tkd])
                            fw.op("pool", lambda e: e.tensor_tensor(out=bb[:], in0=bb[:], in1=kk[:], op=ALU.mult), reads=[tbb, tkk], writes=[tbb])
                            for (c0, c1) in CH5:
                                c1 = min(c1, T)
                                n = c1 - c0
                                tm, ttm = tm5.next()
                                fw.op("dve", lambda e, tm=tm, c0=c0, c1=c1, n=n: e.scalar_tensor_tensor(out=tm[:, :n], in0=rr[:, c0:c1], scalar=pp[:, 7, c:c + 1], in1=kd[:, c0:c1],
                                                                                                 op0=ALU.mult, op1=ALU.mult), reads=[trr, tkd, tpp], writes=[ttm])
                                ps, tps = psr.next()
                                fw.op("pe", lambda e, ps=ps, tm=tm, n=n: e.matmul(ps[:, :n], lhsT=bd1[:], rhs=tm[:, :n], start=True, stop=True), reads=[ttm, tc], writes=[tps])
                                tm2, ttm2 = tm5.next()
                                fw.op("dve", lambda e, tm2=tm2, ps=ps, c0=c0, c1=c1, n=n: e.tensor_tensor(out=tm2[:, :n], in0=ps[:, :n], in1=vv[:, c0:c1], op=ALU.mult),
                                      reads=[tps, tvv], writes=[ttm2])
                                fw.op("pool", lambda e, tm2=tm2, c0=c0, c1=c1, n=n: e.tensor_tensor(out=bon[:, c0:c1], in0=bon[:, c0:c1], in1=tm2[:, :n], op=ALU.add),
                                      reads=[ttm2, tbon], writes=[tbon])
                            cs = P2.sb([128, N], F32)
                            ex = P2.sb([128, N], F32)
                            tcs, tex = Tok(), Tok()
                            ldw = ld[:, off:off + N]
                            fw.op("dve", lambda e: e.tensor_tensor_scan(out=cs[:], data0=smask[:], data1=ldw, initial=0.0, op0=ALU.mult, op1=ALU.add),
                                  reads=[tsm, tld], writes=[tcs])
                            csv = cs[:].rearrange("p (c j) -> p c j", j=CHK)
                            tot = P2.sb([128, NCH, 1], F32)
                            ttot = Tok()
                            fw.op("dve", lambda e: e.tensor_copy(out=tot[:], in_=csv[:, :, CHK - 1:CHK]), reads=[tcs], writes=[ttot])
                            totb = tot[:].to_broadcast([128, NCH, CHK])
                            if di == 1:
                                fw.op("dve", lambda e: e.tensor_tensor(out=csv, in0=totb, in1=csv, op=ALU.subtract), reads=[ttot, tcs], writes=[tcs])
                                fw.op("dve", lambda e: e.tensor_tensor(out=cs[:], in0=cs[:], in1=ldw, op=ALU.add), reads=[tcs, tld], writes=[tcs])
                            fw.op("act", lambda e: e.activation(out=pC[:], in_=tot[:, :, 0], func=AF.Exp), reads=[ttot], writes=[tops])
                            kdw, bbw = kd[:, off:off + N], bb[:, off:off + N]
                            rrw, kkw = rr[:, off:off + N], kk[:, off:off + N]
                            fw.op("act", lambda e: e.activation(out=ex[:], in_=cs[:], func=AF.Exp), reads=[tcs], writes=[tex])
                            fw.op("dve", lambda e: e.tensor_tensor(out=rT[:], in0=rrw, in1=ex[:], op=ALU.mult), reads=[trr, tex], writes=[tops])
                            fw.op("act", lambda e: e.activation(out=ex[:], in_=cs[:], func=AF.Exp, scale=-1.0), reads=[tcs, tops], writes=[tex])
                            fw.op("dve", lambda e: e.tensor_tensor(out=bT[:], in0=bbw, in1=ex[:], op=ALU.mult), reads=[tbb, tex], writes=[tops])
                            fw.op("pool", lambda e: e.tensor_tensor(out=kT[:], in0=kdw, in1=ex[:], op=ALU.mult), reads=[tkd, tex], writes=[tops])
                            e3 = ex
                            te3 = tex
                            fw.op("dve", lambda e: e.tensor_tensor(out=e3[:], in0=cs[:], in1=ldw, op=ALU.subtract), reads=[tcs, tld], writes=[te3])
                            fw.op("act", lambda e: e.activation(out=e3[:], in_=e3[:], func=AF.Exp), reads=[te3], writes=[te3])
                            fw.op("dve", lambda e: e.scalar_tensor_tensor(out=aT[:], in0=kkw, scalar=-1.0, in1=e3[:], op0=ALU.mult, op1=ALU.mult),
                                  reads=[tkk, te3], writes=[tops])
                            e3v = e3[:].rearrange("p (c j) -> p c j", j=CHK)
                            fw.op("dve", lambda e: e.tensor_tensor(out=e3v, in0=totb, in1=csv, op=ALU.subtract), reads=[ttot, tcs, tops, te3], writes=[te3])
                            fw.op("act", lambda e: e.activation(out=e3[:], in_=e3[:], func=AF.Exp), reads=[te3], writes=[te3])
                            bh = P2.sb([128, N], BF16)
                            kh = P2.sb([128, N], BF16)
                            tbh = Tok()
                            fw.op("dve", lambda e: e.tensor_tensor(out=bh[:], in0=bbw, in1=e3[:], op=ALU.mult), reads=[tbb, te3], writes=[tbh])
                            fw.op("pool", lambda e: e.tensor_tensor(out=kh[:], in0=kdw, in1=e3[:], op=ALU.mult), reads=[tkd, te3], writes=[tbh])
                            pst = P2.ps([128, 1024], BF16)
                            tpst = Tok()
                            for (src, dstt) in ((bh, btok), (kh, ktok)):
                                for q in range(NCH // 4):
                                    for j in range(4):
                                        ch = q * 4 + j
                                        fw.op("pe", lambda e, j=j, ch=ch, src=src: e.transpose(pst[0:64, j * 128:(j + 1) * 128], src[:, ch * CHK:(ch + 1) * CHK], idb[:]),
                                              reads=[tbh, tc], writes=[tpst])
                                    fw.op("act", lambda e, q=q, dstt=dstt: e.copy(out=dstt[:, q * 4:(q + 1) * 4, :], in_=pst[0:64, 0:512].rearrange("p (j c) -> p j c", j=4)),
                                          reads=[tpst], writes=[ttok])
                            fw.barrier()
                        with ExitStack() as es3:
                            P3 = Pool_(fw, es3)
                            Hs = P3.sb([128, 64], F32)
                            Hb = P3.sb([128, 64], BF16)
                            tH = Tok()
                            fw.op("pool", lambda e: e.memset(Hs[:], 0.0), writes=[tH])
                            fw.op("pool", lambda e: e.memset(Hb[:], 0.0), writes=[tH])
                            psA = Rot([P3.ps([64, 512]) for _ in range(1)])
                            psB = Rot([P3.ps([64, 512]) for _ in range(1)])
                            psI = Rot([P3.ps([64, 512]) for _ in range(1)])
                            psG = Rot([P3.ps([64, 512]) for _ in range(1)])
                            psH = Rot([P3.ps([128, 512]) for _ in range(1)])
                            psY = Rot([P3.ps([128, 512]) for _ in range(1)])
                            g1b = Rot([P3.sb([64, 2, 64], BF16) for _ in range(2)])
                            nl = Rot([P3.sb([64, 2, 2, 64], F32) for _ in range(2)])
                            nl2 = Rot([P3.sb([64, 2, 2, 64], F32) for _ in range(2)])
                            g2b_ = Rot([P3.sb([64, 4, 64], BF16) for _ in range(2)])
                            Pm = Rot([P3.sb([64, 2, 64], F32) for _ in range(2)])
                            Gs = Rot([P3.sb([64, 128], F32) for _ in range(2)])
                            Ub = Rot([P3.sb([64, 128], BF16) for _ in range(2)])
                            Ys = Rot([P3.sb([64, 128], F32) for _ in range(2)])
                            ms, ml, mi = (0, 1, 2) if di == 0 else (1, 0, 3)
                            order = list(range(NCH)) if di == 0 else list(range(NCH - 1, -1, -1))
                            res = {}

                            def par_gen(i):
                                cc0, cc1 = i * CHK, (i + 1) * CHK
                                pa, tpa = psA.next()
                                pb, tpb = psB.next()
                                for h in range(2):
                                    hp = slice(h * 64, (h + 1) * 64)
                                    for (dst, lt, rt_) in ((pa[:, h * 64:(h + 1) * 64], kT, aT), (pa[:, 128 + h * 64:128 + (h + 1) * 64], bT, aT),
                                                           (pa[:, 256 + h * 64:256 + (h + 1) * 64], aT, bT)):
                                        fw.op("pe", lambda e, dst=dst, lt=lt, rt_=rt_, hp=hp: e.matmul(dst, lhsT=lt[hp, cc0:cc1], rhs=rt_[hp, cc0:cc1], start=True, stop=True),
                                              reads=[tops], writes=[tpa])
                                        yield
                                    for (dst, lt, rt_) in ((pb[:, h * 64:(h + 1) * 64], bT, rT), (pb[:, 128 + h * 64:128 + (h + 1) * 64], kT, rT)):
                                        fw.op("pe", lambda e, dst=dst, lt=lt, rt_=rt_, hp=hp: e.matmul(dst, lhsT=lt[hp, cc0:cc1], rhs=rt_[hp, cc0:cc1], start=True, stop=True),
                                              reads=[tops], writes=[tpb])
                                        yield
                                a1, ta1 = g1b.next()
                                nlt, tnl = nl.next()
                                a45, ta45 = g2b_.next()
                                fw.op("dve", lambda e: e.tensor_tensor(out=a1[:], in0=pa[:, 0:128].rearrange("p (h t) -> p h t", h=2), in1=mrep[:, ms, 0:2, :], op=ALU.mult),
                                      reads=[tpa, tc], writes=[ta1])
                                yield
                                fw.op("dve", lambda e: e.tensor_tensor(out=nlt[:, 0], in0=pa[:, 128:256].rearrange("p (h t) -> p h t", h=2), in1=mrep[:, ms, 0:2, :], op=ALU.mult),
                                      reads=[tpa, tc], writes=[tnl])
                                yield
                                fw.op("dve", lambda e: e.tensor_tensor(out=nlt[:, 1], in0=pa[:, 256:384].rearrange("p (h t) -> p h t", h=2), in1=mrep[:, ml, 0:2, :], op=ALU.mult),
                                      reads=[tpa, tc], writes=[tnl])
                                yield
                                fw.op("dve", lambda e: e.tensor_tensor(out=a45[:], in0=pb[:, 0:256].rearrange("p (h t) -> p h t", h=4), in1=mrep[:, mi, :, :], op=ALU.mult),
                                      reads=[tpb, tc], writes=[ta45])
                                yield
                                pm, tpm = Pm.next()
                                fw.op("dve", lambda e: e.tensor_tensor(out=pm[:], in0=nlt[:, 0], in1=idt[0:64, 0:64].unsqueeze(1).to_broadcast([64, 2, 64]), op=ALU.add),
                                      reads=[tnl, tc], writes=[tpm])
                                yield
                                cur, tcur = nlt, tnl
                                for lev in range(5):
                                    pi_, tpi = psI.next()
                                    for h in range(2):
                                        fw.op("pe", lambda e, pi_=pi_, h=h, cur=cur: e.matmul(pi_[:, h * 64:(h + 1) * 64], lhsT=cur[:, 0, h, :], rhs=cur[:, 1, h, :], start=True, stop=True),
                                              reads=[tcur], writes=[tpi])
                                        yield
                                        fw.op("pe", lambda e, pi_=pi_, h=h, cur=cur: e.matmul(pi_[:, 128 + h * 64:128 + (h + 1) * 64], lhsT=cur[:, 1, h, :], rhs=cur[:, 0, h, :], start=True, stop=True),
                                              reads=[tcur], writes=[tpi])
                                        yield
                                    nxt, tnxt = (nl2.next() if lev % 2 == 0 else nl.next())
                                    fw.op("act", lambda e, nxt=nxt, pi_=pi_: e.copy(out=nxt[:, 1], in_=pi_[:, 0:128].rearrange("p (h t) -> p h t", h=2)), reads=[tpi], writes=[tnxt])
                                    yield
                                    fw.op("act", lambda e, nxt=nxt, pi_=pi_: e.copy(out=nxt[:, 0], in_=pi_[:, 128:256].rearrange("p (h t) -> p h t", h=2)), reads=[tpi], writes=[tnxt])
                                    yield
                                    for h in range(2):
                                        fw.op("pe", lambda e, pi_=pi_, h=h, nxt=nxt: e.matmul(pi_[:, 256 + h * 64:256 + (h + 1) * 64], lhsT=nxt[:, 1, h, :], rhs=pm[:, h, :], start=True, stop=True),
                                              reads=[tnxt, tpm], writes=[tpi])
                                        yield
                                    fw.op("dve", lambda e, pi_=pi_: e.tensor_tensor(out=pm[:], in0=pm[:], in1=pi_[:, 256:384].rearrange("p (h t) -> p h t", h=2), op=ALU.add),
                                          reads=[tpi, tpm], writes=[tpm])
                                    yield
                                    cur, tcur = nxt, tnxt
                                res[i] = (a1, ta1, a45, ta45, pm, tpm)

                            def chain_gen(i):
                                cc0, cc1 = i * CHK, (i + 1) * CHK
                                gch = i + off // CHK
                                a1, ta1, a45, ta45, pm, tpm = res.pop(i)
                                pg, tpg = psG.next()
                                for h in range(2):
                                    hp = slice(h * 64, (h + 1) * 64)
                                    fw.op("pe", lambda e, h=h, hp=hp: e.matmul(pg[:, h * 64:(h + 1) * 64], lhsT=aT[hp, cc0:cc1], rhs=Hb[hp, :], start=True, stop=False),
                                          reads=[tops, tH], writes=[tpg])
                                    yield
                                    fw.op("pe", lambda e, h=h, hp=hp: e.matmul(pg[:, h * 64:(h + 1) * 64], lhsT=a1[:, h, :], rhs=vtok[:, gch, hp], start=False, stop=True),
                                          reads=[ta1, tvt], writes=[tpg])
                                    yield
                                gs, tgs = Gs.next()
                                fw.op("act", lambda e: e.copy(out=gs[:], in_=pg[:, 0:128]), reads=[tpg], writes=[tgs])
                                yield
                                for h in range(2):
                                    fw.op("pe", lambda e, h=h: e.matmul(pg[:, 128 + h * 64:128 + (h + 1) * 64], lhsT=pm[:, h, :], rhs=gs[:, h * 64:(h + 1) * 64], start=True, stop=True),
                                          reads=[tpm, tgs], writes=[tpg])
                                    yield
                                ub, tub = Ub.next()
                                fw.op("dve", lambda e: e.tensor_copy(out=ub[:], in_=pg[:, 128:256]), reads=[tpg], writes=[tub])
                                yield
                                ph, tph = psH.next()
                                py, tpy = psY.next()
                                for h in range(2):
                                    hp = slice(h * 64, (h + 1) * 64)
                                    fw.op("pe", lambda e, h=h, hp=hp: e.matmul(ph[hp, 0:64], lhsT=btok[:, i, hp], rhs=ub[:, hp], start=True, stop=False),
                                          reads=[ttok, tub], writes=[tph])
                                    yield
                                    fw.op("pe", lambda e, h=h, hp=hp: e.matmul(ph[hp, 0:64], lhsT=ktok[:, i, hp], rhs=vtok[:, gch, hp], start=False, stop=True),
                                          reads=[ttok, tvt], writes=[tph])
                                    yield
                                for h in range(2):
                                    hp = slice(h * 64, (h + 1) * 64)
                                    fw.op("pe", lambda e, h=h, hp=hp: e.matmul(py[0:64, hp], lhsT=rT[hp, cc0:cc1], rhs=Hb[hp, :], start=True, stop=False),
                                          reads=[tops, tH], writes=[tpy])
                                    yield
                                    fw.op("pe", lambda e, h=h, hp=hp: e.matmul(py[0:64, hp], lhsT=a45[:, h, :], rhs=ub[:, hp], start=False, stop=False),
                                          reads=[ta45, tub], writes=[tpy])
                                    yield
                                    fw.op("pe", lambda e, h=h, hp=hp: e.matmul(py[0:64, hp], lhsT=a45[:, 2 + h, :], rhs=vtok[:, gch, hp], start=False, stop=True),
                                          reads=[ta45, tvt], writes=[tpy])
                                    yield
                                fw.op("dve", lambda e: e.scalar_tensor_tensor(out=Hs[:], in0=Hs[:], scalar=pC[:, i:i + 1], in1=ph[:, 0:64], op0=ALU.mult, op1=ALU.add),
                                      reads=[tph, tH, tops, tpy], writes=[tH])
                                yield
                                fw.op("act", lambda e: e.copy(out=Hb[:], in_=Hs[:]), reads=[tH, tpy, tpg], writes=[tH])
                                yield
                                ys, tys = Ys.next()
                                fw.op("act", lambda e: e.copy(out=ys[:], in_=py[0:64, 0:128]), reads=[tpy], writes=[tys])
                                yield
                                fw.op("pe", lambda e: e.transpose(py[:, 256:320], ys[:], idt[0:64, 0:64]), reads=[tys, tc], writes=[tpy])
                                yield
                                y0 = off + cc0
                                if y0 >= T:
                                    y0 -= T
                                fw.op("dve", lambda e: e.tensor_tensor(out=yacc[:, y0:y0 + CHK], in0=yacc[:, y0:y0 + CHK], in1=py[:, 256:320], op=ALU.add),
                                      reads=[tpy, tya], writes=[tya])
                                yield

                            for _ in par_gen(order[0]):
                                pass
                            for idx, i in enumerate(order):
                                gp = par_gen(order[idx + 1]) if idx + 1 < len(order) else iter(())
                                gc = chain_gen(i)
                                done_p = done_c = False
                                while not (done_p and done_c):
                                    for _ in range(2):
                                        if not done_p:
                                            try:
                                                next(gp)
                                            except StopIteration:
                                                done_p = True
                                    if not done_c:
                                        try:
                                            next(gc)
                                        except StopIteration:
                                            done_c = True
                            fw.barrier()
                with ExitStack() as es4:
                    P4 = Pool_(fw, es4)
                    psr = Rot([P4.ps() for _ in range(3)])
                    xc = P4.sb([128, 512], F32)
                    sq = P4.sb([128, 512], F32)
                    rs = P4.sb([128, 512], F32)
                    txc, tsq, trs = Tok(), Tok(), Tok()
                    stg = Rot([P4.sb([128, 512], F32) for _ in range(2)])
                    tout = Tok()
                    for (lo, hi, ic) in SEGS:
                        n = hi - lo
                        ps, tps = psr.next()
                        fw.op("pe", lambda e, ps=ps, lo=lo, hi=hi, n=n: e.matmul(ps[:, :n], lhsT=bd1[:], rhs=yacc[:, lo:hi], start=True, stop=True), reads=[tya, tc], writes=[tps])
                        fw.op("dve", lambda e, ps=ps, lo=lo, hi=hi, n=n: e.scalar_tensor_tensor(out=xc[:, :n], in0=ps[:, :n], scalar=-1.0 / 64, in1=yacc[:, lo:hi], op0=ALU.mult, op1=ALU.add),
                              reads=[tps, tya], writes=[txc])
                        fw.op("act", lambda e, n=n: e.activation(out=sq[:, :n], in_=xc[:, :n], func=AF.Square), reads=[txc], writes=[tsq])
                        ps2, tps2 = psr.next()
                        fw.op("pe", lambda e, ps2=ps2, n=n: e.matmul(ps2[:, :n], lhsT=bd1[:], rhs=sq[:, :n], start=True, stop=True), reads=[tsq, tc], writes=[tps2])
                        rstd_from_ps(fw, rs, trs, ps2, tps2, n, 1.0 / 64, epsl[:, 0:1], tpp)
                        fw.op("dve", lambda e, n=n: e.tensor_tensor(out=xc[:, :n], in0=xc[:, :n], in1=rs[:, :n], op=ALU.mult), reads=[txc, trs], writes=[txc])
                        fw.op("act", lambda e, n=n: e.activation(out=xc[:, :n], in_=xc[:, :n], func=AF.Identity, scale=pp[:, 8, c:c + 1], bias=pp[:, 9, c:c + 1]),
                              reads=[txc, tpp], writes=[txc])
                        fw.op("pool", lambda e, lo=lo, hi=hi, n=n: e.tensor_tensor(out=xc[:, :n], in0=xc[:, :n], in1=bon[:, lo:hi], op=ALU.add), reads=[txc, tbon], writes=[txc])
                        ps3, tps3 = psr.next()
                        fw.op("pe", lambda e, ps3=ps3, lo=lo, hi=hi, n=n: e.matmul(ps3[:, :n], lhsT=wB[64:128, c * 128:(c + 1) * 128], rhs=lorB[64:128, lo:hi], start=True, stop=True),
                              reads=[tlw, tlor], writes=[tps3])
                        sg, tsg = stg.next()
                        fw.op("dve", lambda e, sg=sg, ps3=ps3, n=n: e.tensor_tensor(out=sg[:, :n], in0=ps3[:, :n], in1=xc[:, :n], op=ALU.mult), reads=[tps3, txc], writes=[tsg])
                        fw.dma("sp", RWT[c * 128:(c + 1) * 128, lo:hi], sg[:, :n], reads=[tsg], writes=[tout])
                    fw.barrier()
        fw.barrier()
NCORES = 8
DEPTH = 4
S5_KEYS = ["s5_a_re", "s5_a_im", "s5_log_step", "s5_b_re", "s5_b_im", "s5_c_re", "s5_c_im", "s5_d", "s5_glu_w", "s5_glu_b", "s5_out_g"]
RW_KEYS = ["rw_mu", "rw_w0", "rw_w2", "rw_a0", "rw_a2", "rw_g2", "rw_k_k", "rw_k_a", "rw_r_k", "rw_ln_g", "rw_ln_b"]
LAYER_KEYS = (["norm1_g", "norm2_g", "mod_w", "mod_b", "w_in", "w_out", "att_qn_g", "att_kn_g", "att_out_g",
               "ffn_up", "ffn_conv_w", "ffn_conv_b", "ffn_down"] + S5_KEYS + RW_KEYS)
SHARED_KEYS = ["c_ctx", "final_g", "ident", "POS", "CST", "RWMASK", "RWBD"]
PERCORE_KEYS = ["x_b", "ctx_b", "c_b"]
SCRATCH = {"ZT": [2048, TE], "VT": [T, 128], "MOD": [128, 48, 2], "S5T": [256, T], "ATT_T": [512, T], "RWT": [256, T],
           "XT1": [DM, T], "XA": [DM, T], "XB": [DM, T]}


def build_fused(depth=DEPTH):
    nc = bass.Bass("TRN2", target_bir_lowering=False)
    decl = {}

    def ext(name, shape, dt, kind):
        if name not in decl:
            decl[name] = nc.dram_tensor(name, list(shape), dt, kind=kind).ap()
        return decl[name]

    def make_io(l):
        xin = "XA" if l % 2 == 0 else "XB"
        xout = "XB" if l % 2 == 0 else "XA"

        def io(name, shape, dt, role):
            if name in LAYER_KEYS:
                full = ext(name, [DEPTH] + list(shape), dt, "ExternalInput")
                return full[l]
            if name in SHARED_KEYS or name in PERCORE_KEYS:
                return ext(name, shape, dt, "ExternalInput")
            if name == "OUT":
                return ext(name, shape, dt, "ExternalOutput")
            if name == "XT":
                name = xin
            elif name == "XT2":
                name = xout
            return ext(name, SCRATCH[name], dt, "Internal")
        return io
    with ExitStack() as es:
        fw = FW(nc, es)
        stage_p0(fw, make_io(0))
        for l in range(depth):
            io = make_io(l)
            for st in (stage_p1, stage_p2, stage_p3, stage_p4, stage_p5a, stage_p5b):
                st(fw, io)
        stage_p6(fw, make_io(depth))
        fw.barrier()
    return nc, fw


def tile_up(up):
    lead = up.shape[:-2]
    v = up.reshape(lead + (8, 128, 44, 128))
    nd = len(lead)
    v = np.transpose(v, tuple(range(nd)) + (nd + 2, nd + 1, nd + 0, nd + 3))
    return np.ascontiguousarray(v).reshape(lead + (44, 128, 1024))


_FUSED = {}


def kernel(**inp):
    inp = {k: np.ascontiguousarray(np.asarray(v)) for k, v in inp.items()}
    if "nc" not in _FUSED:
        _FUSED["nc"], _FUSED["fw"] = build_fused()
    nc = _FUSED["nc"]
    pos, cst = host_consts()
    rwm, rwbd = rw_consts()
    shared = {k: inp[k] for k in LAYER_KEYS if k != "rw_r_k"}
    shared["rw_r_k"] = inp["rw_r_k"].reshape(DEPTH, 256)
    shared["ffn_up"] = tile_up(inp["ffn_up"])
    shared.update(c_ctx=inp["c_ctx"], final_g=inp["final_g"], ident=np.eye(128, dtype=np.float32),
                  POS=pos, CST=cst, RWMASK=rwm, RWBD=rwbd)
    in_maps = [dict(shared, x_b=inp["x"][b], ctx_b=inp["ctx"][b], c_b=inp["c"][b]) for b in range(NCORES)]
    res = run_bass_kernel_spmd(nc, in_maps, core_ids=list(range(NCORES)))
    return np.stack([res.results[b]["OUT"] for b in range(NCORES)], 0).astype(np.float32)
```

```python
import math
import numpy as np
from contextlib import ExitStack
import concourse.bass as bass
import concourse.mybir as mybir
from concourse.bass_utils import run_bass_kernel_spmd

F32 = mybir.dt.float32
F32R = mybir.dt.float32r
BF16 = mybir.dt.bfloat16
I32 = mybir.dt.int32
ALU = mybir.AluOpType
AF = mybir.ActivationFunctionType
AX = mybir.AxisListType

T = 2304
TE = 2560
NCTX = 256
NLAT = 2048
DM = 1024
SEGS = [(0, 256, 1), (256, 768, 0), (768, 1280, 0), (1280, 1792, 0), (1792, 2304, 0)]
RMS_EPS = 1e-6


class Tok:
    __slots__ = ("w", "r")

    def __init__(self):
        self.w = None
        self.r = {}


class FW:
    ENG = ("pe", "dve", "act", "pool", "sp")
    NDMA = 8

    def __init__(self, nc, es):
        self.nc = nc
        self.es = es
        self.eng = {"pe": nc.tensor, "dve": nc.vector, "act": nc.scalar,
                    "pool": nc.gpsimd, "sp": nc.sync}
        self.sem = {}
        self.cnt = {}
        for e in self.ENG:
            self.sem[e] = es.enter_context(nc.semaphore("s_" + e))
            self.cnt[e] = 0
        self.dq = {}
        for q in ("sp", "pool", "act"):
            ring = []
            for i in range(self.NDMA):
                k = "d_%s_%d" % (q, i)
                self.sem[k] = es.enter_context(nc.semaphore(k))
                self.cnt[k] = 0
                ring.append(k)
            self.dq[q] = [ring, 0]
        self.seen = {e: {} for e in self.ENG}
        self.attach = True
        self.ninst = 0
        self.uid = 0

    def name(self, p):
        self.uid += 1
        return "%s_%d" % (p, self.uid)

    def _deps(self, reads, writes):
        deps = {}
        for t in reads:
            if t.w is not None and deps.get(t.w[0], 0) < t.w[1]:
                deps[t.w[0]] = t.w[1]
        for t in writes:
            if t.w is not None and deps.get(t.w[0], 0) < t.w[1]:
                deps[t.w[0]] = t.w[1]
            for k, v in t.r.items():
                if deps.get(k, 0) < v:
                    deps[k] = v
        return deps

    def _wait(self, e, deps):
        seen = self.seen[e]
        for k, v in deps.items():
            if seen.get(k, 0) < v:
                self.eng[e].wait_ge(self.sem[k], v)
                seen[k] = v

    def op(self, e, fn, reads=(), writes=()):
        deps = self._deps(reads, writes)
        seen = self.seen[e]
        need = [(k, v) for k, v in deps.items() if seen.get(k, 0) < v]
        att = None
        if need and self.attach:
            att = need.pop()
        for k, v in need:
            self.eng[e].wait_ge(self.sem[k], v)
            seen[k] = v
        inst = fn(self.eng[e])
        if att is not None:
            inst._wait_ge(self.sem[att[0]], att[1])
            seen[att[0]] = att[1]
        self.cnt[e] += 1
        inst.then_inc(self.sem[e], 1)
        v = self.cnt[e]
        for t in reads:
            t.r[e] = v
        for t in writes:
            t.w = (e, v)
            t.r = {}
        self.ninst += 1
        return inst

    def dma(self, q, out, in_, reads=(), writes=(), **kw):
        ring, idx = self.dq[q]
        k = ring[idx % len(ring)]
        self.dq[q][1] = idx + 1
        deps = self._deps(reads, writes)
        if self.cnt[k] > 0:
            deps[k] = max(deps.get(k, 0), self.cnt[k])
        self._wait(q, deps)
        inst = self.eng[q].dma_start(out=out, in_=in_, **kw)
        self.cnt[k] += 16
        inst.then_inc(self.sem[k], 16)
        v = self.cnt[k]
        for t in reads:
            t.r[k] = v
        for t in writes:
            t.w = (k, v)
            t.r = {}
        self.ninst += 1
        return inst

    def barrier(self, engines=None):
        allv = {k: v for k, v in self.cnt.items() if v > 0}
        for e in (engines or self.ENG):
            self._wait(e, allv)


class Pool_:
    def __init__(self, fw, es):
        self.fw = fw
        self.es = es
        self.nc = fw.nc

    def sb(self, shape, dt, name="t"):
        return self.es.enter_context(self.nc.sbuf_tensor(self.fw.name(name), list(shape), dt))

    def ps(self, shape=(128, 512), dt=F32, name="ps"):
        return self.es.enter_context(self.nc.psum_tensor(self.fw.name(name), list(shape), dt))


class Rot:
    def __init__(self, bufs):
        self.bufs = bufs
        self.toks = [Tok() for _ in bufs]
        self.i = 0

    def next(self):
        j = self.i % len(self.bufs)
        self.i += 1
        return self.bufs[j], self.toks[j]


def load_w_bf16(fw, P, W, rows, cols, tok, name="w", q="pool", chunk=2048, stage=None):
    kt = rows // 128
    wb = P.sb([128, kt, cols], BF16, name)
    chunk = min(chunk, cols)
    st = stage or Rot([P.sb([128, chunk], F32, "wstg") for _ in range(3)])
    for k in range(kt):
        for c0 in range(0, cols, chunk):
            n = min(chunk, cols - c0)
            sg, tsg = st.next()
            fw.dma("sp", sg[:, :n], W[k * 128:(k + 1) * 128, c0:c0 + n], writes=[tsg])
            fw.op("pool", lambda e, sg=sg, k=k, c0=c0, n=n: e.tensor_copy(out=wb[:, k, c0:c0 + n], in_=sg[:, :n]), reads=[tsg], writes=[tok])
    return wb


def stage_p0(fw, io):
    nc = fw.nc
    xb = io("x_b", [NLAT, DM], F32, "in")
    cb = io("ctx_b", [NCTX, DM], F32, "in")
    ident = io("ident", [128, 128], F32, "in")
    XT = io("XT", [DM, T], F32, "out")
    with ExitStack() as es:
        P = Pool_(fw, es)
        idt = P.sb([128, 128], F32)
        tid = Tok()
        fw.dma("sp", idt[:], ident, writes=[tid])
        xt = P.sb([128, 8, T], F32)
        txt = Tok()
        xin = Rot([P.sb([128, DM], F32) for _ in range(3)])
        pss = Rot([P.ps() for _ in range(4)])
        for tt in range(18):
            src = cb[tt * 128:(tt + 1) * 128, :] if tt < 2 else xb[(tt - 2) * 128:(tt - 1) * 128, :]
            xi, txi = xin.next()
            fw.dma("sp", xi[:], src, writes=[txi])
            for half in range(2):
                ps, tps = pss.next()
                for k in range(4):
                    kk = half * 4 + k
                    fw.op("pe", lambda e, ps=ps, k=k, kk=kk, xi=xi: e.transpose(
                        ps[:, k * 128:(k + 1) * 128], xi[:, kk * 128:(kk + 1) * 128], idt[:]),
                        reads=[txi, tid], writes=[tps])
                eng = "dve" if half == 0 else "act"
                outap = xt[:, half * 4:half * 4 + 4, tt * 128:(tt + 1) * 128]
                inap = ps[:].rearrange("p (k t) -> p k t", k=4)
                if eng == "dve":
                    fw.op("dve", lambda e, o=outap, i=inap: e.tensor_copy(out=o, in_=i), reads=[tps], writes=[txt])
                else:
                    fw.op("act", lambda e, o=outap, i=inap: e.copy(out=o, in_=i), reads=[tps], writes=[txt])
        tout = Tok()
        for k in range(8):
            fw.dma("sp", XT[k * 128:(k + 1) * 128, :], xt[:, k, :], reads=[txt], writes=[tout])
        fw.barrier()


def make_AB(fw, P, MODs, tmod, g_ap, sh_base, sc_base):
    g = P.sb([128, 8], F32)
    tg = Tok()
    fw.dma("sp", g[:], g_ap.rearrange("(k p) -> p k", p=128), writes=[tg], allow_slow_non_contiguous=True)
    AB = P.sb([128, 2, 2, 8], F32)
    tab = Tok()
    for ic in range(2):
        fw.op("dve", lambda e, ic=ic: e.tensor_scalar(out=AB[:, ic, 0, :], in0=MODs[:, sc_base:sc_base + 8, ic],
                                                      scalar1=1.0, scalar2=None, op0=ALU.add),
              reads=[tmod], writes=[tab])
        fw.op("dve", lambda e, ic=ic: e.tensor_tensor(out=AB[:, ic, 0, :], in0=AB[:, ic, 0, :], in1=g[:], op=ALU.mult),
              reads=[tg, tab], writes=[tab])
        fw.op("dve", lambda e, ic=ic: e.tensor_copy(out=AB[:, ic, 1, :], in_=MODs[:, sh_base:sh_base + 8, ic]),
              reads=[tmod], writes=[tab])
    return AB, tab


def norm_mod_seg(fw, P, st, xs, txs, n, ic, AB, tab, outs, touts):
    sq, ones, tones, psr, rs, tmpr = st["sq"], st["ones"], st["tones"], st["psr"], st["rs"], st["tmpr"]
    tsq, trs = st["tsq"], st["trs"]
    fw.op("act", lambda e: e.activation(out=sq[:, :, :n], in_=xs[:, :, :n], func=AF.Square), reads=[txs], writes=[tsq])
    ps, tps = psr.next()
    for k in range(8):
        fw.op("pe", lambda e, k=k: e.matmul(ps[:, :n], lhsT=ones[:], rhs=sq[:, k, :n], start=(k == 0), stop=(k == 7)),
              reads=[tsq, tones], writes=[tps])
    fw.op("act", lambda e: e.activation(out=rs[:, :n], in_=ps[:, :n], func=AF.Ln, scale=1.0 / DM, bias=st["eps"][:, 0:1]),
          reads=[tps, st["teps"]], writes=[trs])
    fw.op("act", lambda e: e.activation(out=rs[:, :n], in_=rs[:, :n], func=AF.Exp, scale=-0.5), reads=[trs], writes=[trs])
    for k in range(8):
        tmp, ttmp = tmpr.next()
        fw.op("dve", lambda e, k=k, tmp=tmp: e.tensor_tensor(out=tmp[:, :n], in0=xs[:, k, :n], in1=rs[:, :n], op=ALU.mult),
              reads=[txs, trs], writes=[ttmp])
        for o in outs(k):
            fw.op("act", lambda e, k=k, tmp=tmp, o=o: e.activation(out=o, in_=tmp[:, :n], func=AF.Identity,
                                                                   scale=AB[:, ic, 0, k:k + 1], bias=AB[:, ic, 1, k:k + 1]),
                  reads=[ttmp, tab], writes=touts)


def norm_state(fw, P):
    st = {}
    st["sq"] = P.sb([128, 8, 512], BF16)
    st["tsq"] = Tok()
    st["ones"] = P.sb([128, 128], BF16)
    st["tones"] = Tok()
    fw.op("pool", lambda e: e.memset(st["ones"][:], 1.0), writes=[st["tones"]])
    st["eps"] = P.sb([128, 1], F32)
    st["teps"] = Tok()
    fw.op("pool", lambda e: e.memset(st["eps"][:], RMS_EPS), writes=[st["teps"]])
    st["psr"] = Rot([P.ps() for _ in range(2)])
    st["rs"] = P.sb([128, 512], F32)
    st["trs"] = Tok()
    st["tmpr"] = Rot([P.sb([128, 512], F32) for _ in range(2)])
    return st


def stage_p1(fw, io):
    XT = io("XT", [DM, T], F32, "in")
    c_b = io("c_b", [DM], F32, "in")
    c_ctx = io("c_ctx", [DM], F32, "in")
    mod_w = io("mod_w", [DM, 6 * DM], F32, "in")
    mod_b = io("mod_b", [6 * DM], F32, "in")
    n1g = io("norm1_g", [DM], F32, "in")
    w_in = io("w_in", [DM, 1984], F32, "in")
    MOD = io("MOD", [128, 48, 2], F32, "out")
    ZT = io("ZT", [2048, TE], F32, "out")
    VT = io("VT", [T, 128], F32, "out")
    XTv = XT.rearrange("(k p) t -> p k t", p=128)
    with ExitStack() as es:
        P = Pool_(fw, es)
        wstage = Rot([P.sb([128, 2048], F32, "wstg") for _ in range(3)])
        tmw = Tok()
        cc = P.sb([128, 8, 2], F32)
        tcc = Tok()
        fw.dma("sp", cc[:, :, 0], c_b.rearrange("(k p) -> p k", p=128), writes=[tcc], allow_slow_non_contiguous=True)
        fw.dma("sp", cc[:, :, 1], c_ctx.rearrange("(k p) -> p k", p=128), writes=[tcc], allow_slow_non_contiguous=True)
        scb = P.sb([128, 8, 2], BF16)
        tscb = Tok()
        fw.op("act", lambda e: e.activation(out=scb[:], in_=cc[:], func=AF.Silu), reads=[tcc], writes=[tscb])
        mb = P.sb([128, 48], F32)
        tmb = Tok()
        fw.dma("sp", mb[:], mod_b.rearrange("(j p) -> p j", p=128), writes=[tmb], allow_slow_non_contiguous=True)
        MODs = P.sb([128, 48, 2], F32)
        tmod = Tok()
        with ExitStack() as es2:
            P2 = Pool_(fw, es2)
            mwb = load_w_bf16(fw, P2, mod_w, DM, 6 * DM, tmw, "modw", stage=wstage)
            psm = P2.ps([128, 512])
            tpsm = Tok()
            for j in range(48):
                for k in range(8):
                    fw.op("pe", lambda e, j=j, k=k: e.matmul(psm[:, 2 * j:2 * j + 2], lhsT=mwb[:, k, j * 128:(j + 1) * 128],
                                                             rhs=scb[:, k, :], start=(k == 0), stop=(k == 7)),
                          reads=[tmw, tscb], writes=[tpsm])
            for ic in range(2):
                fw.op("dve", lambda e, ic=ic: e.tensor_tensor(
                    out=MODs[:, :, ic], in0=psm[:, 0:96].rearrange("p (j c) -> p j c", c=2)[:, :, ic], in1=mb[:], op=ALU.add),
                    reads=[tpsm, tmb], writes=[tmod])
            fw.barrier()
        tmo = Tok()
        fw.dma("sp", MOD, MODs[:], reads=[tmod], writes=[tmo])
        AB, tab = make_AB(fw, P, MODs, tmod, n1g, 0, 8)
        tw = Tok()
        wb = load_w_bf16(fw, P, w_in, DM, 1984, tw, "win", stage=wstage)
        hT = P.sb([128, 8, TE], BF16)
        thT = Tok()
        st = norm_state(fw, P)
        xr = Rot([P.sb([128, 8, 512], F32) for _ in range(2)])
        for (lo, hi, ic) in SEGS:
            n = hi - lo
            xs, txs = xr.next()
            fw.dma("sp", xs[:, :, :n], XTv[:, :, lo:hi], writes=[txs])

            def outs(k, lo=lo, hi=hi, ic=ic):
                o = [hT[:, k, lo:hi]]
                if ic:
                    o.append(hT[:, k, T + lo:T + hi])
                return o
            norm_mod_seg(fw, P, st, xs, txs, n, ic, AB, tab, outs, [thT])
        psr = Rot([P.ps() for _ in range(4)])
        stg = Rot([P.sb([128, 512], F32) for _ in range(4)])
        tz = Tok()
        cnt = 0
        for nt in range(16):
            if nt == 7:
                continue
            M = 64 if nt == 15 else 128
            for cc_ in range(5):
                c0 = cc_ * 512
                ps, tps = psr.next()
                for k in range(8):
                    fw.op("pe", lambda e, ps=ps, k=k, nt=nt, M=M, c0=c0: e.matmul(
                        ps[0:M, :], lhsT=wb[:, k, nt * 128:nt * 128 + M], rhs=hT[:, k, c0:c0 + 512],
                        start=(k == 0), stop=(k == 7)), reads=[tw, thT], writes=[tps])
                sg, tsg = stg.next()
                if cnt % 2 == 0:
                    fw.op("dve", lambda e, sg=sg, ps=ps, M=M: e.tensor_copy(out=sg[0:M, :], in_=ps[0:M, :]), reads=[tps], writes=[tsg])
                else:
                    fw.op("act", lambda e, sg=sg, ps=ps, M=M: e.copy(out=sg[0:M, :], in_=ps[0:M, :]), reads=[tps], writes=[tsg])
                cnt += 1
                fw.dma("sp", ZT[nt * 128:nt * 128 + M, c0:c0 + 512], sg[0:M, :], reads=[tsg], writes=[tz])
        for tt in range(18):
            ps, tps = psr.next()
            for k in range(8):
                fw.op("pe", lambda e, ps=ps, k=k, tt=tt: e.matmul(
                    ps[:, 0:128], lhsT=hT[:, k, tt * 128:(tt + 1) * 128], rhs=wb[:, k, 896:1024],
                    start=(k == 0), stop=(k == 7)), reads=[tw, thT], writes=[tps])
            sg, tsg = stg.next()
            fw.op("dve", lambda e, sg=sg, ps=ps: e.tensor_copy(out=sg[:, 0:128], in_=ps[:, 0:128]), reads=[tps], writes=[tsg])
            fw.dma("sp", VT[tt * 128:(tt + 1) * 128, :], sg[:, 0:128], reads=[tsg], writes=[tz])
        fw.barrier()


def build_program(stage_fns):
    nc = bass.Bass("TRN2", target_bir_lowering=False)
    decl = {}

    def io(name, shape, dt, role):
        if name in decl:
            return decl[name][0]
        kind = "ExternalInput" if role == "in" else "ExternalOutput"
        ap = nc.dram_tensor(name, list(shape), dt, kind=kind).ap()
        decl[name] = (ap, role, shape)
        return ap
    with ExitStack() as es:
        fw = FW(nc, es)
        for fn in stage_fns:
            fn(fw, io)
        fw.barrier()
    return nc, decl, fw


_PROG_CACHE = {}


def run_stage(key, stage_fns, in_maps, ncores):
    if key not in _PROG_CACHE:
        _PROG_CACHE[key] = build_program(stage_fns)
    nc, decl, fw = _PROG_CACHE[key]
    res = run_bass_kernel_spmd(nc, in_maps, core_ids=list(range(ncores)))
    return res.results


def rstd_from_ps(fw, rs, trs, ps, tps, n, scale, epsap, teps, rows=128):
    fw.op("act", lambda e: e.activation(out=rs[0:rows, :n], in_=ps[0:rows, :n], func=AF.Ln, scale=scale, bias=epsap),
          reads=[tps, teps], writes=[trs])
    fw.op("act", lambda e: e.activation(out=rs[0:rows, :n], in_=rs[0:rows, :n], func=AF.Exp, scale=-0.5), reads=[trs], writes=[trs])


def stage_p3(fw, io):
    ZT = io("ZT", [2048, TE], F32, "in")
    VT = io("VT", [T, 128], F32, "in")
    qn_g = io("att_qn_g", [64], F32, "in")
    kn_g = io("att_kn_g", [64], F32, "in")
    og = io("att_out_g", [512], F32, "in")
    POS = io("POS", [128, NLAT], F32, "in")
    CST = io("CST", [128, 260], F32, "in")
    ATT = io("ATT_T", [512, T], F32, "out")
    with ExitStack() as es:
        P = Pool_(fw, es)
        cst = P.sb([128, 260], F32)
        tc = Tok()
        fw.dma("sp", cst[:], CST, writes=[tc])
        permb = P.sb([128, 128], BF16)
        bdb = P.sb([128, 128], BF16)
        fw.op("dve", lambda e: e.tensor_copy(out=permb[:], in_=cst[:, 1:129]), reads=[tc], writes=[tc])
        fw.op("dve", lambda e: e.tensor_copy(out=bdb[:], in_=cst[:, 129:257]), reads=[tc], writes=[tc])
        eps = P.sb([128, 2], F32)
        teps = Tok()
        fw.op("pool", lambda e: e.memset(eps[:, 0:1], RMS_EPS), writes=[teps])
        fw.op("pool", lambda e: e.memset(eps[:, 1:2], 0.0), writes=[teps])
        gq = P.sb([128, 2], F32)
        tg = Tok()
        for h in range(2):
            fw.dma("sp", gq[h * 64:(h + 1) * 64, 0:1], qn_g.rearrange("(p o) -> p o", o=1), writes=[tg])
            fw.dma("sp", gq[h * 64:(h + 1) * 64, 1:2], kn_g.rearrange("(p o) -> p o", o=1), writes=[tg])
        cos = P.sb([128, NLAT], F32)
        sin = P.sb([128, NLAT], F32)
        ttab = Tok()
        with ExitStack() as es2:
            P2 = Pool_(fw, es2)
            ang = P2.sb([128, NLAT], F32)
            tmpf = P2.sb([128, NLAT], F32)
            tmpi = P2.sb([128, NLAT], I32)
            ta = Tok()
            fw.dma("sp", ang[:], POS, writes=[ta])
            fw.op("dve", lambda e: e.tensor_scalar(out=ang[:], in0=ang[:], scalar1=cst[:, 0:1], scalar2=None, op0=ALU.mult),
                  reads=[ta, tc], writes=[ta])
            for (tab, off) in ((sin, 0.0), (cos, math.pi / 2)):
                fw.op("dve", lambda e, off=off: e.tensor_scalar(out=tmpi[:], in0=ang[:], scalar1=off, scalar2=1.0 / (2 * math.pi),
                                                                op0=ALU.add, op1=ALU.mult), reads=[ta], writes=[ta])
                fw.op("dve", lambda e: e.tensor_copy(out=tmpf[:], in_=tmpi[:]), reads=[ta], writes=[ta])
                fw.op("dve", lambda e: e.scalar_tensor_tensor(out=tmpf[:], in0=tmpf[:], scalar=-2 * math.pi, in1=ang[:],
                                                              op0=ALU.mult, op1=ALU.add), reads=[ta], writes=[ta])
                fw.op("dve", lambda e, off=off: e.tensor_scalar(out=tmpf[:], in0=tmpf[:], scalar1=off, scalar2=math.pi,
                                                                op0=ALU.add, op1=ALU.min), reads=[ta], writes=[ta])
                fw.op("dve", lambda e: e.tensor_scalar(out=tmpf[:], in0=tmpf[:], scalar1=-math.pi, scalar2=None, op0=ALU.max),
                      reads=[ta], writes=[ta])
                fw.op("act", lambda e, tab=tab: e.activation(out=tab[:], in_=tmpf[:], func=AF.Sin), reads=[ta], writes=[ttab])
            fw.barrier()
        qb = P.sb([128, 4, T], BF16)
        kd = P.sb([128, 2, T], BF16)
        tq = Tok()
        with ExitStack() as es2:
            P2 = Pool_(fw, es2)
            raw = Rot([P2.sb([128, 512], F32) for _ in range(2)])
            sqr = Rot([P2.sb([128, 512], BF16) for _ in range(2)])
            psr = Rot([P2.ps() for _ in range(2)])
            psr2 = Rot([P2.ps() for _ in range(2)])
            rsr = Rot([P2.sb([128, 512], F32) for _ in range(2)])
            nbr = Rot([P2.sb([128, 512], BF16) for _ in range(2)])
            t1r = Rot([P2.sb([128, 512], F32) for _ in range(2)])
            t2r = Rot([P2.sb([128, 512], F32) for _ in range(2)])
            items = [("q", j) for j in range(4)] + [("k", g) for g in range(2)]
            for (kind, j) in items:
                for (lo, hi, ic) in SEGS:
                    n = hi - lo
                    rw, trw = raw.next()
                    if kind == "q":
                        fw.dma("sp", rw[:, :n], ZT[256 + j * 128:256 + (j + 1) * 128, lo:hi], writes=[trw])
                        gcol = 0
                        dst = qb[:, j, lo:hi]
                    else:
                        for h in range(2):
                            fw.dma("sp", rw[h * 64:(h + 1) * 64, :n], ZT[768 + j * 64:768 + (j + 1) * 64, lo:hi], writes=[trw])
                        gcol = 1
                        dst = kd[:, j, lo:hi]
                    sq, tsq = sqr.next()
                    fw.op("act", lambda e, sq=sq, rw=rw, n=n: e.activation(out=sq[:, :n], in_=rw[:, :n], func=AF.Square), reads=[trw], writes=[tsq])
                    ps, tps = psr.next()
                    fw.op("pe", lambda e, ps=ps, sq=sq, n=n: e.matmul(ps[:, :n], lhsT=bdb[:], rhs=sq[:, :n], start=True, stop=True),
                          reads=[tsq, tc], writes=[tps])
                    rs, trs = rsr.next()
                    rstd_from_ps(fw, rs, trs, ps, tps, n, 1.0, eps[:, 0:1], teps)
                    t1, tt1 = t1r.next()
                    fw.op("dve", lambda e, t1=t1, rw=rw, rs=rs, n=n, gcol=gcol: e.scalar_tensor_tensor(
                        out=t1[:, :n], in0=rw[:, :n], scalar=gq[:, gcol:gcol + 1], in1=rs[:, :n], op0=ALU.mult, op1=ALU.mult),
                        reads=[trw, trs, tg], writes=[tt1])
                    if ic:
                        fw.op("act", lambda e, dst=dst, t1=t1, n=n: e.copy(out=dst, in_=t1[:, :n]), reads=[tt1], writes=[tq])
                        continue
                    nb, tnb = nbr.next()
                    fw.op("act", lambda e, nb=nb, t1=t1, n=n: e.copy(out=nb[:, :n], in_=t1[:, :n]), reads=[tt1], writes=[tnb])
                    ps2, tps2 = psr2.next()
                    fw.op("pe", lambda e, ps2=ps2, nb=nb, n=n: e.matmul(ps2[:, :n], lhsT=permb[:], rhs=nb[:, :n], start=True, stop=True),
                          reads=[tnb, tc], writes=[tps2])
                    p0 = lo - NCTX
                    t2, tt2 = t2r.next()
                    fw.op("dve", lambda e, t2=t2, ps2=ps2, n=n, p0=p0: e.tensor_tensor(out=t2[:, :n], in0=ps2[:, :n], in1=sin[:, p0:p0 + n], op=ALU.mult),
                          reads=[tps2, ttab], writes=[tt2])
                    fw.op("pool", lambda e, t1=t1, nb=nb, n=n, p0=p0: e.tensor_tensor(out=t1[:, :n], in0=nb[:, :n], in1=cos[:, p0:p0 + n], op=ALU.mult),
                          reads=[tnb, ttab, tt1], writes=[tt1])
                    fw.op("pool", lambda e, dst=dst, t1=t1, t2=t2, n=n: e.tensor_tensor(out=dst, in0=t1[:, :n], in1=t2[:, :n], op=ALU.add),
                          reads=[tt1, tt2], writes=[tq])
            fw.barrier()
        va = P.sb([128, 18, 2, 128], BF16)
        tva = Tok()
        fw.op("pool", lambda e: e.memset(va[:], 1.0), writes=[tva])
        for g in range(2):
            fw.dma("pool", va[:, :, g, 0:64], VT.rearrange("(t p) c -> p t c", p=128)[:, :, g * 64:(g + 1) * 64], writes=[tva])
        att = P.sb([128, 4, T], F32)
        tatt = Tok()
        pss = Rot([P.ps() for _ in range(3)])
        pso = Rot([P.ps() for _ in range(2)])
        ptr = Rot([P.sb([128, 512], BF16) for _ in range(3)])
        rcr = Rot([P.sb([64, 512], F32) for _ in range(2)])
        jobs = [(0, 256, 0, 2)] + [(256 + 512 * i, 768 + 512 * i, 0, 18) for i in range(4)]
        for h in range(8):
            g = h // 4
            jt, r0 = h // 2, (h % 2) * 64
            for (qlo, qhi, k0, k1) in jobs:
                n = qhi - qlo
                po, tpo = pso.next()
                for kt in range(k0, k1):
                    ps, tps = pss.next()
                    fw.op("pe", lambda e, ps=ps, kt=kt, g=g, jt=jt, r0=r0, qlo=qlo, qhi=qhi, n=n: e.matmul(
                        ps[:, :n], lhsT=kd[r0:r0 + 64, g, kt * 128:(kt + 1) * 128], rhs=qb[r0:r0 + 64, jt, qlo:qhi],
                        start=True, stop=True), reads=[tq], writes=[tps])
                    pt, tpt = ptr.next()
                    fw.op("act", lambda e, pt=pt, ps=ps, n=n: e.activation(out=pt[:, :n], in_=ps[:, :n], func=AF.Exp, scale=0.125),
                          reads=[tps], writes=[tpt])
                    fw.op("pe", lambda e, po=po, pt=pt, kt=kt, g=g, n=n, k0=k0, k1=k1: e.matmul(
                        po[:, :n], lhsT=va[:, kt, g, :], rhs=pt[:, :n], start=(kt == k0), stop=(kt == k1 - 1)),
                        reads=[tpt, tva], writes=[tpo])
                rc, trc = rcr.next()
                fw.op("dve", lambda e, rc=rc, po=po, n=n: e.reciprocal(out=rc[:, :n], in_=po[64:128, :n]), reads=[tpo], writes=[trc])
                fw.op("dve", lambda e, rc=rc, po=po, n=n, jt=jt, r0=r0, qlo=qlo, qhi=qhi: e.tensor_tensor(
                    out=att[r0:r0 + 64, jt, qlo:qhi], in0=po[0:64, :n], in1=rc[:, :n], op=ALU.mult),
                    reads=[tpo, trc], writes=[tatt])
        ogs = P.sb([128, 4], F32)
        tog = Tok()
        fw.dma("sp", ogs[:], og.rearrange("(k p) -> p k", p=128), writes=[tog], allow_slow_non_contiguous=True)
        ones = P.sb([128, 128], BF16)
        fw.op("pool", lambda e: e.memset(ones[:], 1.0), writes=[tog])
        sq4 = P.sb([128, 4, 512], BF16)
        tsq4 = Tok()
        rs = P.sb([128, 512], F32)
        trs = Tok()
        stg = Rot([P.sb([128, 512], F32) for _ in range(3)])
        tout = Tok()
        for (lo, hi, ic) in SEGS:
            n = hi - lo
            fw.op("act", lambda e, lo=lo, hi=hi, n=n: e.activation(out=sq4[:, :, :n], in_=att[:, :, lo:hi], func=AF.Square), reads=[tatt], writes=[tsq4])
            ps, tps = pss.next()
            for k in range(4):
                fw.op("pe", lambda e, ps=ps, k=k, n=n: e.matmul(ps[:, :n], lhsT=ones[:], rhs=sq4[:, k, :n], start=(k == 0), stop=(k == 3)),
                      reads=[tsq4, tog], writes=[tps])
            rstd_from_ps(fw, rs, trs, ps, tps, n, 1.0 / 512, eps[:, 0:1], teps)
            for k in range(4):
                sg, tsg = stg.next()
                fw.op("dve", lambda e, sg=sg, k=k, lo=lo, hi=hi, n=n: e.scalar_tensor_tensor(
                    out=sg[:, :n], in0=att[:, k, lo:hi], scalar=ogs[:, k:k + 1], in1=rs[:, :n], op0=ALU.mult, op1=ALU.mult),
                    reads=[tatt, trs, tog], writes=[tsg])
                fw.dma("sp", ATT[k * 128:(k + 1) * 128, lo:hi], sg[:, :n], reads=[tsg], writes=[tout])
        fw.barrier()


def host_consts():
    pos = np.zeros((128, NLAT), np.float32)
    inv = np.zeros((128,), np.float32)
    tok = np.arange(NLAT)
    for p in range(128):
        d = p % 64
        pos[p] = (tok // 64) if d < 32 else (tok % 64)
        inv[p] = 10000.0 ** (-(d % 16) / 16.0)
    cst = np.zeros((128, 260), np.float32)
    cst[:, 0] = inv
    perm = np.zeros((128, 128), np.float32)
    for m in range(128):
        d = m % 32
        if d < 16:
            perm[m + 16, m] = -1.0
        else:
            perm[m - 16, m] = 1.0
    cst[:, 1:129] = perm
    bd = np.zeros((128, 128), np.float32)
    bd[:64, :64] = 1.0 / 64
    bd[64:, 64:] = 1.0 / 64
    cst[:, 129:257] = bd
    return pos, cst


def stage_p5a(fw, io):
    XT = io("XT", [DM, T], F32, "in")
    S5T = io("S5T", [256, T], F32, "in")
    ATT = io("ATT_T", [512, T], F32, "in")
    RWT = io("RWT", [256, T], F32, "in")
    MOD = io("MOD", [128, 48, 2], F32, "in")
    w_out = io("w_out", [DM, DM], F32, "in")
    XT1 = io("XT1", [DM, T], F32, "out")
    XTv = XT.rearrange("(k p) t -> p k t", p=128)
    with ExitStack() as es:
        P = Pool_(fw, es)
        MODs = P.sb([128, 48, 2], F32)
        tmod = Tok()
        fw.dma("sp", MODs[:], MOD, writes=[tmod])
        tw = Tok()
        wb = load_w_bf16(fw, P, w_out, DM, DM, tw, "wout")
        cat = P.sb([128, 8, T], BF16)
        tcat = Tok()
        cstg = Rot([P.sb([128, T], F32, "cstg") for _ in range(2)])
        for k in range(8):
            src = S5T[k * 128:(k + 1) * 128, :] if k < 2 else (ATT[(k - 2) * 128:(k - 1) * 128, :] if k < 6 else RWT[(k - 6) * 128:(k - 5) * 128, :])
            sg, tsg = cstg.next()
            fw.dma("sp", sg[:], src, writes=[tsg])
            fw.op("pool", lambda e, sg=sg, k=k: e.tensor_copy(out=cat[:, k, :], in_=sg[:]), reads=[tsg], writes=[tcat])
        xr = Rot([P.sb([128, 8, 512], F32) for _ in range(2)])
        x1r = Rot([P.sb([128, 8, 512], F32) for _ in range(2)])
        psr = Rot([P.ps() for _ in range(4)])
        to1, to2 = Tok(), Tok()
        for (lo, hi, ic) in SEGS:
            n = hi - lo
            xs, txs = xr.next()
            fw.dma("sp", xs[:, :, :n], XTv[:, :, lo:hi], writes=[txs])
            x1, tx1 = x1r.next()
            for d in range(8):
                ps, tps = psr.next()
                for k in range(8):
                    fw.op("pe", lambda e, ps=ps, k=k, d=d, lo=lo, hi=hi, n=n: e.matmul(
                        ps[:, :n], lhsT=wb[:, k, d * 128:(d + 1) * 128], rhs=cat[:, k, lo:hi], start=(k == 0), stop=(k == 7)),
                        reads=[tw, tcat], writes=[tps])
                fw.op("dve", lambda e, ps=ps, d=d, n=n, ic=ic, x1=x1, xs=xs: e.scalar_tensor_tensor(
                    out=x1[:, d, :n], in0=ps[:, :n], scalar=MODs[:, 16 + d, ic:ic + 1], in1=xs[:, d, :n], op0=ALU.mult, op1=ALU.add),
                    reads=[tps, txs, tmod], writes=[tx1])
            for k in range(8):
                fw.dma("sp", XT1[k * 128:(k + 1) * 128, lo:hi], x1[:, k, :n], reads=[tx1], writes=[to1])
        fw.barrier()


def stage_p5b(fw, io):
    XT1 = io("XT1", [DM, T], F32, "in")
    n2g = io("norm2_g", [DM], F32, "in")
    MOD = io("MOD", [128, 48, 2], F32, "in")
    up = io("ffn_up", [44, 128, DM], F32, "in")
    cw = io("ffn_conv_w", [3, 5632], F32, "in")
    cb = io("ffn_conv_b", [5632], F32, "in")
    down = io("ffn_down", [2816, DM], F32, "in")
    XT2 = io("XT2", [DM, T], F32, "out")
    X1v = XT1.rearrange("(k p) t -> p k t", p=128)
    X2v = XT2.rearrange("(k p) t -> p k t", p=128)
    with ExitStack() as es:
        P = Pool_(fw, es)
        MODs = P.sb([128, 48, 2], F32)
        tmod = Tok()
        fw.dma("sp", MODs[:], MOD, writes=[tmod])
        cws = P.sb([128, 44, 3], F32)
        cbs = P.sb([128, 44], F32)
        tcw = Tok()
        for w in range(3):
            fw.dma("sp", cws[:, :, w], cw[w].rearrange("(j p) -> p j", p=128), writes=[tcw], allow_slow_non_contiguous=True)
        fw.dma("sp", cbs[:], cb.rearrange("(j p) -> p j", p=128), writes=[tcw], allow_slow_non_contiguous=True)
        h2 = P.sb([128, 8, T], BF16)
        th2 = Tok()
        AB, tab = make_AB(fw, P, MODs, tmod, n2g, 24, 32)
        with ExitStack() as es2:
            P2 = Pool_(fw, es2)
            st = norm_state(fw, P2)
            xr0 = Rot([P2.sb([128, 8, 512], F32) for _ in range(2)])
            for (lo, hi, ic) in SEGS:
                n = hi - lo
                xs, txs = xr0.next()
                fw.dma("sp", xs[:, :, :n], X1v[:, :, lo:hi], writes=[txs])
                norm_mod_seg(fw, P2, st, xs, txs, n, ic, AB, tab, lambda k, lo=lo, hi=hi: [h2[:, k, lo:hi]], [th2])
            fw.barrier()
        hid = P.sb([128, 11, T], BF16)
        thid = Tok()
        dwb = P.sb([128, 11, DM], BF16)
        tdw = Tok()
        urot = [Rot([P.sb([128, T], F32) for _ in range(2)]) for _ in range(2)]
        y = [P.sb([128, T], F32) for _ in range(2)]
        ty = [Tok(), Tok()]
        psr = Rot([P.ps() for _ in range(4)])
        tx2 = Tok()
        RANGES = [(0, NCTX), (NCTX, T)]
        GROUPS = [(0, 2), (2, 2), (4, 2), (6, 2), (8, 2), (10, 1)]
        for half in range(2):
            with ExitStack() as esu:
                Pu = Pool_(fw, esu)
                ustg = Rot([Pu.sb([128, 8, 256], F32, "ustg") for _ in range(2)])
                for jj in range(11):
                    r0 = (half * 11 + jj) * 128
                    sg, tsg = ustg.next()
                    sgv = sg[:].rearrange("p k n -> p (k n)")[:, 0:DM]
                    fw.dma("sp", sgv, down[r0:r0 + 128, :], writes=[tsg])
                    fw.op("pool", lambda e, sgv=sgv, jj=jj: e.tensor_copy(out=dwb[:, jj, :], in_=sgv), reads=[tsg], writes=[tdw])
                ubr = Rot([Pu.sb([128, 8, 2, 256], BF16, "ub") for _ in range(2)])

                def issue_load(g, half=half):
                    jj0, ng = GROUPS[g]
                    ub, tub = ubr.next()
                    for wh in range(2):
                        sg, tsg = ustg.next()
                        sgv = sg[:].rearrange("p k (j c) -> p (k j c)", j=2).rearrange("p (j k c) -> p j k c", j=2, k=8)
                        for jl in range(ng):
                            jt = wh * 22 + half * 11 + jj0 + jl
                            fw.dma("sp", sgv[:, jl].rearrange("p k c -> p (k c)"), up[jt], writes=[tsg])
                            fw.op("pool", lambda e, sgv=sgv, ub=ub, wh=wh, jl=jl: e.tensor_copy(out=ub[:, :, wh, jl * 128:(jl + 1) * 128], in_=sgv[:, jl]),
                                  reads=[tsg], writes=[tub])
                    return ub, tub
                loaded = issue_load(0)
                for g, (jj0, ng) in enumerate(GROUPS):
                    ub, tub = loaded
                    if g + 1 < len(GROUPS):
                        loaded = issue_load(g + 1)
                    for jl in range(ng):
                        jj = jj0 + jl
                        j = half * 11 + jj
                        for wh in range(2):
                            jc = wh * 22 + j
                            ucur, tucur = urot[wh].next()
                            for si, (lo, hi, ic) in enumerate(SEGS):
                                n = hi - lo
                                ps, tps = psr.next()
                                for k in range(8):
                                    fw.op("pe", lambda e, ps=ps, k=k, wh=wh, ub=ub, jl=jl, lo=lo, hi=hi, n=n: e.matmul(
                                        ps[:, :n], lhsT=ub[:, k, wh, jl * 128:(jl + 1) * 128], rhs=h2[:, k, lo:hi], start=(k == 0), stop=(k == 7)),
                                        reads=[tub, th2], writes=[tps])
                                fw.op("act", lambda e, ps=ps, ucur=ucur, lo=lo, hi=hi, n=n: e.copy(out=ucur[:, lo:hi], in_=ps[:, :n]),
                                      reads=[tps], writes=[tucur])
                            fw.op("act", lambda e, wh=wh, jc=jc, ucur=ucur: e.activation(out=y[wh][:], in_=ucur[:], func=AF.Identity,
                                                                                        scale=cws[:, jc, 1:2], bias=cbs[:, jc:jc + 1]),
                                  reads=[tucur, tcw], writes=[ty[wh]])
                            for (lo, hi) in RANGES:
                                fw.op("dve", lambda e, wh=wh, jc=jc, lo=lo, hi=hi, ucur=ucur: e.scalar_tensor_tensor(
                                    out=y[wh][:, lo + 1:hi], in0=ucur[:, lo:hi - 1], scalar=cws[:, jc, 0:1], in1=y[wh][:, lo + 1:hi],
                                    op0=ALU.mult, op1=ALU.add), reads=[tucur, tcw, ty[wh]], writes=[ty[wh]])
                                fw.op("dve", lambda e, wh=wh, jc=jc, lo=lo, hi=hi, ucur=ucur: e.scalar_tensor_tensor(
                                    out=y[wh][:, lo:hi - 1], in0=ucur[:, lo + 1:hi], scalar=cws[:, jc, 2:3], in1=y[wh][:, lo:hi - 1],
                                    op0=ALU.mult, op1=ALU.add), reads=[tucur, tcw, ty[wh]], writes=[ty[wh]])
                        fw.op("act", lambda e: e.activation(out=y[0][:], in_=y[0][:], func=AF.Silu), reads=[ty[0]], writes=[ty[0]])
                        fw.op("dve", lambda e, jj=jj: e.tensor_tensor(out=hid[:, jj, :], in0=y[0][:], in1=y[1][:], op=ALU.mult),
                              reads=[ty[0], ty[1]], writes=[thid])
                fw.barrier()
            with ExitStack() as esd:
                Pd = Pool_(fw, esd)
                xr = Rot([Pd.sb([128, 8, 512], F32) for _ in range(2)])
                for (lo, hi, ic) in SEGS:
                    n = hi - lo
                    xs, txs = xr.next()
                    src = X1v if half == 0 else X2v
                    fw.dma("sp", xs[:, :, :n], src[:, :, lo:hi], reads=([tx2] if half else []), writes=[txs])
                    for d in range(8):
                        ps, tps = psr.next()
                        for jj in range(11):
                            fw.op("pe", lambda e, ps=ps, jj=jj, d=d, lo=lo, hi=hi, n=n: e.matmul(
                                ps[:, :n], lhsT=dwb[:, jj, d * 128:(d + 1) * 128], rhs=hid[:, jj, lo:hi], start=(jj == 0), stop=(jj == 10)),
                                reads=[tdw, thid], writes=[tps])
                        fw.op("dve", lambda e, ps=ps, d=d, n=n, ic=ic, xs=xs: e.scalar_tensor_tensor(
                            out=xs[:, d, :n], in0=ps[:, :n], scalar=MODs[:, 40 + d, ic:ic + 1], in1=xs[:, d, :n], op0=ALU.mult, op1=ALU.add),
                            reads=[tps, tmod, txs], writes=[txs])
                    for k in range(8):
                        fw.dma("sp", XT2[k * 128:(k + 1) * 128, lo:hi], xs[:, k, :n], reads=[txs], writes=[tx2])
                fw.barrier()


def stage_p6(fw, io):
    XT = io("XT", [DM, T], F32, "in")
    fg = io("final_g", [DM], F32, "in")
    ident = io("ident", [128, 128], F32, "in")
    OUT = io("OUT", [NLAT, DM], F32, "out")
    XTv = XT.rearrange("(k p) t -> p k t", p=128)
    with ExitStack() as es:
        P = Pool_(fw, es)
        idt = P.sb([128, 128], F32)
        tid = Tok()
        fw.dma("sp", idt[:], ident, writes=[tid])
        g = P.sb([128, 8], F32)
        fw.dma("sp", g[:], fg.rearrange("(k p) -> p k", p=128), writes=[tid], allow_slow_non_contiguous=True)
        st = norm_state(fw, P)
        xr = Rot([P.sb([128, 8, 512], F32) for _ in range(2)])
        yr = Rot([P.sb([128, 8, 512], F32) for _ in range(2)])
        psr = Rot([P.ps() for _ in range(4)])
        orr = Rot([P.sb([128, DM], F32) for _ in range(3)])
        tout = Tok()
        for (lo, hi, ic) in SEGS[1:]:
            n = hi - lo
            xs, txs = xr.next()
            fw.dma("sp", xs[:, :, :n], XTv[:, :, lo:hi], writes=[txs])
            sq, ones = st["sq"], st["ones"]
            fw.op("act", lambda e, xs=xs: e.activation(out=sq[:], in_=xs[:], func=AF.Square), reads=[txs], writes=[st["tsq"]])
            ps, tps = st["psr"].next()
            for k in range(8):
                fw.op("pe", lambda e, ps=ps, k=k: e.matmul(ps[:], lhsT=ones[:], rhs=sq[:, k, :], start=(k == 0), stop=(k == 7)),
                      reads=[st["tsq"], st["tones"]], writes=[tps])
            rstd_from_ps(fw, st["rs"], st["trs"], ps, tps, n, 1.0 / DM, st["eps"][:, 0:1], st["teps"])
            ys, tys = yr.next()
            for k in range(8):
                fw.op("dve", lambda e, ys=ys, xs=xs, k=k: e.scalar_tensor_tensor(
                    out=ys[:, k, :], in0=xs[:, k, :], scalar=g[:, k:k + 1], in1=st["rs"][:], op0=ALU.mult, op1=ALU.mult),
                    reads=[txs, st["trs"], tid], writes=[tys])
            for blk in range(4):
                ot, tot = orr.next()
                for half in range(2):
                    ps2, tps2 = psr.next()
                    for k in range(4):
                        kk = half * 4 + k
                        fw.op("pe", lambda e, ps2=ps2, k=k, kk=kk, ys=ys, blk=blk: e.transpose(
                            ps2[:, k * 128:(k + 1) * 128], ys[:, kk, blk * 128:(blk + 1) * 128], idt[:]),
                            reads=[tys, tid], writes=[tps2])
                    if half == 0:
                        fw.op("dve", lambda e, ot=ot, ps2=ps2: e.tensor_copy(out=ot[:, 0:512], in_=ps2[:]), reads=[tps2], writes=[tot])
                    else:
                        fw.op("act", lambda e, ot=ot, ps2=ps2: e.copy(out=ot[:, 512:1024], in_=ps2[:]), reads=[tps2], writes=[tot])
                r0 = lo - NCTX + blk * 128
                fw.dma("sp", OUT[r0:r0 + 128, :], ot[:], reads=[tot], writes=[tout])
        fw.barrier()


def sin_reduced(fw, P, out, src, tsrc, shape, off, tout):
    ti = P.sb(shape, I32)
    tf = P.sb(shape, F32)
    tt = Tok()
    fw.op("dve", lambda e: e.tensor_scalar(out=ti[:], in0=src, scalar1=off, scalar2=1.0 / (2 * math.pi), op0=ALU.add, op1=ALU.mult),
          reads=[tsrc], writes=[tt])
    fw.op("dve", lambda e: e.tensor_copy(out=tf[:], in_=ti[:]), reads=[tt], writes=[tt])
    fw.op("dve", lambda e: e.scalar_tensor_tensor(out=tf[:], in0=tf[:], scalar=-2 * math.pi, in1=src, op0=ALU.mult, op1=ALU.add),
          reads=[tt, tsrc], writes=[tt])
    fw.op("dve", lambda e: e.tensor_scalar(out=tf[:], in0=tf[:], scalar1=off, scalar2=math.pi, op0=ALU.add, op1=ALU.min), reads=[tt], writes=[tt])
    fw.op("dve", lambda e: e.tensor_scalar(out=tf[:], in0=tf[:], scalar1=-math.pi, scalar2=None, op0=ALU.max), reads=[tt], writes=[tt])
    fw.op("act", lambda e: e.activation(out=out, in_=tf[:], func=AF.Sin), reads=[tt], writes=[tout])


def stage_p2(fw, io):
    ZT = io("ZT", [2048, TE], F32, "in")
    a_re = io("s5_a_re", [2, 16, 64], F32, "in")
    a_im = io("s5_a_im", [2, 16, 64], F32, "in")
    lstep = io("s5_log_step", [2, 16], F32, "in")
    b_re = io("s5_b_re", [2, 16, 64, 16], F32, "in")
    b_im = io("s5_b_im", [2, 16, 64, 16], F32, "in")
    c_re = io("s5_c_re", [2, 16, 16, 64], F32, "in")
    c_im = io("s5_c_im", [2, 16, 16, 64], F32, "in")
    dsk = io("s5_d", [256], F32, "in")
    glu_w = io("s5_glu_w", [256, 256], F32, "in")
    glu_b = io("s5_glu_b", [256], F32, "in")
    out_g = io("s5_out_g", [256], F32, "in")
    S5T = io("S5T", [256, T], F32, "out")
    N = T
    with ExitStack() as es:
        P = Pool_(fw, es)
        are = P.sb([128, 2, 8], F32)
        aim = P.sb([128, 2, 8], F32)
        lst = P.sb([128, 2, 8], F32)
        tpar = Tok()
        for di in range(2):
            fw.dma("sp", are[:, di, :], a_re[di].rearrange("(s g) p -> (g p) s", g=2), writes=[tpar], allow_slow_non_contiguous=True)
            fw.dma("sp", aim[:, di, :], a_im[di].rearrange("(s g) p -> (g p) s", g=2), writes=[tpar], allow_slow_non_contiguous=True)
            for g2 in range(2):
                fw.dma("sp", lst[g2 * 64:(g2 + 1) * 64, di:di + 1, :],
                       lstep[di].rearrange("(s g) -> g s", g=2)[g2:g2 + 1, :].partition_broadcast(64), writes=[tpar],
                       allow_slow_non_contiguous=True)
        sh = [128, 2, 8]
        step = P.sb(sh, F32)
        fw.op("act", lambda e: e.activation(out=step[:], in_=lst[:], func=AF.Exp), reads=[tpar], writes=[tpar])
        er = P.sb(sh, F32)
        th = P.sb(sh, F32)
        fw.op("dve", lambda e: e.tensor_tensor(out=er[:], in0=are[:], in1=step[:], op=ALU.mult), reads=[tpar], writes=[tpar])
        fw.op("act", lambda e: e.activation(out=er[:], in_=er[:], func=AF.Exp), reads=[tpar], writes=[tpar])
        fw.op("dve", lambda e: e.tensor_tensor(out=th[:], in0=aim[:], in1=step[:], op=ALU.mult), reads=[tpar], writes=[tpar])
        sn = P.sb(sh, F32)
        cs = P.sb(sh, F32)
        ttrig = Tok()
        sin_reduced(fw, P, sn[:], th[:], tpar, sh, 0.0, ttrig)
        sin_reduced(fw, P, cs[:], th[:], tpar, sh, math.pi / 2, ttrig)
        PW = P.sb([128, 9, 3, 2, 8], F32)
        tpw = Tok()
        fw.op("dve", lambda e: e.tensor_tensor(out=PW[:, 0, 0], in0=er[:], in1=cs[:], op=ALU.mult), reads=[tpar, ttrig], writes=[tpw])
        fw.op("dve", lambda e: e.tensor_tensor(out=PW[:, 0, 1], in0=er[:], in1=sn[:], op=ALU.mult), reads=[tpar, ttrig], writes=[tpw])
        t1 = P.sb(sh, F32)
        t2 = P.sb(sh, F32)
        for k in range(9):
            fw.op("dve", lambda e, k=k: e.tensor_scalar(out=PW[:, k, 2], in0=PW[:, k, 1], scalar1=-1.0, scalar2=None, op0=ALU.mult),
                  reads=[tpw], writes=[tpw])
            if k == 8:
                break
            fw.op("dve", lambda e, k=k: e.tensor_tensor(out=t1[:], in0=PW[:, k, 0], in1=PW[:, k, 0], op=ALU.mult), reads=[tpw], writes=[tpw])
            fw.op("dve", lambda e, k=k: e.tensor_tensor(out=t2[:], in0=PW[:, k, 1], in1=PW[:, k, 1], op=ALU.mult), reads=[tpw], writes=[tpw])
            fw.op("dve", lambda e, k=k: e.tensor_tensor(out=PW[:, k + 1, 0], in0=t1[:], in1=t2[:], op=ALU.subtract), reads=[tpw], writes=[tpw])
            fw.op("dve", lambda e, k=k: e.scalar_tensor_tensor(out=PW[:, k + 1, 1], in0=PW[:, k, 0], scalar=2.0, in1=PW[:, k, 1],
                                                               op0=ALU.mult, op1=ALU.mult), reads=[tpw], writes=[tpw])
        br = P.sb(sh, F32)
        bi = P.sb(sh, F32)
        nbi = P.sb(sh, F32)
        den = P.sb(sh, F32)
        nr = P.sb(sh, F32)
        tb = Tok()
        fw.op("dve", lambda e: e.tensor_tensor(out=den[:], in0=are[:], in1=are[:], op=ALU.mult), reads=[tpar], writes=[tb])
        fw.op("dve", lambda e: e.tensor_tensor(out=t1[:], in0=aim[:], in1=aim[:], op=ALU.mult), reads=[tpar, tpw], writes=[tpw])
        fw.op("dve", lambda e: e.tensor_tensor(out=den[:], in0=den[:], in1=t1[:], op=ALU.add), reads=[tb, tpw], writes=[tb])
        fw.op("dve", lambda e: e.reciprocal(out=den[:], in_=den[:]), reads=[tb], writes=[tb])
        fw.op("dve", lambda e: e.tensor_scalar(out=nr[:], in0=PW[:, 0, 0], scalar1=-1.0, scalar2=None, op0=ALU.add), reads=[tpw], writes=[tb])
        fw.op("dve", lambda e: e.tensor_tensor(out=t1[:], in0=nr[:], in1=are[:], op=ALU.mult), reads=[tb, tpar, tpw], writes=[tpw])
        fw.op("dve", lambda e: e.tensor_tensor(out=t2[:], in0=PW[:, 0, 1], in1=aim[:], op=ALU.mult), reads=[tpw, tpar], writes=[tpw])
        fw.op("dve", lambda e: e.tensor_tensor(out=t1[:], in0=t1[:], in1=t2[:], op=ALU.add), reads=[tpw], writes=[tpw])
        fw.op("dve", lambda e: e.tensor_tensor(out=br[:], in0=t1[:], in1=den[:], op=ALU.mult), reads=[tpw, tb], writes=[tb])
        fw.op("dve", lambda e: e.tensor_tensor(out=t1[:], in0=PW[:, 0, 1], in1=are[:], op=ALU.mult), reads=[tpw, tpar, tb], writes=[tpw])
        fw.op("dve", lambda e: e.tensor_tensor(out=t2[:], in0=nr[:], in1=aim[:], op=ALU.mult), reads=[tb, tpar, tpw], writes=[tpw])
        fw.op("dve", lambda e: e.tensor_tensor(out=t1[:], in0=t1[:], in1=t2[:], op=ALU.subtract), reads=[tpw], writes=[tpw])
        fw.op("dve", lambda e: e.tensor_tensor(out=bi[:], in0=t1[:], in1=den[:], op=ALU.mult), reads=[tpw, tb], writes=[tb])
        fw.op("dve", lambda e: e.tensor_scalar(out=nbi[:], in0=bi[:], scalar1=-1.0, scalar2=None, op0=ALU.mult), reads=[tb], writes=[tb])
        BTb = P.sb([128, 2, 2, 8, 128], BF16)
        CTb = P.sb([128, 2, 2, 8, 128], BF16)
        tBT = Tok()
        tCT = Tok()
        with ExitStack() as es2:
            P2 = Pool_(fw, es2)
            BTf = P2.sb([128, 2, 2, 8, 128], F32)
            CTf = P2.sb([128, 2, 2, 8, 128], F32)
            CT2 = P2.sb([128, 2, 2, 8, 128], F32)
            tf1, tf2 = Tok(), Tok()
            fw.op("pool", lambda e: e.memset(BTf[:], 0.0), writes=[tf1])
            fw.op("pool", lambda e: e.memset(CTf[:], 0.0), writes=[tf2])
            for di in range(2):
                for ri, (bsrc, csrc) in enumerate(((b_re, c_re), (b_im, c_im))):
                    for g in range(16):
                        s, g2 = g // 2, g % 2
                        r0 = (g % 8) * 16
                        fw.dma("sp", BTf[r0:r0 + 16, di, ri, s, g2 * 64:(g2 + 1) * 64], bsrc[di, g].rearrange("p h -> h p"),
                               writes=[tf1], allow_slow_non_contiguous=True)
                        fw.dma("sp", CTf[g2 * 64:(g2 + 1) * 64, di, ri, s, r0:r0 + 16], csrc[di, g].rearrange("h p -> p h"),
                               writes=[tf2], allow_slow_non_contiguous=True)
            fw.op("act", lambda e: e.copy(out=BTb[:], in_=BTf[:]), reads=[tf1], writes=[tBT])
            bsh = [128, 2, 8, 128]
            brb = br[:].unsqueeze(3).to_broadcast(bsh)
            nbib = nbi[:].unsqueeze(3).to_broadcast(bsh)
            tc2 = Tok()
            fw.op("dve", lambda e: e.tensor_tensor(out=CT2[:, :, 0], in0=CTf[:, :, 0], in1=brb, op=ALU.mult), reads=[tf2, tb], writes=[tc2])
            fw.op("pool", lambda e: e.tensor_tensor(out=CT2[:, :, 1], in0=CTf[:, :, 1], in1=nbib, op=ALU.mult), reads=[tf2, tb], writes=[tc2])
            fw.op("dve", lambda e: e.tensor_tensor(out=CT2[:, :, 0], in0=CT2[:, :, 0], in1=CT2[:, :, 1], op=ALU.add), reads=[tc2], writes=[tc2])
            fw.op("act", lambda e: e.copy(out=CTb[:, :, 0], in_=CT2[:, :, 0]), reads=[tc2], writes=[tCT])
            fw.op("dve", lambda e: e.tensor_tensor(out=CT2[:, :, 0], in0=CTf[:, :, 0], in1=nbib, op=ALU.mult), reads=[tf2, tb, tCT, tc2], writes=[tc2])
            fw.op("pool", lambda e: e.tensor_tensor(out=CT2[:, :, 1], in0=CTf[:, :, 1], in1=brb, op=ALU.mult), reads=[tf2, tb, tc2], writes=[tc2])
            fw.op("dve", lambda e: e.tensor_tensor(out=CT2[:, :, 0], in0=CT2[:, :, 0], in1=CT2[:, :, 1], op=ALU.subtract), reads=[tc2], writes=[tc2])
            fw.op("act", lambda e: e.copy(out=CTb[:, :, 1], in_=CT2[:, :, 0]), reads=[tc2], writes=[tCT])
            fw.barrier()
        ub = P.sb([128, 2, TE], BF16)
        tub = Tok()
        for ct in range(2):
            fw.dma("pool", ub[:, ct, :], ZT[ct * 128:(ct + 1) * 128, :], writes=[tub])
        dk = P.sb([128, 2], F32)
        tdk = Tok()
        fw.dma("sp", dk[:], dsk.rearrange("(c p) -> p c", p=128), writes=[tdk], allow_slow_non_contiguous=True)
        y = P.sb([128, 2, T], F32)
        ty = Tok()
        for ct in range(2):
            fw.dma("sp", y[:, ct, :], ZT[ct * 128:(ct + 1) * 128, 0:T], writes=[ty])
        for ct in range(2):
            fw.op("pool", lambda e, ct=ct: e.tensor_scalar(out=y[:, ct, :], in0=y[:, ct, :], scalar1=dk[:, ct:ct + 1], scalar2=None, op0=ALU.mult),
                  reads=[ty, tdk], writes=[ty])
        Xs = Rot([P.sb([128, 2, N], F32) for _ in range(4)])
        xbs = Rot([P.sb([128, 2, N], BF16) for _ in range(2)])
        psr = Rot([P.ps() for _ in range(4)])
        psy = Rot([P.ps() for _ in range(2)])
        tmpy = Rot([P.sb([128, 512], F32) for _ in range(2)])
        CH = [(0, 512), (512, 1024), (1024, 1536), (1536, 2048), (2048, 2304)]

        def prep(di, s):
            off = 0 if di == 0 else 256
            ct = s // 4
            X, tX = Xs.next()
            for ri in range(2):
                for ci, (c0, c1) in enumerate(CH):
                    n = c1 - c0
                    ps, tps = psr.next()
                    fw.op("pe", lambda e, ps=ps, ri=ri, c0=c0, c1=c1, n=n: e.matmul(
                        ps[:, :n], lhsT=BTb[:, di, ri, s, :], rhs=ub[:, ct, off + c0:off + c1], start=True, stop=True),
                        reads=[tBT, tub], writes=[tps])
                    fw.op("act", lambda e, ps=ps, ri=ri, c0=c0, c1=c1, n=n: e.copy(out=X[:, ri, c0:c1], in_=ps[:, :n]),
                          reads=[tps], writes=[tX])
            return X, tX

        def scan_gen(di, s, X, tX):
            def cstep(w_re, w_im, r_re, r_im, k):
                pr = PW[:, k, 0, di, s:s + 1]
                pi = PW[:, k, 1, di, s:s + 1]
                npi = PW[:, k, 2, di, s:s + 1]
                for (o, i0, sc) in ((w_re, r_re, pr), (w_re, r_im, npi), (w_im, r_im, pr), (w_im, r_re, pi)):
                    fw.op("dve", lambda e, o=o, i0=i0, sc=sc: e.scalar_tensor_tensor(out=o, in0=i0, scalar=sc, in1=o, op0=ALU.mult, op1=ALU.add),
                          reads=[tX, tpw], writes=[tX])
                    yield
            for k in range(8):
                st_ = 1 << k
                Xv = [X[:, ri, :].rearrange("p (m c) -> p m c", c=2 * st_) for ri in range(2)]
                if di == 0:
                    yield from cstep(Xv[0][:, :, 2 * st_ - 1], Xv[1][:, :, 2 * st_ - 1], Xv[0][:, :, st_ - 1], Xv[1][:, :, st_ - 1], k)
                else:
                    yield from cstep(Xv[0][:, :, 0], Xv[1][:, :, 0], Xv[0][:, :, st_], Xv[1][:, :, st_], k)
            for i in (range(1, 9) if di == 0 else range(7, -1, -1)):
                if di == 0:
                    w, r = 256 * i + 255, 256 * (i - 1) + 255
                else:
                    w, r = 256 * i, 256 * (i + 1)
                yield from cstep(X[:, 0, w:w + 1], X[:, 1, w:w + 1], X[:, 0, r:r + 1], X[:, 1, r:r + 1], 8)
            for k in range(7, -1, -1):
                st_ = 1 << k
                Xv = [X[:, ri, :].rearrange("p (m c) -> p m c", c=2 * st_) for ri in range(2)]
                if di == 0:
                    yield from cstep(Xv[0][:, 1:, st_ - 1], Xv[1][:, 1:, st_ - 1], Xv[0][:, :-1, 2 * st_ - 1], Xv[1][:, :-1, 2 * st_ - 1], k)
                else:
                    yield from cstep(Xv[0][:, :-1, st_], Xv[1][:, :-1, st_], Xv[0][:, 1:, 0], Xv[1][:, 1:, 0], k)

        def fin(di, s, X, tX):
            ct = s // 4
            xb, txb = xbs.next()
            fw.op("act", lambda e: e.copy(out=xb[:], in_=X[:]), reads=[tX], writes=[txb])
            for (c0, c1) in CH:
                n = c1 - c0
                if di == 0:
                    y0 = c0
                else:
                    y0 = c0 + 256 if c0 < 2048 else 0
                ps, tps = psy.next()
                for ri in range(2):
                    fw.op("pe", lambda e, ps=ps, ri=ri, c0=c0, c1=c1, n=n: e.matmul(
                        ps[:, :n], lhsT=CTb[:, di, ri, s, :], rhs=xb[:, ri, c0:c1], start=(ri == 0), stop=(ri == 1)),
                        reads=[tCT, txb], writes=[tps])
                tm, ttm = tmpy.next()
                fw.op("act", lambda e, tm=tm, ps=ps, n=n: e.copy(out=tm[:, :n], in_=ps[:, :n]), reads=[tps], writes=[ttm])
                fw.op("pool", lambda e, tm=tm, y0=y0, n=n: e.tensor_tensor(out=y[:, ct, y0:y0 + n], in0=y[:, ct, y0:y0 + n], in1=tm[:, :n], op=ALU.add),
                      reads=[ttm, ty], writes=[ty])

        pairs = [(di, s0, s0 + 1) for di in range(2) for s0 in range(0, 8, 2)]
        nxt = [prep(pairs[0][0], pairs[0][1]), prep(pairs[0][0], pairs[0][2])]
        for pi_, (di, s0, s1) in enumerate(pairs):
            cur = nxt
            if pi_ + 1 < len(pairs):
                d2, a0, a1 = pairs[pi_ + 1]
                nxt = [prep(d2, a0), prep(d2, a1)]
            gens = [scan_gen(di, s0, *cur[0]), scan_gen(di, s1, *cur[1])]
            alive = [True, True]
            while any(alive):
                for gi in range(2):
                    if alive[gi]:
                        try:
                            next(gens[gi])
                        except StopIteration:
                            alive[gi] = False
            fin(di, s0, *cur[0])
            fin(di, s1, *cur[1])
        tg = Tok()
        gw = load_w_bf16(fw, P, glu_w, 256, 256, tg, "gluw")
        gb = P.sb([128, 2], F32)
        og = P.sb([128, 2], F32)
        fw.dma("sp", gb[:], glu_b.rearrange("(c p) -> p c", p=128), writes=[tg], allow_slow_non_contiguous=True)
        fw.dma("sp", og[:], out_g.rearrange("(c p) -> p c", p=128), writes=[tg], allow_slow_non_contiguous=True)
        ones = P.sb([128, 128], BF16)
        eps = P.sb([128, 1], F32)
        fw.op("pool", lambda e: e.memset(ones[:], 1.0), writes=[tg])
        fw.op("pool", lambda e: e.memset(eps[:], RMS_EPS), writes=[tg])
        a32 = P.sb([128, 2, 512], F32)
        ab = P.sb([128, 2, 512], BF16)
        w1 = P.sb([128, 2, 512], F32)
        w2 = P.sb([128, 2, 512], F32)
        sqb = P.sb([128, 2, 512], BF16)
        rs = P.sb([128, 512], F32)
        ta, tw1, trs = Tok(), Tok(), Tok()
        stg = Rot([P.sb([128, 512], F32) for _ in range(2)])
        tout = Tok()
        C1 = math.sqrt(2.0 / math.pi)
        for (lo, hi, ic) in SEGS:
            n = hi - lo
            yv = y[:, :, lo:hi]
            fw.op("act", lambda e, yv=yv, n=n: e.activation(out=w1[:, :, :n], in_=yv, func=AF.Square), reads=[ty], writes=[tw1])
            fw.op("dve", lambda e, n=n: e.tensor_scalar(out=w1[:, :, :n], in0=w1[:, :, :n], scalar1=0.044715 * C1, scalar2=C1, op0=ALU.mult, op1=ALU.add),
                  reads=[tw1], writes=[tw1])
            fw.op("dve", lambda e, yv=yv, n=n: e.tensor_tensor(out=w1[:, :, :n], in0=w1[:, :, :n], in1=yv, op=ALU.mult), reads=[tw1, ty], writes=[tw1])
            fw.op("act", lambda e, n=n: e.activation(out=w1[:, :, :n], in_=w1[:, :, :n], func=AF.Tanh), reads=[tw1], writes=[tw1])
            fw.op("dve", lambda e, n=n: e.tensor_scalar(out=w1[:, :, :n], in0=w1[:, :, :n], scalar1=0.5, scalar2=0.5, op0=ALU.mult, op1=ALU.add),
                  reads=[tw1], writes=[tw1])
            fw.op("dve", lambda e, yv=yv, n=n: e.tensor_tensor(out=a32[:, :, :n], in0=w1[:, :, :n], in1=yv, op=ALU.mult), reads=[tw1, ty], writes=[ta])
            fw.op("act", lambda e, n=n: e.copy(out=ab[:, :, :n], in_=a32[:, :, :n]), reads=[ta], writes=[ta])
            for nt in range(2):
                ps, tps = psr.next()
                for k in range(2):
                    fw.op("pe", lambda e, ps=ps, k=k, nt=nt, n=n: e.matmul(ps[:, :n], lhsT=gw[:, k, nt * 128:(nt + 1) * 128], rhs=ab[:, k, :n],
                                                                      start=(k == 0), stop=(k == 1)), reads=[tg, ta], writes=[tps])
                fw.op("act", lambda e, ps=ps, nt=nt, n=n: e.activation(out=w2[:, nt, :n], in_=ps[:, :n], func=AF.Sigmoid, bias=gb[:, nt:nt + 1]),
                      reads=[tps, tg], writes=[tw1])
            fw.op("dve", lambda e, n=n: e.tensor_tensor(out=w2[:, :, :n], in0=w2[:, :, :n], in1=a32[:, :, :n], op=ALU.mult), reads=[tw1, ta], writes=[tw1])
            fw.op("act", lambda e, n=n: e.activation(out=sqb[:, :, :n], in_=w2[:, :, :n], func=AF.Square), reads=[tw1], writes=[tw1])
            ps, tps = psr.next()
            for k in range(2):
                fw.op("pe", lambda e, ps=ps, k=k, n=n: e.matmul(ps[:, :n], lhsT=ones[:], rhs=sqb[:, k, :n], start=(k == 0), stop=(k == 1)),
                      reads=[tw1, tg], writes=[tps])
            rstd_from_ps(fw, rs, trs, ps, tps, n, 1.0 / 256, eps[:, 0:1], tg)
            for k in range(2):
                sg, tsg = stg.next()
                fw.op("dve", lambda e, sg=sg, k=k, n=n: e.scalar_tensor_tensor(out=sg[:, :n], in0=w2[:, k, :n], scalar=og[:, k:k + 1], in1=rs[:, :n],
                                                                             op0=ALU.mult, op1=ALU.mult), reads=[tw1, trs, tg], writes=[tsg])
                fw.dma("sp", S5T[k * 128:(k + 1) * 128, lo:hi], sg[:, :n], reads=[tsg], writes=[tout])
        fw.barrier()


RW_BASE = 1024
CHK = 64
NCH = T // CHK
LN_EPS_RW = 64e-5


def rw_consts():
    idx = np.arange(64)
    m = np.zeros((64, 4, 64), np.float32)
    m[:, 0, :] = (idx[:, None] < idx[None, :])
    m[:, 1, :] = (idx[:, None] > idx[None, :])
    m[:, 2, :] = (idx[:, None] <= idx[None, :])
    m[:, 3, :] = (idx[:, None] >= idx[None, :])
    bd = np.zeros((128, 128), np.float32)
    bd[:64, :64] = 1.0
    bd[64:, 64:] = 1.0
    return m, bd


def stage_p4(fw, io):
    ZT = io("ZT", [2048, TE], F32, "in")
    mu = io("rw_mu", [960], F32, "in")
    w0 = io("rw_w0", [2, 256], F32, "in")
    w2 = io("rw_w2", [2, 32, 256], F32, "in")
    a0 = io("rw_a0", [2, 256], F32, "in")
    a2 = io("rw_a2", [2, 32, 256], F32, "in")
    g2 = io("rw_g2", [64, 256], F32, "in")
    k_k = io("rw_k_k", [256], F32, "in")
    k_a = io("rw_k_a", [256], F32, "in")
    r_k = io("rw_r_k", [256], F32, "in")
    ln_g = io("rw_ln_g", [256], F32, "in")
    ln_b = io("rw_ln_b", [256], F32, "in")
    ident = io("ident", [128, 128], F32, "in")
    MASKS = io("RWMASK", [64, 4, 64], F32, "in")
    BD = io("RWBD", [128, 128], F32, "in")
    RWT = io("RWT", [256, T], F32, "out")
    N = T
    CH5 = [(0, 512), (512, 1024), (1024, 1536), (1536, 2048), (2048, 2560)]
    with ExitStack() as es:
        P = Pool_(fw, es)
        tc = Tok()
        idt = P.sb([128, 128], F32)
        idb = P.sb([128, 128], BF16)
        msk = P.sb([64, 4, 64], F32)
        bd1 = P.sb([128, 128], F32)
        fw.dma("sp", idt[:], ident, writes=[tc])
        fw.dma("sp", msk[:], MASKS, writes=[tc])
        fw.dma("sp", bd1[:], BD, writes=[tc])
        fw.op("dve", lambda e: e.tensor_copy(out=idb[:], in_=idt[:]), reads=[tc], writes=[tc])
        mrep = P.sb([64, 4, 4, 64], F32)
        for rep in range(4):
            fw.op("dve", lambda e, rep=rep: e.tensor_copy(out=mrep[:, :, rep, :], in_=msk[:]), reads=[tc], writes=[tc])
        pp = P.sb([128, 12, 2], F32)
        tpp = Tok()
        srcs = [w0[0], w0[1], a0[0], a0[1], k_k, k_a, k_a, r_k, ln_g, ln_b]
        for i, sap in enumerate(srcs):
            fw.dma("sp", pp[:, i, :], sap.rearrange("(c p) -> p c", p=128), writes=[tpp], allow_slow_non_contiguous=True)
        fw.op("dve", lambda e: e.tensor_scalar(out=pp[:, 6, :], in0=pp[:, 6, :], scalar1=-1.0, scalar2=1.0, op0=ALU.mult, op1=ALU.add),
              reads=[tpp], writes=[tpp])
        epsl = P.sb([128, 2], F32)
        fw.op("pool", lambda e: e.memset(epsl[:, 0:1], LN_EPS_RW), writes=[tpp])
        fw.op("pool", lambda e: e.memset(epsl[:, 1:2], 1e-24), writes=[tpp])
        wA = P.sb([128, 256], BF16)
        wB = P.sb([128, 256], BF16)
        tlw = Tok()
        fw.dma("pool", wA[0:32, :], w2[0], writes=[tlw])
        fw.dma("pool", wA[32:64, :], w2[1], writes=[tlw])
        fw.dma("pool", wA[64:96, :], a2[0], writes=[tlw])
        fw.dma("pool", wB[0:32, :], a2[1], writes=[tlw])
        fw.dma("pool", wB[64:128, :], g2, writes=[tlw])
        smask = P.sb([128, N], BF16)
        tsm = Tok()
        fw.op("pool", lambda e: e.memset(smask[:], 1.0), writes=[tsm])
        fw.op("pool", lambda e: e.memset(smask[:].rearrange("p (c j) -> p c j", j=CHK)[:, :, 0], 0.0), writes=[tsm])

        def load_shift(dst, tdst, row0, rows, P2, post=None, pb=0):
            zt_ = P2.sb([128, TE], F32)
            nbt_ = P2.sb([128, TE], F32)
            mtt_ = P2.sb([128, 2], F32)
            tz, tnb, tm = Tok(), Tok(), Tok()
            ps_ = slice(pb, pb + rows)
            z = zt_[ps_, :]
            fw.dma("sp", z, ZT[RW_BASE + row0:RW_BASE + row0 + rows, :], writes=[tz])
            fw.dma("sp", mtt_[ps_, 0:1], mu[row0:row0 + rows].rearrange("(p o) -> p o", o=1), writes=[tm])
            fw.op("dve", lambda e: e.tensor_scalar(out=mtt_[ps_, 1:2], in0=mtt_[ps_, 0:1], scalar1=0.5, scalar2=None, op0=ALU.mult), reads=[tm], writes=[tm])
            fw.op("dve", lambda e: e.tensor_scalar(out=mtt_[ps_, 0:1], in0=mtt_[ps_, 0:1], scalar1=-1.0, scalar2=1.0, op0=ALU.mult, op1=ALU.add),
                  reads=[tm], writes=[tm])
            fw.op("pool", lambda e: e.memset(nbt_[ps_, 0:1], 0.0), writes=[tnb])
            fw.op("pool", lambda e: e.tensor_copy(out=nbt_[ps_, 1:TE], in_=zt_[ps_, 0:TE - 1]), reads=[tz], writes=[tnb])
            fw.op("pool", lambda e: e.tensor_tensor(out=nbt_[ps_, 0:TE - 1], in0=nbt_[ps_, 0:TE - 1], in1=zt_[ps_, 1:TE], op=ALU.add),
                  reads=[tz, tnb], writes=[tnb])
            for cb in (256, 2304):
                fw.op("pool", lambda e, cb=cb: e.tensor_tensor(out=nbt_[ps_, cb:cb + 1], in0=nbt_[ps_, cb:cb + 1], in1=zt_[ps_, cb - 1:cb], op=ALU.subtract),
                      reads=[tz, tnb], writes=[tnb])
                fw.op("pool", lambda e, cb=cb: e.tensor_tensor(out=nbt_[ps_, cb - 1:cb], in0=nbt_[ps_, cb - 1:cb], in1=zt_[ps_, cb:cb + 1], op=ALU.subtract),
                      reads=[tz, tnb], writes=[tnb])
            fw.op("act", lambda e: e.activation(out=z, in_=z, func=AF.Identity, scale=mtt_[ps_, 0:1]), reads=[tz, tm], writes=[tz])
            if post is None:
                fw.op("dve", lambda e: e.scalar_tensor_tensor(out=dst, in0=nbt_[ps_, :], scalar=mtt_[ps_, 1:2], in1=z, op0=ALU.mult, op1=ALU.add),
                      reads=[tz, tnb, tm], writes=[tdst])
            else:
                fw.op("dve", lambda e: e.scalar_tensor_tensor(out=z, in0=nbt_[ps_, :], scalar=mtt_[ps_, 1:2], in1=z, op0=ALU.mult, op1=ALU.add),
                      reads=[tz, tnb, tm], writes=[tz])
                fw.op("act", lambda e: e.activation(out=dst, in_=z, func=post), reads=[tz], writes=[tdst])

        lorA = P.sb([128, TE], BF16)
        lorB = P.sb([128, TE], BF16)
        tlor = Tok()
        for i in range(4):
            with ExitStack() as es2:
                dstt = lorA[32 * i:32 * i + 32, :] if i < 3 else lorB[0:32, :]
                load_shift(dstt, tlor, 768 + 32 * i, 32, Pool_(fw, es2), post=(AF.Tanh if i < 2 else AF.Copy), pb=(32 * i if i < 3 else 0))
                fw.barrier()
        with ExitStack() as es2:
            load_shift(lorB[64:128, :], tlor, 896, 64, Pool_(fw, es2), post=AF.Sigmoid, pb=64)
            fw.barrier()

        for c in range(2):
            with ExitStack() as esc:
                Pc = Pool_(fw, esc)
                rr = Pc.sb([128, TE], F32)
                kx = Pc.sb([128, TE], F32)
                vv = Pc.sb([128, TE], F32)
                kk = Pc.sb([128, TE], F32)
                trr, tkx, tvv, tkk = Tok(), Tok(), Tok(), Tok()
                vtok = Pc.sb([64, TE // CHK, 128], BF16)
                tvt = Tok()
                yacc = Pc.sb([128, T], F32)
                bon = Pc.sb([128, T], F32)
                tya, tbon = Tok(), Tok()
                fw.op("pool", lambda e: e.memset(yacc[:], 0.0), writes=[tya])
                fw.op("pool", lambda e: e.memset(bon[:], 0.0), writes=[tbon])
                for (dst, tdst, r0) in ((rr, trr, 0), (kx, tkx, 256), (vv, tvv, 512)):
                    with ExitStack() as es2:
                        load_shift(dst[:], tdst, r0 + c * 128, 128, Pool_(fw, es2))
                        fw.barrier()
                with ExitStack() as es2:
                    P2 = Pool_(fw, es2)
                    sq = P2.sb([128, 512], F32)
                    rn = P2.sb([128, 512], F32)
                    tsq, trn = Tok(), Tok()
                    ps1 = P2.ps()
                    tps1 = Tok()
                    fw.op("dve", lambda e: e.tensor_scalar(out=kk[:], in0=kx[:], scalar1=pp[:, 4, c:c + 1], scalar2=None, op0=ALU.mult),
                          reads=[tkx, tpp], writes=[tkk])
                    for (c0, c1) in CH5:
                        fw.op("act", lambda e, c0=c0, c1=c1: e.activation(out=sq[:], in_=kk[:, c0:c1], func=AF.Square), reads=[tkk], writes=[tsq])
                        fw.op("pe", lambda e: e.matmul(ps1[:], lhsT=bd1[:], rhs=sq[:], start=True, stop=True), reads=[tsq, tc], writes=[tps1])
                        rstd_from_ps(fw, rn, trn, ps1, tps1, 512, 1.0, epsl[:, 1:2], tpp)
                        fw.op("dve", lambda e, c0=c0, c1=c1: e.tensor_tensor(out=kk[:, c0:c1], in0=kk[:, c0:c1], in1=rn[:], op=ALU.mult),
                              reads=[tkk, trn], writes=[tkk])
                    vb = P2.sb([128, TE], BF16)
                    tvb = Tok()
                    fw.op("act", lambda e: e.copy(out=vb[:], in_=vv[:]), reads=[tvv], writes=[tvb])
                    pst = P2.ps([128, 1024], BF16)
                    tpst = Tok()
                    for q in range(TE // CHK // 4):
                        for j in range(4):
                            ch = q * 4 + j
                            fw.op("pe", lambda e, j=j, ch=ch: e.transpose(pst[0:64, j * 128:(j + 1) * 128], vb[:, ch * CHK:(ch + 1) * CHK], idb[:]),
                                  reads=[tvb, tc], writes=[tpst])
                        fw.op("dve", lambda e, q=q: e.tensor_copy(out=vtok[:, q * 4:(q + 1) * 4, :], in_=pst[0:64, 0:512].rearrange("p (j c) -> p j c", j=4)),
                              reads=[tpst], writes=[tvt])
                    fw.barrier()

                for di in range(2):
                    off = 0 if di == 0 else 256
                    with ExitStack() as esd:
                        Pd = Pool_(fw, esd)
                        aT = Pd.sb([128, N], BF16)
                        bT = Pd.sb([128, N], BF16)
                        kT = Pd.sb([128, N], BF16)
                        rT = Pd.sb([128, N], BF16)
                        btok = Pd.sb([64, NCH, 128], BF16)
                        ktok = Pd.sb([64, NCH, 128], BF16)
                        pC = Pd.sb([128, NCH], F32)
                        tops = Tok()
                        ttok = Tok()
                        with ExitStack() as es2:
                            P2 = Pool_(fw, es2)
                            ld = P2.sb([128, TE], F32)
                            kd = P2.sb([128, TE], F32)
                            bb = P2.sb([128, TE], F32)
                            tld, tkd, tbb = Tok(), Tok(), Tok()
                            psr = Rot([P2.ps() for _ in range(3)])
                            tm5 = Rot([P2.sb([128, 512], F32) for _ in range(2)])
                            for (c0, c1) in CH5:
                                ps, tps = psr.next()
                                fw.op("pe", lambda e, ps=ps, c0=c0, c1=c1: e.matmul(ps[:], lhsT=wA[32 * di:32 * di + 32, c * 128:(c + 1) * 128], rhs=lorA[32 * di:32 * di + 32, c0:c1],
                                                                                  start=True, stop=True), reads=[tlw, tlor], writes=[tps])
                                fw.op("act", lambda e, ps=ps, c0=c0, c1=c1: e.activation(out=ld[:, c0:c1], in_=ps[:], func=AF.Sigmoid, bias=pp[:, di, c:c + 1]),
                                      reads=[tps, tpp], writes=[tld])
                                ps, tps = psr.next()
                                fw.op("pe", lambda e, ps=ps, c0=c0, c1=c1: e.matmul(ps[:], lhsT=(wA[64:96, c * 128:(c + 1) * 128] if di == 0 else wB[0:32, c * 128:(c + 1) * 128]),
                                                                                  rhs=(lorA[64:96, c0:c1] if di == 0 else lorB[0:32, c0:c1]),
                                                                                  start=True, stop=True), reads=[tlw, tlor], writes=[tps])
                                fw.op("act", lambda e, ps=ps, c0=c0, c1=c1: e.activation(out=bb[:, c0:c1], in_=ps[:], func=AF.Sigmoid, bias=pp[:, 2 + di, c:c + 1]),
                                      reads=[tps, tpp], writes=[tbb])
                            fw.op("pool", lambda e: e.tensor_scalar(out=ld[:], in0=ld[:], scalar1=-math.exp(-0.5), scalar2=None, op0=ALU.mult), reads=[tld], writes=[tld])
                            fw.op("act", lambda e: e.activation(out=kd[:], in_=bb[:], func=AF.Identity, scale=pp[:, 5, c:c + 1], bias=pp[:, 6, c:c + 1]),
                                  reads=[tbb, tpp], writes=[tkd])
                            fw.op("dve", lambda e: e.tensor_tensor(out=kd[:], in0=kd[:], in1=kx[:], op=ALU.mult), reads=[tkd, tkx], writes=[tkd])
                            fw.op("pool", lambda e: e.tensor_tensor(out=bb[:], in0=bb[:], in1=kk[:], op=ALU.mult), reads=[tbb, tkk], writes=[tbb])
                            for (c0, c1) in CH5:
                                c1 = min(c1, T)
                                n = c1 - c0
                                tm, ttm = tm5.next()
                                fw.op("dve", lambda e, tm=tm, c0=c0, c1=c1, n=n: e.scalar_tensor_tensor(out=tm[:, :n], in0=rr[:, c0:c1], scalar=pp[:, 7, c:c + 1], in1=kd[:, c0:c1],
                                                                                                 op0=ALU.mult, op1=ALU.mult), reads=[trr, tkd, tpp], writes=[ttm])
                                ps, tps = psr.next()
                                fw.op("pe", lambda e, ps=ps, tm=tm, n=n: e.matmul(ps[:, :n], lhsT=bd1[:], rhs=tm[:, :n], start=True, stop=True), reads=[ttm, tc], writes=[tps])
                                tm2, ttm2 = tm5.next()
                                fw.op("dve", lambda e, tm2=tm2, ps=ps, c0=c0, c1=c1, n=n: e.tensor_tensor(out=tm2[:, :n], in0=ps[:, :n], in1=vv[:, c0:c1], op=ALU.mult),
                                      reads=[tps, tvv], writes=[ttm2])
                                fw.op("pool", lambda e, tm2=tm2, c0=c0, c1=c1, n=n: e.tensor_tensor(out=bon[:, c0:c1], in0=bon[:, c0:c1], in1=tm2[:, :n], op=ALU.add),
                                      reads=[ttm2, tbon], writes=[tbon])
                            cs = P2.sb([128, N], F32)
                            ex = P2.sb([128, N], F32)
                            tcs, tex = Tok(), Tok()
                            ldw = ld[:, off:off + N]
                            fw.op("dve", lambda e: e.tensor_tensor_scan(out=cs[:], data0=smask[:], data1=ldw, initial=0.0, op0=ALU.mult, op1=ALU.add),
                                  reads=[tsm, tld], writes=[tcs])
                            csv = cs[:].rearrange("p (c j) -> p c j", j=CHK)
                            tot = P2.sb([128, NCH, 1], F32)
                            ttot = Tok()
                            fw.op("dve", lambda e: e.tensor_copy(out=tot[:], in_=csv[:, :, CHK - 1:CHK]), reads=[tcs], writes=[ttot])
                            totb = tot[:].to_broadcast([128, NCH, CHK])
                            if di == 1:
                                fw.op("dve", lambda e: e.tensor_tensor(out=csv, in0=totb, in1=csv, op=ALU.subtract), reads=[ttot, tcs], writes=[tcs])
                                fw.op("dve", lambda e: e.tensor_tensor(out=cs[:], in0=cs[:], in1=ldw, op=ALU.add), reads=[tcs, tld], writes=[tcs])
                            fw.op("act", lambda e: e.activation(out=pC[:], in_=tot[:, :, 0], func=AF.Exp), reads=[ttot], writes=[tops])
                            kdw, bbw = kd[:, off:off + N], bb[:, off:off + N]
                            rrw, kkw = rr[:, off:off + N], kk[:, off:off + N]
                            fw.op("act", lambda e: e.activation(out=ex[:], in_=cs[:], func=AF.Exp), reads=[tcs], writes=[tex])
                            fw.op("dve", lambda e: e.tensor_tensor(out=rT[:], in0=rrw, in1=ex[:], op=ALU.mult), reads=[trr, tex], writes=[tops])
                            fw.op("act", lambda e: e.activation(out=ex[:], in_=cs[:], func=AF.Exp, scale=-1.0), reads=[tcs, tops], writes=[tex])
                            fw.op("dve", lambda e: e.tensor_tensor(out=bT[:], in0=bbw, in1=ex[:], op=ALU.mult), reads=[tbb, tex], writes=[tops])
                            fw.op("pool", lambda e: e.tensor_tensor(out=kT[:], in0=kdw, in1=ex[:], op=ALU.mult), reads=[tkd, tex], writes=[tops])
                            e3 = ex
                            te3 = tex
                            fw.op("dve", lambda e: e.tensor_tensor(out=e3[:], in0=cs[:], in1=ldw, op=ALU.subtract), reads=[tcs, tld], writes=[te3])
                            fw.op("act", lambda e: e.activation(out=e3[:], in_=e3[:], func=AF.Exp), reads=[te3], writes=[te3])
                            fw.op("dve", lambda e: e.scalar_tensor_tensor(out=aT[:], in0=kkw, scalar=-1.0, in1=e3[:], op0=ALU.mult, op1=ALU.mult),
                                  reads=[tkk, te3], writes=[tops])
                            e3v = e3[:].rearrange("p (c j) -> p c j", j=CHK)
                            fw.op("dve", lambda e: e.tensor_tensor(out=e3v, in0=totb, in1=csv, op=ALU.subtract), reads=[ttot, tcs, tops, te3], writes=[te3])
                            fw.op("act", lambda e: e.activation(out=e3[:], in_=e3[:], func=AF.Exp), reads=[te3], writes=[te3])
                            bh = P2.sb([128, N], BF16)
                            kh = P2.sb([128, N], BF16)
                            tbh = Tok()
                            fw.op("dve", lambda e: e.tensor_tensor(out=bh[:], in0=bbw, in1=e3[:], op=ALU.mult), reads=[tbb, te3], writes=[tbh])
                            fw.op("pool", lambda e: e.tensor_tensor(out=kh[:], in0=kdw, in1=e3[:], op=ALU.mult), reads=[tkd, te3], writes=[tbh])
                            pst = P2.ps([128, 1024], BF16)
                            tpst = Tok()
                            for (src, dstt) in ((bh, btok), (kh, ktok)):
                                for q in range(NCH // 4):
                                    for j in range(4):
                                        ch = q * 4 + j
                                        fw.op("pe", lambda e, j=j, ch=ch, src=src: e.transpose(pst[0:64, j * 128:(j + 1) * 128], src[:, ch * CHK:(ch + 1) * CHK], idb[:]),
                                              reads=[tbh, tc], writes=[tpst])
                                    fw.op("act", lambda e, q=q, dstt=dstt: e.copy(out=dstt[:, q * 4:(q + 1) * 4, :], in_=pst[0:64, 0:512].rearrange("p (j c) -> p j c", j=4)),
                                          reads=[tpst], writes=[ttok])
                            fw.barrier()
                        with ExitStack() as es3:
                            P3 = Pool_(fw, es3)
                            Hs = P3.sb([128, 64], F32)
                            Hb = P3.sb([128, 64], BF16)
                            tH = Tok()
                            fw.op("pool", lambda e: e.memset(Hs[:], 0.0), writes=[tH])
                            fw.op("pool", lambda e: e.memset(Hb[:], 0.0), writes=[tH])
                            psA = Rot([P3.ps([64, 512]) for _ in range(1)])
                            psB = Rot([P3.ps([64, 512]) for _ in range(1)])
                            psI = Rot([P3.ps([64, 512]) for _ in range(1)])
                            psG = Rot([P3.ps([64, 512]) for _ in range(1)])
                            psH = Rot([P3.ps([128, 512]) for _ in range(1)])
                            psY = Rot([P3.ps([128, 512]) for _ in range(1)])
                            g1b = Rot([P3.sb([64, 2, 64], BF16) for _ in range(2)])
                            nl = Rot([P3.sb([64, 2, 2, 64], F32) for _ in range(2)])
                            nl2 = Rot([P3.sb([64, 2, 2, 64], F32) for _ in range(2)])
                            g2b_ = Rot([P3.sb([64, 4, 64], BF16) for _ in range(2)])
                            Pm = Rot([P3.sb([64, 2, 64], F32) for _ in range(2)])
                            Gs = Rot([P3.sb([64, 128], F32) for _ in range(2)])
                            Ub = Rot([P3.sb([64, 128], BF16) for _ in range(2)])
                            Ys = Rot([P3.sb([64, 128], F32) for _ in range(2)])
                            ms, ml, mi = (0, 1, 2) if di == 0 else (1, 0, 3)
                            order = list(range(NCH)) if di == 0 else list(range(NCH - 1, -1, -1))
                            res = {}

                            def par_gen(i):
                                cc0, cc1 = i * CHK, (i + 1) * CHK
                                pa, tpa = psA.next()
                                pb, tpb = psB.next()
                                for h in range(2):
                                    hp = slice(h * 64, (h + 1) * 64)
                                    for (dst, lt, rt_) in ((pa[:, h * 64:(h + 1) * 64], kT, aT), (pa[:, 128 + h * 64:128 + (h + 1) * 64], bT, aT),
                                                           (pa[:, 256 + h * 64:256 + (h + 1) * 64], aT, bT)):
                                        fw.op("pe", lambda e, dst=dst, lt=lt, rt_=rt_, hp=hp: e.matmul(dst, lhsT=lt[hp, cc0:cc1], rhs=rt_[hp, cc0:cc1], start=True, stop=True),
                                              reads=[tops], writes=[tpa])
                                        yield
                                    for (dst, lt, rt_) in ((pb[:, h * 64:(h + 1) * 64], bT, rT), (pb[:, 128 + h * 64:128 + (h + 1) * 64], kT, rT)):
                                        fw.op("pe", lambda e, dst=dst, lt=lt, rt_=rt_, hp=hp: e.matmul(dst, lhsT=lt[hp, cc0:cc1], rhs=rt_[hp, cc0:cc1], start=True, stop=True),
                                              reads=[tops], writes=[tpb])
                                        yield
                                a1, ta1 = g1b.next()
                                nlt, tnl = nl.next()
                                a45, ta45 = g2b_.next()
                                fw.op("dve", lambda e: e.tensor_tensor(out=a1[:], in0=pa[:, 0:128].rearrange("p (h t) -> p h t", h=2), in1=mrep[:, ms, 0:2, :], op=ALU.mult),
                                      reads=[tpa, tc], writes=[ta1])
                                yield
                                fw.op("dve", lambda e: e.tensor_tensor(out=nlt[:, 0], in0=pa[:, 128:256].rearrange("p (h t) -> p h t", h=2), in1=mrep[:, ms, 0:2, :], op=ALU.mult),
                                      reads=[tpa, tc], writes=[tnl])
                                yield
                                fw.op("dve", lambda e: e.tensor_tensor(out=nlt[:, 1], in0=pa[:, 256:384].rearrange("p (h t) -> p h t", h=2), in1=mrep[:, ml, 0:2, :], op=ALU.mult),
                                      reads=[tpa, tc], writes=[tnl])
                                yield
                                fw.op("dve", lambda e: e.tensor_tensor(out=a45[:], in0=pb[:, 0:256].rearrange("p (h t) -> p h t", h=4), in1=mrep[:, mi, :, :], op=ALU.mult),
                                      reads=[tpb, tc], writes=[ta45])
                                yield
                                pm, tpm = Pm.next()
                                fw.op("dve", lambda e: e.tensor_tensor(out=pm[:], in0=nlt[:, 0], in1=idt[0:64, 0:64].unsqueeze(1).to_broadcast([64, 2, 64]), op=ALU.add),
                                      reads=[tnl, tc], writes=[tpm])
                                yield
                                cur, tcur = nlt, tnl
                                for lev in range(5):
                                    pi_, tpi = psI.next()
                                    for h in range(2):
                                        fw.op("pe", lambda e, pi_=pi_, h=h, cur=cur: e.matmul(pi_[:, h * 64:(h + 1) * 64], lhsT=cur[:, 0, h, :], rhs=cur[:, 1, h, :], start=True, stop=True),
                                              reads=[tcur], writes=[tpi])
                                        yield
                                        fw.op("pe", lambda e, pi_=pi_, h=h, cur=cur: e.matmul(pi_[:, 128 + h * 64:128 + (h + 1) * 64], lhsT=cur[:, 1, h, :], rhs=cur[:, 0, h, :], start=True, stop=True),
                                              reads=[tcur], writes=[tpi])
                                        yield
                                    nxt, tnxt = (nl2.next() if lev % 2 == 0 else nl.next())
                                    fw.op("act", lambda e, nxt=nxt, pi_=pi_: e.copy(out=nxt[:, 1], in_=pi_[:, 0:128].rearrange("p (h t) -> p h t", h=2)), reads=[tpi], writes=[tnxt])
                                    yield
                                    fw.op("act", lambda e, nxt=nxt, pi_=pi_: e.copy(out=nxt[:, 0], in_=pi_[:, 128:256].rearrange("p (h t) -> p h t", h=2)), reads=[tpi], writes=[tnxt])
                                    yield
                                    for h in range(2):
                                        fw.op("pe", lambda e, pi_=pi_, h=h, nxt=nxt: e.matmul(pi_[:, 256 + h * 64:256 + (h + 1) * 64], lhsT=nxt[:, 1, h, :], rhs=pm[:, h, :], start=True, stop=True),
                                              reads=[tnxt, tpm], writes=[tpi])
                                        yield
                                    fw.op("dve", lambda e, pi_=pi_: e.tensor_tensor(out=pm[:], in0=pm[:], in1=pi_[:, 256:384].rearrange("p (h t) -> p h t", h=2), op=ALU.add),
                                          reads=[tpi, tpm], writes=[tpm])
                                    yield
                                    cur, tcur = nxt, tnxt
                                res[i] = (a1, ta1, a45, ta45, pm, tpm)

                            def chain_gen(i):
                                cc0, cc1 = i * CHK, (i + 1) * CHK
                                gch = i + off // CHK
                                a1, ta1, a45, ta45, pm, tpm = res.pop(i)
                                pg, tpg = psG.next()
                                for h in range(2):
                                    hp = slice(h * 64, (h + 1) * 64)
                                    fw.op("pe", lambda e, h=h, hp=hp: e.matmul(pg[:, h * 64:(h + 1) * 64], lhsT=aT[hp, cc0:cc1], rhs=Hb[hp, :], start=True, stop=False),
                                          reads=[tops, tH], writes=[tpg])
                                    yield
                                    fw.op("pe", lambda e, h=h, hp=hp: e.matmul(pg[:, h * 64:(h + 1) * 64], lhsT=a1[:, h, :], rhs=vtok[:, gch, hp], start=False, stop=True),
                                          reads=[ta1, tvt], writes=[tpg])
                                    yield
                                gs, tgs = Gs.next()
                                fw.op("act", lambda e: e.copy(out=gs[:], in_=pg[:, 0:128]), reads=[tpg], writes=[tgs])
                                yield
                                for h in range(2):
                                    fw.op("pe", lambda e, h=h: e.matmul(pg[:, 128 + h * 64:128 + (h + 1) * 64], lhsT=pm[:, h, :], rhs=gs[:, h * 64:(h + 1) * 64], start=True, stop=True),
                                          reads=[tpm, tgs], writes=[tpg])
                                    yield
                                ub, tub = Ub.next()
                                fw.op("dve", lambda e: e.tensor_copy(out=ub[:], in_=pg[:, 128:256]), reads=[tpg], writes=[tub])
                                yield
                                ph, tph = psH.next()
                                py, tpy = psY.next()
                                for h in range(2):
                                    hp = slice(h * 64, (h + 1) * 64)
                                    fw.op("pe", lambda e, h=h, hp=hp: e.matmul(ph[hp, 0:64], lhsT=btok[:, i, hp], rhs=ub[:, hp], start=True, stop=False),
                                          reads=[ttok, tub], writes=[tph])
                                    yield
                                    fw.op("pe", lambda e, h=h, hp=hp: e.matmul(ph[hp, 0:64], lhsT=ktok[:, i, hp], rhs=vtok[:, gch, hp], start=False, stop=True),
                                          reads=[ttok, tvt], writes=[tph])
                                    yield
                                for h in range(2):
                                    hp = slice(h * 64, (h + 1) * 64)
                                    fw.op("pe", lambda e, h=h, hp=hp: e.matmul(py[0:64, hp], lhsT=rT[hp, cc0:cc1], rhs=Hb[hp, :], start=True, stop=False),
                                          reads=[tops, tH], writes=[tpy])
                                    yield
                                    fw.op("pe", lambda e, h=h, hp=hp: e.matmul(py[0:64, hp], lhsT=a45[:, h, :], rhs=ub[:, hp], start=False, stop=False),
                                          reads=[ta45, tub], writes=[tpy])
                                    yield
                                    fw.op("pe", lambda e, h=h, hp=hp: e.matmul(py[0:64, hp], lhsT=a45[:, 2 + h, :], rhs=vtok[:, gch, hp], start=False, stop=True),
                                          reads=[ta45, tvt], writes=[tpy])
                                    yield
                                fw.op("dve", lambda e: e.scalar_tensor_tensor(out=Hs[:], in0=Hs[:], scalar=pC[:, i:i + 1], in1=ph[:, 0:64], op0=ALU.mult, op1=ALU.add),
                                      reads=[tph, tH, tops, tpy], writes=[tH])
                                yield
                                fw.op("act", lambda e: e.copy(out=Hb[:], in_=Hs[:]), reads=[tH, tpy, tpg], writes=[tH])
                                yield
                                ys, tys = Ys.next()
                                fw.op("act", lambda e: e.copy(out=ys[:], in_=py[0:64, 0:128]), reads=[tpy], writes=[tys])
                                yield
                                fw.op("pe", lambda e: e.transpose(py[:, 256:320], ys[:], idt[0:64, 0:64]), reads=[tys, tc], writes=[tpy])
                                yield
                                y0 = off + cc0
                                if y0 >= T:
                                    y0 -= T
                                fw.op("dve", lambda e: e.tensor_tensor(out=yacc[:, y0:y0 + CHK], in0=yacc[:, y0:y0 + CHK], in1=py[:, 256:320], op=ALU.add),
                                      reads=[tpy, tya], writes=[tya])
                                yield

                            for _ in par_gen(order[0]):
                                pass
                            for idx, i in enumerate(order):
                                gp = par_gen(order[idx + 1]) if idx + 1 < len(order) else iter(())
                                gc = chain_gen(i)
                                done_p = done_c = False
                                while not (done_p and done_c):
                                    for _ in range(2):
                                        if not done_p:
                                            try:
                                                next(gp)
                                            except StopIteration:
                                                done_p = True
                                    if not done_c:
                                        try:
                                            next(gc)
                                        except StopIteration:
                                            done_c = True
                            fw.barrier()
                with ExitStack() as es4:
                    P4 = Pool_(fw, es4)
                    psr = Rot([P4.ps() for _ in range(3)])
                    xc = P4.sb([128, 512], F32)
                    sq = P4.sb([128, 512], F32)
                    rs = P4.sb([128, 512], F32)
                    txc, tsq, trs = Tok(), Tok(), Tok()
                    stg = Rot([P4.sb([128, 512], F32) for _ in range(2)])
                    tout = Tok()
                    for (lo, hi, ic) in SEGS:
                        n = hi - lo
                        ps, tps = psr.next()
                        fw.op("pe", lambda e, ps=ps, lo=lo, hi=hi, n=n: e.matmul(ps[:, :n], lhsT=bd1[:], rhs=yacc[:, lo:hi], start=True, stop=True), reads=[tya, tc], writes=[tps])
                        fw.op("dve", lambda e, ps=ps, lo=lo, hi=hi, n=n: e.scalar_tensor_tensor(out=xc[:, :n], in0=ps[:, :n], scalar=-1.0 / 64, in1=yacc[:, lo:hi], op0=ALU.mult, op1=ALU.add),
                              reads=[tps, tya], writes=[txc])
                        fw.op("act", lambda e, n=n: e.activation(out=sq[:, :n], in_=xc[:, :n], func=AF.Square), reads=[txc], writes=[tsq])
                        ps2, tps2 = psr.next()
                        fw.op("pe", lambda e, ps2=ps2, n=n: e.matmul(ps2[:, :n], lhsT=bd1[:], rhs=sq[:, :n], start=True, stop=True), reads=[tsq, tc], writes=[tps2])
                        rstd_from_ps(fw, rs, trs, ps2, tps2, n, 1.0 / 64, epsl[:, 0:1], tpp)
                        fw.op("dve", lambda e, n=n: e.tensor_tensor(out=xc[:, :n], in0=xc[:, :n], in1=rs[:, :n], op=ALU.mult), reads=[txc, trs], writes=[txc])
                        fw.op("act", lambda e, n=n: e.activation(out=xc[:, :n], in_=xc[:, :n], func=AF.Identity, scale=pp[:, 8, c:c + 1], bias=pp[:, 9, c:c + 1]),
                              reads=[txc, tpp], writes=[txc])
                        fw.op("pool", lambda e, lo=lo, hi=hi, n=n: e.tensor_tensor(out=xc[:, :n], in0=xc[:, :n], in1=bon[:, lo:hi], op=ALU.add), reads=[txc, tbon], writes=[txc])
                        ps3, tps3 = psr.next()
                        fw.op("pe", lambda e, ps3=ps3, lo=lo, hi=hi, n=n: e.matmul(ps3[:, :n], lhsT=wB[64:128, c * 128:(c + 1) * 128], rhs=lorB[64:128, lo:hi], start=True, stop=True),
                              reads=[tlw, tlor], writes=[tps3])
                        sg, tsg = stg.next()
                        fw.op("dve", lambda e, sg=sg, ps3=ps3, n=n: e.tensor_tensor(out=sg[:, :n], in0=ps3[:, :n], in1=xc[:, :n], op=ALU.mult), reads=[tps3, txc], writes=[tsg])
                        fw.dma("sp", RWT[c * 128:(c + 1) * 128, lo:hi], sg[:, :n], reads=[tsg], writes=[tout])
                    fw.barrier()
        fw.barrier()
NCORES = 8
DEPTH = 4
S5_KEYS = ["s5_a_re", "s5_a_im", "s5_log_step", "s5_b_re", "s5_b_im", "s5_c_re", "s5_c_im", "s5_d", "s5_glu_w", "s5_glu_b", "s5_out_g"]
RW_KEYS = ["rw_mu", "rw_w0", "rw_w2", "rw_a0", "rw_a2", "rw_g2", "rw_k_k", "rw_k_a", "rw_r_k", "rw_ln_g", "rw_ln_b"]
LAYER_KEYS = (["norm1_g", "norm2_g", "mod_w", "mod_b", "w_in", "w_out", "att_qn_g", "att_kn_g", "att_out_g",
               "ffn_up", "ffn_conv_w", "ffn_conv_b", "ffn_down"] + S5_KEYS + RW_KEYS)
SHARED_KEYS = ["c_ctx", "final_g", "ident", "POS", "CST", "RWMASK", "RWBD"]
PERCORE_KEYS = ["x_b", "ctx_b", "c_b"]
SCRATCH = {"ZT": [2048, TE], "VT": [T, 128], "MOD": [128, 48, 2], "S5T": [256, T], "ATT_T": [512, T], "RWT": [256, T],
           "XT1": [DM, T], "XA": [DM, T], "XB": [DM, T]}


def build_fused(depth=DEPTH):
    nc = bass.Bass("TRN2", target_bir_lowering=False)
    decl = {}

    def ext(name, shape, dt, kind):
        if name not in decl:
            decl[name] = nc.dram_tensor(name, list(shape), dt, kind=kind).ap()
        return decl[name]

    def make_io(l):
        xin = "XA" if l % 2 == 0 else "XB"
        xout = "XB" if l % 2 == 0 else "XA"

        def io(name, shape, dt, role):
            if name in LAYER_KEYS:
                full = ext(name, [DEPTH] + list(shape), dt, "ExternalInput")
                return full[l]
            if name in SHARED_KEYS or name in PERCORE_KEYS:
                return ext(name, shape, dt, "ExternalInput")
            if name == "OUT":
                return ext(name, shape, dt, "ExternalOutput")
            if name == "XT":
                name = xin
            elif name == "XT2":
                name = xout
            return ext(name, SCRATCH[name], dt, "Internal")
        return io
    with ExitStack() as es:
        fw = FW(nc, es)
        stage_p0(fw, make_io(0))
        for l in range(depth):
            io = make_io(l)
            for st in (stage_p1, stage_p2, stage_p3, stage_p4, stage_p5a, stage_p5b):
                st(fw, io)
        stage_p6(fw, make_io(depth))
        fw.barrier()
    return nc, fw


def tile_up(up):
    lead = up.shape[:-2]
    v = up.reshape(lead + (8, 128, 44, 128))
    nd = len(lead)
    v = np.transpose(v, tuple(range(nd)) + (nd + 2, nd + 1, nd + 0, nd + 3))
    return np.ascontiguousarray(v).reshape(lead + (44, 128, 1024))


_FUSED = {}


def kernel(**inp):
    inp = {k: np.ascontiguousarray(np.asarray(v)) for k, v in inp.items()}
    if "nc" not in _FUSED:
        _FUSED["nc"], _FUSED["fw"] = build_fused()
    nc = _FUSED["nc"]
    pos, cst = host_consts()
    rwm, rwbd = rw_consts()
    shared = {k: inp[k] for k in LAYER_KEYS if k != "rw_r_k"}
    shared["rw_r_k"] = inp["rw_r_k"].reshape(DEPTH, 256)
    shared["ffn_up"] = tile_up(inp["ffn_up"])
    shared.update(c_ctx=inp["c_ctx"], final_g=inp["final_g"], ident=np.eye(128, dtype=np.float32),
                  POS=pos, CST=cst, RWMASK=rwm, RWBD=rwbd)
    in_maps = [dict(shared, x_b=inp["x"][b], ctx_b=inp["ctx"][b], c_b=inp["c"][b]) for b in range(NCORES)]
    res = run_bass_kernel_spmd(nc, in_maps, core_ids=list(range(NCORES)))
    return np.stack([res.results[b]["OUT"] for b in range(NCORES)], 0).astype(np.float32)
```

```python
import math
import numpy as np
from contextlib import ExitStack
import concourse.bass as bass
import concourse.mybir as mybir
from concourse.bass_utils import run_bass_kernel_spmd

F32 = mybir.dt.float32
F32R = mybir.dt.float32r
BF16 = mybir.dt.bfloat16
I32 = mybir.dt.int32
ALU = mybir.AluOpType
AF = mybir.ActivationFunctionType
AX = mybir.AxisListType

T = 2304
TE = 2560
NCTX = 256
NLAT = 2048
DM = 1024
SEGS = [(0, 256, 1), (256, 768, 0), (768, 1280, 0), (1280, 1792, 0), (1792, 2304, 0)]
RMS_EPS = 1e-6


class Tok:
    __slots__ = ("w", "r")

    def __init__(self):
        self.w = None
        self.r = {}


class FW:
    ENG = ("pe", "dve", "act", "pool", "sp")
    NDMA = 8

    def __init__(self, nc, es):
        self.nc = nc
        self.es = es
        self.eng = {"pe": nc.tensor, "dve": nc.vector, "act": nc.scalar,
                    "pool": nc.gpsimd, "sp": nc.sync}
        self.sem = {}
        self.cnt = {}
        for e in self.ENG:
            self.sem[e] = es.enter_context(nc.semaphore("s_" + e))
            self.cnt[e] = 0
        self.dq = {}
        for q in ("sp", "pool", "act"):
            ring = []
            for i in range(self.NDMA):
                k = "d_%s_%d" % (q, i)
                self.sem[k] = es.enter_context(nc.semaphore(k))
                self.cnt[k] = 0
                ring.append(k)
            self.dq[q] = [ring, 0]
        self.seen = {e: {} for e in self.ENG}
        self.attach = True
        self.ninst = 0
        self.uid = 0

    def name(self, p):
        self.uid += 1
        return "%s_%d" % (p, self.uid)

    def _deps(self, reads, writes):
        deps = {}
        for t in reads:
            if t.w is not None and deps.get(t.w[0], 0) < t.w[1]:
                deps[t.w[0]] = t.w[1]
        for t in writes:
            if t.w is not None and deps.get(t.w[0], 0) < t.w[1]:
                deps[t.w[0]] = t.w[1]
            for k, v in t.r.items():
                if deps.get(k, 0) < v:
                    deps[k] = v
        return deps

    def _wait(self, e, deps):
        seen = self.seen[e]
        for k, v in deps.items():
            if seen.get(k, 0) < v:
                self.eng[e].wait_ge(self.sem[k], v)
                seen[k] = v

    def op(self, e, fn, reads=(), writes=()):
        deps = self._deps(reads, writes)
        seen = self.seen[e]
        need = [(k, v) for k, v in deps.items() if seen.get(k, 0) < v]
        att = None
        if need and self.attach:
            att = need.pop()
        for k, v in need:
            self.eng[e].wait_ge(self.sem[k], v)
            seen[k] = v
        inst = fn(self.eng[e])
        if att is not None:
            inst._wait_ge(self.sem[att[0]], att[1])
            seen[att[0]] = att[1]
        self.cnt[e] += 1
        inst.then_inc(self.sem[e], 1)
        v = self.cnt[e]
        for t in reads:
            t.r[e] = v
        for t in writes:
            t.w = (e, v)
            t.r = {}
        self.ninst += 1
        return inst

    def dma(self, q, out, in_, reads=(), writes=(), **kw):
        ring, idx = self.dq[q]
        k = ring[idx % len(ring)]
        self.dq[q][1] = idx + 1
        deps = self._deps(reads, writes)
        if self.cnt[k] > 0:
            deps[k] = max(deps.get(k, 0), self.cnt[k])
        self._wait(q, deps)
        inst = self.eng[q].dma_start(out=out, in_=in_, **kw)
        self.cnt[k] += 16
        inst.then_inc(self.sem[k], 16)
        v = self.cnt[k]
        for t in reads:
            t.r[k] = v
        for t in writes:
            t.w = (k, v)
            t.r = {}
        self.ninst += 1
        return inst

    def barrier(self, engines=None):
        allv = {k: v for k, v in self.cnt.items() if v > 0}
        for e in (engines or self.ENG):
            self._wait(e, allv)


class Pool_:
    def __init__(self, fw, es):
        self.fw = fw
        self.es = es
        self.nc = fw.nc

    def sb(self, shape, dt, name="t"):
        return self.es.enter_context(self.nc.sbuf_tensor(self.fw.name(name), list(shape), dt))

    def ps(self, shape=(128, 512), dt=F32, name="ps"):
        return self.es.enter_context(self.nc.psum_tensor(self.fw.name(name), list(shape), dt))


class Rot:
    def __init__(self, bufs):
        self.bufs = bufs
        self.toks = [Tok() for _ in bufs]
        self.i = 0

    def next(self):
        j = self.i % len(self.bufs)
        self.i += 1
        return self.bufs[j], self.toks[j]


def load_w_bf16(fw, P, W, rows, cols, tok, name="w", q="pool", chunk=2048, stage=None):
    kt = rows // 128
    wb = P.sb([128, kt, cols], BF16, name)
    chunk = min(chunk, cols)
    st = stage or Rot([P.sb([128, chunk], F32, "wstg") for _ in range(3)])
    for k in range(kt):
        for c0 in range(0, cols, chunk):
            n = min(chunk, cols - c0)
            sg, tsg = st.next()
            fw.dma("sp", sg[:, :n], W[k * 128:(k + 1) * 128, c0:c0 + n], writes=[tsg])
            fw.op("pool", lambda e, sg=sg, k=k, c0=c0, n=n: e.tensor_copy(out=wb[:, k, c0:c0 + n], in_=sg[:, :n]), reads=[tsg], writes=[tok])
    return wb


def stage_p0(fw, io):
    nc = fw.nc
    xb = io("x_b", [NLAT, DM], F32, "in")
    cb = io("ctx_b", [NCTX, DM], F32, "in")
    ident = io("ident", [128, 128], F32, "in")
    XT = io("XT", [DM, T], F32, "out")
    with ExitStack() as es:
        P = Pool_(fw, es)
        idt = P.sb([128, 128], F32)
        tid = Tok()
        fw.dma("sp", idt[:], ident, writes=[tid])
        xt = P.sb([128, 8, T], F32)
        txt = Tok()
        xin = Rot([P.sb([128, DM], F32) for _ in range(3)])
        pss = Rot([P.ps() for _ in range(4)])
        for tt in range(18):
            src = cb[tt * 128:(tt + 1) * 128, :] if tt < 2 else xb[(tt - 2) * 128:(tt - 1) * 128, :]
            xi, txi = xin.next()
            fw.dma("sp", xi[:], src, writes=[txi])
            for half in range(2):
                ps, tps = pss.next()
                for k in range(4):
                    kk = half * 4 + k
                    fw.op("pe", lambda e, ps=ps, k=k, kk=kk, xi=xi: e.transpose(
                        ps[:, k * 128:(k + 1) * 128], xi[:, kk * 128:(kk + 1) * 128], idt[:]),
                        reads=[txi, tid], writes=[tps])
                eng = "dve" if half == 0 else "act"
                outap = xt[:, half * 4:half * 4 + 4, tt * 128:(tt + 1) * 128]
                inap = ps[:].rearrange("p (k t) -> p k t", k=4)
                if eng == "dve":
                    fw.op("dve", lambda e, o=outap, i=inap: e.tensor_copy(out=o, in_=i), reads=[tps], writes=[txt])
                else:
                    fw.op("act", lambda e, o=outap, i=inap: e.copy(out=o, in_=i), reads=[tps], writes=[txt])
        tout = Tok()
        for k in range(8):
            fw.dma("sp", XT[k * 128:(k + 1) * 128, :], xt[:, k, :], reads=[txt], writes=[tout])
        fw.barrier()


def make_AB(fw, P, MODs, tmod, g_ap, sh_base, sc_base):
    g = P.sb([128, 8], F32)
    tg = Tok()
    fw.dma("sp", g[:], g_ap.rearrange("(k p) -> p k", p=128), writes=[tg], allow_slow_non_contiguous=True)
    AB = P.sb([128, 2, 2, 8], F32)
    tab = Tok()
    for ic in range(2):
        fw.op("dve", lambda e, ic=ic: e.tensor_scalar(out=AB[:, ic, 0, :], in0=MODs[:, sc_base:sc_base + 8, ic],
                                                      scalar1=1.0, scalar2=None, op0=ALU.add),
              reads=[tmod], writes=[tab])
        fw.op("dve", lambda e, ic=ic: e.tensor_tensor(out=AB[:, ic, 0, :], in0=AB[:, ic, 0, :], in1=g[:], op=ALU.mult),
              reads=[tg, tab], writes=[tab])
        fw.op("dve", lambda e, ic=ic: e.tensor_copy(out=AB[:, ic, 1, :], in_=MODs[:, sh_base:sh_base + 8, ic]),
              reads=[tmod], writes=[tab])
    return AB, tab


def norm_mod_seg(fw, P, st, xs, txs, n, ic, AB, tab, outs, touts):
    sq, ones, tones, psr, rs, tmpr = st["sq"], st["ones"], st["tones"], st["psr"], st["rs"], st["tmpr"]
    tsq, trs = st["tsq"], st["trs"]
    fw.op("act", lambda e: e.activation(out=sq[:, :, :n], in_=xs[:, :, :n], func=AF.Square), reads=[txs], writes=[tsq])
    ps, tps = psr.next()
    for k in range(8):
        fw.op("pe", lambda e, k=k: e.matmul(ps[:, :n], lhsT=ones[:], rhs=sq[:, k, :n], start=(k == 0), stop=(k == 7)),
              reads=[tsq, tones], writes=[tps])
    fw.op("act", lambda e: e.activation(out=rs[:, :n], in_=ps[:, :n], func=AF.Ln, scale=1.0 / DM, bias=st["eps"][:, 0:1]),
          reads=[tps, st["teps"]], writes=[trs])
    fw.op("act", lambda e: e.activation(out=rs[:, :n], in_=rs[:, :n], func=AF.Exp, scale=-0.5), reads=[trs], writes=[trs])
    for k in range(8):
        tmp, ttmp = tmpr.next()
        fw.op("dve", lambda e, k=k, tmp=tmp: e.tensor_tensor(out=tmp[:, :n], in0=xs[:, k, :n], in1=rs[:, :n], op=ALU.mult),
              reads=[txs, trs], writes=[ttmp])
        for o in outs(k):
            fw.op("act", lambda e, k=k, tmp=tmp, o=o: e.activation(out=o, in_=tmp[:, :n], func=AF.Identity,
                                                                   scale=AB[:, ic, 0, k:k + 1], bias=AB[:, ic, 1, k:k + 1]),
                  reads=[ttmp, tab], writes=touts)


def norm_state(fw, P):
    st = {}
    st["sq"] = P.sb([128, 8, 512], BF16)
    st["tsq"] = Tok()
    st["ones"] = P.sb([128, 128], BF16)
    st["tones"] = Tok()
    fw.op("pool", lambda e: e.memset(st["ones"][:], 1.0), writes=[st["tones"]])
    st["eps"] = P.sb([128, 1], F32)
    st["teps"] = Tok()
    fw.op("pool", lambda e: e.memset(st["eps"][:], RMS_EPS), writes=[st["teps"]])
    st["psr"] = Rot([P.ps() for _ in range(2)])
    st["rs"] = P.sb([128, 512], F32)
    st["trs"] = Tok()
    st["tmpr"] = Rot([P.sb([128, 512], F32) for _ in range(2)])
    return st


def stage_p1(fw, io):
    XT = io("XT", [DM, T], F32, "in")
    c_b = io("c_b", [DM], F32, "in")
    c_ctx = io("c_ctx", [DM], F32, "in")
    mod_w = io("mod_w", [DM, 6 * DM], F32, "in")
    mod_b = io("mod_b", [6 * DM], F32, "in")
    n1g = io("norm1_g", [DM], F32, "in")
    w_in = io("w_in", [DM, 1984], F32, "in")
    MOD = io("MOD", [128, 48, 2], F32, "out")
    ZT = io("ZT", [2048, TE], F32, "out")
    VT = io("VT", [T, 128], F32, "out")
    XTv = XT.rearrange("(k p) t -> p k t", p=128)
    with ExitStack() as es:
        P = Pool_(fw, es)
        wstage = Rot([P.sb([128, 2048], F32, "wstg") for _ in range(3)])
        tmw = Tok()
        cc = P.sb([128, 8, 2], F32)
        tcc = Tok()
        fw.dma("sp", cc[:, :, 0], c_b.rearrange("(k p) -> p k", p=128), writes=[tcc], allow_slow_non_contiguous=True)
        fw.dma("sp", cc[:, :, 1], c_ctx.rearrange("(k p) -> p k", p=128), writes=[tcc], allow_slow_non_contiguous=True)
        scb = P.sb([128, 8, 2], BF16)
        tscb = Tok()
        fw.op("act", lambda e: e.activation(out=scb[:], in_=cc[:], func=AF.Silu), reads=[tcc], writes=[tscb])
        mb = P.sb([128, 48], F32)
        tmb = Tok()
        fw.dma("sp", mb[:], mod_b.rearrange("(j p) -> p j", p=128), writes=[tmb], allow_slow_non_contiguous=True)
        MODs = P.sb([128, 48, 2], F32)
        tmod = Tok()
        with ExitStack() as es2:
            P2 = Pool_(fw, es2)
            mwb = load_w_bf16(fw, P2, mod_w, DM, 6 * DM, tmw, "modw", stage=wstage)
            psm = P2.ps([128, 512])
            tpsm = Tok()
            for j in range(48):
                for k in range(8):
                    fw.op("pe", lambda e, j=j, k=k: e.matmul(psm[:, 2 * j:2 * j + 2], lhsT=mwb[:, k, j * 128:(j + 1) * 128],
                                                             rhs=scb[:, k, :], start=(k == 0), stop=(k == 7)),
                          reads=[tmw, tscb], writes=[tpsm])
            for ic in range(2):
                fw.op("dve", lambda e, ic=ic: e.tensor_tensor(
                    out=MODs[:, :, ic], in0=psm[:, 0:96].rearrange("p (j c) -> p j c", c=2)[:, :, ic], in1=mb[:], op=ALU.add),
                    reads=[tpsm, tmb], writes=[tmod])
            fw.barrier()
        tmo = Tok()
        fw.dma("sp", MOD, MODs[:], reads=[tmod], writes=[tmo])
        AB, tab = make_AB(fw, P, MODs, tmod, n1g, 0, 8)
        tw = Tok()
        wb = load_w_bf16(fw, P, w_in, DM, 1984, tw, "win", stage=wstage)
        hT = P.sb([128, 8, TE], BF16)
        thT = Tok()
        st = norm_state(fw, P)
        xr = Rot([P.sb([128, 8, 512], F32) for _ in range(2)])
        for (lo, hi, ic) in SEGS:
            n = hi - lo
            xs, txs = xr.next()
            fw.dma("sp", xs[:, :, :n], XTv[:, :, lo:hi], writes=[txs])

            def outs(k, lo=lo, hi=hi, ic=ic):
                o = [hT[:, k, lo:hi]]
                if ic:
                    o.append(hT[:, k, T + lo:T + hi])
                return o
            norm_mod_seg(fw, P, st, xs, txs, n, ic, AB, tab, outs, [thT])
        psr = Rot([P.ps() for _ in range(4)])
        stg = Rot([P.sb([128, 512], F32) for _ in range(4)])
        tz = Tok()
        cnt = 0
        for nt in range(16):
            if nt == 7:
                continue
            M = 64 if nt == 15 else 128
            for cc_ in range(5):
                c0 = cc_ * 512
                ps, tps = psr.next()
                for k in range(8):
                    fw.op("pe", lambda e, ps=ps, k=k, nt=nt, M=M, c0=c0: e.matmul(
                        ps[0:M, :], lhsT=wb[:, k, nt * 128:nt * 128 + M], rhs=hT[:, k, c0:c0 + 512],
                        start=(k == 0), stop=(k == 7)), reads=[tw, thT], writes=[tps])
                sg, tsg = stg.next()
                if cnt % 2 == 0:
                    fw.op("dve", lambda e, sg=sg, ps=ps, M=M: e.tensor_copy(out=sg[0:M, :], in_=ps[0:M, :]), reads=[tps], writes=[tsg])
                else:
                    fw.op("act", lambda e, sg=sg, ps=ps, M=M: e.copy(out=sg[0:M, :], in_=ps[0:M, :]), reads=[tps], writes=[tsg])
                cnt += 1
                fw.dma("sp", ZT[nt * 128:nt * 128 + M, c0:c0 + 512], sg[0:M, :], reads=[tsg], writes=[tz])
        for tt in range(18):
            ps, tps = psr.next()
            for k in range(8):
                fw.op("pe", lambda e, ps=ps, k=k, tt=tt: e.matmul(
                    ps[:, 0:128], lhsT=hT[:, k, tt * 128:(tt + 1) * 128], rhs=wb[:, k, 896:1024],
                    start=(k == 0), stop=(k == 7)), reads=[tw, thT], writes=[tps])
            sg, tsg = stg.next()
            fw.op("dve", lambda e, sg=sg, ps=ps: e.tensor_copy(out=sg[:, 0:128], in_=ps[:, 0:128]), reads=[tps], writes=[tsg])
            fw.dma("sp", VT[tt * 128:(tt + 1) * 128, :], sg[:, 0:128], reads=[tsg], writes=[tz])
        fw.barrier()


def build_program(stage_fns):
    nc = bass.Bass("TRN2", target_bir_lowering=False)
    decl = {}

    def io(name, shape, dt, role):
        if name in decl:
            return decl[name][0]
        kind = "ExternalInput" if role == "in" else "ExternalOutput"
        ap = nc.dram_tensor(name, list(shape), dt, kind=kind).ap()
        decl[name] = (ap, role, shape)
        return ap
    with ExitStack() as es:
        fw = FW(nc, es)
        for fn in stage_fns:
            fn(fw, io)
        fw.barrier()
    return nc, decl, fw


_PROG_CACHE = {}


def run_stage(key, stage_fns, in_maps, ncores):
    if key not in _PROG_CACHE:
        _PROG_CACHE[key] = build_program(stage_fns)
    nc, decl, fw = _PROG_CACHE[key]
    res = run_bass_kernel_spmd(nc, in_maps, core_ids=list(range(ncores)))
    return res.results


def rstd_from_ps(fw, rs, trs, ps, tps, n, scale, epsap, teps, rows=128):
    fw.op("act", lambda e: e.activation(out=rs[0:rows, :n], in_=ps[0:rows, :n], func=AF.Ln, scale=scale, bias=epsap),
          reads=[tps, teps], writes=[trs])
    fw.op("act", lambda e: e.activation(out=rs[0:rows, :n], in_=rs[0:rows, :n], func=AF.Exp, scale=-0.5), reads=[trs], writes=[trs])


def stage_p3(fw, io):
    ZT = io("ZT", [2048, TE], F32, "in")
    VT = io("VT", [T, 128], F32, "in")
    qn_g = io("att_qn_g", [64], F32, "in")
    kn_g = io("att_kn_g", [64], F32, "in")
    og = io("att_out_g", [512], F32, "in")
    POS = io("POS", [128, NLAT], F32, "in")
    CST = io("CST", [128, 260], F32, "in")
    ATT = io("ATT_T", [512, T], F32, "out")
    with ExitStack() as es:
        P = Pool_(fw, es)
        cst = P.sb([128, 260], F32)
        tc = Tok()
        fw.dma("sp", cst[:], CST, writes=[tc])
        permb = P.sb([128, 128], BF16)
        bdb = P.sb([128, 128], BF16)
        fw.op("dve", lambda e: e.tensor_copy(out=permb[:], in_=cst[:, 1:129]), reads=[tc], writes=[tc])
        fw.op("dve", lambda e: e.tensor_copy(out=bdb[:], in_=cst[:, 129:257]), reads=[tc], writes=[tc])
        eps = P.sb([128, 2], F32)
        teps = Tok()
        fw.op("pool", lambda e: e.memset(eps[:, 0:1], RMS_EPS), writes=[teps])
        fw.op("pool", lambda e: e.memset(eps[:, 1:2], 0.0), writes=[teps])
        gq = P.sb([128, 2], F32)
        tg = Tok()
        for h in range(2):
            fw.dma("sp", gq[h * 64:(h + 1) * 64, 0:1], qn_g.rearrange("(p o) -> p o", o=1), writes=[tg])
            fw.dma("sp", gq[h * 64:(h + 1) * 64, 1:2], kn_g.rearrange("(p o) -> p o", o=1), writes=[tg])
        cos = P.sb([128, NLAT], F32)
        sin = P.sb([128, NLAT], F32)
        ttab = Tok()
        with ExitStack() as es2:
            P2 = Pool_(fw, es2)
            ang = P2.sb([128, NLAT], F32)
            tmpf = P2.sb([128, NLAT], F32)
            tmpi = P2.sb([128, NLAT], I32)
            ta = Tok()
            fw.dma("sp", ang[:], POS, writes=[ta])
            fw.op("dve", lambda e: e.tensor_scalar(out=ang[:], in0=ang[:], scalar1=cst[:, 0:1], scalar2=None, op0=ALU.mult),
                  reads=[ta, tc], writes=[ta])
            for (tab, off) in ((sin, 0.0), (cos, math.pi / 2)):
                fw.op("dve", lambda e, off=off: e.tensor_scalar(out=tmpi[:], in0=ang[:], scalar1=off, scalar2=1.0 / (2 * math.pi),
                                                                op0=ALU.add, op1=ALU.mult), reads=[ta], writes=[ta])
                fw.op("dve", lambda e: e.tensor_copy(out=tmpf[:], in_=tmpi[:]), reads=[ta], writes=[ta])
                fw.op("dve", lambda e: e.scalar_tensor_tensor(out=tmpf[:], in0=tmpf[:], scalar=-2 * math.pi, in1=ang[:],
                                                              op0=ALU.mult, op1=ALU.add), reads=[ta], writes=[ta])
                fw.op("dve", lambda e, off=off: e.tensor_scalar(out=tmpf[:], in0=tmpf[:], scalar1=off, scalar2=math.pi,
                                                                op0=ALU.add, op1=ALU.min), reads=[ta], writes=[ta])
                fw.op("dve", lambda e: e.tensor_scalar(out=tmpf[:], in0=tmpf[:], scalar1=-math.pi, scalar2=None, op0=ALU.max),
                      reads=[ta], writes=[ta])
                fw.op("act", lambda e, tab=tab: e.activation(out=tab[:], in_=tmpf[:], func=AF.Sin), reads=[ta], writes=[ttab])
            fw.barrier()
        qb = P.sb([128, 4, T], BF16)
        kd = P.sb([128, 2, T], BF16)
        tq = Tok()
        with ExitStack() as es2:
            P2 = Pool_(fw, es2)
            raw = Rot([P2.sb([128, 512], F32) for _ in range(2)])
            sqr = Rot([P2.sb([128, 512], BF16) for _ in range(2)])
            psr = Rot([P2.ps() for _ in range(2)])
            psr2 = Rot([P2.ps() for _ in range(2)])
            rsr = Rot([P2.sb([128, 512], F32) for _ in range(2)])
            nbr = Rot([P2.sb([128, 512], BF16) for _ in range(2)])
            t1r = Rot([P2.sb([128, 512], F32) for _ in range(2)])
            t2r = Rot([P2.sb([128, 512], F32) for _ in range(2)])
            items = [("q", j) for j in range(4)] + [("k", g) for g in range(2)]
            for (kind, j) in items:
                for (lo, hi, ic) in SEGS:
                    n = hi - lo
                    rw, trw = raw.next()
                    if kind == "q":
                        fw.dma("sp", rw[:, :n], ZT[256 + j * 128:256 + (j + 1) * 128, lo:hi], writes=[trw])
                        gcol = 0
                        dst = qb[:, j, lo:hi]
                    else:
                        for h in range(2):
                            fw.dma("sp", rw[h * 64:(h + 1) * 64, :n], ZT[768 + j * 64:768 + (j + 1) * 64, lo:hi], writes=[trw])
                        gcol = 1
                        dst = kd[:, j, lo:hi]
                    sq, tsq = sqr.next()
                    fw.op("act", lambda e, sq=sq, rw=rw, n=n: e.activation(out=sq[:, :n], in_=rw[:, :n], func=AF.Square), reads=[trw], writes=[tsq])
                    ps, tps = psr.next()
                    fw.op("pe", lambda e, ps=ps, sq=sq, n=n: e.matmul(ps[:, :n], lhsT=bdb[:], rhs=sq[:, :n], start=True, stop=True),
                          reads=[tsq, tc], writes=[tps])
                    rs, trs = rsr.next()
                    rstd_from_ps(fw, rs, trs, ps, tps, n, 1.0, eps[:, 0:1], teps)
                    t1, tt1 = t1r.next()
                    fw.op("dve", lambda e, t1=t1, rw=rw, rs=rs, n=n, gcol=gcol: e.scalar_tensor_tensor(
                        out=t1[:, :n], in0=rw[:, :n], scalar=gq[:, gcol:gcol + 1], in1=rs[:, :n], op0=ALU.mult, op1=ALU.mult),
                        reads=[trw, trs, tg], writes=[tt1])
                    if ic:
                        fw.op("act", lambda e, dst=dst, t1=t1, n=n: e.copy(out=dst, in_=t1[:, :n]), reads=[tt1], writes=[tq])
                        continue
                    nb, tnb = nbr.next()
                    fw.op("act", lambda e, nb=nb, t1=t1, n=n: e.copy(out=nb[:, :n], in_=t1[:, :n]), reads=[tt1], writes=[tnb])
                    ps2, tps2 = psr2.next()
                    fw.op("pe", lambda e, ps2=ps2, nb=nb, n=n: e.matmul(ps2[:, :n], lhsT=permb[:], rhs=nb[:, :n], start=True, stop=True),
                          reads=[tnb, tc], writes=[tps2])
                    p0 = lo - NCTX
                    t2, tt2 = t2r.next()
                    fw.op("dve", lambda e, t2=t2, ps2=ps2, n=n, p0=p0: e.tensor_tensor(out=t2[:, :n], in0=ps2[:, :n], in1=sin[:, p0:p0 + n], op=ALU.mult),
                          reads=[tps2, ttab], writes=[tt2])
                    fw.op("pool", lambda e, t1=t1, nb=nb, n=n, p0=p0: e.tensor_tensor(out=t1[:, :n], in0=nb[:, :n], in1=cos[:, p0:p0 + n], op=ALU.mult),
                          reads=[tnb, ttab, tt1], writes=[tt1])
                    fw.op("pool", lambda e, dst=dst, t1=t1, t2=t2, n=n: e.tensor_tensor(out=dst, in0=t1[:, :n], in1=t2[:, :n], op=ALU.add),
                          reads=[tt1, tt2], writes=[tq])
            fw.barrier()
        va = P.sb([128, 18, 2, 128], BF16)
        tva = Tok()
        fw.op("pool", lambda e: e.memset(va[:], 1.0), writes=[tva])
        for g in range(2):
            fw.dma("pool", va[:, :, g, 0:64], VT.rearrange("(t p) c -> p t c", p=128)[:, :, g * 64:(g + 1) * 64], writes=[tva])
        att = P.sb([128, 4, T], F32)
        tatt = Tok()
        pss = Rot([P.ps() for _ in range(3)])
        pso = Rot([P.ps() for _ in range(2)])
        ptr = Rot([P.sb([128, 512], BF16) for _ in range(3)])
        rcr = Rot([P.sb([64, 512], F32) for _ in range(2)])
        jobs = [(0, 256, 0, 2)] + [(256 + 512 * i, 768 + 512 * i, 0, 18) for i in range(4)]
        for h in range(8):
            g = h // 4
            jt, r0 = h // 2, (h % 2) * 64
            for (qlo, qhi, k0, k1) in jobs:
                n = qhi - qlo
                po, tpo = pso.next()
                for kt in range(k0, k1):
                    ps, tps = pss.next()
                    fw.op("pe", lambda e, ps=ps, kt=kt, g=g, jt=jt, r0=r0, qlo=qlo, qhi=qhi, n=n: e.matmul(
                        ps[:, :n], lhsT=kd[r0:r0 + 64, g, kt * 128:(kt + 1) * 128], rhs=qb[r0:r0 + 64, jt, qlo:qhi],
                        start=True, stop=True), reads=[tq], writes=[tps])
                    pt, tpt = ptr.next()
                    fw.op("act", lambda e, pt=pt, ps=ps, n=n: e.activation(out=pt[:, :n], in_=ps[:, :n], func=AF.Exp, scale=0.125),
                          reads=[tps], writes=[tpt])
                    fw.op("pe", lambda e, po=po, pt=pt, kt=kt, g=g, n=n, k0=k0, k1=k1: e.matmul(
                        po[:, :n], lhsT=va[:, kt, g, :], rhs=pt[:, :n], start=(kt == k0), stop=(kt == k1 - 1)),
                        reads=[tpt, tva], writes=[tpo])
                rc, trc = rcr.next()
                fw.op("dve", lambda e, rc=rc, po=po, n=n: e.reciprocal(out=rc[:, :n], in_=po[64:128, :n]), reads=[tpo], writes=[trc])
                fw.op("dve", lambda e, rc=rc, po=po, n=n, jt=jt, r0=r0, qlo=qlo, qhi=qhi: e.tensor_tensor(
                    out=att[r0:r0 + 64, jt, qlo:qhi], in0=po[0:64, :n], in1=rc[:, :n], op=ALU.mult),
                    reads=[tpo, trc], writes=[tatt])
        ogs = P.sb([128, 4], F32)
        tog = Tok()
        fw.dma("sp", ogs[:], og.rearrange("(k p) -> p k", p=128), writes=[tog], allow_slow_non_contiguous=True)
        ones = P.sb([128, 128], BF16)
        fw.op("pool", lambda e: e.memset(ones[:], 1.0), writes=[tog])
        sq4 = P.sb([128, 4, 512], BF16)
        tsq4 = Tok()
        rs = P.sb([128, 512], F32)
        trs = Tok()
        stg = Rot([P.sb([128, 512], F32) for _ in range(3)])
        tout = Tok()
        for (lo, hi, ic) in SEGS:
            n = hi - lo
            fw.op("act", lambda e, lo=lo, hi=hi, n=n: e.activation(out=sq4[:, :, :n], in_=att[:, :, lo:hi], func=AF.Square), reads=[tatt], writes=[tsq4])
            ps, tps = pss.next()
            for k in range(4):
                fw.op("pe", lambda e, ps=ps, k=k, n=n: e.matmul(ps[:, :n], lhsT=ones[:], rhs=sq4[:, k, :n], start=(k == 0), stop=(k == 3)),
                      reads=[tsq4, tog], writes=[tps])
            rstd_from_ps(fw, rs, trs, ps, tps, n, 1.0 / 512, eps[:, 0:1], teps)
            for k in range(4):
                sg, tsg = stg.next()
                fw.op("dve", lambda e, sg=sg, k=k, lo=lo, hi=hi, n=n: e.scalar_tensor_tensor(
                    out=sg[:, :n], in0=att[:, k, lo:hi], scalar=ogs[:, k:k + 1], in1=rs[:, :n], op0=ALU.mult, op1=ALU.mult),
                    reads=[tatt, trs, tog], writes=[tsg])
                fw.dma("sp", ATT[k * 128:(k + 1) * 128, lo:hi], sg[:, :n], reads=[tsg], writes=[tout])
        fw.barrier()


def host_consts():
    pos = np.zeros((128, NLAT), np.float32)
    inv = np.zeros((128,), np.float32)
    tok = np.arange(NLAT)
    for p in range(128):
        d = p % 64
        pos[p] = (tok // 64) if d < 32 else (tok % 64)
        inv[p] = 10000.0 ** (-(d % 16) / 16.0)
    cst = np.zeros((128, 260), np.float32)
    cst[:, 0] = inv
    perm = np.zeros((128, 128), np.float32)
    for m in range(128):
        d = m % 32
        if d < 16:
            perm[m + 16, m] = -1.0
        else:
            perm[m - 16, m] = 1.0
    cst[:, 1:129] = perm
    bd = np.zeros((128, 128), np.float32)
    bd[:64, :64] = 1.0 / 64
    bd[64:, 64:] = 1.0 / 64
    cst[:, 129:257] = bd
    return pos, cst


def stage_p5a(fw, io):
    XT = io("XT", [DM, T], F32, "in")
    S5T = io("S5T", [256, T], F32, "in")
    ATT = io("ATT_T", [512, T], F32, "in")
    RWT = io("RWT", [256, T], F32, "in")
    MOD = io("MOD", [128, 48, 2], F32, "in")
    w_out = io("w_out", [DM, DM], F32, "in")
    XT1 = io("XT1", [DM, T], F32, "out")
    XTv = XT.rearrange("(k p) t -> p k t", p=128)
    with ExitStack() as es:
        P = Pool_(fw, es)
        MODs = P.sb([128, 48, 2], F32)
        tmod = Tok()
        fw.dma("sp", MODs[:], MOD, writes=[tmod])
        tw = Tok()
        wb = load_w_bf16(fw, P, w_out, DM, DM, tw, "wout")
        cat = P.sb([128, 8, T], BF16)
        tcat = Tok()
        cstg = Rot([P.sb([128, T], F32, "cstg") for _ in range(2)])
        for k in range(8):
            src = S5T[k * 128:(k + 1) * 128, :] if k < 2 else (ATT[(k - 2) * 128:(k - 1) * 128, :] if k < 6 else RWT[(k - 6) * 128:(k - 5) * 128, :])
            sg, tsg = cstg.next()
            fw.dma("sp", sg[:], src, writes=[tsg])
            fw.op("pool", lambda e, sg=sg, k=k: e.tensor_copy(out=cat[:, k, :], in_=sg[:]), reads=[tsg], writes=[tcat])
        xr = Rot([P.sb([128, 8, 512], F32) for _ in range(2)])
        x1r = Rot([P.sb([128, 8, 512], F32) for _ in range(2)])
        psr = Rot([P.ps() for _ in range(4)])
        to1, to2 = Tok(), Tok()
        for (lo, hi, ic) in SEGS:
            n = hi - lo
            xs, txs = xr.next()
            fw.dma("sp", xs[:, :, :n], XTv[:, :, lo:hi], writes=[txs])
            x1, tx1 = x1r.next()
            for d in range(8):
                ps, tps = psr.next()
                for k in range(8):
                    fw.op("pe", lambda e, ps=ps, k=k, d=d, lo=lo, hi=hi, n=n: e.matmul(
                        ps[:, :n], lhsT=wb[:, k, d * 128:(d + 1) * 128], rhs=cat[:, k, lo:hi], start=(k == 0), stop=(k == 7)),
                        reads=[tw, tcat], writes=[tps])
                fw.op("dve", lambda e, ps=ps, d=d, n=n, ic=ic, x1=x1, xs=xs: e.scalar_tensor_tensor(
                    out=x1[:, d, :n], in0=ps[:, :n], scalar=MODs[:, 16 + d, ic:ic + 1], in1=xs[:, d, :n], op0=ALU.mult, op1=ALU.add),
                    reads=[tps, txs, tmod], writes=[tx1])
            for k in range(8):
                fw.dma("sp", XT1[k * 128:(k + 1) * 128, lo:hi], x1[:, k, :n], reads=[tx1], writes=[to1])
        fw.barrier()


def stage_p5b(fw, io):
    XT1 = io("XT1", [DM, T], F32, "in")
    n2g = io("norm2_g", [DM], F32, "in")
    MOD = io("MOD", [128, 48, 2], F32, "in")
    up = io("ffn_up", [44, 128, DM], F32, "in")
    cw = io("ffn_conv_w", [3, 5632], F32, "in")
    cb = io("ffn_conv_b", [5632], F32, "in")
    down = io("ffn_down", [2816, DM], F32, "in")
    XT2 = io("XT2", [DM, T], F32, "out")
    X1v = XT1.rearrange("(k p) t -> p k t", p=128)
    X2v = XT2.rearrange("(k p) t -> p k t", p=128)
    with ExitStack() as es:
        P = Pool_(fw, es)
        MODs = P.sb([128, 48, 2], F32)
        tmod = Tok()
        fw.dma("sp", MODs[:], MOD, writes=[tmod])
        cws = P.sb([128, 44, 3], F32)
        cbs = P.sb([128, 44], F32)
        tcw = Tok()
        for w in range(3):
            fw.dma("sp", cws[:, :, w], cw[w].rearrange("(j p) -> p j", p=128), writes=[tcw], allow_slow_non_contiguous=True)
        fw.dma("sp", cbs[:], cb.rearrange("(j p) -> p j", p=128), writes=[tcw], allow_slow_non_contiguous=True)
        h2 = P.sb([128, 8, T], BF16)
        th2 = Tok()
        AB, tab = make_AB(fw, P, MODs, tmod, n2g, 24, 32)
        with ExitStack() as es2:
            P2 = Pool_(fw, es2)
            st = norm_state(fw, P2)
            xr0 = Rot([P2.sb([128, 8, 512], F32) for _ in range(2)])
            for (lo, hi, ic) in SEGS:
                n = hi - lo
                xs, txs = xr0.next()
                fw.dma("sp", xs[:, :, :n], X1v[:, :, lo:hi], writes=[txs])
                norm_mod_seg(fw, P2, st, xs, txs, n, ic, AB, tab, lambda k, lo=lo, hi=hi: [h2[:, k, lo:hi]], [th2])
            fw.barrier()
        hid = P.sb([128, 11, T], BF16)
        thid = Tok()
        dwb = P.sb([128, 11, DM], BF16)
        tdw = Tok()
        urot = [Rot([P.sb([128, T], F32) for _ in range(2)]) for _ in range(2)]
        y = [P.sb([128, T], F32) for _ in range(2)]
        ty = [Tok(), Tok()]
        psr = Rot([P.ps() for _ in range(4)])
        tx2 = Tok()
        RANGES = [(0, NCTX), (NCTX, T)]
        GROUPS = [(0, 2), (2, 2), (4, 2), (6, 2), (8, 2), (10, 1)]
        for half in range(2):
            with ExitStack() as esu:
                Pu = Pool_(fw, esu)
                ustg = Rot([Pu.sb([128, 8, 256], F32, "ustg") for _ in range(2)])
                for jj in range(11):
                    r0 = (half * 11 + jj) * 128
                    sg, tsg = ustg.next()
                    sgv = sg[:].rearrange("p k n -> p (k n)")[:, 0:DM]
                    fw.dma("sp", sgv, down[r0:r0 + 128, :], writes=[tsg])
                    fw.op("pool", lambda e, sgv=sgv, jj=jj: e.tensor_copy(out=dwb[:, jj, :], in_=sgv), reads=[tsg], writes=[tdw])
                ubr = Rot([Pu.sb([128, 8, 2, 256], BF16, "ub") for _ in range(2)])

                def issue_load(g, half=half):
                    jj0, ng = GROUPS[g]
                    ub, tub = ubr.next()
                    for wh in range(2):
                        sg, tsg = ustg.next()
                        sgv = sg[:].rearrange("p k (j c) -> p (k j c)", j=2).rearrange("p (j k c) -> p j k c", j=2, k=8)
                        for jl in range(ng):
                            jt = wh * 22 + half * 11 + jj0 + jl
                            fw.dma("sp", sgv[:, jl].rearrange("p k c -> p (k c)"), up[jt], writes=[tsg])
                            fw.op("pool", lambda e, sgv=sgv, ub=ub, wh=wh, jl=jl: e.tensor_copy(out=ub[:, :, wh, jl * 128:(jl + 1) * 128], in_=sgv[:, jl]),
                                  reads=[tsg], writes=[tub])
                    return ub, tub
                loaded = issue_load(0)
                for g, (jj0, ng) in enumerate(GROUPS):
                    ub, tub = loaded
                    if g + 1 < len(GROUPS):
                        loaded = issue_load(g + 1)
                    for jl in range(ng):
                        jj = jj0 + jl
                        j = half * 11 + jj
                        for wh in range(2):
                            jc = wh * 22 + j
                            ucur, tucur = urot[wh].next()
                            for si, (lo, hi, ic) in enumerate(SEGS):
                                n = hi - lo
                                ps, tps = psr.next()
                                for k in range(8):
                                    fw.op("pe", lambda e, ps=ps, k=k, wh=wh, ub=ub, jl=jl, lo=lo, hi=hi, n=n: e.matmul(
                                        ps[:, :n], lhsT=ub[:, k, wh, jl * 128:(jl + 1) * 128], rhs=h2[:, k, lo:hi], start=(k == 0), stop=(k == 7)),
                                        reads=[tub, th2], writes=[tps])
                                fw.op("act", lambda e, ps=ps, ucur=ucur, lo=lo, hi=hi, n=n: e.copy(out=ucur[:, lo:hi], in_=ps[:, :n]),
                                      reads=[tps], writes=[tucur])
                            fw.op("act", lambda e, wh=wh, jc=jc, ucur=ucur: e.activation(out=y[wh][:], in_=ucur[:], func=AF.Identity,
                                                                                        scale=cws[:, jc, 1:2], bias=cbs[:, jc:jc + 1]),
                                  reads=[tucur, tcw], writes=[ty[wh]])
                            for (lo, hi) in RANGES:
                                fw.op("dve", lambda e, wh=wh, jc=jc, lo=lo, hi=hi, ucur=ucur: e.scalar_tensor_tensor(
                                    out=y[wh][:, lo + 1:hi], in0=ucur[:, lo:hi - 1], scalar=cws[:, jc, 0:1], in1=y[wh][:, lo + 1:hi],
                                    op0=ALU.mult, op1=ALU.add), reads=[tucur, tcw, ty[wh]], writes=[ty[wh]])
                                fw.op("dve", lambda e, wh=wh, jc=jc, lo=lo, hi=hi, ucur=ucur: e.scalar_tensor_tensor(
                                    out=y[wh][:, lo:hi - 1], in0=ucur[:, lo + 1:hi], scalar=cws[:, jc, 2:3], in1=y[wh][:, lo:hi - 1],
                                    op0=ALU.mult, op1=ALU.add), reads=[tucur, tcw, ty[wh]], writes=[ty[wh]])
                        fw.op("act", lambda e: e.activation(out=y[0][:], in_=y[0][:], func=AF.Silu), reads=[ty[0]], writes=[ty[0]])
                        fw.op("dve", lambda e, jj=jj: e.tensor_tensor(out=hid[:, jj, :], in0=y[0][:], in1=y[1][:], op=ALU.mult),
                              reads=[ty[0], ty[1]], writes=[thid])
                fw.barrier()
            with ExitStack() as esd:
                Pd = Pool_(fw, esd)
                xr = Rot([Pd.sb([128, 8, 512], F32) for _ in range(2)])
                for (lo, hi, ic) in SEGS:
                    n = hi - lo
                    xs, txs = xr.next()
                    src = X1v if half == 0 else X2v
                    fw.dma("sp", xs[:, :, :n], src[:, :, lo:hi], reads=([tx2] if half else []), writes=[txs])
                    for d in range(8):
                        ps, tps = psr.next()
                        for jj in range(11):
                            fw.op("pe", lambda e, ps=ps, jj=jj, d=d, lo=lo, hi=hi, n=n: e.matmul(
                                ps[:, :n], lhsT=dwb[:, jj, d * 128:(d + 1) * 128], rhs=hid[:, jj, lo:hi], start=(jj == 0), stop=(jj == 10)),
                                reads=[tdw, thid], writes=[tps])
                        fw.op("dve", lambda e, ps=ps, d=d, n=n, ic=ic, xs=xs: e.scalar_tensor_tensor(
                            out=xs[:, d, :n], in0=ps[:, :n], scalar=MODs[:, 40 + d, ic:ic + 1], in1=xs[:, d, :n], op0=ALU.mult, op1=ALU.add),
                            reads=[tps, tmod, txs], writes=[txs])
                    for k in range(8):
                        fw.dma("sp", XT2[k * 128:(k + 1) * 128, lo:hi], xs[:, k, :n], reads=[txs], writes=[tx2])
                fw.barrier()


def stage_p6(fw, io):
    XT = io("XT", [DM, T], F32, "in")
    fg = io("final_g", [DM], F32, "in")
    ident = io("ident", [128, 128], F32, "in")
    OUT = io("OUT", [NLAT, DM], F32, "out")
    XTv = XT.rearrange("(k p) t -> p k t", p=128)
    with ExitStack() as es:
        P = Pool_(fw, es)
        idt = P.sb([128, 128], F32)
        tid = Tok()
        fw.dma("sp", idt[:], ident, writes=[tid])
        g = P.sb([128, 8], F32)
        fw.dma("sp", g[:], fg.rearrange("(k p) -> p k", p=128), writes=[tid], allow_slow_non_contiguous=True)
        st = norm_state(fw, P)
        xr = Rot([P.sb([128, 8, 512], F32) for _ in range(2)])
        yr = Rot([P.sb([128, 8, 512], F32) for _ in range(2)])
        psr = Rot([P.ps() for _ in range(4)])
        orr = Rot([P.sb([128, DM], F32) for _ in range(3)])
        tout = Tok()
        for (lo, hi, ic) in SEGS[1:]:
            n = hi - lo
            xs, txs = xr.next()
            fw.dma("sp", xs[:, :, :n], XTv[:, :, lo:hi], writes=[txs])
            sq, ones = st["sq"], st["ones"]
            fw.op("act", lambda e, xs=xs: e.activation(out=sq[:], in_=xs[:], func=AF.Square), reads=[txs], writes=[st["tsq"]])
            ps, tps = st["psr"].next()
            for k in range(8):
                fw.op("pe", lambda e, ps=ps, k=k: e.matmul(ps[:], lhsT=ones[:], rhs=sq[:, k, :], start=(k == 0), stop=(k == 7)),
                      reads=[st["tsq"], st["tones"]], writes=[tps])
            rstd_from_ps(fw, st["rs"], st["trs"], ps, tps, n, 1.0 / DM, st["eps"][:, 0:1], st["teps"])
            ys, tys = yr.next()
            for k in range(8):
                fw.op("dve", lambda e, ys=ys, xs=xs, k=k: e.scalar_tensor_tensor(
                    out=ys[:, k, :], in0=xs[:, k, :], scalar=g[:, k:k + 1], in1=st["rs"][:], op0=ALU.mult, op1=ALU.mult),
                    reads=[txs, st["trs"], tid], writes=[tys])
            for blk in range(4):
                ot, tot = orr.next()
                for half in range(2):
                    ps2, tps2 = psr.next()
                    for k in range(4):
                        kk = half * 4 + k
                        fw.op("pe", lambda e, ps2=ps2, k=k, kk=kk, ys=ys, blk=blk: e.transpose(
                            ps2[:, k * 128:(k + 1) * 128], ys[:, kk, blk * 128:(blk + 1) * 128], idt[:]),
                            reads=[tys, tid], writes=[tps2])
                    if half == 0:
                        fw.op("dve", lambda e, ot=ot, ps2=ps2: e.tensor_copy(out=ot[:, 0:512], in_=ps2[:]), reads=[tps2], writes=[tot])
                    else:
                        fw.op("act", lambda e, ot=ot, ps2=ps2: e.copy(out=ot[:, 512:1024], in_=ps2[:]), reads=[tps2], writes=[tot])
                r0 = lo - NCTX + blk * 128
                fw.dma("sp", OUT[r0:r0 + 128, :], ot[:], reads=[tot], writes=[tout])
        fw.barrier()


def sin_reduced(fw, P, out, src, tsrc, shape, off, tout):
    ti = P.sb(shape, I32)
    tf = P.sb(shape, F32)
    tt = Tok()
    fw.op("dve", lambda e: e.tensor_scalar(out=ti[:], in0=src, scalar1=off, scalar2=1.0 / (2 * math.pi), op0=ALU.add, op1=ALU.mult),
          reads=[tsrc], writes=[tt])
    fw.op("dve", lambda e: e.tensor_copy(out=tf[:], in_=ti[:]), reads=[tt], writes=[tt])
    fw.op("dve", lambda e: e.scalar_tensor_tensor(out=tf[:], in0=tf[:], scalar=-2 * math.pi, in1=src, op0=ALU.mult, op1=ALU.add),
          reads=[tt, tsrc], writes=[tt])
    fw.op("dve", lambda e: e.tensor_scalar(out=tf[:], in0=tf[:], scalar1=off, scalar2=math.pi, op0=ALU.add, op1=ALU.min), reads=[tt], writes=[tt])
    fw.op("dve", lambda e: e.tensor_scalar(out=tf[:], in0=tf[:], scalar1=-math.pi, scalar2=None, op0=ALU.max), reads=[tt], writes=[tt])
    fw.op("act", lambda e: e.activation(out=out, in_=tf[:], func=AF.Sin), reads=[tt], writes=[tout])


def stage_p2(fw, io):
    ZT = io("ZT", [2048, TE], F32, "in")
    a_re = io("s5_a_re", [2, 16, 64], F32, "in")
    a_im = io("s5_a_im", [2, 16, 64], F32, "in")
    lstep = io("s5_log_step", [2, 16], F32, "in")
    b_re = io("s5_b_re", [2, 16, 64, 16], F32, "in")
    b_im = io("s5_b_im", [2, 16, 64, 16], F32, "in")
    c_re = io("s5_c_re", [2, 16, 16, 64], F32, "in")
    c_im = io("s5_c_im", [2, 16, 16, 64], F32, "in")
    dsk = io("s5_d", [256], F32, "in")
    glu_w = io("s5_glu_w", [256, 256], F32, "in")
    glu_b = io("s5_glu_b", [256], F32, "in")
    out_g = io("s5_out_g", [256], F32, "in")
    S5T = io("S5T", [256, T], F32, "out")
    N = T
    with ExitStack() as es:
        P = Pool_(fw, es)
        are = P.sb([128, 2, 8], F32)
        aim = P.sb([128, 2, 8], F32)
        lst = P.sb([128, 2, 8], F32)
        tpar = Tok()
        for di in range(2):
            fw.dma("sp", are[:, di, :], a_re[di].rearrange("(s g) p -> (g p) s", g=2), writes=[tpar], allow_slow_non_contiguous=True)
            fw.dma("sp", aim[:, di, :], a_im[di].rearrange("(s g) p -> (g p) s", g=2), writes=[tpar], allow_slow_non_contiguous=True)
            for g2 in range(2):
                fw.dma("sp", lst[g2 * 64:(g2 + 1) * 64, di:di + 1, :],
                       lstep[di].rearrange("(s g) -> g s", g=2)[g2:g2 + 1, :].partition_broadcast(64), writes=[tpar],
                       allow_slow_non_contiguous=True)
        sh = [128, 2, 8]
        step = P.sb(sh, F32)
        fw.op("act", lambda e: e.activation(out=step[:], in_=lst[:], func=AF.Exp), reads=[tpar], writes=[tpar])
        er = P.sb(sh, F32)
        th = P.sb(sh, F32)
        fw.op("dve", lambda e: e.tensor_tensor(out=er[:], in0=are[:], in1=step[:], op=ALU.mult), reads=[tpar], writes=[tpar])
        fw.op("act", lambda e: e.activation(out=er[:], in_=er[:], func=AF.Exp), reads=[tpar], writes=[tpar])
        fw.op("dve", lambda e: e.tensor_tensor(out=th[:], in0=aim[:], in1=step[:], op=ALU.mult), reads=[tpar], writes=[tpar])
        sn = P.sb(sh, F32)
        cs = P.sb(sh, F32)
        ttrig = Tok()
        sin_reduced(fw, P, sn[:], th[:], tpar, sh, 0.0, ttrig)
        sin_reduced(fw, P, cs[:], th[:], tpar, sh, math.pi / 2, ttrig)
        PW = P.sb([128, 9, 3, 2, 8], F32)
        tpw = Tok()
        fw.op("dve", lambda e: e.tensor_tensor(out=PW[:, 0, 0], in0=er[:], in1=cs[:], op=ALU.mult), reads=[tpar, ttrig], writes=[tpw])
        fw.op("dve", lambda e: e.tensor_tensor(out=PW[:, 0, 1], in0=er[:], in1=sn[:], op=ALU.mult), reads=[tpar, ttrig], writes=[tpw])
        t1 = P.sb(sh, F32)
        t2 = P.sb(sh, F32)
        for k in range(9):
            fw.op("dve", lambda e, k=k: e.tensor_scalar(out=PW[:, k, 2], in0=PW[:, k, 1], scalar1=-1.0, scalar2=None, op0=ALU.mult),
                  reads=[tpw], writes=[tpw])
            if k == 8:
                break
            fw.op("dve", lambda e, k=k: e.tensor_tensor(out=t1[:], in0=PW[:, k, 0], in1=PW[:, k, 0], op=ALU.mult), reads=[tpw], writes=[tpw])
            fw.op("dve", lambda e, k=k: e.tensor_tensor(out=t2[:], in0=PW[:, k, 1], in1=PW[:, k, 1], op=ALU.mult), reads=[tpw], writes=[tpw])
            fw.op("dve", lambda e, k=k: e.tensor_tensor(out=PW[:, k + 1, 0], in0=t1[:], in1=t2[:], op=ALU.subtract), reads=[tpw], writes=[tpw])
            fw.op("dve", lambda e, k=k: e.scalar_tensor_tensor(out=PW[:, k + 1, 1], in0=PW[:, k, 0], scalar=2.0, in1=PW[:, k, 1],
                                                               op0=ALU.mult, op1=ALU.mult), reads=[tpw], writes=[tpw])
        br = P.sb(sh, F32)
        bi = P.sb(sh, F32)
        nbi = P.sb(sh, F32)
        den = P.sb(sh, F32)
        nr = P.sb(sh, F32)
        tb = Tok()
        fw.op("dve", lambda e: e.tensor_tensor(out=den[:], in0=are[:], in1=are[:], op=ALU.mult), reads=[tpar], writes=[tb])
        fw.op("dve", lambda e: e.tensor_tensor(out=t1[:], in0=aim[:], in1=aim[:], op=ALU.mult), reads=[tpar, tpw], writes=[tpw])
        fw.op("dve", lambda e: e.tensor_tensor(out=den[:], in0=den[:], in1=t1[:], op=ALU.add), reads=[tb, tpw], writes=[tb])
        fw.op("dve", lambda e: e.reciprocal(out=den[:], in_=den[:]), reads=[tb], writes=[tb])
        fw.op("dve", lambda e: e.tensor_scalar(out=nr[:], in0=PW[:, 0, 0], scalar1=-1.0, scalar2=None, op0=ALU.add), reads=[tpw], writes=[tb])
        fw.op("dve", lambda e: e.tensor_tensor(out=t1[:], in0=nr[:], in1=are[:], op=ALU.mult), reads=[tb, tpar, tpw], writes=[tpw])
        fw.op("dve", lambda e: e.tensor_tensor(out=t2[:], in0=PW[:, 0, 1], in1=aim[:], op=ALU.mult), reads=[tpw, tpar], writes=[tpw])
        fw.op("dve", lambda e: e.tensor_tensor(out=t1[:], in0=t1[:], in1=t2[:], op=ALU.add), reads=[tpw], writes=[tpw])
        fw.op("dve", lambda e: e.tensor_tensor(out=br[:], in0=t1[:], in1=den[:], op=ALU.mult), reads=[tpw, tb], writes=[tb])
        fw.op("dve", lambda e: e.tensor_tensor(out=t1[:], in0=PW[:, 0, 1], in1=are[:], op=ALU.mult), reads=[tpw, tpar, tb], writes=[tpw])
        fw.op("dve", lambda e: e.tensor_tensor(out=t2[:], in0=nr[:], in1=aim[:], op=ALU.mult), reads=[tb, tpar, tpw], writes=[tpw])
        fw.op("dve", lambda e: e.tensor_tensor(out=t1[:], in0=t1[:], in1=t2[:], op=ALU.subtract), reads=[tpw], writes=[tpw])
        fw.op("dve", lambda e: e.tensor_tensor(out=bi[:], in0=t1[:], in1=den[:], op=ALU.mult), reads=[tpw, tb], writes=[tb])
        fw.op("dve", lambda e: e.tensor_scalar(out=nbi[:], in0=bi[:], scalar1=-1.0, scalar2=None, op0=ALU.mult), reads=[tb], writes=[tb])
        BTb = P.sb([128, 2, 2, 8, 128], BF16)
        CTb = P.sb([128, 2, 2, 8, 128], BF16)
        tBT = Tok()
        tCT = Tok()
        with ExitStack() as es2:
            P2 = Pool_(fw, es2)
            BTf = P2.sb([128, 2, 2, 8, 128], F32)
            CTf = P2.sb([128, 2, 2, 8, 128], F32)
            CT2 = P2.sb([128, 2, 2, 8, 128], F32)
            tf1, tf2 = Tok(), Tok()
            fw.op("pool", lambda e: e.memset(BTf[:], 0.0), writes=[tf1])
            fw.op("pool", lambda e: e.memset(CTf[:], 0.0), writes=[tf2])
            for di in range(2):
                for ri, (bsrc, csrc) in enumerate(((b_re, c_re), (b_im, c_im))):
                    for g in range(16):
                        s, g2 = g // 2, g % 2
                        r0 = (g % 8) * 16
                        fw.dma("sp", BTf[r0:r0 + 16, di, ri, s, g2 * 64:(g2 + 1) * 64], bsrc[di, g].rearrange("p h -> h p"),
                               writes=[tf1], allow_slow_non_contiguous=True)
                        fw.dma("sp", CTf[g2 * 64:(g2 + 1) * 64, di, ri, s, r0:r0 + 16], csrc[di, g].rearrange("h p -> p h"),
                               writes=[tf2], allow_slow_non_contiguous=True)
            fw.op("act", lambda e: e.copy(out=BTb[:], in_=BTf[:]), reads=[tf1], writes=[tBT])
            bsh = [128, 2, 8, 128]
            brb = br[:].unsqueeze(3).to_broadcast(bsh)
            nbib = nbi[:].unsqueeze(3).to_broadcast(bsh)
            tc2 = Tok()
            fw.op("dve", lambda e: e.tensor_tensor(out=CT2[:, :, 0], in0=CTf[:, :, 0], in1=brb, op=ALU.mult), reads=[tf2, tb], writes=[tc2])
            fw.op("pool", lambda e: e.tensor_tensor(out=CT2[:, :, 1], in0=CTf[:, :, 1], in1=nbib, op=ALU.mult), reads=[tf2, tb], writes=[tc2])
            fw.op("dve", lambda e: e.tensor_tensor(out=CT2[:, :, 0], in0=CT2[:, :, 0], in1=CT2[:, :, 1], op=ALU.add), reads=[tc2], writes=[tc2])
            fw.op("act", lambda e: e.copy(out=CTb[:, :, 0], in_=CT2[:, :, 0]), reads=[tc2], writes=[tCT])
            fw.op("dve", lambda e: e.tensor_tensor(out=CT2[:, :, 0], in0=CTf[:, :, 0], in1=nbib, op=ALU.mult), reads=[tf2, tb, tCT, tc2], writes=[tc2])
            fw.op("pool", lambda e: e.tensor_tensor(out=CT2[:, :, 1], in0=CTf[:, :, 1], in1=brb, op=ALU.mult), reads=[tf2, tb, tc2], writes=[tc2])
            fw.op("dve", lambda e: e.tensor_tensor(out=CT2[:, :, 0], in0=CT2[:, :, 0], in1=CT2[:, :, 1], op=ALU.subtract), reads=[tc2], writes=[tc2])
            fw.op("act", lambda e: e.copy(out=CTb[:, :, 1], in_=CT2[:, :, 0]), reads=[tc2], writes=[tCT])
            fw.barrier()
        ub = P.sb([128, 2, TE], BF16)
        tub = Tok()
        for ct in range(2):
            fw.dma("pool", ub[:, ct, :], ZT[ct * 128:(ct + 1) * 128, :], writes=[tub])
        dk = P.sb([128, 2], F32)
        tdk = Tok()
        fw.dma("sp", dk[:], dsk.rearrange("(c p) -> p c", p=128), writes=[tdk], allow_slow_non_contiguous=True)
        y = P.sb([128, 2, T], F32)
        ty = Tok()
        for ct in range(2):
            fw.dma("sp", y[:, ct, :], ZT[ct * 128:(ct + 1) * 128, 0:T], writes=[ty])
        for ct in range(2):
            fw.op("pool", lambda e, ct=ct: e.tensor_scalar(out=y[:, ct, :], in0=y[:, ct, :], scalar1=dk[:, ct:ct + 1], scalar2=None, op0=ALU.mult),
                  reads=[ty, tdk], writes=[ty])
        Xs = Rot([P.sb([128, 2, N], F32) for _ in range(4)])
        xbs = Rot([P.sb([128, 2, N], BF16) for _ in range(2)])
        psr = Rot([P.ps() for _ in range(4)])
        psy = Rot([P.ps() for _ in range(2)])
        tmpy = Rot([P.sb([128, 512], F32) for _ in range(2)])
        CH = [(0, 512), (512, 1024), (1024, 1536), (1536, 2048), (2048, 2304)]

        def prep(di, s):
            off = 0 if di == 0 else 256
            ct = s // 4
            X, tX = Xs.next()
            for ri in range(2):
                for ci, (c0, c1) in enumerate(CH):
                    n = c1 - c0
                    ps, tps = psr.next()
                    fw.op("pe", lambda e, ps=ps, ri=ri, c0=c0, c1=c1, n=n: e.matmul(
                        ps[:, :n], lhsT=BTb[:, di, ri, s, :], rhs=ub[:, ct, off + c0:off + c1], start=True, stop=True),
                        reads=[tBT, tub], writes=[tps])
                    fw.op("act", lambda e, ps=ps, ri=ri, c0=c0, c1=c1, n=n: e.copy(out=X[:, ri, c0:c1], in_=ps[:, :n]),
                          reads=[tps], writes=[tX])
            return X, tX

        def scan_gen(di, s, X, tX):
            def cstep(w_re, w_im, r_re, r_im, k):
                pr = PW[:, k, 0, di, s:s + 1]
                pi = PW[:, k, 1, di, s:s + 1]
                npi = PW[:, k, 2, di, s:s + 1]
                for (o, i0, sc) in ((w_re, r_re, pr), (w_re, r_im, npi), (w_im, r_im, pr), (w_im, r_re, pi)):
                    fw.op("dve", lambda e, o=o, i0=i0, sc=sc: e.scalar_tensor_tensor(out=o, in0=i0, scalar=sc, in1=o, op0=ALU.mult, op1=ALU.add),
                          reads=[tX, tpw], writes=[tX])
                    yield
            for k in range(8):
                st_ = 1 << k
                Xv = [X[:, ri, :].rearrange("p (m c) -> p m c", c=2 * st_) for ri in range(2)]
                if di == 0:
                    yield from cstep(Xv[0][:, :, 2 * st_ - 1], Xv[1][:, :, 2 * st_ - 1], Xv[0][:, :, st_ - 1], Xv[1][:, :, st_ - 1], k)
                else:
                    yield from cstep(Xv[0][:, :, 0], Xv[1][:, :, 0], Xv[0][:, :, st_], Xv[1][:, :, st_], k)
            for i in (range(1, 9) if di == 0 else range(7, -1, -1)):
                if di == 0:
                    w, r = 256 * i + 255, 256 * (i - 1) + 255
                else:
                    w, r = 256 * i, 256 * (i + 1)
                yield from cstep(X[:, 0, w:w + 1], X[:, 1, w:w + 1], X[:, 0, r:r + 1], X[:, 1, r:r + 1], 8)
            for k in range(7, -1, -1):
                st_ = 1 << k
                Xv = [X[:, ri, :].rearrange("p (m c) -> p m c", c=2 * st_) for ri in range(2)]
                if di == 0:
                    yield from cstep(Xv[0][:, 1:, st_ - 1], Xv[1][:, 1:, st_ - 1], Xv[0][:, :-1, 2 * st_ - 1], Xv[1][:, :-1, 2 * st_ - 1], k)
                else:
                    yield from cstep(Xv[0][:, :-1, st_], Xv[1][:, :-1, st_], Xv[0][:, 1:, 0], Xv[1][:, 1:, 0], k)

        def fin(di, s, X, tX):
            ct = s // 4
            xb, txb = xbs.next()
            fw.op("act", lambda e: e.copy(out=xb[:], in_=X[:]), reads=[tX], writes=[txb])
            for (c0, c1) in CH:
                n = c1 - c0
                if di == 0:
                    y0 = c0
                else:
                    y0 = c0 + 256 if c0 < 2048 else 0
                ps, tps = psy.next()
                for ri in range(2):
                    fw.op("pe", lambda e, ps=ps, ri=ri, c0=c0, c1=c1, n=n: e.matmul(
                        ps[:, :n], lhsT=CTb[:, di, ri, s, :], rhs=xb[:, ri, c0:c1], start=(ri == 0), stop=(ri == 1)),
                        reads=[tCT, txb], writes=[tps])
                tm, ttm = tmpy.next()
                fw.op("act", lambda e, tm=tm, ps=ps, n=n: e.copy(out=tm[:, :n], in_=ps[:, :n]), reads=[tps], writes=[ttm])
                fw.op("pool", lambda e, tm=tm, y0=y0, n=n: e.tensor_tensor(out=y[:, ct, y0:y0 + n], in0=y[:, ct, y0:y0 + n], in1=tm[:, :n], op=ALU.add),
                      reads=[ttm, ty], writes=[ty])

        pairs = [(di, s0, s0 + 1) for di in range(2) for s0 in range(0, 8, 2)]
        nxt = [prep(pairs[0][0], pairs[0][1]), prep(pairs[0][0], pairs[0][2])]
        for pi_, (di, s0, s1) in enumerate(pairs):
            cur = nxt
            if pi_ + 1 < len(pairs):
                d2, a0, a1 = pairs[pi_ + 1]
                nxt = [prep(d2, a0), prep(d2, a1)]
            gens = [scan_gen(di, s0, *cur[0]), scan_gen(di, s1, *cur[1])]
            alive = [True, True]
            while any(alive):
                for gi in range(2):
                    if alive[gi]:
                        try:
                            next(gens[gi])
                        except StopIteration:
                            alive[gi] = False
            fin(di, s0, *cur[0])
            fin(di, s1, *cur[1])
        tg = Tok()
        gw = load_w_bf16(fw, P, glu_w, 256, 256, tg, "gluw")
        gb = P.sb([128, 2], F32)
        og = P.sb([128, 2], F32)
        fw.dma("sp", gb[:], glu_b.rearrange("(c p) -> p c", p=128), writes=[tg], allow_slow_non_contiguous=True)
        fw.dma("sp", og[:], out_g.rearrange("(c p) -> p c", p=128), writes=[tg], allow_slow_non_contiguous=True)
        ones = P.sb([128, 128], BF16)
        eps = P.sb([128, 1], F32)
        fw.op("pool", lambda e: e.memset(ones[:], 1.0), writes=[tg])
        fw.op("pool", lambda e: e.memset(eps[:], RMS_EPS), writes=[tg])
        a32 = P.sb([128, 2, 512], F32)
        ab = P.sb([128, 2, 512], BF16)
        w1 = P.sb([128, 2, 512], F32)
        w2 = P.sb([128, 2, 512], F32)
        sqb = P.sb([128, 2, 512], BF16)
        rs = P.sb([128, 512], F32)
        ta, tw1, trs = Tok(), Tok(), Tok()
        stg = Rot([P.sb([128, 512], F32) for _ in range(2)])
        tout = Tok()
        C1 = math.sqrt(2.0 / math.pi)
        for (lo, hi, ic) in SEGS:
            n = hi - lo
            yv = y[:, :, lo:hi]
            fw.op("act", lambda e, yv=yv, n=n: e.activation(out=w1[:, :, :n], in_=yv, func=AF.Square), reads=[ty], writes=[tw1])
            fw.op("dve", lambda e, n=n: e.tensor_scalar(out=w1[:, :, :n], in0=w1[:, :, :n], scalar1=0.044715 * C1, scalar2=C1, op0=ALU.mult, op1=ALU.add),
                  reads=[tw1], writes=[tw1])
            fw.op("dve", lambda e, yv=yv, n=n: e.tensor_tensor(out=w1[:, :, :n], in0=w1[:, :, :n], in1=yv, op=ALU.mult), reads=[tw1, ty], writes=[tw1])
            fw.op("act", lambda e, n=n: e.activation(out=w1[:, :, :n], in_=w1[:, :, :n], func=AF.Tanh), reads=[tw1], writes=[tw1])
            fw.op("dve", lambda e, n=n: e.tensor_scalar(out=w1[:, :, :n], in0=w1[:, :, :n], scalar1=0.5, scalar2=0.5, op0=ALU.mult, op1=ALU.add),
                  reads=[tw1], writes=[tw1])
            fw.op("dve", lambda e, yv=yv, n=n: e.tensor_tensor(out=a32[:, :, :n], in0=w1[:, :, :n], in1=yv, op=ALU.mult), reads=[tw1, ty], writes=[ta])
            fw.op("act", lambda e, n=n: e.copy(out=ab[:, :, :n], in_=a32[:, :, :n]), reads=[ta], writes=[ta])
            for nt in range(2):
                ps, tps = psr.next()
                for k in range(2):
                    fw.op("pe", lambda e, ps=ps, k=k, nt=nt, n=n: e.matmul(ps[:, :n], lhsT=gw[:, k, nt * 128:(nt + 1) * 128], rhs=ab[:, k, :n],
                                                                      start=(k == 0), stop=(k == 1)), reads=[tg, ta], writes=[tps])
                fw.op("act", lambda e, ps=ps, nt=nt, n=n: e.activation(out=w2[:, nt, :n], in_=ps[:, :n], func=AF.Sigmoid, bias=gb[:, nt:nt + 1]),
                      reads=[tps, tg], writes=[tw1])
            fw.op("dve", lambda e, n=n: e.tensor_tensor(out=w2[:, :, :n], in0=w2[:, :, :n], in1=a32[:, :, :n], op=ALU.mult), reads=[tw1, ta], writes=[tw1])
            fw.op("act", lambda e, n=n: e.activation(out=sqb[:, :, :n], in_=w2[:, :, :n], func=AF.Square), reads=[tw1], writes=[tw1])
            ps, tps = psr.next()
            for k in range(2):
                fw.op("pe", lambda e, ps=ps, k=k, n=n: e.matmul(ps[:, :n], lhsT=ones[:], rhs=sqb[:, k, :n], start=(k == 0), stop=(k == 1)),
                      reads=[tw1, tg], writes=[tps])
            rstd_from_ps(fw, rs, trs, ps, tps, n, 1.0 / 256, eps[:, 0:1], tg)
            for k in range(2):
                sg, tsg = stg.next()
                fw.op("dve", lambda e, sg=sg, k=k, n=n: e.scalar_tensor_tensor(out=sg[:, :n], in0=w2[:, k, :n], scalar=og[:, k:k + 1], in1=rs[:, :n],
                                                                             op0=ALU.mult, op1=ALU.mult), reads=[tw1, trs, tg], writes=[tsg])
                fw.dma("sp", S5T[k * 128:(k + 1) * 128, lo:hi], sg[:, :n], reads=[tsg], writes=[tout])
        fw.barrier()


RW_BASE = 1024
CHK = 64
NCH = T // CHK
LN_EPS_RW = 64e-5


def rw_consts():
    idx = np.arange(64)
    m = np.zeros((64, 4, 64), np.float32)
    m[:, 0, :] = (idx[:, None] < idx[None, :])
    m[:, 1, :] = (idx[:, None] > idx[None, :])
    m[:, 2, :] = (idx[:, None] <= idx[None, :])
    m[:, 3, :] = (idx[:, None] >= idx[None, :])
    bd = np.zeros((128, 128), np.float32)
    bd[:64, :64] = 1.0
    bd[64:, 64:] = 1.0
    return m, bd


def stage_p4(fw, io):
    ZT = io("ZT", [2048, TE], F32, "in")
    mu = io("rw_mu", [960], F32, "in")
    w0 = io("rw_w0", [2, 256], F32, "in")
    w2 = io("rw_w2", [2, 32, 256], F32, "in")
    a0 = io("rw_a0", [2, 256], F32, "in")
    a2 = io("rw_a2", [2, 32, 256], F32, "in")
    g2 = io("rw_g2", [64, 256], F32, "in")
    k_k = io("rw_k_k", [256], F32, "in")
    k_a = io("rw_k_a", [256], F32, "in")
    r_k = io("rw_r_k", [256], F32, "in")
    ln_g = io("rw_ln_g", [256], F32, "in")
    ln_b = io("rw_ln_b", [256], F32, "in")
    ident = io("ident", [128, 128], F32, "in")
    MASKS = io("RWMASK", [64, 4, 64], F32, "in")
    BD = io("RWBD", [128, 128], F32, "in")
    RWT = io("RWT", [256, T], F32, "out")
    N = T
    CH5 = [(0, 512), (512, 1024), (1024, 1536), (1536, 2048), (2048, 2560)]
    with ExitStack() as es:
        P = Pool_(fw, es)
        tc = Tok()
        idt = P.sb([128, 128], F32)
        idb = P.sb([128, 128], BF16)
        msk = P.sb([64, 4, 64], F32)
        bd1 = P.sb([128, 128], F32)
        fw.dma("sp", idt[:], ident, writes=[tc])
        fw.dma("sp", msk[:], MASKS, writes=[tc])
        fw.dma("sp", bd1[:], BD, writes=[tc])
        fw.op("dve", lambda e: e.tensor_copy(out=idb[:], in_=idt[:]), reads=[tc], writes=[tc])
        mrep = P.sb([64, 4, 4, 64], F32)
        for rep in range(4):
            fw.op("dve", lambda e, rep=rep: e.tensor_copy(out=mrep[:, :, rep, :], in_=msk[:]), reads=[tc], writes=[tc])
        pp = P.sb([128, 12, 2], F32)
        tpp = Tok()
        srcs = [w0[0], w0[1], a0[0], a0[1], k_k, k_a, k_a, r_k, ln_g, ln_b]
        for i, sap in enumerate(srcs):
            fw.dma("sp", pp[:, i, :], sap.rearrange("(c p) -> p c", p=128), writes=[tpp], allow_slow_non_contiguous=True)
        fw.op("dve", lambda e: e.tensor_scalar(out=pp[:, 6, :], in0=pp[:, 6, :], scalar1=-1.0, scalar2=1.0, op0=ALU.mult, op1=ALU.add),
              reads=[tpp], writes=[tpp])
        epsl = P.sb([128, 2], F32)
        fw.op("pool", lambda e: e.memset(epsl[:, 0:1], LN_EPS_RW), writes=[tpp])
        fw.op("pool", lambda e: e.memset(epsl[:, 1:2], 1e-24), writes=[tpp])
        wA = P.sb([128, 256], BF16)
        wB = P.sb([128, 256], BF16)
        tlw = Tok()
        fw.dma("pool", wA[0:32, :], w2[0], writes=[tlw])
        fw.dma("pool", wA[32:64, :], w2[1], writes=[tlw])
        fw.dma("pool", wA[64:96, :], a2[0], writes=[tlw])
        fw.dma("pool", wB[0:32, :], a2[1], writes=[tlw])
        fw.dma("pool", wB[64:128, :], g2, writes=[tlw])
        smask = P.sb([128, N], BF16)
        tsm = Tok()
        fw.op("pool", lambda e: e.memset(smask[:], 1.0), writes=[tsm])
        fw.op("pool", lambda e: e.memset(smask[:].rearrange("p (c j) -> p c j", j=CHK)[:, :, 0], 0.0), writes=[tsm])

        def load_shift(dst, tdst, row0, rows, P2, post=None, pb=0):
            zt_ = P2.sb([128, TE], F32)
            nbt_ = P2.sb([128, TE], F32)
            mtt_ = P2.sb([128, 2], F32)
            tz, tnb, tm = Tok(), Tok(), Tok()
            ps_ = slice(pb, pb + rows)
            z = zt_[ps_, :]
            fw.dma("sp", z, ZT[RW_BASE + row0:RW_BASE + row0 + rows, :], writes=[tz])
            fw.dma("sp", mtt_[ps_, 0:1], mu[row0:row0 + rows].rearrange("(p o) -> p o", o=1), writes=[tm])
            fw.op("dve", lambda e: e.tensor_scalar(out=mtt_[ps_, 1:2], in0=mtt_[ps_, 0:1], scalar1=0.5, scalar2=None, op0=ALU.mult), reads=[tm], writes=[tm])
            fw.op("dve", lambda e: e.tensor_scalar(out=mtt_[ps_, 0:1], in0=mtt_[ps_, 0:1], scalar1=-1.0, scalar2=1.0, op0=ALU.mult, op1=ALU.add),
                  reads=[tm], writes=[tm])
            fw.op("pool", lambda e: e.memset(nbt_[ps_, 0:1], 0.0), writes=[tnb])
            fw.op("pool", lambda e: e.tensor_copy(out=nbt_[ps_, 1:TE], in_=zt_[ps_, 0:TE - 1]), reads=[tz], writes=[tnb])
            fw.op("pool", lambda e: e.tensor_tensor(out=nbt_[ps_, 0:TE - 1], in0=nbt_[ps_, 0:TE - 1], in1=zt_[ps_, 1:TE], op=ALU.add),
                  reads=[tz, tnb], writes=[tnb])
            for cb in (256, 2304):
                fw.op("pool", lambda e, cb=cb: e.tensor_tensor(out=nbt_[ps_, cb:cb + 1], in0=nbt_[ps_, cb:cb + 1], in1=zt_[ps_, cb - 1:cb], op=ALU.subtract),
                      reads=[tz, tnb], writes=[tnb])
                fw.op("pool", lambda e, cb=cb: e.tensor_tensor(out=nbt_[ps_, cb - 1:cb], in0=nbt_[ps_, cb - 1:cb], in1=zt_[ps_, cb:cb + 1], op=ALU.subtract),
                      reads=[tz, tnb], writes=[tnb])
            fw.op("act", lambda e: e.activation(out=z, in_=z, func=AF.Identity, scale=mtt_[ps_, 0:1]), reads=[tz, tm], writes=[tz])
            if post is None:
                fw.op("dve", lambda e: e.scalar_tensor_tensor(out=dst, in0=nbt_[ps_, :], scalar=mtt_[ps_, 1:2], in1=z, op0=ALU.mult, op1=ALU.add),
                      reads=[tz, tnb, tm], writes=[tdst])
            else:
                fw.op("dve", lambda e: e.scalar_tensor_tensor(out=z, in0=nbt_[ps_, :], scalar=mtt_[ps_, 1:2], in1=z, op0=ALU.mult, op1=ALU.add),
                      reads=[tz, tnb, tm], writes=[tz])
                fw.op("act", lambda e: e.activation(out=dst, in_=z, func=post), reads=[tz], writes=[tdst])

        lorA = P.sb([128, TE], BF16)
        lorB = P.sb([128, TE], BF16)
        tlor = Tok()
        for i in range(4):
            with ExitStack() as es2:
                dstt = lorA[32 * i:32 * i + 32, :] if i < 3 else lorB[0:32, :]
                load_shift(dstt, tlor, 768 + 32 * i, 32, Pool_(fw, es2), post=(AF.Tanh if i < 2 else AF.Copy), pb=(32 * i if i < 3 else 0))
                fw.barrier()
        with ExitStack() as es2:
            load_shift(lorB[64:128, :], tlor, 896, 64, Pool_(fw, es2), post=AF.Sigmoid, pb=64)
            fw.barrier()

        for c in range(2):
            with ExitStack() as esc:
                Pc = Pool_(fw, esc)
                rr = Pc.sb([128, TE], F32)
                kx = Pc.sb([128, TE], F32)
                vv = Pc.sb([128, TE], F32)
                kk = Pc.sb([128, TE], F32)
                trr, tkx, tvv, tkk = Tok(), Tok(), Tok(), Tok()
                vtok = Pc.sb([64, TE // CHK, 128], BF16)
                tvt = Tok()
                yacc = Pc.sb([128, T], F32)
                bon = Pc.sb([128, T], F32)
                tya, tbon = Tok(), Tok()
                fw.op("pool", lambda e: e.memset(yacc[:], 0.0), writes=[tya])
                fw.op("pool", lambda e: e.memset(bon[:], 0.0), writes=[tbon])
                for (dst, tdst, r0) in ((rr, trr, 0), (kx, tkx, 256), (vv, tvv, 512)):
                    with ExitStack() as es2:
                        load_shift(dst[:], tdst, r0 + c * 128, 128, Pool_(fw, es2))
                        fw.barrier()
                with ExitStack() as es2:
                    P2 = Pool_(fw, es2)
                    sq = P2.sb([128, 512], F32)
                    rn = P2.sb([128, 512], F32)
                    tsq, trn = Tok(), Tok()
                    ps1 = P2.ps()
                    tps1 = Tok()
                    fw.op("dve", lambda e: e.tensor_scalar(out=kk[:], in0=kx[:], scalar1=pp[:, 4, c:c + 1], scalar2=None, op0=ALU.mult),
                          reads=[tkx, tpp], writes=[tkk])
                    for (c0, c1) in CH5:
                        fw.op("act", lambda e, c0=c0, c1=c1: e.activation(out=sq[:], in_=kk[:, c0:c1], func=AF.Square), reads=[tkk], writes=[tsq])
                        fw.op("pe", lambda e: e.matmul(ps1[:], lhsT=bd1[:], rhs=sq[:], start=True, stop=True), reads=[tsq, tc], writes=[tps1])
                        rstd_from_ps(fw, rn, trn, ps1, tps1, 512, 1.0, epsl[:, 1:2], tpp)
                        fw.op("dve", lambda e, c0=c0, c1=c1: e.tensor_tensor(out=kk[:, c0:c1], in0=kk[:, c0:c1], in1=rn[:], op=ALU.mult),
                              reads=[tkk, trn], writes=[tkk])
                    vb = P2.sb([128, TE], BF16)
                    tvb = Tok()
                    fw.op("act", lambda e: e.copy(out=vb[:], in_=vv[:]), reads=[tvv], writes=[tvb])
                    pst = P2.ps([128, 1024], BF16)
                    tpst = Tok()
                    for q in range(TE // CHK // 4):
                        for j in range(4):
                            ch = q * 4 + j
                            fw.op("pe", lambda e, j=j, ch=ch: e.transpose(pst[0:64, j * 128:(j + 1) * 128], vb[:, ch * CHK:(ch + 1) * CHK], idb[:]),
                                  reads=[tvb, tc], writes=[tpst])
                        fw.op("dve", lambda e, q=q: e.tensor_copy(out=vtok[:, q * 4:(q + 1) * 4, :], in_=pst[0:64, 0:512].rearrange("p (j c) -> p j c", j=4)),
                              reads=[tpst], writes=[tvt])
                    fw.barrier()

                for di in range(2):
                    off = 0 if di == 0 else 256
                    with ExitStack() as esd:
                        Pd = Pool_(fw, esd)
                        aT = Pd.sb([128, N], BF16)
                        bT = Pd.sb([128, N], BF16)
                        kT = Pd.sb([128, N], BF16)
                        rT = Pd.sb([128, N], BF16)
                        btok = Pd.sb([64, NCH, 128], BF16)
                        ktok = Pd.sb([64, NCH, 128], BF16)
                        pC = Pd.sb([128, NCH], F32)
                        tops = Tok()
                        ttok = Tok()
                        with ExitStack() as es2:
                            P2 = Pool_(fw, es2)
                            ld = P2.sb([128, TE], F32)
                            kd = P2.sb([128, TE], F32)
                            bb = P2.sb([128, TE], F32)
                            tld, tkd, tbb = Tok(), Tok(), Tok()
                            psr = Rot([P2.ps() for _ in range(3)])
                            tm5 = Rot([P2.sb([128, 512], F32) for _ in range(2)])
                            for (c0, c1) in CH5:
                                ps, tps = psr.next()
                                fw.op("pe", lambda e, ps=ps, c0=c0, c1=c1: e.matmul(ps[:], lhsT=wA[32 * di:32 * di + 32, c * 128:(c + 1) * 128], rhs=lorA[32 * di:32 * di + 32, c0:c1],
                                                                                  start=True, stop=True), reads=[tlw, tlor], writes=[tps])
                                fw.op("act", lambda e, ps=ps, c0=c0, c1=c1: e.activation(out=ld[:, c0:c1], in_=ps[:], func=AF.Sigmoid, bias=pp[:, di, c:c + 1]),
                                      reads=[tps, tpp], writes=[tld])
                                ps, tps = psr.next()
                                fw.op("pe", lambda e, ps=ps, c0=c0, c1=c1: e.matmul(ps[:], lhsT=(wA[64:96, c * 128:(c + 1) * 128] if di == 0 else wB[0:32, c * 128:(c + 1) * 128]),
                                                                                  rhs=(lorA[64:96, c0:c1] if di == 0 else lorB[0:32, c0:c1]),
                                                                                  start=True, stop=True), reads=[tlw, tlor], writes=[tps])
                                fw.op("act", lambda e, ps=ps, c0=c0, c1=c1: e.activation(out=bb[:, c0:c1], in_=ps[:], func=AF.Sigmoid, bias=pp[:, 2 + di, c:c + 1]),
                                      reads=[tps, tpp], writes=[tbb])
                            fw.op("pool", lambda e: e.tensor_scalar(out=ld[:], in0=ld[:], scalar1=-math.exp(-0.5), scalar2=None, op0=ALU.mult), reads=[tld], writes=[tld])
                            fw.op("act", lambda e: e.activation(out=kd[:], in_=bb[:], func=AF.Identity, scale=pp[:, 5, c:c + 1], bias=pp[:, 6, c:c + 1]),
                                  reads=[tbb, tpp], writes=[tkd])
                            fw.op("dve", lambda e: e.tensor_tensor(out=kd[:], in0=kd[:], in1=kx[:], op=ALU.mult), reads=[tkd, tkx], writes=[tkd])
                            fw.op("pool", lambda e: e.tensor_tensor(out=bb[:], in0=bb[:], in1=kk[:], op=ALU.mult), reads=[tbb, tkk], writes=[tbb])
                            for (c0, c1) in CH5:
                                c1 = min(c1, T)
                                n = c1 - c0
                                tm, ttm = tm5.next()
                                fw.op("dve", lambda e, tm=tm, c0=c0, c1=c1, n=n: e.scalar_tensor_tensor(out=tm[:, :n], in0=rr[:, c0:c1], scalar=pp[:, 7, c:c + 1], in1=kd[:, c0:c1],
                                                                                                 op0=ALU.mult, op1=ALU.mult), reads=[trr, tkd, tpp], writes=[ttm])
                                ps, tps = psr.next()
                                fw.op("pe", lambda e, ps=ps, tm=tm, n=n: e.matmul(ps[:, :n], lhsT=bd1[:], rhs=tm[:, :n], start=True, stop=True), reads=[ttm, tc], writes=[tps])
                                tm2, ttm2 = tm5.next()
                                fw.op("dve", lambda e, tm2=tm2, ps=ps, c0=c0, c1=c1, n=n: e.tensor_tensor(out=tm2[:, :n], in0=ps[:, :n], in1=vv[:, c0:c1], op=ALU.mult),
                                      reads=[tps, tvv], writes=[ttm2])
                                fw.op("pool", lambda e, tm2=tm2, c0=c0, c1=c1, n=n: e.tensor_tensor(out=bon[:, c0:c1], in0=bon[:, c0:c1], in1=tm2[:, :n], op=ALU.add),
                                      reads=[ttm2, tbon], writes=[tbon])
                            cs = P2.sb([128, N], F32)
                            ex = P2.sb([128, N], F32)
                            tcs, tex = Tok(), Tok()
                            ldw = ld[:, off:off + N]
                            fw.op("dve", lambda e: e.tensor_tensor_scan(out=cs[:], data0=smask[:], data1=ldw, initial=0.0, op0=ALU.mult, op1=ALU.add),
                                  reads=[tsm, tld], writes=[tcs])
                            csv = cs[:].rearrange("p (c j) -> p c j", j=CHK)
                            tot = P2.sb([128, NCH, 1], F32)
                            ttot = Tok()
                            fw.op("dve", lambda e: e.tensor_copy(out=tot[:], in_=csv[:, :, CHK - 1:CHK]), reads=[tcs], writes=[ttot])
                            totb = tot[:].to_broadcast([128, NCH, CHK])
                            if di == 1:
                                fw.op("dve", lambda e: e.tensor_tensor(out=csv, in0=totb, in1=csv, op=ALU.subtract), reads=[ttot, tcs], writes=[tcs])
                                fw.op("dve", lambda e: e.tensor_tensor(out=cs[:], in0=cs[:], in1=ldw, op=ALU.add), reads=[tcs, tld], writes=[tcs])
                            fw.op("act", lambda e: e.activation(out=pC[:], in_=tot[:, :, 0], func=AF.Exp), reads=[ttot], writes=[tops])
                            kdw, bbw = kd[:, off:off + N], bb[:, off:off + N]
                            rrw, kkw = rr[:, off:off + N], kk[:, off:off + N]
                            fw.op("act", lambda e: e.activation(out=ex[:], in_=cs[:], func=AF.Exp), reads=[tcs], writes=[tex])
                            fw.op("dve", lambda e: e.tensor_tensor(out=rT[:], in0=rrw, in1=ex[:], op=ALU.mult), reads=[trr, tex], writes=[tops])
                            fw.op("act", lambda e: e.activation(out=ex[:], in_=cs[:], func=AF.Exp, scale=-1.0), reads=[tcs, tops], writes=[tex])
                            fw.op("dve", lambda e: e.tensor_tensor(out=bT[:], in0=bbw, in1=ex[:], op=ALU.mult), reads=[tbb, tex], writes=[tops])
                            fw.op("pool", lambda e: e.tensor_tensor(out=kT[:], in0=kdw, in1=ex[:], op=ALU.mult), reads=[tkd, tex], writes=[tops])
                            e3 = ex
                            te3 = tex
                            fw.op("dve", lambda e: e.tensor_tensor(out=e3[:], in0=cs[:], in1=ldw, op=ALU.subtract), reads=[tcs, tld], writes=[te3])
                            fw.op("act", lambda e: e.activation(out=e3[:], in_=e3[:], func=AF.Exp), reads=[te3], writes=[te3])
                            fw.op("dve", lambda e: e.scalar_tensor_tensor(out=aT[:], in0=kkw, scalar=-1.0, in1=e3[:], op0=ALU.mult, op1=ALU.mult),
                                  reads=[tkk, te3], writes=[tops])
                            e3v = e3[:].rearrange("p (c j) -> p c j", j=CHK)
                            fw.op("dve", lambda e: e.tensor_tensor(out=e3v, in0=totb, in1=csv, op=ALU.subtract), reads=[ttot, tcs, tops, te3], writes=[te3])
                            fw.op("act", lambda e: e.activation(out=e3[:], in_=e3[:], func=AF.Exp), reads=[te3], writes=[te3])
                            bh = P2.sb([128, N], BF16)
                            kh = P2.sb([128, N], BF16)
                            tbh = Tok()
                            fw.op("dve", lambda e: e.tensor_tensor(out=bh[:], in0=bbw, in1=e3[:], op=ALU.mult), reads=[tbb, te3], writes=[tbh])
                            fw.op("pool", lambda e: e.tensor_tensor(out=kh[:], in0=kdw, in1=e3[:], op=ALU.mult), reads=[tkd, te3], writes=[tbh])
                            pst = P2.ps([128, 1024], BF16)
                            tpst = Tok()
                            for (src, dstt) in ((bh, btok), (kh, ktok)):
                                for q in range(NCH // 4):
                                    for j in range(4):
                                        ch = q * 4 + j
                                        fw.op("pe", lambda e, j=j, ch=ch, src=src: e.transpose(pst[0:64, j * 128:(j + 1) * 128], src[:, ch * CHK:(ch + 1) * CHK], idb[:]),
                                              reads=[tbh, tc], writes=[tpst])
                                    fw.op("act", lambda e, q=q, dstt=dstt: e.copy(out=dstt[:, q * 4:(q + 1) * 4, :], in_=pst[0:64, 0:512].rearrange("p (j c) -> p j c", j=4)),
                                          reads=[tpst], writes=[ttok])
                            fw.barrier()
                        with ExitStack() as es3:
                            P3 = Pool_(fw, es3)
                            Hs = P3.sb([128, 64], F32)
                            Hb = P3.sb([128, 64], BF16)
                            tH = Tok()
                            fw.op("pool", lambda e: e.memset(Hs[:], 0.0), writes=[tH])
                            fw.op("pool", lambda e: e.memset(Hb[:], 0.0), writes=[tH])
                            psA = Rot([P3.ps([64, 512]) for _ in range(1)])
                            psB = Rot([P3.ps([64, 512]) for _ in range(1)])
                            psI = Rot([P3.ps([64, 512]) for _ in range(1)])
                            psT = Rot([P3.ps([64, 512]) for _ in range(1)])
                            psG = Rot([P3.ps([64, 512]) for _ in range(1)])
                            psH = Rot([P3.ps([128, 512]) for _ in range(1)])
                            psY = Rot([P3.ps([128, 512]) for _ in range(1)])
                            g1b = Rot([P3.sb([64, 2, 64], BF16) for _ in range(2)])
                            nl = Rot([P3.sb([64, 2, 2, 64], F32) for _ in range(2)])
                            nl2 = Rot([P3.sb([64, 2, 2, 64], F32) for _ in range(2)])
                            g2b_ = Rot([P3.sb([64, 4, 64], BF16) for _ in range(2)])
                            Pm = Rot([P3.sb([64, 2, 64], F32) for _ in range(2)])
                            Gs = Rot([P3.sb([64, 128], F32) for _ in range(2)])
                            Ub = Rot([P3.sb([64, 128], BF16) for _ in range(2)])
                            Ys = Rot([P3.sb([64, 128], F32) for _ in range(2)])
                            ms, ml, mi = (0, 1, 2) if di == 0 else (1, 0, 3)
                            order = list(range(NCH)) if di == 0 else list(range(NCH - 1, -1, -1))
                            res = {}

                            def par_gen(i):
                                cc0, cc1 = i * CHK, (i + 1) * CHK
                                pa, tpa = psA.next()
                                pb, tpb = psB.next()
                                for h in range(2):
                                    hp = slice(h * 64, (h + 1) * 64)
                                    for (dst, lt, rt_) in ((pa[:, h * 64:(h + 1) * 64], kT, aT), (pa[:, 128 + h * 64:128 + (h + 1) * 64], bT, aT),
                                                           (pa[:, 256 + h * 64:256 + (h + 1) * 64], aT, bT)):
                                        fw.op("pe", lambda e, dst=dst, lt=lt, rt_=rt_, hp=hp: e.matmul(dst, lhsT=lt[hp, cc0:cc1], rhs=rt_[hp, cc0:cc1], start=True, stop=True),
                                              reads=[tops], writes=[tpa])
                                        yield
                                    for (dst, lt, rt_) in ((pb[:, h * 64:(h + 1) * 64], bT, rT), (pb[:, 128 + h * 64:128 + (h + 1) * 64], kT, rT)):
                                        fw.op("pe", lambda e, dst=dst, lt=lt, rt_=rt_, hp=hp: e.matmul(dst, lhsT=lt[hp, cc0:cc1], rhs=rt_[hp, cc0:cc1], start=True, stop=True),
                                              reads=[tops], writes=[tpb])
                                        yield
                                a1, ta1 = g1b.next()
                                nlt, tnl = nl.next()
                                a45, ta45 = g2b_.next()
                                fw.op("dve", lambda e: e.tensor_tensor(out=a1[:], in0=pa[:, 0:128].rearrange("p (h t) -> p h t", h=2), in1=mrep[:, ms, 0:2, :], op=ALU.mult),
                                      reads=[tpa, tc], writes=[ta1])
                                yield
                                fw.op("dve", lambda e: e.tensor_tensor(out=nlt[:, 0], in0=pa[:, 128:256].rearrange("p (h t) -> p h t", h=2), in1=mrep[:, ms, 0:2, :], op=ALU.mult),
                                      reads=[tpa, tc], writes=[tnl])
                                yield
                                fw.op("dve", lambda e: e.tensor_tensor(out=nlt[:, 1], in0=pa[:, 256:384].rearrange("p (h t) -> p h t", h=2), in1=mrep[:, ml, 0:2, :], op=ALU.mult),
                                      reads=[tpa, tc], writes=[tnl])
                                yield
                                fw.op("dve", lambda e: e.tensor_tensor(out=a45[:], in0=pb[:, 0:256].rearrange("p (h t) -> p h t", h=4), in1=mrep[:, mi, :, :], op=ALU.mult),
                                      reads=[tpb, tc], writes=[ta45])
                                yield
                                pm, tpm = Pm.next()
                                fw.op("dve", lambda e: e.tensor_tensor(out=pm[:], in0=nlt[:, 0], in1=idt[0:64, 0:64].unsqueeze(1).to_broadcast([64, 2, 64]), op=ALU.add),
                                      reads=[tnl, tc], writes=[tpm])
                                yield
                                cur, tcur = nlt, tnl
                                for lev in range(5):
                                    pi_, tpi = psI.next()
                                    for h in range(2):
                                        fw.op("pe", lambda e, pi_=pi_, h=h, cur=cur: e.matmul(pi_[:, h * 64:(h + 1) * 64], lhsT=cur[:, 0, h, :], rhs=cur[:, 1, h, :], start=True, stop=True),
                                              reads=[tcur], writes=[tpi])
                                        yield
                                    nxt, tnxt = (nl2.next() if lev % 2 == 0 else nl.next())
                                    fw.op("act", lambda e, nxt=nxt, pi_=pi_: e.copy(out=nxt[:, 1], in_=pi_[:, 0:128].rearrange("p (h t) -> p h t", h=2)), reads=[tpi], writes=[tnxt])
                                    yield
                                    for h in range(2):
                                        fw.op("pe", lambda e, pi_=pi_, h=h, nxt=nxt: e.matmul(pi_[:, 256 + h * 64:256 + (h + 1) * 64], lhsT=nxt[:, 1, h, :], rhs=pm[:, h, :], start=True, stop=True),
                                              reads=[tnxt, tpm], writes=[tpi])
                                        yield
                                    if lev < 4:
                                        pt_, tpt = psT.next()
                                        for h in range(2):
                                            fw.op("pe", lambda e, pt_=pt_, h=h, nxt=nxt: e.transpose(pt_[:, h * 64:(h + 1) * 64], nxt[:, 1, h, :], idt[0:64, 0:64]),
                                                  reads=[tnxt, tc], writes=[tpt])
                                            yield
                                    fw.op("dve", lambda e, pi_=pi_: e.tensor_tensor(out=pm[:], in0=pm[:], in1=pi_[:, 256:384].rearrange("p (h t) -> p h t", h=2), op=ALU.add),
                                          reads=[tpi, tpm], writes=[tpm])
                                    yield
                                    if lev < 4:
                                        fw.op("act", lambda e, nxt=nxt, pt_=pt_: e.copy(out=nxt[:, 0], in_=pt_[:, 0:128].rearrange("p (h t) -> p h t", h=2)), reads=[tpt], writes=[tnxt])
                                        yield
                                    cur, tcur = nxt, tnxt
                                res[i] = (a1, ta1, a45, ta45, pm, tpm)

                            def chain_gen(i):
                                cc0, cc1 = i * CHK, (i + 1) * CHK
                                gch = i + off // CHK
                                a1, ta1, a45, ta45, pm, tpm = res.pop(i)
                                pg, tpg = psG.next()
                                for h in range(2):
                                    hp = slice(h * 64, (h + 1) * 64)
                                    fw.op("pe", lambda e, h=h, hp=hp: e.matmul(pg[:, h * 64:(h + 1) * 64], lhsT=aT[hp, cc0:cc1], rhs=Hb[hp, :], start=True, stop=False),
                                          reads=[tops, tH], writes=[tpg])
                                    yield
                                    fw.op("pe", lambda e, h=h, hp=hp: e.matmul(pg[:, h * 64:(h + 1) * 64], lhsT=a1[:, h, :], rhs=vtok[:, gch, hp], start=False, stop=True),
                                          reads=[ta1, tvt], writes=[tpg])
                                    yield
                                gs, tgs = Gs.next()
                                fw.op("act", lambda e: e.copy(out=gs[:], in_=pg[:, 0:128]), reads=[tpg], writes=[tgs])
                                yield
                                for h in range(2):
                                    fw.op("pe", lambda e, h=h: e.matmul(pg[:, 128 + h * 64:128 + (h + 1) * 64], lhsT=pm[:, h, :], rhs=gs[:, h * 64:(h + 1) * 64], start=True, stop=True),
                                          reads=[tpm, tgs], writes=[tpg])
                                    yield
                                ub, tub = Ub.next()
                                fw.op("dve", lambda e: e.tensor_copy(out=ub[:], in_=pg[:, 128:256]), reads=[tpg], writes=[tub])
                                yield
                                ph, tph = psH.next()
                                py, tpy = psY.next()
                                for h in range(2):
                                    hp = slice(h * 64, (h + 1) * 64)
                                    fw.op("pe", lambda e, h=h, hp=hp: e.matmul(ph[hp, 0:64], lhsT=btok[:, i, hp], rhs=ub[:, hp], start=True, stop=False),
                                          reads=[ttok, tub], writes=[tph])
                                    yield
                                    fw.op("pe", lambda e, h=h, hp=hp: e.matmul(ph[hp, 0:64], lhsT=ktok[:, i, hp], rhs=vtok[:, gch, hp], start=False, stop=True),
                                          reads=[ttok, tvt], writes=[tph])
                                    yield
                                for h in range(2):
                                    hp = slice(h * 64, (h + 1) * 64)
                                    fw.op("pe", lambda e, h=h, hp=hp: e.matmul(py[0:64, hp], lhsT=rT[hp, cc0:cc1], rhs=Hb[hp, :], start=True, stop=False),
                                          reads=[tops, tH], writes=[tpy])
                                    yield
                                    fw.op("pe", lambda e, h=h, hp=hp: e.matmul(py[0:64, hp], lhsT=a45[:, h, :], rhs=ub[:, hp], start=False, stop=False),
                                          reads=[ta45, tub], writes=[tpy])
                                    yield
                                    fw.op("pe", lambda e, h=h, hp=hp: e.matmul(py[0:64, hp], lhsT=a45[:, 2 + h, :], rhs=vtok[:, gch, hp], start=False, stop=True),
                                          reads=[ta45, tvt], writes=[tpy])
                                    yield
                                fw.op("dve", lambda e: e.scalar_tensor_tensor(out=Hs[:], in0=Hs[:], scalar=pC[:, i:i + 1], in1=ph[:, 0:64], op0=ALU.mult, op1=ALU.add),
                                      reads=[tph, tH, tops, tpy], writes=[tH])
                                yield
                                fw.op("act", lambda e: e.copy(out=Hb[:], in_=Hs[:]), reads=[tH, tpy, tpg], writes=[tH])
                                yield
                                ys, tys = Ys.next()
                                fw.op("act", lambda e: e.copy(out=ys[:], in_=py[0:64, 0:128]), reads=[tpy], writes=[tys])
                                yield
                                fw.op("pe", lambda e: e.transpose(py[:, 256:320], ys[:], idt[0:64, 0:64]), reads=[tys, tc], writes=[tpy])
                                yield
                                y0 = off + cc0
                                if y0 >= T:
                                    y0 -= T
                                fw.op("dve", lambda e: e.tensor_tensor(out=yacc[:, y0:y0 + CHK], in0=yacc[:, y0:y0 + CHK], in1=py[:, 256:320], op=ALU.add),
                                      reads=[tpy, tya], writes=[tya])
                                yield

                            for _ in par_gen(order[0]):
                                pass
                            for idx, i in enumerate(order):
                                gp = par_gen(order[idx + 1]) if idx + 1 < len(order) else iter(())
                                gc = chain_gen(i)
                                done_p = done_c = False
                                while not (done_p and done_c):
                                    for _ in range(2):
                                        if not done_p:
                                            try:
                                                next(gp)
                                            except StopIteration:
                                                done_p = True
                                    if not done_c:
                                        try:
                                            next(gc)
                                        except StopIteration:
                                            done_c = True
                            fw.barrier()
                with ExitStack() as es4:
                    P4 = Pool_(fw, es4)
                    psr = Rot([P4.ps() for _ in range(3)])
                    xc = P4.sb([128, 512], F32)
                    sq = P4.sb([128, 512], F32)
                    rs = P4.sb([128, 512], F32)
                    txc, tsq, trs = Tok(), Tok(), Tok()
                    stg = Rot([P4.sb([128, 512], F32) for _ in range(2)])
                    tout = Tok()
                    for (lo, hi, ic) in SEGS:
                        n = hi - lo
                        ps, tps = psr.next()
                        fw.op("pe", lambda e, ps=ps, lo=lo, hi=hi, n=n: e.matmul(ps[:, :n], lhsT=bd1[:], rhs=yacc[:, lo:hi], start=True, stop=True), reads=[tya, tc], writes=[tps])
                        fw.op("dve", lambda e, ps=ps, lo=lo, hi=hi, n=n: e.scalar_tensor_tensor(out=xc[:, :n], in0=ps[:, :n], scalar=-1.0 / 64, in1=yacc[:, lo:hi], op0=ALU.mult, op1=ALU.add),
                              reads=[tps, tya], writes=[txc])
                        fw.op("act", lambda e, n=n: e.activation(out=sq[:, :n], in_=xc[:, :n], func=AF.Square), reads=[txc], writes=[tsq])
                        ps2, tps2 = psr.next()
                        fw.op("pe", lambda e, ps2=ps2, n=n: e.matmul(ps2[:, :n], lhsT=bd1[:], rhs=sq[:, :n], start=True, stop=True), reads=[tsq, tc], writes=[tps2])
                        rstd_from_ps(fw, rs, trs, ps2, tps2, n, 1.0 / 64, epsl[:, 0:1], tpp)
                        fw.op("dve", lambda e, n=n: e.tensor_tensor(out=xc[:, :n], in0=xc[:, :n], in1=rs[:, :n], op=ALU.mult), reads=[txc, trs], writes=[txc])
                        fw.op("act", lambda e, n=n: e.activation(out=xc[:, :n], in_=xc[:, :n], func=AF.Identity, scale=pp[:, 8, c:c + 1], bias=pp[:, 9, c:c + 1]),
                              reads=[txc, tpp], writes=[txc])
                        fw.op("pool", lambda e, lo=lo, hi=hi, n=n: e.tensor_tensor(out=xc[:, :n], in0=xc[:, :n], in1=bon[:, lo:hi], op=ALU.add), reads=[txc, tbon], writes=[txc])
                        ps3, tps3 = psr.next()
                        fw.op("pe", lambda e, ps3=ps3, lo=lo, hi=hi, n=n: e.matmul(ps3[:, :n], lhsT=wB[64:128, c * 128:(c + 1) * 128], rhs=lorB[64:128, lo:hi], start=True, stop=True),
                              reads=[tlw, tlor], writes=[tps3])
                        sg, tsg = stg.next()
                        fw.op("dve", lambda e, sg=sg, ps3=ps3, n=n: e.tensor_tensor(out=sg[:, :n], in0=ps3[:, :n], in1=xc[:, :n], op=ALU.mult), reads=[tps3, txc], writes=[tsg])
                        fw.dma("sp", RWT[c * 128:(c + 1) * 128, lo:hi], sg[:, :n], reads=[tsg], writes=[tout])
                    fw.barrier()
        fw.barrier()
NCORES = 8
DEPTH = 4
S5_KEYS = ["s5_a_re", "s5_a_im", "s5_log_step", "s5_b_re", "s5_b_im", "s5_c_re", "s5_c_im", "s5_d", "s5_glu_w", "s5_glu_b", "s5_out_g"]
RW_KEYS = ["rw_mu", "rw_w0", "rw_w2", "rw_a0", "rw_a2", "rw_g2", "rw_k_k", "rw_k_a", "rw_r_k", "rw_ln_g", "rw_ln_b"]
LAYER_KEYS = (["norm1_g", "norm2_g", "mod_w", "mod_b", "w_in", "w_out", "att_qn_g", "att_kn_g", "att_out_g",
               "ffn_up", "ffn_conv_w", "ffn_conv_b", "ffn_down"] + S5_KEYS + RW_KEYS)
SHARED_KEYS = ["c_ctx", "final_g", "ident", "POS", "CST", "RWMASK", "RWBD"]
PERCORE_KEYS = ["x_b", "ctx_b", "c_b"]
SCRATCH = {"ZT": [2048, TE], "VT": [T, 128], "MOD": [128, 48, 2], "S5T": [256, T], "ATT_T": [512, T], "RWT": [256, T],
           "XT1": [DM, T], "XA": [DM, T], "XB": [DM, T]}


def build_fused(depth=DEPTH):
    nc = bass.Bass("TRN2", target_bir_lowering=False)
    decl = {}

    def ext(name, shape, dt, kind):
        if name not in decl:
            decl[name] = nc.dram_tensor(name, list(shape), dt, kind=kind).ap()
        return decl[name]

    def make_io(l):
        xin = "XA" if l % 2 == 0 else "XB"
        xout = "XB" if l % 2 == 0 else "XA"

        def io(name, shape, dt, role):
            if name in LAYER_KEYS:
                full = ext(name, [DEPTH] + list(shape), dt, "ExternalInput")
                return full[l]
            if name in SHARED_KEYS or name in PERCORE_KEYS:
                return ext(name, shape, dt, "ExternalInput")
            if name == "OUT":
                return ext(name, shape, dt, "ExternalOutput")
            if name == "XT":
                name = xin
            elif name == "XT2":
                name = xout
            return ext(name, SCRATCH[name], dt, "Internal")
        return io
    with ExitStack() as es:
        fw = FW(nc, es)
        stage_p0(fw, make_io(0))
        for l in range(depth):
            io = make_io(l)
            for st in (stage_p1, stage_p2, stage_p3, stage_p4, stage_p5a, stage_p5b):
                st(fw, io)
        stage_p6(fw, make_io(depth))
        fw.barrier()
    return nc, fw


def tile_up(up):
    lead = up.shape[:-2]
    v = up.reshape(lead + (8, 128, 44, 128))
    nd = len(lead)
    v = np.transpose(v, tuple(range(nd)) + (nd + 2, nd + 1, nd + 0, nd + 3))
    return np.ascontiguousarray(v).reshape(lead + (44, 128, 1024))


_FUSED = {}


def kernel(**inp):
    inp = {k: np.ascontiguousarray(np.asarray(v)) for k, v in inp.items()}
    if "nc" not in _FUSED:
        _FUSED["nc"], _FUSED["fw"] = build_fused()
    nc = _FUSED["nc"]
    pos, cst = host_consts()
    rwm, rwbd = rw_consts()
    shared = {k: inp[k] for k in LAYER_KEYS if k != "rw_r_k"}
    shared["rw_r_k"] = inp["rw_r_k"].reshape(DEPTH, 256)
    shared["ffn_up"] = tile_up(inp["ffn_up"])
    shared.update(c_ctx=inp["c_ctx"], final_g=inp["final_g"], ident=np.eye(128, dtype=np.float32),
                  POS=pos, CST=cst, RWMASK=rwm, RWBD=rwbd)
    in_maps = [dict(shared, x_b=inp["x"][b], ctx_b=inp["ctx"][b], c_b=inp["c"][b]) for b in range(NCORES)]
    res = run_bass_kernel_spmd(nc, in_maps, core_ids=list(range(NCORES)))
    return np.stack([res.results[b]["OUT"] for b in range(NCORES)], 0).astype(np.float32)
```

```python
import math
import numpy as np
from contextlib import ExitStack
import concourse.bass as bass
import concourse.mybir as mybir
from concourse.bass_utils import run_bass_kernel_spmd

F32 = mybir.dt.float32
F32R = mybir.dt.float32r
BF16 = mybir.dt.bfloat16
I32 = mybir.dt.int32
ALU = mybir.AluOpType
AF = mybir.ActivationFunctionType
AX = mybir.AxisListType

T = 2304
TE = 2560
NCTX = 256
NLAT = 2048
DM = 1024
SEGS = [(0, 256, 1), (256, 768, 0), (768, 1280, 0), (1280, 1792, 0), (1792, 2304, 0)]
RMS_EPS = 1e-6


class Tok:
    __slots__ = ("w", "r")

    def __init__(self):
        self.w = None
        self.r = {}


class FW:
    ENG = ("pe", "dve", "act", "pool", "sp")
    NDMA = 8

    def __init__(self, nc, es):
        self.nc = nc
        self.es = es
        self.eng = {"pe": nc.tensor, "dve": nc.vector, "act": nc.scalar,
                    "pool": nc.gpsimd, "sp": nc.sync}
        self.sem = {}
        self.cnt = {}
        for e in self.ENG:
            self.sem[e] = es.enter_context(nc.semaphore("s_" + e))
            self.cnt[e] = 0
        self.dq = {}
        for q in ("sp", "pool", "act"):
            ring = []
            for i in range(self.NDMA):
                k = "d_%s_%d" % (q, i)
                self.sem[k] = es.enter_context(nc.semaphore(k))
                self.cnt[k] = 0
                ring.append(k)
            self.dq[q] = [ring, 0]
        self.seen = {e: {} for e in self.ENG}
        self.attach = True
        self.ninst = 0
        self.uid = 0

    def name(self, p):
        self.uid += 1
        return "%s_%d" % (p, self.uid)

    def _deps(self, reads, writes):
        deps = {}
        for t in reads:
            if t.w is not None and deps.get(t.w[0], 0) < t.w[1]:
                deps[t.w[0]] = t.w[1]
        for t in writes:
            if t.w is not None and deps.get(t.w[0], 0) < t.w[1]:
                deps[t.w[0]] = t.w[1]
            for k, v in t.r.items():
                if deps.get(k, 0) < v:
                    deps[k] = v
        return deps

    def _wait(self, e, deps):
        seen = self.seen[e]
        for k, v in deps.items():
            if seen.get(k, 0) < v:
                self.eng[e].wait_ge(self.sem[k], v)
                seen[k] = v

    def op(self, e, fn, reads=(), writes=()):
        deps = self._deps(reads, writes)
        seen = self.seen[e]
        need = [(k, v) for k, v in deps.items() if seen.get(k, 0) < v]
        att = None
        if need and self.attach:
            att = need.pop()
        for k, v in need:
            self.eng[e].wait_ge(self.sem[k], v)
            seen[k] = v
        inst = fn(self.eng[e])
        if att is not None:
            inst._wait_ge(self.sem[att[0]], att[1])
            seen[att[0]] = att[1]
        self.cnt[e] += 1
        inst.then_inc(self.sem[e], 1)
        v = self.cnt[e]
        for t in reads:
            t.r[e] = v
        for t in writes:
            t.w = (e, v)
            t.r = {}
        self.ninst += 1
        return inst

    def dma(self, q, out, in_, reads=(), writes=(), **kw):
        ring, idx = self.dq[q]
        k = ring[idx % len(ring)]
        self.dq[q][1] = idx + 1
        deps = self._deps(reads, writes)
        if self.cnt[k] > 0:
            deps[k] = max(deps.get(k, 0), self.cnt[k])
        self._wait(q, deps)
        inst = self.eng[q].dma_start(out=out, in_=in_, **kw)
        self.cnt[k] += 16
        inst.then_inc(self.sem[k], 16)
        v = self.cnt[k]
        for t in reads:
            t.r[k] = v
        for t in writes:
            t.w = (k, v)
            t.r = {}
        self.ninst += 1
        return inst

    def barrier(self, engines=None):
        allv = {k: v for k, v in self.cnt.items() if v > 0}
        for e in (engines or self.ENG):
            self._wait(e, allv)


class Pool_:
    def __init__(self, fw, es):
        self.fw = fw
        self.es = es
        self.nc = fw.nc

    def sb(self, shape, dt, name="t"):
        return self.es.enter_context(self.nc.sbuf_tensor(self.fw.name(name), list(shape), dt))

    def ps(self, shape=(128, 512), dt=F32, name="ps"):
        return self.es.enter_context(self.nc.psum_tensor(self.fw.name(name), list(shape), dt))


class Rot:
    def __init__(self, bufs):
        self.bufs = bufs
        self.toks = [Tok() for _ in bufs]
        self.i = 0

    def next(self):
        j = self.i % len(self.bufs)
        self.i += 1
        return self.bufs[j], self.toks[j]


def load_w_bf16(fw, P, W, rows, cols, tok, name="w", q="pool", chunk=2048, stage=None):
    kt = rows // 128
    wb = P.sb([128, kt, cols], BF16, name)
    chunk = min(chunk, cols)
    st = stage or Rot([P.sb([128, chunk], F32, "wstg") for _ in range(3)])
    for k in range(kt):
        for c0 in range(0, cols, chunk):
            n = min(chunk, cols - c0)
            sg, tsg = st.next()
            fw.dma("sp", sg[:, :n], W[k * 128:(k + 1) * 128, c0:c0 + n], writes=[tsg])
            fw.op("pool", lambda e, sg=sg, k=k, c0=c0, n=n: e.tensor_copy(out=wb[:, k, c0:c0 + n], in_=sg[:, :n]), reads=[tsg], writes=[tok])
    return wb


def stage_p0(fw, io):
    nc = fw.nc
    xb = io("x_b", [NLAT, DM], F32, "in")
    cb = io("ctx_b", [NCTX, DM], F32, "in")
    ident = io("ident", [128, 128], F32, "in")
    XT = io("XT", [DM, T], F32, "out")
    with ExitStack() as es:
        P = Pool_(fw, es)
        idt = P.sb([128, 128], F32)
        tid = Tok()
        fw.dma("sp", idt[:], ident, writes=[tid])
        xt = P.sb([128, 8, T], F32)
        txt = Tok()
        xin = Rot([P.sb([128, DM], F32) for _ in range(3)])
        pss = Rot([P.ps() for _ in range(4)])
        for tt in range(18):
            src = cb[tt * 128:(tt + 1) * 128, :] if tt < 2 else xb[(tt - 2) * 128:(tt - 1) * 128, :]
            xi, txi = xin.next()
            fw.dma("sp", xi[:], src, writes=[txi])
            for half in range(2):
                ps, tps = pss.next()
                for k in range(4):
                    kk = half * 4 + k
                    fw.op("pe", lambda e, ps=ps, k=k, kk=kk, xi=xi: e.transpose(
                        ps[:, k * 128:(k + 1) * 128], xi[:, kk * 128:(kk + 1) * 128], idt[:]),
                        reads=[txi, tid], writes=[tps])
                eng = "dve" if half == 0 else "act"
                outap = xt[:, half * 4:half * 4 + 4, tt * 128:(tt + 1) * 128]
                inap = ps[:].rearrange("p (k t) -> p k t", k=4)
                if eng == "dve":
                    fw.op("dve", lambda e, o=outap, i=inap: e.tensor_copy(out=o, in_=i), reads=[tps], writes=[txt])
                else:
                    fw.op("act", lambda e, o=outap, i=inap: e.copy(out=o, in_=i), reads=[tps], writes=[txt])
        tout = Tok()
        for k in range(8):
            fw.dma("sp", XT[k * 128:(k + 1) * 128, :], xt[:, k, :], reads=[txt], writes=[tout])
        fw.barrier()


def make_AB(fw, P, MODs, tmod, g_ap, sh_base, sc_base):
    g = P.sb([128, 8], F32)
    tg = Tok()
    fw.dma("sp", g[:], g_ap.rearrange("(k p) -> p k", p=128), writes=[tg], allow_slow_non_contiguous=True)
    AB = P.sb([128, 2, 2, 8], F32)
    tab = Tok()
    for ic in range(2):
        fw.op("dve", lambda e, ic=ic: e.tensor_scalar(out=AB[:, ic, 0, :], in0=MODs[:, sc_base:sc_base + 8, ic],
                                                      scalar1=1.0, scalar2=None, op0=ALU.add),
              reads=[tmod], writes=[tab])
        fw.op("dve", lambda e, ic=ic: e.tensor_tensor(out=AB[:, ic, 0, :], in0=AB[:, ic, 0, :], in1=g[:], op=ALU.mult),
              reads=[tg, tab], writes=[tab])
        fw.op("dve", lambda e, ic=ic: e.tensor_copy(out=AB[:, ic, 1, :], in_=MODs[:, sh_base:sh_base + 8, ic]),
              reads=[tmod], writes=[tab])
    return AB, tab


def norm_mod_seg(fw, P, st, xs, txs, n, ic, AB, tab, outs, touts):
    sq, ones, tones, psr, rs, tmpr = st["sq"], st["ones"], st["tones"], st["psr"], st["rs"], st["tmpr"]
    tsq, trs = st["tsq"], st["trs"]
    fw.op("act", lambda e: e.activation(out=sq[:, :, :n], in_=xs[:, :, :n], func=AF.Square), reads=[txs], writes=[tsq])
    ps, tps = psr.next()
    for k in range(8):
        fw.op("pe", lambda e, k=k: e.matmul(ps[:, :n], lhsT=ones[:], rhs=sq[:, k, :n], start=(k == 0), stop=(k == 7)),
              reads=[tsq, tones], writes=[tps])
    fw.op("act", lambda e: e.activation(out=rs[:, :n], in_=ps[:, :n], func=AF.Ln, scale=1.0 / DM, bias=st["eps"][:, 0:1]),
          reads=[tps, st["teps"]], writes=[trs])
    fw.op("act", lambda e: e.activation(out=rs[:, :n], in_=rs[:, :n], func=AF.Exp, scale=-0.5), reads=[trs], writes=[trs])
    for k in range(8):
        tmp, ttmp = tmpr.next()
        fw.op("dve", lambda e, k=k, tmp=tmp: e.tensor_tensor(out=tmp[:, :n], in0=xs[:, k, :n], in1=rs[:, :n], op=ALU.mult),
              reads=[txs, trs], writes=[ttmp])
        for o in outs(k):
            fw.op("act", lambda e, k=k, tmp=tmp, o=o: e.activation(out=o, in_=tmp[:, :n], func=AF.Identity,
                                                                   scale=AB[:, ic, 0, k:k + 1], bias=AB[:, ic, 1, k:k + 1]),
                  reads=[ttmp, tab], writes=touts)


def norm_state(fw, P):
    st = {}
    st["sq"] = P.sb([128, 8, 512], BF16)
    st["tsq"] = Tok()
    st["ones"] = P.sb([128, 128], BF16)
    st["tones"] = Tok()
    fw.op("pool", lambda e: e.memset(st["ones"][:], 1.0), writes=[st["tones"]])
    st["eps"] = P.sb([128, 1], F32)
    st["teps"] = Tok()
    fw.op("pool", lambda e: e.memset(st["eps"][:], RMS_EPS), writes=[st["teps"]])
    st["psr"] = Rot([P.ps() for _ in range(2)])
    st["rs"] = P.sb([128, 512], F32)
    st["trs"] = Tok()
    st["tmpr"] = Rot([P.sb([128, 512], F32) for _ in range(2)])
    return st


def stage_p1(fw, io):
    XT = io("XT", [DM, T], F32, "in")
    c_b = io("c_b", [DM], F32, "in")
    c_ctx = io("c_ctx", [DM], F32, "in")
    mod_w = io("mod_w", [DM, 6 * DM], F32, "in")
    mod_b = io("mod_b", [6 * DM], F32, "in")
    n1g = io("norm1_g", [DM], F32, "in")
    w_in = io("w_in", [DM, 1984], F32, "in")
    MOD = io("MOD", [128, 48, 2], F32, "out")
    ZT = io("ZT", [2048, TE], F32, "out")
    VT = io("VT", [T, 128], F32, "out")
    XTv = XT.rearrange("(k p) t -> p k t", p=128)
    with ExitStack() as es:
        P = Pool_(fw, es)
        wstage = Rot([P.sb([128, 2048], F32, "wstg") for _ in range(3)])
        tmw = Tok()
        cc = P.sb([128, 8, 2], F32)
        tcc = Tok()
        fw.dma("sp", cc[:, :, 0], c_b.rearrange("(k p) -> p k", p=128), writes=[tcc], allow_slow_non_contiguous=True)
        fw.dma("sp", cc[:, :, 1], c_ctx.rearrange("(k p) -> p k", p=128), writes=[tcc], allow_slow_non_contiguous=True)
        scb = P.sb([128, 8, 2], BF16)
        tscb = Tok()
        fw.op("act", lambda e: e.activation(out=scb[:], in_=cc[:], func=AF.Silu), reads=[tcc], writes=[tscb])
        mb = P.sb([128, 48], F32)
        tmb = Tok()
        fw.dma("sp", mb[:], mod_b.rearrange("(j p) -> p j", p=128), writes=[tmb], allow_slow_non_contiguous=True)
        MODs = P.sb([128, 48, 2], F32)
        tmod = Tok()
        with ExitStack() as es2:
            P2 = Pool_(fw, es2)
            mwb = load_w_bf16(fw, P2, mod_w, DM, 6 * DM, tmw, "modw", stage=wstage)
            psm = P2.ps([128, 512])
            tpsm = Tok()
            for j in range(48):
                for k in range(8):
                    fw.op("pe", lambda e, j=j, k=k: e.matmul(psm[:, 2 * j:2 * j + 2], lhsT=mwb[:, k, j * 128:(j + 1) * 128],
                                                             rhs=scb[:, k, :], start=(k == 0), stop=(k == 7)),
                          reads=[tmw, tscb], writes=[tpsm])
            for ic in range(2):
                fw.op("dve", lambda e, ic=ic: e.tensor_tensor(
                    out=MODs[:, :, ic], in0=psm[:, 0:96].rearrange("p (j c) -> p j c", c=2)[:, :, ic], in1=mb[:], op=ALU.add),
                    reads=[tpsm, tmb], writes=[tmod])
            fw.barrier()
        tmo = Tok()
        fw.dma("sp", MOD, MODs[:], reads=[tmod], writes=[tmo])
        AB, tab = make_AB(fw, P, MODs, tmod, n1g, 0, 8)
        tw = Tok()
        wb = load_w_bf16(fw, P, w_in, DM, 1984, tw, "win", stage=wstage)
        hT = P.sb([128, 8, TE], BF16)
        thT = Tok()
        st = norm_state(fw, P)
        xr = Rot([P.sb([128, 8, 512], F32) for _ in range(2)])
        for (lo, hi, ic) in SEGS:
            n = hi - lo
            xs, txs = xr.next()
            fw.dma("sp", xs[:, :, :n], XTv[:, :, lo:hi], writes=[txs])

            def outs(k, lo=lo, hi=hi, ic=ic):
                o = [hT[:, k, lo:hi]]
                if ic:
                    o.append(hT[:, k, T + lo:T + hi])
                return o
            norm_mod_seg(fw, P, st, xs, txs, n, ic, AB, tab, outs, [thT])
        psr = Rot([P.ps() for _ in range(4)])
        stg = Rot([P.sb([128, 512], F32) for _ in range(4)])
        tz = Tok()
        cnt = 0
        for nt in range(16):
            if nt == 7:
                continue
            M = 64 if nt == 15 else 128
            for cc_ in range(5):
                c0 = cc_ * 512
                ps, tps = psr.next()
                for k in range(8):
                    fw.op("pe", lambda e, ps=ps, k=k, nt=nt, M=M, c0=c0: e.matmul(
                        ps[0:M, :], lhsT=wb[:, k, nt * 128:nt * 128 + M], rhs=hT[:, k, c0:c0 + 512],
                        start=(k == 0), stop=(k == 7)), reads=[tw, thT], writes=[tps])
                sg, tsg = stg.next()
                if cnt % 2 == 0:
                    fw.op("dve", lambda e, sg=sg, ps=ps, M=M: e.tensor_copy(out=sg[0:M, :], in_=ps[0:M, :]), reads=[tps], writes=[tsg])
                else:
                    fw.op("act", lambda e, sg=sg, ps=ps, M=M: e.copy(out=sg[0:M, :], in_=ps[0:M, :]), reads=[tps], writes=[tsg])
                cnt += 1
                fw.dma("sp", ZT[nt * 128:nt * 128 + M, c0:c0 + 512], sg[0:M, :], reads=[tsg], writes=[tz])
        for tt in range(18):
            ps, tps = psr.next()
            for k in range(8):
                fw.op("pe", lambda e, ps=ps, k=k, tt=tt: e.matmul(
                    ps[:, 0:128], lhsT=hT[:, k, tt * 128:(tt + 1) * 128], rhs=wb[:, k, 896:1024],
                    start=(k == 0), stop=(k == 7)), reads=[tw, thT], writes=[tps])
            sg, tsg = stg.next()
            fw.op("dve", lambda e, sg=sg, ps=ps: e.tensor_copy(out=sg[:, 0:128], in_=ps[:, 0:128]), reads=[tps], writes=[tsg])
            fw.dma("sp", VT[tt * 128:(tt + 1) * 128, :], sg[:, 0:128], reads=[tsg], writes=[tz])
        fw.barrier()


def build_program(stage_fns):
    nc = bass.Bass("TRN2", target_bir_lowering=False)
    decl = {}

    def io(name, shape, dt, role):
        if name in decl:
            return decl[name][0]
        kind = "ExternalInput" if role == "in" else "ExternalOutput"
        ap = nc.dram_tensor(name, list(shape), dt, kind=kind).ap()
        decl[name] = (ap, role, shape)
        return ap
    with ExitStack() as es:
        fw = FW(nc, es)
        for fn in stage_fns:
            fn(fw, io)
        fw.barrier()
    return nc, decl, fw


_PROG_CACHE = {}


def run_stage(key, stage_fns, in_maps, ncores):
    if key not in _PROG_CACHE:
        _PROG_CACHE[key] = build_program(stage_fns)
    nc, decl, fw = _PROG_CACHE[key]
    res = run_bass_kernel_spmd(nc, in_maps, core_ids=list(range(ncores)))
    return res.results


def rstd_from_ps(fw, rs, trs, ps, tps, n, scale, epsap, teps, rows=128):
    fw.op("act", lambda e: e.activation(out=rs[0:rows, :n], in_=ps[0:rows, :n], func=AF.Ln, scale=scale, bias=epsap),
          reads=[tps, teps], writes=[trs])
    fw.op("act", lambda e: e.activation(out=rs[0:rows, :n], in_=rs[0:rows, :n], func=AF.Exp, scale=-0.5), reads=[trs], writes=[trs])


def stage_p3(fw, io):
    ZT = io("ZT", [2048, TE], F32, "in")
    VT = io("VT", [T, 128], F32, "in")
    qn_g = io("att_qn_g", [64], F32, "in")
    kn_g = io("att_kn_g", [64], F32, "in")
    og = io("att_out_g", [512], F32, "in")
    POS = io("POS", [128, NLAT], F32, "in")
    CST = io("CST", [128, 260], F32, "in")
    ATT = io("ATT_T", [512, T], F32, "out")
    with ExitStack() as es:
        P = Pool_(fw, es)
        cst = P.sb([128, 260], F32)
        tc = Tok()
        fw.dma("sp", cst[:], CST, writes=[tc])
        permb = P.sb([128, 128], BF16)
        bdb = P.sb([128, 128], BF16)
        fw.op("dve", lambda e: e.tensor_copy(out=permb[:], in_=cst[:, 1:129]), reads=[tc], writes=[tc])
        fw.op("dve", lambda e: e.tensor_copy(out=bdb[:], in_=cst[:, 129:257]), reads=[tc], writes=[tc])
        eps = P.sb([128, 2], F32)
        teps = Tok()
        fw.op("pool", lambda e: e.memset(eps[:, 0:1], RMS_EPS), writes=[teps])
        fw.op("pool", lambda e: e.memset(eps[:, 1:2], 0.0), writes=[teps])
        gq = P.sb([128, 2], F32)
        tg = Tok()
        for h in range(2):
            fw.dma("sp", gq[h * 64:(h + 1) * 64, 0:1], qn_g.rearrange("(p o) -> p o", o=1), writes=[tg])
            fw.dma("sp", gq[h * 64:(h + 1) * 64, 1:2], kn_g.rearrange("(p o) -> p o", o=1), writes=[tg])
        cos = P.sb([128, NLAT], F32)
        sin = P.sb([128, NLAT], F32)
        ttab = Tok()
        with ExitStack() as es2:
            P2 = Pool_(fw, es2)
            ang = P2.sb([128, NLAT], F32)
            tmpf = P2.sb([128, NLAT], F32)
            tmpi = P2.sb([128, NLAT], I32)
            ta = Tok()
            fw.dma("sp", ang[:], POS, writes=[ta])
            fw.op("dve", lambda e: e.tensor_scalar(out=ang[:], in0=ang[:], scalar1=cst[:, 0:1], scalar2=None, op0=ALU.mult),
                  reads=[ta, tc], writes=[ta])
            for (tab, off) in ((sin, 0.0), (cos, math.pi / 2)):
                fw.op("dve", lambda e, off=off: e.tensor_scalar(out=tmpi[:], in0=ang[:], scalar1=off, scalar2=1.0 / (2 * math.pi),
                                                                op0=ALU.add, op1=ALU.mult), reads=[ta], writes=[ta])
                fw.op("dve", lambda e: e.tensor_copy(out=tmpf[:], in_=tmpi[:]), reads=[ta], writes=[ta])
                fw.op("dve", lambda e: e.scalar_tensor_tensor(out=tmpf[:], in0=tmpf[:], scalar=-2 * math.pi, in1=ang[:],
                                                              op0=ALU.mult, op1=ALU.add), reads=[ta], writes=[ta])
                fw.op("dve", lambda e, off=off: e.tensor_scalar(out=tmpf[:], in0=tmpf[:], scalar1=off, scalar2=math.pi,
                                                                op0=ALU.add, op1=ALU.min), reads=[ta], writes=[ta])
                fw.op("dve", lambda e: e.tensor_scalar(out=tmpf[:], in0=tmpf[:], scalar1=-math.pi, scalar2=None, op0=ALU.max),
                      reads=[ta], writes=[ta])
                fw.op("act", lambda e, tab=tab: e.activation(out=tab[:], in_=tmpf[:], func=AF.Sin), reads=[ta], writes=[ttab])
            fw.barrier()
        qb = P.sb([128, 4, T], BF16)
        kd = P.sb([128, 2, T], BF16)
        tq = Tok()
        with ExitStack() as es2:
            P2 = Pool_(fw, es2)
            raw = Rot([P2.sb([128, 512], F32) for _ in range(2)])
            sqr = Rot([P2.sb([128, 512], BF16) for _ in range(2)])
            psr = Rot([P2.ps() for _ in range(2)])
            psr2 = Rot([P2.ps() for _ in range(2)])
            rsr = Rot([P2.sb([128, 512], F32) for _ in range(2)])
            nbr = Rot([P2.sb([128, 512], BF16) for _ in range(2)])
            t1r = Rot([P2.sb([128, 512], F32) for _ in range(2)])
            t2r = Rot([P2.sb([128, 512], F32) for _ in range(2)])
            items = [("q", j) for j in range(4)] + [("k", g) for g in range(2)]
            for (kind, j) in items:
                for (lo, hi, ic) in SEGS:
                    n = hi - lo
                    rw, trw = raw.next()
                    if kind == "q":
                        fw.dma("sp", rw[:, :n], ZT[256 + j * 128:256 + (j + 1) * 128, lo:hi], writes=[trw])
                        gcol = 0
                        dst = qb[:, j, lo:hi]
                    else:
                        for h in range(2):
                            fw.dma("sp", rw[h * 64:(h + 1) * 64, :n], ZT[768 + j * 64:768 + (j + 1) * 64, lo:hi], writes=[trw])
                        gcol = 1
                        dst = kd[:, j, lo:hi]
                    sq, tsq = sqr.next()
                    fw.op("act", lambda e, sq=sq, rw=rw, n=n: e.activation(out=sq[:, :n], in_=rw[:, :n], func=AF.Square), reads=[trw], writes=[tsq])
                    ps, tps = psr.next()
                    fw.op("pe", lambda e, ps=ps, sq=sq, n=n: e.matmul(ps[:, :n], lhsT=bdb[:], rhs=sq[:, :n], start=True, stop=True),
                          reads=[tsq, tc], writes=[tps])
                    rs, trs = rsr.next()
                    rstd_from_ps(fw, rs, trs, ps, tps, n, 1.0, eps[:, 0:1], teps)
                    t1, tt1 = t1r.next()
                    fw.op("dve", lambda e, t1=t1, rw=rw, rs=rs, n=n, gcol=gcol: e.scalar_tensor_tensor(
                        out=t1[:, :n], in0=rw[:, :n], scalar=gq[:, gcol:gcol + 1], in1=rs[:, :n], op0=ALU.mult, op1=ALU.mult),
                        reads=[trw, trs, tg], writes=[tt1])
                    if ic:
                        fw.op("act", lambda e, dst=dst, t1=t1, n=n: e.copy(out=dst, in_=t1[:, :n]), reads=[tt1], writes=[tq])
                        continue
                    nb, tnb = nbr.next()
                    fw.op("act", lambda e, nb=nb, t1=t1, n=n: e.copy(out=nb[:, :n], in_=t1[:, :n]), reads=[tt1], writes=[tnb])
                    ps2, tps2 = psr2.next()
                    fw.op("pe", lambda e, ps2=ps2, nb=nb, n=n: e.matmul(ps2[:, :n], lhsT=permb[:], rhs=nb[:, :n], start=True, stop=True),
                          reads=[tnb, tc], writes=[tps2])
                    p0 = lo - NCTX
                    t2, tt2 = t2r.next()
                    fw.op("dve", lambda e, t2=t2, ps2=ps2, n=n, p0=p0: e.tensor_tensor(out=t2[:, :n], in0=ps2[:, :n], in1=sin[:, p0:p0 + n], op=ALU.mult),
                          reads=[tps2, ttab], writes=[tt2])
                    fw.op("pool", lambda e, t1=t1, nb=nb, n=n, p0=p0: e.tensor_tensor(out=t1[:, :n], in0=nb[:, :n], in1=cos[:, p0:p0 + n], op=ALU.mult),
                          reads=[tnb, ttab, tt1], writes=[tt1])
                    fw.op("pool", lambda e, dst=dst, t1=t1, t2=t2, n=n: e.tensor_tensor(out=dst, in0=t1[:, :n], in1=t2[:, :n], op=ALU.add),
                          reads=[tt1, tt2], writes=[tq])
            fw.barrier()
        va = P.sb([128, 18, 2, 128], BF16)
        tva = Tok()
        fw.op("pool", lambda e: e.memset(va[:], 1.0), writes=[tva])
        for g in range(2):
            fw.dma("pool", va[:, :, g, 0:64], VT.rearrange("(t p) c -> p t c", p=128)[:, :, g * 64:(g + 1) * 64], writes=[tva])
        att = P.sb([128, 4, T], F32)
        tatt = Tok()
        pss = Rot([P.ps() for _ in range(3)])
        pso = Rot([P.ps() for _ in range(2)])
        ptr = Rot([P.sb([128, 512], BF16) for _ in range(3)])
        rcr = Rot([P.sb([64, 512], F32) for _ in range(2)])
        jobs = [(0, 256, 0, 2)] + [(256 + 512 * i, 768 + 512 * i, 0, 18) for i in range(4)]
        for h in range(8):
            g = h // 4
            jt, r0 = h // 2, (h % 2) * 64
            for (qlo, qhi, k0, k1) in jobs:
                n = qhi - qlo
                po, tpo = pso.next()
                for kt in range(k0, k1):
                    ps, tps = pss.next()
                    fw.op("pe", lambda e, ps=ps, kt=kt, g=g, jt=jt, r0=r0, qlo=qlo, qhi=qhi, n=n: e.matmul(
                        ps[:, :n], lhsT=kd[r0:r0 + 64, g, kt * 128:(kt + 1) * 128], rhs=qb[r0:r0 + 64, jt, qlo:qhi],
                        start=True, stop=True), reads=[tq], writes=[tps])
                    pt, tpt = ptr.next()
                    fw.op("act", lambda e, pt=pt, ps=ps, n=n: e.activation(out=pt[:, :n], in_=ps[:, :n], func=AF.Exp, scale=0.125),
                          reads=[tps], writes=[tpt])
                    fw.op("pe", lambda e, po=po, pt=pt, kt=kt, g=g, n=n, k0=k0, k1=k1: e.matmul(
                        po[:, :n], lhsT=va[:, kt, g, :], rhs=pt[:, :n], start=(kt == k0), stop=(kt == k1 - 1)),
                        reads=[tpt, tva], writes=[tpo])
                rc, trc = rcr.next()
                fw.op("dve", lambda e, rc=rc, po=po, n=n: e.reciprocal(out=rc[:, :n], in_=po[64:128, :n]), reads=[tpo], writes=[trc])
                fw.op("dve", lambda e, rc=rc, po=po, n=n, jt=jt, r0=r0, qlo=qlo, qhi=qhi: e.tensor_tensor(
                    out=att[r0:r0 + 64, jt, qlo:qhi], in0=po[0:64, :n], in1=rc[:, :n], op=ALU.mult),
                    reads=[tpo, trc], writes=[tatt])
        ogs = P.sb([128, 4], F32)
        tog = Tok()
        fw.dma("sp", ogs[:], og.rearrange("(k p) -> p k", p=128), writes=[tog], allow_slow_non_contiguous=True)
        ones = P.sb([128, 128], BF16)
        fw.op("pool", lambda e: e.memset(ones[:], 1.0), writes=[tog])
        sq4 = P.sb([128, 4, 512], BF16)
        tsq4 = Tok()
        rs = P.sb([128, 512], F32)
        trs = Tok()
        stg = Rot([P.sb([128, 512], F32) for _ in range(3)])
        tout = Tok()
        for (lo, hi, ic) in SEGS:
            n = hi - lo
            fw.op("act", lambda e, lo=lo, hi=hi, n=n: e.activation(out=sq4[:, :, :n], in_=att[:, :, lo:hi], func=AF.Square), reads=[tatt], writes=[tsq4])
            ps, tps = pss.next()
            for k in range(4):
                fw.op("pe", lambda e, ps=ps, k=k, n=n: e.matmul(ps[:, :n], lhsT=ones[:], rhs=sq4[:, k, :n], start=(k == 0), stop=(k == 3)),
                      reads=[tsq4, tog], writes=[tps])
            rstd_from_ps(fw, rs, trs, ps, tps, n, 1.0 / 512, eps[:, 0:1], teps)
            for k in range(4):
                sg, tsg = stg.next()
                fw.op("dve", lambda e, sg=sg, k=k, lo=lo, hi=hi, n=n: e.scalar_tensor_tensor(
                    out=sg[:, :n], in0=att[:, k, lo:hi], scalar=ogs[:, k:k + 1], in1=rs[:, :n], op0=ALU.mult, op1=ALU.mult),
                    reads=[tatt, trs, tog], writes=[tsg])
                fw.dma("sp", ATT[k * 128:(k + 1) * 128, lo:hi], sg[:, :n], reads=[tsg], writes=[tout])
        fw.barrier()


def host_consts():
    pos = np.zeros((128, NLAT), np.float32)
    inv = np.zeros((128,), np.float32)
    tok = np.arange(NLAT)
    for p in range(128):
        d = p % 64
        pos[p] = (tok // 64) if d < 32 else (tok % 64)
        inv[p] = 10000.0 ** (-(d % 16) / 16.0)
    cst = np.zeros((128, 260), np.float32)
    cst[:, 0] = inv
    perm = np.zeros((128, 128), np.float32)
    for m in range(128):
        d = m % 32
        if d < 16:
            perm[m + 16, m] = -1.0
        else:
            perm[m - 16, m] = 1.0
    cst[:, 1:129] = perm
    bd = np.zeros((128, 128), np.float32)
    bd[:64, :64] = 1.0 / 64
    bd[64:, 64:] = 1.0 / 64
    cst[:, 129:257] = bd
    return pos, cst


def stage_p5a(fw, io):
    XT = io("XT", [DM, T], F32, "in")
    S5T = io("S5T", [256, T], F32, "in")
    ATT = io("ATT_T", [512, T], F32, "in")
    RWT = io("RWT", [256, T], F32, "in")
    MOD = io("MOD", [128, 48, 2], F32, "in")
    w_out = io("w_out", [DM, DM], F32, "in")
    XT1 = io("XT1", [DM, T], F32, "out")
    XTv = XT.rearrange("(k p) t -> p k t", p=128)
    with ExitStack() as es:
        P = Pool_(fw, es)
        MODs = P.sb([128, 48, 2], F32)
        tmod = Tok()
        fw.dma("sp", MODs[:], MOD, writes=[tmod])
        tw = Tok()
        wb = load_w_bf16(fw, P, w_out, DM, DM, tw, "wout")
        cat = P.sb([128, 8, T], BF16)
        tcat = Tok()
        cstg = Rot([P.sb([128, T], F32, "cstg") for _ in range(2)])
        for k in range(8):
            src = S5T[k * 128:(k + 1) * 128, :] if k < 2 else (ATT[(k - 2) * 128:(k - 1) * 128, :] if k < 6 else RWT[(k - 6) * 128:(k - 5) * 128, :])
            sg, tsg = cstg.next()
            fw.dma("sp", sg[:], src, writes=[tsg])
            fw.op("pool", lambda e, sg=sg, k=k: e.tensor_copy(out=cat[:, k, :], in_=sg[:]), reads=[tsg], writes=[tcat])
        xr = Rot([P.sb([128, 8, 512], F32) for _ in range(2)])
        x1r = Rot([P.sb([128, 8, 512], F32) for _ in range(2)])
        psr = Rot([P.ps() for _ in range(4)])
        to1, to2 = Tok(), Tok()
        for (lo, hi, ic) in SEGS:
            n = hi - lo
            xs, txs = xr.next()
            fw.dma("sp", xs[:, :, :n], XTv[:, :, lo:hi], writes=[txs])
            x1, tx1 = x1r.next()
            for d in range(8):
                ps, tps = psr.next()
                for k in range(8):
                    fw.op("pe", lambda e, ps=ps, k=k, d=d, lo=lo, hi=hi, n=n: e.matmul(
                        ps[:, :n], lhsT=wb[:, k, d * 128:(d + 1) * 128], rhs=cat[:, k, lo:hi], start=(k == 0), stop=(k == 7)),
                        reads=[tw, tcat], writes=[tps])
                fw.op("dve", lambda e, ps=ps, d=d, n=n, ic=ic, x1=x1, xs=xs: e.scalar_tensor_tensor(
                    out=x1[:, d, :n], in0=ps[:, :n], scalar=MODs[:, 16 + d, ic:ic + 1], in1=xs[:, d, :n], op0=ALU.mult, op1=ALU.add),
                    reads=[tps, txs, tmod], writes=[tx1])
            for k in range(8):
                fw.dma("sp", XT1[k * 128:(k + 1) * 128, lo:hi], x1[:, k, :n], reads=[tx1], writes=[to1])
        fw.barrier()


def stage_p5b(fw, io):
    XT1 = io("XT1", [DM, T], F32, "in")
    n2g = io("norm2_g", [DM], F32, "in")
    MOD = io("MOD", [128, 48, 2], F32, "in")
    up = io("ffn_up", [44, 128, DM], F32, "in")
    cw = io("ffn_conv_w", [3, 5632], F32, "in")
    cb = io("ffn_conv_b", [5632], F32, "in")
    down = io("ffn_down", [2816, DM], F32, "in")
    XT2 = io("XT2", [DM, T], F32, "out")
    X1v = XT1.rearrange("(k p) t -> p k t", p=128)
    X2v = XT2.rearrange("(k p) t -> p k t", p=128)
    with ExitStack() as es:
        P = Pool_(fw, es)
        MODs = P.sb([128, 48, 2], F32)
        tmod = Tok()
        fw.dma("sp", MODs[:], MOD, writes=[tmod])
        cws = P.sb([128, 44, 3], F32)
        cbs = P.sb([128, 44], F32)
        tcw = Tok()
        for w in range(3):
            fw.dma("sp", cws[:, :, w], cw[w].rearrange("(j p) -> p j", p=128), writes=[tcw], allow_slow_non_contiguous=True)
        fw.dma("sp", cbs[:], cb.rearrange("(j p) -> p j", p=128), writes=[tcw], allow_slow_non_contiguous=True)
        h2 = P.sb([128, 8, T], BF16)
        th2 = Tok()
        AB, tab = make_AB(fw, P, MODs, tmod, n2g, 24, 32)
        with ExitStack() as es2:
            P2 = Pool_(fw, es2)
            st = norm_state(fw, P2)
            xr0 = Rot([P2.sb([128, 8, 512], F32) for _ in range(2)])
            for (lo, hi, ic) in SEGS:
                n = hi - lo
                xs, txs = xr0.next()
                fw.dma("sp", xs[:, :, :n], X1v[:, :, lo:hi], writes=[txs])
                norm_mod_seg(fw, P2, st, xs, txs, n, ic, AB, tab, lambda k, lo=lo, hi=hi: [h2[:, k, lo:hi]], [th2])
            fw.barrier()
        hid = P.sb([128, 11, T], BF16)
        thid = Tok()
        dwb = P.sb([128, 11, DM], BF16)
        tdw = Tok()
        urot = [Rot([P.sb([128, T], F32) for _ in range(2)]) for _ in range(2)]
        y = [P.sb([128, T], F32) for _ in range(2)]
        ty = [Tok(), Tok()]
        psr = Rot([P.ps() for _ in range(4)])
        tx2 = Tok()
        RANGES = [(0, NCTX), (NCTX, T)]
        GROUPS = [(0, 2), (2, 2), (4, 2), (6, 2), (8, 2), (10, 1)]
        for half in range(2):
            with ExitStack() as esu:
                Pu = Pool_(fw, esu)
                ustg = Rot([Pu.sb([128, 8, 256], F32, "ustg") for _ in range(2)])
                for jj in range(11):
                    r0 = (half * 11 + jj) * 128
                    sg, tsg = ustg.next()
                    sgv = sg[:].rearrange("p k n -> p (k n)")[:, 0:DM]
                    fw.dma("sp", sgv, down[r0:r0 + 128, :], writes=[tsg])
                    fw.op("pool", lambda e, sgv=sgv, jj=jj: e.tensor_copy(out=dwb[:, jj, :], in_=sgv), reads=[tsg], writes=[tdw])
                ubr = Rot([Pu.sb([128, 8, 2, 256], BF16, "ub") for _ in range(2)])

                def issue_load(g, half=half):
                    jj0, ng = GROUPS[g]
                    ub, tub = ubr.next()
                    for wh in range(2):
                        sg, tsg = ustg.next()
                        sgv = sg[:].rearrange("p k (j c) -> p (k j c)", j=2).rearrange("p (j k c) -> p j k c", j=2, k=8)
                        for jl in range(ng):
                            jt = wh * 22 + half * 11 + jj0 + jl
                            fw.dma("sp", sgv[:, jl].rearrange("p k c -> p (k c)"), up[jt], writes=[tsg])
                            fw.op("pool", lambda e, sgv=sgv, ub=ub, wh=wh, jl=jl: e.tensor_copy(out=ub[:, :, wh, jl * 128:(jl + 1) * 128], in_=sgv[:, jl]),
                                  reads=[tsg], writes=[tub])
                    return ub, tub
                loaded = issue_load(0)
                for g, (jj0, ng) in enumerate(GROUPS):
                    ub, tub = loaded
                    if g + 1 < len(GROUPS):
                        loaded = issue_load(g + 1)
                    for jl in range(ng):
                        jj = jj0 + jl
                        j = half * 11 + jj
                        for wh in range(2):
                            jc = wh * 22 + j
                            ucur, tucur = urot[wh].next()
                            for si, (lo, hi, ic) in enumerate(SEGS):
                                n = hi - lo
                                ps, tps = psr.next()
                                for k in range(8):
                                    fw.op("pe", lambda e, ps=ps, k=k, wh=wh, ub=ub, jl=jl, lo=lo, hi=hi, n=n: e.matmul(
                                        ps[:, :n], lhsT=ub[:, k, wh, jl * 128:(jl + 1) * 128], rhs=h2[:, k, lo:hi], start=(k == 0), stop=(k == 7)),
                                        reads=[tub, th2], writes=[tps])
                                fw.op("act", lambda e, ps=ps, ucur=ucur, lo=lo, hi=hi, n=n: e.copy(out=ucur[:, lo:hi], in_=ps[:, :n]),
                                      reads=[tps], writes=[tucur])
                            fw.op("act", lambda e, wh=wh, jc=jc, ucur=ucur: e.activation(out=y[wh][:], in_=ucur[:], func=AF.Identity,
                                                                                        scale=cws[:, jc, 1:2], bias=cbs[:, jc:jc + 1]),
                                  reads=[tucur, tcw], writes=[ty[wh]])
                            for (lo, hi) in RANGES:
                                fw.op("dve", lambda e, wh=wh, jc=jc, lo=lo, hi=hi, ucur=ucur: e.scalar_tensor_tensor(
                                    out=y[wh][:, lo + 1:hi], in0=ucur[:, lo:hi - 1], scalar=cws[:, jc, 0:1], in1=y[wh][:, lo + 1:hi],
                                    op0=ALU.mult, op1=ALU.add), reads=[tucur, tcw, ty[wh]], writes=[ty[wh]])
                                fw.op("dve", lambda e, wh=wh, jc=jc, lo=lo, hi=hi, ucur=ucur: e.scalar_tensor_tensor(
                                    out=y[wh][:, lo:hi - 1], in0=ucur[:, lo + 1:hi], scalar=cws[:, jc, 2:3], in1=y[wh][:, lo:hi - 1],
                                    op0=ALU.mult, op1=ALU.add), reads=[tucur, tcw, ty[wh]], writes=[ty[wh]])
                        fw.op("act", lambda e: e.activation(out=y[0][:], in_=y[0][:], func=AF.Silu), reads=[ty[0]], writes=[ty[0]])
                        fw.op("dve", lambda e, jj=jj: e.tensor_tensor(out=hid[:, jj, :], in0=y[0][:], in1=y[1][:], op=ALU.mult),
                              reads=[ty[0], ty[1]], writes=[thid])
                fw.barrier()
            with ExitStack() as esd:
                Pd = Pool_(fw, esd)
                xr = Rot([Pd.sb([128, 8, 512], F32) for _ in range(2)])
                for (lo, hi, ic) in SEGS:
                    n = hi - lo
                    xs, txs = xr.next()
                    src = X1v if half == 0 else X2v
                    fw.dma("sp", xs[:, :, :n], src[:, :, lo:hi], reads=([tx2] if half else []), writes=[txs])
                    for d in range(8):
                        ps, tps = psr.next()
                        for jj in range(11):
                            fw.op("pe", lambda e, ps=ps, jj=jj, d=d, lo=lo, hi=hi, n=n: e.matmul(
                                ps[:, :n], lhsT=dwb[:, jj, d * 128:(d + 1) * 128], rhs=hid[:, jj, lo:hi], start=(jj == 0), stop=(jj == 10)),
                                reads=[tdw, thid], writes=[tps])
                        fw.op("dve", lambda e, ps=ps, d=d, n=n, ic=ic, xs=xs: e.scalar_tensor_tensor(
                            out=xs[:, d, :n], in0=ps[:, :n], scalar=MODs[:, 40 + d, ic:ic + 1], in1=xs[:, d, :n], op0=ALU.mult, op1=ALU.add),
                            reads=[tps, tmod, txs], writes=[txs])
                    for k in range(8):
                        fw.dma("sp", XT2[k * 128:(k + 1) * 128, lo:hi], xs[:, k, :n], reads=[txs], writes=[tx2])
                fw.barrier()


def stage_p6(fw, io):
    XT = io("XT", [DM, T], F32, "in")
    fg = io("final_g", [DM], F32, "in")
    ident = io("ident", [128, 128], F32, "in")
    OUT = io("OUT", [NLAT, DM], F32, "out")
    XTv = XT.rearrange("(k p) t -> p k t", p=128)
    with ExitStack() as es:
        P = Pool_(fw, es)
        idt = P.sb([128, 128], F32)
        tid = Tok()
        fw.dma("sp", idt[:], ident, writes=[tid])
        g = P.sb([128, 8], F32)
        fw.dma("sp", g[:], fg.rearrange("(k p) -> p k", p=128), writes=[tid], allow_slow_non_contiguous=True)
        st = norm_state(fw, P)
        xr = Rot([P.sb([128, 8, 512], F32) for _ in range(2)])
        yr = Rot([P.sb([128, 8, 512], F32) for _ in range(2)])
        psr = Rot([P.ps() for _ in range(4)])
        orr = Rot([P.sb([128, DM], F32) for _ in range(3)])
        tout = Tok()
        for (lo, hi, ic) in SEGS[1:]:
            n = hi - lo
            xs, txs = xr.next()
            fw.dma("sp", xs[:, :, :n], XTv[:, :, lo:hi], writes=[txs])
            sq, ones = st["sq"], st["ones"]
            fw.op("act", lambda e, xs=xs: e.activation(out=sq[:], in_=xs[:], func=AF.Square), reads=[txs], writes=[st["tsq"]])
            ps, tps = st["psr"].next()
            for k in range(8):
                fw.op("pe", lambda e, ps=ps, k=k: e.matmul(ps[:], lhsT=ones[:], rhs=sq[:, k, :], start=(k == 0), stop=(k == 7)),
                      reads=[st["tsq"], st["tones"]], writes=[tps])
            rstd_from_ps(fw, st["rs"], st["trs"], ps, tps, n, 1.0 / DM, st["eps"][:, 0:1], st["teps"])
            ys, tys = yr.next()
            for k in range(8):
                fw.op("dve", lambda e, ys=ys, xs=xs, k=k: e.scalar_tensor_tensor(
                    out=ys[:, k, :], in0=xs[:, k, :], scalar=g[:, k:k + 1], in1=st["rs"][:], op0=ALU.mult, op1=ALU.mult),
                    reads=[txs, st["trs"], tid], writes=[tys])
            for blk in range(4):
                ot, tot = orr.next()
                for half in range(2):
                    ps2, tps2 = psr.next()
                    for k in range(4):
                        kk = half * 4 + k
                        fw.op("pe", lambda e, ps2=ps2, k=k, kk=kk, ys=ys, blk=blk: e.transpose(
                            ps2[:, k * 128:(k + 1) * 128], ys[:, kk, blk * 128:(blk + 1) * 128], idt[:]),
                            reads=[tys, tid], writes=[tps2])
                    if half == 0:
                        fw.op("dve", lambda e, ot=ot, ps2=ps2: e.tensor_copy(out=ot[:, 0:512], in_=ps2[:]), reads=[tps2], writes=[tot])
                    else:
                        fw.op("act", lambda e, ot=ot, ps2=ps2: e.copy(out=ot[:, 512:1024], in_=ps2[:]), reads=[tps2], writes=[tot])
                r0 = lo - NCTX + blk * 128
                fw.dma("sp", OUT[r0:r0 + 128, :], ot[:], reads=[tot], writes=[tout])
        fw.barrier()


def sin_reduced(fw, P, out, src, tsrc, shape, off, tout):
    ti = P.sb(shape, I32)
    tf = P.sb(shape, F32)
    tt = Tok()
    fw.op("dve", lambda e: e.tensor_scalar(out=ti[:], in0=src, scalar1=off, scalar2=1.0 / (2 * math.pi), op0=ALU.add, op1=ALU.mult),
          reads=[tsrc], writes=[tt])
    fw.op("dve", lambda e: e.tensor_copy(out=tf[:], in_=ti[:]), reads=[tt], writes=[tt])
    fw.op("dve", lambda e: e.scalar_tensor_tensor(out=tf[:], in0=tf[:], scalar=-2 * math.pi, in1=src, op0=ALU.mult, op1=ALU.add),
          reads=[tt, tsrc], writes=[tt])
    fw.op("dve", lambda e: e.tensor_scalar(out=tf[:], in0=tf[:], scalar1=off, scalar2=math.pi, op0=ALU.add, op1=ALU.min), reads=[tt], writes=[tt])
    fw.op("dve", lambda e: e.tensor_scalar(out=tf[:], in0=tf[:], scalar1=-math.pi, scalar2=None, op0=ALU.max), reads=[tt], writes=[tt])
    fw.op("act", lambda e: e.activation(out=out, in_=tf[:], func=AF.Sin), reads=[tt], writes=[tout])


def stage_p2(fw, io):
    ZT = io("ZT", [2048, TE], F32, "in")
    a_re = io("s5_a_re", [2, 16, 64], F32, "in")
    a_im = io("s5_a_im", [2, 16, 64], F32, "in")
    lstep = io("s5_log_step", [2, 16], F32, "in")
    b_re = io("s5_b_re", [2, 16, 64, 16], F32, "in")
    b_im = io("s5_b_im", [2, 16, 64, 16], F32, "in")
    c_re = io("s5_c_re", [2, 16, 16, 64], F32, "in")
    c_im = io("s5_c_im", [2, 16, 16, 64], F32, "in")
    dsk = io("s5_d", [256], F32, "in")
    glu_w = io("s5_glu_w", [256, 256], F32, "in")
    glu_b = io("s5_glu_b", [256], F32, "in")
    out_g = io("s5_out_g", [256], F32, "in")
    S5T = io("S5T", [256, T], F32, "out")
    N = T
    with ExitStack() as es:
        P = Pool_(fw, es)
        are = P.sb([128, 2, 8], F32)
        aim = P.sb([128, 2, 8], F32)
        lst = P.sb([128, 2, 8], F32)
        tpar = Tok()
        for di in range(2):
            fw.dma("sp", are[:, di, :], a_re[di].rearrange("(s g) p -> (g p) s", g=2), writes=[tpar], allow_slow_non_contiguous=True)
            fw.dma("sp", aim[:, di, :], a_im[di].rearrange("(s g) p -> (g p) s", g=2), writes=[tpar], allow_slow_non_contiguous=True)
            for g2 in range(2):
                fw.dma("sp", lst[g2 * 64:(g2 + 1) * 64, di:di + 1, :],
                       lstep[di].rearrange("(s g) -> g s", g=2)[g2:g2 + 1, :].partition_broadcast(64), writes=[tpar],
                       allow_slow_non_contiguous=True)
        sh = [128, 2, 8]
        step = P.sb(sh, F32)
        fw.op("act", lambda e: e.activation(out=step[:], in_=lst[:], func=AF.Exp), reads=[tpar], writes=[tpar])
        er = P.sb(sh, F32)
        th = P.sb(sh, F32)
        fw.op("dve", lambda e: e.tensor_tensor(out=er[:], in0=are[:], in1=step[:], op=ALU.mult), reads=[tpar], writes=[tpar])
        fw.op("act", lambda e: e.activation(out=er[:], in_=er[:], func=AF.Exp), reads=[tpar], writes=[tpar])
        fw.op("dve", lambda e: e.tensor_tensor(out=th[:], in0=aim[:], in1=step[:], op=ALU.mult), reads=[tpar], writes=[tpar])
        sn = P.sb(sh, F32)
        cs = P.sb(sh, F32)
        ttrig = Tok()
        sin_reduced(fw, P, sn[:], th[:], tpar, sh, 0.0, ttrig)
        sin_reduced(fw, P, cs[:], th[:], tpar, sh, math.pi / 2, ttrig)
        PW = P.sb([128, 9, 3, 2, 8], F32)
        tpw = Tok()
        fw.op("dve", lambda e: e.tensor_tensor(out=PW[:, 0, 0], in0=er[:], in1=cs[:], op=ALU.mult), reads=[tpar, ttrig], writes=[tpw])
        fw.op("dve", lambda e: e.tensor_tensor(out=PW[:, 0, 1], in0=er[:], in1=sn[:], op=ALU.mult), reads=[tpar, ttrig], writes=[tpw])
        t1 = P.sb(sh, F32)
        t2 = P.sb(sh, F32)
        for k in range(9):
            fw.op("dve", lambda e, k=k: e.tensor_scalar(out=PW[:, k, 2], in0=PW[:, k, 1], scalar1=-1.0, scalar2=None, op0=ALU.mult),
                  reads=[tpw], writes=[tpw])
            if k == 8:
                break
            fw.op("dve", lambda e, k=k: e.tensor_tensor(out=t1[:], in0=PW[:, k, 0], in1=PW[:, k, 0], op=ALU.mult), reads=[tpw], writes=[tpw])
            fw.op("dve", lambda e, k=k: e.tensor_tensor(out=t2[:], in0=PW[:, k, 1], in1=PW[:, k, 1], op=ALU.mult), reads=[tpw], writes=[tpw])
            fw.op("dve", lambda e, k=k: e.tensor_tensor(out=PW[:, k + 1, 0], in0=t1[:], in1=t2[:], op=ALU.subtract), reads=[tpw], writes=[tpw])
            fw.op("dve", lambda e, k=k: e.scalar_tensor_tensor(out=PW[:, k + 1, 1], in0=PW[:, k, 0], scalar=2.0, in1=PW[:, k, 1],
                                                               op0=ALU.mult, op1=ALU.mult), reads=[tpw], writes=[tpw])
        br = P.sb(sh, F32)
        bi = P.sb(sh, F32)
        nbi = P.sb(sh, F32)
        den = P.sb(sh, F32)
        nr = P.sb(sh, F32)
        tb = Tok()
        fw.op("dve", lambda e: e.tensor_tensor(out=den[:], in0=are[:], in1=are[:], op=ALU.mult), reads=[tpar], writes=[tb])
        fw.op("dve", lambda e: e.tensor_tensor(out=t1[:], in0=aim[:], in1=aim[:], op=ALU.mult), reads=[tpar, tpw], writes=[tpw])
        fw.op("dve", lambda e: e.tensor_tensor(out=den[:], in0=den[:], in1=t1[:], op=ALU.add), reads=[tb, tpw], writes=[tb])
        fw.op("dve", lambda e: e.reciprocal(out=den[:], in_=den[:]), reads=[tb], writes=[tb])
        fw.op("dve", lambda e: e.tensor_scalar(out=nr[:], in0=PW[:, 0, 0], scalar1=-1.0, scalar2=None, op0=ALU.add), reads=[tpw], writes=[tb])
        fw.op("dve", lambda e: e.tensor_tensor(out=t1[:], in0=nr[:], in1=are[:], op=ALU.mult), reads=[tb, tpar, tpw], writes=[tpw])
        fw.op("dve", lambda e: e.tensor_tensor(out=t2[:], in0=PW[:, 0, 1], in1=aim[:], op=ALU.mult), reads=[tpw, tpar], writes=[tpw])
        fw.op("dve", lambda e: e.tensor_tensor(out=t1[:], in0=t1[:], in1=t2[:], op=ALU.add), reads=[tpw], writes=[tpw])
        fw.op("dve", lambda e: e.tensor_tensor(out=br[:], in0=t1[:], in1=den[:], op=ALU.mult), reads=[tpw, tb], writes=[tb])
        fw.op("dve", lambda e: e.tensor_tensor(out=t1[:], in0=PW[:, 0, 1], in1=are[:], op=ALU.mult), reads=[tpw, tpar, tb], writes=[tpw])
        fw.op("dve", lambda e: e.tensor_tensor(out=t2[:], in0=nr[:], in1=aim[:], op=ALU.mult), reads=[tb, tpar, tpw], writes=[tpw])
        fw.op("dve", lambda e: e.tensor_tensor(out=t1[:], in0=t1[:], in1=t2[:], op=ALU.subtract), reads=[tpw], writes=[tpw])
        fw.op("dve", lambda e: e.tensor_tensor(out=bi[:], in0=t1[:], in1=den[:], op=ALU.mult), reads=[tpw, tb], writes=[tb])
        fw.op("dve", lambda e: e.tensor_scalar(out=nbi[:], in0=bi[:], scalar1=-1.0, scalar2=None, op0=ALU.mult), reads=[tb], writes=[tb])
        BTb = P.sb([128, 2, 2, 8, 128], BF16)
        CTb = P.sb([128, 2, 2, 8, 128], BF16)
        tBT = Tok()
        tCT = Tok()
        with ExitStack() as es2:
            P2 = Pool_(fw, es2)
            BTf = P2.sb([128, 2, 2, 8, 128], F32)
            CTf = P2.sb([128, 2, 2, 8, 128], F32)
            CT2 = P2.sb([128, 2, 2, 8, 128], F32)
            tf1, tf2 = Tok(), Tok()
            fw.op("pool", lambda e: e.memset(BTf[:], 0.0), writes=[tf1])
            fw.op("pool", lambda e: e.memset(CTf[:], 0.0), writes=[tf2])
            for di in range(2):
                for ri, (bsrc, csrc) in enumerate(((b_re, c_re), (b_im, c_im))):
                    for g in range(16):
                        s, g2 = g // 2, g % 2
                        r0 = (g % 8) * 16
                        fw.dma("sp", BTf[r0:r0 + 16, di, ri, s, g2 * 64:(g2 + 1) * 64], bsrc[di, g].rearrange("p h -> h p"),
                               writes=[tf1], allow_slow_non_contiguous=True)
                        fw.dma("sp", CTf[g2 * 64:(g2 + 1) * 64, di, ri, s, r0:r0 + 16], csrc[di, g].rearrange("h p -> p h"),
                               writes=[tf2], allow_slow_non_contiguous=True)
            fw.op("act", lambda e: e.copy(out=BTb[:], in_=BTf[:]), reads=[tf1], writes=[tBT])
            bsh = [128, 2, 8, 128]
            brb = br[:].unsqueeze(3).to_broadcast(bsh)
            nbib = nbi[:].unsqueeze(3).to_broadcast(bsh)
            tc2 = Tok()
            fw.op("dve", lambda e: e.tensor_tensor(out=CT2[:, :, 0], in0=CTf[:, :, 0], in1=brb, op=ALU.mult), reads=[tf2, tb], writes=[tc2])
            fw.op("pool", lambda e: e.tensor_tensor(out=CT2[:, :, 1], in0=CTf[:, :, 1], in1=nbib, op=ALU.mult), reads=[tf2, tb], writes=[tc2])
            fw.op("dve", lambda e: e.tensor_tensor(out=CT2[:, :, 0], in0=CT2[:, :, 0], in1=CT2[:, :, 1], op=ALU.add), reads=[tc2], writes=[tc2])
            fw.op("act", lambda e: e.copy(out=CTb[:, :, 0], in_=CT2[:, :, 0]), reads=[tc2], writes=[tCT])
            fw.op("dve", lambda e: e.tensor_tensor(out=CT2[:, :, 0], in0=CTf[:, :, 0], in1=nbib, op=ALU.mult), reads=[tf2, tb, tCT, tc2], writes=[tc2])
            fw.op("pool", lambda e: e.tensor_tensor(out=CT2[:, :, 1], in0=CTf[:, :, 1], in1=brb, op=ALU.mult), reads=[tf2, tb, tc2], writes=[tc2])
            fw.op("dve", lambda e: e.tensor_tensor(out=CT2[:, :, 0], in0=CT2[:, :, 0], in1=CT2[:, :, 1], op=ALU.subtract), reads=[tc2], writes=[tc2])
            fw.op("act", lambda e: e.copy(out=CTb[:, :, 1], in_=CT2[:, :, 0]), reads=[tc2], writes=[tCT])
            fw.barrier()
        ub = P.sb([128, 2, TE], BF16)
        tub = Tok()
        for ct in range(2):
            fw.dma("pool", ub[:, ct, :], ZT[ct * 128:(ct + 1) * 128, :], writes=[tub])
        dk = P.sb([128, 2], F32)
        tdk = Tok()
        fw.dma("sp", dk[:], dsk.rearrange("(c p) -> p c", p=128), writes=[tdk], allow_slow_non_contiguous=True)
        y = P.sb([128, 2, T], F32)
        ty = Tok()
        for ct in range(2):
            fw.dma("sp", y[:, ct, :], ZT[ct * 128:(ct + 1) * 128, 0:T], writes=[ty])
        for ct in range(2):
            fw.op("pool", lambda e, ct=ct: e.tensor_scalar(out=y[:, ct, :], in0=y[:, ct, :], scalar1=dk[:, ct:ct + 1], scalar2=None, op0=ALU.mult),
                  reads=[ty, tdk], writes=[ty])
        Xs = Rot([P.sb([128, 2, N], F32) for _ in range(4)])
        xbs = Rot([P.sb([128, 2, N], BF16) for _ in range(2)])
        psr = Rot([P.ps() for _ in range(4)])
        psy = Rot([P.ps() for _ in range(2)])
        tmpy = Rot([P.sb([128, 512], F32) for _ in range(2)])
        CH = [(0, 512), (512, 1024), (1024, 1536), (1536, 2048), (2048, 2304)]

        def prep(di, s):
            off = 0 if di == 0 else 256
            ct = s // 4
            X, tX = Xs.next()
            for ri in range(2):
                for ci, (c0, c1) in enumerate(CH):
                    n = c1 - c0
                    ps, tps = psr.next()
                    fw.op("pe", lambda e, ps=ps, ri=ri, c0=c0, c1=c1, n=n: e.matmul(
                        ps[:, :n], lhsT=BTb[:, di, ri, s, :], rhs=ub[:, ct, off + c0:off + c1], start=True, stop=True),
                        reads=[tBT, tub], writes=[tps])
                    fw.op("act", lambda e, ps=ps, ri=ri, c0=c0, c1=c1, n=n: e.copy(out=X[:, ri, c0:c1], in_=ps[:, :n]),
                          reads=[tps], writes=[tX])
            return X, tX

        def scan_gen(di, s, X, tX):
            def cstep(w_re, w_im, r_re, r_im, k):
                pr = PW[:, k, 0, di, s:s + 1]
                pi = PW[:, k, 1, di, s:s + 1]
                npi = PW[:, k, 2, di, s:s + 1]
                for (o, i0, sc) in ((w_re, r_re, pr), (w_re, r_im, npi), (w_im, r_im, pr), (w_im, r_re, pi)):
                    fw.op("dve", lambda e, o=o, i0=i0, sc=sc: e.scalar_tensor_tensor(out=o, in0=i0, scalar=sc, in1=o, op0=ALU.mult, op1=ALU.add),
                          reads=[tX, tpw], writes=[tX])
                    yield
            for k in range(8):
                st_ = 1 << k
                Xv = [X[:, ri, :].rearrange("p (m c) -> p m c", c=2 * st_) for ri in range(2)]
                if di == 0:
                    yield from cstep(Xv[0][:, :, 2 * st_ - 1], Xv[1][:, :, 2 * st_ - 1], Xv[0][:, :, st_ - 1], Xv[1][:, :, st_ - 1], k)
                else:
                    yield from cstep(Xv[0][:, :, 0], Xv[1][:, :, 0], Xv[0][:, :, st_], Xv[1][:, :, st_], k)
            for i in (range(1, 9) if di == 0 else range(7, -1, -1)):
                if di == 0:
                    w, r = 256 * i + 255, 256 * (i - 1) + 255
                else:
                    w, r = 256 * i, 256 * (i + 1)
                yield from cstep(X[:, 0, w:w + 1], X[:, 1, w:w + 1], X[:, 0, r:r + 1], X[:, 1, r:r + 1], 8)
            for k in range(7, -1, -1):
                st_ = 1 << k
                Xv = [X[:, ri, :].rearrange("p (m c) -> p m c", c=2 * st_) for ri in range(2)]
                if di == 0:
                    yield from cstep(Xv[0][:, 1:, st_ - 1], Xv[1][:, 1:, st_ - 1], Xv[0][:, :-1, 2 * st_ - 1], Xv[1][:, :-1, 2 * st_ - 1], k)
                else:
                    yield from cstep(Xv[0][:, :-1, st_], Xv[1][:, :-1, st_], Xv[0][:, 1:, 0], Xv[1][:, 1:, 0], k)

        def fin(di, s, X, tX):
            ct = s // 4
            xb, txb = xbs.next()
            fw.op("act", lambda e: e.copy(out=xb[:], in_=X[:]), reads=[tX], writes=[txb])
            for (c0, c1) in CH:
                n = c1 - c0
                if di == 0:
                    y0 = c0
                else:
                    y0 = c0 + 256 if c0 < 2048 else 0
                ps, tps = psy.next()
                for ri in range(2):
                    fw.op("pe", lambda e, ps=ps, ri=ri, c0=c0, c1=c1, n=n: e.matmul(
                        ps[:, :n], lhsT=CTb[:, di, ri, s, :], rhs=xb[:, ri, c0:c1], start=(ri == 0), stop=(ri == 1)),
                        reads=[tCT, txb], writes=[tps])
                tm, ttm = tmpy.next()
                fw.op("act", lambda e, tm=tm, ps=ps, n=n: e.copy(out=tm[:, :n], in_=ps[:, :n]), reads=[tps], writes=[ttm])
                fw.op("pool", lambda e, tm=tm, y0=y0, n=n: e.tensor_tensor(out=y[:, ct, y0:y0 + n], in0=y[:, ct, y0:y0 + n], in1=tm[:, :n], op=ALU.add),
                      reads=[ttm, ty], writes=[ty])

        pairs = [(di, s0, s0 + 1) for di in range(2) for s0 in range(0, 8, 2)]
        nxt = [prep(pairs[0][0], pairs[0][1]), prep(pairs[0][0], pairs[0][2])]
        for pi_, (di, s0, s1) in enumerate(pairs):
            cur = nxt
            if pi_ + 1 < len(pairs):
                d2, a0, a1 = pairs[pi_ + 1]
                nxt = [prep(d2, a0), prep(d2, a1)]
            gens = [scan_gen(di, s0, *cur[0]), scan_gen(di, s1, *cur[1])]
            alive = [True, True]
            while any(alive):
                for gi in range(2):
                    if alive[gi]:
                        try:
                            next(gens[gi])
                        except StopIteration:
                            alive[gi] = False
            fin(di, s0, *cur[0])
            fin(di, s1, *cur[1])
        tg = Tok()
        gw = load_w_bf16(fw, P, glu_w, 256, 256, tg, "gluw")
        gb = P.sb([128, 2], F32)
        og = P.sb([128, 2], F32)
        fw.dma("sp", gb[:], glu_b.rearrange("(c p) -> p c", p=128), writes=[tg], allow_slow_non_contiguous=True)
        fw.dma("sp", og[:], out_g.rearrange("(c p) -> p c", p=128), writes=[tg], allow_slow_non_contiguous=True)
        ones = P.sb([128, 128], BF16)
        eps = P.sb([128, 1], F32)
        fw.op("pool", lambda e: e.memset(ones[:], 1.0), writes=[tg])
        fw.op("pool", lambda e: e.memset(eps[:], RMS_EPS), writes=[tg])
        a32 = P.sb([128, 2, 512], F32)
        ab = P.sb([128, 2, 512], BF16)
        w1 = P.sb([128, 2, 512], F32)
        w2 = P.sb([128, 2, 512], F32)
        sqb = P.sb([128, 2, 512], BF16)
        rs = P.sb([128, 512], F32)
        ta, tw1, trs = Tok(), Tok(), Tok()
        stg = Rot([P.sb([128, 512], F32) for _ in range(2)])
        tout = Tok()
        C1 = math.sqrt(2.0 / math.pi)
        for (lo, hi, ic) in SEGS:
            n = hi - lo
            yv = y[:, :, lo:hi]
            fw.op("act", lambda e, yv=yv, n=n: e.activation(out=w1[:, :, :n], in_=yv, func=AF.Square), reads=[ty], writes=[tw1])
            fw.op("dve", lambda e, n=n: e.tensor_scalar(out=w1[:, :, :n], in0=w1[:, :, :n], scalar1=0.044715 * C1, scalar2=C1, op0=ALU.mult, op1=ALU.add),
                  reads=[tw1], writes=[tw1])
            fw.op("dve", lambda e, yv=yv, n=n: e.tensor_tensor(out=w1[:, :, :n], in0=w1[:, :, :n], in1=yv, op=ALU.mult), reads=[tw1, ty], writes=[tw1])
            fw.op("act", lambda e, n=n: e.activation(out=w1[:, :, :n], in_=w1[:, :, :n], func=AF.Tanh), reads=[tw1], writes=[tw1])
            fw.op("dve", lambda e, n=n: e.tensor_scalar(out=w1[:, :, :n], in0=w1[:, :, :n], scalar1=0.5, scalar2=0.5, op0=ALU.mult, op1=ALU.add),
                  reads=[tw1], writes=[tw1])
            fw.op("dve", lambda e, yv=yv, n=n: e.tensor_tensor(out=a32[:, :, :n], in0=w1[:, :, :n], in1=yv, op=ALU.mult), reads=[tw1, ty], writes=[ta])
            fw.op("act", lambda e, n=n: e.copy(out=ab[:, :, :n], in_=a32[:, :, :n]), reads=[ta], writes=[ta])
            for nt in range(2):
                ps, tps = psr.next()
                for k in range(2):
                    fw.op("pe", lambda e, ps=ps, k=k, nt=nt, n=n: e.matmul(ps[:, :n], lhsT=gw[:, k, nt * 128:(nt + 1) * 128], rhs=ab[:, k, :n],
                                                                      start=(k == 0), stop=(k == 1)), reads=[tg, ta], writes=[tps])
                fw.op("act", lambda e, ps=ps, nt=nt, n=n: e.activation(out=w2[:, nt, :n], in_=ps[:, :n], func=AF.Sigmoid, bias=gb[:, nt:nt + 1]),
                      reads=[tps, tg], writes=[tw1])
            fw.op("dve", lambda e, n=n: e.tensor_tensor(out=w2[:, :, :n], in0=w2[:, :, :n], in1=a32[:, :, :n], op=ALU.mult), reads=[tw1, ta], writes=[tw1])
            fw.op("act", lambda e, n=n: e.activation(out=sqb[:, :, :n], in_=w2[:, :, :n], func=AF.Square), reads=[tw1], writes=[tw1])
            ps, tps = psr.next()
            for k in range(2):
                fw.op("pe", lambda e, ps=ps, k=k, n=n: e.matmul(ps[:, :n], lhsT=ones[:], rhs=sqb[:, k, :n], start=(k == 0), stop=(k == 1)),
                      reads=[tw1, tg], writes=[tps])
            rstd_from_ps(fw, rs, trs, ps, tps, n, 1.0 / 256, eps[:, 0:1], tg)
            for k in range(2):
                sg, tsg = stg.next()
                fw.op("dve", lambda e, sg=sg, k=k, n=n: e.scalar_tensor_tensor(out=sg[:, :n], in0=w2[:, k, :n], scalar=og[:, k:k + 1], in1=rs[:, :n],
                                                                             op0=ALU.mult, op1=ALU.mult), reads=[tw1, trs, tg], writes=[tsg])
                fw.dma("sp", S5T[k * 128:(k + 1) * 128, lo:hi], sg[:, :n], reads=[tsg], writes=[tout])
        fw.barrier()


def gen_p2(fw, io):
    ZT = io("ZT", [2048, TE], F32, "in")
    a_re = io("s5_a_re", [2, 16, 64], F32, "in")
    a_im = io("s5_a_im", [2, 16, 64], F32, "in")
    lstep = io("s5_log_step", [2, 16], F32, "in")
    b_re = io("s5_b_re", [2, 16, 64, 16], F32, "in")
    b_im = io("s5_b_im", [2, 16, 64, 16], F32, "in")
    c_re = io("s5_c_re", [2, 16, 16, 64], F32, "in")
    c_im = io("s5_c_im", [2, 16, 16, 64], F32, "in")
    dsk = io("s5_d", [256], F32, "in")
    glu_w = io("s5_glu_w", [256, 256], F32, "in")
    glu_b = io("s5_glu_b", [256], F32, "in")
    out_g = io("s5_out_g", [256], F32, "in")
    S5T = io("S5T", [256, T], F32, "out")
    N = T
    with ExitStack() as es:
        P = Pool_(fw, es)
        are = P.sb([128, 2, 8], F32)
        aim = P.sb([128, 2, 8], F32)
        lst = P.sb([128, 2, 8], F32)
        tpar = Tok()
        for di in range(2):
            fw.dma("sp", are[:, di, :], a_re[di].rearrange("(s g) p -> (g p) s", g=2), writes=[tpar], allow_slow_non_contiguous=True)
            fw.dma("sp", aim[:, di, :], a_im[di].rearrange("(s g) p -> (g p) s", g=2), writes=[tpar], allow_slow_non_contiguous=True)
            for g2 in range(2):
                fw.dma("sp", lst[g2 * 64:(g2 + 1) * 64, di:di + 1, :],
                       lstep[di].rearrange("(s g) -> g s", g=2)[g2:g2 + 1, :].partition_broadcast(64), writes=[tpar],
                       allow_slow_non_contiguous=True)
        sh = [128, 2, 8]
        step = P.sb(sh, F32)
        fw.op("act", lambda e: e.activation(out=step[:], in_=lst[:], func=AF.Exp), reads=[tpar], writes=[tpar])
        er = P.sb(sh, F32)
        th = P.sb(sh, F32)
        fw.op("dve", lambda e: e.tensor_tensor(out=er[:], in0=are[:], in1=step[:], op=ALU.mult), reads=[tpar], writes=[tpar])
        fw.op("act", lambda e: e.activation(out=er[:], in_=er[:], func=AF.Exp), reads=[tpar], writes=[tpar])
        fw.op("dve", lambda e: e.tensor_tensor(out=th[:], in0=aim[:], in1=step[:], op=ALU.mult), reads=[tpar], writes=[tpar])
        sn = P.sb(sh, F32)
        cs = P.sb(sh, F32)
        ttrig = Tok()
        sin_reduced(fw, P, sn[:], th[:], tpar, sh, 0.0, ttrig)
        sin_reduced(fw, P, cs[:], th[:], tpar, sh, math.pi / 2, ttrig)
        PW = P.sb([128, 9, 3, 2, 8], F32)
        tpw = Tok()
        fw.op("dve", lambda e: e.tensor_tensor(out=PW[:, 0, 0], in0=er[:], in1=cs[:], op=ALU.mult), reads=[tpar, ttrig], writes=[tpw])
        fw.op("dve", lambda e: e.tensor_tensor(out=PW[:, 0, 1], in0=er[:], in1=sn[:], op=ALU.mult), reads=[tpar, ttrig], writes=[tpw])
        t1 = P.sb(sh, F32)
        t2 = P.sb(sh, F32)
        for k in range(9):
            fw.op("dve", lambda e, k=k: e.tensor_scalar(out=PW[:, k, 2], in0=PW[:, k, 1], scalar1=-1.0, scalar2=None, op0=ALU.mult),
                  reads=[tpw], writes=[tpw])
            if k == 8:
                break
            fw.op("dve", lambda e, k=k: e.tensor_tensor(out=t1[:], in0=PW[:, k, 0], in1=PW[:, k, 0], op=ALU.mult), reads=[tpw], writes=[tpw])
            fw.op("dve", lambda e, k=k: e.tensor_tensor(out=t2[:], in0=PW[:, k, 1], in1=PW[:, k, 1], op=ALU.mult), reads=[tpw], writes=[tpw])
            fw.op("dve", lambda e, k=k: e.tensor_tensor(out=PW[:, k + 1, 0], in0=t1[:], in1=t2[:], op=ALU.subtract), reads=[tpw], writes=[tpw])
            fw.op("dve", lambda e, k=k: e.scalar_tensor_tensor(out=PW[:, k + 1, 1], in0=PW[:, k, 0], scalar=2.0, in1=PW[:, k, 1],
                                                               op0=ALU.mult, op1=ALU.mult), reads=[tpw], writes=[tpw])
        br = P.sb(sh, F32)
        bi = P.sb(sh, F32)
        nbi = P.sb(sh, F32)
        den = P.sb(sh, F32)
        nr = P.sb(sh, F32)
        tb = Tok()
        fw.op("dve", lambda e: e.tensor_tensor(out=den[:], in0=are[:], in1=are[:], op=ALU.mult), reads=[tpar], writes=[tb])
        fw.op("dve", lambda e: e.tensor_tensor(out=t1[:], in0=aim[:], in1=aim[:], op=ALU.mult), reads=[tpar, tpw], writes=[tpw])
        fw.op("dve", lambda e: e.tensor_tensor(out=den[:], in0=den[:], in1=t1[:], op=ALU.add), reads=[tb, tpw], writes=[tb])
        fw.op("dve", lambda e: e.reciprocal(out=den[:], in_=den[:]), reads=[tb], writes=[tb])
        fw.op("dve", lambda e: e.tensor_scalar(out=nr[:], in0=PW[:, 0, 0], scalar1=-1.0, scalar2=None, op0=ALU.add), reads=[tpw], writes=[tb])
        fw.op("dve", lambda e: e.tensor_tensor(out=t1[:], in0=nr[:], in1=are[:], op=ALU.mult), reads=[tb, tpar, tpw], writes=[tpw])
        fw.op("dve", lambda e: e.tensor_tensor(out=t2[:], in0=PW[:, 0, 1], in1=aim[:], op=ALU.mult), reads=[tpw, tpar], writes=[tpw])
        fw.op("dve", lambda e: e.tensor_tensor(out=t1[:], in0=t1[:], in1=t2[:], op=ALU.add), reads=[tpw], writes=[tpw])
        fw.op("dve", lambda e: e.tensor_tensor(out=br[:], in0=t1[:], in1=den[:], op=ALU.mult), reads=[tpw, tb], writes=[tb])
        fw.op("dve", lambda e: e.tensor_tensor(out=t1[:], in0=PW[:, 0, 1], in1=are[:], op=ALU.mult), reads=[tpw, tpar, tb], writes=[tpw])
        fw.op("dve", lambda e: e.tensor_tensor(out=t2[:], in0=nr[:], in1=aim[:], op=ALU.mult), reads=[tb, tpar, tpw], writes=[tpw])
        fw.op("dve", lambda e: e.tensor_tensor(out=t1[:], in0=t1[:], in1=t2[:], op=ALU.subtract), reads=[tpw], writes=[tpw])
        fw.op("dve", lambda e: e.tensor_tensor(out=bi[:], in0=t1[:], in1=den[:], op=ALU.mult), reads=[tpw, tb], writes=[tb])
        fw.op("dve", lambda e: e.tensor_scalar(out=nbi[:], in0=bi[:], scalar1=-1.0, scalar2=None, op0=ALU.mult), reads=[tb], writes=[tb])
        BTb = P.sb([128, 2, 2, 8, 128], BF16)
        CTb = P.sb([128, 2, 2, 8, 128], BF16)
        tBT = Tok()
        tCT = Tok()
        with ExitStack() as es2:
            P2 = Pool_(fw, es2)
            BTf = P2.sb([128, 2, 2, 8, 128], F32)
            CTf = P2.sb([128, 2, 2, 8, 128], F32)
            CT2 = P2.sb([128, 2, 2, 8, 128], F32)
            tf1, tf2 = Tok(), Tok()
            fw.op("pool", lambda e: e.memset(BTf[:], 0.0), writes=[tf1])
            fw.op("pool", lambda e: e.memset(CTf[:], 0.0), writes=[tf2])
            for di in range(2):
                for ri, (bsrc, csrc) in enumerate(((b_re, c_re), (b_im, c_im))):
                    for g in range(16):
                        s, g2 = g // 2, g % 2
                        r0 = (g % 8) * 16
                        fw.dma("sp", BTf[r0:r0 + 16, di, ri, s, g2 * 64:(g2 + 1) * 64], bsrc[di, g].rearrange("p h -> h p"),
                               writes=[tf1], allow_slow_non_contiguous=True)
                        fw.dma("sp", CTf[g2 * 64:(g2 + 1) * 64, di, ri, s, r0:r0 + 16], csrc[di, g].rearrange("h p -> p h"),
                               writes=[tf2], allow_slow_non_contiguous=True)
            fw.op("act", lambda e: e.copy(out=BTb[:], in_=BTf[:]), reads=[tf1], writes=[tBT])
            bsh = [128, 2, 8, 128]
            brb = br[:].unsqueeze(3).to_broadcast(bsh)
            nbib = nbi[:].unsqueeze(3).to_broadcast(bsh)
            tc2 = Tok()
            fw.op("dve", lambda e: e.tensor_tensor(out=CT2[:, :, 0], in0=CTf[:, :, 0], in1=brb, op=ALU.mult), reads=[tf2, tb], writes=[tc2])
            fw.op("pool", lambda e: e.tensor_tensor(out=CT2[:, :, 1], in0=CTf[:, :, 1], in1=nbib, op=ALU.mult), reads=[tf2, tb], writes=[tc2])
            fw.op("dve", lambda e: e.tensor_tensor(out=CT2[:, :, 0], in0=CT2[:, :, 0], in1=CT2[:, :, 1], op=ALU.add), reads=[tc2], writes=[tc2])
            fw.op("act", lambda e: e.copy(out=CTb[:, :, 0], in_=CT2[:, :, 0]), reads=[tc2], writes=[tCT])
            fw.op("dve", lambda e: e.tensor_tensor(out=CT2[:, :, 0], in0=CTf[:, :, 0], in1=nbib, op=ALU.mult), reads=[tf2, tb, tCT, tc2], writes=[tc2])
            fw.op("pool", lambda e: e.tensor_tensor(out=CT2[:, :, 1], in0=CTf[:, :, 1], in1=brb, op=ALU.mult), reads=[tf2, tb, tc2], writes=[tc2])
            fw.op("dve", lambda e: e.tensor_tensor(out=CT2[:, :, 0], in0=CT2[:, :, 0], in1=CT2[:, :, 1], op=ALU.subtract), reads=[tc2], writes=[tc2])
            fw.op("act", lambda e: e.copy(out=CTb[:, :, 1], in_=CT2[:, :, 0]), reads=[tc2], writes=[tCT])
            fw.barrier()
        ub = P.sb([128, 2, TE], BF16)
        tub = Tok()
        for ct in range(2):
            fw.dma("pool", ub[:, ct, :], ZT[ct * 128:(ct + 1) * 128, :], writes=[tub])
        dk = P.sb([128, 2], F32)
        tdk = Tok()
        fw.dma("sp", dk[:], dsk.rearrange("(c p) -> p c", p=128), writes=[tdk], allow_slow_non_contiguous=True)
        y = P.sb([128, 2, T], F32)
        ty = Tok()
        for ct in range(2):
            fw.dma("sp", y[:, ct, :], ZT[ct * 128:(ct + 1) * 128, 0:T], writes=[ty])
        for ct in range(2):
            fw.op("pool", lambda e, ct=ct: e.tensor_scalar(out=y[:, ct, :], in0=y[:, ct, :], scalar1=dk[:, ct:ct + 1], scalar2=None, op0=ALU.mult),
                  reads=[ty, tdk], writes=[ty])
        Xs = Rot([P.sb([128, 2, N], F32) for _ in range(2)])
        xbs = Rot([P.sb([128, 2, N], BF16) for _ in range(2)])
        psr = Rot([P.ps() for _ in range(2)])
        psy = Rot([P.ps() for _ in range(1)])
        tmpy = Rot([P.sb([128, 512], F32) for _ in range(2)])
        CH = [(0, 512), (512, 1024), (1024, 1536), (1536, 2048), (2048, 2304)]

        def prep(di, s):
            off = 0 if di == 0 else 256
            ct = s // 4
            X, tX = Xs.next()
            for ri in range(2):
                for ci, (c0, c1) in enumerate(CH):
                    n = c1 - c0
                    ps, tps = psr.next()
                    fw.op("pe", lambda e, ps=ps, ri=ri, c0=c0, c1=c1, n=n: e.matmul(
                        ps[:, :n], lhsT=BTb[:, di, ri, s, :], rhs=ub[:, ct, off + c0:off + c1], start=True, stop=True),
                        reads=[tBT, tub], writes=[tps])
                    fw.op("act", lambda e, ps=ps, ri=ri, c0=c0, c1=c1, n=n: e.copy(out=X[:, ri, c0:c1], in_=ps[:, :n]),
                          reads=[tps], writes=[tX])
            return X, tX

        def scan_gen(di, s, X, tX):
            def cstep(w_re, w_im, r_re, r_im, k):
                pr = PW[:, k, 0, di, s:s + 1]
                pi = PW[:, k, 1, di, s:s + 1]
                npi = PW[:, k, 2, di, s:s + 1]
                for (o, i0, sc) in ((w_re, r_re, pr), (w_re, r_im, npi), (w_im, r_im, pr), (w_im, r_re, pi)):
                    fw.op("dve", lambda e, o=o, i0=i0, sc=sc: e.scalar_tensor_tensor(out=o, in0=i0, scalar=sc, in1=o, op0=ALU.mult, op1=ALU.add),
                          reads=[tX, tpw], writes=[tX])
                    yield
            for k in range(8):
                st_ = 1 << k
                Xv = [X[:, ri, :].rearrange("p (m c) -> p m c", c=2 * st_) for ri in range(2)]
                if di == 0:
                    yield from cstep(Xv[0][:, :, 2 * st_ - 1], Xv[1][:, :, 2 * st_ - 1], Xv[0][:, :, st_ - 1], Xv[1][:, :, st_ - 1], k)
                else:
                    yield from cstep(Xv[0][:, :, 0], Xv[1][:, :, 0], Xv[0][:, :, st_], Xv[1][:, :, st_], k)
            for i in (range(1, 9) if di == 0 else range(7, -1, -1)):
                if di == 0:
                    w, r = 256 * i + 255, 256 * (i - 1) + 255
                else:
                    w, r = 256 * i, 256 * (i + 1)
                yield from cstep(X[:, 0, w:w + 1], X[:, 1, w:w + 1], X[:, 0, r:r + 1], X[:, 1, r:r + 1], 8)
            for k in range(7, -1, -1):
                st_ = 1 << k
                Xv = [X[:, ri, :].rearrange("p (m c) -> p m c", c=2 * st_) for ri in range(2)]
                if di == 0:
                    yield from cstep(Xv[0][:, 1:, st_ - 1], Xv[1][:, 1:, st_ - 1], Xv[0][:, :-1, 2 * st_ - 1], Xv[1][:, :-1, 2 * st_ - 1], k)
                else:
                    yield from cstep(Xv[0][:, :-1, st_], Xv[1][:, :-1, st_], Xv[0][:, 1:, 0], Xv[1][:, 1:, 0], k)

        def fin(di, s, X, tX):
            ct = s // 4
            xb, txb = xbs.next()
            fw.op("act", lambda e: e.copy(out=xb[:], in_=X[:]), reads=[tX], writes=[txb])
            for (c0, c1) in CH:
                n = c1 - c0
                if di == 0:
                    y0 = c0
                else:
                    y0 = c0 + 256 if c0 < 2048 else 0
                ps, tps = psy.next()
                for ri in range(2):
                    fw.op("pe", lambda e, ps=ps, ri=ri, c0=c0, c1=c1, n=n: e.matmul(
                        ps[:, :n], lhsT=CTb[:, di, ri, s, :], rhs=xb[:, ri, c0:c1], start=(ri == 0), stop=(ri == 1)),
                        reads=[tCT, txb], writes=[tps])
                tm, ttm = tmpy.next()
                fw.op("act", lambda e, tm=tm, ps=ps, n=n: e.copy(out=tm[:, :n], in_=ps[:, :n]), reads=[tps], writes=[ttm])
                fw.op("pool", lambda e, tm=tm, y0=y0, n=n: e.tensor_tensor(out=y[:, ct, y0:y0 + n], in0=y[:, ct, y0:y0 + n], in1=tm[:, :n], op=ALU.add),
                      reads=[ttm, ty], writes=[ty])

        pairs = [(di, s0, s0 + 1) for di in range(2) for s0 in range(0, 8, 2)]
        yield "core"
        for pi_, (di, s0, s1) in enumerate(pairs):
            cur = [prep(di, s0), prep(di, s1)]
            yield
            gens = [scan_gen(di, s0, *cur[0]), scan_gen(di, s1, *cur[1])]
            alive = [True, True]
            while any(alive):
                for gi in range(2):
                    if alive[gi]:
                        try:
                            next(gens[gi])
                            yield
                        except StopIteration:
                            alive[gi] = False
            fin(di, s0, *cur[0])
            yield
            fin(di, s1, *cur[1])
            yield
        yield "fin"
        tg = Tok()
        gw = load_w_bf16(fw, P, glu_w, 256, 256, tg, "gluw")
        gb = P.sb([128, 2], F32)
        og = P.sb([128, 2], F32)
        fw.dma("sp", gb[:], glu_b.rearrange("(c p) -> p c", p=128), writes=[tg], allow_slow_non_contiguous=True)
        fw.dma("sp", og[:], out_g.rearrange("(c p) -> p c", p=128), writes=[tg], allow_slow_non_contiguous=True)
        ones = P.sb([128, 128], BF16)
        eps = P.sb([128, 1], F32)
        fw.op("pool", lambda e: e.memset(ones[:], 1.0), writes=[tg])
        fw.op("pool", lambda e: e.memset(eps[:], RMS_EPS), writes=[tg])
        a32 = P.sb([128, 2, 512], F32)
        ab = P.sb([128, 2, 512], BF16)
        w1 = P.sb([128, 2, 512], F32)
        w2 = P.sb([128, 2, 512], F32)
        sqb = P.sb([128, 2, 512], BF16)
        rs = P.sb([128, 512], F32)
        ta, tw1, trs = Tok(), Tok(), Tok()
        stg = Rot([P.sb([128, 512], F32) for _ in range(2)])
        tout = Tok()
        C1 = math.sqrt(2.0 / math.pi)
        for (lo, hi, ic) in SEGS:
            n = hi - lo
            yv = y[:, :, lo:hi]
            fw.op("act", lambda e, yv=yv, n=n: e.activation(out=w1[:, :, :n], in_=yv, func=AF.Square), reads=[ty], writes=[tw1])
            fw.op("dve", lambda e, n=n: e.tensor_scalar(out=w1[:, :, :n], in0=w1[:, :, :n], scalar1=0.044715 * C1, scalar2=C1, op0=ALU.mult, op1=ALU.add),
                  reads=[tw1], writes=[tw1])
            fw.op("dve", lambda e, yv=yv, n=n: e.tensor_tensor(out=w1[:, :, :n], in0=w1[:, :, :n], in1=yv, op=ALU.mult), reads=[tw1, ty], writes=[tw1])
            fw.op("act", lambda e, n=n: e.activation(out=w1[:, :, :n], in_=w1[:, :, :n], func=AF.Tanh), reads=[tw1], writes=[tw1])
            fw.op("dve", lambda e, n=n: e.tensor_scalar(out=w1[:, :, :n], in0=w1[:, :, :n], scalar1=0.5, scalar2=0.5, op0=ALU.mult, op1=ALU.add),
                  reads=[tw1], writes=[tw1])
            fw.op("dve", lambda e, yv=yv, n=n: e.tensor_tensor(out=a32[:, :, :n], in0=w1[:, :, :n], in1=yv, op=ALU.mult), reads=[tw1, ty], writes=[ta])
            fw.op("act", lambda e, n=n: e.copy(out=ab[:, :, :n], in_=a32[:, :, :n]), reads=[ta], writes=[ta])
            for nt in range(2):
                ps, tps = psr.next()
                for k in range(2):
                    fw.op("pe", lambda e, ps=ps, k=k, nt=nt, n=n: e.matmul(ps[:, :n], lhsT=gw[:, k, nt * 128:(nt + 1) * 128], rhs=ab[:, k, :n],
                                                                      start=(k == 0), stop=(k == 1)), reads=[tg, ta], writes=[tps])
                fw.op("act", lambda e, ps=ps, nt=nt, n=n: e.activation(out=w2[:, nt, :n], in_=ps[:, :n], func=AF.Sigmoid, bias=gb[:, nt:nt + 1]),
                      reads=[tps, tg], writes=[tw1])
            fw.op("dve", lambda e, n=n: e.tensor_tensor(out=w2[:, :, :n], in0=w2[:, :, :n], in1=a32[:, :, :n], op=ALU.mult), reads=[tw1, ta], writes=[tw1])
            fw.op("act", lambda e, n=n: e.activation(out=sqb[:, :, :n], in_=w2[:, :, :n], func=AF.Square), reads=[tw1], writes=[tw1])
            ps, tps = psr.next()
            for k in range(2):
                fw.op("pe", lambda e, ps=ps, k=k, n=n: e.matmul(ps[:, :n], lhsT=ones[:], rhs=sqb[:, k, :n], start=(k == 0), stop=(k == 1)),
                      reads=[tw1, tg], writes=[tps])
            rstd_from_ps(fw, rs, trs, ps, tps, n, 1.0 / 256, eps[:, 0:1], tg)
            for k in range(2):
                sg, tsg = stg.next()
                fw.op("dve", lambda e, sg=sg, k=k, n=n: e.scalar_tensor_tensor(out=sg[:, :n], in0=w2[:, k, :n], scalar=og[:, k:k + 1], in1=rs[:, :n],
                                                                             op0=ALU.mult, op1=ALU.mult), reads=[tw1, trs, tg], writes=[tsg])
                fw.dma("sp", S5T[k * 128:(k + 1) * 128, lo:hi], sg[:, :n], reads=[tsg], writes=[tout])
        fw.barrier()


def gen_p3(fw, io):
    ZT = io("ZT", [2048, TE], F32, "in")
    VT = io("VT", [T, 128], F32, "in")
    qn_g = io("att_qn_g", [64], F32, "in")
    kn_g = io("att_kn_g", [64], F32, "in")
    og = io("att_out_g", [512], F32, "in")
    POS = io("POS", [128, NLAT], F32, "in")
    CST = io("CST", [128, 260], F32, "in")
    ATT = io("ATT_T", [512, T], F32, "out")
    with ExitStack() as es:
        P = Pool_(fw, es)
        cst = P.sb([128, 260], F32)
        tc = Tok()
        fw.dma("sp", cst[:], CST, writes=[tc])
        permb = P.sb([128, 128], BF16)
        bdb = P.sb([128, 128], BF16)
        fw.op("dve", lambda e: e.tensor_copy(out=permb[:], in_=cst[:, 1:129]), reads=[tc], writes=[tc])
        fw.op("dve", lambda e: e.tensor_copy(out=bdb[:], in_=cst[:, 129:257]), reads=[tc], writes=[tc])
        eps = P.sb([128, 2], F32)
        teps = Tok()
        fw.op("pool", lambda e: e.memset(eps[:, 0:1], RMS_EPS), writes=[teps])
        fw.op("pool", lambda e: e.memset(eps[:, 1:2], 0.0), writes=[teps])
        gq = P.sb([128, 2], F32)
        tg = Tok()
        for h in range(2):
            fw.dma("sp", gq[h * 64:(h + 1) * 64, 0:1], qn_g.rearrange("(p o) -> p o", o=1), writes=[tg])
            fw.dma("sp", gq[h * 64:(h + 1) * 64, 1:2], kn_g.rearrange("(p o) -> p o", o=1), writes=[tg])
        qb = P.sb([128, 4, T], BF16)
        kd = P.sb([128, 2, T], BF16)
        tq = Tok()
        esr = ExitStack()
        Pr = Pool_(fw, esr)
        cos = Pr.sb([128, NLAT], F32)
        sin = Pr.sb([128, NLAT], F32)
        ttab = Tok()
        with ExitStack() as es2:
            P2 = Pool_(fw, es2)
            ang = P2.sb([128, NLAT], F32)
            tmpf = P2.sb([128, NLAT], F32)
            tmpi = P2.sb([128, NLAT], I32)
            ta = Tok()
            fw.dma("sp", ang[:], POS, writes=[ta])
            fw.op("dve", lambda e: e.tensor_scalar(out=ang[:], in0=ang[:], scalar1=cst[:, 0:1], scalar2=None, op0=ALU.mult),
                  reads=[ta, tc], writes=[ta])
            for (tab, off) in ((sin, 0.0), (cos, math.pi / 2)):
                fw.op("dve", lambda e, off=off: e.tensor_scalar(out=tmpi[:], in0=ang[:], scalar1=off, scalar2=1.0 / (2 * math.pi),
                                                                op0=ALU.add, op1=ALU.mult), reads=[ta], writes=[ta])
                fw.op("dve", lambda e: e.tensor_copy(out=tmpf[:], in_=tmpi[:]), reads=[ta], writes=[ta])
                fw.op("dve", lambda e: e.scalar_tensor_tensor(out=tmpf[:], in0=tmpf[:], scalar=-2 * math.pi, in1=ang[:],
                                                              op0=ALU.mult, op1=ALU.add), reads=[ta], writes=[ta])
                fw.op("dve", lambda e, off=off: e.tensor_scalar(out=tmpf[:], in0=tmpf[:], scalar1=off, scalar2=math.pi,
                                                                op0=ALU.add, op1=ALU.min), reads=[ta], writes=[ta])
                fw.op("dve", lambda e: e.tensor_scalar(out=tmpf[:], in0=tmpf[:], scalar1=-math.pi, scalar2=None, op0=ALU.max),
                      reads=[ta], writes=[ta])
                fw.op("act", lambda e, tab=tab: e.activation(out=tab[:], in_=tmpf[:], func=AF.Sin), reads=[ta], writes=[ttab])
            fw.barrier()
        with ExitStack() as es2:
            P2 = Pool_(fw, es2)
            raw = Rot([P2.sb([128, 512], F32) for _ in range(2)])
            sqr = Rot([P2.sb([128, 512], BF16) for _ in range(2)])
            psr = Rot([P2.ps() for _ in range(2)])
            psr2 = Rot([P2.ps() for _ in range(2)])
            rsr = Rot([P2.sb([128, 512], F32) for _ in range(2)])
            nbr = Rot([P2.sb([128, 512], BF16) for _ in range(2)])
            t1r = Rot([P2.sb([128, 512], F32) for _ in range(2)])
            t2r = Rot([P2.sb([128, 512], F32) for _ in range(2)])
            items = [("q", j) for j in range(4)] + [("k", g) for g in range(2)]
            for (kind, j) in items:
                for (lo, hi, ic) in SEGS:
                    n = hi - lo
                    rw, trw = raw.next()
                    if kind == "q":
                        fw.dma("sp", rw[:, :n], ZT[256 + j * 128:256 + (j + 1) * 128, lo:hi], writes=[trw])
                        gcol = 0
                        dst = qb[:, j, lo:hi]
                    else:
                        for h in range(2):
                            fw.dma("sp", rw[h * 64:(h + 1) * 64, :n], ZT[768 + j * 64:768 + (j + 1) * 64, lo:hi], writes=[trw])
                        gcol = 1
                        dst = kd[:, j, lo:hi]
                    sq, tsq = sqr.next()
                    fw.op("act", lambda e, sq=sq, rw=rw, n=n: e.activation(out=sq[:, :n], in_=rw[:, :n], func=AF.Square), reads=[trw], writes=[tsq])
                    ps, tps = psr.next()
                    fw.op("pe", lambda e, ps=ps, sq=sq, n=n: e.matmul(ps[:, :n], lhsT=bdb[:], rhs=sq[:, :n], start=True, stop=True),
                          reads=[tsq, tc], writes=[tps])
                    rs, trs = rsr.next()
                    rstd_from_ps(fw, rs, trs, ps, tps, n, 1.0, eps[:, 0:1], teps)
                    t1, tt1 = t1r.next()
                    fw.op("dve", lambda e, t1=t1, rw=rw, rs=rs, n=n, gcol=gcol: e.scalar_tensor_tensor(
                        out=t1[:, :n], in0=rw[:, :n], scalar=gq[:, gcol:gcol + 1], in1=rs[:, :n], op0=ALU.mult, op1=ALU.mult),
                        reads=[trw, trs, tg], writes=[tt1])
                    if ic:
                        fw.op("act", lambda e, dst=dst, t1=t1, n=n: e.copy(out=dst, in_=t1[:, :n]), reads=[tt1], writes=[tq])
                        continue
                    nb, tnb = nbr.next()
                    fw.op("act", lambda e, nb=nb, t1=t1, n=n: e.copy(out=nb[:, :n], in_=t1[:, :n]), reads=[tt1], writes=[tnb])
                    ps2, tps2 = psr2.next()
                    fw.op("pe", lambda e, ps2=ps2, nb=nb, n=n: e.matmul(ps2[:, :n], lhsT=permb[:], rhs=nb[:, :n], start=True, stop=True),
                          reads=[tnb, tc], writes=[tps2])
                    p0 = lo - NCTX
                    t2, tt2 = t2r.next()
                    fw.op("dve", lambda e, t2=t2, ps2=ps2, n=n, p0=p0: e.tensor_tensor(out=t2[:, :n], in0=ps2[:, :n], in1=sin[:, p0:p0 + n], op=ALU.mult),
                          reads=[tps2, ttab], writes=[tt2])
                    fw.op("pool", lambda e, t1=t1, nb=nb, n=n, p0=p0: e.tensor_tensor(out=t1[:, :n], in0=nb[:, :n], in1=cos[:, p0:p0 + n], op=ALU.mult),
                          reads=[tnb, ttab, tt1], writes=[tt1])
                    fw.op("pool", lambda e, dst=dst, t1=t1, t2=t2, n=n: e.tensor_tensor(out=dst, in0=t1[:, :n], in1=t2[:, :n], op=ALU.add),
                          reads=[tt1, tt2], writes=[tq])
            fw.barrier()
        esr.close()
        va = P.sb([128, 18, 2, 128], BF16)
        tva = Tok()
        fw.op("pool", lambda e: e.memset(va[:], 1.0), writes=[tva])
        for g in range(2):
            fw.dma("pool", va[:, :, g, 0:64], VT.rearrange("(t p) c -> p t c", p=128)[:, :, g * 64:(g + 1) * 64], writes=[tva])
        att = P.sb([128, 4, T], BF16)
        tatt = Tok()
        pss = Rot([P.ps() for _ in range(2)])
        pso = Rot([P.ps() for _ in range(3)])
        pend = []
        yield "core"
        ptr = Rot([P.sb([128, 512], BF16) for _ in range(3)])
        rcr = Rot([P.sb([64, 512], F32) for _ in range(2)])
        jobs = [(0, 256, 0, 2)] + [(256 + 512 * i, 768 + 512 * i, 0, 18) for i in range(4)]
        for h in range(8):
            g = h // 4
            jt, r0 = h // 2, (h % 2) * 64
            for (qlo, qhi, k0, k1) in jobs:
                n = qhi - qlo
                po, tpo = pso.next()
                for kt in range(k0, k1):
                    ps, tps = pss.next()
                    fw.op("pe", lambda e, ps=ps, kt=kt, g=g, jt=jt, r0=r0, qlo=qlo, qhi=qhi, n=n: e.matmul(
                        ps[:, :n], lhsT=kd[r0:r0 + 64, g, kt * 128:(kt + 1) * 128], rhs=qb[r0:r0 + 64, jt, qlo:qhi],
                        start=True, stop=True), reads=[tq], writes=[tps])
                    pt, tpt = ptr.next()
                    fw.op("act", lambda e, pt=pt, ps=ps, n=n: e.activation(out=pt[:, :n], in_=ps[:, :n], func=AF.Exp, scale=0.125),
                          reads=[tps], writes=[tpt])
                    fw.op("pe", lambda e, po=po, pt=pt, kt=kt, g=g, n=n, k0=k0, k1=k1: e.matmul(
                        po[:, :n], lhsT=va[:, kt, g, :], rhs=pt[:, :n], start=(kt == k0), stop=(kt == k1 - 1)),
                        reads=[tpt, tva], writes=[tpo])
                def norm_job(po=po, tpo=tpo, n=n, jt=jt, r0=r0, qlo=qlo, qhi=qhi):
                    rc, trc = rcr.next()
                    fw.op("dve", lambda e: e.reciprocal(out=rc[:, :n], in_=po[64:128, :n]), reads=[tpo], writes=[trc])
                    fw.op("dve", lambda e: e.tensor_tensor(out=att[r0:r0 + 64, jt, qlo:qhi], in0=po[0:64, :n], in1=rc[:, :n], op=ALU.mult),
                          reads=[tpo, trc], writes=[tatt])
                pend.append(norm_job)
                if len(pend) > 2:
                    pend.pop(0)()
                yield
        for nj in pend:
            nj()
        yield "fin"
        ogs = P.sb([128, 4], F32)
        tog = Tok()
        fw.dma("sp", ogs[:], og.rearrange("(k p) -> p k", p=128), writes=[tog], allow_slow_non_contiguous=True)
        ones = P.sb([128, 128], BF16)
        fw.op("pool", lambda e: e.memset(ones[:], 1.0), writes=[tog])
        sq4 = P.sb([128, 4, 512], BF16)
        tsq4 = Tok()
        rs = P.sb([128, 512], F32)
        trs = Tok()
        stg = Rot([P.sb([128, 512], F32) for _ in range(3)])
        tout = Tok()
        for (lo, hi, ic) in SEGS:
            n = hi - lo
            fw.op("act", lambda e, lo=lo, hi=hi, n=n: e.activation(out=sq4[:, :, :n], in_=att[:, :, lo:hi], func=AF.Square), reads=[tatt], writes=[tsq4])
            ps, tps = pss.next()
            for k in range(4):
                fw.op("pe", lambda e, ps=ps, k=k, n=n: e.matmul(ps[:, :n], lhsT=ones[:], rhs=sq4[:, k, :n], start=(k == 0), stop=(k == 3)),
                      reads=[tsq4, tog], writes=[tps])
            rstd_from_ps(fw, rs, trs, ps, tps, n, 1.0 / 512, eps[:, 0:1], teps)
            for k in range(4):
                sg, tsg = stg.next()
                fw.op("dve", lambda e, sg=sg, k=k, lo=lo, hi=hi, n=n: e.scalar_tensor_tensor(
                    out=sg[:, :n], in0=att[:, k, lo:hi], scalar=ogs[:, k:k + 1], in1=rs[:, :n], op0=ALU.mult, op1=ALU.mult),
                    reads=[tatt, trs, tog], writes=[tsg])
                fw.dma("sp", ATT[k * 128:(k + 1) * 128, lo:hi], sg[:, :n], reads=[tsg], writes=[tout])
        fw.barrier()


def stage_p23(fw, io):
    g2 = gen_p2(fw, io)
    g3 = gen_p3(fw, io)

    def until(g, tag):
        for v in g:
            if v == tag:
                return True
        return False
    until(g2, "core")
    until(g3, "core")
    a2 = a3 = True
    R = 36
    while a2 or a3:
        if a2:
            for _ in range(R):
                if next(g2) == "fin":
                    a2 = False
                    break
        if a3:
            if next(g3) == "fin":
                a3 = False
    for _ in g3:
        pass
    for _ in g2:
        pass


RW_BASE = 1024
CHK = 64
NCH = T // CHK
LN_EPS_RW = 64e-5


def rw_consts():
    idx = np.arange(64)
    m = np.zeros((64, 4, 64), np.float32)
    m[:, 0, :] = (idx[:, None] < idx[None, :])
    m[:, 1, :] = (idx[:, None] > idx[None, :])
    m[:, 2, :] = (idx[:, None] <= idx[None, :])
    m[:, 3, :] = (idx[:, None] >= idx[None, :])
    bd = np.zeros((128, 128), np.float32)
    bd[:64, :64] = 1.0
    bd[64:, 64:] = 1.0
    return m, bd


def stage_p4(fw, io):
    ZT = io("ZT", [2048, TE], F32, "in")
    mu = io("rw_mu", [960], F32, "in")
    w0 = io("rw_w0", [2, 256], F32, "in")
    w2 = io("rw_w2", [2, 32, 256], F32, "in")
    a0 = io("rw_a0", [2, 256], F32, "in")
    a2 = io("rw_a2", [2, 32, 256], F32, "in")
    g2 = io("rw_g2", [64, 256], F32, "in")
    k_k = io("rw_k_k", [256], F32, "in")
    k_a = io("rw_k_a", [256], F32, "in")
    r_k = io("rw_r_k", [256], F32, "in")
    ln_g = io("rw_ln_g", [256], F32, "in")
    ln_b = io("rw_ln_b", [256], F32, "in")
    ident = io("ident", [128, 128], F32, "in")
    MASKS = io("RWMASK", [64, 4, 64], F32, "in")
    BD = io("RWBD", [128, 128], F32, "in")
    RWT = io("RWT", [256, T], F32, "out")
    N = T
    CH5 = [(0, 512), (512, 1024), (1024, 1536), (1536, 2048), (2048, 2560)]
    with ExitStack() as es:
        P = Pool_(fw, es)
        tc = Tok()
        idt = P.sb([128, 128], F32)
        idb = P.sb([128, 128], BF16)
        msk = P.sb([64, 4, 64], F32)
        bd1 = P.sb([128, 128], F32)
        fw.dma("sp", idt[:], ident, writes=[tc])
        fw.dma("sp", msk[:], MASKS, writes=[tc])
        fw.dma("sp", bd1[:], BD, writes=[tc])
        fw.op("dve", lambda e: e.tensor_copy(out=idb[:], in_=idt[:]), reads=[tc], writes=[tc])
        mrep = P.sb([64, 4, 4, 64], F32)
        for rep in range(4):
            fw.op("dve", lambda e, rep=rep: e.tensor_copy(out=mrep[:, :, rep, :], in_=msk[:]), reads=[tc], writes=[tc])
        pp = P.sb([128, 12, 2], F32)
        tpp = Tok()
        srcs = [w0[0], w0[1], a0[0], a0[1], k_k, k_a, k_a, r_k, ln_g, ln_b]
        for i, sap in enumerate(srcs):
            fw.dma("sp", pp[:, i, :], sap.rearrange("(c p) -> p c", p=128), writes=[tpp], allow_slow_non_contiguous=True)
        fw.op("dve", lambda e: e.tensor_scalar(out=pp[:, 6, :], in0=pp[:, 6, :], scalar1=-1.0, scalar2=1.0, op0=ALU.mult, op1=ALU.add),
              reads=[tpp], writes=[tpp])
        epsl = P.sb([128, 2], F32)
        fw.op("pool", lambda e: e.memset(epsl[:, 0:1], LN_EPS_RW), writes=[tpp])
        fw.op("pool", lambda e: e.memset(epsl[:, 1:2], 1e-24), writes=[tpp])
        wA = P.sb([128, 256], BF16)
        wB = P.sb([128, 256], BF16)
        tlw = Tok()
        fw.dma("pool", wA[0:32, :], w2[0], writes=[tlw])
        fw.dma("pool", wA[32:64, :], w2[1], writes=[tlw])
        fw.dma("pool", wA[64:96, :], a2[0], writes=[tlw])
        fw.dma("pool", wB[0:32, :], a2[1], writes=[tlw])
        fw.dma("pool", wB[64:128, :], g2, writes=[tlw])
        smask = P.sb([128, N], BF16)
        tsm = Tok()
        fw.op("pool", lambda e: e.memset(smask[:], 1.0), writes=[tsm])
        fw.op("pool", lambda e: e.memset(smask[:].rearrange("p (c j) -> p c j", j=CHK)[:, :, 0], 0.0), writes=[tsm])

        def load_shift(dst, tdst, row0, rows, P2, post=None, pb=0):
            zt_ = P2.sb([128, TE], F32)
            nbt_ = P2.sb([128, TE], F32)
            mtt_ = P2.sb([128, 2], F32)
            tz, tnb, tm = Tok(), Tok(), Tok()
            ps_ = slice(pb, pb + rows)
            z = zt_[ps_, :]
            fw.dma("sp", z, ZT[RW_BASE + row0:RW_BASE + row0 + rows, :], writes=[tz])
            fw.dma("sp", mtt_[ps_, 0:1], mu[row0:row0 + rows].rearrange("(p o) -> p o", o=1), writes=[tm])
            fw.op("dve", lambda e: e.tensor_scalar(out=mtt_[ps_, 1:2], in0=mtt_[ps_, 0:1], scalar1=0.5, scalar2=None, op0=ALU.mult), reads=[tm], writes=[tm])
            fw.op("dve", lambda e: e.tensor_scalar(out=mtt_[ps_, 0:1], in0=mtt_[ps_, 0:1], scalar1=-1.0, scalar2=1.0, op0=ALU.mult, op1=ALU.add),
                  reads=[tm], writes=[tm])
            fw.op("pool", lambda e: e.memset(nbt_[ps_, 0:1], 0.0), writes=[tnb])
            fw.op("pool", lambda e: e.tensor_copy(out=nbt_[ps_, 1:TE], in_=zt_[ps_, 0:TE - 1]), reads=[tz], writes=[tnb])
            fw.op("pool", lambda e: e.tensor_tensor(out=nbt_[ps_, 0:TE - 1], in0=nbt_[ps_, 0:TE - 1], in1=zt_[ps_, 1:TE], op=ALU.add),
                  reads=[tz, tnb], writes=[tnb])
            for cb in (256, 2304):
                fw.op("pool", lambda e, cb=cb: e.tensor_tensor(out=nbt_[ps_, cb:cb + 1], in0=nbt_[ps_, cb:cb + 1], in1=zt_[ps_, cb - 1:cb], op=ALU.subtract),
                      reads=[tz, tnb], writes=[tnb])
                fw.op("pool", lambda e, cb=cb: e.tensor_tensor(out=nbt_[ps_, cb - 1:cb], in0=nbt_[ps_, cb - 1:cb], in1=zt_[ps_, cb:cb + 1], op=ALU.subtract),
                      reads=[tz, tnb], writes=[tnb])
            fw.op("act", lambda e: e.activation(out=z, in_=z, func=AF.Identity, scale=mtt_[ps_, 0:1]), reads=[tz, tm], writes=[tz])
            if post is None:
                fw.op("dve", lambda e: e.scalar_tensor_tensor(out=dst, in0=nbt_[ps_, :], scalar=mtt_[ps_, 1:2], in1=z, op0=ALU.mult, op1=ALU.add),
                      reads=[tz, tnb, tm], writes=[tdst])
            else:
                fw.op("dve", lambda e: e.scalar_tensor_tensor(out=z, in0=nbt_[ps_, :], scalar=mtt_[ps_, 1:2], in1=z, op0=ALU.mult, op1=ALU.add),
                      reads=[tz, tnb, tm], writes=[tz])
                fw.op("act", lambda e: e.activation(out=dst, in_=z, func=post), reads=[tz], writes=[tdst])

        lorA = P.sb([128, TE], BF16)
        lorB = P.sb([128, TE], BF16)
        tlor = Tok()
        for i in range(4):
            with ExitStack() as es2:
                dstt = lorA[32 * i:32 * i + 32, :] if i < 3 else lorB[0:32, :]
                load_shift(dstt, tlor, 768 + 32 * i, 32, Pool_(fw, es2), post=(AF.Tanh if i < 2 else AF.Copy), pb=(32 * i if i < 3 else 0))
                fw.barrier()
        with ExitStack() as es2:
            load_shift(lorB[64:128, :], tlor, 896, 64, Pool_(fw, es2), post=AF.Sigmoid, pb=64)
            fw.barrier()

        for c in range(2):
            with ExitStack() as esc:
                Pc = Pool_(fw, esc)
                rr = Pc.sb([128, TE], F32)
                kx = Pc.sb([128, TE], F32)
                vv = Pc.sb([128, TE], F32)
                kk = Pc.sb([128, TE], F32)
                trr, tkx, tvv, tkk = Tok(), Tok(), Tok(), Tok()
                vtok = Pc.sb([64, TE // CHK, 128], BF16)
                tvt = Tok()
                yacc = Pc.sb([128, T], F32)
                bon = Pc.sb([128, T], F32)
                tya, tbon = Tok(), Tok()
                fw.op("pool", lambda e: e.memset(yacc[:], 0.0), writes=[tya])
                fw.op("pool", lambda e: e.memset(bon[:], 0.0), writes=[tbon])
                for (dst, tdst, r0) in ((rr, trr, 0), (kx, tkx, 256), (vv, tvv, 512)):
                    with ExitStack() as es2:
                        load_shift(dst[:], tdst, r0 + c * 128, 128, Pool_(fw, es2))
                        fw.barrier()
                with ExitStack() as es2:
                    P2 = Pool_(fw, es2)
                    sq = P2.sb([128, 512], F32)
                    rn = P2.sb([128, 512], F32)
                    tsq, trn = Tok(), Tok()
                    ps1 = P2.ps()
                    tps1 = Tok()
                    fw.op("dve", lambda e: e.tensor_scalar(out=kk[:], in0=kx[:], scalar1=pp[:, 4, c:c + 1], scalar2=None, op0=ALU.mult),
                          reads=[tkx, tpp], writes=[tkk])
                    for (c0, c1) in CH5:
                        fw.op("act", lambda e, c0=c0, c1=c1: e.activation(out=sq[:], in_=kk[:, c0:c1], func=AF.Square), reads=[tkk], writes=[tsq])
                        fw.op("pe", lambda e: e.matmul(ps1[:], lhsT=bd1[:], rhs=sq[:], start=True, stop=True), reads=[tsq, tc], writes=[tps1])
                        rstd_from_ps(fw, rn, trn, ps1, tps1, 512, 1.0, epsl[:, 1:2], tpp)
                        fw.op("dve", lambda e, c0=c0, c1=c1: e.tensor_tensor(out=kk[:, c0:c1], in0=kk[:, c0:c1], in1=rn[:], op=ALU.mult),
                              reads=[tkk, trn], writes=[tkk])
                    vb = P2.sb([128, TE], BF16)
                    tvb = Tok()
                    fw.op("act", lambda e: e.copy(out=vb[:], in_=vv[:]), reads=[tvv], writes=[tvb])
                    pst = P2.ps([128, 1024], BF16)
                    tpst = Tok()
                    for q in range(TE // CHK // 4):
                        for j in range(4):
                            ch = q * 4 + j
                            fw.op("pe", lambda e, j=j, ch=ch: e.transpose(pst[0:64, j * 128:(j + 1) * 128], vb[:, ch * CHK:(ch + 1) * CHK], idb[:]),
                                  reads=[tvb, tc], writes=[tpst])
                        fw.op("dve", lambda e, q=q: e.tensor_copy(out=vtok[:, q * 4:(q + 1) * 4, :], in_=pst[0:64, 0:512].rearrange("p (j c) -> p j c", j=4)),
                              reads=[tpst], writes=[tvt])
                    fw.barrier()

                for di in range(2):
                    off = 0 if di == 0 else 256
                    with ExitStack() as esd:
                        Pd = Pool_(fw, esd)
                        aT = Pd.sb([128, N], BF16)
                        bT = Pd.sb([128, N], BF16)
                        kT = Pd.sb([128, N], BF16)
                        rT = Pd.sb([128, N], BF16)
                        btok = Pd.sb([64, NCH, 128], BF16)
                        ktok = Pd.sb([64, NCH, 128], BF16)
                        pC = Pd.sb([128, NCH], F32)
                        tops = Tok()
                        ttok = Tok()
                        with ExitStack() as es2:
                            P2 = Pool_(fw, es2)
                            ld = P2.sb([128, TE], F32)
                            kd = P2.sb([128, TE], F32)
                            bb = P2.sb([128, TE], F32)
                            tld, tkd, tbb = Tok(), Tok(), Tok()
                            psr = Rot([P2.ps() for _ in range(3)])
                            tm5 = Rot([P2.sb([128, 512], F32) for _ in range(2)])
                            for (c0, c1) in CH5:
                                ps, tps = psr.next()
                                fw.op("pe", lambda e, ps=ps, c0=c0, c1=c1: e.matmul(ps[:], lhsT=wA[32 * di:32 * di + 32, c * 128:(c + 1) * 128], rhs=lorA[32 * di:32 * di + 32, c0:c1],
                                                                                  start=True, stop=True), reads=[tlw, tlor], writes=[tps])
                                fw.op("act", lambda e, ps=ps, c0=c0, c1=c1: e.activation(out=ld[:, c0:c1], in_=ps[:], func=AF.Sigmoid, bias=pp[:, di, c:c + 1]),
                                      reads=[tps, tpp], writes=[tld])
                                ps, tps = psr.next()
                                fw.op("pe", lambda e, ps=ps, c0=c0, c1=c1: e.matmul(ps[:], lhsT=(wA[64:96, c * 128:(c + 1) * 128] if di == 0 else wB[0:32, c * 128:(c + 1) * 128]),
                                                                                  rhs=(lorA[64:96, c0:c1] if di == 0 else lorB[0:32, c0:c1]),
                                                                                  start=True, stop=True), reads=[tlw, tlor], writes=[tps])
                                fw.op("act", lambda e, ps=ps, c0=c0, c1=c1: e.activation(out=bb[:, c0:c1], in_=ps[:], func=AF.Sigmoid, bias=pp[:, 2 + di, c:c + 1]),
                                      reads=[tps, tpp], writes=[tbb])
                            fw.op("pool", lambda e: e.tensor_scalar(out=ld[:], in0=ld[:], scalar1=-math.exp(-0.5), scalar2=None, op0=ALU.mult), reads=[tld], writes=[tld])
                            fw.op("act", lambda e: e.activation(out=kd[:], in_=bb[:], func=AF.Identity, scale=pp[:, 5, c:c + 1], bias=pp[:, 6, c:c + 1]),
                                  reads=[tbb, tpp], writes=[tkd])
                            fw.op("dve", lambda e: e.tensor_tensor(out=kd[:], in0=kd[:], in1=kx[:], op=ALU.mult), reads=[tkd, tkx], writes=[tkd])
                            fw.op("pool", lambda e: e.tensor_tensor(out=bb[:], in0=bb[:], in1=kk[:], op=ALU.mult), reads=[tbb, tkk], writes=[tbb])
                            for (c0, c1) in CH5:
                                c1 = min(c1, T)
                                n = c1 - c0
                                tm, ttm = tm5.next()
                                fw.op("dve", lambda e, tm=tm, c0=c0, c1=c1, n=n: e.scalar_tensor_tensor(out=tm[:, :n], in0=rr[:, c0:c1], scalar=pp[:, 7, c:c + 1], in1=kd[:, c0:c1],
                                                                                                 op0=ALU.mult, op1=ALU.mult), reads=[trr, tkd, tpp], writes=[ttm])
                                ps, tps = psr.next()
                                fw.op("pe", lambda e, ps=ps, tm=tm, n=n: e.matmul(ps[:, :n], lhsT=bd1[:], rhs=tm[:, :n], start=True, stop=True), reads=[ttm, tc], writes=[tps])
                                tm2, ttm2 = tm5.next()
                                fw.op("dve", lambda e, tm2=tm2, ps=ps, c0=c0, c1=c1, n=n: e.tensor_tensor(out=tm2[:, :n], in0=ps[:, :n], in1=vv[:, c0:c1], op=ALU.mult),
                                      reads=[tps, tvv], writes=[ttm2])
                                fw.op("pool", lambda e, tm2=tm2, c0=c0, c1=c1, n=n: e.tensor_tensor(out=bon[:, c0:c1], in0=bon[:, c0:c1], in1=tm2[:, :n], op=ALU.add),
                                      reads=[ttm2, tbon], writes=[tbon])
                            cs = P2.sb([128, N], F32)
                            ex = P2.sb([128, N], F32)
                            tcs, tex = Tok(), Tok()
                            ldw = ld[:, off:off + N]
                            fw.op("dve", lambda e: e.tensor_tensor_scan(out=cs[:], data0=smask[:], data1=ldw, initial=0.0, op0=ALU.mult, op1=ALU.add),
                                  reads=[tsm, tld], writes=[tcs])
                            csv = cs[:].rearrange("p (c j) -> p c j", j=CHK)
                            tot = P2.sb([128, NCH, 1], F32)
                            ttot = Tok()
                            fw.op("dve", lambda e: e.tensor_copy(out=tot[:], in_=csv[:, :, CHK - 1:CHK]), reads=[tcs], writes=[ttot])
                            totb = tot[:].to_broadcast([128, NCH, CHK])
                            if di == 1:
                                fw.op("dve", lambda e: e.tensor_tensor(out=csv, in0=totb, in1=csv, op=ALU.subtract), reads=[ttot, tcs], writes=[tcs])
                                fw.op("dve", lambda e: e.tensor_tensor(out=cs[:], in0=cs[:], in1=ldw, op=ALU.add), reads=[tcs, tld], writes=[tcs])
                            fw.op("act", lambda e: e.activation(out=pC[:], in_=tot[:, :, 0], func=AF.Exp), reads=[ttot], writes=[tops])
                            kdw, bbw = kd[:, off:off + N], bb[:, off:off + N]
                            rrw, kkw = rr[:, off:off + N], kk[:, off:off + N]
                            fw.op("act", lambda e: e.activation(out=ex[:], in_=cs[:], func=AF.Exp), reads=[tcs], writes=[tex])
                            fw.op("dve", lambda e: e.tensor_tensor(out=rT[:], in0=rrw, in1=ex[:], op=ALU.mult), reads=[trr, tex], writes=[tops])
                            fw.op("act", lambda e: e.activation(out=ex[:], in_=cs[:], func=AF.Exp, scale=-1.0), reads=[tcs, tops], writes=[tex])
                            fw.op("dve", lambda e: e.tensor_tensor(out=bT[:], in0=bbw, in1=ex[:], op=ALU.mult), reads=[tbb, tex], writes=[tops])
                            fw.op("pool", lambda e: e.tensor_tensor(out=kT[:], in0=kdw, in1=ex[:], op=ALU.mult), reads=[tkd, tex], writes=[tops])
                            e3 = ex
                            te3 = tex
                            fw.op("dve", lambda e: e.tensor_tensor(out=e3[:], in0=cs[:], in1=ldw, op=ALU.subtract), reads=[tcs, tld], writes=[te3])
                            fw.op("act", lambda e: e.activation(out=e3[:], in_=e3[:], func=AF.Exp), reads=[te3], writes=[te3])
                            fw.op("dve", lambda e: e.scalar_tensor_tensor(out=aT[:], in0=kkw, scalar=-1.0, in1=e3[:], op0=ALU.mult, op1=ALU.mult),
                                  reads=[tkk, te3], writes=[tops])
                            e3v = e3[:].rearrange("p (c j) -> p c j", j=CHK)
                            fw.op("dve", lambda e: e.tensor_tensor(out=e3v, in0=totb, in1=csv, op=ALU.subtract), reads=[ttot, tcs, tops, te3], writes=[te3])
                            fw.op("act", lambda e: e.activation(out=e3[:], in_=e3[:], func=AF.Exp), reads=[te3], writes=[te3])
                            bh = P2.sb([128, N], BF16)
                            kh = P2.sb([128, N], BF16)
                            tbh = Tok()
                            fw.op("dve", lambda e: e.tensor_tensor(out=bh[:], in0=bbw, in1=e3[:], op=ALU.mult), reads=[tbb, te3], writes=[tbh])
                            fw.op("pool", lambda e: e.tensor_tensor(out=kh[:], in0=kdw, in1=e3[:], op=ALU.mult), reads=[tkd, te3], writes=[tbh])
                            pst = P2.ps([128, 1024], BF16)
                            tpst = Tok()
                            for (src, dstt) in ((bh, btok), (kh, ktok)):
                                for q in range(NCH // 4):
                                    for j in range(4):
                                        ch = q * 4 + j
                                        fw.op("pe", lambda e, j=j, ch=ch, src=src: e.transpose(pst[0:64, j * 128:(j + 1) * 128], src[:, ch * CHK:(ch + 1) * CHK], idb[:]),
                                              reads=[tbh, tc], writes=[tpst])
                                    fw.op("act", lambda e, q=q, dstt=dstt: e.copy(out=dstt[:, q * 4:(q + 1) * 4, :], in_=pst[0:64, 0:512].rearrange("p (j c) -> p j c", j=4)),
                                          reads=[tpst], writes=[ttok])
                            fw.barrier()
                        with ExitStack() as es3:
                            P3 = Pool_(fw, es3)
                            Hs = P3.sb([128, 64], F32)
                            Hb = P3.sb([128, 64], BF16)
                            tH = Tok()
                            fw.op("pool", lambda e: e.memset(Hs[:], 0.0), writes=[tH])
                            fw.op("pool", lambda e: e.memset(Hb[:], 0.0), writes=[tH])
                            psA = Rot([P3.ps([64, 512]) for _ in range(1)])
                            psB = Rot([P3.ps([64, 512]) for _ in range(1)])
                            psI = Rot([P3.ps([64, 512]) for _ in range(1)])
                            psT = Rot([P3.ps([64, 512]) for _ in range(1)])
                            psG = Rot([P3.ps([64, 512]) for _ in range(1)])
                            psH = Rot([P3.ps([128, 512]) for _ in range(1)])
                            psY = Rot([P3.ps([128, 512]) for _ in range(1)])
                            g1b = Rot([P3.sb([64, 2, 64], BF16) for _ in range(2)])
                            nl = Rot([P3.sb([64, 2, 2, 64], F32) for _ in range(2)])
                            nl2 = Rot([P3.sb([64, 2, 2, 64], F32) for _ in range(2)])
                            g2b_ = Rot([P3.sb([64, 4, 64], BF16) for _ in range(2)])
                            Pm = Rot([P3.sb([64, 2, 64], F32) for _ in range(2)])
                            Gs = Rot([P3.sb([64, 128], F32) for _ in range(2)])
                            Ub = Rot([P3.sb([64, 128], BF16) for _ in range(2)])
                            Ys = Rot([P3.sb([64, 128], F32) for _ in range(2)])
                            ms, ml, mi = (0, 1, 2) if di == 0 else (1, 0, 3)
                            order = list(range(NCH)) if di == 0 else list(range(NCH - 1, -1, -1))
                            res = {}

                            def par_gen(i):
                                cc0, cc1 = i * CHK, (i + 1) * CHK
                                pa, tpa = psA.next()
                                pb, tpb = psB.next()
                                for h in range(2):
                                    hp = slice(h * 64, (h + 1) * 64)
                                    for (dst, lt, rt_) in ((pa[:, h * 64:(h + 1) * 64], kT, aT), (pa[:, 128 + h * 64:128 + (h + 1) * 64], bT, aT),
                                                           (pa[:, 256 + h * 64:256 + (h + 1) * 64], aT, bT)):
                                        fw.op("pe", lambda e, dst=dst, lt=lt, rt_=rt_, hp=hp: e.matmul(dst, lhsT=lt[hp, cc0:cc1], rhs=rt_[hp, cc0:cc1], start=True, stop=True),
                                              reads=[tops], writes=[tpa])
                                        yield
                                    for (dst, lt, rt_) in ((pb[:, h * 64:(h + 1) * 64], bT, rT), (pb[:, 128 + h * 64:128 + (h + 1) * 64], kT, rT)):
                                        fw.op("pe", lambda e, dst=dst, lt=lt, rt_=rt_, hp=hp: e.matmul(dst, lhsT=lt[hp, cc0:cc1], rhs=rt_[hp, cc0:cc1], start=True, stop=True),
                                              reads=[tops], writes=[tpb])
                                        yield
                                a1, ta1 = g1b.next()
                                nlt, tnl = nl.next()
                                a45, ta45 = g2b_.next()
                                fw.op("dve", lambda e: e.tensor_tensor(out=a1[:], in0=pa[:, 0:128].rearrange("p (h t) -> p h t", h=2), in1=mrep[:, ms, 0:2, :], op=ALU.mult),
                                      reads=[tpa, tc], writes=[ta1])
                                yield
                                fw.op("dve", lambda e: e.tensor_tensor(out=nlt[:, 0], in0=pa[:, 128:256].rearrange("p (h t) -> p h t", h=2), in1=mrep[:, ms, 0:2, :], op=ALU.mult),
                                      reads=[tpa, tc], writes=[tnl])
                                yield
                                fw.op("dve", lambda e: e.tensor_tensor(out=nlt[:, 1], in0=pa[:, 256:384].rearrange("p (h t) -> p h t", h=2), in1=mrep[:, ml, 0:2, :], op=ALU.mult),
                                      reads=[tpa, tc], writes=[tnl])
                                yield
                                fw.op("dve", lambda e: e.tensor_tensor(out=a45[:], in0=pb[:, 0:256].rearrange("p (h t) -> p h t", h=4), in1=mrep[:, mi, :, :], op=ALU.mult),
                                      reads=[tpb, tc], writes=[ta45])
                                yield
                                pm, tpm = Pm.next()
                                fw.op("dve", lambda e: e.tensor_tensor(out=pm[:], in0=nlt[:, 0], in1=idt[0:64, 0:64].unsqueeze(1).to_broadcast([64, 2, 64]), op=ALU.add),
                                      reads=[tnl, tc], writes=[tpm])
                                yield
                                cur, tcur = nlt, tnl
                                for lev in range(5):
                                    pi_, tpi = psI.next()
                                    for h in range(2):
                                        fw.op("pe", lambda e, pi_=pi_, h=h, cur=cur: e.matmul(pi_[:, h * 64:(h + 1) * 64], lhsT=cur[:, 0, h, :], rhs=cur[:, 1, h, :], start=True, stop=True),
                                              reads=[tcur], writes=[tpi])
                                        yield
                                    nxt, tnxt = (nl2.next() if lev % 2 == 0 else nl.next())
                                    fw.op("act", lambda e, nxt=nxt, pi_=pi_: e.copy(out=nxt[:, 1], in_=pi_[:, 0:128].rearrange("p (h t) -> p h t", h=2)), reads=[tpi], writes=[tnxt])
                                    yield
                                    for h in range(2):
                                        fw.op("pe", lambda e, pi_=pi_, h=h, nxt=nxt: e.matmul(pi_[:, 256 + h * 64:256 + (h + 1) * 64], lhsT=nxt[:, 1, h, :], rhs=pm[:, h, :], start=True, stop=True),
                                              reads=[tnxt, tpm], writes=[tpi])
                                        yield
                                    if lev < 4:
                                        pt_, tpt = psT.next()
                                        for h in range(2):
                                            fw.op("pe", lambda e, pt_=pt_, h=h, nxt=nxt: e.transpose(pt_[:, h * 64:(h + 1) * 64], nxt[:, 1, h, :], idt[0:64, 0:64]),
                                                  reads=[tnxt, tc], writes=[tpt])
                                            yield
                                    fw.op("dve", lambda e, pi_=pi_: e.tensor_tensor(out=pm[:], in0=pm[:], in1=pi_[:, 256:384].rearrange("p (h t) -> p h t", h=2), op=ALU.add),
                                          reads=[tpi, tpm], writes=[tpm])
                                    yield
                                    if lev < 4:
                                        fw.op("act", lambda e, nxt=nxt, pt_=pt_: e.copy(out=nxt[:, 0], in_=pt_[:, 0:128].rearrange("p (h t) -> p h t", h=2)), reads=[tpt], writes=[tnxt])
                                        yield
                                    cur, tcur = nxt, tnxt
                                res[i] = (a1, ta1, a45, ta45, pm, tpm)

                            def chain_gen(i):
                                cc0, cc1 = i * CHK, (i + 1) * CHK
                                gch = i + off // CHK
                                a1, ta1, a45, ta45, pm, tpm = res.pop(i)
                                pg, tpg = psG.next()
                                for h in range(2):
                                    hp = slice(h * 64, (h + 1) * 64)
                                    fw.op("pe", lambda e, h=h, hp=hp: e.matmul(pg[:, h * 64:(h + 1) * 64], lhsT=aT[hp, cc0:cc1], rhs=Hb[hp, :], start=True, stop=False),
                                          reads=[tops, tH], writes=[tpg])
                                    yield
                                    fw.op("pe", lambda e, h=h, hp=hp: e.matmul(pg[:, h * 64:(h + 1) * 64], lhsT=a1[:, h, :], rhs=vtok[:, gch, hp], start=False, stop=True),
                                          reads=[ta1, tvt], writes=[tpg])
                                    yield
                                gs, tgs = Gs.next()
                                fw.op("act", lambda e: e.copy(out=gs[:], in_=pg[:, 0:128]), reads=[tpg], writes=[tgs])
                                yield
                                for h in range(2):
                                    fw.op("pe", lambda e, h=h: e.matmul(pg[:, 128 + h * 64:128 + (h + 1) * 64], lhsT=pm[:, h, :], rhs=gs[:, h * 64:(h + 1) * 64], start=True, stop=True),
                                          reads=[tpm, tgs], writes=[tpg])
                                    yield
                                ub, tub = Ub.next()
                                fw.op("dve", lambda e: e.tensor_copy(out=ub[:], in_=pg[:, 128:256]), reads=[tpg], writes=[tub])
                                yield
                                ph, tph = psH.next()
                                py, tpy = psY.next()
                                for h in range(2):
                                    hp = slice(h * 64, (h + 1) * 64)
                                    fw.op("pe", lambda e, h=h, hp=hp: e.matmul(ph[hp, 0:64], lhsT=btok[:, i, hp], rhs=ub[:, hp], start=True, stop=False),
                                          reads=[ttok, tub], writes=[tph])
                                    yield
                                    fw.op("pe", lambda e, h=h, hp=hp: e.matmul(ph[hp, 0:64], lhsT=ktok[:, i, hp], rhs=vtok[:, gch, hp], start=False, stop=True),
                                          reads=[ttok, tvt], writes=[tph])
                                    yield
                                for h in range(2):
                                    hp = slice(h * 64, (h + 1) * 64)
                                    fw.op("pe", lambda e, h=h, hp=hp: e.matmul(py[0:64, hp], lhsT=rT[hp, cc0:cc1], rhs=Hb[hp, :], start=True, stop=False),
                                          reads=[tops, tH], writes=[tpy])
                                    yield
                                    fw.op("pe", lambda e, h=h, hp=hp: e.matmul(py[0:64, hp], lhsT=a45[:, h, :], rhs=ub[:, hp], start=False, stop=False),
                                          reads=[ta45, tub], writes=[tpy])
                                    yield
                                    fw.op("pe", lambda e, h=h, hp=hp: e.matmul(py[0:64, hp], lhsT=a45[:, 2 + h, :], rhs=vtok[:, gch, hp], start=False, stop=True),
                                          reads=[ta45, tvt], writes=[tpy])
                                    yield
                                fw.op("dve", lambda e: e.scalar_tensor_tensor(out=Hs[:], in0=Hs[:], scalar=pC[:, i:i + 1], in1=ph[:, 0:64], op0=ALU.mult, op1=ALU.add),
                                      reads=[tph, tH, tops, tpy], writes=[tH])
                                yield
                                fw.op("act", lambda e: e.copy(out=Hb[:], in_=Hs[:]), reads=[tH, tpy, tpg], writes=[tH])
                                yield
                                ys, tys = Ys.next()
                                fw.op("act", lambda e: e.copy(out=ys[:], in_=py[0:64, 0:128]), reads=[tpy], writes=[tys])
                                yield
                                fw.op("pe", lambda e: e.transpose(py[:, 256:320], ys[:], idt[0:64, 0:64]), reads=[tys, tc], writes=[tpy])
                                yield
                                y0 = off + cc0
                                if y0 >= T:
                                    y0 -= T
                                fw.op("dve", lambda e: e.tensor_tensor(out=yacc[:, y0:y0 + CHK], in0=yacc[:, y0:y0 + CHK], in1=py[:, 256:320], op=ALU.add),
                                      reads=[tpy, tya], writes=[tya])
                                yield

                            for _ in par_gen(order[0]):
                                pass
                            for idx, i in enumerate(order):
                                gp = par_gen(order[idx + 1]) if idx + 1 < len(order) else iter(())
                                gc = chain_gen(i)
                                done_p = done_c = False
                                while not (done_p and done_c):
                                    for _ in range(2):
                                        if not done_p:
                                            try:
                                                next(gp)
                                            except StopIteration:
                                                done_p = True
                                    if not done_c:
                                        try:
                                            next(gc)
                                        except StopIteration:
                                            done_c = True
                            fw.barrier()
                with ExitStack() as es4:
                    P4 = Pool_(fw, es4)
                    psr = Rot([P4.ps() for _ in range(3)])
                    xc = P4.sb([128, 512], F32)
                    sq = P4.sb([128, 512], F32)
                    rs = P4.sb([128, 512], F32)
                    txc, tsq, trs = Tok(), Tok(), Tok()
                    stg = Rot([P4.sb([128, 512], F32) for _ in range(2)])
                    tout = Tok()
                    for (lo, hi, ic) in SEGS:
                        n = hi - lo
                        ps, tps = psr.next()
                        fw.op("pe", lambda e, ps=ps, lo=lo, hi=hi, n=n: e.matmul(ps[:, :n], lhsT=bd1[:], rhs=yacc[:, lo:hi], start=True, stop=True), reads=[tya, tc], writes=[tps])
                        fw.op("dve", lambda e, ps=ps, lo=lo, hi=hi, n=n: e.scalar_tensor_tensor(out=xc[:, :n], in0=ps[:, :n], scalar=-1.0 / 64, in1=yacc[:, lo:hi], op0=ALU.mult, op1=ALU.add),
                              reads=[tps, tya], writes=[txc])
                        fw.op("act", lambda e, n=n: e.activation(out=sq[:, :n], in_=xc[:, :n], func=AF.Square), reads=[txc], writes=[tsq])
                        ps2, tps2 = psr.next()
                        fw.op("pe", lambda e, ps2=ps2, n=n: e.matmul(ps2[:, :n], lhsT=bd1[:], rhs=sq[:, :n], start=True, stop=True), reads=[tsq, tc], writes=[tps2])
                        rstd_from_ps(fw, rs, trs, ps2, tps2, n, 1.0 / 64, epsl[:, 0:1], tpp)
                        fw.op("dve", lambda e, n=n: e.tensor_tensor(out=xc[:, :n], in0=xc[:, :n], in1=rs[:, :n], op=ALU.mult), reads=[txc, trs], writes=[txc])
                        fw.op("act", lambda e, n=n: e.activation(out=xc[:, :n], in_=xc[:, :n], func=AF.Identity, scale=pp[:, 8, c:c + 1], bias=pp[:, 9, c:c + 1]),
                              reads=[txc, tpp], writes=[txc])
                        fw.op("pool", lambda e, lo=lo, hi=hi, n=n: e.tensor_tensor(out=xc[:, :n], in0=xc[:, :n], in1=bon[:, lo:hi], op=ALU.add), reads=[txc, tbon], writes=[txc])
                        ps3, tps3 = psr.next()
                        fw.op("pe", lambda e, ps3=ps3, lo=lo, hi=hi, n=n: e.matmul(ps3[:, :n], lhsT=wB[64:128, c * 128:(c + 1) * 128], rhs=lorB[64:128, lo:hi], start=True, stop=True),
                              reads=[tlw, tlor], writes=[tps3])
                        sg, tsg = stg.next()
                        fw.op("dve", lambda e, sg=sg, ps3=ps3, n=n: e.tensor_tensor(out=sg[:, :n], in0=ps3[:, :n], in1=xc[:, :n], op=ALU.mult), reads=[tps3, txc], writes=[tsg])
                        fw.dma("sp", RWT[c * 128:(c + 1) * 128, lo:hi], sg[:, :n], reads=[tsg], writes=[tout])
                    fw.barrier()
        fw.barrier()
NCORES = 8
DEPTH = 4
S5_KEYS = ["s5_a_re", "s5_a_im", "s5_log_step", "s5_b_re", "s5_b_im", "s5_c_re", "s5_c_im", "s5_d", "s5_glu_w", "s5_glu_b", "s5_out_g"]
RW_KEYS = ["rw_mu", "rw_w0", "rw_w2", "rw_a0", "rw_a2", "rw_g2", "rw_k_k", "rw_k_a", "rw_r_k", "rw_ln_g", "rw_ln_b"]
LAYER_KEYS = (["norm1_g", "norm2_g", "mod_w", "mod_b", "w_in", "w_out", "att_qn_g", "att_kn_g", "att_out_g",
               "ffn_up", "ffn_conv_w", "ffn_conv_b", "ffn_down"] + S5_KEYS + RW_KEYS)
SHARED_KEYS = ["c_ctx", "final_g", "ident", "POS", "CST", "RWMASK", "RWBD"]
PERCORE_KEYS = ["x_b", "ctx_b", "c_b"]
SCRATCH = {"ZT": [2048, TE], "VT": [T, 128], "MOD": [128, 48, 2], "S5T": [256, T], "ATT_T": [512, T], "RWT": [256, T],
           "XT1": [DM, T], "XA": [DM, T], "XB": [DM, T]}


def build_fused(depth=DEPTH):
    nc = bass.Bass("TRN2", target_bir_lowering=False)
    decl = {}

    def ext(name, shape, dt, kind):
        if name not in decl:
            decl[name] = nc.dram_tensor(name, list(shape), dt, kind=kind).ap()
        return decl[name]

    def make_io(l):
        xin = "XA" if l % 2 == 0 else "XB"
        xout = "XB" if l % 2 == 0 else "XA"

        def io(name, shape, dt, role):
            if name in LAYER_KEYS:
                full = ext(name, [DEPTH] + list(shape), dt, "ExternalInput")
                return full[l]
            if name in SHARED_KEYS or name in PERCORE_KEYS:
                return ext(name, shape, dt, "ExternalInput")
            if name == "OUT":
                return ext(name, shape, dt, "ExternalOutput")
            if name == "XT":
                name = xin
            elif name == "XT2":
                name = xout
            return ext(name, SCRATCH[name], dt, "Internal")
        return io
    with ExitStack() as es:
        fw = FW(nc, es)
        stage_p0(fw, make_io(0))
        for l in range(depth):
            io = make_io(l)
            for st in (stage_p1, stage_p23, stage_p4, stage_p5a, stage_p5b):
                st(fw, io)
        stage_p6(fw, make_io(depth))
        fw.barrier()
    return nc, fw


def tile_up(up):
    lead = up.shape[:-2]
    v = up.reshape(lead + (8, 128, 44, 128))
    nd = len(lead)
    v = np.transpose(v, tuple(range(nd)) + (nd + 2, nd + 1, nd + 0, nd + 3))
    return np.ascontiguousarray(v).reshape(lead + (44, 128, 1024))


_FUSED = {}


def kernel(**inp):
    inp = {k: np.ascontiguousarray(np.asarray(v)) for k, v in inp.items()}
    if "nc" not in _FUSED:
        _FUSED["nc"], _FUSED["fw"] = build_fused()
    nc = _FUSED["nc"]
    pos, cst = host_consts()
    rwm, rwbd = rw_consts()
    shared = {k: inp[k] for k in LAYER_KEYS if k != "rw_r_k"}
    shared["rw_r_k"] = inp["rw_r_k"].reshape(DEPTH, 256)
    shared["ffn_up"] = tile_up(inp["ffn_up"])
    shared.update(c_ctx=inp["c_ctx"], final_g=inp["final_g"], ident=np.eye(128, dtype=np.float32),
                  POS=pos, CST=cst, RWMASK=rwm, RWBD=rwbd)
    in_maps = [dict(shared, x_b=inp["x"][b], ctx_b=inp["ctx"][b], c_b=inp["c"][b]) for b in range(NCORES)]
    res = run_bass_kernel_spmd(nc, in_maps, core_ids=list(range(NCORES)))
    return np.stack([res.results[b]["OUT"] for b in range(NCORES)], 0).astype(np.float32)
```

```python
import math
import numpy as np
from contextlib import ExitStack
import concourse.bass as bass
import concourse.mybir as mybir
from concourse.bass_utils import run_bass_kernel_spmd

F32 = mybir.dt.float32
F32R = mybir.dt.float32r
BF16 = mybir.dt.bfloat16
I32 = mybir.dt.int32
ALU = mybir.AluOpType
AF = mybir.ActivationFunctionType
AX = mybir.AxisListType

T = 2304
TE = 2560
NCTX = 256
NLAT = 2048
DM = 1024
SEGS = [(0, 256, 1), (256, 768, 0), (768, 1280, 0), (1280, 1792, 0), (1792, 2304, 0)]
RMS_EPS = 1e-6


class Tok:
    __slots__ = ("w", "r")

    def __init__(self):
        self.w = None
        self.r = {}


class FW:
    ENG = ("pe", "dve", "act", "pool", "sp")
    NDMA = 8

    def __init__(self, nc, es):
        self.nc = nc
        self.es = es
        self.eng = {"pe": nc.tensor, "dve": nc.vector, "act": nc.scalar,
                    "pool": nc.gpsimd, "sp": nc.sync}
        self.sem = {}
        self.cnt = {}
        for e in self.ENG:
            self.sem[e] = es.enter_context(nc.semaphore("s_" + e))
            self.cnt[e] = 0
        self.dq = {}
        for q in ("sp", "pool", "act"):
            ring = []
            for i in range(self.NDMA):
                k = "d_%s_%d" % (q, i)
                self.sem[k] = es.enter_context(nc.semaphore(k))
                self.cnt[k] = 0
                ring.append(k)
            self.dq[q] = [ring, 0]
        self.seen = {e: {} for e in self.ENG}
        self.attach = True
        self.ninst = 0
        self.uid = 0

    def name(self, p):
        self.uid += 1
        return "%s_%d" % (p, self.uid)

    def _deps(self, reads, writes):
        deps = {}
        for t in reads:
            if t.w is not None and deps.get(t.w[0], 0) < t.w[1]:
                deps[t.w[0]] = t.w[1]
        for t in writes:
            if t.w is not None and deps.get(t.w[0], 0) < t.w[1]:
                deps[t.w[0]] = t.w[1]
            for k, v in t.r.items():
                if deps.get(k, 0) < v:
                    deps[k] = v
        return deps

    def _wait(self, e, deps):
        seen = self.seen[e]
        for k, v in deps.items():
            if seen.get(k, 0) < v:
                self.eng[e].wait_ge(self.sem[k], v)
                seen[k] = v

    def op(self, e, fn, reads=(), writes=()):
        deps = self._deps(reads, writes)
        seen = self.seen[e]
        need = [(k, v) for k, v in deps.items() if seen.get(k, 0) < v]
        att = None
        if need and self.attach:
            att = need.pop()
        for k, v in need:
            self.eng[e].wait_ge(self.sem[k], v)
            seen[k] = v
        inst = fn(self.eng[e])
        if att is not None:
            inst._wait_ge(self.sem[att[0]], att[1])
            seen[att[0]] = att[1]
        self.cnt[e] += 1
        inst.then_inc(self.sem[e], 1)
        v = self.cnt[e]
        for t in reads:
            t.r[e] = v
        for t in writes:
            t.w = (e, v)
            t.r = {}
        self.ninst += 1
        return inst

    def dma(self, q, out, in_, reads=(), writes=(), **kw):
        ring, idx = self.dq[q]
        k = ring[idx % len(ring)]
        self.dq[q][1] = idx + 1
        deps = self._deps(reads, writes)
        if self.cnt[k] > 0:
            deps[k] = max(deps.get(k, 0), self.cnt[k])
        self._wait(q, deps)
        inst = self.eng[q].dma_start(out=out, in_=in_, **kw)
        self.cnt[k] += 16
        inst.then_inc(self.sem[k], 16)
        v = self.cnt[k]
        for t in reads:
            t.r[k] = v
        for t in writes:
            t.w = (k, v)
            t.r = {}
        self.ninst += 1
        return inst

    def barrier(self, engines=None):
        allv = {k: v for k, v in self.cnt.items() if v > 0}
        for e in (engines or self.ENG):
            self._wait(e, allv)


class Pool_:
    def __init__(self, fw, es):
        self.fw = fw
        self.es = es
        self.nc = fw.nc

    def sb(self, shape, dt, name="t"):
        return self.es.enter_context(self.nc.sbuf_tensor(self.fw.name(name), list(shape), dt))

    def ps(self, shape=(128, 512), dt=F32, name="ps"):
        return self.es.enter_context(self.nc.psum_tensor(self.fw.name(name), list(shape), dt))


class Rot:
    def __init__(self, bufs):
        self.bufs = bufs
        self.toks = [Tok() for _ in bufs]
        self.i = 0

    def next(self):
        j = self.i % len(self.bufs)
        self.i += 1
        return self.bufs[j], self.toks[j]


def load_w_bf16(fw, P, W, rows, cols, tok, name="w", q="pool", chunk=2048, stage=None):
    kt = rows // 128
    wb = P.sb([128, kt, cols], BF16, name)
    chunk = min(chunk, cols)
    st = stage or Rot([P.sb([128, chunk], F32, "wstg") for _ in range(3)])
    for k in range(kt):
        for c0 in range(0, cols, chunk):
            n = min(chunk, cols - c0)
            sg, tsg = st.next()
            fw.dma("sp", sg[:, :n], W[k * 128:(k + 1) * 128, c0:c0 + n], writes=[tsg])
            fw.op("pool", lambda e, sg=sg, k=k, c0=c0, n=n: e.tensor_copy(out=wb[:, k, c0:c0 + n], in_=sg[:, :n]), reads=[tsg], writes=[tok])
    return wb


def stage_p0(fw, io):
    nc = fw.nc
    xb = io("x_b", [NLAT, DM], F32, "in")
    cb = io("ctx_b", [NCTX, DM], F32, "in")
    ident = io("ident", [128, 128], F32, "in")
    XT = io("XT", [DM, T], F32, "out")
    with ExitStack() as es:
        P = Pool_(fw, es)
        idt = P.sb([128, 128], F32)
        tid = Tok()
        fw.dma("sp", idt[:], ident, writes=[tid])
        xt = P.sb([128, 8, T], F32)
        txt = Tok()
        xin = Rot([P.sb([128, DM], F32) for _ in range(3)])
        pss = Rot([P.ps() for _ in range(4)])
        for tt in range(18):
            src = cb[tt * 128:(tt + 1) * 128, :] if tt < 2 else xb[(tt - 2) * 128:(tt - 1) * 128, :]
            xi, txi = xin.next()
            fw.dma("sp", xi[:], src, writes=[txi])
            for half in range(2):
                ps, tps = pss.next()
                for k in range(4):
                    kk = half * 4 + k
                    fw.op("pe", lambda e, ps=ps, k=k, kk=kk, xi=xi: e.transpose(
                        ps[:, k * 128:(k + 1) * 128], xi[:, kk * 128:(kk + 1) * 128], idt[:]),
                        reads=[txi, tid], writes=[tps])
                eng = "dve" if half == 0 else "act"
                outap = xt[:, half * 4:half * 4 + 4, tt * 128:(tt + 1) * 128]
                inap = ps[:].rearrange("p (k t) -> p k t", k=4)
                if eng == "dve":
                    fw.op("dve", lambda e, o=outap, i=inap: e.tensor_copy(out=o, in_=i), reads=[tps], writes=[txt])
                else:
                    fw.op("act", lambda e, o=outap, i=inap: e.copy(out=o, in_=i), reads=[tps], writes=[txt])
        tout = Tok()
        for k in range(8):
            fw.dma("sp", XT[k * 128:(k + 1) * 128, :], xt[:, k, :], reads=[txt], writes=[Tok()])
        fw.barrier()


def make_AB(fw, P, MODs, tmod, g_ap, sh_base, sc_base):
    g = P.sb([128, 8], F32)
    tg = Tok()
    fw.dma("sp", g[:], g_ap.rearrange("(k p) -> p k", p=128), writes=[tg], allow_slow_non_contiguous=True)
    AB = P.sb([128, 2, 2, 8], F32)
    tab = Tok()
    for ic in range(2):
        fw.op("dve", lambda e, ic=ic: e.tensor_scalar(out=AB[:, ic, 0, :], in0=MODs[:, sc_base:sc_base + 8, ic],
                                                      scalar1=1.0, scalar2=None, op0=ALU.add),
              reads=[tmod], writes=[tab])
        fw.op("dve", lambda e, ic=ic: e.tensor_tensor(out=AB[:, ic, 0, :], in0=AB[:, ic, 0, :], in1=g[:], op=ALU.mult),
              reads=[tg, tab], writes=[tab])
        fw.op("dve", lambda e, ic=ic: e.tensor_copy(out=AB[:, ic, 1, :], in_=MODs[:, sh_base:sh_base + 8, ic]),
              reads=[tmod], writes=[tab])
    return AB, tab


def norm_mod_seg(fw, P, st, xs, txs, n, ic, AB, tab, outs, touts):
    sq, ones, tones, psr, rs, tmpr = st["sq"], st["ones"], st["tones"], st["psr"], st["rs"], st["tmpr"]
    tsq, trs = st["tsq"], st["trs"]
    fw.op("act", lambda e: e.activation(out=sq[:, :, :n], in_=xs[:, :, :n], func=AF.Square), reads=[txs], writes=[tsq])
    ps, tps = psr.next()
    for k in range(8):
        fw.op("pe", lambda e, k=k: e.matmul(ps[:, :n], lhsT=ones[:], rhs=sq[:, k, :n], start=(k == 0), stop=(k == 7)),
              reads=[tsq, tones], writes=[tps])
    fw.op("act", lambda e: e.activation(out=rs[:, :n], in_=ps[:, :n], func=AF.Ln, scale=1.0 / DM, bias=st["eps"][:, 0:1]),
          reads=[tps, st["teps"]], writes=[trs])
    fw.op("act", lambda e: e.activation(out=rs[:, :n], in_=rs[:, :n], func=AF.Exp, scale=-0.5), reads=[trs], writes=[trs])
    for k in range(8):
        tmp, ttmp = tmpr.next()
        fw.op("dve", lambda e, k=k, tmp=tmp: e.tensor_tensor(out=tmp[:, :n], in0=xs[:, k, :n], in1=rs[:, :n], op=ALU.mult),
              reads=[txs, trs], writes=[ttmp])
        for o in outs(k):
            fw.op("act", lambda e, k=k, tmp=tmp, o=o: e.activation(out=o, in_=tmp[:, :n], func=AF.Identity,
                                                                   scale=AB[:, ic, 0, k:k + 1], bias=AB[:, ic, 1, k:k + 1]),
                  reads=[ttmp, tab], writes=touts)


def norm_state(fw, P):
    st = {}
    st["sq"] = P.sb([128, 8, 512], BF16)
    st["tsq"] = Tok()
    st["ones"] = P.sb([128, 128], BF16)
    st["tones"] = Tok()
    fw.op("pool", lambda e: e.memset(st["ones"][:], 1.0), writes=[st["tones"]])
    st["eps"] = P.sb([128, 1], F32)
    st["teps"] = Tok()
    fw.op("pool", lambda e: e.memset(st["eps"][:], RMS_EPS), writes=[st["teps"]])
    st["psr"] = Rot([P.ps() for _ in range(2)])
    st["rs"] = P.sb([128, 512], F32)
    st["trs"] = Tok()
    st["tmpr"] = Rot([P.sb([128, 512], F32) for _ in range(2)])
    return st


def stage_p1(fw, io):
    XT = io("XT", [DM, T], F32, "in")
    c_b = io("c_b", [DM], F32, "in")
    c_ctx = io("c_ctx", [DM], F32, "in")
    mod_w = io("mod_w", [DM, 6 * DM], F32, "in")
    mod_b = io("mod_b", [6 * DM], F32, "in")
    n1g = io("norm1_g", [DM], F32, "in")
    w_in = io("w_in", [DM, 1984], F32, "in")
    MOD = io("MOD", [128, 48, 2], F32, "out")
    ZT = io("ZT", [2048, TE], F32, "out")
    VT = io("VT", [T, 128], F32, "out")
    XTv = XT.rearrange("(k p) t -> p k t", p=128)
    with ExitStack() as es:
        P = Pool_(fw, es)
        wstage = Rot([P.sb([128, 2048], F32, "wstg") for _ in range(3)])
        tmw = Tok()
        cc = P.sb([128, 8, 2], F32)
        tcc = Tok()
        fw.dma("sp", cc[:, :, 0], c_b.rearrange("(k p) -> p k", p=128), writes=[tcc], allow_slow_non_contiguous=True)
        fw.dma("sp", cc[:, :, 1], c_ctx.rearrange("(k p) -> p k", p=128), writes=[tcc], allow_slow_non_contiguous=True)
        scb = P.sb([128, 8, 2], BF16)
        tscb = Tok()
        fw.op("act", lambda e: e.activation(out=scb[:], in_=cc[:], func=AF.Silu), reads=[tcc], writes=[tscb])
        mb = P.sb([128, 48], F32)
        tmb = Tok()
        fw.dma("sp", mb[:], mod_b.rearrange("(j p) -> p j", p=128), writes=[tmb], allow_slow_non_contiguous=True)
        MODs = P.sb([128, 48, 2], F32)
        tmod = Tok()
        with ExitStack() as es2:
            P2 = Pool_(fw, es2)
            mwb = load_w_bf16(fw, P2, mod_w, DM, 6 * DM, tmw, "modw", stage=wstage)
            psm = P2.ps([128, 512])
            tpsm = Tok()
            for j in range(48):
                for k in range(8):
                    fw.op("pe", lambda e, j=j, k=k: e.matmul(psm[:, 2 * j:2 * j + 2], lhsT=mwb[:, k, j * 128:(j + 1) * 128],
                                                             rhs=scb[:, k, :], start=(k == 0), stop=(k == 7)),
                          reads=[tmw, tscb], writes=[tpsm])
            for ic in range(2):
                fw.op("dve", lambda e, ic=ic: e.tensor_tensor(
                    out=MODs[:, :, ic], in0=psm[:, 0:96].rearrange("p (j c) -> p j c", c=2)[:, :, ic], in1=mb[:], op=ALU.add),
                    reads=[tpsm, tmb], writes=[tmod])
            fw.barrier()
        tmo = Tok()
        fw.dma("sp", MOD, MODs[:], reads=[tmod], writes=[tmo])
        AB, tab = make_AB(fw, P, MODs, tmod, n1g, 0, 8)
        tw = Tok()
        wb = load_w_bf16(fw, P, w_in, DM, 1984, tw, "win", stage=wstage)
        hT = P.sb([128, 8, TE], BF16)
        thT = Tok()
        st = norm_state(fw, P)
        xr = Rot([P.sb([128, 8, 512], F32) for _ in range(2)])
        for (lo, hi, ic) in SEGS:
            n = hi - lo
            xs, txs = xr.next()
            fw.dma("sp", xs[:, :, :n], XTv[:, :, lo:hi], writes=[txs])

            def outs(k, lo=lo, hi=hi, ic=ic):
                o = [hT[:, k, lo:hi]]
                if ic:
                    o.append(hT[:, k, T + lo:T + hi])
                return o
            norm_mod_seg(fw, P, st, xs, txs, n, ic, AB, tab, outs, [thT])
        psr = Rot([P.ps() for _ in range(4)])
        stg = Rot([P.sb([128, 512], F32) for _ in range(4)])
        tz = Tok()
        cnt = 0
        for nt in range(16):
            if nt == 7:
                continue
            M = 64 if nt == 15 else 128
            for cc_ in range(5):
                c0 = cc_ * 512
                ps, tps = psr.next()
                for k in range(8):
                    fw.op("pe", lambda e, ps=ps, k=k, nt=nt, M=M, c0=c0: e.matmul(
                        ps[0:M, :], lhsT=wb[:, k, nt * 128:nt * 128 + M], rhs=hT[:, k, c0:c0 + 512],
                        start=(k == 0), stop=(k == 7)), reads=[tw, thT], writes=[tps])
                sg, tsg = stg.next()
                if cnt % 2 == 0:
                    fw.op("dve", lambda e, sg=sg, ps=ps, M=M: e.tensor_copy(out=sg[0:M, :], in_=ps[0:M, :]), reads=[tps], writes=[tsg])
                else:
                    fw.op("act", lambda e, sg=sg, ps=ps, M=M: e.copy(out=sg[0:M, :], in_=ps[0:M, :]), reads=[tps], writes=[tsg])
                cnt += 1
                fw.dma("sp", ZT[nt * 128:nt * 128 + M, c0:c0 + 512], sg[0:M, :], reads=[tsg], writes=[Tok()])
        for tt in range(18):
            ps, tps = psr.next()
            for k in range(8):
                fw.op("pe", lambda e, ps=ps, k=k, tt=tt: e.matmul(
                    ps[:, 0:128], lhsT=hT[:, k, tt * 128:(tt + 1) * 128], rhs=wb[:, k, 896:1024],
                    start=(k == 0), stop=(k == 7)), reads=[tw, thT], writes=[tps])
            sg, tsg = stg.next()
            fw.op("dve", lambda e, sg=sg, ps=ps: e.tensor_copy(out=sg[:, 0:128], in_=ps[:, 0:128]), reads=[tps], writes=[tsg])
            fw.dma("sp", VT[tt * 128:(tt + 1) * 128, :], sg[:, 0:128], reads=[tsg], writes=[Tok()])
        fw.barrier()


def build_program(stage_fns):
    nc = bass.Bass("TRN2", target_bir_lowering=False)
    decl = {}

    def io(name, shape, dt, role):
        if name in decl:
            return decl[name][0]
        kind = "ExternalInput" if role == "in" else "ExternalOutput"
        ap = nc.dram_tensor(name, list(shape), dt, kind=kind).ap()
        decl[name] = (ap, role, shape)
        return ap
    with ExitStack() as es:
        fw = FW(nc, es)
        for fn in stage_fns:
            fn(fw, io)
        fw.barrier()
    return nc, decl, fw


_PROG_CACHE = {}


def run_stage(key, stage_fns, in_maps, ncores):
    if key not in _PROG_CACHE:
        _PROG_CACHE[key] = build_program(stage_fns)
    nc, decl, fw = _PROG_CACHE[key]
    res = run_bass_kernel_spmd(nc, in_maps, core_ids=list(range(ncores)))
    return res.results


def rstd_from_ps(fw, rs, trs, ps, tps, n, scale, epsap, teps, rows=128):
    fw.op("act", lambda e: e.activation(out=rs[0:rows, :n], in_=ps[0:rows, :n], func=AF.Ln, scale=scale, bias=epsap),
          reads=[tps, teps], writes=[trs])
    fw.op("act", lambda e: e.activation(out=rs[0:rows, :n], in_=rs[0:rows, :n], func=AF.Exp, scale=-0.5), reads=[trs], writes=[trs])


def stage_p3(fw, io):
    ZT = io("ZT", [2048, TE], F32, "in")
    VT = io("VT", [T, 128], F32, "in")
    qn_g = io("att_qn_g", [64], F32, "in")
    kn_g = io("att_kn_g", [64], F32, "in")
    og = io("att_out_g", [512], F32, "in")
    POS = io("POS", [128, NLAT], F32, "in")
    CST = io("CST", [128, 260], F32, "in")
    ATT = io("ATT_T", [512, T], F32, "out")
    with ExitStack() as es:
        P = Pool_(fw, es)
        cst = P.sb([128, 260], F32)
        tc = Tok()
        fw.dma("sp", cst[:], CST, writes=[tc])
        permb = P.sb([128, 128], BF16)
        bdb = P.sb([128, 128], BF16)
        fw.op("dve", lambda e: e.tensor_copy(out=permb[:], in_=cst[:, 1:129]), reads=[tc], writes=[tc])
        fw.op("dve", lambda e: e.tensor_copy(out=bdb[:], in_=cst[:, 129:257]), reads=[tc], writes=[tc])
        eps = P.sb([128, 2], F32)
        teps = Tok()
        fw.op("pool", lambda e: e.memset(eps[:, 0:1], RMS_EPS), writes=[teps])
        fw.op("pool", lambda e: e.memset(eps[:, 1:2], 0.0), writes=[teps])
        gq = P.sb([128, 2], F32)
        tg = Tok()
        for h in range(2):
            fw.dma("sp", gq[h * 64:(h + 1) * 64, 0:1], qn_g.rearrange("(p o) -> p o", o=1), writes=[tg])
            fw.dma("sp", gq[h * 64:(h + 1) * 64, 1:2], kn_g.rearrange("(p o) -> p o", o=1), writes=[tg])
        cos = P.sb([128, NLAT], F32)
        sin = P.sb([128, NLAT], F32)
        ttab = Tok()
        with ExitStack() as es2:
            P2 = Pool_(fw, es2)
            ang = P2.sb([128, NLAT], F32)
            tmpf = P2.sb([128, NLAT], F32)
            tmpi = P2.sb([128, NLAT], I32)
            ta = Tok()
            fw.dma("sp", ang[:], POS, writes=[ta])
            fw.op("dve", lambda e: e.tensor_scalar(out=ang[:], in0=ang[:], scalar1=cst[:, 0:1], scalar2=None, op0=ALU.mult),
                  reads=[ta, tc], writes=[ta])
            for (tab, off) in ((sin, 0.0), (cos, math.pi / 2)):
                fw.op("dve", lambda e, off=off: e.tensor_scalar(out=tmpi[:], in0=ang[:], scalar1=off, scalar2=1.0 / (2 * math.pi),
                                                                op0=ALU.add, op1=ALU.mult), reads=[ta], writes=[ta])
                fw.op("dve", lambda e: e.tensor_copy(out=tmpf[:], in_=tmpi[:]), reads=[ta], writes=[ta])
                fw.op("dve", lambda e: e.scalar_tensor_tensor(out=tmpf[:], in0=tmpf[:], scalar=-2 * math.pi, in1=ang[:],
                                                              op0=ALU.mult, op1=ALU.add), reads=[ta], writes=[ta])
                fw.op("dve", lambda e, off=off: e.tensor_scalar(out=tmpf[:], in0=tmpf[:], scalar1=off, scalar2=math.pi,
                                                                op0=ALU.add, op1=ALU.min), reads=[ta], writes=[ta])
                fw.op("dve", lambda e: e.tensor_scalar(out=tmpf[:], in0=tmpf[:], scalar1=-math.pi, scalar2=None, op0=ALU.max),
                      reads=[ta], writes=[ta])
                fw.op("act", lambda e, tab=tab: e.activation(out=tab[:], in_=tmpf[:], func=AF.Sin), reads=[ta], writes=[ttab])
            fw.barrier()
        qb = P.sb([128, 4, T], BF16)
        kd = P.sb([128, 2, T], BF16)
        tq = Tok()
        with ExitStack() as es2:
            P2 = Pool_(fw, es2)
            raw = Rot([P2.sb([128, 512], F32) for _ in range(2)])
            sqr = Rot([P2.sb([128, 512], BF16) for _ in range(2)])
            psr = Rot([P2.ps() for _ in range(2)])
            psr2 = Rot([P2.ps() for _ in range(2)])
            rsr = Rot([P2.sb([128, 512], F32) for _ in range(2)])
            nbr = Rot([P2.sb([128, 512], BF16) for _ in range(2)])
            t1r = Rot([P2.sb([128, 512], F32) for _ in range(2)])
            t2r = Rot([P2.sb([128, 512], F32) for _ in range(2)])
            items = [("q", j) for j in range(4)] + [("k", g) for g in range(2)]
            for (kind, j) in items:
                for (lo, hi, ic) in SEGS:
                    n = hi - lo
                    rw, trw = raw.next()
                    if kind == "q":
                        fw.dma("sp", rw[:, :n], ZT[256 + j * 128:256 + (j + 1) * 128, lo:hi], writes=[trw])
                        gcol = 0
                        dst = qb[:, j, lo:hi]
                    else:
                        for h in range(2):
                            fw.dma("sp", rw[h * 64:(h + 1) * 64, :n], ZT[768 + j * 64:768 + (j + 1) * 64, lo:hi], writes=[trw])
                        gcol = 1
                        dst = kd[:, j, lo:hi]
                    sq, tsq = sqr.next()
                    fw.op("act", lambda e, sq=sq, rw=rw, n=n: e.activation(out=sq[:, :n], in_=rw[:, :n], func=AF.Square), reads=[trw], writes=[tsq])
                    ps, tps = psr.next()
                    fw.op("pe", lambda e, ps=ps, sq=sq, n=n: e.matmul(ps[:, :n], lhsT=bdb[:], rhs=sq[:, :n], start=True, stop=True),
                          reads=[tsq, tc], writes=[tps])
                    rs, trs = rsr.next()
                    rstd_from_ps(fw, rs, trs, ps, tps, n, 1.0, eps[:, 0:1], teps)
                    t1, tt1 = t1r.next()
                    fw.op("dve", lambda e, t1=t1, rw=rw, rs=rs, n=n, gcol=gcol: e.scalar_tensor_tensor(
                        out=t1[:, :n], in0=rw[:, :n], scalar=gq[:, gcol:gcol + 1], in1=rs[:, :n], op0=ALU.mult, op1=ALU.mult),
                        reads=[trw, trs, tg], writes=[tt1])
                    if ic:
                        fw.op("act", lambda e, dst=dst, t1=t1, n=n: e.copy(out=dst, in_=t1[:, :n]), reads=[tt1], writes=[tq])
                        continue
                    nb, tnb = nbr.next()
                    fw.op("act", lambda e, nb=nb, t1=t1, n=n: e.copy(out=nb[:, :n], in_=t1[:, :n]), reads=[tt1], writes=[tnb])
                    ps2, tps2 = psr2.next()
                    fw.op("pe", lambda e, ps2=ps2, nb=nb, n=n: e.matmul(ps2[:, :n], lhsT=permb[:], rhs=nb[:, :n], start=True, stop=True),
                          reads=[tnb, tc], writes=[tps2])
                    p0 = lo - NCTX
                    t2, tt2 = t2r.next()
                    fw.op("dve", lambda e, t2=t2, ps2=ps2, n=n, p0=p0: e.tensor_tensor(out=t2[:, :n], in0=ps2[:, :n], in1=sin[:, p0:p0 + n], op=ALU.mult),
                          reads=[tps2, ttab], writes=[tt2])
                    fw.op("pool", lambda e, t1=t1, nb=nb, n=n, p0=p0: e.tensor_tensor(out=t1[:, :n], in0=nb[:, :n], in1=cos[:, p0:p0 + n], op=ALU.mult),
                          reads=[tnb, ttab, tt1], writes=[tt1])
                    fw.op("pool", lambda e, dst=dst, t1=t1, t2=t2, n=n: e.tensor_tensor(out=dst, in0=t1[:, :n], in1=t2[:, :n], op=ALU.add),
                          reads=[tt1, tt2], writes=[tq])
            fw.barrier()
        va = P.sb([128, 18, 2, 128], BF16)
        tva = Tok()
        fw.op("pool", lambda e: e.memset(va[:], 1.0), writes=[tva])
        for g in range(2):
            fw.dma("pool", va[:, :, g, 0:64], VT.rearrange("(t p) c -> p t c", p=128)[:, :, g * 64:(g + 1) * 64], writes=[tva])
        att = P.sb([128, 4, T], F32)
        tatt = Tok()
        pss = Rot([P.ps() for _ in range(3)])
        pso = Rot([P.ps() for _ in range(2)])
        ptr = Rot([P.sb([128, 512], BF16) for _ in range(3)])
        rcr = Rot([P.sb([64, 512], F32) for _ in range(2)])
        jobs = [(0, 256, 0, 2)] + [(256 + 512 * i, 768 + 512 * i, 0, 18) for i in range(4)]
        for h in range(8):
            g = h // 4
            jt, r0 = h // 2, (h % 2) * 64
            for (qlo, qhi, k0, k1) in jobs:
                n = qhi - qlo
                po, tpo = pso.next()
                for kt in range(k0, k1):
                    ps, tps = pss.next()
                    fw.op("pe", lambda e, ps=ps, kt=kt, g=g, jt=jt, r0=r0, qlo=qlo, qhi=qhi, n=n: e.matmul(
                        ps[:, :n], lhsT=kd[r0:r0 + 64, g, kt * 128:(kt + 1) * 128], rhs=qb[r0:r0 + 64, jt, qlo:qhi],
                        start=True, stop=True), reads=[tq], writes=[tps])
                    pt, tpt = ptr.next()
                    fw.op("act", lambda e, pt=pt, ps=ps, n=n: e.activation(out=pt[:, :n], in_=ps[:, :n], func=AF.Exp, scale=0.125),
                          reads=[tps], writes=[tpt])
                    fw.op("pe", lambda e, po=po, pt=pt, kt=kt, g=g, n=n, k0=k0, k1=k1: e.matmul(
                        po[:, :n], lhsT=va[:, kt, g, :], rhs=pt[:, :n], start=(kt == k0), stop=(kt == k1 - 1)),
                        reads=[tpt, tva], writes=[tpo])
                rc, trc = rcr.next()
                fw.op("dve", lambda e, rc=rc, po=po, n=n: e.reciprocal(out=rc[:, :n], in_=po[64:128, :n]), reads=[tpo], writes=[trc])
                fw.op("dve", lambda e, rc=rc, po=po, n=n, jt=jt, r0=r0, qlo=qlo, qhi=qhi: e.tensor_tensor(
                    out=att[r0:r0 + 64, jt, qlo:qhi], in0=po[0:64, :n], in1=rc[:, :n], op=ALU.mult),
                    reads=[tpo, trc], writes=[tatt])
        ogs = P.sb([128, 4], F32)
        tog = Tok()
        fw.dma("sp", ogs[:], og.rearrange("(k p) -> p k", p=128), writes=[tog], allow_slow_non_contiguous=True)
        ones = P.sb([128, 128], BF16)
        fw.op("pool", lambda e: e.memset(ones[:], 1.0), writes=[tog])
        sq4 = P.sb([128, 4, 512], BF16)
        tsq4 = Tok()
        rs = P.sb([128, 512], F32)
        trs = Tok()
        stg = Rot([P.sb([128, 512], F32) for _ in range(3)])
        tout = Tok()
        for (lo, hi, ic) in SEGS:
            n = hi - lo
            fw.op("act", lambda e, lo=lo, hi=hi, n=n: e.activation(out=sq4[:, :, :n], in_=att[:, :, lo:hi], func=AF.Square), reads=[tatt], writes=[tsq4])
            ps, tps = pss.next()
            for k in range(4):
                fw.op("pe", lambda e, ps=ps, k=k, n=n: e.matmul(ps[:, :n], lhsT=ones[:], rhs=sq4[:, k, :n], start=(k == 0), stop=(k == 3)),
                      reads=[tsq4, tog], writes=[tps])
            rstd_from_ps(fw, rs, trs, ps, tps, n, 1.0 / 512, eps[:, 0:1], teps)
            for k in range(4):
                sg, tsg = stg.next()
                fw.op("dve", lambda e, sg=sg, k=k, lo=lo, hi=hi, n=n: e.scalar_tensor_tensor(
                    out=sg[:, :n], in0=att[:, k, lo:hi], scalar=ogs[:, k:k + 1], in1=rs[:, :n], op0=ALU.mult, op1=ALU.mult),
                    reads=[tatt, trs, tog], writes=[tsg])
                fw.dma("sp", ATT[k * 128:(k + 1) * 128, lo:hi], sg[:, :n], reads=[tsg], writes=[Tok()])
        fw.barrier()


def host_consts():
    pos = np.zeros((128, NLAT), np.float32)
    inv = np.zeros((128,), np.float32)
    tok = np.arange(NLAT)
    for p in range(128):
        d = p % 64
        pos[p] = (tok // 64) if d < 32 else (tok % 64)
        inv[p] = 10000.0 ** (-(d % 16) / 16.0)
    cst = np.zeros((128, 260), np.float32)
    cst[:, 0] = inv
    perm = np.zeros((128, 128), np.float32)
    for m in range(128):
        d = m % 32
        if d < 16:
            perm[m + 16, m] = -1.0
        else:
            perm[m - 16, m] = 1.0
    cst[:, 1:129] = perm
    bd = np.zeros((128, 128), np.float32)
    bd[:64, :64] = 1.0 / 64
    bd[64:, 64:] = 1.0 / 64
    cst[:, 129:257] = bd
    return pos, cst


def stage_p5a(fw, io):
    XT = io("XT", [DM, T], F32, "in")
    S5T = io("S5T", [256, T], F32, "in")
    ATT = io("ATT_T", [512, T], F32, "in")
    RWT = io("RWT", [256, T], F32, "in")
    MOD = io("MOD", [128, 48, 2], F32, "in")
    w_out = io("w_out", [DM, DM], F32, "in")
    XT1 = io("XT1", [DM, T], F32, "out")
    XTv = XT.rearrange("(k p) t -> p k t", p=128)
    with ExitStack() as es:
        P = Pool_(fw, es)
        MODs = P.sb([128, 48, 2], F32)
        tmod = Tok()
        fw.dma("sp", MODs[:], MOD, writes=[tmod])
        tw = Tok()
        wb = load_w_bf16(fw, P, w_out, DM, DM, tw, "wout")
        cat = P.sb([128, 8, T], BF16)
        tcat = Tok()
        cstg = Rot([P.sb([128, T], F32, "cstg") for _ in range(2)])
        for k in range(8):
            src = S5T[k * 128:(k + 1) * 128, :] if k < 2 else (ATT[(k - 2) * 128:(k - 1) * 128, :] if k < 6 else RWT[(k - 6) * 128:(k - 5) * 128, :])
            sg, tsg = cstg.next()
            fw.dma("sp", sg[:], src, writes=[tsg])
            fw.op("pool", lambda e, sg=sg, k=k: e.tensor_copy(out=cat[:, k, :], in_=sg[:]), reads=[tsg], writes=[tcat])
        xr = Rot([P.sb([128, 8, 512], F32) for _ in range(2)])
        x1r = Rot([P.sb([128, 8, 512], F32) for _ in range(2)])
        psr = Rot([P.ps() for _ in range(4)])
        to1, to2 = Tok(), Tok()
        for (lo, hi, ic) in SEGS:
            n = hi - lo
            xs, txs = xr.next()
            fw.dma("sp", xs[:, :, :n], XTv[:, :, lo:hi], writes=[txs])
            x1, tx1 = x1r.next()
            for d in range(8):
                ps, tps = psr.next()
                for k in range(8):
                    fw.op("pe", lambda e, ps=ps, k=k, d=d, lo=lo, hi=hi, n=n: e.matmul(
                        ps[:, :n], lhsT=wb[:, k, d * 128:(d + 1) * 128], rhs=cat[:, k, lo:hi], start=(k == 0), stop=(k == 7)),
                        reads=[tw, tcat], writes=[tps])
                fw.op("dve", lambda e, ps=ps, d=d, n=n, ic=ic, x1=x1, xs=xs: e.scalar_tensor_tensor(
                    out=x1[:, d, :n], in0=ps[:, :n], scalar=MODs[:, 16 + d, ic:ic + 1], in1=xs[:, d, :n], op0=ALU.mult, op1=ALU.add),
                    reads=[tps, txs, tmod], writes=[tx1])
            for k in range(8):
                fw.dma("sp", XT1[k * 128:(k + 1) * 128, lo:hi], x1[:, k, :n], reads=[tx1], writes=[Tok()])
        fw.barrier()


def stage_p5b(fw, io):
    XT1 = io("XT1", [DM, T], F32, "in")
    n2g = io("norm2_g", [DM], F32, "in")
    MOD = io("MOD", [128, 48, 2], F32, "in")
    up = io("ffn_up", [44, 128, DM], F32, "in")
    cw = io("ffn_conv_w", [3, 5632], F32, "in")
    cb = io("ffn_conv_b", [5632], F32, "in")
    down = io("ffn_down", [2816, DM], F32, "in")
    XT2 = io("XT2", [DM, T], F32, "out")
    X1v = XT1.rearrange("(k p) t -> p k t", p=128)
    X2v = XT2.rearrange("(k p) t -> p k t", p=128)
    with ExitStack() as es:
        P = Pool_(fw, es)
        MODs = P.sb([128, 48, 2], F32)
        tmod = Tok()
        fw.dma("sp", MODs[:], MOD, writes=[tmod])
        cws = P.sb([128, 44, 3], F32)
        cbs = P.sb([128, 44], F32)
        tcw = Tok()
        for w in range(3):
            fw.dma("sp", cws[:, :, w], cw[w].rearrange("(j p) -> p j", p=128), writes=[tcw], allow_slow_non_contiguous=True)
        fw.dma("sp", cbs[:], cb.rearrange("(j p) -> p j", p=128), writes=[tcw], allow_slow_non_contiguous=True)
        h2 = P.sb([128, 8, T], BF16)
        th2 = Tok()
        AB, tab = make_AB(fw, P, MODs, tmod, n2g, 24, 32)
        with ExitStack() as es2:
            P2 = Pool_(fw, es2)
            st = norm_state(fw, P2)
            xr0 = Rot([P2.sb([128, 8, 512], F32) for _ in range(2)])
            for (lo, hi, ic) in SEGS:
                n = hi - lo
                xs, txs = xr0.next()
                fw.dma("sp", xs[:, :, :n], X1v[:, :, lo:hi], writes=[txs])
                norm_mod_seg(fw, P2, st, xs, txs, n, ic, AB, tab, lambda k, lo=lo, hi=hi: [h2[:, k, lo:hi]], [th2])
            fw.barrier()
        hid = P.sb([128, 11, T], BF16)
        thid = Tok()
        dwb = P.sb([128, 11, DM], BF16)
        tdw = Tok()
        urot = [Rot([P.sb([128, T], F32) for _ in range(2)]) for _ in range(2)]
        y = [P.sb([128, T], F32) for _ in range(2)]
        ty = [Tok(), Tok()]
        psr = Rot([P.ps() for _ in range(4)])
        tx2 = Tok()
        RANGES = [(0, NCTX), (NCTX, T)]
        GROUPS = [(0, 2), (2, 2), (4, 2), (6, 2), (8, 2), (10, 1)]
        for half in range(2):
            with ExitStack() as esu:
                Pu = Pool_(fw, esu)
                ustg = Rot([Pu.sb([128, 8, 256], F32, "ustg") for _ in range(2)])
                for jj in range(11):
                    r0 = (half * 11 + jj) * 128
                    sg, tsg = ustg.next()
                    sgv = sg[:].rearrange("p k n -> p (k n)")[:, 0:DM]
                    fw.dma("sp", sgv, down[r0:r0 + 128, :], writes=[tsg])
                    fw.op("pool", lambda e, sgv=sgv, jj=jj: e.tensor_copy(out=dwb[:, jj, :], in_=sgv), reads=[tsg], writes=[tdw])
                ubr = Rot([Pu.sb([128, 8, 2, 256], BF16, "ub") for _ in range(2)])

                def issue_load(g, half=half):
                    jj0, ng = GROUPS[g]
                    ub, tub = ubr.next()
                    for wh in range(2):
                        sg, tsg = ustg.next()
                        sgv = sg[:].rearrange("p k (j c) -> p (k j c)", j=2).rearrange("p (j k c) -> p j k c", j=2, k=8)
                        for jl in range(ng):
                            jt = wh * 22 + half * 11 + jj0 + jl
                            fw.dma("sp", sgv[:, jl].rearrange("p k c -> p (k c)"), up[jt], writes=[tsg])
                            fw.op("pool", lambda e, sgv=sgv, ub=ub, wh=wh, jl=jl: e.tensor_copy(out=ub[:, :, wh, jl * 128:(jl + 1) * 128], in_=sgv[:, jl]),
                                  reads=[tsg], writes=[tub])
                    return ub, tub
                loaded = issue_load(0)
                for g, (jj0, ng) in enumerate(GROUPS):
                    ub, tub = loaded
                    if g + 1 < len(GROUPS):
                        loaded = issue_load(g + 1)
                    for jl in range(ng):
                        jj = jj0 + jl
                        j = half * 11 + jj
                        for wh in range(2):
                            jc = wh * 22 + j
                            ucur, tucur = urot[wh].next()
                            for si, (lo, hi, ic) in enumerate(SEGS):
                                n = hi - lo
                                ps, tps = psr.next()
                                for k in range(8):
                                    fw.op("pe", lambda e, ps=ps, k=k, wh=wh, ub=ub, jl=jl, lo=lo, hi=hi, n=n: e.matmul(
                                        ps[:, :n], lhsT=ub[:, k, wh, jl * 128:(jl + 1) * 128], rhs=h2[:, k, lo:hi], start=(k == 0), stop=(k == 7)),
                                        reads=[tub, th2], writes=[tps])
                                fw.op("act", lambda e, ps=ps, ucur=ucur, lo=lo, hi=hi, n=n: e.copy(out=ucur[:, lo:hi], in_=ps[:, :n]),
                                      reads=[tps], writes=[tucur])
                            fw.op("act", lambda e, wh=wh, jc=jc, ucur=ucur: e.activation(out=y[wh][:], in_=ucur[:], func=AF.Identity,
                                                                                        scale=cws[:, jc, 1:2], bias=cbs[:, jc:jc + 1]),
                                  reads=[tucur, tcw], writes=[ty[wh]])
                            for (lo, hi) in RANGES:
                                fw.op("dve", lambda e, wh=wh, jc=jc, lo=lo, hi=hi, ucur=ucur: e.scalar_tensor_tensor(
                                    out=y[wh][:, lo + 1:hi], in0=ucur[:, lo:hi - 1], scalar=cws[:, jc, 0:1], in1=y[wh][:, lo + 1:hi],
                                    op0=ALU.mult, op1=ALU.add), reads=[tucur, tcw, ty[wh]], writes=[ty[wh]])
                                fw.op("dve", lambda e, wh=wh, jc=jc, lo=lo, hi=hi, ucur=ucur: e.scalar_tensor_tensor(
                                    out=y[wh][:, lo:hi - 1], in0=ucur[:, lo + 1:hi], scalar=cws[:, jc, 2:3], in1=y[wh][:, lo:hi - 1],
                                    op0=ALU.mult, op1=ALU.add), reads=[tucur, tcw, ty[wh]], writes=[ty[wh]])
                        fw.op("act", lambda e: e.activation(out=y[0][:], in_=y[0][:], func=AF.Silu), reads=[ty[0]], writes=[ty[0]])
                        fw.op("dve", lambda e, jj=jj: e.tensor_tensor(out=hid[:, jj, :], in0=y[0][:], in1=y[1][:], op=ALU.mult),
                              reads=[ty[0], ty[1]], writes=[thid])
                fw.barrier()
            with ExitStack() as esd:
                Pd = Pool_(fw, esd)
                xr = Rot([Pd.sb([128, 8, 512], F32) for _ in range(2)])
                for (lo, hi, ic) in SEGS:
                    n = hi - lo
                    xs, txs = xr.next()
                    src = X1v if half == 0 else X2v
                    fw.dma("sp", xs[:, :, :n], src[:, :, lo:hi], reads=([tx2] if half else []), writes=[txs])
                    for d in range(8):
                        ps, tps = psr.next()
                        for jj in range(11):
                            fw.op("pe", lambda e, ps=ps, jj=jj, d=d, lo=lo, hi=hi, n=n: e.matmul(
                                ps[:, :n], lhsT=dwb[:, jj, d * 128:(d + 1) * 128], rhs=hid[:, jj, lo:hi], start=(jj == 0), stop=(jj == 10)),
                                reads=[tdw, thid], writes=[tps])
                        fw.op("dve", lambda e, ps=ps, d=d, n=n, ic=ic, xs=xs: e.scalar_tensor_tensor(
                            out=xs[:, d, :n], in0=ps[:, :n], scalar=MODs[:, 40 + d, ic:ic + 1], in1=xs[:, d, :n], op0=ALU.mult, op1=ALU.add),
                            reads=[tps, tmod, txs], writes=[txs])
                    for k in range(8):
                        fw.dma("sp", XT2[k * 128:(k + 1) * 128, lo:hi], xs[:, k, :n], reads=[txs], writes=[tx2])
                fw.barrier()


def stage_p6(fw, io):
    XT = io("XT", [DM, T], F32, "in")
    fg = io("final_g", [DM], F32, "in")
    ident = io("ident", [128, 128], F32, "in")
    OUT = io("OUT", [NLAT, DM], F32, "out")
    XTv = XT.rearrange("(k p) t -> p k t", p=128)
    with ExitStack() as es:
        P = Pool_(fw, es)
        idt = P.sb([128, 128], F32)
        tid = Tok()
        fw.dma("sp", idt[:], ident, writes=[tid])
        g = P.sb([128, 8], F32)
        fw.dma("sp", g[:], fg.rearrange("(k p) -> p k", p=128), writes=[tid], allow_slow_non_contiguous=True)
        st = norm_state(fw, P)
        xr = Rot([P.sb([128, 8, 512], F32) for _ in range(2)])
        yr = Rot([P.sb([128, 8, 512], F32) for _ in range(2)])
        psr = Rot([P.ps() for _ in range(4)])
        orr = Rot([P.sb([128, DM], F32) for _ in range(3)])
        tout = Tok()
        for (lo, hi, ic) in SEGS[1:]:
            n = hi - lo
            xs, txs = xr.next()
            fw.dma("sp", xs[:, :, :n], XTv[:, :, lo:hi], writes=[txs])
            sq, ones = st["sq"], st["ones"]
            fw.op("act", lambda e, xs=xs: e.activation(out=sq[:], in_=xs[:], func=AF.Square), reads=[txs], writes=[st["tsq"]])
            ps, tps = st["psr"].next()
            for k in range(8):
                fw.op("pe", lambda e, ps=ps, k=k: e.matmul(ps[:], lhsT=ones[:], rhs=sq[:, k, :], start=(k == 0), stop=(k == 7)),
                      reads=[st["tsq"], st["tones"]], writes=[tps])
            rstd_from_ps(fw, st["rs"], st["trs"], ps, tps, n, 1.0 / DM, st["eps"][:, 0:1], st["teps"])
            ys, tys = yr.next()
            for k in range(8):
                fw.op("dve", lambda e, ys=ys, xs=xs, k=k: e.scalar_tensor_tensor(
                    out=ys[:, k, :], in0=xs[:, k, :], scalar=g[:, k:k + 1], in1=st["rs"][:], op0=ALU.mult, op1=ALU.mult),
                    reads=[txs, st["trs"], tid], writes=[tys])
            for blk in range(4):
                ot, tot = orr.next()
                for half in range(2):
                    ps2, tps2 = psr.next()
                    for k in range(4):
                        kk = half * 4 + k
                        fw.op("pe", lambda e, ps2=ps2, k=k, kk=kk, ys=ys, blk=blk: e.transpose(
                            ps2[:, k * 128:(k + 1) * 128], ys[:, kk, blk * 128:(blk + 1) * 128], idt[:]),
                            reads=[tys, tid], writes=[tps2])
                    if half == 0:
                        fw.op("dve", lambda e, ot=ot, ps2=ps2: e.tensor_copy(out=ot[:, 0:512], in_=ps2[:]), reads=[tps2], writes=[tot])
                    else:
                        fw.op("act", lambda e, ot=ot, ps2=ps2: e.copy(out=ot[:, 512:1024], in_=ps2[:]), reads=[tps2], writes=[tot])
                r0 = lo - NCTX + blk * 128
                fw.dma("sp", OUT[r0:r0 + 128, :], ot[:], reads=[tot], writes=[Tok()])
        fw.barrier()


def sin_reduced(fw, P, out, src, tsrc, shape, off, tout):
    ti = P.sb(shape, I32)
    tf = P.sb(shape, F32)
    tt = Tok()
    fw.op("dve", lambda e: e.tensor_scalar(out=ti[:], in0=src, scalar1=off, scalar2=1.0 / (2 * math.pi), op0=ALU.add, op1=ALU.mult),
          reads=[tsrc], writes=[tt])
    fw.op("dve", lambda e: e.tensor_copy(out=tf[:], in_=ti[:]), reads=[tt], writes=[tt])
    fw.op("dve", lambda e: e.scalar_tensor_tensor(out=tf[:], in0=tf[:], scalar=-2 * math.pi, in1=src, op0=ALU.mult, op1=ALU.add),
          reads=[tt, tsrc], writes=[tt])
    fw.op("dve", lambda e: e.tensor_scalar(out=tf[:], in0=tf[:], scalar1=off, scalar2=math.pi, op0=ALU.add, op1=ALU.min), reads=[tt], writes=[tt])
    fw.op("dve", lambda e: e.tensor_scalar(out=tf[:], in0=tf[:], scalar1=-math.pi, scalar2=None, op0=ALU.max), reads=[tt], writes=[tt])
    fw.op("act", lambda e: e.activation(out=out, in_=tf[:], func=AF.Sin), reads=[tt], writes=[tout])


def stage_p2(fw, io):
    ZT = io("ZT", [2048, TE], F32, "in")
    a_re = io("s5_a_re", [2, 16, 64], F32, "in")
    a_im = io("s5_a_im", [2, 16, 64], F32, "in")
    lstep = io("s5_log_step", [2, 16], F32, "in")
    b_re = io("s5_b_re", [2, 16, 64, 16], F32, "in")
    b_im = io("s5_b_im", [2, 16, 64, 16], F32, "in")
    c_re = io("s5_c_re", [2, 16, 16, 64], F32, "in")
    c_im = io("s5_c_im", [2, 16, 16, 64], F32, "in")
    dsk = io("s5_d", [256], F32, "in")
    glu_w = io("s5_glu_w", [256, 256], F32, "in")
    glu_b = io("s5_glu_b", [256], F32, "in")
    out_g = io("s5_out_g", [256], F32, "in")
    S5T = io("S5T", [256, T], F32, "out")
    N = T
    with ExitStack() as es:
        P = Pool_(fw, es)
        are = P.sb([128, 2, 8], F32)
        aim = P.sb([128, 2, 8], F32)
        lst = P.sb([128, 2, 8], F32)
        tpar = Tok()
        for di in range(2):
            fw.dma("sp", are[:, di, :], a_re[di].rearrange("(s g) p -> (g p) s", g=2), writes=[tpar], allow_slow_non_contiguous=True)
            fw.dma("sp", aim[:, di, :], a_im[di].rearrange("(s g) p -> (g p) s", g=2), writes=[tpar], allow_slow_non_contiguous=True)
            for g2 in range(2):
                fw.dma("sp", lst[g2 * 64:(g2 + 1) * 64, di:di + 1, :],
                       lstep[di].rearrange("(s g) -> g s", g=2)[g2:g2 + 1, :].partition_broadcast(64), writes=[tpar],
                       allow_slow_non_contiguous=True)
        sh = [128, 2, 8]
        step = P.sb(sh, F32)
        fw.op("act", lambda e: e.activation(out=step[:], in_=lst[:], func=AF.Exp), reads=[tpar], writes=[tpar])
        er = P.sb(sh, F32)
        th = P.sb(sh, F32)
        fw.op("dve", lambda e: e.tensor_tensor(out=er[:], in0=are[:], in1=step[:], op=ALU.mult), reads=[tpar], writes=[tpar])
        fw.op("act", lambda e: e.activation(out=er[:], in_=er[:], func=AF.Exp), reads=[tpar], writes=[tpar])
        fw.op("dve", lambda e: e.tensor_tensor(out=th[:], in0=aim[:], in1=step[:], op=ALU.mult), reads=[tpar], writes=[tpar])
        sn = P.sb(sh, F32)
        cs = P.sb(sh, F32)
        ttrig = Tok()
        sin_reduced(fw, P, sn[:], th[:], tpar, sh, 0.0, ttrig)
        sin_reduced(fw, P, cs[:], th[:], tpar, sh, math.pi / 2, ttrig)
        PW = P.sb([128, 9, 3, 2, 8], F32)
        tpw = Tok()
        fw.op("dve", lambda e: e.tensor_tensor(out=PW[:, 0, 0], in0=er[:], in1=cs[:], op=ALU.mult), reads=[tpar, ttrig], writes=[tpw])
        fw.op("dve", lambda e: e.tensor_tensor(out=PW[:, 0, 1], in0=er[:], in1=sn[:], op=ALU.mult), reads=[tpar, ttrig], writes=[tpw])
        t1 = P.sb(sh, F32)
        t2 = P.sb(sh, F32)
        for k in range(9):
            fw.op("dve", lambda e, k=k: e.tensor_scalar(out=PW[:, k, 2], in0=PW[:, k, 1], scalar1=-1.0, scalar2=None, op0=ALU.mult),
                  reads=[tpw], writes=[tpw])
            if k == 8:
                break
            fw.op("dve", lambda e, k=k: e.tensor_tensor(out=t1[:], in0=PW[:, k, 0], in1=PW[:, k, 0], op=ALU.mult), reads=[tpw], writes=[tpw])
            fw.op("dve", lambda e, k=k: e.tensor_tensor(out=t2[:], in0=PW[:, k, 1], in1=PW[:, k, 1], op=ALU.mult), reads=[tpw], writes=[tpw])
            fw.op("dve", lambda e, k=k: e.tensor_tensor(out=PW[:, k + 1, 0], in0=t1[:], in1=t2[:], op=ALU.subtract), reads=[tpw], writes=[tpw])
            fw.op("dve", lambda e, k=k: e.scalar_tensor_tensor(out=PW[:, k + 1, 1], in0=PW[:, k, 0], scalar=2.0, in1=PW[:, k, 1],
                                                               op0=ALU.mult, op1=ALU.mult), reads=[tpw], writes=[tpw])
        br = P.sb(sh, F32)
        bi = P.sb(sh, F32)
        nbi = P.sb(sh, F32)
        den = P.sb(sh, F32)
        nr = P.sb(sh, F32)
        tb = Tok()
        fw.op("dve", lambda e: e.tensor_tensor(out=den[:], in0=are[:], in1=are[:], op=ALU.mult), reads=[tpar], writes=[tb])
        fw.op("dve", lambda e: e.tensor_tensor(out=t1[:], in0=aim[:], in1=aim[:], op=ALU.mult), reads=[tpar, tpw], writes=[tpw])
        fw.op("dve", lambda e: e.tensor_tensor(out=den[:], in0=den[:], in1=t1[:], op=ALU.add), reads=[tb, tpw], writes=[tb])
        fw.op("dve", lambda e: e.reciprocal(out=den[:], in_=den[:]), reads=[tb], writes=[tb])
        fw.op("dve", lambda e: e.tensor_scalar(out=nr[:], in0=PW[:, 0, 0], scalar1=-1.0, scalar2=None, op0=ALU.add), reads=[tpw], writes=[tb])
        fw.op("dve", lambda e: e.tensor_tensor(out=t1[:], in0=nr[:], in1=are[:], op=ALU.mult), reads=[tb, tpar, tpw], writes=[tpw])
        fw.op("dve", lambda e: e.tensor_tensor(out=t2[:], in0=PW[:, 0, 1], in1=aim[:], op=ALU.mult), reads=[tpw, tpar], writes=[tpw])
        fw.op("dve", lambda e: e.tensor_tensor(out=t1[:], in0=t1[:], in1=t2[:], op=ALU.add), reads=[tpw], writes=[tpw])
        fw.op("dve", lambda e: e.tensor_tensor(out=br[:], in0=t1[:], in1=den[:], op=ALU.mult), reads=[tpw, tb], writes=[tb])
        fw.op("dve", lambda e: e.tensor_tensor(out=t1[:], in0=PW[:, 0, 1], in1=are[:], op=ALU.mult), reads=[tpw, tpar, tb], writes=[tpw])
        fw.op("dve", lambda e: e.tensor_tensor(out=t2[:], in0=nr[:], in1=aim[:], op=ALU.mult), reads=[tb, tpar, tpw], writes=[tpw])
        fw.op("dve", lambda e: e.tensor_tensor(out=t1[:], in0=t1[:], in1=t2[:], op=ALU.subtract), reads=[tpw], writes=[tpw])
        fw.op("dve", lambda e: e.tensor_tensor(out=bi[:], in0=t1[:], in1=den[:], op=ALU.mult), reads=[tpw, tb], writes=[tb])
        fw.op("dve", lambda e: e.tensor_scalar(out=nbi[:], in0=bi[:], scalar1=-1.0, scalar2=None, op0=ALU.mult), reads=[tb], writes=[tb])
        BTb = P.sb([128, 2, 2, 8, 128], BF16)
        CTb = P.sb([128, 2, 2, 8, 128], BF16)
        tBT = Tok()
        tCT = Tok()
        with ExitStack() as es2:
            P2 = Pool_(fw, es2)
            BTf = P2.sb([128, 2, 2, 8, 128], F32)
            CTf = P2.sb([128, 2, 2, 8, 128], F32)
            CT2 = P2.sb([128, 2, 2, 8, 128], F32)
            tf1, tf2 = Tok(), Tok()
            fw.op("pool", lambda e: e.memset(BTf[:], 0.0), writes=[tf1])
            fw.op("pool", lambda e: e.memset(CTf[:], 0.0), writes=[tf2])
            for di in range(2):
                for ri, (bsrc, csrc) in enumerate(((b_re, c_re), (b_im, c_im))):
                    for g in range(16):
                        s, g2 = g // 2, g % 2
                        r0 = (g % 8) * 16
                        fw.dma("sp", BTf[r0:r0 + 16, di, ri, s, g2 * 64:(g2 + 1) * 64], bsrc[di, g].rearrange("p h -> h p"),
                               writes=[tf1], allow_slow_non_contiguous=True)
                        fw.dma("sp", CTf[g2 * 64:(g2 + 1) * 64, di, ri, s, r0:r0 + 16], csrc[di, g].rearrange("h p -> p h"),
                               writes=[tf2], allow_slow_non_contiguous=True)
            fw.op("act", lambda e: e.copy(out=BTb[:], in_=BTf[:]), reads=[tf1], writes=[tBT])
            bsh = [128, 2, 8, 128]
            brb = br[:].unsqueeze(3).to_broadcast(bsh)
            nbib = nbi[:].unsqueeze(3).to_broadcast(bsh)
            tc2 = Tok()
            fw.op("dve", lambda e: e.tensor_tensor(out=CT2[:, :, 0], in0=CTf[:, :, 0], in1=brb, op=ALU.mult), reads=[tf2, tb], writes=[tc2])
            fw.op("pool", lambda e: e.tensor_tensor(out=CT2[:, :, 1], in0=CTf[:, :, 1], in1=nbib, op=ALU.mult), reads=[tf2, tb], writes=[tc2])
            fw.op("dve", lambda e: e.tensor_tensor(out=CT2[:, :, 0], in0=CT2[:, :, 0], in1=CT2[:, :, 1], op=ALU.add), reads=[tc2], writes=[tc2])
            fw.op("act", lambda e: e.copy(out=CTb[:, :, 0], in_=CT2[:, :, 0]), reads=[tc2], writes=[tCT])
            fw.op("dve", lambda e: e.tensor_tensor(out=CT2[:, :, 0], in0=CTf[:, :, 0], in1=nbib, op=ALU.mult), reads=[tf2, tb, tCT, tc2], writes=[tc2])
            fw.op("pool", lambda e: e.tensor_tensor(out=CT2[:, :, 1], in0=CTf[:, :, 1], in1=brb, op=ALU.mult), reads=[tf2, tb, tc2], writes=[tc2])
            fw.op("dve", lambda e: e.tensor_tensor(out=CT2[:, :, 0], in0=CT2[:, :, 0], in1=CT2[:, :, 1], op=ALU.subtract), reads=[tc2], writes=[tc2])
            fw.op("act", lambda e: e.copy(out=CTb[:, :, 1], in_=CT2[:, :, 0]), reads=[tc2], writes=[tCT])
            fw.barrier()
        ub = P.sb([128, 2, TE], BF16)
        tub = Tok()
        for ct in range(2):
            fw.dma("pool", ub[:, ct, :], ZT[ct * 128:(ct + 1) * 128, :], writes=[tub])
        dk = P.sb([128, 2], F32)
        tdk = Tok()
        fw.dma("sp", dk[:], dsk.rearrange("(c p) -> p c", p=128), writes=[tdk], allow_slow_non_contiguous=True)
        y = P.sb([128, 2, T], F32)
        ty = Tok()
        for ct in range(2):
            fw.dma("sp", y[:, ct, :], ZT[ct * 128:(ct + 1) * 128, 0:T], writes=[ty])
        for ct in range(2):
            fw.op("pool", lambda e, ct=ct: e.tensor_scalar(out=y[:, ct, :], in0=y[:, ct, :], scalar1=dk[:, ct:ct + 1], scalar2=None, op0=ALU.mult),
                  reads=[ty, tdk], writes=[ty])
        Xs = Rot([P.sb([128, 2, N], F32) for _ in range(4)])
        xbs = Rot([P.sb([128, 2, N], BF16) for _ in range(2)])
        psr = Rot([P.ps() for _ in range(4)])
        psy = Rot([P.ps() for _ in range(2)])
        tmpy = Rot([P.sb([128, 512], F32) for _ in range(2)])
        CH = [(0, 512), (512, 1024), (1024, 1536), (1536, 2048), (2048, 2304)]

        def prep(di, s):
            off = 0 if di == 0 else 256
            ct = s // 4
            X, tX = Xs.next()
            for ri in range(2):
                for ci, (c0, c1) in enumerate(CH):
                    n = c1 - c0
                    ps, tps = psr.next()
                    fw.op("pe", lambda e, ps=ps, ri=ri, c0=c0, c1=c1, n=n: e.matmul(
                        ps[:, :n], lhsT=BTb[:, di, ri, s, :], rhs=ub[:, ct, off + c0:off + c1], start=True, stop=True),
                        reads=[tBT, tub], writes=[tps])
                    fw.op("act", lambda e, ps=ps, ri=ri, c0=c0, c1=c1, n=n: e.copy(out=X[:, ri, c0:c1], in_=ps[:, :n]),
                          reads=[tps], writes=[tX])
            return X, tX

        def scan_gen(di, s, X, tX):
            def cstep(w_re, w_im, r_re, r_im, k):
                pr = PW[:, k, 0, di, s:s + 1]
                pi = PW[:, k, 1, di, s:s + 1]
                npi = PW[:, k, 2, di, s:s + 1]
                for (o, i0, sc) in ((w_re, r_re, pr), (w_re, r_im, npi), (w_im, r_im, pr), (w_im, r_re, pi)):
                    fw.op("dve", lambda e, o=o, i0=i0, sc=sc: e.scalar_tensor_tensor(out=o, in0=i0, scalar=sc, in1=o, op0=ALU.mult, op1=ALU.add),
                          reads=[tX, tpw], writes=[tX])
                    yield
            for k in range(8):
                st_ = 1 << k
                Xv = [X[:, ri, :].rearrange("p (m c) -> p m c", c=2 * st_) for ri in range(2)]
                if di == 0:
                    yield from cstep(Xv[0][:, :, 2 * st_ - 1], Xv[1][:, :, 2 * st_ - 1], Xv[0][:, :, st_ - 1], Xv[1][:, :, st_ - 1], k)
                else:
                    yield from cstep(Xv[0][:, :, 0], Xv[1][:, :, 0], Xv[0][:, :, st_], Xv[1][:, :, st_], k)
            for i in (range(1, 9) if di == 0 else range(7, -1, -1)):
                if di == 0:
                    w, r = 256 * i + 255, 256 * (i - 1) + 255
                else:
                    w, r = 256 * i, 256 * (i + 1)
                yield from cstep(X[:, 0, w:w + 1], X[:, 1, w:w + 1], X[:, 0, r:r + 1], X[:, 1, r:r + 1], 8)
            for k in range(7, -1, -1):
                st_ = 1 << k
                Xv = [X[:, ri, :].rearrange("p (m c) -> p m c", c=2 * st_) for ri in range(2)]
                if di == 0:
                    yield from cstep(Xv[0][:, 1:, st_ - 1], Xv[1][:, 1:, st_ - 1], Xv[0][:, :-1, 2 * st_ - 1], Xv[1][:, :-1, 2 * st_ - 1], k)
                else:
                    yield from cstep(Xv[0][:, :-1, st_], Xv[1][:, :-1, st_], Xv[0][:, 1:, 0], Xv[1][:, 1:, 0], k)

        def fin(di, s, X, tX):
            ct = s // 4
            xb, txb = xbs.next()
            fw.op("act", lambda e: e.copy(out=xb[:], in_=X[:]), reads=[tX], writes=[txb])
            for (c0, c1) in CH:
                n = c1 - c0
                if di == 0:
                    y0 = c0
                else:
                    y0 = c0 + 256 if c0 < 2048 else 0
                ps, tps = psy.next()
                for ri in range(2):
                    fw.op("pe", lambda e, ps=ps, ri=ri, c0=c0, c1=c1, n=n: e.matmul(
                        ps[:, :n], lhsT=CTb[:, di, ri, s, :], rhs=xb[:, ri, c0:c1], start=(ri == 0), stop=(ri == 1)),
                        reads=[tCT, txb], writes=[tps])
                tm, ttm = tmpy.next()
                fw.op("act", lambda e, tm=tm, ps=ps, n=n: e.copy(out=tm[:, :n], in_=ps[:, :n]), reads=[tps], writes=[ttm])
                fw.op("pool", lambda e, tm=tm, y0=y0, n=n: e.tensor_tensor(out=y[:, ct, y0:y0 + n], in0=y[:, ct, y0:y0 + n], in1=tm[:, :n], op=ALU.add),
                      reads=[ttm, ty], writes=[ty])

        pairs = [(di, s0, s0 + 1) for di in range(2) for s0 in range(0, 8, 2)]
        nxt = [prep(pairs[0][0], pairs[0][1]), prep(pairs[0][0], pairs[0][2])]
        for pi_, (di, s0, s1) in enumerate(pairs):
            cur = nxt
            if pi_ + 1 < len(pairs):
                d2, a0, a1 = pairs[pi_ + 1]
                nxt = [prep(d2, a0), prep(d2, a1)]
            gens = [scan_gen(di, s0, *cur[0]), scan_gen(di, s1, *cur[1])]
            alive = [True, True]
            while any(alive):
                for gi in range(2):
                    if alive[gi]:
                        try:
                            next(gens[gi])
                        except StopIteration:
                            alive[gi] = False
            fin(di, s0, *cur[0])
            fin(di, s1, *cur[1])
        tg = Tok()
        gw = load_w_bf16(fw, P, glu_w, 256, 256, tg, "gluw")
        gb = P.sb([128, 2], F32)
        og = P.sb([128, 2], F32)
        fw.dma("sp", gb[:], glu_b.rearrange("(c p) -> p c", p=128), writes=[tg], allow_slow_non_contiguous=True)
        fw.dma("sp", og[:], out_g.rearrange("(c p) -> p c", p=128), writes=[tg], allow_slow_non_contiguous=True)
        ones = P.sb([128, 128], BF16)
        eps = P.sb([128, 1], F32)
        fw.op("pool", lambda e: e.memset(ones[:], 1.0), writes=[tg])
        fw.op("pool", lambda e: e.memset(eps[:], RMS_EPS), writes=[tg])
        a32 = P.sb([128, 2, 512], F32)
        ab = P.sb([128, 2, 512], BF16)
        w1 = P.sb([128, 2, 512], F32)
        w2 = P.sb([128, 2, 512], F32)
        sqb = P.sb([128, 2, 512], BF16)
        rs = P.sb([128, 512], F32)
        ta, tw1, trs = Tok(), Tok(), Tok()
        stg = Rot([P.sb([128, 512], F32) for _ in range(2)])
        tout = Tok()
        C1 = math.sqrt(2.0 / math.pi)
        for (lo, hi, ic) in SEGS:
            n = hi - lo
            yv = y[:, :, lo:hi]
            fw.op("act", lambda e, yv=yv, n=n: e.activation(out=w1[:, :, :n], in_=yv, func=AF.Square), reads=[ty], writes=[tw1])
            fw.op("dve", lambda e, n=n: e.tensor_scalar(out=w1[:, :, :n], in0=w1[:, :, :n], scalar1=0.044715 * C1, scalar2=C1, op0=ALU.mult, op1=ALU.add),
                  reads=[tw1], writes=[tw1])
            fw.op("dve", lambda e, yv=yv, n=n: e.tensor_tensor(out=w1[:, :, :n], in0=w1[:, :, :n], in1=yv, op=ALU.mult), reads=[tw1, ty], writes=[tw1])
            fw.op("act", lambda e, n=n: e.activation(out=w1[:, :, :n], in_=w1[:, :, :n], func=AF.Tanh), reads=[tw1], writes=[tw1])
            fw.op("dve", lambda e, n=n: e.tensor_scalar(out=w1[:, :, :n], in0=w1[:, :, :n], scalar1=0.5, scalar2=0.5, op0=ALU.mult, op1=ALU.add),
                  reads=[tw1], writes=[tw1])
            fw.op("dve", lambda e, yv=yv, n=n: e.tensor_tensor(out=a32[:, :, :n], in0=w1[:, :, :n], in1=yv, op=ALU.mult), reads=[tw1, ty], writes=[ta])
            fw.op("act", lambda e, n=n: e.copy(out=ab[:, :, :n], in_=a32[:, :, :n]), reads=[ta], writes=[ta])
            for nt in range(2):
                ps, tps = psr.next()
                for k in range(2):
                    fw.op("pe", lambda e, ps=ps, k=k, nt=nt, n=n: e.matmul(ps[:, :n], lhsT=gw[:, k, nt * 128:(nt + 1) * 128], rhs=ab[:, k, :n],
                                                                      start=(k == 0), stop=(k == 1)), reads=[tg, ta], writes=[tps])
                fw.op("act", lambda e, ps=ps, nt=nt, n=n: e.activation(out=w2[:, nt, :n], in_=ps[:, :n], func=AF.Sigmoid, bias=gb[:, nt:nt + 1]),
                      reads=[tps, tg], writes=[tw1])
            fw.op("dve", lambda e, n=n: e.tensor_tensor(out=w2[:, :, :n], in0=w2[:, :, :n], in1=a32[:, :, :n], op=ALU.mult), reads=[tw1, ta], writes=[tw1])
            fw.op("act", lambda e, n=n: e.activation(out=sqb[:, :, :n], in_=w2[:, :, :n], func=AF.Square), reads=[tw1], writes=[tw1])
            ps, tps = psr.next()
            for k in range(2):
                fw.op("pe", lambda e, ps=ps, k=k, n=n: e.matmul(ps[:, :n], lhsT=ones[:], rhs=sqb[:, k, :n], start=(k == 0), stop=(k == 1)),
                      reads=[tw1, tg], writes=[tps])
            rstd_from_ps(fw, rs, trs, ps, tps, n, 1.0 / 256, eps[:, 0:1], tg)
            for k in range(2):
                sg, tsg = stg.next()
                fw.op("dve", lambda e, sg=sg, k=k, n=n: e.scalar_tensor_tensor(out=sg[:, :n], in0=w2[:, k, :n], scalar=og[:, k:k + 1], in1=rs[:, :n],
                                                                             op0=ALU.mult, op1=ALU.mult), reads=[tw1, trs, tg], writes=[tsg])
                fw.dma("sp", S5T[k * 128:(k + 1) * 128, lo:hi], sg[:, :n], reads=[tsg], writes=[Tok()])
        fw.barrier()


def gen_p2(fw, io):
    ZT = io("ZT", [2048, TE], F32, "in")
    a_re = io("s5_a_re", [2, 16, 64], F32, "in")
    a_im = io("s5_a_im", [2, 16, 64], F32, "in")
    lstep = io("s5_log_step", [2, 16], F32, "in")
    b_re = io("s5_b_re", [2, 16, 64, 16], F32, "in")
    b_im = io("s5_b_im", [2, 16, 64, 16], F32, "in")
    c_re = io("s5_c_re", [2, 16, 16, 64], F32, "in")
    c_im = io("s5_c_im", [2, 16, 16, 64], F32, "in")
    dsk = io("s5_d", [256], F32, "in")
    glu_w = io("s5_glu_w", [256, 256], F32, "in")
    glu_b = io("s5_glu_b", [256], F32, "in")
    out_g = io("s5_out_g", [256], F32, "in")
    S5T = io("S5T", [256, T], F32, "out")
    N = T
    with ExitStack() as es:
        P = Pool_(fw, es)
        are = P.sb([128, 2, 8], F32)
        aim = P.sb([128, 2, 8], F32)
        lst = P.sb([128, 2, 8], F32)
        tpar = Tok()
        for di in range(2):
            fw.dma("sp", are[:, di, :], a_re[di].rearrange("(s g) p -> (g p) s", g=2), writes=[tpar], allow_slow_non_contiguous=True)
            fw.dma("sp", aim[:, di, :], a_im[di].rearrange("(s g) p -> (g p) s", g=2), writes=[tpar], allow_slow_non_contiguous=True)
            for g2 in range(2):
                fw.dma("sp", lst[g2 * 64:(g2 + 1) * 64, di:di + 1, :],
                       lstep[di].rearrange("(s g) -> g s", g=2)[g2:g2 + 1, :].partition_broadcast(64), writes=[tpar],
                       allow_slow_non_contiguous=True)
        sh = [128, 2, 8]
        step = P.sb(sh, F32)
        fw.op("act", lambda e: e.activation(out=step[:], in_=lst[:], func=AF.Exp), reads=[tpar], writes=[tpar])
        er = P.sb(sh, F32)
        th = P.sb(sh, F32)
        fw.op("dve", lambda e: e.tensor_tensor(out=er[:], in0=are[:], in1=step[:], op=ALU.mult), reads=[tpar], writes=[tpar])
        fw.op("act", lambda e: e.activation(out=er[:], in_=er[:], func=AF.Exp), reads=[tpar], writes=[tpar])
        fw.op("dve", lambda e: e.tensor_tensor(out=th[:], in0=aim[:], in1=step[:], op=ALU.mult), reads=[tpar], writes=[tpar])
        sn = P.sb(sh, F32)
        cs = P.sb(sh, F32)
        ttrig = Tok()
        sin_reduced(fw, P, sn[:], th[:], tpar, sh, 0.0, ttrig)
        sin_reduced(fw, P, cs[:], th[:], tpar, sh, math.pi / 2, ttrig)
        PW = P.sb([128, 9, 3, 2, 8], F32)
        tpw = Tok()
        fw.op("dve", lambda e: e.tensor_tensor(out=PW[:, 0, 0], in0=er[:], in1=cs[:], op=ALU.mult), reads=[tpar, ttrig], writes=[tpw])
        fw.op("dve", lambda e: e.tensor_tensor(out=PW[:, 0, 1], in0=er[:], in1=sn[:], op=ALU.mult), reads=[tpar, ttrig], writes=[tpw])
        t1 = P.sb(sh, F32)
        t2 = P.sb(sh, F32)
        for k in range(9):
            fw.op("dve", lambda e, k=k: e.tensor_scalar(out=PW[:, k, 2], in0=PW[:, k, 1], scalar1=-1.0, scalar2=None, op0=ALU.mult),
                  reads=[tpw], writes=[tpw])
            if k == 8:
                break
            fw.op("dve", lambda e, k=k: e.tensor_tensor(out=t1[:], in0=PW[:, k, 0], in1=PW[:, k, 0], op=ALU.mult), reads=[tpw], writes=[tpw])
            fw.op("dve", lambda e, k=k: e.tensor_tensor(out=t2[:], in0=PW[:, k, 1], in1=PW[:, k, 1], op=ALU.mult), reads=[tpw], writes=[tpw])
            fw.op("dve", lambda e, k=k: e.tensor_tensor(out=PW[:, k + 1, 0], in0=t1[:], in1=t2[:], op=ALU.subtract), reads=[tpw], writes=[tpw])
            fw.op("dve", lambda e, k=k: e.scalar_tensor_tensor(out=PW[:, k + 1, 1], in0=PW[:, k, 0], scalar=2.0, in1=PW[:, k, 1],
                                                               op0=ALU.mult, op1=ALU.mult), reads=[tpw], writes=[tpw])
        br = P.sb(sh, F32)
        bi = P.sb(sh, F32)
        nbi = P.sb(sh, F32)
        den = P.sb(sh, F32)
        nr = P.sb(sh, F32)
        tb = Tok()
        fw.op("dve", lambda e: e.tensor_tensor(out=den[:], in0=are[:], in1=are[:], op=ALU.mult), reads=[tpar], writes=[tb])
        fw.op("dve", lambda e: e.tensor_tensor(out=t1[:], in0=aim[:], in1=aim[:], op=ALU.mult), reads=[tpar, tpw], writes=[tpw])
        fw.op("dve", lambda e: e.tensor_tensor(out=den[:], in0=den[:], in1=t1[:], op=ALU.add), reads=[tb, tpw], writes=[tb])
        fw.op("dve", lambda e: e.reciprocal(out=den[:], in_=den[:]), reads=[tb], writes=[tb])
        fw.op("dve", lambda e: e.tensor_scalar(out=nr[:], in0=PW[:, 0, 0], scalar1=-1.0, scalar2=None, op0=ALU.add), reads=[tpw], writes=[tb])
        fw.op("dve", lambda e: e.tensor_tensor(out=t1[:], in0=nr[:], in1=are[:], op=ALU.mult), reads=[tb, tpar, tpw], writes=[tpw])
        fw.op("dve", lambda e: e.tensor_tensor(out=t2[:], in0=PW[:, 0, 1], in1=aim[:], op=ALU.mult), reads=[tpw, tpar], writes=[tpw])
        fw.op("dve", lambda e: e.tensor_tensor(out=t1[:], in0=t1[:], in1=t2[:], op=ALU.add), reads=[tpw], writes=[tpw])
        fw.op("dve", lambda e: e.tensor_tensor(out=br[:], in0=t1[:], in1=den[:], op=ALU.mult), reads=[tpw, tb], writes=[tb])
        fw.op("dve", lambda e: e.tensor_tensor(out=t1[:], in0=PW[:, 0, 1], in1=are[:], op=ALU.mult), reads=[tpw, tpar, tb], writes=[tpw])
        fw.op("dve", lambda e: e.tensor_tensor(out=t2[:], in0=nr[:], in1=aim[:], op=ALU.mult), reads=[tb, tpar, tpw], writes=[tpw])
        fw.op("dve", lambda e: e.tensor_tensor(out=t1[:], in0=t1[:], in1=t2[:], op=ALU.subtract), reads=[tpw], writes=[tpw])
        fw.op("dve", lambda e: e.tensor_tensor(out=bi[:], in0=t1[:], in1=den[:], op=ALU.mult), reads=[tpw, tb], writes=[tb])
        fw.op("dve", lambda e: e.tensor_scalar(out=nbi[:], in0=bi[:], scalar1=-1.0, scalar2=None, op0=ALU.mult), reads=[tb], writes=[tb])
        BTb = P.sb([128, 2, 2, 8, 128], BF16)
        CTb = P.sb([128, 2, 2, 8, 128], BF16)
        tBT = Tok()
        tCT = Tok()
        with ExitStack() as es2:
            P2 = Pool_(fw, es2)
            BTf = P2.sb([128, 2, 2, 8, 128], F32)
            CTf = P2.sb([128, 2, 2, 8, 128], F32)
            CT2 = P2.sb([128, 2, 2, 8, 128], F32)
            tf1, tf2 = Tok(), Tok()
            fw.op("pool", lambda e: e.memset(BTf[:], 0.0), writes=[tf1])
            fw.op("pool", lambda e: e.memset(CTf[:], 0.0), writes=[tf2])
            for di in range(2):
                for ri, (bsrc, csrc) in enumerate(((b_re, c_re), (b_im, c_im))):
                    for g in range(16):
                        s, g2 = g // 2, g % 2
                        r0 = (g % 8) * 16
                        fw.dma("sp", BTf[r0:r0 + 16, di, ri, s, g2 * 64:(g2 + 1) * 64], bsrc[di, g].rearrange("p h -> h p"),
                               writes=[tf1], allow_slow_non_contiguous=True)
                        fw.dma("sp", CTf[g2 * 64:(g2 + 1) * 64, di, ri, s, r0:r0 + 16], csrc[di, g].rearrange("h p -> p h"),
                               writes=[tf2], allow_slow_non_contiguous=True)
            fw.op("act", lambda e: e.copy(out=BTb[:], in_=BTf[:]), reads=[tf1], writes=[tBT])
            bsh = [128, 2, 8, 128]
            brb = br[:].unsqueeze(3).to_broadcast(bsh)
            nbib = nbi[:].unsqueeze(3).to_broadcast(bsh)
            tc2 = Tok()
            fw.op("dve", lambda e: e.tensor_tensor(out=CT2[:, :, 0], in0=CTf[:, :, 0], in1=brb, op=ALU.mult), reads=[tf2, tb], writes=[tc2])
            fw.op("pool", lambda e: e.tensor_tensor(out=CT2[:, :, 1], in0=CTf[:, :, 1], in1=nbib, op=ALU.mult), reads=[tf2, tb], writes=[tc2])
            fw.op("dve", lambda e: e.tensor_tensor(out=CT2[:, :, 0], in0=CT2[:, :, 0], in1=CT2[:, :, 1], op=ALU.add), reads=[tc2], writes=[tc2])
            fw.op("act", lambda e: e.copy(out=CTb[:, :, 0], in_=CT2[:, :, 0]), reads=[tc2], writes=[tCT])
            fw.op("dve", lambda e: e.tensor_tensor(out=CT2[:, :, 0], in0=CTf[:, :, 0], in1=nbib, op=ALU.mult), reads=[tf2, tb, tCT, tc2], writes=[tc2])
            fw.op("pool", lambda e: e.tensor_tensor(out=CT2[:, :, 1], in0=CTf[:, :, 1], in1=brb, op=ALU.mult), reads=[tf2, tb, tc2], writes=[tc2])
            fw.op("dve", lambda e: e.tensor_tensor(out=CT2[:, :, 0], in0=CT2[:, :, 0], in1=CT2[:, :, 1], op=ALU.subtract), reads=[tc2], writes=[tc2])
            fw.op("act", lambda e: e.copy(out=CTb[:, :, 1], in_=CT2[:, :, 0]), reads=[tc2], writes=[tCT])
            fw.barrier()
        ub = P.sb([128, 2, TE], BF16)
        tub = Tok()
        for ct in range(2):
            fw.dma("pool", ub[:, ct, :], ZT[ct * 128:(ct + 1) * 128, :], writes=[tub])
        dk = P.sb([128, 2], F32)
        tdk = Tok()
        fw.dma("sp", dk[:], dsk.rearrange("(c p) -> p c", p=128), writes=[tdk], allow_slow_non_contiguous=True)
        y = P.sb([128, 2, T], F32)
        ty = Tok()
        for ct in range(2):
            fw.dma("sp", y[:, ct, :], ZT[ct * 128:(ct + 1) * 128, 0:T], writes=[ty])
        for ct in range(2):
            fw.op("pool", lambda e, ct=ct: e.tensor_scalar(out=y[:, ct, :], in0=y[:, ct, :], scalar1=dk[:, ct:ct + 1], scalar2=None, op0=ALU.mult),
                  reads=[ty, tdk], writes=[ty])
        Xs = Rot([P.sb([128, 2, N], F32) for _ in range(2)])
        xbs = Rot([P.sb([128, 2, N], BF16) for _ in range(2)])
        psr = Rot([P.ps() for _ in range(2)])
        psy = Rot([P.ps() for _ in range(1)])
        tmpy = Rot([P.sb([128, 512], F32) for _ in range(2)])
        CH = [(0, 512), (512, 1024), (1024, 1536), (1536, 2048), (2048, 2304)]

        def prep(di, s):
            off = 0 if di == 0 else 256
            ct = s // 4
            X, tX = Xs.next()
            for ri in range(2):
                for ci, (c0, c1) in enumerate(CH):
                    n = c1 - c0
                    ps, tps = psr.next()
                    fw.op("pe", lambda e, ps=ps, ri=ri, c0=c0, c1=c1, n=n: e.matmul(
                        ps[:, :n], lhsT=BTb[:, di, ri, s, :], rhs=ub[:, ct, off + c0:off + c1], start=True, stop=True),
                        reads=[tBT, tub], writes=[tps])
                    fw.op("act", lambda e, ps=ps, ri=ri, c0=c0, c1=c1, n=n: e.copy(out=X[:, ri, c0:c1], in_=ps[:, :n]),
                          reads=[tps], writes=[tX])
            return X, tX

        def scan_gen(di, s, X, tX):
            def cstep(w_re, w_im, r_re, r_im, k):
                pr = PW[:, k, 0, di, s:s + 1]
                pi = PW[:, k, 1, di, s:s + 1]
                npi = PW[:, k, 2, di, s:s + 1]
                for (o, i0, sc) in ((w_re, r_re, pr), (w_re, r_im, npi), (w_im, r_im, pr), (w_im, r_re, pi)):
                    fw.op("dve", lambda e, o=o, i0=i0, sc=sc: e.scalar_tensor_tensor(out=o, in0=i0, scalar=sc, in1=o, op0=ALU.mult, op1=ALU.add),
                          reads=[tX, tpw], writes=[tX])
                    yield
            for k in range(8):
                st_ = 1 << k
                Xv = [X[:, ri, :].rearrange("p (m c) -> p m c", c=2 * st_) for ri in range(2)]
                if di == 0:
                    yield from cstep(Xv[0][:, :, 2 * st_ - 1], Xv[1][:, :, 2 * st_ - 1], Xv[0][:, :, st_ - 1], Xv[1][:, :, st_ - 1], k)
                else:
                    yield from cstep(Xv[0][:, :, 0], Xv[1][:, :, 0], Xv[0][:, :, st_], Xv[1][:, :, st_], k)
            for i in (range(1, 9) if di == 0 else range(7, -1, -1)):
                if di == 0:
                    w, r = 256 * i + 255, 256 * (i - 1) + 255
                else:
                    w, r = 256 * i, 256 * (i + 1)
                yield from cstep(X[:, 0, w:w + 1], X[:, 1, w:w + 1], X[:, 0, r:r + 1], X[:, 1, r:r + 1], 8)
            for k in range(7, -1, -1):
                st_ = 1 << k
                Xv = [X[:, ri, :].rearrange("p (m c) -> p m c", c=2 * st_) for ri in range(2)]
                if di == 0:
                    yield from cstep(Xv[0][:, 1:, st_ - 1], Xv[1][:, 1:, st_ - 1], Xv[0][:, :-1, 2 * st_ - 1], Xv[1][:, :-1, 2 * st_ - 1], k)
                else:
                    yield from cstep(Xv[0][:, :-1, st_], Xv[1][:, :-1, st_], Xv[0][:, 1:, 0], Xv[1][:, 1:, 0], k)

        def fin(di, s, X, tX):
            ct = s // 4
            xb, txb = xbs.next()
            fw.op("act", lambda e: e.copy(out=xb[:], in_=X[:]), reads=[tX], writes=[txb])
            for (c0, c1) in CH:
                n = c1 - c0
                if di == 0:
                    y0 = c0
                else:
                    y0 = c0 + 256 if c0 < 2048 else 0
                ps, tps = psy.next()
                for ri in range(2):
                    fw.op("pe", lambda e, ps=ps, ri=ri, c0=c0, c1=c1, n=n: e.matmul(
                        ps[:, :n], lhsT=CTb[:, di, ri, s, :], rhs=xb[:, ri, c0:c1], start=(ri == 0), stop=(ri == 1)),
                        reads=[tCT, txb], writes=[tps])
                tm, ttm = tmpy.next()
                fw.op("act", lambda e, tm=tm, ps=ps, n=n: e.copy(out=tm[:, :n], in_=ps[:, :n]), reads=[tps], writes=[ttm])
                fw.op("pool", lambda e, tm=tm, y0=y0, n=n: e.tensor_tensor(out=y[:, ct, y0:y0 + n], in0=y[:, ct, y0:y0 + n], in1=tm[:, :n], op=ALU.add),
                      reads=[ttm, ty], writes=[ty])

        pairs = [(di, s0, s0 + 1) for di in range(2) for s0 in range(0, 8, 2)]
        yield "core"
        for pi_, (di, s0, s1) in enumerate(pairs):
            cur = [prep(di, s0), prep(di, s1)]
            yield
            gens = [scan_gen(di, s0, *cur[0]), scan_gen(di, s1, *cur[1])]
            alive = [True, True]
            while any(alive):
                for gi in range(2):
                    if alive[gi]:
                        try:
                            next(gens[gi])
                            yield
                        except StopIteration:
                            alive[gi] = False
            fin(di, s0, *cur[0])
            yield
            fin(di, s1, *cur[1])
            yield
        yield "fin"
        tg = Tok()
        gw = load_w_bf16(fw, P, glu_w, 256, 256, tg, "gluw")
        gb = P.sb([128, 2], F32)
        og = P.sb([128, 2], F32)
        fw.dma("sp", gb[:], glu_b.rearrange("(c p) -> p c", p=128), writes=[tg], allow_slow_non_contiguous=True)
        fw.dma("sp", og[:], out_g.rearrange("(c p) -> p c", p=128), writes=[tg], allow_slow_non_contiguous=True)
        ones = P.sb([128, 128], BF16)
        eps = P.sb([128, 1], F32)
        fw.op("pool", lambda e: e.memset(ones[:], 1.0), writes=[tg])
        fw.op("pool", lambda e: e.memset(eps[:], RMS_EPS), writes=[tg])
        a32 = P.sb([128, 2, 512], F32)
        ab = P.sb([128, 2, 512], BF16)
        w1 = P.sb([128, 2, 512], F32)
        w2 = P.sb([128, 2, 512], F32)
        sqb = P.sb([128, 2, 512], BF16)
        rs = P.sb([128, 512], F32)
        ta, tw1, trs = Tok(), Tok(), Tok()
        stg = Rot([P.sb([128, 512], F32) for _ in range(2)])
        tout = Tok()
        C1 = math.sqrt(2.0 / math.pi)
        for (lo, hi, ic) in SEGS:
            n = hi - lo
            yv = y[:, :, lo:hi]
            fw.op("act", lambda e, yv=yv, n=n: e.activation(out=w1[:, :, :n], in_=yv, func=AF.Square), reads=[ty], writes=[tw1])
            fw.op("dve", lambda e, n=n: e.tensor_scalar(out=w1[:, :, :n], in0=w1[:, :, :n], scalar1=0.044715 * C1, scalar2=C1, op0=ALU.mult, op1=ALU.add),
                  reads=[tw1], writes=[tw1])
            fw.op("dve", lambda e, yv=yv, n=n: e.tensor_tensor(out=w1[:, :, :n], in0=w1[:, :, :n], in1=yv, op=ALU.mult), reads=[tw1, ty], writes=[tw1])
            fw.op("act", lambda e, n=n: e.activation(out=w1[:, :, :n], in_=w1[:, :, :n], func=AF.Tanh), reads=[tw1], writes=[tw1])
            fw.op("dve", lambda e, n=n: e.tensor_scalar(out=w1[:, :, :n], in0=w1[:, :, :n], scalar1=0.5, scalar2=0.5, op0=ALU.mult, op1=ALU.add),
                  reads=[tw1], writes=[tw1])
            fw.op("dve", lambda e, yv=yv, n=n: e.tensor_tensor(out=a32[:, :, :n], in0=w1[:, :, :n], in1=yv, op=ALU.mult), reads=[tw1, ty], writes=[ta])
            fw.op("act", lambda e, n=n: e.copy(out=ab[:, :, :n], in_=a32[:, :, :n]), reads=[ta], writes=[ta])
            for nt in range(2):
                ps, tps = psr.next()
                for k in range(2):
                    fw.op("pe", lambda e, ps=ps, k=k, nt=nt, n=n: e.matmul(ps[:, :n], lhsT=gw[:, k, nt * 128:(nt + 1) * 128], rhs=ab[:, k, :n],
                                                                      start=(k == 0), stop=(k == 1)), reads=[tg, ta], writes=[tps])
                fw.op("act", lambda e, ps=ps, nt=nt, n=n: e.activation(out=w2[:, nt, :n], in_=ps[:, :n], func=AF.Sigmoid, bias=gb[:, nt:nt + 1]),
                      reads=[tps, tg], writes=[tw1])
            fw.op("dve", lambda e, n=n: e.tensor_tensor(out=w2[:, :, :n], in0=w2[:, :, :n], in1=a32[:, :, :n], op=ALU.mult), reads=[tw1, ta], writes=[tw1])
            fw.op("act", lambda e, n=n: e.activation(out=sqb[:, :, :n], in_=w2[:, :, :n], func=AF.Square), reads=[tw1], writes=[tw1])
            ps, tps = psr.next()
            for k in range(2):
                fw.op("pe", lambda e, ps=ps, k=k, n=n: e.matmul(ps[:, :n], lhsT=ones[:], rhs=sqb[:, k, :n], start=(k == 0), stop=(k == 1)),
                      reads=[tw1, tg], writes=[tps])
            rstd_from_ps(fw, rs, trs, ps, tps, n, 1.0 / 256, eps[:, 0:1], tg)
            for k in range(2):
                sg, tsg = stg.next()
                fw.op("dve", lambda e, sg=sg, k=k, n=n: e.scalar_tensor_tensor(out=sg[:, :n], in0=w2[:, k, :n], scalar=og[:, k:k + 1], in1=rs[:, :n],
                                                                             op0=ALU.mult, op1=ALU.mult), reads=[tw1, trs, tg], writes=[tsg])
                fw.dma("sp", S5T[k * 128:(k + 1) * 128, lo:hi], sg[:, :n], reads=[tsg], writes=[Tok()])
        fw.barrier()


def gen_p3(fw, io):
    ZT = io("ZT", [2048, TE], F32, "in")
    VT = io("VT", [T, 128], F32, "in")
    qn_g = io("att_qn_g", [64], F32, "in")
    kn_g = io("att_kn_g", [64], F32, "in")
    og = io("att_out_g", [512], F32, "in")
    POS = io("POS", [128, NLAT], F32, "in")
    CST = io("CST", [128, 260], F32, "in")
    ATT = io("ATT_T", [512, T], F32, "out")
    with ExitStack() as es:
        P = Pool_(fw, es)
        cst = P.sb([128, 260], F32)
        tc = Tok()
        fw.dma("sp", cst[:], CST, writes=[tc])
        permb = P.sb([128, 128], BF16)
        bdb = P.sb([128, 128], BF16)
        fw.op("dve", lambda e: e.tensor_copy(out=permb[:], in_=cst[:, 1:129]), reads=[tc], writes=[tc])
        fw.op("dve", lambda e: e.tensor_copy(out=bdb[:], in_=cst[:, 129:257]), reads=[tc], writes=[tc])
        eps = P.sb([128, 2], F32)
        teps = Tok()
        fw.op("pool", lambda e: e.memset(eps[:, 0:1], RMS_EPS), writes=[teps])
        fw.op("pool", lambda e: e.memset(eps[:, 1:2], 0.0), writes=[teps])
        gq = P.sb([128, 2], F32)
        tg = Tok()
        for h in range(2):
            fw.dma("sp", gq[h * 64:(h + 1) * 64, 0:1], qn_g.rearrange("(p o) -> p o", o=1), writes=[tg])
            fw.dma("sp", gq[h * 64:(h + 1) * 64, 1:2], kn_g.rearrange("(p o) -> p o", o=1), writes=[tg])
        qb = P.sb([128, 4, T], BF16)
        kd = P.sb([128, 2, T], BF16)
        tq = Tok()
        esr = ExitStack()
        Pr = Pool_(fw, esr)
        cos = Pr.sb([128, NLAT], F32)
        sin = Pr.sb([128, NLAT], F32)
        ttab = Tok()
        with ExitStack() as es2:
            P2 = Pool_(fw, es2)
            ang = P2.sb([128, NLAT], F32)
            tmpf = P2.sb([128, NLAT], F32)
            tmpi = P2.sb([128, NLAT], I32)
            ta = Tok()
            fw.dma("sp", ang[:], POS, writes=[ta])
            fw.op("dve", lambda e: e.tensor_scalar(out=ang[:], in0=ang[:], scalar1=cst[:, 0:1], scalar2=None, op0=ALU.mult),
                  reads=[ta, tc], writes=[ta])
            for (tab, off) in ((sin, 0.0), (cos, math.pi / 2)):
                fw.op("dve", lambda e, off=off: e.tensor_scalar(out=tmpi[:], in0=ang[:], scalar1=off, scalar2=1.0 / (2 * math.pi),
                                                                op0=ALU.add, op1=ALU.mult), reads=[ta], writes=[ta])
                fw.op("dve", lambda e: e.tensor_copy(out=tmpf[:], in_=tmpi[:]), reads=[ta], writes=[ta])
                fw.op("dve", lambda e: e.scalar_tensor_tensor(out=tmpf[:], in0=tmpf[:], scalar=-2 * math.pi, in1=ang[:],
                                                              op0=ALU.mult, op1=ALU.add), reads=[ta], writes=[ta])
                fw.op("dve", lambda e, off=off: e.tensor_scalar(out=tmpf[:], in0=tmpf[:], scalar1=off, scalar2=math.pi,
                                                                op0=ALU.add, op1=ALU.min), reads=[ta], writes=[ta])
                fw.op("dve", lambda e: e.tensor_scalar(out=tmpf[:], in0=tmpf[:], scalar1=-math.pi, scalar2=None, op0=ALU.max),
                      reads=[ta], writes=[ta])
                fw.op("act", lambda e, tab=tab: e.activation(out=tab[:], in_=tmpf[:], func=AF.Sin), reads=[ta], writes=[ttab])
            fw.barrier()
        with ExitStack() as es2:
            P2 = Pool_(fw, es2)
            raw = Rot([P2.sb([128, 512], F32) for _ in range(2)])
            sqr = Rot([P2.sb([128, 512], BF16) for _ in range(2)])
            psr = Rot([P2.ps() for _ in range(2)])
            psr2 = Rot([P2.ps() for _ in range(2)])
            rsr = Rot([P2.sb([128, 512], F32) for _ in range(2)])
            nbr = Rot([P2.sb([128, 512], BF16) for _ in range(2)])
            t1r = Rot([P2.sb([128, 512], F32) for _ in range(2)])
            t2r = Rot([P2.sb([128, 512], F32) for _ in range(2)])
            items = [("q", j) for j in range(4)] + [("k", g) for g in range(2)]
            for (kind, j) in items:
                for (lo, hi, ic) in SEGS:
                    n = hi - lo
                    rw, trw = raw.next()
                    if kind == "q":
                        fw.dma("sp", rw[:, :n], ZT[256 + j * 128:256 + (j + 1) * 128, lo:hi], writes=[trw])
                        gcol = 0
                        dst = qb[:, j, lo:hi]
                    else:
                        for h in range(2):
                            fw.dma("sp", rw[h * 64:(h + 1) * 64, :n], ZT[768 + j * 64:768 + (j + 1) * 64, lo:hi], writes=[trw])
                        gcol = 1
                        dst = kd[:, j, lo:hi]
                    sq, tsq = sqr.next()
                    fw.op("act", lambda e, sq=sq, rw=rw, n=n: e.activation(out=sq[:, :n], in_=rw[:, :n], func=AF.Square), reads=[trw], writes=[tsq])
                    ps, tps = psr.next()
                    fw.op("pe", lambda e, ps=ps, sq=sq, n=n: e.matmul(ps[:, :n], lhsT=bdb[:], rhs=sq[:, :n], start=True, stop=True),
                          reads=[tsq, tc], writes=[tps])
                    rs, trs = rsr.next()
                    rstd_from_ps(fw, rs, trs, ps, tps, n, 1.0, eps[:, 0:1], teps)
                    t1, tt1 = t1r.next()
                    fw.op("dve", lambda e, t1=t1, rw=rw, rs=rs, n=n, gcol=gcol: e.scalar_tensor_tensor(
                        out=t1[:, :n], in0=rw[:, :n], scalar=gq[:, gcol:gcol + 1], in1=rs[:, :n], op0=ALU.mult, op1=ALU.mult),
                        reads=[trw, trs, tg], writes=[tt1])
                    if ic:
                        fw.op("act", lambda e, dst=dst, t1=t1, n=n: e.copy(out=dst, in_=t1[:, :n]), reads=[tt1], writes=[tq])
                        continue
                    nb, tnb = nbr.next()
                    fw.op("act", lambda e, nb=nb, t1=t1, n=n: e.copy(out=nb[:, :n], in_=t1[:, :n]), reads=[tt1], writes=[tnb])
                    ps2, tps2 = psr2.next()
                    fw.op("pe", lambda e, ps2=ps2, nb=nb, n=n: e.matmul(ps2[:, :n], lhsT=permb[:], rhs=nb[:, :n], start=True, stop=True),
                          reads=[tnb, tc], writes=[tps2])
                    p0 = lo - NCTX
                    t2, tt2 = t2r.next()
                    fw.op("dve", lambda e, t2=t2, ps2=ps2, n=n, p0=p0: e.tensor_tensor(out=t2[:, :n], in0=ps2[:, :n], in1=sin[:, p0:p0 + n], op=ALU.mult),
                          reads=[tps2, ttab], writes=[tt2])
                    fw.op("pool", lambda e, t1=t1, nb=nb, n=n, p0=p0: e.tensor_tensor(out=t1[:, :n], in0=nb[:, :n], in1=cos[:, p0:p0 + n], op=ALU.mult),
                          reads=[tnb, ttab, tt1], writes=[tt1])
                    fw.op("pool", lambda e, dst=dst, t1=t1, t2=t2, n=n: e.tensor_tensor(out=dst, in0=t1[:, :n], in1=t2[:, :n], op=ALU.add),
                          reads=[tt1, tt2], writes=[tq])
            fw.barrier()
        esr.close()
        va = P.sb([128, 18, 2, 128], BF16)
        tva = Tok()
        fw.op("pool", lambda e: e.memset(va[:], 1.0), writes=[tva])
        for g in range(2):
            fw.dma("pool", va[:, :, g, 0:64], VT.rearrange("(t p) c -> p t c", p=128)[:, :, g * 64:(g + 1) * 64], writes=[tva])
        att = P.sb([128, 4, T], BF16)
        tatt = Tok()
        pss = Rot([P.ps() for _ in range(2)])
        pso = Rot([P.ps() for _ in range(3)])
        pend = []
        yield "core"
        ptr = Rot([P.sb([128, 512], BF16) for _ in range(3)])
        rcr = Rot([P.sb([64, 512], F32) for _ in range(2)])
        jobs = [(0, 256, 0, 2)] + [(256 + 512 * i, 768 + 512 * i, 0, 18) for i in range(4)]
        for h in range(8):
            g = h // 4
            jt, r0 = h // 2, (h % 2) * 64
            for (qlo, qhi, k0, k1) in jobs:
                n = qhi - qlo
                po, tpo = pso.next()
                for kt in range(k0, k1):
                    ps, tps = pss.next()
                    fw.op("pe", lambda e, ps=ps, kt=kt, g=g, jt=jt, r0=r0, qlo=qlo, qhi=qhi, n=n: e.matmul(
                        ps[:, :n], lhsT=kd[r0:r0 + 64, g, kt * 128:(kt + 1) * 128], rhs=qb[r0:r0 + 64, jt, qlo:qhi],
                        start=True, stop=True), reads=[tq], writes=[tps])
                    pt, tpt = ptr.next()
                    fw.op("act", lambda e, pt=pt, ps=ps, n=n: e.activation(out=pt[:, :n], in_=ps[:, :n], func=AF.Exp, scale=0.125),
                          reads=[tps], writes=[tpt])
                    fw.op("pe", lambda e, po=po, pt=pt, kt=kt, g=g, n=n, k0=k0, k1=k1: e.matmul(
                        po[:, :n], lhsT=va[:, kt, g, :], rhs=pt[:, :n], start=(kt == k0), stop=(kt == k1 - 1)),
                        reads=[tpt, tva], writes=[tpo])
                def norm_job(po=po, tpo=tpo, n=n, jt=jt, r0=r0, qlo=qlo, qhi=qhi):
                    rc, trc = rcr.next()
                    fw.op("dve", lambda e: e.reciprocal(out=rc[:, :n], in_=po[64:128, :n]), reads=[tpo], writes=[trc])
                    fw.op("dve", lambda e: e.tensor_tensor(out=att[r0:r0 + 64, jt, qlo:qhi], in0=po[0:64, :n], in1=rc[:, :n], op=ALU.mult),
                          reads=[tpo, trc], writes=[tatt])
                pend.append(norm_job)
                if len(pend) > 2:
                    pend.pop(0)()
                yield
        for nj in pend:
            nj()
        yield "fin"
        ogs = P.sb([128, 4], F32)
        tog = Tok()
        fw.dma("sp", ogs[:], og.rearrange("(k p) -> p k", p=128), writes=[tog], allow_slow_non_contiguous=True)
        ones = P.sb([128, 128], BF16)
        fw.op("pool", lambda e: e.memset(ones[:], 1.0), writes=[tog])
        sq4 = P.sb([128, 4, 512], BF16)
        tsq4 = Tok()
        rs = P.sb([128, 512], F32)
        trs = Tok()
        stg = Rot([P.sb([128, 512], F32) for _ in range(3)])
        tout = Tok()
        for (lo, hi, ic) in SEGS:
            n = hi - lo
            fw.op("act", lambda e, lo=lo, hi=hi, n=n: e.activation(out=sq4[:, :, :n], in_=att[:, :, lo:hi], func=AF.Square), reads=[tatt], writes=[tsq4])
            ps, tps = pss.next()
            for k in range(4):
                fw.op("pe", lambda e, ps=ps, k=k, n=n: e.matmul(ps[:, :n], lhsT=ones[:], rhs=sq4[:, k, :n], start=(k == 0), stop=(k == 3)),
                      reads=[tsq4, tog], writes=[tps])
            rstd_from_ps(fw, rs, trs, ps, tps, n, 1.0 / 512, eps[:, 0:1], teps)
            for k in range(4):
                sg, tsg = stg.next()
                fw.op("dve", lambda e, sg=sg, k=k, lo=lo, hi=hi, n=n: e.scalar_tensor_tensor(
                    out=sg[:, :n], in0=att[:, k, lo:hi], scalar=ogs[:, k:k + 1], in1=rs[:, :n], op0=ALU.mult, op1=ALU.mult),
                    reads=[tatt, trs, tog], writes=[tsg])
                fw.dma("sp", ATT[k * 128:(k + 1) * 128, lo:hi], sg[:, :n], reads=[tsg], writes=[Tok()])
        fw.barrier()


def stage_p23(fw, io):
    g2 = gen_p2(fw, io)
    g3 = gen_p3(fw, io)

    def until(g, tag):
        for v in g:
            if v == tag:
                return True
        return False
    until(g2, "core")
    until(g3, "core")
    a2 = a3 = True
    R = 36
    while a2 or a3:
        if a2:
            for _ in range(R):
                if next(g2) == "fin":
                    a2 = False
                    break
        if a3:
            if next(g3) == "fin":
                a3 = False
    for _ in g3:
        pass
    for _ in g2:
        pass


RW_BASE = 1024
CHK = 64
NCH = T // CHK
LN_EPS_RW = 64e-5


def rw_consts():
    idx = np.arange(64)
    m = np.zeros((64, 4, 64), np.float32)
    m[:, 0, :] = (idx[:, None] < idx[None, :])
    m[:, 1, :] = (idx[:, None] > idx[None, :])
    m[:, 2, :] = (idx[:, None] <= idx[None, :])
    m[:, 3, :] = (idx[:, None] >= idx[None, :])
    bd = np.zeros((128, 128), np.float32)
    bd[:64, :64] = 1.0
    bd[64:, 64:] = 1.0
    return m, bd


def stage_p4(fw, io):
    ZT = io("ZT", [2048, TE], F32, "in")
    mu = io("rw_mu", [960], F32, "in")
    w0 = io("rw_w0", [2, 256], F32, "in")
    w2 = io("rw_w2", [2, 32, 256], F32, "in")
    a0 = io("rw_a0", [2, 256], F32, "in")
    a2 = io("rw_a2", [2, 32, 256], F32, "in")
    g2 = io("rw_g2", [64, 256], F32, "in")
    k_k = io("rw_k_k", [256], F32, "in")
    k_a = io("rw_k_a", [256], F32, "in")
    r_k = io("rw_r_k", [256], F32, "in")
    ln_g = io("rw_ln_g", [256], F32, "in")
    ln_b = io("rw_ln_b", [256], F32, "in")
    ident = io("ident", [128, 128], F32, "in")
    MASKS = io("RWMASK", [64, 4, 64], F32, "in")
    BD = io("RWBD", [128, 128], F32, "in")
    RWT = io("RWT", [256, T], F32, "out")
    N = T
    CH5 = [(0, 512), (512, 1024), (1024, 1536), (1536, 2048), (2048, 2560)]
    with ExitStack() as es:
        P = Pool_(fw, es)
        tc = Tok()
        idt = P.sb([128, 128], F32)
        idb = P.sb([128, 128], BF16)
        msk = P.sb([64, 4, 64], F32)
        bd1 = P.sb([128, 128], F32)
        fw.dma("sp", idt[:], ident, writes=[tc])
        fw.dma("sp", msk[:], MASKS, writes=[tc])
        fw.dma("sp", bd1[:], BD, writes=[tc])
        fw.op("dve", lambda e: e.tensor_copy(out=idb[:], in_=idt[:]), reads=[tc], writes=[tc])
        mrep = P.sb([64, 4, 4, 64], F32)
        for rep in range(4):
            fw.op("dve", lambda e, rep=rep: e.tensor_copy(out=mrep[:, :, rep, :], in_=msk[:]), reads=[tc], writes=[tc])
        pp = P.sb([128, 12, 2], F32)
        tpp = Tok()
        srcs = [w0[0], w0[1], a0[0], a0[1], k_k, k_a, k_a, r_k, ln_g, ln_b]
        for i, sap in enumerate(srcs):
            fw.dma("sp", pp[:, i, :], sap.rearrange("(c p) -> p c", p=128), writes=[tpp], allow_slow_non_contiguous=True)
        fw.op("dve", lambda e: e.tensor_scalar(out=pp[:, 6, :], in0=pp[:, 6, :], scalar1=-1.0, scalar2=1.0, op0=ALU.mult, op1=ALU.add),
              reads=[tpp], writes=[tpp])
        epsl = P.sb([128, 2], F32)
        fw.op("pool", lambda e: e.memset(epsl[:, 0:1], LN_EPS_RW), writes=[tpp])
        fw.op("pool", lambda e: e.memset(epsl[:, 1:2], 1e-24), writes=[tpp])
        wA = P.sb([128, 256], BF16)
        wB = P.sb([128, 256], BF16)
        tlw = Tok()
        fw.dma("pool", wA[0:32, :], w2[0], writes=[tlw])
        fw.dma("pool", wA[32:64, :], w2[1], writes=[tlw])
        fw.dma("pool", wA[64:96, :], a2[0], writes=[tlw])
        fw.dma("pool", wB[0:32, :], a2[1], writes=[tlw])
        fw.dma("pool", wB[64:128, :], g2, writes=[tlw])
        smask = P.sb([128, N], BF16)
        tsm = Tok()
        fw.op("pool", lambda e: e.memset(smask[:], 1.0), writes=[tsm])
        fw.op("pool", lambda e: e.memset(smask[:].rearrange("p (c j) -> p c j", j=CHK)[:, :, 0], 0.0), writes=[tsm])

        def load_shift(dst, tdst, row0, rows, P2, post=None, pb=0):
            zt_ = P2.sb([128, TE], F32)
            nbt_ = P2.sb([128, TE], F32)
            mtt_ = P2.sb([128, 2], F32)
            tz, tnb, tm = Tok(), Tok(), Tok()
            ps_ = slice(pb, pb + rows)
            z = zt_[ps_, :]
            fw.dma("sp", z, ZT[RW_BASE + row0:RW_BASE + row0 + rows, :], writes=[tz])
            fw.dma("sp", mtt_[ps_, 0:1], mu[row0:row0 + rows].rearrange("(p o) -> p o", o=1), writes=[tm])
            fw.op("dve", lambda e: e.tensor_scalar(out=mtt_[ps_, 1:2], in0=mtt_[ps_, 0:1], scalar1=0.5, scalar2=None, op0=ALU.mult), reads=[tm], writes=[tm])
            fw.op("dve", lambda e: e.tensor_scalar(out=mtt_[ps_, 0:1], in0=mtt_[ps_, 0:1], scalar1=-1.0, scalar2=1.0, op0=ALU.mult, op1=ALU.add),
                  reads=[tm], writes=[tm])
            fw.op("pool", lambda e: e.memset(nbt_[ps_, 0:1], 0.0), writes=[tnb])
            fw.op("pool", lambda e: e.tensor_copy(out=nbt_[ps_, 1:TE], in_=zt_[ps_, 0:TE - 1]), reads=[tz], writes=[tnb])
            fw.op("pool", lambda e: e.tensor_tensor(out=nbt_[ps_, 0:TE - 1], in0=nbt_[ps_, 0:TE - 1], in1=zt_[ps_, 1:TE], op=ALU.add),
                  reads=[tz, tnb], writes=[tnb])
            for cb in (256, 2304):
                fw.op("pool", lambda e, cb=cb: e.tensor_tensor(out=nbt_[ps_, cb:cb + 1], in0=nbt_[ps_, cb:cb + 1], in1=zt_[ps_, cb - 1:cb], op=ALU.subtract),
                      reads=[tz, tnb], writes=[tnb])
                fw.op("pool", lambda e, cb=cb: e.tensor_tensor(out=nbt_[ps_, cb - 1:cb], in0=nbt_[ps_, cb - 1:cb], in1=zt_[ps_, cb:cb + 1], op=ALU.subtract),
                      reads=[tz, tnb], writes=[tnb])
            fw.op("act", lambda e: e.activation(out=z, in_=z, func=AF.Identity, scale=mtt_[ps_, 0:1]), reads=[tz, tm], writes=[tz])
            if post is None:
                fw.op("dve", lambda e: e.scalar_tensor_tensor(out=dst, in0=nbt_[ps_, :], scalar=mtt_[ps_, 1:2], in1=z, op0=ALU.mult, op1=ALU.add),
                      reads=[tz, tnb, tm], writes=[tdst])
            else:
                fw.op("dve", lambda e: e.scalar_tensor_tensor(out=z, in0=nbt_[ps_, :], scalar=mtt_[ps_, 1:2], in1=z, op0=ALU.mult, op1=ALU.add),
                      reads=[tz, tnb, tm], writes=[tz])
                fw.op("act", lambda e: e.activation(out=dst, in_=z, func=post), reads=[tz], writes=[tdst])

        lorA = P.sb([128, TE], BF16)
        lorB = P.sb([128, TE], BF16)
        tlor = Tok()
        for i in range(4):
            with ExitStack() as es2:
                dstt = lorA[32 * i:32 * i + 32, :] if i < 3 else lorB[0:32, :]
                load_shift(dstt, tlor, 768 + 32 * i, 32, Pool_(fw, es2), post=(AF.Tanh if i < 2 else AF.Copy), pb=(32 * i if i < 3 else 0))
                fw.barrier()
        with ExitStack() as es2:
            load_shift(lorB[64:128, :], tlor, 896, 64, Pool_(fw, es2), post=AF.Sigmoid, pb=64)
            fw.barrier()

        for c in range(2):
            with ExitStack() as esc:
                Pc = Pool_(fw, esc)
                rr = Pc.sb([128, TE], F32)
                kx = Pc.sb([128, TE], F32)
                vv = Pc.sb([128, TE], F32)
                kk = Pc.sb([128, TE], F32)
                trr, tkx, tvv, tkk = Tok(), Tok(), Tok(), Tok()
                vtok = Pc.sb([64, TE // CHK, 128], BF16)
                tvt = Tok()
                yacc = Pc.sb([128, T], F32)
                bon = Pc.sb([128, T], F32)
                tya, tbon = Tok(), Tok()
                fw.op("pool", lambda e: e.memset(yacc[:], 0.0), writes=[tya])
                fw.op("pool", lambda e: e.memset(bon[:], 0.0), writes=[tbon])
                for (dst, tdst, r0) in ((rr, trr, 0), (kx, tkx, 256), (vv, tvv, 512)):
                    with ExitStack() as es2:
                        load_shift(dst[:], tdst, r0 + c * 128, 128, Pool_(fw, es2))
                        fw.barrier()
                with ExitStack() as es2:
                    P2 = Pool_(fw, es2)
                    sq = P2.sb([128, 512], F32)
                    rn = P2.sb([128, 512], F32)
                    tsq, trn = Tok(), Tok()
                    ps1 = P2.ps()
                    tps1 = Tok()
                    fw.op("dve", lambda e: e.tensor_scalar(out=kk[:], in0=kx[:], scalar1=pp[:, 4, c:c + 1], scalar2=None, op0=ALU.mult),
                          reads=[tkx, tpp], writes=[tkk])
                    for (c0, c1) in CH5:
                        fw.op("act", lambda e, c0=c0, c1=c1: e.activation(out=sq[:], in_=kk[:, c0:c1], func=AF.Square), reads=[tkk], writes=[tsq])
                        fw.op("pe", lambda e: e.matmul(ps1[:], lhsT=bd1[:], rhs=sq[:], start=True, stop=True), reads=[tsq, tc], writes=[tps1])
                        rstd_from_ps(fw, rn, trn, ps1, tps1, 512, 1.0, epsl[:, 1:2], tpp)
                        fw.op("dve", lambda e, c0=c0, c1=c1: e.tensor_tensor(out=kk[:, c0:c1], in0=kk[:, c0:c1], in1=rn[:], op=ALU.mult),
                              reads=[tkk, trn], writes=[tkk])
                    vb = P2.sb([128, TE], BF16)
                    tvb = Tok()
                    fw.op("act", lambda e: e.copy(out=vb[:], in_=vv[:]), reads=[tvv], writes=[tvb])
                    pst = P2.ps([128, 1024], BF16)
                    tpst = Tok()
                    for q in range(TE // CHK // 4):
                        for j in range(4):
                            ch = q * 4 + j
                            fw.op("pe", lambda e, j=j, ch=ch: e.transpose(pst[0:64, j * 128:(j + 1) * 128], vb[:, ch * CHK:(ch + 1) * CHK], idb[:]),
                                  reads=[tvb, tc], writes=[tpst])
                        fw.op("dve", lambda e, q=q: e.tensor_copy(out=vtok[:, q * 4:(q + 1) * 4, :], in_=pst[0:64, 0:512].rearrange("p (j c) -> p j c", j=4)),
                              reads=[tpst], writes=[tvt])
                    fw.barrier()

                for di in range(2):
                    off = 0 if di == 0 else 256
                    with ExitStack() as esd:
                        Pd = Pool_(fw, esd)
                        aT = Pd.sb([128, N], BF16)
                        bT = Pd.sb([128, N], BF16)
                        kT = Pd.sb([128, N], BF16)
                        rT = Pd.sb([128, N], BF16)
                        btok = Pd.sb([64, NCH, 128], BF16)
                        ktok = Pd.sb([64, NCH, 128], BF16)
                        pC = Pd.sb([128, NCH], F32)
                        tops = Tok()
                        ttok = Tok()
                        with ExitStack() as es2:
                            P2 = Pool_(fw, es2)
                            ld = P2.sb([128, TE], F32)
                            kd = P2.sb([128, TE], F32)
                            bb = P2.sb([128, TE], F32)
                            tld, tkd, tbb = Tok(), Tok(), Tok()
                            psr = Rot([P2.ps() for _ in range(3)])
                            tm5 = Rot([P2.sb([128, 512], F32) for _ in range(2)])
                            for (c0, c1) in CH5:
                                ps, tps = psr.next()
                                fw.op("pe", lambda e, ps=ps, c0=c0, c1=c1: e.matmul(ps[:], lhsT=wA[32 * di:32 * di + 32, c * 128:(c + 1) * 128], rhs=lorA[32 * di:32 * di + 32, c0:c1],
                                                                                  start=True, stop=True), reads=[tlw, tlor], writes=[tps])
                                fw.op("act", lambda e, ps=ps, c0=c0, c1=c1: e.activation(out=ld[:, c0:c1], in_=ps[:], func=AF.Sigmoid, bias=pp[:, di, c:c + 1]),
                                      reads=[tps, tpp], writes=[tld])
                                ps, tps = psr.next()
                                fw.op("pe", lambda e, ps=ps, c0=c0, c1=c1: e.matmul(ps[:], lhsT=(wA[64:96, c * 128:(c + 1) * 128] if di == 0 else wB[0:32, c * 128:(c + 1) * 128]),
                                                                                  rhs=(lorA[64:96, c0:c1] if di == 0 else lorB[0:32, c0:c1]),
                                                                                  start=True, stop=True), reads=[tlw, tlor], writes=[tps])
                                fw.op("act", lambda e, ps=ps, c0=c0, c1=c1: e.activation(out=bb[:, c0:c1], in_=ps[:], func=AF.Sigmoid, bias=pp[:, 2 + di, c:c + 1]),
                                      reads=[tps, tpp], writes=[tbb])
                            fw.op("pool", lambda e: e.tensor_scalar(out=ld[:], in0=ld[:], scalar1=-math.exp(-0.5), scalar2=None, op0=ALU.mult), reads=[tld], writes=[tld])
                            fw.op("act", lambda e: e.activation(out=kd[:], in_=bb[:], func=AF.Identity, scale=pp[:, 5, c:c + 1], bias=pp[:, 6, c:c + 1]),
                                  reads=[tbb, tpp], writes=[tkd])
                            fw.op("dve", lambda e: e.tensor_tensor(out=kd[:], in0=kd[:], in1=kx[:], op=ALU.mult), reads=[tkd, tkx], writes=[tkd])
                            fw.op("pool", lambda e: e.tensor_tensor(out=bb[:], in0=bb[:], in1=kk[:], op=ALU.mult), reads=[tbb, tkk], writes=[tbb])
                            for (c0, c1) in CH5:
                                c1 = min(c1, T)
                                n = c1 - c0
                                tm, ttm = tm5.next()
                                fw.op("dve", lambda e, tm=tm, c0=c0, c1=c1, n=n: e.scalar_tensor_tensor(out=tm[:, :n], in0=rr[:, c0:c1], scalar=pp[:, 7, c:c + 1], in1=kd[:, c0:c1],
                                                                                                 op0=ALU.mult, op1=ALU.mult), reads=[trr, tkd, tpp], writes=[ttm])
                                ps, tps = psr.next()
                                fw.op("pe", lambda e, ps=ps, tm=tm, n=n: e.matmul(ps[:, :n], lhsT=bd1[:], rhs=tm[:, :n], start=True, stop=True), reads=[ttm, tc], writes=[tps])
                                tm2, ttm2 = tm5.next()
                                fw.op("dve", lambda e, tm2=tm2, ps=ps, c0=c0, c1=c1, n=n: e.tensor_tensor(out=tm2[:, :n], in0=ps[:, :n], in1=vv[:, c0:c1], op=ALU.mult),
                                      reads=[tps, tvv], writes=[ttm2])
                                fw.op("pool", lambda e, tm2=tm2, c0=c0, c1=c1, n=n: e.tensor_tensor(out=bon[:, c0:c1], in0=bon[:, c0:c1], in1=tm2[:, :n], op=ALU.add),
                                      reads=[ttm2, tbon], writes=[tbon])
                            cs = P2.sb([128, N], F32)
                            ex = P2.sb([128, N], F32)
                            tcs, tex = Tok(), Tok()
                            ldw = ld[:, off:off + N]
                            fw.op("dve", lambda e: e.tensor_tensor_scan(out=cs[:], data0=smask[:], data1=ldw, initial=0.0, op0=ALU.mult, op1=ALU.add),
                                  reads=[tsm, tld], writes=[tcs])
                            csv = cs[:].rearrange("p (c j) -> p c j", j=CHK)
                            tot = P2.sb([128, NCH, 1], F32)
                            ttot = Tok()
                            fw.op("dve", lambda e: e.tensor_copy(out=tot[:], in_=csv[:, :, CHK - 1:CHK]), reads=[tcs], writes=[ttot])
                            totb = tot[:].to_broadcast([128, NCH, CHK])
                            if di == 1:
                                fw.op("dve", lambda e: e.tensor_tensor(out=csv, in0=totb, in1=csv, op=ALU.subtract), reads=[ttot, tcs], writes=[tcs])
                                fw.op("dve", lambda e: e.tensor_tensor(out=cs[:], in0=cs[:], in1=ldw, op=ALU.add), reads=[tcs, tld], writes=[tcs])
                            fw.op("act", lambda e: e.activation(out=pC[:], in_=tot[:, :, 0], func=AF.Exp), reads=[ttot], writes=[tops])
                            kdw, bbw = kd[:, off:off + N], bb[:, off:off + N]
                            rrw, kkw = rr[:, off:off + N], kk[:, off:off + N]
                            fw.op("act", lambda e: e.activation(out=ex[:], in_=cs[:], func=AF.Exp), reads=[tcs], writes=[tex])
                            fw.op("dve", lambda e: e.tensor_tensor(out=rT[:], in0=rrw, in1=ex[:], op=ALU.mult), reads=[trr, tex], writes=[tops])
                            fw.op("act", lambda e: e.activation(out=ex[:], in_=cs[:], func=AF.Exp, scale=-1.0), reads=[tcs, tops], writes=[tex])
                            fw.op("dve", lambda e: e.tensor_tensor(out=bT[:], in0=bbw, in1=ex[:], op=ALU.mult), reads=[tbb, tex], writes=[tops])
                            fw.op("pool", lambda e: e.tensor_tensor(out=kT[:], in0=kdw, in1=ex[:], op=ALU.mult), reads=[tkd, tex], writes=[tops])
                            e3 = ex
                            te3 = tex
                            fw.op("dve", lambda e: e.tensor_tensor(out=e3[:], in0=cs[:], in1=ldw, op=ALU.subtract), reads=[tcs, tld], writes=[te3])
                            fw.op("act", lambda e: e.activation(out=e3[:], in_=e3[:], func=AF.Exp), reads=[te3], writes=[te3])
                            fw.op("dve", lambda e: e.scalar_tensor_tensor(out=aT[:], in0=kkw, scalar=-1.0, in1=e3[:], op0=ALU.mult, op1=ALU.mult),
                                  reads=[tkk, te3], writes=[tops])
                            e3v = e3[:].rearrange("p (c j) -> p c j", j=CHK)
                            fw.op("dve", lambda e: e.tensor_tensor(out=e3v, in0=totb, in1=csv, op=ALU.subtract), reads=[ttot, tcs, tops, te3], writes=[te3])
                            fw.op("act", lambda e: e.activation(out=e3[:], in_=e3[:], func=AF.Exp), reads=[te3], writes=[te3])
                            bh = P2.sb([128, N], BF16)
                            kh = P2.sb([128, N], BF16)
                            tbh = Tok()
                            fw.op("dve", lambda e: e.tensor_tensor(out=bh[:], in0=bbw, in1=e3[:], op=ALU.mult), reads=[tbb, te3], writes=[tbh])
                            fw.op("pool", lambda e: e.tensor_tensor(out=kh[:], in0=kdw, in1=e3[:], op=ALU.mult), reads=[tkd, te3], writes=[tbh])
                            pst = P2.ps([128, 1024], BF16)
                            tpst = Tok()
                            for (src, dstt) in ((bh, btok), (kh, ktok)):
                                for q in range(NCH // 4):
                                    for j in range(4):
                                        ch = q * 4 + j
                                        fw.op("pe", lambda e, j=j, ch=ch, src=src: e.transpose(pst[0:64, j * 128:(j + 1) * 128], src[:, ch * CHK:(ch + 1) * CHK], idb[:]),
                                              reads=[tbh, tc], writes=[tpst])
                                    fw.op("act", lambda e, q=q, dstt=dstt: e.copy(out=dstt[:, q * 4:(q + 1) * 4, :], in_=pst[0:64, 0:512].rearrange("p (j c) -> p j c", j=4)),
                                          reads=[tpst], writes=[ttok])
                            fw.barrier()
                        with ExitStack() as es3:
                            P3 = Pool_(fw, es3)
                            Hs = P3.sb([128, 64], F32)
                            Hb = P3.sb([128, 64], BF16)
                            tH = Tok()
                            fw.op("pool", lambda e: e.memset(Hs[:], 0.0), writes=[tH])
                            fw.op("pool", lambda e: e.memset(Hb[:], 0.0), writes=[tH])
                            psA = Rot([P3.ps([64, 512]) for _ in range(1)])
                            psB = Rot([P3.ps([64, 512]) for _ in range(1)])
                            psI = Rot([P3.ps([64, 512]) for _ in range(1)])
                            psT = Rot([P3.ps([64, 512]) for _ in range(1)])
                            psG = Rot([P3.ps([64, 512]) for _ in range(1)])
                            psH = Rot([P3.ps([128, 512]) for _ in range(1)])
                            psY = Rot([P3.ps([128, 512]) for _ in range(1)])
                            g1b = Rot([P3.sb([64, 2, 64], BF16) for _ in range(2)])
                            nl = Rot([P3.sb([64, 2, 2, 64], F32) for _ in range(2)])
                            nl2 = Rot([P3.sb([64, 2, 2, 64], F32) for _ in range(2)])
                            g2b_ = Rot([P3.sb([64, 4, 64], BF16) for _ in range(2)])
                            Pm = Rot([P3.sb([64, 2, 64], F32) for _ in range(2)])
                            Gs = Rot([P3.sb([64, 128], F32) for _ in range(2)])
                            Ub = Rot([P3.sb([64, 128], BF16) for _ in range(2)])
                            Ys = Rot([P3.sb([64, 128], F32) for _ in range(2)])
                            ms, ml, mi = (0, 1, 2) if di == 0 else (1, 0, 3)
                            order = list(range(NCH)) if di == 0 else list(range(NCH - 1, -1, -1))
                            res = {}

                            def par_gen(i):
                                cc0, cc1 = i * CHK, (i + 1) * CHK
                                pa, tpa = psA.next()
                                pb, tpb = psB.next()
                                for h in range(2):
                                    hp = slice(h * 64, (h + 1) * 64)
                                    for (dst, lt, rt_) in ((pa[:, h * 64:(h + 1) * 64], kT, aT), (pa[:, 128 + h * 64:128 + (h + 1) * 64], bT, aT),
                                                           (pa[:, 256 + h * 64:256 + (h + 1) * 64], aT, bT)):
                                        fw.op("pe", lambda e, dst=dst, lt=lt, rt_=rt_, hp=hp: e.matmul(dst, lhsT=lt[hp, cc0:cc1], rhs=rt_[hp, cc0:cc1], start=True, stop=True),
                                              reads=[tops], writes=[tpa])
                                        yield
                                    for (dst, lt, rt_) in ((pb[:, h * 64:(h + 1) * 64], bT, rT), (pb[:, 128 + h * 64:128 + (h + 1) * 64], kT, rT)):
                                        fw.op("pe", lambda e, dst=dst, lt=lt, rt_=rt_, hp=hp: e.matmul(dst, lhsT=lt[hp, cc0:cc1], rhs=rt_[hp, cc0:cc1], start=True, stop=True),
                                              reads=[tops], writes=[tpb])
                                        yield
                                a1, ta1 = g1b.next()
                                nlt, tnl = nl.next()
                                a45, ta45 = g2b_.next()
                                fw.op("dve", lambda e: e.tensor_tensor(out=a1[:], in0=pa[:, 0:128].rearrange("p (h t) -> p h t", h=2), in1=mrep[:, ms, 0:2, :], op=ALU.mult),
                                      reads=[tpa, tc], writes=[ta1])
                                yield
                                fw.op("dve", lambda e: e.tensor_tensor(out=nlt[:, 0], in0=pa[:, 128:256].rearrange("p (h t) -> p h t", h=2), in1=mrep[:, ms, 0:2, :], op=ALU.mult),
                                      reads=[tpa, tc], writes=[tnl])
                                yield
                                fw.op("dve", lambda e: e.tensor_tensor(out=nlt[:, 1], in0=pa[:, 256:384].rearrange("p (h t) -> p h t", h=2), in1=mrep[:, ml, 0:2, :], op=ALU.mult),
                                      reads=[tpa, tc], writes=[tnl])
                                yield
                                fw.op("dve", lambda e: e.tensor_tensor(out=a45[:], in0=pb[:, 0:256].rearrange("p (h t) -> p h t", h=4), in1=mrep[:, mi, :, :], op=ALU.mult),
                                      reads=[tpb, tc], writes=[ta45])
                                yield
                                pm, tpm = Pm.next()
                                fw.op("dve", lambda e: e.tensor_tensor(out=pm[:], in0=nlt[:, 0], in1=idt[0:64, 0:64].unsqueeze(1).to_broadcast([64, 2, 64]), op=ALU.add),
                                      reads=[tnl, tc], writes=[tpm])
                                yield
                                cur, tcur = nlt, tnl
                                for lev in range(5):
                                    pi_, tpi = psI.next()
                                    for h in range(2):
                                        fw.op("pe", lambda e, pi_=pi_, h=h, cur=cur: e.matmul(pi_[:, h * 64:(h + 1) * 64], lhsT=cur[:, 0, h, :], rhs=cur[:, 1, h, :], start=True, stop=True),
                                              reads=[tcur], writes=[tpi])
                                        yield
                                    nxt, tnxt = (nl2.next() if lev % 2 == 0 else nl.next())
                                    fw.op("act", lambda e, nxt=nxt, pi_=pi_: e.copy(out=nxt[:, 1], in_=pi_[:, 0:128].rearrange("p (h t) -> p h t", h=2)), reads=[tpi], writes=[tnxt])
                                    yield
                                    for h in range(2):
                                        fw.op("pe", lambda e, pi_=pi_, h=h, nxt=nxt: e.matmul(pi_[:, 256 + h * 64:256 + (h + 1) * 64], lhsT=nxt[:, 1, h, :], rhs=pm[:, h, :], start=True, stop=True),
                                              reads=[tnxt, tpm], writes=[tpi])
                                        yield
                                    if lev < 4:
                                        pt_, tpt = psT.next()
                                        for h in range(2):
                                            fw.op("pe", lambda e, pt_=pt_, h=h, nxt=nxt: e.transpose(pt_[:, h * 64:(h + 1) * 64], nxt[:, 1, h, :], idt[0:64, 0:64]),
                                                  reads=[tnxt, tc], writes=[tpt])
                                            yield
                                    fw.op("dve", lambda e, pi_=pi_: e.tensor_tensor(out=pm[:], in0=pm[:], in1=pi_[:, 256:384].rearrange("p (h t) -> p h t", h=2), op=ALU.add),
                                          reads=[tpi, tpm], writes=[tpm])
                                    yield
                                    if lev < 4:
                                        fw.op("act", lambda e, nxt=nxt, pt_=pt_: e.copy(out=nxt[:, 0], in_=pt_[:, 0:128].rearrange("p (h t) -> p h t", h=2)), reads=[tpt], writes=[tnxt])
                                        yield
                                    cur, tcur = nxt, tnxt
                                res[i] = (a1, ta1, a45, ta45, pm, tpm)

                            def chain_gen(i):
                                cc0, cc1 = i * CHK, (i + 1) * CHK
                                gch = i + off // CHK
                                a1, ta1, a45, ta45, pm, tpm = res.pop(i)
                                pg, tpg = psG.next()
                                for h in range(2):
                                    hp = slice(h * 64, (h + 1) * 64)
                                    fw.op("pe", lambda e, h=h, hp=hp: e.matmul(pg[:, h * 64:(h + 1) * 64], lhsT=aT[hp, cc0:cc1], rhs=Hb[hp, :], start=True, stop=False),
                                          reads=[tops, tH], writes=[tpg])
                                    yield
                                    fw.op("pe", lambda e, h=h, hp=hp: e.matmul(pg[:, h * 64:(h + 1) * 64], lhsT=a1[:, h, :], rhs=vtok[:, gch, hp], start=False, stop=True),
                                          reads=[ta1, tvt], writes=[tpg])
                                    yield
                                gs, tgs = Gs.next()
                                fw.op("act", lambda e: e.copy(out=gs[:], in_=pg[:, 0:128]), reads=[tpg], writes=[tgs])
                                yield
                                for h in range(2):
                                    fw.op("pe", lambda e, h=h: e.matmul(pg[:, 128 + h * 64:128 + (h + 1) * 64], lhsT=pm[:, h, :], rhs=gs[:, h * 64:(h + 1) * 64], start=True, stop=True),
                                          reads=[tpm, tgs], writes=[tpg])
                                    yield
                                ub, tub = Ub.next()
                                fw.op("dve", lambda e: e.tensor_copy(out=ub[:], in_=pg[:, 128:256]), reads=[tpg], writes=[tub])
                                yield
                                ph, tph = psH.next()
                                py, tpy = psY.next()
                                for h in range(2):
                                    hp = slice(h * 64, (h + 1) * 64)
                                    fw.op("pe", lambda e, h=h, hp=hp: e.matmul(ph[hp, 0:64], lhsT=btok[:, i, hp], rhs=ub[:, hp], start=True, stop=False),
                                          reads=[ttok, tub], writes=[tph])
                                    yield
                                    fw.op("pe", lambda e, h=h, hp=hp: e.matmul(ph[hp, 0:64], lhsT=ktok[:, i, hp], rhs=vtok[:, gch, hp], start=False, stop=True),
                                          reads=[ttok, tvt], writes=[tph])
                                    yield
                                for h in range(2):
                                    hp = slice(h * 64, (h + 1) * 64)
                                    fw.op("pe", lambda e, h=h, hp=hp: e.matmul(py[0:64, hp], lhsT=rT[hp, cc0:cc1], rhs=Hb[hp, :], start=True, stop=False),
                                          reads=[tops, tH], writes=[tpy])
                                    yield
                                    fw.op("pe", lambda e, h=h, hp=hp: e.matmul(py[0:64, hp], lhsT=a45[:, h, :], rhs=ub[:, hp], start=False, stop=False),
                                          reads=[ta45, tub], writes=[tpy])
                                    yield
                                    fw.op("pe", lambda e, h=h, hp=hp: e.matmul(py[0:64, hp], lhsT=a45[:, 2 + h, :], rhs=vtok[:, gch, hp], start=False, stop=True),
                                          reads=[ta45, tvt], writes=[tpy])
                                    yield
                                fw.op("dve", lambda e: e.scalar_tensor_tensor(out=Hs[:], in0=Hs[:], scalar=pC[:, i:i + 1], in1=ph[:, 0:64], op0=ALU.mult, op1=ALU.add),
                                      reads=[tph, tH, tops, tpy], writes=[tH])
                                yield
                                fw.op("act", lambda e: e.copy(out=Hb[:], in_=Hs[:]), reads=[tH, tpy, tpg], writes=[tH])
                                yield
                                ys, tys = Ys.next()
                                fw.op("act", lambda e: e.copy(out=ys[:], in_=py[0:64, 0:128]), reads=[tpy], writes=[tys])
                                yield
                                fw.op("pe", lambda e: e.transpose(py[:, 256:320], ys[:], idt[0:64, 0:64]), reads=[tys, tc], writes=[tpy])
                                yield
                                y0 = off + cc0
                                if y0 >= T:
                                    y0 -= T
                                fw.op("dve", lambda e: e.tensor_tensor(out=yacc[:, y0:y0 + CHK], in0=yacc[:, y0:y0 + CHK], in1=py[:, 256:320], op=ALU.add),
                                      reads=[tpy, tya], writes=[tya])
                                yield

                            for _ in par_gen(order[0]):
                                pass
                            for idx, i in enumerate(order):
                                gp = par_gen(order[idx + 1]) if idx + 1 < len(order) else iter(())
                                gc = chain_gen(i)
                                done_p = done_c = False
                                while not (done_p and done_c):
                                    for _ in range(2):
                                        if not done_p:
                                            try:
                                                next(gp)
                                            except StopIteration:
                                                done_p = True
                                    if not done_c:
                                        try:
                                            next(gc)
                                        except StopIteration:
                                            done_c = True
                            fw.barrier()
                with ExitStack() as es4:
                    P4 = Pool_(fw, es4)
                    psr = Rot([P4.ps() for _ in range(3)])
                    xc = P4.sb([128, 512], F32)
                    sq = P4.sb([128, 512], F32)
                    rs = P4.sb([128, 512], F32)
                    txc, tsq, trs = Tok(), Tok(), Tok()
                    stg = Rot([P4.sb([128, 512], F32) for _ in range(2)])
                    tout = Tok()
                    for (lo, hi, ic) in SEGS:
                        n = hi - lo
                        ps, tps = psr.next()
                        fw.op("pe", lambda e, ps=ps, lo=lo, hi=hi, n=n: e.matmul(ps[:, :n], lhsT=bd1[:], rhs=yacc[:, lo:hi], start=True, stop=True), reads=[tya, tc], writes=[tps])
                        fw.op("dve", lambda e, ps=ps, lo=lo, hi=hi, n=n: e.scalar_tensor_tensor(out=xc[:, :n], in0=ps[:, :n], scalar=-1.0 / 64, in1=yacc[:, lo:hi], op0=ALU.mult, op1=ALU.add),
                              reads=[tps, tya], writes=[txc])
                        fw.op("act", lambda e, n=n: e.activation(out=sq[:, :n], in_=xc[:, :n], func=AF.Square), reads=[txc], writes=[tsq])
                        ps2, tps2 = psr.next()
                        fw.op("pe", lambda e, ps2=ps2, n=n: e.matmul(ps2[:, :n], lhsT=bd1[:], rhs=sq[:, :n], start=True, stop=True), reads=[tsq, tc], writes=[tps2])
                        rstd_from_ps(fw, rs, trs, ps2, tps2, n, 1.0 / 64, epsl[:, 0:1], tpp)
                        fw.op("dve", lambda e, n=n: e.tensor_tensor(out=xc[:, :n], in0=xc[:, :n], in1=rs[:, :n], op=ALU.mult), reads=[txc, trs], writes=[txc])
                        fw.op("act", lambda e, n=n: e.activation(out=xc[:, :n], in_=xc[:, :n], func=AF.Identity, scale=pp[:, 8, c:c + 1], bias=pp[:, 9, c:c + 1]),
                              reads=[txc, tpp], writes=[txc])
                        fw.op("pool", lambda e, lo=lo, hi=hi, n=n: e.tensor_tensor(out=xc[:, :n], in0=xc[:, :n], in1=bon[:, lo:hi], op=ALU.add), reads=[txc, tbon], writes=[txc])
                        ps3, tps3 = psr.next()
                        fw.op("pe", lambda e, ps3=ps3, lo=lo, hi=hi, n=n: e.matmul(ps3[:, :n], lhsT=wB[64:128, c * 128:(c + 1) * 128], rhs=lorB[64:128, lo:hi], start=True, stop=True),
                              reads=[tlw, tlor], writes=[tps3])
                        sg, tsg = stg.next()
                        fw.op("dve", lambda e, sg=sg, ps3=ps3, n=n: e.tensor_tensor(out=sg[:, :n], in0=ps3[:, :n], in1=xc[:, :n], op=ALU.mult), reads=[tps3, txc], writes=[tsg])
                        fw.dma("sp", RWT[c * 128:(c + 1) * 128, lo:hi], sg[:, :n], reads=[tsg], writes=[Tok()])
                    fw.barrier()
        fw.barrier()
NCORES = 8
DEPTH = 4
S5_KEYS = ["s5_a_re", "s5_a_im", "s5_log_step", "s5_b_re", "s5_b_im", "s5_c_re", "s5_c_im", "s5_d", "s5_glu_w", "s5_glu_b", "s5_out_g"]
RW_KEYS = ["rw_mu", "rw_w0", "rw_w2", "rw_a0", "rw_a2", "rw_g2", "rw_k_k", "rw_k_a", "rw_r_k", "rw_ln_g", "rw_ln_b"]
LAYER_KEYS = (["norm1_g", "norm2_g", "mod_w", "mod_b", "w_in", "w_out", "att_qn_g", "att_kn_g", "att_out_g",
               "ffn_up", "ffn_conv_w", "ffn_conv_b", "ffn_down"] + S5_KEYS + RW_KEYS)
SHARED_KEYS = ["c_ctx", "final_g", "ident", "POS", "CST", "RWMASK", "RWBD"]
PERCORE_KEYS = ["x_b", "ctx_b", "c_b"]
SCRATCH = {"ZT": [2048, TE], "VT": [T, 128], "MOD": [128, 48, 2], "S5T": [256, T], "ATT_T": [512, T], "RWT": [256, T],
           "XT1": [DM, T], "XA": [DM, T], "XB": [DM, T]}


def build_fused(depth=DEPTH):
    nc = bass.Bass("TRN2", target_bir_lowering=False)
    decl = {}

    def ext(name, shape, dt, kind):
        if name not in decl:
            decl[name] = nc.dram_tensor(name, list(shape), dt, kind=kind).ap()
        return decl[name]

    def make_io(l):
        xin = "XA" if l % 2 == 0 else "XB"
        xout = "XB" if l % 2 == 0 else "XA"

        def io(name, shape, dt, role):
            if name in LAYER_KEYS:
                full = ext(name, [DEPTH] + list(shape), dt, "ExternalInput")
                return full[l]
            if name in SHARED_KEYS or name in PERCORE_KEYS:
                return ext(name, shape, dt, "ExternalInput")
            if name == "OUT":
                return ext(name, shape, dt, "ExternalOutput")
            if name == "XT":
                name = xin
            elif name == "XT2":
                name = xout
            return ext(name, SCRATCH[name], dt, "Internal")
        return io
    with ExitStack() as es:
        fw = FW(nc, es)
        stage_p0(fw, make_io(0))
        for l in range(depth):
            io = make_io(l)
            for st in (stage_p1, stage_p23, stage_p4, stage_p5a, stage_p5b):
                st(fw, io)
        stage_p6(fw, make_io(depth))
        fw.barrier()
    return nc, fw


def tile_up(up):
    lead = up.shape[:-2]
    v = up.reshape(lead + (8, 128, 44, 128))
    nd = len(lead)
    v = np.transpose(v, tuple(range(nd)) + (nd + 2, nd + 1, nd + 0, nd + 3))
    return np.ascontiguousarray(v).reshape(lead + (44, 128, 1024))


_FUSED = {}


def kernel(**inp):
    inp = {k: np.ascontiguousarray(np.asarray(v)) for k, v in inp.items()}
    if "nc" not in _FUSED:
        _FUSED["nc"], _FUSED["fw"] = build_fused()
    nc = _FUSED["nc"]
    pos, cst = host_consts()
    rwm, rwbd = rw_consts()
    shared = {k: inp[k] for k in LAYER_KEYS if k != "rw_r_k"}
    shared["rw_r_k"] = inp["rw_r_k"].reshape(DEPTH, 256)
    shared["ffn_up"] = tile_up(inp["ffn_up"])
    shared.update(c_ctx=inp["c_ctx"], final_g=inp["final_g"], ident=np.eye(128, dtype=np.float32),
                  POS=pos, CST=cst, RWMASK=rwm, RWBD=rwbd)
    in_maps = [dict(shared, x_b=inp["x"][b], ctx_b=inp["ctx"][b], c_b=inp["c"][b]) for b in range(NCORES)]
    res = run_bass_kernel_spmd(nc, in_maps, core_ids=list(range(NCORES)))
    return np.stack([res.results[b]["OUT"] for b in range(NCORES)], 0).astype(np.float32)
```

```python
import math
import numpy as np
from contextlib import ExitStack
import concourse.bass as bass
import concourse.mybir as mybir
from concourse.bass_utils import run_bass_kernel_spmd

F32 = mybir.dt.float32
F32R = mybir.dt.float32r
BF16 = mybir.dt.bfloat16
I32 = mybir.dt.int32
ALU = mybir.AluOpType
AF = mybir.ActivationFunctionType
AX = mybir.AxisListType

T = 2304
TE = 2560
NCTX = 256
NLAT = 2048
DM = 1024
SEGS = [(0, 256, 1), (256, 768, 0), (768, 1280, 0), (1280, 1792, 0), (1792, 2304, 0)]
RMS_EPS = 1e-6


class Tok:
    __slots__ = ("w", "r")

    def __init__(self):
        self.w = None
        self.r = {}


class FW:
    ENG = ("pe", "dve", "act", "pool", "sp")
    NDMA = 8

    def __init__(self, nc, es):
        self.nc = nc
        self.es = es
        self.eng = {"pe": nc.tensor, "dve": nc.vector, "act": nc.scalar,
                    "pool": nc.gpsimd, "sp": nc.sync}
        self.sem = {}
        self.cnt = {}
        for e in self.ENG:
            self.sem[e] = es.enter_context(nc.semaphore("s_" + e))
            self.cnt[e] = 0
        self.dq = {}
        for q in ("sp", "pool", "act"):
            ring = []
            for i in range(self.NDMA):
                k = "d_%s_%d" % (q, i)
                self.sem[k] = es.enter_context(nc.semaphore(k))
                self.cnt[k] = 0
                ring.append(k)
            self.dq[q] = [ring, 0]
        self.seen = {e: {} for e in self.ENG}
        self.attach = True
        self.ninst = 0
        self.uid = 0

    def name(self, p):
        self.uid += 1
        return "%s_%d" % (p, self.uid)

    def _deps(self, reads, writes):
        deps = {}
        for t in reads:
            if t.w is not None and deps.get(t.w[0], 0) < t.w[1]:
                deps[t.w[0]] = t.w[1]
        for t in writes:
            if t.w is not None and deps.get(t.w[0], 0) < t.w[1]:
                deps[t.w[0]] = t.w[1]
            for k, v in t.r.items():
                if deps.get(k, 0) < v:
                    deps[k] = v
        return deps

    def _wait(self, e, deps):
        seen = self.seen[e]
        for k, v in deps.items():
            if seen.get(k, 0) < v:
                self.eng[e].wait_ge(self.sem[k], v)
                seen[k] = v

    def op(self, e, fn, reads=(), writes=()):
        deps = self._deps(reads, writes)
        seen = self.seen[e]
        need = [(k, v) for k, v in deps.items() if seen.get(k, 0) < v]
        att = None
        if need and self.attach:
            att = need.pop()
        for k, v in need:
            self.eng[e].wait_ge(self.sem[k], v)
            seen[k] = v
        inst = fn(self.eng[e])
        if att is not None:
            inst._wait_ge(self.sem[att[0]], att[1])
            seen[att[0]] = att[1]
        self.cnt[e] += 1
        inst.then_inc(self.sem[e], 1)
        v = self.cnt[e]
        for t in reads:
            t.r[e] = v
        for t in writes:
            t.w = (e, v)
            t.r = {}
        self.ninst += 1
        return inst

    def dma(self, q, out, in_, reads=(), writes=(), **kw):
        ring, idx = self.dq[q]
        k = ring[idx % len(ring)]
        self.dq[q][1] = idx + 1
        deps = self._deps(reads, writes)
        if self.cnt[k] > 0:
            deps[k] = max(deps.get(k, 0), self.cnt[k])
        self._wait(q, deps)
        inst = self.eng[q].dma_start(out=out, in_=in_, **kw)
        self.cnt[k] += 16
        inst.then_inc(self.sem[k], 16)
        v = self.cnt[k]
        for t in reads:
            t.r[k] = v
        for t in writes:
            t.w = (k, v)
            t.r = {}
        self.ninst += 1
        return inst

    def barrier(self, engines=None):
        allv = {k: v for k, v in self.cnt.items() if v > 0}
        for e in (engines or self.ENG):
            self._wait(e, allv)


class Pool_:
    def __init__(self, fw, es):
        self.fw = fw
        self.es = es
        self.nc = fw.nc

    def sb(self, shape, dt, name="t"):
        return self.es.enter_context(self.nc.sbuf_tensor(self.fw.name(name), list(shape), dt))

    def ps(self, shape=(128, 512), dt=F32, name="ps"):
        return self.es.enter_context(self.nc.psum_tensor(self.fw.name(name), list(shape), dt))


class Rot:
    def __init__(self, bufs):
        self.bufs = bufs
        self.toks = [Tok() for _ in bufs]
        self.i = 0

    def next(self):
        j = self.i % len(self.bufs)
        self.i += 1
        return self.bufs[j], self.toks[j]


def load_w_bf16(fw, P, W, rows, cols, tok, name="w", q="pool", chunk=2048, stage=None, cast_engs=("pool",)):
    kt = rows // 128
    wb = P.sb([128, kt, cols], BF16, name)
    chunk = min(chunk, cols)
    st = stage or Rot([P.sb([128, chunk], F32, "wstg") for _ in range(3)])
    ci = 0
    for k in range(kt):
        for c0 in range(0, cols, chunk):
            n = min(chunk, cols - c0)
            sg, tsg = st.next()
            fw.dma("sp", sg[:, :n], W[k * 128:(k + 1) * 128, c0:c0 + n], writes=[tsg])
            ce = cast_engs[ci % len(cast_engs)]
            ci += 1
            if ce == "act":
                fw.op("act", lambda e, sg=sg, k=k, c0=c0, n=n: e.copy(out=wb[:, k, c0:c0 + n], in_=sg[:, :n]), reads=[tsg], writes=[tok])
            else:
                fw.op(ce, lambda e, sg=sg, k=k, c0=c0, n=n: e.tensor_copy(out=wb[:, k, c0:c0 + n], in_=sg[:, :n]), reads=[tsg], writes=[tok])
    return wb


def stage_p0(fw, io):
    nc = fw.nc
    xb = io("x_b", [NLAT, DM], F32, "in")
    cb = io("ctx_b", [NCTX, DM], F32, "in")
    ident = io("ident", [128, 128], F32, "in")
    XT = io("XT", [DM, T], F32, "out")
    with ExitStack() as es:
        P = Pool_(fw, es)
        idt = P.sb([128, 128], F32)
        tid = Tok()
        fw.dma("sp", idt[:], ident, writes=[tid])
        xt = P.sb([128, 8, T], F32)
        txt = Tok()
        xin = Rot([P.sb([128, DM], F32) for _ in range(3)])
        pss = Rot([P.ps() for _ in range(4)])
        for tt in range(18):
            src = cb[tt * 128:(tt + 1) * 128, :] if tt < 2 else xb[(tt - 2) * 128:(tt - 1) * 128, :]
            xi, txi = xin.next()
            fw.dma("sp", xi[:], src, writes=[txi])
            for half in range(2):
                ps, tps = pss.next()
                for k in range(4):
                    kk = half * 4 + k
                    fw.op("pe", lambda e, ps=ps, k=k, kk=kk, xi=xi: e.transpose(
                        ps[:, k * 128:(k + 1) * 128], xi[:, kk * 128:(kk + 1) * 128], idt[:]),
                        reads=[txi, tid], writes=[tps])
                eng = "dve" if half == 0 else "act"
                outap = xt[:, half * 4:half * 4 + 4, tt * 128:(tt + 1) * 128]
                inap = ps[:].rearrange("p (k t) -> p k t", k=4)
                if eng == "dve":
                    fw.op("dve", lambda e, o=outap, i=inap: e.tensor_copy(out=o, in_=i), reads=[tps], writes=[txt])
                else:
                    fw.op("act", lambda e, o=outap, i=inap: e.copy(out=o, in_=i), reads=[tps], writes=[txt])
        tout = Tok()
        for k in range(8):
            fw.dma("sp", XT[k * 128:(k + 1) * 128, :], xt[:, k, :], reads=[txt], writes=[Tok()])
        fw.barrier()


def make_AB(fw, P, MODs, tmod, g_ap, sh_base, sc_base):
    g = P.sb([128, 8], F32)
    tg = Tok()
    fw.dma("sp", g[:], g_ap.rearrange("(k p) -> p k", p=128), writes=[tg], allow_slow_non_contiguous=True)
    AB = P.sb([128, 2, 2, 8], F32)
    tab = Tok()
    for ic in range(2):
        fw.op("dve", lambda e, ic=ic: e.tensor_scalar(out=AB[:, ic, 0, :], in0=MODs[:, sc_base:sc_base + 8, ic],
                                                      scalar1=1.0, scalar2=None, op0=ALU.add),
              reads=[tmod], writes=[tab])
        fw.op("dve", lambda e, ic=ic: e.tensor_tensor(out=AB[:, ic, 0, :], in0=AB[:, ic, 0, :], in1=g[:], op=ALU.mult),
              reads=[tg, tab], writes=[tab])
        fw.op("dve", lambda e, ic=ic: e.tensor_copy(out=AB[:, ic, 1, :], in_=MODs[:, sh_base:sh_base + 8, ic]),
              reads=[tmod], writes=[tab])
    return AB, tab


def norm_mod_seg(fw, P, st, xs, txs, n, ic, AB, tab, outs, touts):
    sq, ones, tones, psr, rs, tmpr = st["sq"], st["ones"], st["tones"], st["psr"], st["rs"], st["tmpr"]
    tsq, trs = st["tsq"], st["trs"]
    fw.op("act", lambda e: e.activation(out=sq[:, :, :n], in_=xs[:, :, :n], func=AF.Square), reads=[txs], writes=[tsq])
    ps, tps = psr.next()
    for k in range(8):
        fw.op("pe", lambda e, k=k: e.matmul(ps[:, :n], lhsT=ones[:], rhs=sq[:, k, :n], start=(k == 0), stop=(k == 7)),
              reads=[tsq, tones], writes=[tps])
    fw.op("act", lambda e: e.activation(out=rs[:, :n], in_=ps[:, :n], func=AF.Ln, scale=1.0 / DM, bias=st["eps"][:, 0:1]),
          reads=[tps, st["teps"]], writes=[trs])
    fw.op("act", lambda e: e.activation(out=rs[:, :n], in_=rs[:, :n], func=AF.Exp, scale=-0.5), reads=[trs], writes=[trs])
    for k in range(8):
        tmp, ttmp = tmpr.next()
        fw.op("dve", lambda e, k=k, tmp=tmp: e.tensor_tensor(out=tmp[:, :n], in0=xs[:, k, :n], in1=rs[:, :n], op=ALU.mult),
              reads=[txs, trs], writes=[ttmp])
        for o in outs(k):
            fw.op("act", lambda e, k=k, tmp=tmp, o=o: e.activation(out=o, in_=tmp[:, :n], func=AF.Identity,
                                                                   scale=AB[:, ic, 0, k:k + 1], bias=AB[:, ic, 1, k:k + 1]),
                  reads=[ttmp, tab], writes=touts)


def norm_state(fw, P):
    st = {}
    st["sq"] = P.sb([128, 8, 512], BF16)
    st["tsq"] = Tok()
    st["ones"] = P.sb([128, 128], BF16)
    st["tones"] = Tok()
    fw.op("pool", lambda e: e.memset(st["ones"][:], 1.0), writes=[st["tones"]])
    st["eps"] = P.sb([128, 1], F32)
    st["teps"] = Tok()
    fw.op("pool", lambda e: e.memset(st["eps"][:], RMS_EPS), writes=[st["teps"]])
    st["psr"] = Rot([P.ps() for _ in range(2)])
    st["rs"] = P.sb([128, 512], F32)
    st["trs"] = Tok()
    st["tmpr"] = Rot([P.sb([128, 512], F32) for _ in range(2)])
    return st


def stage_p1(fw, io):
    XT = io("XT", [DM, T], F32, "in")
    c_b = io("c_b", [DM], F32, "in")
    c_ctx = io("c_ctx", [DM], F32, "in")
    mod_w = io("mod_w", [DM, 6 * DM], F32, "in")
    mod_b = io("mod_b", [6 * DM], F32, "in")
    n1g = io("norm1_g", [DM], F32, "in")
    w_in = io("w_in", [DM, 1984], F32, "in")
    MOD = io("MOD", [128, 48, 2], F32, "out")
    ZT = io("ZT", [2048, TE], F32, "out")
    VT = io("VT", [T, 128], F32, "out")
    XTv = XT.rearrange("(k p) t -> p k t", p=128)
    with ExitStack() as es:
        P = Pool_(fw, es)
        wstage = Rot([P.sb([128, 2048], F32, "wstg") for _ in range(3)])
        tmw = Tok()
        cc = P.sb([128, 8, 2], F32)
        tcc = Tok()
        fw.dma("sp", cc[:, :, 0], c_b.rearrange("(k p) -> p k", p=128), writes=[tcc], allow_slow_non_contiguous=True)
        fw.dma("sp", cc[:, :, 1], c_ctx.rearrange("(k p) -> p k", p=128), writes=[tcc], allow_slow_non_contiguous=True)
        scb = P.sb([128, 8, 2], BF16)
        tscb = Tok()
        fw.op("act", lambda e: e.activation(out=scb[:], in_=cc[:], func=AF.Silu), reads=[tcc], writes=[tscb])
        mb = P.sb([128, 48], F32)
        tmb = Tok()
        fw.dma("sp", mb[:], mod_b.rearrange("(j p) -> p j", p=128), writes=[tmb], allow_slow_non_contiguous=True)
        MODs = P.sb([128, 48, 2], F32)
        tmod = Tok()
        with ExitStack() as es2:
            P2 = Pool_(fw, es2)
            mwb = load_w_bf16(fw, P2, mod_w, DM, 6 * DM, tmw, "modw", stage=wstage, cast_engs=("dve", "act", "pool"))
            psm = P2.ps([128, 512])
            tpsm = Tok()
            for j in range(48):
                for k in range(8):
                    fw.op("pe", lambda e, j=j, k=k: e.matmul(psm[:, 2 * j:2 * j + 2], lhsT=mwb[:, k, j * 128:(j + 1) * 128],
                                                             rhs=scb[:, k, :], start=(k == 0), stop=(k == 7)),
                          reads=[tmw, tscb], writes=[tpsm])
            for ic in range(2):
                fw.op("dve", lambda e, ic=ic: e.tensor_tensor(
                    out=MODs[:, :, ic], in0=psm[:, 0:96].rearrange("p (j c) -> p j c", c=2)[:, :, ic], in1=mb[:], op=ALU.add),
                    reads=[tpsm, tmb], writes=[tmod])
            fw.barrier()
        tmo = Tok()
        fw.dma("sp", MOD, MODs[:], reads=[tmod], writes=[tmo])
        AB, tab = make_AB(fw, P, MODs, tmod, n1g, 0, 8)
        tw = Tok()
        wb = load_w_bf16(fw, P, w_in, DM, 1984, tw, "win", stage=wstage, cast_engs=("dve", "act", "pool"))
        hT = P.sb([128, 8, TE], BF16)
        thT = Tok()
        st = norm_state(fw, P)
        xr = Rot([P.sb([128, 8, 512], F32) for _ in range(2)])
        for (lo, hi, ic) in SEGS:
            n = hi - lo
            xs, txs = xr.next()
            fw.dma("sp", xs[:, :, :n], XTv[:, :, lo:hi], writes=[txs])

            def outs(k, lo=lo, hi=hi, ic=ic):
                o = [hT[:, k, lo:hi]]
                if ic:
                    o.append(hT[:, k, T + lo:T + hi])
                return o
            norm_mod_seg(fw, P, st, xs, txs, n, ic, AB, tab, outs, [thT])
        psr = Rot([P.ps() for _ in range(4)])
        stg = Rot([P.sb([128, 512], F32) for _ in range(4)])
        tz = Tok()
        cnt = 0
        for nt in range(16):
            if nt == 7:
                continue
            M = 64 if nt == 15 else 128
            for cc_ in range(5):
                c0 = cc_ * 512
                ps, tps = psr.next()
                for k in range(8):
                    fw.op("pe", lambda e, ps=ps, k=k, nt=nt, M=M, c0=c0: e.matmul(
                        ps[0:M, :], lhsT=wb[:, k, nt * 128:nt * 128 + M], rhs=hT[:, k, c0:c0 + 512],
                        start=(k == 0), stop=(k == 7)), reads=[tw, thT], writes=[tps])
                sg, tsg = stg.next()
                if cnt % 2 == 0:
                    fw.op("dve", lambda e, sg=sg, ps=ps, M=M: e.tensor_copy(out=sg[0:M, :], in_=ps[0:M, :]), reads=[tps], writes=[tsg])
                else:
                    fw.op("act", lambda e, sg=sg, ps=ps, M=M: e.copy(out=sg[0:M, :], in_=ps[0:M, :]), reads=[tps], writes=[tsg])
                cnt += 1
                fw.dma("sp", ZT[nt * 128:nt * 128 + M, c0:c0 + 512], sg[0:M, :], reads=[tsg], writes=[Tok()])
        for tt in range(18):
            ps, tps = psr.next()
            for k in range(8):
                fw.op("pe", lambda e, ps=ps, k=k, tt=tt: e.matmul(
                    ps[:, 0:128], lhsT=hT[:, k, tt * 128:(tt + 1) * 128], rhs=wb[:, k, 896:1024],
                    start=(k == 0), stop=(k == 7)), reads=[tw, thT], writes=[tps])
            sg, tsg = stg.next()
            fw.op("dve", lambda e, sg=sg, ps=ps: e.tensor_copy(out=sg[:, 0:128], in_=ps[:, 0:128]), reads=[tps], writes=[tsg])
            fw.dma("sp", VT[tt * 128:(tt + 1) * 128, :], sg[:, 0:128], reads=[tsg], writes=[Tok()])
        fw.barrier()


def build_program(stage_fns):
    nc = bass.Bass("TRN2", target_bir_lowering=False)
    decl = {}

    def io(name, shape, dt, role):
        if name in decl:
            return decl[name][0]
        kind = "ExternalInput" if role == "in" else "ExternalOutput"
        ap = nc.dram_tensor(name, list(shape), dt, kind=kind).ap()
        decl[name] = (ap, role, shape)
        return ap
    with ExitStack() as es:
        fw = FW(nc, es)
        for fn in stage_fns:
            fn(fw, io)
        fw.barrier()
    return nc, decl, fw


_PROG_CACHE = {}


def run_stage(key, stage_fns, in_maps, ncores):
    if key not in _PROG_CACHE:
        _PROG_CACHE[key] = build_program(stage_fns)
    nc, decl, fw = _PROG_CACHE[key]
    res = run_bass_kernel_spmd(nc, in_maps, core_ids=list(range(ncores)))
    return res.results


def rstd_from_ps(fw, rs, trs, ps, tps, n, scale, epsap, teps, rows=128):
    fw.op("act", lambda e: e.activation(out=rs[0:rows, :n], in_=ps[0:rows, :n], func=AF.Ln, scale=scale, bias=epsap),
          reads=[tps, teps], writes=[trs])
    fw.op("act", lambda e: e.activation(out=rs[0:rows, :n], in_=rs[0:rows, :n], func=AF.Exp, scale=-0.5), reads=[trs], writes=[trs])


def stage_p3(fw, io):
    ZT = io("ZT", [2048, TE], F32, "in")
    VT = io("VT", [T, 128], F32, "in")
    qn_g = io("att_qn_g", [64], F32, "in")
    kn_g = io("att_kn_g", [64], F32, "in")
    og = io("att_out_g", [512], F32, "in")
    POS = io("POS", [128, NLAT], F32, "in")
    CST = io("CST", [128, 260], F32, "in")
    ATT = io("ATT_T", [512, T], F32, "out")
    with ExitStack() as es:
        P = Pool_(fw, es)
        cst = P.sb([128, 260], F32)
        tc = Tok()
        fw.dma("sp", cst[:], CST, writes=[tc])
        permb = P.sb([128, 128], BF16)
        bdb = P.sb([128, 128], BF16)
        fw.op("dve", lambda e: e.tensor_copy(out=permb[:], in_=cst[:, 1:129]), reads=[tc], writes=[tc])
        fw.op("dve", lambda e: e.tensor_copy(out=bdb[:], in_=cst[:, 129:257]), reads=[tc], writes=[tc])
        eps = P.sb([128, 2], F32)
        teps = Tok()
        fw.op("pool", lambda e: e.memset(eps[:, 0:1], RMS_EPS), writes=[teps])
        fw.op("pool", lambda e: e.memset(eps[:, 1:2], 0.0), writes=[teps])
        gq = P.sb([128, 2], F32)
        tg = Tok()
        for h in range(2):
            fw.dma("sp", gq[h * 64:(h + 1) * 64, 0:1], qn_g.rearrange("(p o) -> p o", o=1), writes=[tg])
            fw.dma("sp", gq[h * 64:(h + 1) * 64, 1:2], kn_g.rearrange("(p o) -> p o", o=1), writes=[tg])
        cos = P.sb([128, NLAT], F32)
        sin = P.sb([128, NLAT], F32)
        ttab = Tok()
        with ExitStack() as es2:
            P2 = Pool_(fw, es2)
            ang = P2.sb([128, NLAT], F32)
            tmpf = P2.sb([128, NLAT], F32)
            tmpi = P2.sb([128, NLAT], I32)
            ta = Tok()
            fw.dma("sp", ang[:], POS, writes=[ta])
            fw.op("dve", lambda e: e.tensor_scalar(out=ang[:], in0=ang[:], scalar1=cst[:, 0:1], scalar2=None, op0=ALU.mult),
                  reads=[ta, tc], writes=[ta])
            for (tab, off) in ((sin, 0.0), (cos, math.pi / 2)):
                fw.op("dve", lambda e, off=off: e.tensor_scalar(out=tmpi[:], in0=ang[:], scalar1=off, scalar2=1.0 / (2 * math.pi),
                                                                op0=ALU.add, op1=ALU.mult), reads=[ta], writes=[ta])
                fw.op("dve", lambda e: e.tensor_copy(out=tmpf[:], in_=tmpi[:]), reads=[ta], writes=[ta])
                fw.op("dve", lambda e: e.scalar_tensor_tensor(out=tmpf[:], in0=tmpf[:], scalar=-2 * math.pi, in1=ang[:],
                                                              op0=ALU.mult, op1=ALU.add), reads=[ta], writes=[ta])
                fw.op("dve", lambda e, off=off: e.tensor_scalar(out=tmpf[:], in0=tmpf[:], scalar1=off, scalar2=math.pi,
                                                                op0=ALU.add, op1=ALU.min), reads=[ta], writes=[ta])
                fw.op("dve", lambda e: e.tensor_scalar(out=tmpf[:], in0=tmpf[:], scalar1=-math.pi, scalar2=None, op0=ALU.max),
                      reads=[ta], writes=[ta])
                fw.op("act", lambda e, tab=tab: e.activation(out=tab[:], in_=tmpf[:], func=AF.Sin), reads=[ta], writes=[ttab])
            fw.barrier()
        qb = P.sb([128, 4, T], BF16)
        kd = P.sb([128, 2, T], BF16)
        tq = Tok()
        with ExitStack() as es2:
            P2 = Pool_(fw, es2)
            raw = Rot([P2.sb([128, 512], F32) for _ in range(2)])
            sqr = Rot([P2.sb([128, 512], BF16) for _ in range(2)])
            psr = Rot([P2.ps() for _ in range(2)])
            psr2 = Rot([P2.ps() for _ in range(2)])
            rsr = Rot([P2.sb([128, 512], F32) for _ in range(2)])
            nbr = Rot([P2.sb([128, 512], BF16) for _ in range(2)])
            t1r = Rot([P2.sb([128, 512], F32) for _ in range(2)])
            t2r = Rot([P2.sb([128, 512], F32) for _ in range(2)])
            items = [("q", j) for j in range(4)] + [("k", g) for g in range(2)]
            for (kind, j) in items:
                for (lo, hi, ic) in SEGS:
                    n = hi - lo
                    rw, trw = raw.next()
                    if kind == "q":
                        fw.dma("sp", rw[:, :n], ZT[256 + j * 128:256 + (j + 1) * 128, lo:hi], writes=[trw])
                        gcol = 0
                        dst = qb[:, j, lo:hi]
                    else:
                        for h in range(2):
                            fw.dma("sp", rw[h * 64:(h + 1) * 64, :n], ZT[768 + j * 64:768 + (j + 1) * 64, lo:hi], writes=[trw])
                        gcol = 1
                        dst = kd[:, j, lo:hi]
                    sq, tsq = sqr.next()
                    fw.op("act", lambda e, sq=sq, rw=rw, n=n: e.activation(out=sq[:, :n], in_=rw[:, :n], func=AF.Square), reads=[trw], writes=[tsq])
                    ps, tps = psr.next()
                    fw.op("pe", lambda e, ps=ps, sq=sq, n=n: e.matmul(ps[:, :n], lhsT=bdb[:], rhs=sq[:, :n], start=True, stop=True),
                          reads=[tsq, tc], writes=[tps])
                    rs, trs = rsr.next()
                    rstd_from_ps(fw, rs, trs, ps, tps, n, 1.0, eps[:, 0:1], teps)
                    t1, tt1 = t1r.next()
                    fw.op("dve", lambda e, t1=t1, rw=rw, rs=rs, n=n, gcol=gcol: e.scalar_tensor_tensor(
                        out=t1[:, :n], in0=rw[:, :n], scalar=gq[:, gcol:gcol + 1], in1=rs[:, :n], op0=ALU.mult, op1=ALU.mult),
                        reads=[trw, trs, tg], writes=[tt1])
                    if ic:
                        fw.op("act", lambda e, dst=dst, t1=t1, n=n: e.copy(out=dst, in_=t1[:, :n]), reads=[tt1], writes=[tq])
                        continue
                    nb, tnb = nbr.next()
                    fw.op("act", lambda e, nb=nb, t1=t1, n=n: e.copy(out=nb[:, :n], in_=t1[:, :n]), reads=[tt1], writes=[tnb])
                    ps2, tps2 = psr2.next()
                    fw.op("pe", lambda e, ps2=ps2, nb=nb, n=n: e.matmul(ps2[:, :n], lhsT=permb[:], rhs=nb[:, :n], start=True, stop=True),
                          reads=[tnb, tc], writes=[tps2])
                    p0 = lo - NCTX
                    t2, tt2 = t2r.next()
                    fw.op("dve", lambda e, t2=t2, ps2=ps2, n=n, p0=p0: e.tensor_tensor(out=t2[:, :n], in0=ps2[:, :n], in1=sin[:, p0:p0 + n], op=ALU.mult),
                          reads=[tps2, ttab], writes=[tt2])
                    fw.op("pool", lambda e, t1=t1, nb=nb, n=n, p0=p0: e.tensor_tensor(out=t1[:, :n], in0=nb[:, :n], in1=cos[:, p0:p0 + n], op=ALU.mult),
                          reads=[tnb, ttab, tt1], writes=[tt1])
                    fw.op("pool", lambda e, dst=dst, t1=t1, t2=t2, n=n: e.tensor_tensor(out=dst, in0=t1[:, :n], in1=t2[:, :n], op=ALU.add),
                          reads=[tt1, tt2], writes=[tq])
            fw.barrier()
        va = P.sb([128, 18, 2, 128], BF16)
        tva = Tok()
        fw.op("pool", lambda e: e.memset(va[:], 1.0), writes=[tva])
        for g in range(2):
            fw.dma("pool", va[:, :, g, 0:64], VT.rearrange("(t p) c -> p t c", p=128)[:, :, g * 64:(g + 1) * 64], writes=[tva])
        att = P.sb([128, 4, T], F32)
        tatt = Tok()
        pss = Rot([P.ps() for _ in range(3)])
        pso = Rot([P.ps() for _ in range(2)])
        ptr = Rot([P.sb([128, 512], BF16) for _ in range(3)])
        rcr = Rot([P.sb([64, 512], F32) for _ in range(2)])
        jobs = [(0, 256, 0, 2)] + [(256 + 512 * i, 768 + 512 * i, 0, 18) for i in range(4)]
        for h in range(8):
            g = h // 4
            jt, r0 = h // 2, (h % 2) * 64
            for (qlo, qhi, k0, k1) in jobs:
                n = qhi - qlo
                po, tpo = pso.next()
                for kt in range(k0, k1):
                    ps, tps = pss.next()
                    fw.op("pe", lambda e, ps=ps, kt=kt, g=g, jt=jt, r0=r0, qlo=qlo, qhi=qhi, n=n: e.matmul(
                        ps[:, :n], lhsT=kd[r0:r0 + 64, g, kt * 128:(kt + 1) * 128], rhs=qb[r0:r0 + 64, jt, qlo:qhi],
                        start=True, stop=True), reads=[tq], writes=[tps])
                    pt, tpt = ptr.next()
                    fw.op("act", lambda e, pt=pt, ps=ps, n=n: e.activation(out=pt[:, :n], in_=ps[:, :n], func=AF.Exp, scale=0.125),
                          reads=[tps], writes=[tpt])
                    fw.op("pe", lambda e, po=po, pt=pt, kt=kt, g=g, n=n, k0=k0, k1=k1: e.matmul(
                        po[:, :n], lhsT=va[:, kt, g, :], rhs=pt[:, :n], start=(kt == k0), stop=(kt == k1 - 1)),
                        reads=[tpt, tva], writes=[tpo])
                rc, trc = rcr.next()
                fw.op("dve", lambda e, rc=rc, po=po, n=n: e.reciprocal(out=rc[:, :n], in_=po[64:128, :n]), reads=[tpo], writes=[trc])
                fw.op("dve", lambda e, rc=rc, po=po, n=n, jt=jt, r0=r0, qlo=qlo, qhi=qhi: e.tensor_tensor(
                    out=att[r0:r0 + 64, jt, qlo:qhi], in0=po[0:64, :n], in1=rc[:, :n], op=ALU.mult),
                    reads=[tpo, trc], writes=[tatt])
        ogs = P.sb([128, 4], F32)
        tog = Tok()
        fw.dma("sp", ogs[:], og.rearrange("(k p) -> p k", p=128), writes=[tog], allow_slow_non_contiguous=True)
        ones = P.sb([128, 128], BF16)
        fw.op("pool", lambda e: e.memset(ones[:], 1.0), writes=[tog])
        sq4 = P.sb([128, 4, 512], BF16)
        tsq4 = Tok()
        rs = P.sb([128, 512], F32)
        trs = Tok()
        stg = Rot([P.sb([128, 512], F32) for _ in range(3)])
        tout = Tok()
        for (lo, hi, ic) in SEGS:
            n = hi - lo
            fw.op("act", lambda e, lo=lo, hi=hi, n=n: e.activation(out=sq4[:, :, :n], in_=att[:, :, lo:hi], func=AF.Square), reads=[tatt], writes=[tsq4])
            ps, tps = pss.next()
            for k in range(4):
                fw.op("pe", lambda e, ps=ps, k=k, n=n: e.matmul(ps[:, :n], lhsT=ones[:], rhs=sq4[:, k, :n], start=(k == 0), stop=(k == 3)),
                      reads=[tsq4, tog], writes=[tps])
            rstd_from_ps(fw, rs, trs, ps, tps, n, 1.0 / 512, eps[:, 0:1], teps)
            for k in range(4):
                sg, tsg = stg.next()
                fw.op("dve", lambda e, sg=sg, k=k, lo=lo, hi=hi, n=n: e.scalar_tensor_tensor(
                    out=sg[:, :n], in0=att[:, k, lo:hi], scalar=ogs[:, k:k + 1], in1=rs[:, :n], op0=ALU.mult, op1=ALU.mult),
                    reads=[tatt, trs, tog], writes=[tsg])
                fw.dma("sp", ATT[k * 128:(k + 1) * 128, lo:hi], sg[:, :n], reads=[tsg], writes=[Tok()])
        fw.barrier()


def host_consts():
    pos = np.zeros((128, NLAT), np.float32)
    inv = np.zeros((128,), np.float32)
    tok = np.arange(NLAT)
    for p in range(128):
        d = p % 64
        pos[p] = (tok // 64) if d < 32 else (tok % 64)
        inv[p] = 10000.0 ** (-(d % 16) / 16.0)
    cst = np.zeros((128, 260), np.float32)
    cst[:, 0] = inv
    perm = np.zeros((128, 128), np.float32)
    for m in range(128):
        d = m % 32
        if d < 16:
            perm[m + 16, m] = -1.0
        else:
            perm[m - 16, m] = 1.0
    cst[:, 1:129] = perm
    bd = np.zeros((128, 128), np.float32)
    bd[:64, :64] = 1.0 / 64
    bd[64:, 64:] = 1.0 / 64
    cst[:, 129:257] = bd
    return pos, cst


def stage_p5a(fw, io):
    XT = io("XT", [DM, T], F32, "in")
    S5T = io("S5T", [256, T], F32, "in")
    ATT = io("ATT_T", [512, T], F32, "in")
    RWT = io("RWT", [256, T], F32, "in")
    MOD = io("MOD", [128, 48, 2], F32, "in")
    w_out = io("w_out", [DM, DM], F32, "in")
    XT1 = io("XT1", [DM, T], F32, "out")
    XTv = XT.rearrange("(k p) t -> p k t", p=128)
    with ExitStack() as es:
        P = Pool_(fw, es)
        MODs = P.sb([128, 48, 2], F32)
        tmod = Tok()
        fw.dma("sp", MODs[:], MOD, writes=[tmod])
        tw = Tok()
        wb = load_w_bf16(fw, P, w_out, DM, DM, tw, "wout", cast_engs=("dve", "act", "pool"))
        cat = P.sb([128, 8, T], BF16)
        tcat = Tok()
        cstg = Rot([P.sb([128, T], F32, "cstg") for _ in range(2)])
        for k in range(8):
            src = S5T[k * 128:(k + 1) * 128, :] if k < 2 else (ATT[(k - 2) * 128:(k - 1) * 128, :] if k < 6 else RWT[(k - 6) * 128:(k - 5) * 128, :])
            sg, tsg = cstg.next()
            fw.dma("sp", sg[:], src, writes=[tsg])
            if k % 2 == 0:
                fw.op("dve", lambda e, sg=sg, k=k: e.tensor_copy(out=cat[:, k, :], in_=sg[:]), reads=[tsg], writes=[tcat])
            else:
                fw.op("act", lambda e, sg=sg, k=k: e.copy(out=cat[:, k, :], in_=sg[:]), reads=[tsg], writes=[tcat])
        xr = Rot([P.sb([128, 8, 512], F32) for _ in range(2)])
        x1r = Rot([P.sb([128, 8, 512], F32) for _ in range(2)])
        psr = Rot([P.ps() for _ in range(4)])
        to1, to2 = Tok(), Tok()
        for (lo, hi, ic) in SEGS:
            n = hi - lo
            xs, txs = xr.next()
            fw.dma("sp", xs[:, :, :n], XTv[:, :, lo:hi], writes=[txs])
            x1, tx1 = x1r.next()
            for d in range(8):
                ps, tps = psr.next()
                for k in range(8):
                    fw.op("pe", lambda e, ps=ps, k=k, d=d, lo=lo, hi=hi, n=n: e.matmul(
                        ps[:, :n], lhsT=wb[:, k, d * 128:(d + 1) * 128], rhs=cat[:, k, lo:hi], start=(k == 0), stop=(k == 7)),
                        reads=[tw, tcat], writes=[tps])
                fw.op("dve", lambda e, ps=ps, d=d, n=n, ic=ic, x1=x1, xs=xs: e.scalar_tensor_tensor(
                    out=x1[:, d, :n], in0=ps[:, :n], scalar=MODs[:, 16 + d, ic:ic + 1], in1=xs[:, d, :n], op0=ALU.mult, op1=ALU.add),
                    reads=[tps, txs, tmod], writes=[tx1])
            for k in range(8):
                fw.dma("sp", XT1[k * 128:(k + 1) * 128, lo:hi], x1[:, k, :n], reads=[tx1], writes=[Tok()])
        fw.barrier()


def stage_p5b(fw, io):
    XT1 = io("XT1", [DM, T], F32, "in")
    n2g = io("norm2_g", [DM], F32, "in")
    MOD = io("MOD", [128, 48, 2], F32, "in")
    up = io("ffn_up", [44, 128, DM], F32, "in")
    cw = io("ffn_conv_w", [3, 5632], F32, "in")
    cb = io("ffn_conv_b", [5632], F32, "in")
    down = io("ffn_down", [2816, DM], F32, "in")
    XT2 = io("XT2", [DM, T], F32, "out")
    X1v = XT1.rearrange("(k p) t -> p k t", p=128)
    X2v = XT2.rearrange("(k p) t -> p k t", p=128)
    with ExitStack() as es:
        P = Pool_(fw, es)
        MODs = P.sb([128, 48, 2], F32)
        tmod = Tok()
        fw.dma("sp", MODs[:], MOD, writes=[tmod])
        cws = P.sb([128, 44, 3], F32)
        cbs = P.sb([128, 44], F32)
        tcw = Tok()
        for w in range(3):
            fw.dma("sp", cws[:, :, w], cw[w].rearrange("(j p) -> p j", p=128), writes=[tcw], allow_slow_non_contiguous=True)
        fw.dma("sp", cbs[:], cb.rearrange("(j p) -> p j", p=128), writes=[tcw], allow_slow_non_contiguous=True)
        h2 = P.sb([128, 8, T], BF16)
        th2 = Tok()
        AB, tab = make_AB(fw, P, MODs, tmod, n2g, 24, 32)
        with ExitStack() as es2:
            P2 = Pool_(fw, es2)
            st = norm_state(fw, P2)
            xr0 = Rot([P2.sb([128, 8, 512], F32) for _ in range(2)])
            for (lo, hi, ic) in SEGS:
                n = hi - lo
                xs, txs = xr0.next()
                fw.dma("sp", xs[:, :, :n], X1v[:, :, lo:hi], writes=[txs])
                norm_mod_seg(fw, P2, st, xs, txs, n, ic, AB, tab, lambda k, lo=lo, hi=hi: [h2[:, k, lo:hi]], [th2])
            fw.barrier()
        hid = P.sb([128, 11, T], BF16)
        thid = Tok()
        dwb = P.sb([128, 11, DM], BF16)
        tdw = Tok()
        urot = [Rot([P.sb([128, T], F32) for _ in range(2)]) for _ in range(2)]
        y = [P.sb([128, T], F32) for _ in range(2)]
        ty = [Tok(), Tok()]
        psr = Rot([P.ps() for _ in range(4)])
        tx2 = Tok()
        RANGES = [(0, NCTX), (NCTX, T)]
        GROUPS = [(0, 2), (2, 2), (4, 2), (6, 2), (8, 2), (10, 1)]
        for half in range(2):
            with ExitStack() as esu:
                Pu = Pool_(fw, esu)
                ustg = Rot([Pu.sb([128, 8, 256], F32, "ustg") for _ in range(2)])
                for jj in range(11):
                    r0 = (half * 11 + jj) * 128
                    sg, tsg = ustg.next()
                    sgv = sg[:].rearrange("p k n -> p (k n)")[:, 0:DM]
                    fw.dma("sp", sgv, down[r0:r0 + 128, :], writes=[tsg])
                    fw.op("pool", lambda e, sgv=sgv, jj=jj: e.tensor_copy(out=dwb[:, jj, :], in_=sgv), reads=[tsg], writes=[tdw])
                ubr = Rot([Pu.sb([128, 8, 2, 256], BF16, "ub") for _ in range(2)])

                def issue_load(g, half=half):
                    jj0, ng = GROUPS[g]
                    ub, tub = ubr.next()
                    for wh in range(2):
                        sg, tsg = ustg.next()
                        sgv = sg[:].rearrange("p k (j c) -> p (k j c)", j=2).rearrange("p (j k c) -> p j k c", j=2, k=8)
                        for jl in range(ng):
                            jt = wh * 22 + half * 11 + jj0 + jl
                            fw.dma("sp", sgv[:, jl].rearrange("p k c -> p (k c)"), up[jt], writes=[tsg])
                            fw.op("pool", lambda e, sgv=sgv, ub=ub, wh=wh, jl=jl: e.tensor_copy(out=ub[:, :, wh, jl * 128:(jl + 1) * 128], in_=sgv[:, jl]),
                                  reads=[tsg], writes=[tub])
                    return ub, tub
                loaded = issue_load(0)
                for g, (jj0, ng) in enumerate(GROUPS):
                    ub, tub = loaded
                    if g + 1 < len(GROUPS):
                        loaded = issue_load(g + 1)
                    for jl in range(ng):
                        jj = jj0 + jl
                        j = half * 11 + jj
                        for wh in range(2):
                            jc = wh * 22 + j
                            ucur, tucur = urot[wh].next()
                            for si, (lo, hi, ic) in enumerate(SEGS):
                                n = hi - lo
                                ps, tps = psr.next()
                                for k in range(8):
                                    fw.op("pe", lambda e, ps=ps, k=k, wh=wh, ub=ub, jl=jl, lo=lo, hi=hi, n=n: e.matmul(
                                        ps[:, :n], lhsT=ub[:, k, wh, jl * 128:(jl + 1) * 128], rhs=h2[:, k, lo:hi], start=(k == 0), stop=(k == 7)),
                                        reads=[tub, th2], writes=[tps])
                                fw.op("act", lambda e, ps=ps, ucur=ucur, lo=lo, hi=hi, n=n: e.copy(out=ucur[:, lo:hi], in_=ps[:, :n]),
                                      reads=[tps], writes=[tucur])
                            fw.op("act", lambda e, wh=wh, jc=jc, ucur=ucur: e.activation(out=y[wh][:], in_=ucur[:], func=AF.Identity,
                                                                                        scale=cws[:, jc, 1:2], bias=cbs[:, jc:jc + 1]),
                                  reads=[tucur, tcw], writes=[ty[wh]])
                            for (lo, hi) in RANGES:
                                fw.op("dve", lambda e, wh=wh, jc=jc, lo=lo, hi=hi, ucur=ucur: e.scalar_tensor_tensor(
                                    out=y[wh][:, lo + 1:hi], in0=ucur[:, lo:hi - 1], scalar=cws[:, jc, 0:1], in1=y[wh][:, lo + 1:hi],
                                    op0=ALU.mult, op1=ALU.add), reads=[tucur, tcw, ty[wh]], writes=[ty[wh]])
                                fw.op("dve", lambda e, wh=wh, jc=jc, lo=lo, hi=hi, ucur=ucur: e.scalar_tensor_tensor(
                                    out=y[wh][:, lo:hi - 1], in0=ucur[:, lo + 1:hi], scalar=cws[:, jc, 2:3], in1=y[wh][:, lo:hi - 1],
                                    op0=ALU.mult, op1=ALU.add), reads=[tucur, tcw, ty[wh]], writes=[ty[wh]])
                        fw.op("act", lambda e: e.activation(out=y[0][:], in_=y[0][:], func=AF.Silu), reads=[ty[0]], writes=[ty[0]])
                        fw.op("dve", lambda e, jj=jj: e.tensor_tensor(out=hid[:, jj, :], in0=y[0][:], in1=y[1][:], op=ALU.mult),
                              reads=[ty[0], ty[1]], writes=[thid])
                fw.barrier()
            with ExitStack() as esd:
                Pd = Pool_(fw, esd)
                xr = Rot([Pd.sb([128, 8, 512], F32) for _ in range(2)])
                for (lo, hi, ic) in SEGS:
                    n = hi - lo
                    xs, txs = xr.next()
                    src = X1v if half == 0 else X2v
                    fw.dma("sp", xs[:, :, :n], src[:, :, lo:hi], reads=([tx2] if half else []), writes=[txs])
                    for d in range(8):
                        ps, tps = psr.next()
                        for jj in range(11):
                            fw.op("pe", lambda e, ps=ps, jj=jj, d=d, lo=lo, hi=hi, n=n: e.matmul(
                                ps[:, :n], lhsT=dwb[:, jj, d * 128:(d + 1) * 128], rhs=hid[:, jj, lo:hi], start=(jj == 0), stop=(jj == 10)),
                                reads=[tdw, thid], writes=[tps])
                        fw.op("dve", lambda e, ps=ps, d=d, n=n, ic=ic, xs=xs: e.scalar_tensor_tensor(
                            out=xs[:, d, :n], in0=ps[:, :n], scalar=MODs[:, 40 + d, ic:ic + 1], in1=xs[:, d, :n], op0=ALU.mult, op1=ALU.add),
                            reads=[tps, tmod, txs], writes=[txs])
                    for k in range(8):
                        fw.dma("sp", XT2[k * 128:(k + 1) * 128, lo:hi], xs[:, k, :n], reads=[txs], writes=[tx2])
                fw.barrier()


def stage_p6(fw, io):
    XT = io("XT", [DM, T], F32, "in")
    fg = io("final_g", [DM], F32, "in")
    ident = io("ident", [128, 128], F32, "in")
    OUT = io("OUT", [NLAT, DM], F32, "out")
    XTv = XT.rearrange("(k p) t -> p k t", p=128)
    with ExitStack() as es:
        P = Pool_(fw, es)
        idt = P.sb([128, 128], F32)
        tid = Tok()
        fw.dma("sp", idt[:], ident, writes=[tid])
        g = P.sb([128, 8], F32)
        fw.dma("sp", g[:], fg.rearrange("(k p) -> p k", p=128), writes=[tid], allow_slow_non_contiguous=True)
        st = norm_state(fw, P)
        xr = Rot([P.sb([128, 8, 512], F32) for _ in range(2)])
        yr = Rot([P.sb([128, 8, 512], F32) for _ in range(2)])
        psr = Rot([P.ps() for _ in range(4)])
        orr = Rot([P.sb([128, DM], F32) for _ in range(3)])
        tout = Tok()
        for (lo, hi, ic) in SEGS[1:]:
            n = hi - lo
            xs, txs = xr.next()
            fw.dma("sp", xs[:, :, :n], XTv[:, :, lo:hi], writes=[txs])
            sq, ones = st["sq"], st["ones"]
            fw.op("act", lambda e, xs=xs: e.activation(out=sq[:], in_=xs[:], func=AF.Square), reads=[txs], writes=[st["tsq"]])
            ps, tps = st["psr"].next()
            for k in range(8):
                fw.op("pe", lambda e, ps=ps, k=k: e.matmul(ps[:], lhsT=ones[:], rhs=sq[:, k, :], start=(k == 0), stop=(k == 7)),
                      reads=[st["tsq"], st["tones"]], writes=[tps])
            rstd_from_ps(fw, st["rs"], st["trs"], ps, tps, n, 1.0 / DM, st["eps"][:, 0:1], st["teps"])
            ys, tys = yr.next()
            for k in range(8):
                fw.op("dve", lambda e, ys=ys, xs=xs, k=k: e.scalar_tensor_tensor(
                    out=ys[:, k, :], in0=xs[:, k, :], scalar=g[:, k:k + 1], in1=st["rs"][:], op0=ALU.mult, op1=ALU.mult),
                    reads=[txs, st["trs"], tid], writes=[tys])
            for blk in range(4):
                ot, tot = orr.next()
                for half in range(2):
                    ps2, tps2 = psr.next()
                    for k in range(4):
                        kk = half * 4 + k
                        fw.op("pe", lambda e, ps2=ps2, k=k, kk=kk, ys=ys, blk=blk: e.transpose(
                            ps2[:, k * 128:(k + 1) * 128], ys[:, kk, blk * 128:(blk + 1) * 128], idt[:]),
                            reads=[tys, tid], writes=[tps2])
                    if half == 0:
                        fw.op("dve", lambda e, ot=ot, ps2=ps2: e.tensor_copy(out=ot[:, 0:512], in_=ps2[:]), reads=[tps2], writes=[tot])
                    else:
                        fw.op("act", lambda e, ot=ot, ps2=ps2: e.copy(out=ot[:, 512:1024], in_=ps2[:]), reads=[tps2], writes=[tot])
                r0 = lo - NCTX + blk * 128
                fw.dma("sp", OUT[r0:r0 + 128, :], ot[:], reads=[tot], writes=[Tok()])
        fw.barrier()


def sin_reduced(fw, P, out, src, tsrc, shape, off, tout):
    ti = P.sb(shape, I32)
    tf = P.sb(shape, F32)
    tt = Tok()
    fw.op("dve", lambda e: e.tensor_scalar(out=ti[:], in0=src, scalar1=off, scalar2=1.0 / (2 * math.pi), op0=ALU.add, op1=ALU.mult),
          reads=[tsrc], writes=[tt])
    fw.op("dve", lambda e: e.tensor_copy(out=tf[:], in_=ti[:]), reads=[tt], writes=[tt])
    fw.op("dve", lambda e: e.scalar_tensor_tensor(out=tf[:], in0=tf[:], scalar=-2 * math.pi, in1=src, op0=ALU.mult, op1=ALU.add),
          reads=[tt, tsrc], writes=[tt])
    fw.op("dve", lambda e: e.tensor_scalar(out=tf[:], in0=tf[:], scalar1=off, scalar2=math.pi, op0=ALU.add, op1=ALU.min), reads=[tt], writes=[tt])
    fw.op("dve", lambda e: e.tensor_scalar(out=tf[:], in0=tf[:], scalar1=-math.pi, scalar2=None, op0=ALU.max), reads=[tt], writes=[tt])
    fw.op("act", lambda e: e.activation(out=out, in_=tf[:], func=AF.Sin), reads=[tt], writes=[tout])


def stage_p2(fw, io):
    ZT = io("ZT", [2048, TE], F32, "in")
    a_re = io("s5_a_re", [2, 16, 64], F32, "in")
    a_im = io("s5_a_im", [2, 16, 64], F32, "in")
    lstep = io("s5_log_step", [2, 16], F32, "in")
    b_re = io("s5_b_re", [2, 16, 64, 16], F32, "in")
    b_im = io("s5_b_im", [2, 16, 64, 16], F32, "in")
    c_re = io("s5_c_re", [2, 16, 16, 64], F32, "in")
    c_im = io("s5_c_im", [2, 16, 16, 64], F32, "in")
    dsk = io("s5_d", [256], F32, "in")
    glu_w = io("s5_glu_w", [256, 256], F32, "in")
    glu_b = io("s5_glu_b", [256], F32, "in")
    out_g = io("s5_out_g", [256], F32, "in")
    S5T = io("S5T", [256, T], F32, "out")
    N = T
    with ExitStack() as es:
        P = Pool_(fw, es)
        are = P.sb([128, 2, 8], F32)
        aim = P.sb([128, 2, 8], F32)
        lst = P.sb([128, 2, 8], F32)
        tpar = Tok()
        for di in range(2):
            fw.dma("sp", are[:, di, :], a_re[di].rearrange("(s g) p -> (g p) s", g=2), writes=[tpar], allow_slow_non_contiguous=True)
            fw.dma("sp", aim[:, di, :], a_im[di].rearrange("(s g) p -> (g p) s", g=2), writes=[tpar], allow_slow_non_contiguous=True)
            for g2 in range(2):
                fw.dma("sp", lst[g2 * 64:(g2 + 1) * 64, di:di + 1, :],
                       lstep[di].rearrange("(s g) -> g s", g=2)[g2:g2 + 1, :].partition_broadcast(64), writes=[tpar],
                       allow_slow_non_contiguous=True)
        sh = [128, 2, 8]
        step = P.sb(sh, F32)
        fw.op("act", lambda e: e.activation(out=step[:], in_=lst[:], func=AF.Exp), reads=[tpar], writes=[tpar])
        er = P.sb(sh, F32)
        th = P.sb(sh, F32)
        fw.op("dve", lambda e: e.tensor_tensor(out=er[:], in0=are[:], in1=step[:], op=ALU.mult), reads=[tpar], writes=[tpar])
        fw.op("act", lambda e: e.activation(out=er[:], in_=er[:], func=AF.Exp), reads=[tpar], writes=[tpar])
        fw.op("dve", lambda e: e.tensor_tensor(out=th[:], in0=aim[:], in1=step[:], op=ALU.mult), reads=[tpar], writes=[tpar])
        sn = P.sb(sh, F32)
        cs = P.sb(sh, F32)
        ttrig = Tok()
        sin_reduced(fw, P, sn[:], th[:], tpar, sh, 0.0, ttrig)
        sin_reduced(fw, P, cs[:], th[:], tpar, sh, math.pi / 2, ttrig)
        PW = P.sb([128, 9, 3, 2, 8], F32)
        tpw = Tok()
        fw.op("dve", lambda e: e.tensor_tensor(out=PW[:, 0, 0], in0=er[:], in1=cs[:], op=ALU.mult), reads=[tpar, ttrig], writes=[tpw])
        fw.op("dve", lambda e: e.tensor_tensor(out=PW[:, 0, 1], in0=er[:], in1=sn[:], op=ALU.mult), reads=[tpar, ttrig], writes=[tpw])
        t1 = P.sb(sh, F32)
        t2 = P.sb(sh, F32)
        for k in range(9):
            fw.op("dve", lambda e, k=k: e.tensor_scalar(out=PW[:, k, 2], in0=PW[:, k, 1], scalar1=-1.0, scalar2=None, op0=ALU.mult),
                  reads=[tpw], writes=[tpw])
            if k == 8:
                break
            fw.op("dve", lambda e, k=k: e.tensor_tensor(out=t1[:], in0=PW[:, k, 0], in1=PW[:, k, 0], op=ALU.mult), reads=[tpw], writes=[tpw])
            fw.op("dve", lambda e, k=k: e.tensor_tensor(out=t2[:], in0=PW[:, k, 1], in1=PW[:, k, 1], op=ALU.mult), reads=[tpw], writes=[tpw])
            fw.op("dve", lambda e, k=k: e.tensor_tensor(out=PW[:, k + 1, 0], in0=t1[:], in1=t2[:], op=ALU.subtract), reads=[tpw], writes=[tpw])
            fw.op("dve", lambda e, k=k: e.scalar_tensor_tensor(out=PW[:, k + 1, 1], in0=PW[:, k, 0], scalar=2.0, in1=PW[:, k, 1],
                                                               op0=ALU.mult, op1=ALU.mult), reads=[tpw], writes=[tpw])
        br = P.sb(sh, F32)
        bi = P.sb(sh, F32)
        nbi = P.sb(sh, F32)
        den = P.sb(sh, F32)
        nr = P.sb(sh, F32)
        tb = Tok()
        fw.op("dve", lambda e: e.tensor_tensor(out=den[:], in0=are[:], in1=are[:], op=ALU.mult), reads=[tpar], writes=[tb])
        fw.op("dve", lambda e: e.tensor_tensor(out=t1[:], in0=aim[:], in1=aim[:], op=ALU.mult), reads=[tpar, tpw], writes=[tpw])
        fw.op("dve", lambda e: e.tensor_tensor(out=den[:], in0=den[:], in1=t1[:], op=ALU.add), reads=[tb, tpw], writes=[tb])
        fw.op("dve", lambda e: e.reciprocal(out=den[:], in_=den[:]), reads=[tb], writes=[tb])
        fw.op("dve", lambda e: e.tensor_scalar(out=nr[:], in0=PW[:, 0, 0], scalar1=-1.0, scalar2=None, op0=ALU.add), reads=[tpw], writes=[tb])
        fw.op("dve", lambda e: e.tensor_tensor(out=t1[:], in0=nr[:], in1=are[:], op=ALU.mult), reads=[tb, tpar, tpw], writes=[tpw])
        fw.op("dve", lambda e: e.tensor_tensor(out=t2[:], in0=PW[:, 0, 1], in1=aim[:], op=ALU.mult), reads=[tpw, tpar], writes=[tpw])
        fw.op("dve", lambda e: e.tensor_tensor(out=t1[:], in0=t1[:], in1=t2[:], op=ALU.add), reads=[tpw], writes=[tpw])
        fw.op("dve", lambda e: e.tensor_tensor(out=br[:], in0=t1[:], in1=den[:], op=ALU.mult), reads=[tpw, tb], writes=[tb])
        fw.op("dve", lambda e: e.tensor_tensor(out=t1[:], in0=PW[:, 0, 1], in1=are[:], op=ALU.mult), reads=[tpw, tpar, tb], writes=[tpw])
        fw.op("dve", lambda e: e.tensor_tensor(out=t2[:], in0=nr[:], in1=aim[:], op=ALU.mult), reads=[tb, tpar, tpw], writes=[tpw])
        fw.op("dve", lambda e: e.tensor_tensor(out=t1[:], in0=t1[:], in1=t2[:], op=ALU.subtract), reads=[tpw], writes=[tpw])
        fw.op("dve", lambda e: e.tensor_tensor(out=bi[:], in0=t1[:], in1=den[:], op=ALU.mult), reads=[tpw, tb], writes=[tb])
        fw.op("dve", lambda e: e.tensor_scalar(out=nbi[:], in0=bi[:], scalar1=-1.0, scalar2=None, op0=ALU.mult), reads=[tb], writes=[tb])
        BTb = P.sb([128, 2, 2, 8, 128], BF16)
        CTb = P.sb([128, 2, 2, 8, 128], BF16)
        tBT = Tok()
        tCT = Tok()
        with ExitStack() as es2:
            P2 = Pool_(fw, es2)
            BTf = P2.sb([128, 2, 2, 8, 128], F32)
            CTf = P2.sb([128, 2, 2, 8, 128], F32)
            CT2 = P2.sb([128, 2, 2, 8, 128], F32)
            tf1, tf2 = Tok(), Tok()
            fw.op("pool", lambda e: e.memset(BTf[:], 0.0), writes=[tf1])
            fw.op("pool", lambda e: e.memset(CTf[:], 0.0), writes=[tf2])
            for di in range(2):
                for ri, (bsrc, csrc) in enumerate(((b_re, c_re), (b_im, c_im))):
                    for g in range(16):
                        s, g2 = g // 2, g % 2
                        r0 = (g % 8) * 16
                        fw.dma("sp", BTf[r0:r0 + 16, di, ri, s, g2 * 64:(g2 + 1) * 64], bsrc[di, g].rearrange("p h -> h p"),
                               writes=[tf1], allow_slow_non_contiguous=True)
                        fw.dma("sp", CTf[g2 * 64:(g2 + 1) * 64, di, ri, s, r0:r0 + 16], csrc[di, g].rearrange("h p -> p h"),
                               writes=[tf2], allow_slow_non_contiguous=True)
            fw.op("act", lambda e: e.copy(out=BTb[:], in_=BTf[:]), reads=[tf1], writes=[tBT])
            bsh = [128, 2, 8, 128]
            brb = br[:].unsqueeze(3).to_broadcast(bsh)
            nbib = nbi[:].unsqueeze(3).to_broadcast(bsh)
            tc2 = Tok()
            fw.op("dve", lambda e: e.tensor_tensor(out=CT2[:, :, 0], in0=CTf[:, :, 0], in1=brb, op=ALU.mult), reads=[tf2, tb], writes=[tc2])
            fw.op("pool", lambda e: e.tensor_tensor(out=CT2[:, :, 1], in0=CTf[:, :, 1], in1=nbib, op=ALU.mult), reads=[tf2, tb], writes=[tc2])
            fw.op("dve", lambda e: e.tensor_tensor(out=CT2[:, :, 0], in0=CT2[:, :, 0], in1=CT2[:, :, 1], op=ALU.add), reads=[tc2], writes=[tc2])
            fw.op("act", lambda e: e.copy(out=CTb[:, :, 0], in_=CT2[:, :, 0]), reads=[tc2], writes=[tCT])
            fw.op("dve", lambda e: e.tensor_tensor(out=CT2[:, :, 0], in0=CTf[:, :, 0], in1=nbib, op=ALU.mult), reads=[tf2, tb, tCT, tc2], writes=[tc2])
            fw.op("pool", lambda e: e.tensor_tensor(out=CT2[:, :, 1], in0=CTf[:, :, 1], in1=brb, op=ALU.mult), reads=[tf2, tb, tc2], writes=[tc2])
            fw.op("dve", lambda e: e.tensor_tensor(out=CT2[:, :, 0], in0=CT2[:, :, 0], in1=CT2[:, :, 1], op=ALU.subtract), reads=[tc2], writes=[tc2])
            fw.op("act", lambda e: e.copy(out=CTb[:, :, 1], in_=CT2[:, :, 0]), reads=[tc2], writes=[tCT])
            fw.barrier()
        ub = P.sb([128, 2, TE], BF16)
        tub = Tok()
        for ct in range(2):
            fw.dma("pool", ub[:, ct, :], ZT[ct * 128:(ct + 1) * 128, :], writes=[tub])
        dk = P.sb([128, 2], F32)
        tdk = Tok()
        fw.dma("sp", dk[:], dsk.rearrange("(c p) -> p c", p=128), writes=[tdk], allow_slow_non_contiguous=True)
        y = P.sb([128, 2, T], F32)
        ty = Tok()
        for ct in range(2):
            fw.dma("sp", y[:, ct, :], ZT[ct * 128:(ct + 1) * 128, 0:T], writes=[ty])
        for ct in range(2):
            fw.op("pool", lambda e, ct=ct: e.tensor_scalar(out=y[:, ct, :], in0=y[:, ct, :], scalar1=dk[:, ct:ct + 1], scalar2=None, op0=ALU.mult),
                  reads=[ty, tdk], writes=[ty])
        Xs = Rot([P.sb([128, 2, N], F32) for _ in range(4)])
        xbs = Rot([P.sb([128, 2, N], BF16) for _ in range(2)])
        psr = Rot([P.ps() for _ in range(4)])
        psy = Rot([P.ps() for _ in range(2)])
        tmpy = Rot([P.sb([128, 512], F32) for _ in range(2)])
        CH = [(0, 512), (512, 1024), (1024, 1536), (1536, 2048), (2048, 2304)]

        def prep(di, s):
            off = 0 if di == 0 else 256
            ct = s // 4
            X, tX = Xs.next()
            for ri in range(2):
                for ci, (c0, c1) in enumerate(CH):
                    n = c1 - c0
                    ps, tps = psr.next()
                    fw.op("pe", lambda e, ps=ps, ri=ri, c0=c0, c1=c1, n=n: e.matmul(
                        ps[:, :n], lhsT=BTb[:, di, ri, s, :], rhs=ub[:, ct, off + c0:off + c1], start=True, stop=True),
                        reads=[tBT, tub], writes=[tps])
                    fw.op("act", lambda e, ps=ps, ri=ri, c0=c0, c1=c1, n=n: e.copy(out=X[:, ri, c0:c1], in_=ps[:, :n]),
                          reads=[tps], writes=[tX])
            return X, tX

        def scan_gen(di, s, X, tX):
            def cstep(w_re, w_im, r_re, r_im, k):
                pr = PW[:, k, 0, di, s:s + 1]
                pi = PW[:, k, 1, di, s:s + 1]
                npi = PW[:, k, 2, di, s:s + 1]
                for (o, i0, sc) in ((w_re, r_re, pr), (w_re, r_im, npi), (w_im, r_im, pr), (w_im, r_re, pi)):
                    fw.op("dve", lambda e, o=o, i0=i0, sc=sc: e.scalar_tensor_tensor(out=o, in0=i0, scalar=sc, in1=o, op0=ALU.mult, op1=ALU.add),
                          reads=[tX, tpw], writes=[tX])
                    yield
            for k in range(8):
                st_ = 1 << k
                Xv = [X[:, ri, :].rearrange("p (m c) -> p m c", c=2 * st_) for ri in range(2)]
                if di == 0:
                    yield from cstep(Xv[0][:, :, 2 * st_ - 1], Xv[1][:, :, 2 * st_ - 1], Xv[0][:, :, st_ - 1], Xv[1][:, :, st_ - 1], k)
                else:
                    yield from cstep(Xv[0][:, :, 0], Xv[1][:, :, 0], Xv[0][:, :, st_], Xv[1][:, :, st_], k)
            for i in (range(1, 9) if di == 0 else range(7, -1, -1)):
                if di == 0:
                    w, r = 256 * i + 255, 256 * (i - 1) + 255
                else:
                    w, r = 256 * i, 256 * (i + 1)
                yield from cstep(X[:, 0, w:w + 1], X[:, 1, w:w + 1], X[:, 0, r:r + 1], X[:, 1, r:r + 1], 8)
            for k in range(7, -1, -1):
                st_ = 1 << k
                Xv = [X[:, ri, :].rearrange("p (m c) -> p m c", c=2 * st_) for ri in range(2)]
                if di == 0:
                    yield from cstep(Xv[0][:, 1:, st_ - 1], Xv[1][:, 1:, st_ - 1], Xv[0][:, :-1, 2 * st_ - 1], Xv[1][:, :-1, 2 * st_ - 1], k)
                else:
                    yield from cstep(Xv[0][:, :-1, st_], Xv[1][:, :-1, st_], Xv[0][:, 1:, 0], Xv[1][:, 1:, 0], k)

        def fin(di, s, X, tX):
            ct = s // 4
            xb, txb = xbs.next()
            fw.op("act", lambda e: e.copy(out=xb[:], in_=X[:]), reads=[tX], writes=[txb])
            for (c0, c1) in CH:
                n = c1 - c0
                if di == 0:
                    y0 = c0
                else:
                    y0 = c0 + 256 if c0 < 2048 else 0
                ps, tps = psy.next()
                for ri in range(2):
                    fw.op("pe", lambda e, ps=ps, ri=ri, c0=c0, c1=c1, n=n: e.matmul(
                        ps[:, :n], lhsT=CTb[:, di, ri, s, :], rhs=xb[:, ri, c0:c1], start=(ri == 0), stop=(ri == 1)),
                        reads=[tCT, txb], writes=[tps])
                tm, ttm = tmpy.next()
                fw.op("act", lambda e, tm=tm, ps=ps, n=n: e.copy(out=tm[:, :n], in_=ps[:, :n]), reads=[tps], writes=[ttm])
                fw.op("pool", lambda e, tm=tm, y0=y0, n=n: e.tensor_tensor(out=y[:, ct, y0:y0 + n], in0=y[:, ct, y0:y0 + n], in1=tm[:, :n], op=ALU.add),
                      reads=[ttm, ty], writes=[ty])

        pairs = [(di, s0, s0 + 1) for di in range(2) for s0 in range(0, 8, 2)]
        nxt = [prep(pairs[0][0], pairs[0][1]), prep(pairs[0][0], pairs[0][2])]
        for pi_, (di, s0, s1) in enumerate(pairs):
            cur = nxt
            if pi_ + 1 < len(pairs):
                d2, a0, a1 = pairs[pi_ + 1]
                nxt = [prep(d2, a0), prep(d2, a1)]
            gens = [scan_gen(di, s0, *cur[0]), scan_gen(di, s1, *cur[1])]
            alive = [True, True]
            while any(alive):
                for gi in range(2):
                    if alive[gi]:
                        try:
                            next(gens[gi])
                        except StopIteration:
                            alive[gi] = False
            fin(di, s0, *cur[0])
            fin(di, s1, *cur[1])
        tg = Tok()
        gw = load_w_bf16(fw, P, glu_w, 256, 256, tg, "gluw")
        gb = P.sb([128, 2], F32)
        og = P.sb([128, 2], F32)
        fw.dma("sp", gb[:], glu_b.rearrange("(c p) -> p c", p=128), writes=[tg], allow_slow_non_contiguous=True)
        fw.dma("sp", og[:], out_g.rearrange("(c p) -> p c", p=128), writes=[tg], allow_slow_non_contiguous=True)
        ones = P.sb([128, 128], BF16)
        eps = P.sb([128, 1], F32)
        fw.op("pool", lambda e: e.memset(ones[:], 1.0), writes=[tg])
        fw.op("pool", lambda e: e.memset(eps[:], RMS_EPS), writes=[tg])
        a32 = P.sb([128, 2, 512], F32)
        ab = P.sb([128, 2, 512], BF16)
        w1 = P.sb([128, 2, 512], F32)
        w2 = P.sb([128, 2, 512], F32)
        sqb = P.sb([128, 2, 512], BF16)
        rs = P.sb([128, 512], F32)
        ta, tw1, trs = Tok(), Tok(), Tok()
        stg = Rot([P.sb([128, 512], F32) for _ in range(2)])
        tout = Tok()
        C1 = math.sqrt(2.0 / math.pi)
        for (lo, hi, ic) in SEGS:
            n = hi - lo
            yv = y[:, :, lo:hi]
            fw.op("act", lambda e, yv=yv, n=n: e.activation(out=w1[:, :, :n], in_=yv, func=AF.Square), reads=[ty], writes=[tw1])
            fw.op("dve", lambda e, n=n: e.tensor_scalar(out=w1[:, :, :n], in0=w1[:, :, :n], scalar1=0.044715 * C1, scalar2=C1, op0=ALU.mult, op1=ALU.add),
                  reads=[tw1], writes=[tw1])
            fw.op("dve", lambda e, yv=yv, n=n: e.tensor_tensor(out=w1[:, :, :n], in0=w1[:, :, :n], in1=yv, op=ALU.mult), reads=[tw1, ty], writes=[tw1])
            fw.op("act", lambda e, n=n: e.activation(out=w1[:, :, :n], in_=w1[:, :, :n], func=AF.Tanh), reads=[tw1], writes=[tw1])
            fw.op("dve", lambda e, n=n: e.tensor_scalar(out=w1[:, :, :n], in0=w1[:, :, :n], scalar1=0.5, scalar2=0.5, op0=ALU.mult, op1=ALU.add),
                  reads=[tw1], writes=[tw1])
            fw.op("dve", lambda e, yv=yv, n=n: e.tensor_tensor(out=a32[:, :, :n], in0=w1[:, :, :n], in1=yv, op=ALU.mult), reads=[tw1, ty], writes=[ta])
            fw.op("act", lambda e, n=n: e.copy(out=ab[:, :, :n], in_=a32[:, :, :n]), reads=[ta], writes=[ta])
            for nt in range(2):
                ps, tps = psr.next()
                for k in range(2):
                    fw.op("pe", lambda e, ps=ps, k=k, nt=nt, n=n: e.matmul(ps[:, :n], lhsT=gw[:, k, nt * 128:(nt + 1) * 128], rhs=ab[:, k, :n],
                                                                      start=(k == 0), stop=(k == 1)), reads=[tg, ta], writes=[tps])
                fw.op("act", lambda e, ps=ps, nt=nt, n=n: e.activation(out=w2[:, nt, :n], in_=ps[:, :n], func=AF.Sigmoid, bias=gb[:, nt:nt + 1]),
                      reads=[tps, tg], writes=[tw1])
            fw.op("dve", lambda e, n=n: e.tensor_tensor(out=w2[:, :, :n], in0=w2[:, :, :n], in1=a32[:, :, :n], op=ALU.mult), reads=[tw1, ta], writes=[tw1])
            fw.op("act", lambda e, n=n: e.activation(out=sqb[:, :, :n], in_=w2[:, :, :n], func=AF.Square), reads=[tw1], writes=[tw1])
            ps, tps = psr.next()
            for k in range(2):
                fw.op("pe", lambda e, ps=ps, k=k, n=n: e.matmul(ps[:, :n], lhsT=ones[:], rhs=sqb[:, k, :n], start=(k == 0), stop=(k == 1)),
                      reads=[tw1, tg], writes=[tps])
            rstd_from_ps(fw, rs, trs, ps, tps, n, 1.0 / 256, eps[:, 0:1], tg)
            for k in range(2):
                sg, tsg = stg.next()
                fw.op("dve", lambda e, sg=sg, k=k, n=n: e.scalar_tensor_tensor(out=sg[:, :n], in0=w2[:, k, :n], scalar=og[:, k:k + 1], in1=rs[:, :n],
                                                                             op0=ALU.mult, op1=ALU.mult), reads=[tw1, trs, tg], writes=[tsg])
                fw.dma("sp", S5T[k * 128:(k + 1) * 128, lo:hi], sg[:, :n], reads=[tsg], writes=[Tok()])
        fw.barrier()


def gen_p2(fw, io):
    ZT = io("ZT", [2048, TE], F32, "in")
    a_re = io("s5_a_re", [2, 16, 64], F32, "in")
    a_im = io("s5_a_im", [2, 16, 64], F32, "in")
    lstep = io("s5_log_step", [2, 16], F32, "in")
    b_re = io("s5_b_re", [2, 16, 64, 16], F32, "in")
    b_im = io("s5_b_im", [2, 16, 64, 16], F32, "in")
    c_re = io("s5_c_re", [2, 16, 16, 64], F32, "in")
    c_im = io("s5_c_im", [2, 16, 16, 64], F32, "in")
    dsk = io("s5_d", [256], F32, "in")
    glu_w = io("s5_glu_w", [256, 256], F32, "in")
    glu_b = io("s5_glu_b", [256], F32, "in")
    out_g = io("s5_out_g", [256], F32, "in")
    S5T = io("S5T", [256, T], F32, "out")
    N = T
    with ExitStack() as es:
        P = Pool_(fw, es)
        are = P.sb([128, 2, 8], F32)
        aim = P.sb([128, 2, 8], F32)
        lst = P.sb([128, 2, 8], F32)
        tpar = Tok()
        for di in range(2):
            fw.dma("sp", are[:, di, :], a_re[di].rearrange("(s g) p -> (g p) s", g=2), writes=[tpar], allow_slow_non_contiguous=True)
            fw.dma("sp", aim[:, di, :], a_im[di].rearrange("(s g) p -> (g p) s", g=2), writes=[tpar], allow_slow_non_contiguous=True)
            for g2 in range(2):
                fw.dma("sp", lst[g2 * 64:(g2 + 1) * 64, di:di + 1, :],
                       lstep[di].rearrange("(s g) -> g s", g=2)[g2:g2 + 1, :].partition_broadcast(64), writes=[tpar],
                       allow_slow_non_contiguous=True)
        sh = [128, 2, 8]
        step = P.sb(sh, F32)
        fw.op("act", lambda e: e.activation(out=step[:], in_=lst[:], func=AF.Exp), reads=[tpar], writes=[tpar])
        er = P.sb(sh, F32)
        th = P.sb(sh, F32)
        fw.op("dve", lambda e: e.tensor_tensor(out=er[:], in0=are[:], in1=step[:], op=ALU.mult), reads=[tpar], writes=[tpar])
        fw.op("act", lambda e: e.activation(out=er[:], in_=er[:], func=AF.Exp), reads=[tpar], writes=[tpar])
        fw.op("dve", lambda e: e.tensor_tensor(out=th[:], in0=aim[:], in1=step[:], op=ALU.mult), reads=[tpar], writes=[tpar])
        sn = P.sb(sh, F32)
        cs = P.sb(sh, F32)
        ttrig = Tok()
        sin_reduced(fw, P, sn[:], th[:], tpar, sh, 0.0, ttrig)
        sin_reduced(fw, P, cs[:], th[:], tpar, sh, math.pi / 2, ttrig)
        PW = P.sb([128, 9, 3, 2, 8], F32)
        tpw = Tok()
        fw.op("dve", lambda e: e.tensor_tensor(out=PW[:, 0, 0], in0=er[:], in1=cs[:], op=ALU.mult), reads=[tpar, ttrig], writes=[tpw])
        fw.op("dve", lambda e: e.tensor_tensor(out=PW[:, 0, 1], in0=er[:], in1=sn[:], op=ALU.mult), reads=[tpar, ttrig], writes=[tpw])
        t1 = P.sb(sh, F32)
        t2 = P.sb(sh, F32)
        for k in range(9):
            fw.op("dve", lambda e, k=k: e.tensor_scalar(out=PW[:, k, 2], in0=PW[:, k, 1], scalar1=-1.0, scalar2=None, op0=ALU.mult),
                  reads=[tpw], writes=[tpw])
            if k == 8:
                break
            fw.op("dve", lambda e, k=k: e.tensor_tensor(out=t1[:], in0=PW[:, k, 0], in1=PW[:, k, 0], op=ALU.mult), reads=[tpw], writes=[tpw])
            fw.op("dve", lambda e, k=k: e.tensor_tensor(out=t2[:], in0=PW[:, k, 1], in1=PW[:, k, 1], op=ALU.mult), reads=[tpw], writes=[tpw])
            fw.op("dve", lambda e, k=k: e.tensor_tensor(out=PW[:, k + 1, 0], in0=t1[:], in1=t2[:], op=ALU.subtract), reads=[tpw], writes=[tpw])
            fw.op("dve", lambda e, k=k: e.scalar_tensor_tensor(out=PW[:, k + 1, 1], in0=PW[:, k, 0], scalar=2.0, in1=PW[:, k, 1],
                                                               op0=ALU.mult, op1=ALU.mult), reads=[tpw], writes=[tpw])
        br = P.sb(sh, F32)
        bi = P.sb(sh, F32)
        nbi = P.sb(sh, F32)
        den = P.sb(sh, F32)
        nr = P.sb(sh, F32)
        tb = Tok()
        fw.op("dve", lambda e: e.tensor_tensor(out=den[:], in0=are[:], in1=are[:], op=ALU.mult), reads=[tpar], writes=[tb])
        fw.op("dve", lambda e: e.tensor_tensor(out=t1[:], in0=aim[:], in1=aim[:], op=ALU.mult), reads=[tpar, tpw], writes=[tpw])
        fw.op("dve", lambda e: e.tensor_tensor(out=den[:], in0=den[:], in1=t1[:], op=ALU.add), reads=[tb, tpw], writes=[tb])
        fw.op("dve", lambda e: e.reciprocal(out=den[:], in_=den[:]), reads=[tb], writes=[tb])
        fw.op("dve", lambda e: e.tensor_scalar(out=nr[:], in0=PW[:, 0, 0], scalar1=-1.0, scalar2=None, op0=ALU.add), reads=[tpw], writes=[tb])
        fw.op("dve", lambda e: e.tensor_tensor(out=t1[:], in0=nr[:], in1=are[:], op=ALU.mult), reads=[tb, tpar, tpw], writes=[tpw])
        fw.op("dve", lambda e: e.tensor_tensor(out=t2[:], in0=PW[:, 0, 1], in1=aim[:], op=ALU.mult), reads=[tpw, tpar], writes=[tpw])
        fw.op("dve", lambda e: e.tensor_tensor(out=t1[:], in0=t1[:], in1=t2[:], op=ALU.add), reads=[tpw], writes=[tpw])
        fw.op("dve", lambda e: e.tensor_tensor(out=br[:], in0=t1[:], in1=den[:], op=ALU.mult), reads=[tpw, tb], writes=[tb])
        fw.op("dve", lambda e: e.tensor_tensor(out=t1[:], in0=PW[:, 0, 1], in1=are[:], op=ALU.mult), reads=[tpw, tpar, tb], writes=[tpw])
        fw.op("dve", lambda e: e.tensor_tensor(out=t2[:], in0=nr[:], in1=aim[:], op=ALU.mult), reads=[tb, tpar, tpw], writes=[tpw])
        fw.op("dve", lambda e: e.tensor_tensor(out=t1[:], in0=t1[:], in1=t2[:], op=ALU.subtract), reads=[tpw], writes=[tpw])
        fw.op("dve", lambda e: e.tensor_tensor(out=bi[:], in0=t1[:], in1=den[:], op=ALU.mult), reads=[tpw, tb], writes=[tb])
        fw.op("dve", lambda e: e.tensor_scalar(out=nbi[:], in0=bi[:], scalar1=-1.0, scalar2=None, op0=ALU.mult), reads=[tb], writes=[tb])
        BTb = P.sb([128, 2, 2, 8, 128], BF16)
        CTb = P.sb([128, 2, 2, 8, 128], BF16)
        tBT = Tok()
        tCT = Tok()
        with ExitStack() as es2:
            P2 = Pool_(fw, es2)
            BTf = P2.sb([128, 2, 2, 8, 128], F32)
            CTf = P2.sb([128, 2, 2, 8, 128], F32)
            CT2 = P2.sb([128, 2, 2, 8, 128], F32)
            tf1, tf2 = Tok(), Tok()
            fw.op("pool", lambda e: e.memset(BTf[:], 0.0), writes=[tf1])
            fw.op("pool", lambda e: e.memset(CTf[:], 0.0), writes=[tf2])
            for di in range(2):
                for ri, (bsrc, csrc) in enumerate(((b_re, c_re), (b_im, c_im))):
                    for g in range(16):
                        s, g2 = g // 2, g % 2
                        r0 = (g % 8) * 16
                        fw.dma("sp", BTf[r0:r0 + 16, di, ri, s, g2 * 64:(g2 + 1) * 64], bsrc[di, g].rearrange("p h -> h p"),
                               writes=[tf1], allow_slow_non_contiguous=True)
                        fw.dma("sp", CTf[g2 * 64:(g2 + 1) * 64, di, ri, s, r0:r0 + 16], csrc[di, g].rearrange("h p -> p h"),
                               writes=[tf2], allow_slow_non_contiguous=True)
            fw.op("act", lambda e: e.copy(out=BTb[:], in_=BTf[:]), reads=[tf1], writes=[tBT])
            bsh = [128, 2, 8, 128]
            brb = br[:].unsqueeze(3).to_broadcast(bsh)
            nbib = nbi[:].unsqueeze(3).to_broadcast(bsh)
            tc2 = Tok()
            fw.op("dve", lambda e: e.tensor_tensor(out=CT2[:, :, 0], in0=CTf[:, :, 0], in1=brb, op=ALU.mult), reads=[tf2, tb], writes=[tc2])
            fw.op("pool", lambda e: e.tensor_tensor(out=CT2[:, :, 1], in0=CTf[:, :, 1], in1=nbib, op=ALU.mult), reads=[tf2, tb], writes=[tc2])
            fw.op("dve", lambda e: e.tensor_tensor(out=CT2[:, :, 0], in0=CT2[:, :, 0], in1=CT2[:, :, 1], op=ALU.add), reads=[tc2], writes=[tc2])
            fw.op("act", lambda e: e.copy(out=CTb[:, :, 0], in_=CT2[:, :, 0]), reads=[tc2], writes=[tCT])
            fw.op("dve", lambda e: e.tensor_tensor(out=CT2[:, :, 0], in0=CTf[:, :, 0], in1=nbib, op=ALU.mult), reads=[tf2, tb, tCT, tc2], writes=[tc2])
            fw.op("pool", lambda e: e.tensor_tensor(out=CT2[:, :, 1], in0=CTf[:, :, 1], in1=brb, op=ALU.mult), reads=[tf2, tb, tc2], writes=[tc2])
            fw.op("dve", lambda e: e.tensor_tensor(out=CT2[:, :, 0], in0=CT2[:, :, 0], in1=CT2[:, :, 1], op=ALU.subtract), reads=[tc2], writes=[tc2])
            fw.op("act", lambda e: e.copy(out=CTb[:, :, 1], in_=CT2[:, :, 0]), reads=[tc2], writes=[tCT])
            fw.barrier()
        ub = P.sb([128, 2, TE], BF16)
        tub = Tok()
        for ct in range(2):
            fw.dma("pool", ub[:, ct, :], ZT[ct * 128:(ct + 1) * 128, :], writes=[tub])
        dk = P.sb([128, 2], F32)
        tdk = Tok()
        fw.dma("sp", dk[:], dsk.rearrange("(c p) -> p c", p=128), writes=[tdk], allow_slow_non_contiguous=True)
        y = P.sb([128, 2, T], F32)
        ty = Tok()
        for ct in range(2):
            fw.dma("sp", y[:, ct, :], ZT[ct * 128:(ct + 1) * 128, 0:T], writes=[ty])
        for ct in range(2):
            fw.op("pool", lambda e, ct=ct: e.tensor_scalar(out=y[:, ct, :], in0=y[:, ct, :], scalar1=dk[:, ct:ct + 1], scalar2=None, op0=ALU.mult),
                  reads=[ty, tdk], writes=[ty])
        Xs = Rot([P.sb([128, 2, N], F32) for _ in range(2)])
        xbs = Rot([P.sb([128, 2, N], BF16) for _ in range(2)])
        psr = Rot([P.ps() for _ in range(2)])
        psy = Rot([P.ps() for _ in range(1)])
        tmpy = Rot([P.sb([128, 512], F32) for _ in range(2)])
        CH = [(0, 512), (512, 1024), (1024, 1536), (1536, 2048), (2048, 2304)]

        def prep(di, s):
            off = 0 if di == 0 else 256
            ct = s // 4
            X, tX = Xs.next()
            for ri in range(2):
                for ci, (c0, c1) in enumerate(CH):
                    n = c1 - c0
                    ps, tps = psr.next()
                    fw.op("pe", lambda e, ps=ps, ri=ri, c0=c0, c1=c1, n=n: e.matmul(
                        ps[:, :n], lhsT=BTb[:, di, ri, s, :], rhs=ub[:, ct, off + c0:off + c1], start=True, stop=True),
                        reads=[tBT, tub], writes=[tps])
                    fw.op("act", lambda e, ps=ps, ri=ri, c0=c0, c1=c1, n=n: e.copy(out=X[:, ri, c0:c1], in_=ps[:, :n]),
                          reads=[tps], writes=[tX])
            return X, tX

        def scan_gen(di, s, X, tX):
            def cstep(w_re, w_im, r_re, r_im, k):
                pr = PW[:, k, 0, di, s:s + 1]
                pi = PW[:, k, 1, di, s:s + 1]
                npi = PW[:, k, 2, di, s:s + 1]
                for (o, i0, sc) in ((w_re, r_re, pr), (w_re, r_im, npi), (w_im, r_im, pr), (w_im, r_re, pi)):
                    fw.op("dve", lambda e, o=o, i0=i0, sc=sc: e.scalar_tensor_tensor(out=o, in0=i0, scalar=sc, in1=o, op0=ALU.mult, op1=ALU.add),
                          reads=[tX, tpw], writes=[tX])
                    yield
            for k in range(8):
                st_ = 1 << k
                Xv = [X[:, ri, :].rearrange("p (m c) -> p m c", c=2 * st_) for ri in range(2)]
                if di == 0:
                    yield from cstep(Xv[0][:, :, 2 * st_ - 1], Xv[1][:, :, 2 * st_ - 1], Xv[0][:, :, st_ - 1], Xv[1][:, :, st_ - 1], k)
                else:
                    yield from cstep(Xv[0][:, :, 0], Xv[1][:, :, 0], Xv[0][:, :, st_], Xv[1][:, :, st_], k)
            for i in (range(1, 9) if di == 0 else range(7, -1, -1)):
                if di == 0:
                    w, r = 256 * i + 255, 256 * (i - 1) + 255
                else:
                    w, r = 256 * i, 256 * (i + 1)
                yield from cstep(X[:, 0, w:w + 1], X[:, 1, w:w + 1], X[:, 0, r:r + 1], X[:, 1, r:r + 1], 8)
            for k in range(7, -1, -1):
                st_ = 1 << k
                Xv = [X[:, ri, :].rearrange("p (m c) -> p m c", c=2 * st_) for ri in range(2)]
                if di == 0:
                    yield from cstep(Xv[0][:, 1:, st_ - 1], Xv[1][:, 1:, st_ - 1], Xv[0][:, :-1, 2 * st_ - 1], Xv[1][:, :-1, 2 * st_ - 1], k)
                else:
                    yield from cstep(Xv[0][:, :-1, st_], Xv[1][:, :-1, st_], Xv[0][:, 1:, 0], Xv[1][:, 1:, 0], k)

        def fin(di, s, X, tX):
            ct = s // 4
            xb, txb = xbs.next()
            fw.op("act", lambda e: e.copy(out=xb[:], in_=X[:]), reads=[tX], writes=[txb])
            for (c0, c1) in CH:
                n = c1 - c0
                if di == 0:
                    y0 = c0
                else:
                    y0 = c0 + 256 if c0 < 2048 else 0
                ps, tps = psy.next()
                for ri in range(2):
                    fw.op("pe", lambda e, ps=ps, ri=ri, c0=c0, c1=c1, n=n: e.matmul(
                        ps[:, :n], lhsT=CTb[:, di, ri, s, :], rhs=xb[:, ri, c0:c1], start=(ri == 0), stop=(ri == 1)),
                        reads=[tCT, txb], writes=[tps])
                tm, ttm = tmpy.next()
                fw.op("act", lambda e, tm=tm, ps=ps, n=n: e.copy(out=tm[:, :n], in_=ps[:, :n]), reads=[tps], writes=[ttm])
                fw.op("pool", lambda e, tm=tm, y0=y0, n=n: e.tensor_tensor(out=y[:, ct, y0:y0 + n], in0=y[:, ct, y0:y0 + n], in1=tm[:, :n], op=ALU.add),
                      reads=[ttm, ty], writes=[ty])

        pairs = [(di, s0, s0 + 1) for di in range(2) for s0 in range(0, 8, 2)]
        yield "core"
        for pi_, (di, s0, s1) in enumerate(pairs):
            cur = [prep(di, s0), prep(di, s1)]
            yield
            gens = [scan_gen(di, s0, *cur[0]), scan_gen(di, s1, *cur[1])]
            alive = [True, True]
            while any(alive):
                for gi in range(2):
                    if alive[gi]:
                        try:
                            next(gens[gi])
                            yield
                        except StopIteration:
                            alive[gi] = False
            fin(di, s0, *cur[0])
            yield
            fin(di, s1, *cur[1])
            yield
        yield "fin"
        tg = Tok()
        gw = load_w_bf16(fw, P, glu_w, 256, 256, tg, "gluw")
        gb = P.sb([128, 2], F32)
        og = P.sb([128, 2], F32)
        fw.dma("sp", gb[:], glu_b.rearrange("(c p) -> p c", p=128), writes=[tg], allow_slow_non_contiguous=True)
        fw.dma("sp", og[:], out_g.rearrange("(c p) -> p c", p=128), writes=[tg], allow_slow_non_contiguous=True)
        ones = P.sb([128, 128], BF16)
        eps = P.sb([128, 1], F32)
        fw.op("pool", lambda e: e.memset(ones[:], 1.0), writes=[tg])
        fw.op("pool", lambda e: e.memset(eps[:], RMS_EPS), writes=[tg])
        a32 = P.sb([128, 2, 512], F32)
        ab = P.sb([128, 2, 512], BF16)
        w1 = P.sb([128, 2, 512], F32)
        w2 = P.sb([128, 2, 512], F32)
        sqb = P.sb([128, 2, 512], BF16)
        rs = P.sb([128, 512], F32)
        ta, tw1, trs = Tok(), Tok(), Tok()
        stg = Rot([P.sb([128, 512], F32) for _ in range(2)])
        tout = Tok()
        C1 = math.sqrt(2.0 / math.pi)
        for (lo, hi, ic) in SEGS:
            n = hi - lo
            yv = y[:, :, lo:hi]
            fw.op("act", lambda e, yv=yv, n=n: e.activation(out=w1[:, :, :n], in_=yv, func=AF.Square), reads=[ty], writes=[tw1])
            fw.op("dve", lambda e, n=n: e.tensor_scalar(out=w1[:, :, :n], in0=w1[:, :, :n], scalar1=0.044715 * C1, scalar2=C1, op0=ALU.mult, op1=ALU.add),
                  reads=[tw1], writes=[tw1])
            fw.op("dve", lambda e, yv=yv, n=n: e.tensor_tensor(out=w1[:, :, :n], in0=w1[:, :, :n], in1=yv, op=ALU.mult), reads=[tw1, ty], writes=[tw1])
            fw.op("act", lambda e, n=n: e.activation(out=w1[:, :, :n], in_=w1[:, :, :n], func=AF.Tanh), reads=[tw1], writes=[tw1])
            fw.op("dve", lambda e, n=n: e.tensor_scalar(out=w1[:, :, :n], in0=w1[:, :, :n], scalar1=0.5, scalar2=0.5, op0=ALU.mult, op1=ALU.add),
                  reads=[tw1], writes=[tw1])
            fw.op("dve", lambda e, yv=yv, n=n: e.tensor_tensor(out=a32[:, :, :n], in0=w1[:, :, :n], in1=yv, op=ALU.mult), reads=[tw1, ty], writes=[ta])
            fw.op("act", lambda e, n=n: e.copy(out=ab[:, :, :n], in_=a32[:, :, :n]), reads=[ta], writes=[ta])
            for nt in range(2):
                ps, tps = psr.next()
                for k in range(2):
                    fw.op("pe", lambda e, ps=ps, k=k, nt=nt, n=n: e.matmul(ps[:, :n], lhsT=gw[:, k, nt * 128:(nt + 1) * 128], rhs=ab[:, k, :n],
                                                                      start=(k == 0), stop=(k == 1)), reads=[tg, ta], writes=[tps])
                fw.op("act", lambda e, ps=ps, nt=nt, n=n: e.activation(out=w2[:, nt, :n], in_=ps[:, :n], func=AF.Sigmoid, bias=gb[:, nt:nt + 1]),
                      reads=[tps, tg], writes=[tw1])
            fw.op("dve", lambda e, n=n: e.tensor_tensor(out=w2[:, :, :n], in0=w2[:, :, :n], in1=a32[:, :, :n], op=ALU.mult), reads=[tw1, ta], writes=[tw1])
            fw.op("act", lambda e, n=n: e.activation(out=sqb[:, :, :n], in_=w2[:, :, :n], func=AF.Square), reads=[tw1], writes=[tw1])
            ps, tps = psr.next()
            for k in range(2):
                fw.op("pe", lambda e, ps=ps, k=k, n=n: e.matmul(ps[:, :n], lhsT=ones[:], rhs=sqb[:, k, :n], start=(k == 0), stop=(k == 1)),
                      reads=[tw1, tg], writes=[tps])
            rstd_from_ps(fw, rs, trs, ps, tps, n, 1.0 / 256, eps[:, 0:1], tg)
            for k in range(2):
                sg, tsg = stg.next()
                fw.op("dve", lambda e, sg=sg, k=k, n=n: e.scalar_tensor_tensor(out=sg[:, :n], in0=w2[:, k, :n], scalar=og[:, k:k + 1], in1=rs[:, :n],
                                                                             op0=ALU.mult, op1=ALU.mult), reads=[tw1, trs, tg], writes=[tsg])
                fw.dma("sp", S5T[k * 128:(k + 1) * 128, lo:hi], sg[:, :n], reads=[tsg], writes=[Tok()])
        fw.barrier()


def gen_p3(fw, io):
    ZT = io("ZT", [2048, TE], F32, "in")
    VT = io("VT", [T, 128], F32, "in")
    qn_g = io("att_qn_g", [64], F32, "in")
    kn_g = io("att_kn_g", [64], F32, "in")
    og = io("att_out_g", [512], F32, "in")
    POS = io("POS", [128, NLAT], F32, "in")
    CST = io("CST", [128, 260], F32, "in")
    ATT = io("ATT_T", [512, T], F32, "out")
    with ExitStack() as es:
        P = Pool_(fw, es)
        cst = P.sb([128, 260], F32)
        tc = Tok()
        fw.dma("sp", cst[:], CST, writes=[tc])
        permb = P.sb([128, 128], BF16)
        bdb = P.sb([128, 128], BF16)
        fw.op("dve", lambda e: e.tensor_copy(out=permb[:], in_=cst[:, 1:129]), reads=[tc], writes=[tc])
        fw.op("dve", lambda e: e.tensor_copy(out=bdb[:], in_=cst[:, 129:257]), reads=[tc], writes=[tc])
        eps = P.sb([128, 2], F32)
        teps = Tok()
        fw.op("pool", lambda e: e.memset(eps[:, 0:1], RMS_EPS), writes=[teps])
        fw.op("pool", lambda e: e.memset(eps[:, 1:2], 0.0), writes=[teps])
        gq = P.sb([128, 2], F32)
        tg = Tok()
        for h in range(2):
            fw.dma("sp", gq[h * 64:(h + 1) * 64, 0:1], qn_g.rearrange("(p o) -> p o", o=1), writes=[tg])
            fw.dma("sp", gq[h * 64:(h + 1) * 64, 1:2], kn_g.rearrange("(p o) -> p o", o=1), writes=[tg])
        qb = P.sb([128, 4, T], BF16)
        kd = P.sb([128, 2, T], BF16)
        tq = Tok()
        esr = ExitStack()
        Pr = Pool_(fw, esr)
        cos = Pr.sb([128, NLAT], F32)
        sin = Pr.sb([128, NLAT], F32)
        ttab = Tok()
        with ExitStack() as es2:
            P2 = Pool_(fw, es2)
            ang = P2.sb([128, NLAT], F32)
            tmpf = P2.sb([128, NLAT], F32)
            tmpi = P2.sb([128, NLAT], I32)
            ta = Tok()
            fw.dma("sp", ang[:], POS, writes=[ta])
            fw.op("dve", lambda e: e.tensor_scalar(out=ang[:], in0=ang[:], scalar1=cst[:, 0:1], scalar2=None, op0=ALU.mult),
                  reads=[ta, tc], writes=[ta])
            for (tab, off) in ((sin, 0.0), (cos, math.pi / 2)):
                fw.op("dve", lambda e, off=off: e.tensor_scalar(out=tmpi[:], in0=ang[:], scalar1=off, scalar2=1.0 / (2 * math.pi),
                                                                op0=ALU.add, op1=ALU.mult), reads=[ta], writes=[ta])
                fw.op("dve", lambda e: e.tensor_copy(out=tmpf[:], in_=tmpi[:]), reads=[ta], writes=[ta])
                fw.op("dve", lambda e: e.scalar_tensor_tensor(out=tmpf[:], in0=tmpf[:], scalar=-2 * math.pi, in1=ang[:],
                                                              op0=ALU.mult, op1=ALU.add), reads=[ta], writes=[ta])
                fw.op("dve", lambda e, off=off: e.tensor_scalar(out=tmpf[:], in0=tmpf[:], scalar1=off, scalar2=math.pi,
                                                                op0=ALU.add, op1=ALU.min), reads=[ta], writes=[ta])
                fw.op("dve", lambda e: e.tensor_scalar(out=tmpf[:], in0=tmpf[:], scalar1=-math.pi, scalar2=None, op0=ALU.max),
                      reads=[ta], writes=[ta])
                fw.op("act", lambda e, tab=tab: e.activation(out=tab[:], in_=tmpf[:], func=AF.Sin), reads=[ta], writes=[ttab])
            fw.barrier()
        with ExitStack() as es2:
            P2 = Pool_(fw, es2)
            raw = Rot([P2.sb([128, 512], F32) for _ in range(2)])
            sqr = Rot([P2.sb([128, 512], BF16) for _ in range(2)])
            psr = Rot([P2.ps() for _ in range(2)])
            psr2 = Rot([P2.ps() for _ in range(2)])
            rsr = Rot([P2.sb([128, 512], F32) for _ in range(2)])
            nbr = Rot([P2.sb([128, 512], BF16) for _ in range(2)])
            t1r = Rot([P2.sb([128, 512], F32) for _ in range(2)])
            t2r = Rot([P2.sb([128, 512], F32) for _ in range(2)])
            items = [("q", j) for j in range(4)] + [("k", g) for g in range(2)]
            for (kind, j) in items:
                for (lo, hi, ic) in SEGS:
                    n = hi - lo
                    rw, trw = raw.next()
                    if kind == "q":
                        fw.dma("sp", rw[:, :n], ZT[256 + j * 128:256 + (j + 1) * 128, lo:hi], writes=[trw])
                        gcol = 0
                        dst = qb[:, j, lo:hi]
                    else:
                        for h in range(2):
                            fw.dma("sp", rw[h * 64:(h + 1) * 64, :n], ZT[768 + j * 64:768 + (j + 1) * 64, lo:hi], writes=[trw])
                        gcol = 1
                        dst = kd[:, j, lo:hi]
                    sq, tsq = sqr.next()
                    fw.op("act", lambda e, sq=sq, rw=rw, n=n: e.activation(out=sq[:, :n], in_=rw[:, :n], func=AF.Square), reads=[trw], writes=[tsq])
                    ps, tps = psr.next()
                    fw.op("pe", lambda e, ps=ps, sq=sq, n=n: e.matmul(ps[:, :n], lhsT=bdb[:], rhs=sq[:, :n], start=True, stop=True),
                          reads=[tsq, tc], writes=[tps])
                    rs, trs = rsr.next()
                    rstd_from_ps(fw, rs, trs, ps, tps, n, 1.0, eps[:, 0:1], teps)
                    t1, tt1 = t1r.next()
                    fw.op("dve", lambda e, t1=t1, rw=rw, rs=rs, n=n, gcol=gcol: e.scalar_tensor_tensor(
                        out=t1[:, :n], in0=rw[:, :n], scalar=gq[:, gcol:gcol + 1], in1=rs[:, :n], op0=ALU.mult, op1=ALU.mult),
                        reads=[trw, trs, tg], writes=[tt1])
                    if ic:
                        fw.op("act", lambda e, dst=dst, t1=t1, n=n: e.copy(out=dst, in_=t1[:, :n]), reads=[tt1], writes=[tq])
                        continue
                    nb, tnb = nbr.next()
                    fw.op("act", lambda e, nb=nb, t1=t1, n=n: e.copy(out=nb[:, :n], in_=t1[:, :n]), reads=[tt1], writes=[tnb])
                    ps2, tps2 = psr2.next()
                    fw.op("pe", lambda e, ps2=ps2, nb=nb, n=n: e.matmul(ps2[:, :n], lhsT=permb[:], rhs=nb[:, :n], start=True, stop=True),
                          reads=[tnb, tc], writes=[tps2])
                    p0 = lo - NCTX
                    t2, tt2 = t2r.next()
                    fw.op("dve", lambda e, t2=t2, ps2=ps2, n=n, p0=p0: e.tensor_tensor(out=t2[:, :n], in0=ps2[:, :n], in1=sin[:, p0:p0 + n], op=ALU.mult),
                          reads=[tps2, ttab], writes=[tt2])
                    fw.op("pool", lambda e, t1=t1, nb=nb, n=n, p0=p0: e.tensor_tensor(out=t1[:, :n], in0=nb[:, :n], in1=cos[:, p0:p0 + n], op=ALU.mult),
                          reads=[tnb, ttab, tt1], writes=[tt1])
                    fw.op("pool", lambda e, dst=dst, t1=t1, t2=t2, n=n: e.tensor_tensor(out=dst, in0=t1[:, :n], in1=t2[:, :n], op=ALU.add),
                          reads=[tt1, tt2], writes=[tq])
            fw.barrier()
        esr.close()
        va = P.sb([128, 18, 2, 128], BF16)
        tva = Tok()
        fw.op("pool", lambda e: e.memset(va[:], 1.0), writes=[tva])
        for g in range(2):
            fw.dma("pool", va[:, :, g, 0:64], VT.rearrange("(t p) c -> p t c", p=128)[:, :, g * 64:(g + 1) * 64], writes=[tva])
        att = P.sb([128, 4, T], BF16)
        tatt = Tok()
        pss = Rot([P.ps() for _ in range(2)])
        pso = Rot([P.ps() for _ in range(3)])
        pend = []
        yield "core"
        ptr = Rot([P.sb([128, 512], BF16) for _ in range(3)])
        rcr = Rot([P.sb([64, 512], F32) for _ in range(2)])
        jobs = [(0, 256, 0, 2)] + [(256 + 512 * i, 768 + 512 * i, 0, 18) for i in range(4)]
        for h in range(8):
            g = h // 4
            jt, r0 = h // 2, (h % 2) * 64
            for (qlo, qhi, k0, k1) in jobs:
                n = qhi - qlo
                po, tpo = pso.next()
                for kt in range(k0, k1):
                    ps, tps = pss.next()
                    fw.op("pe", lambda e, ps=ps, kt=kt, g=g, jt=jt, r0=r0, qlo=qlo, qhi=qhi, n=n: e.matmul(
                        ps[:, :n], lhsT=kd[r0:r0 + 64, g, kt * 128:(kt + 1) * 128], rhs=qb[r0:r0 + 64, jt, qlo:qhi],
                        start=True, stop=True), reads=[tq], writes=[tps])
                    pt, tpt = ptr.next()
                    fw.op("act", lambda e, pt=pt, ps=ps, n=n: e.activation(out=pt[:, :n], in_=ps[:, :n], func=AF.Exp, scale=0.125),
                          reads=[tps], writes=[tpt])
                    fw.op("pe", lambda e, po=po, pt=pt, kt=kt, g=g, n=n, k0=k0, k1=k1: e.matmul(
                        po[:, :n], lhsT=va[:, kt, g, :], rhs=pt[:, :n], start=(kt == k0), stop=(kt == k1 - 1)),
                        reads=[tpt, tva], writes=[tpo])
                def norm_job(po=po, tpo=tpo, n=n, jt=jt, r0=r0, qlo=qlo, qhi=qhi):
                    rc, trc = rcr.next()
                    fw.op("dve", lambda e: e.reciprocal(out=rc[:, :n], in_=po[64:128, :n]), reads=[tpo], writes=[trc])
                    fw.op("dve", lambda e: e.tensor_tensor(out=att[r0:r0 + 64, jt, qlo:qhi], in0=po[0:64, :n], in1=rc[:, :n], op=ALU.mult),
                          reads=[tpo, trc], writes=[tatt])
                pend.append(norm_job)
                if len(pend) > 2:
                    pend.pop(0)()
                yield
        for nj in pend:
            nj()
        yield "fin"
        ogs = P.sb([128, 4], F32)
        tog = Tok()
        fw.dma("sp", ogs[:], og.rearrange("(k p) -> p k", p=128), writes=[tog], allow_slow_non_contiguous=True)
        ones = P.sb([128, 128], BF16)
        fw.op("pool", lambda e: e.memset(ones[:], 1.0), writes=[tog])
        sq4 = P.sb([128, 4, 512], BF16)
        tsq4 = Tok()
        rs = P.sb([128, 512], F32)
        trs = Tok()
        stg = Rot([P.sb([128, 512], F32) for _ in range(3)])
        tout = Tok()
        for (lo, hi, ic) in SEGS:
            n = hi - lo
            fw.op("act", lambda e, lo=lo, hi=hi, n=n: e.activation(out=sq4[:, :, :n], in_=att[:, :, lo:hi], func=AF.Square), reads=[tatt], writes=[tsq4])
            ps, tps = pss.next()
            for k in range(4):
                fw.op("pe", lambda e, ps=ps, k=k, n=n: e.matmul(ps[:, :n], lhsT=ones[:], rhs=sq4[:, k, :n], start=(k == 0), stop=(k == 3)),
                      reads=[tsq4, tog], writes=[tps])
            rstd_from_ps(fw, rs, trs, ps, tps, n, 1.0 / 512, eps[:, 0:1], teps)
            for k in range(4):
                sg, tsg = stg.next()
                fw.op("dve", lambda e, sg=sg, k=k, lo=lo, hi=hi, n=n: e.scalar_tensor_tensor(
                    out=sg[:, :n], in0=att[:, k, lo:hi], scalar=ogs[:, k:k + 1], in1=rs[:, :n], op0=ALU.mult, op1=ALU.mult),
                    reads=[tatt, trs, tog], writes=[tsg])
                fw.dma("sp", ATT[k * 128:(k + 1) * 128, lo:hi], sg[:, :n], reads=[tsg], writes=[Tok()])
        fw.barrier()


def stage_p23(fw, io):
    g2 = gen_p2(fw, io)
    g3 = gen_p3(fw, io)

    def until(g, tag):
        for v in g:
            if v == tag:
                return True
        return False
    until(g2, "core")
    until(g3, "core")
    a2 = a3 = True
    R = 36
    while a2 or a3:
        if a2:
            for _ in range(R):
                if next(g2) == "fin":
                    a2 = False
                    break
        if a3:
            if next(g3) == "fin":
                a3 = False
    for _ in g3:
        pass
    for _ in g2:
        pass


RW_BASE = 1024
CHK = 64
NCH = T // CHK
LN_EPS_RW = 64e-5


def rw_consts():
    idx = np.arange(64)
    m = np.zeros((64, 4, 64), np.float32)
    m[:, 0, :] = (idx[:, None] < idx[None, :])
    m[:, 1, :] = (idx[:, None] > idx[None, :])
    m[:, 2, :] = (idx[:, None] <= idx[None, :])
    m[:, 3, :] = (idx[:, None] >= idx[None, :])
    bd = np.zeros((128, 128), np.float32)
    bd[:64, :64] = 1.0
    bd[64:, 64:] = 1.0
    return m, bd


def stage_p4(fw, io):
    ZT = io("ZT", [2048, TE], F32, "in")
    mu = io("rw_mu", [960], F32, "in")
    w0 = io("rw_w0", [2, 256], F32, "in")
    w2 = io("rw_w2", [2, 32, 256], F32, "in")
    a0 = io("rw_a0", [2, 256], F32, "in")
    a2 = io("rw_a2", [2, 32, 256], F32, "in")
    g2 = io("rw_g2", [64, 256], F32, "in")
    k_k = io("rw_k_k", [256], F32, "in")
    k_a = io("rw_k_a", [256], F32, "in")
    r_k = io("rw_r_k", [256], F32, "in")
    ln_g = io("rw_ln_g", [256], F32, "in")
    ln_b = io("rw_ln_b", [256], F32, "in")
    ident = io("ident", [128, 128], F32, "in")
    MASKS = io("RWMASK", [64, 4, 64], F32, "in")
    BD = io("RWBD", [128, 128], F32, "in")
    RWT = io("RWT", [256, T], F32, "out")
    N = T
    CH5 = [(0, 512), (512, 1024), (1024, 1536), (1536, 2048), (2048, 2560)]
    with ExitStack() as es:
        P = Pool_(fw, es)
        tc = Tok()
        idt = P.sb([128, 128], F32)
        idb = P.sb([128, 128], BF16)
        msk = P.sb([64, 4, 64], F32)
        bd1 = P.sb([128, 128], F32)
        fw.dma("sp", idt[:], ident, writes=[tc])
        fw.dma("sp", msk[:], MASKS, writes=[tc])
        fw.dma("sp", bd1[:], BD, writes=[tc])
        fw.op("dve", lambda e: e.tensor_copy(out=idb[:], in_=idt[:]), reads=[tc], writes=[tc])
        mrep = P.sb([64, 4, 4, 64], F32)
        for rep in range(4):
            fw.op("dve", lambda e, rep=rep: e.tensor_copy(out=mrep[:, :, rep, :], in_=msk[:]), reads=[tc], writes=[tc])
        pp = P.sb([128, 12, 2], F32)
        tpp = Tok()
        srcs = [w0[0], w0[1], a0[0], a0[1], k_k, k_a, k_a, r_k, ln_g, ln_b]
        for i, sap in enumerate(srcs):
            fw.dma("sp", pp[:, i, :], sap.rearrange("(c p) -> p c", p=128), writes=[tpp], allow_slow_non_contiguous=True)
        fw.op("dve", lambda e: e.tensor_scalar(out=pp[:, 6, :], in0=pp[:, 6, :], scalar1=-1.0, scalar2=1.0, op0=ALU.mult, op1=ALU.add),
              reads=[tpp], writes=[tpp])
        epsl = P.sb([128, 2], F32)
        fw.op("pool", lambda e: e.memset(epsl[:, 0:1], LN_EPS_RW), writes=[tpp])
        fw.op("pool", lambda e: e.memset(epsl[:, 1:2], 1e-24), writes=[tpp])
        wA = P.sb([128, 256], BF16)
        wB = P.sb([128, 256], BF16)
        tlw = Tok()
        fw.dma("pool", wA[0:32, :], w2[0], writes=[tlw])
        fw.dma("pool", wA[32:64, :], w2[1], writes=[tlw])
        fw.dma("pool", wA[64:96, :], a2[0], writes=[tlw])
        fw.dma("pool", wB[0:32, :], a2[1], writes=[tlw])
        fw.dma("pool", wB[64:128, :], g2, writes=[tlw])
        smask = P.sb([128, N], BF16)
        tsm = Tok()
        fw.op("pool", lambda e: e.memset(smask[:], 1.0), writes=[tsm])
        fw.op("pool", lambda e: e.memset(smask[:].rearrange("p (c j) -> p c j", j=CHK)[:, :, 0], 0.0), writes=[tsm])

        def load_shift(dst, tdst, row0, rows, P2, post=None, pb=0):
            zt_ = P2.sb([128, TE], F32)
            nbt_ = P2.sb([128, TE], F32)
            mtt_ = P2.sb([128, 2], F32)
            tz, tnb, tm = Tok(), Tok(), Tok()
            ps_ = slice(pb, pb + rows)
            z = zt_[ps_, :]
            fw.dma("sp", z, ZT[RW_BASE + row0:RW_BASE + row0 + rows, :], writes=[tz])
            fw.dma("sp", mtt_[ps_, 0:1], mu[row0:row0 + rows].rearrange("(p o) -> p o", o=1), writes=[tm])
            fw.op("dve", lambda e: e.tensor_scalar(out=mtt_[ps_, 1:2], in0=mtt_[ps_, 0:1], scalar1=0.5, scalar2=None, op0=ALU.mult), reads=[tm], writes=[tm])
            fw.op("dve", lambda e: e.tensor_scalar(out=mtt_[ps_, 0:1], in0=mtt_[ps_, 0:1], scalar1=-1.0, scalar2=1.0, op0=ALU.mult, op1=ALU.add),
                  reads=[tm], writes=[tm])
            fw.op("pool", lambda e: e.memset(nbt_[ps_, 0:1], 0.0), writes=[tnb])
            fw.op("pool", lambda e: e.tensor_copy(out=nbt_[ps_, 1:TE], in_=zt_[ps_, 0:TE - 1]), reads=[tz], writes=[tnb])
            fw.op("pool", lambda e: e.tensor_tensor(out=nbt_[ps_, 0:TE - 1], in0=nbt_[ps_, 0:TE - 1], in1=zt_[ps_, 1:TE], op=ALU.add),
                  reads=[tz, tnb], writes=[tnb])
            for cb in (256, 2304):
                fw.op("pool", lambda e, cb=cb: e.tensor_tensor(out=nbt_[ps_, cb:cb + 1], in0=nbt_[ps_, cb:cb + 1], in1=zt_[ps_, cb - 1:cb], op=ALU.subtract),
                      reads=[tz, tnb], writes=[tnb])
                fw.op("pool", lambda e, cb=cb: e.tensor_tensor(out=nbt_[ps_, cb - 1:cb], in0=nbt_[ps_, cb - 1:cb], in1=zt_[ps_, cb:cb + 1], op=ALU.subtract),
                      reads=[tz, tnb], writes=[tnb])
            fw.op("act", lambda e: e.activation(out=z, in_=z, func=AF.Identity, scale=mtt_[ps_, 0:1]), reads=[tz, tm], writes=[tz])
            if post is None:
                fw.op("dve", lambda e: e.scalar_tensor_tensor(out=dst, in0=nbt_[ps_, :], scalar=mtt_[ps_, 1:2], in1=z, op0=ALU.mult, op1=ALU.add),
                      reads=[tz, tnb, tm], writes=[tdst])
            else:
                fw.op("dve", lambda e: e.scalar_tensor_tensor(out=z, in0=nbt_[ps_, :], scalar=mtt_[ps_, 1:2], in1=z, op0=ALU.mult, op1=ALU.add),
                      reads=[tz, tnb, tm], writes=[tz])
                fw.op("act", lambda e: e.activation(out=dst, in_=z, func=post), reads=[tz], writes=[tdst])

        lorA = P.sb([128, TE], BF16)
        lorB = P.sb([128, TE], BF16)
        tlor = Tok()
        for i in range(4):
            with ExitStack() as es2:
                dstt = lorA[32 * i:32 * i + 32, :] if i < 3 else lorB[0:32, :]
                load_shift(dstt, tlor, 768 + 32 * i, 32, Pool_(fw, es2), post=(AF.Tanh if i < 2 else AF.Copy), pb=(32 * i if i < 3 else 0))
                fw.barrier()
        with ExitStack() as es2:
            load_shift(lorB[64:128, :], tlor, 896, 64, Pool_(fw, es2), post=AF.Sigmoid, pb=64)
            fw.barrier()

        for c in range(2):
            with ExitStack() as esc:
                Pc = Pool_(fw, esc)
                rr = Pc.sb([128, TE], F32)
                kx = Pc.sb([128, TE], F32)
                vv = Pc.sb([128, TE], F32)
                kk = Pc.sb([128, TE], F32)
                trr, tkx, tvv, tkk = Tok(), Tok(), Tok(), Tok()
                vtok = Pc.sb([64, TE // CHK, 128], BF16)
                tvt = Tok()
                yacc = Pc.sb([128, T], F32)
                bon = Pc.sb([128, T], F32)
                tya, tbon = Tok(), Tok()
                fw.op("pool", lambda e: e.memset(yacc[:], 0.0), writes=[tya])
                fw.op("pool", lambda e: e.memset(bon[:], 0.0), writes=[tbon])
                for (dst, tdst, r0) in ((rr, trr, 0), (kx, tkx, 256), (vv, tvv, 512)):
                    with ExitStack() as es2:
                        load_shift(dst[:], tdst, r0 + c * 128, 128, Pool_(fw, es2))
                        fw.barrier()
                with ExitStack() as es2:
                    P2 = Pool_(fw, es2)
                    sq = P2.sb([128, 512], F32)
                    rn = P2.sb([128, 512], F32)
                    tsq, trn = Tok(), Tok()
                    ps1 = P2.ps()
                    tps1 = Tok()
                    fw.op("dve", lambda e: e.tensor_scalar(out=kk[:], in0=kx[:], scalar1=pp[:, 4, c:c + 1], scalar2=None, op0=ALU.mult),
                          reads=[tkx, tpp], writes=[tkk])
                    for (c0, c1) in CH5:
                        fw.op("act", lambda e, c0=c0, c1=c1: e.activation(out=sq[:], in_=kk[:, c0:c1], func=AF.Square), reads=[tkk], writes=[tsq])
                        fw.op("pe", lambda e: e.matmul(ps1[:], lhsT=bd1[:], rhs=sq[:], start=True, stop=True), reads=[tsq, tc], writes=[tps1])
                        rstd_from_ps(fw, rn, trn, ps1, tps1, 512, 1.0, epsl[:, 1:2], tpp)
                        fw.op("dve", lambda e, c0=c0, c1=c1: e.tensor_tensor(out=kk[:, c0:c1], in0=kk[:, c0:c1], in1=rn[:], op=ALU.mult),
                              reads=[tkk, trn], writes=[tkk])
                    vb = P2.sb([128, TE], BF16)
                    tvb = Tok()
                    fw.op("act", lambda e: e.copy(out=vb[:], in_=vv[:]), reads=[tvv], writes=[tvb])
                    pst = P2.ps([128, 1024], BF16)
                    tpst = Tok()
                    for q in range(TE // CHK // 4):
                        for j in range(4):
                            ch = q * 4 + j
                            fw.op("pe", lambda e, j=j, ch=ch: e.transpose(pst[0:64, j * 128:(j + 1) * 128], vb[:, ch * CHK:(ch + 1) * CHK], idb[:]),
                                  reads=[tvb, tc], writes=[tpst])
                        fw.op("dve", lambda e, q=q: e.tensor_copy(out=vtok[:, q * 4:(q + 1) * 4, :], in_=pst[0:64, 0:512].rearrange("p (j c) -> p j c", j=4)),
                              reads=[tpst], writes=[tvt])
                    fw.barrier()

                for di in range(2):
                    off = 0 if di == 0 else 256
                    with ExitStack() as esd:
                        Pd = Pool_(fw, esd)
                        aT = Pd.sb([128, N], BF16)
                        bT = Pd.sb([128, N], BF16)
                        kT = Pd.sb([128, N], BF16)
                        rT = Pd.sb([128, N], BF16)
                        btok = Pd.sb([64, NCH, 128], BF16)
                        ktok = Pd.sb([64, NCH, 128], BF16)
                        pC = Pd.sb([128, NCH], F32)
                        tops = Tok()
                        ttok = Tok()
                        with ExitStack() as es2:
                            P2 = Pool_(fw, es2)
                            ld = P2.sb([128, TE], F32)
                            kd = P2.sb([128, TE], F32)
                            bb = P2.sb([128, TE], F32)
                            tld, tkd, tbb = Tok(), Tok(), Tok()
                            psr = Rot([P2.ps() for _ in range(3)])
                            tm5 = Rot([P2.sb([128, 512], F32) for _ in range(2)])
                            for (c0, c1) in CH5:
                                ps, tps = psr.next()
                                fw.op("pe", lambda e, ps=ps, c0=c0, c1=c1: e.matmul(ps[:], lhsT=wA[32 * di:32 * di + 32, c * 128:(c + 1) * 128], rhs=lorA[32 * di:32 * di + 32, c0:c1],
                                                                                  start=True, stop=True), reads=[tlw, tlor], writes=[tps])
                                fw.op("act", lambda e, ps=ps, c0=c0, c1=c1: e.activation(out=ld[:, c0:c1], in_=ps[:], func=AF.Sigmoid, bias=pp[:, di, c:c + 1]),
                                      reads=[tps, tpp], writes=[tld])
                                ps, tps = psr.next()
                                fw.op("pe", lambda e, ps=ps, c0=c0, c1=c1: e.matmul(ps[:], lhsT=(wA[64:96, c * 128:(c + 1) * 128] if di == 0 else wB[0:32, c * 128:(c + 1) * 128]),
                                                                                  rhs=(lorA[64:96, c0:c1] if di == 0 else lorB[0:32, c0:c1]),
                                                                                  start=True, stop=True), reads=[tlw, tlor], writes=[tps])
                                fw.op("act", lambda e, ps=ps, c0=c0, c1=c1: e.activation(out=bb[:, c0:c1], in_=ps[:], func=AF.Sigmoid, bias=pp[:, 2 + di, c:c + 1]),
                                      reads=[tps, tpp], writes=[tbb])
                            fw.op("pool", lambda e: e.tensor_scalar(out=ld[:], in0=ld[:], scalar1=-math.exp(-0.5), scalar2=None, op0=ALU.mult), reads=[tld], writes=[tld])
                            fw.op("act", lambda e: e.activation(out=kd[:], in_=bb[:], func=AF.Identity, scale=pp[:, 5, c:c + 1], bias=pp[:, 6, c:c + 1]),
                                  reads=[tbb, tpp], writes=[tkd])
                            fw.op("dve", lambda e: e.tensor_tensor(out=kd[:], in0=kd[:], in1=kx[:], op=ALU.mult), reads=[tkd, tkx], writes=[tkd])
                            fw.op("pool", lambda e: e.tensor_tensor(out=bb[:], in0=bb[:], in1=kk[:], op=ALU.mult), reads=[tbb, tkk], writes=[tbb])
                            for (c0, c1) in CH5:
                                c1 = min(c1, T)
                                n = c1 - c0
                                tm, ttm = tm5.next()
                                fw.op("dve", lambda e, tm=tm, c0=c0, c1=c1, n=n: e.scalar_tensor_tensor(out=tm[:, :n], in0=rr[:, c0:c1], scalar=pp[:, 7, c:c + 1], in1=kd[:, c0:c1],
                                                                                                 op0=ALU.mult, op1=ALU.mult), reads=[trr, tkd, tpp], writes=[ttm])
                                ps, tps = psr.next()
                                fw.op("pe", lambda e, ps=ps, tm=tm, n=n: e.matmul(ps[:, :n], lhsT=bd1[:], rhs=tm[:, :n], start=True, stop=True), reads=[ttm, tc], writes=[tps])
                                tm2, ttm2 = tm5.next()
                                fw.op("dve", lambda e, tm2=tm2, ps=ps, c0=c0, c1=c1, n=n: e.tensor_tensor(out=tm2[:, :n], in0=ps[:, :n], in1=vv[:, c0:c1], op=ALU.mult),
                                      reads=[tps, tvv], writes=[ttm2])
                                fw.op("pool", lambda e, tm2=tm2, c0=c0, c1=c1, n=n: e.tensor_tensor(out=bon[:, c0:c1], in0=bon[:, c0:c1], in1=tm2[:, :n], op=ALU.add),
                                      reads=[ttm2, tbon], writes=[tbon])
                            cs = P2.sb([128, N], F32)
                            ex = P2.sb([128, N], F32)
                            tcs, tex = Tok(), Tok()
                            ldw = ld[:, off:off + N]
                            fw.op("dve", lambda e: e.tensor_tensor_scan(out=cs[:], data0=smask[:], data1=ldw, initial=0.0, op0=ALU.mult, op1=ALU.add),
                                  reads=[tsm, tld], writes=[tcs])
                            csv = cs[:].rearrange("p (c j) -> p c j", j=CHK)
                            tot = P2.sb([128, NCH, 1], F32)
                            ttot = Tok()
                            fw.op("dve", lambda e: e.tensor_copy(out=tot[:], in_=csv[:, :, CHK - 1:CHK]), reads=[tcs], writes=[ttot])
                            totb = tot[:].to_broadcast([128, NCH, CHK])
                            if di == 1:
                                fw.op("dve", lambda e: e.tensor_tensor(out=csv, in0=totb, in1=csv, op=ALU.subtract), reads=[ttot, tcs], writes=[tcs])
                                fw.op("dve", lambda e: e.tensor_tensor(out=cs[:], in0=cs[:], in1=ldw, op=ALU.add), reads=[tcs, tld], writes=[tcs])
                            fw.op("act", lambda e: e.activation(out=pC[:], in_=tot[:, :, 0], func=AF.Exp), reads=[ttot], writes=[tops])
                            kdw, bbw = kd[:, off:off + N], bb[:, off:off + N]
                            rrw, kkw = rr[:, off:off + N], kk[:, off:off + N]
                            fw.op("act", lambda e: e.activation(out=ex[:], in_=cs[:], func=AF.Exp), reads=[tcs], writes=[tex])
                            fw.op("dve", lambda e: e.tensor_tensor(out=rT[:], in0=rrw, in1=ex[:], op=ALU.mult), reads=[trr, tex], writes=[tops])
                            fw.op("act", lambda e: e.activation(out=ex[:], in_=cs[:], func=AF.Exp, scale=-1.0), reads=[tcs, tops], writes=[tex])
                            fw.op("dve", lambda e: e.tensor_tensor(out=bT[:], in0=bbw, in1=ex[:], op=ALU.mult), reads=[tbb, tex], writes=[tops])
                            fw.op("pool", lambda e: e.tensor_tensor(out=kT[:], in0=kdw, in1=ex[:], op=ALU.mult), reads=[tkd, tex], writes=[tops])
                            e3 = ex
                            te3 = tex
                            fw.op("dve", lambda e: e.tensor_tensor(out=e3[:], in0=cs[:], in1=ldw, op=ALU.subtract), reads=[tcs, tld], writes=[te3])
                            fw.op("act", lambda e: e.activation(out=e3[:], in_=e3[:], func=AF.Exp), reads=[te3], writes=[te3])
                            fw.op("dve", lambda e: e.scalar_tensor_tensor(out=aT[:], in0=kkw, scalar=-1.0, in1=e3[:], op0=ALU.mult, op1=ALU.mult),
                                  reads=[tkk, te3], writes=[tops])
                            e3v = e3[:].rearrange("p (c j) -> p c j", j=CHK)
                            fw.op("dve", lambda e: e.tensor_tensor(out=e3v, in0=totb, in1=csv, op=ALU.subtract), reads=[ttot, tcs, tops, te3], writes=[te3])
                            fw.op("act", lambda e: e.activation(out=e3[:], in_=e3[:], func=AF.Exp), reads=[te3], writes=[te3])
                            bh = P2.sb([128, N], BF16)
                            kh = P2.sb([128, N], BF16)
                            tbh = Tok()
                            fw.op("dve", lambda e: e.tensor_tensor(out=bh[:], in0=bbw, in1=e3[:], op=ALU.mult), reads=[tbb, te3], writes=[tbh])
                            fw.op("pool", lambda e: e.tensor_tensor(out=kh[:], in0=kdw, in1=e3[:], op=ALU.mult), reads=[tkd, te3], writes=[tbh])
                            pst = P2.ps([128, 1024], BF16)
                            tpst = Tok()
                            for (src, dstt) in ((bh, btok), (kh, ktok)):
                                for q in range(NCH // 4):
                                    for j in range(4):
                                        ch = q * 4 + j
                                        fw.op("pe", lambda e, j=j, ch=ch, src=src: e.transpose(pst[0:64, j * 128:(j + 1) * 128], src[:, ch * CHK:(ch + 1) * CHK], idb[:]),
                                              reads=[tbh, tc], writes=[tpst])
                                    fw.op("act", lambda e, q=q, dstt=dstt: e.copy(out=dstt[:, q * 4:(q + 1) * 4, :], in_=pst[0:64, 0:512].rearrange("p (j c) -> p j c", j=4)),
                                          reads=[tpst], writes=[ttok])
                            fw.barrier()
                        with ExitStack() as es3:
                            P3 = Pool_(fw, es3)
                            Hs = P3.sb([128, 64], F32)
                            Hb = P3.sb([128, 64], BF16)
                            tH = Tok()
                            fw.op("pool", lambda e: e.memset(Hs[:], 0.0), writes=[tH])
                            fw.op("pool", lambda e: e.memset(Hb[:], 0.0), writes=[tH])
                            psA = Rot([P3.ps([64, 512]) for _ in range(1)])
                            psB = Rot([P3.ps([64, 512]) for _ in range(1)])
                            psI = Rot([P3.ps([64, 512]) for _ in range(1)])
                            psT = Rot([P3.ps([64, 512]) for _ in range(1)])
                            psG = Rot([P3.ps([64, 512]) for _ in range(1)])
                            psH = Rot([P3.ps([128, 512]) for _ in range(1)])
                            psY = Rot([P3.ps([128, 512]) for _ in range(1)])
                            g1b = Rot([P3.sb([64, 2, 64], BF16) for _ in range(2)])
                            nl = Rot([P3.sb([64, 2, 2, 64], F32) for _ in range(2)])
                            nl2 = Rot([P3.sb([64, 2, 2, 64], F32) for _ in range(2)])
                            g2b_ = Rot([P3.sb([64, 4, 64], BF16) for _ in range(2)])
                            Pm = Rot([P3.sb([64, 2, 64], F32) for _ in range(2)])
                            Gs = Rot([P3.sb([64, 128], F32) for _ in range(2)])
                            Ub = Rot([P3.sb([64, 128], BF16) for _ in range(2)])
                            Ys = Rot([P3.sb([64, 128], F32) for _ in range(2)])
                            ms, ml, mi = (0, 1, 2) if di == 0 else (1, 0, 3)
                            order = list(range(NCH)) if di == 0 else list(range(NCH - 1, -1, -1))
                            res = {}

                            def par_gen(i):
                                cc0, cc1 = i * CHK, (i + 1) * CHK
                                pa, tpa = psA.next()
                                pb, tpb = psB.next()
                                for h in range(2):
                                    hp = slice(h * 64, (h + 1) * 64)
                                    for (dst, lt, rt_) in ((pa[:, h * 64:(h + 1) * 64], kT, aT), (pa[:, 128 + h * 64:128 + (h + 1) * 64], bT, aT),
                                                           (pa[:, 256 + h * 64:256 + (h + 1) * 64], aT, bT)):
                                        fw.op("pe", lambda e, dst=dst, lt=lt, rt_=rt_, hp=hp: e.matmul(dst, lhsT=lt[hp, cc0:cc1], rhs=rt_[hp, cc0:cc1], start=True, stop=True),
                                              reads=[tops], writes=[tpa])
                                        yield
                                    for (dst, lt, rt_) in ((pb[:, h * 64:(h + 1) * 64], bT, rT), (pb[:, 128 + h * 64:128 + (h + 1) * 64], kT, rT)):
                                        fw.op("pe", lambda e, dst=dst, lt=lt, rt_=rt_, hp=hp: e.matmul(dst, lhsT=lt[hp, cc0:cc1], rhs=rt_[hp, cc0:cc1], start=True, stop=True),
                                              reads=[tops], writes=[tpb])
                                        yield
                                a1, ta1 = g1b.next()
                                nlt, tnl = nl.next()
                                a45, ta45 = g2b_.next()
                                fw.op("dve", lambda e: e.tensor_tensor(out=a1[:], in0=pa[:, 0:128].rearrange("p (h t) -> p h t", h=2), in1=mrep[:, ms, 0:2, :], op=ALU.mult),
                                      reads=[tpa, tc], writes=[ta1])
                                yield
                                fw.op("dve", lambda e: e.tensor_tensor(out=nlt[:, 0], in0=pa[:, 128:256].rearrange("p (h t) -> p h t", h=2), in1=mrep[:, ms, 0:2, :], op=ALU.mult),
                                      reads=[tpa, tc], writes=[tnl])
                                yield
                                fw.op("dve", lambda e: e.tensor_tensor(out=nlt[:, 1], in0=pa[:, 256:384].rearrange("p (h t) -> p h t", h=2), in1=mrep[:, ml, 0:2, :], op=ALU.mult),
                                      reads=[tpa, tc], writes=[tnl])
                                yield
                                fw.op("dve", lambda e: e.tensor_tensor(out=a45[:], in0=pb[:, 0:256].rearrange("p (h t) -> p h t", h=4), in1=mrep[:, mi, :, :], op=ALU.mult),
                                      reads=[tpb, tc], writes=[ta45])
                                yield
                                pm, tpm = Pm.next()
                                fw.op("dve", lambda e: e.tensor_tensor(out=pm[:], in0=nlt[:, 0], in1=idt[0:64, 0:64].unsqueeze(1).to_broadcast([64, 2, 64]), op=ALU.add),
                                      reads=[tnl, tc], writes=[tpm])
                                yield
                                cur, tcur = nlt, tnl
                                for lev in range(5):
                                    pi_, tpi = psI.next()
                                    for h in range(2):
                                        fw.op("pe", lambda e, pi_=pi_, h=h, cur=cur: e.matmul(pi_[:, h * 64:(h + 1) * 64], lhsT=cur[:, 0, h, :], rhs=cur[:, 1, h, :], start=True, stop=True),
                                              reads=[tcur], writes=[tpi])
                                        yield
                                    nxt, tnxt = (nl2.next() if lev % 2 == 0 else nl.next())
                                    fw.op("act", lambda e, nxt=nxt, pi_=pi_: e.copy(out=nxt[:, 1], in_=pi_[:, 0:128].rearrange("p (h t) -> p h t", h=2)), reads=[tpi], writes=[tnxt])
                                    yield
                                    for h in range(2):
                                        fw.op("pe", lambda e, pi_=pi_, h=h, nxt=nxt: e.matmul(pi_[:, 256 + h * 64:256 + (h + 1) * 64], lhsT=nxt[:, 1, h, :], rhs=pm[:, h, :], start=True, stop=True),
                                              reads=[tnxt, tpm], writes=[tpi])
                                        yield
                                    if lev < 4:
                                        pt_, tpt = psT.next()
                                        for h in range(2):
                                            fw.op("pe", lambda e, pt_=pt_, h=h, nxt=nxt: e.transpose(pt_[:, h * 64:(h + 1) * 64], nxt[:, 1, h, :], idt[0:64, 0:64]),
                                                  reads=[tnxt, tc], writes=[tpt])
                                            yield
                                    fw.op("dve", lambda e, pi_=pi_: e.tensor_tensor(out=pm[:], in0=pm[:], in1=pi_[:, 256:384].rearrange("p (h t) -> p h t", h=2), op=ALU.add),
                                          reads=[tpi, tpm], writes=[tpm])
                                    yield
                                    if lev < 4:
                                        fw.op("act", lambda e, nxt=nxt, pt_=pt_: e.copy(out=nxt[:, 0], in_=pt_[:, 0:128].rearrange("p (h t) -> p h t", h=2)), reads=[tpt], writes=[tnxt])
                                        yield
                                    cur, tcur = nxt, tnxt
                                res[i] = (a1, ta1, a45, ta45, pm, tpm)

                            def chain_gen(i):
                                cc0, cc1 = i * CHK, (i + 1) * CHK
                                gch = i + off // CHK
                                a1, ta1, a45, ta45, pm, tpm = res.pop(i)
                                pg, tpg = psG.next()
                                for h in range(2):
                                    hp = slice(h * 64, (h + 1) * 64)
                                    fw.op("pe", lambda e, h=h, hp=hp: e.matmul(pg[:, h * 64:(h + 1) * 64], lhsT=aT[hp, cc0:cc1], rhs=Hb[hp, :], start=True, stop=False),
                                          reads=[tops, tH], writes=[tpg])
                                    yield
                                    fw.op("pe", lambda e, h=h, hp=hp: e.matmul(pg[:, h * 64:(h + 1) * 64], lhsT=a1[:, h, :], rhs=vtok[:, gch, hp], start=False, stop=True),
                                          reads=[ta1, tvt], writes=[tpg])
                                    yield
                                gs, tgs = Gs.next()
                                fw.op("act", lambda e: e.copy(out=gs[:], in_=pg[:, 0:128]), reads=[tpg], writes=[tgs])
                                yield
                                for h in range(2):
                                    fw.op("pe", lambda e, h=h: e.matmul(pg[:, 128 + h * 64:128 + (h + 1) * 64], lhsT=pm[:, h, :], rhs=gs[:, h * 64:(h + 1) * 64], start=True, stop=True),
                                          reads=[tpm, tgs], writes=[tpg])
                                    yield
                                ub, tub = Ub.next()
                                fw.op("dve", lambda e: e.tensor_copy(out=ub[:], in_=pg[:, 128:256]), reads=[tpg], writes=[tub])
                                yield
                                ph, tph = psH.next()
                                py, tpy = psY.next()
                                for h in range(2):
                                    hp = slice(h * 64, (h + 1) * 64)
                                    fw.op("pe", lambda e, h=h, hp=hp: e.matmul(ph[hp, 0:64], lhsT=btok[:, i, hp], rhs=ub[:, hp], start=True, stop=False),
                                          reads=[ttok, tub], writes=[tph])
                                    yield
                                    fw.op("pe", lambda e, h=h, hp=hp: e.matmul(ph[hp, 0:64], lhsT=ktok[:, i, hp], rhs=vtok[:, gch, hp], start=False, stop=True),
                                          reads=[ttok, tvt], writes=[tph])
                                    yield
                                for h in range(2):
                                    hp = slice(h * 64, (h + 1) * 64)
                                    fw.op("pe", lambda e, h=h, hp=hp: e.matmul(py[0:64, hp], lhsT=rT[hp, cc0:cc1], rhs=Hb[hp, :], start=True, stop=False),
                                          reads=[tops, tH], writes=[tpy])
                                    yield
                                    fw.op("pe", lambda e, h=h, hp=hp: e.matmul(py[0:64, hp], lhsT=a45[:, h, :], rhs=ub[:, hp], start=False, stop=False),
                                          reads=[ta45, tub], writes=[tpy])
                                    yield
                                    fw.op("pe", lambda e, h=h, hp=hp: e.matmul(py[0:64, hp], lhsT=a45[:, 2 + h, :], rhs=vtok[:, gch, hp], start=False, stop=True),
                                          reads=[ta45, tvt], writes=[tpy])
                                    yield
                                fw.op("dve", lambda e: e.scalar_tensor_tensor(out=Hs[:], in0=Hs[:], scalar=pC[:, i:i + 1], in1=ph[:, 0:64], op0=ALU.mult, op1=ALU.add),
                                      reads=[tph, tH, tops, tpy], writes=[tH])
                                yield
                                fw.op("act", lambda e: e.copy(out=Hb[:], in_=Hs[:]), reads=[tH, tpy, tpg], writes=[tH])
                                yield
                                ys, tys = Ys.next()
                                fw.op("act", lambda e: e.copy(out=ys[:], in_=py[0:64, 0:128]), reads=[tpy], writes=[tys])
                                yield
                                fw.op("pe", lambda e: e.transpose(py[:, 256:320], ys[:], idt[0:64, 0:64]), reads=[tys, tc], writes=[tpy])
                                yield
                                y0 = off + cc0
                                if y0 >= T:
                                    y0 -= T
                                fw.op("dve", lambda e: e.tensor_tensor(out=yacc[:, y0:y0 + CHK], in0=yacc[:, y0:y0 + CHK], in1=py[:, 256:320], op=ALU.add),
                                      reads=[tpy, tya], writes=[tya])
                                yield

                            for _ in par_gen(order[0]):
                                pass
                            for idx, i in enumerate(order):
                                gp = par_gen(order[idx + 1]) if idx + 1 < len(order) else iter(())
                                gc = chain_gen(i)
                                done_p = done_c = False
                                while not (done_p and done_c):
                                    for _ in range(2):
                                        if not done_p:
                                            try:
                                                next(gp)
                                            except StopIteration:
                                                done_p = True
                                    if not done_c:
                                        try:
                                            next(gc)
                                        except StopIteration:
                                            done_c = True
                            fw.barrier()
                with ExitStack() as es4:
                    P4 = Pool_(fw, es4)
                    psr = Rot([P4.ps() for _ in range(3)])
                    xc = P4.sb([128, 512], F32)
                    sq = P4.sb([128, 512], F32)
                    rs = P4.sb([128, 512], F32)
                    txc, tsq, trs = Tok(), Tok(), Tok()
                    stg = Rot([P4.sb([128, 512], F32) for _ in range(2)])
                    tout = Tok()
                    for (lo, hi, ic) in SEGS:
                        n = hi - lo
                        ps, tps = psr.next()
                        fw.op("pe", lambda e, ps=ps, lo=lo, hi=hi, n=n: e.matmul(ps[:, :n], lhsT=bd1[:], rhs=yacc[:, lo:hi], start=True, stop=True), reads=[tya, tc], writes=[tps])
                        fw.op("dve", lambda e, ps=ps, lo=lo, hi=hi, n=n: e.scalar_tensor_tensor(out=xc[:, :n], in0=ps[:, :n], scalar=-1.0 / 64, in1=yacc[:, lo:hi], op0=ALU.mult, op1=ALU.add),
                              reads=[tps, tya], writes=[txc])
                        fw.op("act", lambda e, n=n: e.activation(out=sq[:, :n], in_=xc[:, :n], func=AF.Square), reads=[txc], writes=[tsq])
                        ps2, tps2 = psr.next()
                        fw.op("pe", lambda e, ps2=ps2, n=n: e.matmul(ps2[:, :n], lhsT=bd1[:], rhs=sq[:, :n], start=True, stop=True), reads=[tsq, tc], writes=[tps2])
                        rstd_from_ps(fw, rs, trs, ps2, tps2, n, 1.0 / 64, epsl[:, 0:1], tpp)
                        fw.op("dve", lambda e, n=n: e.tensor_tensor(out=xc[:, :n], in0=xc[:, :n], in1=rs[:, :n], op=ALU.mult), reads=[txc, trs], writes=[txc])
                        fw.op("act", lambda e, n=n: e.activation(out=xc[:, :n], in_=xc[:, :n], func=AF.Identity, scale=pp[:, 8, c:c + 1], bias=pp[:, 9, c:c + 1]),
                              reads=[txc, tpp], writes=[txc])
                        fw.op("pool", lambda e, lo=lo, hi=hi, n=n: e.tensor_tensor(out=xc[:, :n], in0=xc[:, :n], in1=bon[:, lo:hi], op=ALU.add), reads=[txc, tbon], writes=[txc])
                        ps3, tps3 = psr.next()
                        fw.op("pe", lambda e, ps3=ps3, lo=lo, hi=hi, n=n: e.matmul(ps3[:, :n], lhsT=wB[64:128, c * 128:(c + 1) * 128], rhs=lorB[64:128, lo:hi], start=True, stop=True),
                              reads=[tlw, tlor], writes=[tps3])
                        sg, tsg = stg.next()
                        fw.op("dve", lambda e, sg=sg, ps3=ps3, n=n: e.tensor_tensor(out=sg[:, :n], in0=ps3[:, :n], in1=xc[:, :n], op=ALU.mult), reads=[tps3, txc], writes=[tsg])
                        fw.dma("sp", RWT[c * 128:(c + 1) * 128, lo:hi], sg[:, :n], reads=[tsg], writes=[Tok()])
                    fw.barrier()
        fw.barrier()
NCORES = 8
DEPTH = 4
S5_KEYS = ["s5_a_re", "s5_a_im", "s5_log_step", "s5_b_re", "s5_b_im", "s5_c_re", "s5_c_im", "s5_d", "s5_glu_w", "s5_glu_b", "s5_out_g"]
RW_KEYS = ["rw_mu", "rw_w0", "rw_w2", "rw_a0", "rw_a2", "rw_g2", "rw_k_k", "rw_k_a", "rw_r_k", "rw_ln_g", "rw_ln_b"]
LAYER_KEYS = (["norm1_g", "norm2_g", "mod_w", "mod_b", "w_in", "w_out", "att_qn_g", "att_kn_g", "att_out_g",
               "ffn_up", "ffn_conv_w", "ffn_conv_b", "ffn_down"] + S5_KEYS + RW_KEYS)
SHARED_KEYS = ["c_ctx", "final_g", "ident", "POS", "CST", "RWMASK", "RWBD"]
PERCORE_KEYS = ["x_b", "ctx_b", "c_b"]
SCRATCH = {"ZT": [2048, TE], "VT": [T, 128], "MOD": [128, 48, 2], "S5T": [256, T], "ATT_T": [512, T], "RWT": [256, T],
           "XT1": [DM, T], "XA": [DM, T], "XB": [DM, T]}


def build_fused(depth=DEPTH):
    nc = bass.Bass("TRN2", target_bir_lowering=False)
    decl = {}

    def ext(name, shape, dt, kind):
        if name not in decl:
            decl[name] = nc.dram_tensor(name, list(shape), dt, kind=kind).ap()
        return decl[name]

    def make_io(l):
        xin = "XA" if l % 2 == 0 else "XB"
        xout = "XB" if l % 2 == 0 else "XA"

        def io(name, shape, dt, role):
            if name in LAYER_KEYS:
                full = ext(name, [DEPTH] + list(shape), dt, "ExternalInput")
                return full[l]
            if name in SHARED_KEYS or name in PERCORE_KEYS:
                return ext(name, shape, dt, "ExternalInput")
            if name == "OUT":
                return ext(name, shape, dt, "ExternalOutput")
            if name == "XT":
                name = xin
            elif name == "XT2":
                name = xout
            return ext(name, SCRATCH[name], dt, "Internal")
        return io
    with ExitStack() as es:
        fw = FW(nc, es)
        stage_p0(fw, make_io(0))
        for l in range(depth):
            io = make_io(l)
            for st in (stage_p1, stage_p23, stage_p4, stage_p5a, stage_p5b):
                st(fw, io)
        stage_p6(fw, make_io(depth))
        fw.barrier()
    return nc, fw


def tile_up(up):
    lead = up.shape[:-2]
    v = up.reshape(lead + (8, 128, 44, 128))
    nd = len(lead)
    v = np.transpose(v, tuple(range(nd)) + (nd + 2, nd + 1, nd + 0, nd + 3))
    return np.ascontiguousarray(v).reshape(lead + (44, 128, 1024))


_FUSED = {}


def kernel(**inp):
    inp = {k: np.ascontiguousarray(np.asarray(v)) for k, v in inp.items()}
    if "nc" not in _FUSED:
        _FUSED["nc"], _FUSED["fw"] = build_fused()
    nc = _FUSED["nc"]
    pos, cst = host_consts()
    rwm, rwbd = rw_consts()
    shared = {k: inp[k] for k in LAYER_KEYS if k != "rw_r_k"}
    shared["rw_r_k"] = inp["rw_r_k"].reshape(DEPTH, 256)
    shared["ffn_up"] = tile_up(inp["ffn_up"])
    shared.update(c_ctx=inp["c_ctx"], final_g=inp["final_g"], ident=np.eye(128, dtype=np.float32),
                  POS=pos, CST=cst, RWMASK=rwm, RWBD=rwbd)
    in_maps = [dict(shared, x_b=inp["x"][b], ctx_b=inp["ctx"][b], c_b=inp["c"][b]) for b in range(NCORES)]
    res = run_bass_kernel_spmd(nc, in_maps, core_ids=list(range(NCORES)))
    return np.stack([res.results[b]["OUT"] for b in range(NCORES)], 0).astype(np.float32)
```

```python
import math
import numpy as np
from contextlib import ExitStack
import concourse.bass as bass
import concourse.mybir as mybir
from concourse.bass_utils import run_bass_kernel_spmd

F32 = mybir.dt.float32
F32R = mybir.dt.float32r
BF16 = mybir.dt.bfloat16
I32 = mybir.dt.int32
ALU = mybir.AluOpType
AF = mybir.ActivationFunctionType
AX = mybir.AxisListType

T = 2304
TE = 2560
NCTX = 256
NLAT = 2048
DM = 1024
SEGS = [(0, 256, 1), (256, 768, 0), (768, 1280, 0), (1280, 1792, 0), (1792, 2304, 0)]
RMS_EPS = 1e-6


class Tok:
    __slots__ = ("w", "r")

    def __init__(self):
        self.w = None
        self.r = {}


class FW:
    ENG = ("pe", "dve", "act", "pool", "sp")
    NDMA = 8

    def __init__(self, nc, es):
        self.nc = nc
        self.es = es
        self.eng = {"pe": nc.tensor, "dve": nc.vector, "act": nc.scalar,
                    "pool": nc.gpsimd, "sp": nc.sync}
        self.sem = {}
        self.cnt = {}
        for e in self.ENG:
            self.sem[e] = es.enter_context(nc.semaphore("s_" + e))
            self.cnt[e] = 0
        self.dq = {}
        for q in ("sp", "pool", "act"):
            ring = []
            for i in range(self.NDMA):
                k = "d_%s_%d" % (q, i)
                self.sem[k] = es.enter_context(nc.semaphore(k))
                self.cnt[k] = 0
                ring.append(k)
            self.dq[q] = [ring, 0]
        self.seen = {e: {} for e in self.ENG}
        self.attach = True
        self.ninst = 0
        self.uid = 0

    def name(self, p):
        self.uid += 1
        return "%s_%d" % (p, self.uid)

    def _deps(self, reads, writes):
        deps = {}
        for t in reads:
            if t.w is not None and deps.get(t.w[0], 0) < t.w[1]:
                deps[t.w[0]] = t.w[1]
        for t in writes:
            if t.w is not None and deps.get(t.w[0], 0) < t.w[1]:
                deps[t.w[0]] = t.w[1]
            for k, v in t.r.items():
                if deps.get(k, 0) < v:
                    deps[k] = v
        return deps

    def _wait(self, e, deps):
        seen = self.seen[e]
        for k, v in deps.items():
            if seen.get(k, 0) < v:
                self.eng[e].wait_ge(self.sem[k], v)
                seen[k] = v

    def op(self, e, fn, reads=(), writes=()):
        deps = self._deps(reads, writes)
        seen = self.seen[e]
        need = [(k, v) for k, v in deps.items() if seen.get(k, 0) < v]
        att = None
        if need and self.attach:
            att = need.pop()
        for k, v in need:
            self.eng[e].wait_ge(self.sem[k], v)
            seen[k] = v
        inst = fn(self.eng[e])
        if att is not None:
            inst._wait_ge(self.sem[att[0]], att[1])
            seen[att[0]] = att[1]
        self.cnt[e] += 1
        inst.then_inc(self.sem[e], 1)
        v = self.cnt[e]
        for t in reads:
            t.r[e] = v
        for t in writes:
            t.w = (e, v)
            t.r = {}
        self.ninst += 1
        return inst

    def dma(self, q, out, in_, reads=(), writes=(), **kw):
        ring, idx = self.dq[q]
        k = ring[idx % len(ring)]
        self.dq[q][1] = idx + 1
        deps = self._deps(reads, writes)
        if self.cnt[k] > 0:
            deps[k] = max(deps.get(k, 0), self.cnt[k])
        self._wait(q, deps)
        inst = self.eng[q].dma_start(out=out, in_=in_, **kw)
        self.cnt[k] += 16
        inst.then_inc(self.sem[k], 16)
        v = self.cnt[k]
        for t in reads:
            t.r[k] = v
        for t in writes:
            t.w = (k, v)
            t.r = {}
        self.ninst += 1
        return inst

    def barrier(self, engines=None):
        allv = {k: v for k, v in self.cnt.items() if v > 0}
        for e in (engines or self.ENG):
            self._wait(e, allv)


class Pool_:
    def __init__(self, fw, es):
        self.fw = fw
        self.es = es
        self.nc = fw.nc

    def sb(self, shape, dt, name="t"):
        return self.es.enter_context(self.nc.sbuf_tensor(self.fw.name(name), list(shape), dt))

    def ps(self, shape=(128, 512), dt=F32, name="ps"):
        return self.es.enter_context(self.nc.psum_tensor(self.fw.name(name), list(shape), dt))


class Rot:
    def __init__(self, bufs):
        self.bufs = bufs
        self.toks = [Tok() for _ in bufs]
        self.i = 0

    def next(self):
        j = self.i % len(self.bufs)
        self.i += 1
        return self.bufs[j], self.toks[j]


def load_w_bf16(fw, P, W, rows, cols, tok, name="w", q="pool", chunk=2048, stage=None, cast_engs=("pool",)):
    kt = rows // 128
    wb = P.sb([128, kt, cols], BF16, name)
    chunk = min(chunk, cols)
    st = stage or Rot([P.sb([128, chunk], F32, "wstg") for _ in range(3)])
    ci = 0
    for k in range(kt):
        for c0 in range(0, cols, chunk):
            n = min(chunk, cols - c0)
            sg, tsg = st.next()
            fw.dma("sp", sg[:, :n], W[k * 128:(k + 1) * 128, c0:c0 + n], writes=[tsg])
            ce = cast_engs[ci % len(cast_engs)]
            ci += 1
            if ce == "act":
                fw.op("act", lambda e, sg=sg, k=k, c0=c0, n=n: e.copy(out=wb[:, k, c0:c0 + n], in_=sg[:, :n]), reads=[tsg], writes=[tok])
            else:
                fw.op(ce, lambda e, sg=sg, k=k, c0=c0, n=n: e.tensor_copy(out=wb[:, k, c0:c0 + n], in_=sg[:, :n]), reads=[tsg], writes=[tok])
    return wb


def stage_p0(fw, io):
    nc = fw.nc
    xb = io("x_b", [NLAT, DM], F32, "in")
    cb = io("ctx_b", [NCTX, DM], F32, "in")
    ident = io("ident", [128, 128], F32, "in")
    XT = io("XT", [DM, T], F32, "out")
    with ExitStack() as es:
        P = Pool_(fw, es)
        idt = P.sb([128, 128], F32)
        tid = Tok()
        fw.dma("sp", idt[:], ident, writes=[tid])
        xt = P.sb([128, 8, T], F32)
        txt = Tok()
        xin = Rot([P.sb([128, DM], F32) for _ in range(3)])
        pss = Rot([P.ps() for _ in range(4)])
        for tt in range(18):
            src = cb[tt * 128:(tt + 1) * 128, :] if tt < 2 else xb[(tt - 2) * 128:(tt - 1) * 128, :]
            xi, txi = xin.next()
            fw.dma("sp", xi[:], src, writes=[txi])
            for half in range(2):
                ps, tps = pss.next()
                for k in range(4):
                    kk = half * 4 + k
                    fw.op("pe", lambda e, ps=ps, k=k, kk=kk, xi=xi: e.transpose(
                        ps[:, k * 128:(k + 1) * 128], xi[:, kk * 128:(kk + 1) * 128], idt[:]),
                        reads=[txi, tid], writes=[tps])
                eng = "dve" if half == 0 else "act"
                outap = xt[:, half * 4:half * 4 + 4, tt * 128:(tt + 1) * 128]
                inap = ps[:].rearrange("p (k t) -> p k t", k=4)
                if eng == "dve":
                    fw.op("dve", lambda e, o=outap, i=inap: e.tensor_copy(out=o, in_=i), reads=[tps], writes=[txt])
                else:
                    fw.op("act", lambda e, o=outap, i=inap: e.copy(out=o, in_=i), reads=[tps], writes=[txt])
        tout = Tok()
        for k in range(8):
            fw.dma("sp", XT[k * 128:(k + 1) * 128, :], xt[:, k, :], reads=[txt], writes=[Tok()])
        fw.barrier()


def make_AB(fw, P, MODs, tmod, g_ap, sh_base, sc_base):
    g = P.sb([128, 8], F32)
    tg = Tok()
    fw.dma("sp", g[:], g_ap.rearrange("(k p) -> p k", p=128), writes=[tg], allow_slow_non_contiguous=True)
    AB = P.sb([128, 2, 2, 8], F32)
    tab = Tok()
    for ic in range(2):
        fw.op("dve", lambda e, ic=ic: e.tensor_scalar(out=AB[:, ic, 0, :], in0=MODs[:, sc_base:sc_base + 8, ic],
                                                      scalar1=1.0, scalar2=None, op0=ALU.add),
              reads=[tmod], writes=[tab])
        fw.op("dve", lambda e, ic=ic: e.tensor_tensor(out=AB[:, ic, 0, :], in0=AB[:, ic, 0, :], in1=g[:], op=ALU.mult),
              reads=[tg, tab], writes=[tab])
        fw.op("dve", lambda e, ic=ic: e.tensor_copy(out=AB[:, ic, 1, :], in_=MODs[:, sh_base:sh_base + 8, ic]),
              reads=[tmod], writes=[tab])
    return AB, tab


def norm_mod_seg(fw, P, st, xs, txs, n, ic, AB, tab, outs, touts):
    sq, ones, tones, psr, rs, tmpr = st["sq"], st["ones"], st["tones"], st["psr"], st["rs"], st["tmpr"]
    tsq, trs = st["tsq"], st["trs"]
    fw.op("act", lambda e: e.activation(out=sq[:, :, :n], in_=xs[:, :, :n], func=AF.Square), reads=[txs], writes=[tsq])
    ps, tps = psr.next()
    for k in range(8):
        fw.op("pe", lambda e, k=k: e.matmul(ps[:, :n], lhsT=ones[:], rhs=sq[:, k, :n], start=(k == 0), stop=(k == 7)),
              reads=[tsq, tones], writes=[tps])
    fw.op("act", lambda e: e.activation(out=rs[:, :n], in_=ps[:, :n], func=AF.Ln, scale=1.0 / DM, bias=st["eps"][:, 0:1]),
          reads=[tps, st["teps"]], writes=[trs])
    fw.op("act", lambda e: e.activation(out=rs[:, :n], in_=rs[:, :n], func=AF.Exp, scale=-0.5), reads=[trs], writes=[trs])
    for k in range(8):
        tmp, ttmp = tmpr.next()
        fw.op("dve", lambda e, k=k, tmp=tmp: e.tensor_tensor(out=tmp[:, :n], in0=xs[:, k, :n], in1=rs[:, :n], op=ALU.mult),
              reads=[txs, trs], writes=[ttmp])
        for o in outs(k):
            fw.op("act", lambda e, k=k, tmp=tmp, o=o: e.activation(out=o, in_=tmp[:, :n], func=AF.Identity,
                                                                   scale=AB[:, ic, 0, k:k + 1], bias=AB[:, ic, 1, k:k + 1]),
                  reads=[ttmp, tab], writes=touts)


def norm_state(fw, P):
    st = {}
    st["sq"] = P.sb([128, 8, 512], BF16)
    st["tsq"] = Tok()
    st["ones"] = P.sb([128, 128], BF16)
    st["tones"] = Tok()
    fw.op("pool", lambda e: e.memset(st["ones"][:], 1.0), writes=[st["tones"]])
    st["eps"] = P.sb([128, 1], F32)
    st["teps"] = Tok()
    fw.op("pool", lambda e: e.memset(st["eps"][:], RMS_EPS), writes=[st["teps"]])
    st["psr"] = Rot([P.ps() for _ in range(2)])
    st["rs"] = P.sb([128, 512], F32)
    st["trs"] = Tok()
    st["tmpr"] = Rot([P.sb([128, 512], F32) for _ in range(2)])
    return st


def stage_p1(fw, io):
    XT = io("XT", [DM, T], F32, "in")
    c_b = io("c_b", [DM], F32, "in")
    c_ctx = io("c_ctx", [DM], F32, "in")
    mod_w = io("mod_w", [DM, 6 * DM], F32, "in")
    mod_b = io("mod_b", [6 * DM], F32, "in")
    n1g = io("norm1_g", [DM], F32, "in")
    w_in = io("w_in", [DM, 1984], F32, "in")
    MOD = io("MOD", [128, 48, 2], F32, "out")
    ZT = io("ZT", [2048, TE], F32, "out")
    VT = io("VT", [T, 128], F32, "out")
    XTv = XT.rearrange("(k p) t -> p k t", p=128)
    with ExitStack() as es:
        P = Pool_(fw, es)
        wstage = Rot([P.sb([128, 2048], F32, "wstg") for _ in range(3)])
        tmw = Tok()
        cc = P.sb([128, 8, 2], F32)
        tcc = Tok()
        fw.dma("sp", cc[:, :, 0], c_b.rearrange("(k p) -> p k", p=128), writes=[tcc], allow_slow_non_contiguous=True)
        fw.dma("sp", cc[:, :, 1], c_ctx.rearrange("(k p) -> p k", p=128), writes=[tcc], allow_slow_non_contiguous=True)
        scb = P.sb([128, 8, 2], BF16)
        tscb = Tok()
        fw.op("act", lambda e: e.activation(out=scb[:], in_=cc[:], func=AF.Silu), reads=[tcc], writes=[tscb])
        mb = P.sb([128, 48], F32)
        tmb = Tok()
        fw.dma("sp", mb[:], mod_b.rearrange("(j p) -> p j", p=128), writes=[tmb], allow_slow_non_contiguous=True)
        MODs = P.sb([128, 48, 2], F32)
        tmod = Tok()
        with ExitStack() as es2:
            P2 = Pool_(fw, es2)
            mwb = load_w_bf16(fw, P2, mod_w, DM, 6 * DM, tmw, "modw", stage=wstage, cast_engs=("dve", "act", "pool"))
            psm = P2.ps([128, 512])
            tpsm = Tok()
            for j in range(48):
                for k in range(8):
                    fw.op("pe", lambda e, j=j, k=k: e.matmul(psm[:, 2 * j:2 * j + 2], lhsT=mwb[:, k, j * 128:(j + 1) * 128],
                                                             rhs=scb[:, k, :], start=(k == 0), stop=(k == 7)),
                          reads=[tmw, tscb], writes=[tpsm])
            for ic in range(2):
                fw.op("dve", lambda e, ic=ic: e.tensor_tensor(
                    out=MODs[:, :, ic], in0=psm[:, 0:96].rearrange("p (j c) -> p j c", c=2)[:, :, ic], in1=mb[:], op=ALU.add),
                    reads=[tpsm, tmb], writes=[tmod])
            fw.barrier()
        tmo = Tok()
        fw.dma("sp", MOD, MODs[:], reads=[tmod], writes=[tmo])
        AB, tab = make_AB(fw, P, MODs, tmod, n1g, 0, 8)
        tw = Tok()
        wb = load_w_bf16(fw, P, w_in, DM, 1984, tw, "win", stage=wstage, cast_engs=("dve", "act", "pool"))
        hT = P.sb([128, 8, TE], BF16)
        thT = Tok()
        st = norm_state(fw, P)
        xr = Rot([P.sb([128, 8, 512], F32) for _ in range(2)])
        for (lo, hi, ic) in SEGS:
            n = hi - lo
            xs, txs = xr.next()
            fw.dma("sp", xs[:, :, :n], XTv[:, :, lo:hi], writes=[txs])

            def outs(k, lo=lo, hi=hi, ic=ic):
                o = [hT[:, k, lo:hi]]
                if ic:
                    o.append(hT[:, k, T + lo:T + hi])
                return o
            norm_mod_seg(fw, P, st, xs, txs, n, ic, AB, tab, outs, [thT])
        psr = Rot([P.ps() for _ in range(4)])
        stg = Rot([P.sb([128, 512], F32) for _ in range(4)])
        tz = Tok()
        cnt = 0
        for nt in range(16):
            if nt == 7:
                continue
            M = 64 if nt == 15 else 128
            for cc_ in range(5):
                c0 = cc_ * 512
                ps, tps = psr.next()
                for k in range(8):
                    fw.op("pe", lambda e, ps=ps, k=k, nt=nt, M=M, c0=c0: e.matmul(
                        ps[0:M, :], lhsT=wb[:, k, nt * 128:nt * 128 + M], rhs=hT[:, k, c0:c0 + 512],
                        start=(k == 0), stop=(k == 7)), reads=[tw, thT], writes=[tps])
                sg, tsg = stg.next()
                if cnt % 2 == 0:
                    fw.op("dve", lambda e, sg=sg, ps=ps, M=M: e.tensor_copy(out=sg[0:M, :], in_=ps[0:M, :]), reads=[tps], writes=[tsg])
                else:
                    fw.op("act", lambda e, sg=sg, ps=ps, M=M: e.copy(out=sg[0:M, :], in_=ps[0:M, :]), reads=[tps], writes=[tsg])
                cnt += 1
                fw.dma("sp", ZT[nt * 128:nt * 128 + M, c0:c0 + 512], sg[0:M, :], reads=[tsg], writes=[Tok()])
        for tt in range(18):
            ps, tps = psr.next()
            for k in range(8):
                fw.op("pe", lambda e, ps=ps, k=k, tt=tt: e.matmul(
                    ps[:, 0:128], lhsT=hT[:, k, tt * 128:(tt + 1) * 128], rhs=wb[:, k, 896:1024],
                    start=(k == 0), stop=(k == 7)), reads=[tw, thT], writes=[tps])
            sg, tsg = stg.next()
            fw.op("dve", lambda e, sg=sg, ps=ps: e.tensor_copy(out=sg[:, 0:128], in_=ps[:, 0:128]), reads=[tps], writes=[tsg])
            fw.dma("sp", VT[tt * 128:(tt + 1) * 128, :], sg[:, 0:128], reads=[tsg], writes=[Tok()])
        fw.barrier()


def build_program(stage_fns):
    nc = bass.Bass("TRN2", target_bir_lowering=False)
    decl = {}

    def io(name, shape, dt, role):
        if name in decl:
            return decl[name][0]
        kind = "ExternalInput" if role == "in" else "ExternalOutput"
        ap = nc.dram_tensor(name, list(shape), dt, kind=kind).ap()
        decl[name] = (ap, role, shape)
        return ap
    with ExitStack() as es:
        fw = FW(nc, es)
        for fn in stage_fns:
            fn(fw, io)
        fw.barrier()
    return nc, decl, fw


_PROG_CACHE = {}


def run_stage(key, stage_fns, in_maps, ncores):
    if key not in _PROG_CACHE:
        _PROG_CACHE[key] = build_program(stage_fns)
    nc, decl, fw = _PROG_CACHE[key]
    res = run_bass_kernel_spmd(nc, in_maps, core_ids=list(range(ncores)))
    return res.results


def rstd_from_ps(fw, rs, trs, ps, tps, n, scale, epsap, teps, rows=128):
    fw.op("act", lambda e: e.activation(out=rs[0:rows, :n], in_=ps[0:rows, :n], func=AF.Ln, scale=scale, bias=epsap),
          reads=[tps, teps], writes=[trs])
    fw.op("act", lambda e: e.activation(out=rs[0:rows, :n], in_=rs[0:rows, :n], func=AF.Exp, scale=-0.5), reads=[trs], writes=[trs])


def stage_p3(fw, io):
    ZT = io("ZT", [2048, TE], F32, "in")
    VT = io("VT", [T, 128], F32, "in")
    qn_g = io("att_qn_g", [64], F32, "in")
    kn_g = io("att_kn_g", [64], F32, "in")
    og = io("att_out_g", [512], F32, "in")
    POS = io("POS", [128, NLAT], F32, "in")
    CST = io("CST", [128, 260], F32, "in")
    ATT = io("ATT_T", [512, T], F32, "out")
    with ExitStack() as es:
        P = Pool_(fw, es)
        cst = P.sb([128, 260], F32)
        tc = Tok()
        fw.dma("sp", cst[:], CST, writes=[tc])
        permb = P.sb([128, 128], BF16)
        bdb = P.sb([128, 128], BF16)
        fw.op("dve", lambda e: e.tensor_copy(out=permb[:], in_=cst[:, 1:129]), reads=[tc], writes=[tc])
        fw.op("dve", lambda e: e.tensor_copy(out=bdb[:], in_=cst[:, 129:257]), reads=[tc], writes=[tc])
        eps = P.sb([128, 2], F32)
        teps = Tok()
        fw.op("pool", lambda e: e.memset(eps[:, 0:1], RMS_EPS), writes=[teps])
        fw.op("pool", lambda e: e.memset(eps[:, 1:2], 0.0), writes=[teps])
        gq = P.sb([128, 2], F32)
        tg = Tok()
        for h in range(2):
            fw.dma("sp", gq[h * 64:(h + 1) * 64, 0:1], qn_g.rearrange("(p o) -> p o", o=1), writes=[tg])
            fw.dma("sp", gq[h * 64:(h + 1) * 64, 1:2], kn_g.rearrange("(p o) -> p o", o=1), writes=[tg])
        cos = P.sb([128, NLAT], F32)
        sin = P.sb([128, NLAT], F32)
        ttab = Tok()
        with ExitStack() as es2:
            P2 = Pool_(fw, es2)
            ang = P2.sb([128, NLAT], F32)
            tmpf = P2.sb([128, NLAT], F32)
            tmpi = P2.sb([128, NLAT], I32)
            ta = Tok()
            fw.dma("sp", ang[:], POS, writes=[ta])
            fw.op("dve", lambda e: e.tensor_scalar(out=ang[:], in0=ang[:], scalar1=cst[:, 0:1], scalar2=None, op0=ALU.mult),
                  reads=[ta, tc], writes=[ta])
            for (tab, off) in ((sin, 0.0), (cos, math.pi / 2)):
                fw.op("dve", lambda e, off=off: e.tensor_scalar(out=tmpi[:], in0=ang[:], scalar1=off, scalar2=1.0 / (2 * math.pi),
                                                                op0=ALU.add, op1=ALU.mult), reads=[ta], writes=[ta])
                fw.op("dve", lambda e: e.tensor_copy(out=tmpf[:], in_=tmpi[:]), reads=[ta], writes=[ta])
                fw.op("dve", lambda e: e.scalar_tensor_tensor(out=tmpf[:], in0=tmpf[:], scalar=-2 * math.pi, in1=ang[:],
                                                              op0=ALU.mult, op1=ALU.add), reads=[ta], writes=[ta])
                fw.op("dve", lambda e, off=off: e.tensor_scalar(out=tmpf[:], in0=tmpf[:], scalar1=off, scalar2=math.pi,
                                                                op0=ALU.add, op1=ALU.min), reads=[ta], writes=[ta])
                fw.op("dve", lambda e: e.tensor_scalar(out=tmpf[:], in0=tmpf[:], scalar1=-math.pi, scalar2=None, op0=ALU.max),
                      reads=[ta], writes=[ta])
                fw.op("act", lambda e, tab=tab: e.activation(out=tab[:], in_=tmpf[:], func=AF.Sin), reads=[ta], writes=[ttab])
            fw.barrier()
        qb = P.sb([128, 4, T], BF16)
        kd = P.sb([128, 2, T], BF16)
        tq = Tok()
        with ExitStack() as es2:
            P2 = Pool_(fw, es2)
            raw = Rot([P2.sb([128, 512], F32) for _ in range(2)])
            sqr = Rot([P2.sb([128, 512], BF16) for _ in range(2)])
            psr = Rot([P2.ps() for _ in range(2)])
            psr2 = Rot([P2.ps() for _ in range(2)])
            rsr = Rot([P2.sb([128, 512], F32) for _ in range(2)])
            nbr = Rot([P2.sb([128, 512], BF16) for _ in range(2)])
            t1r = Rot([P2.sb([128, 512], F32) for _ in range(2)])
            t2r = Rot([P2.sb([128, 512], F32) for _ in range(2)])
            items = [("q", j) for j in range(4)] + [("k", g) for g in range(2)]
            for (kind, j) in items:
                for (lo, hi, ic) in SEGS:
                    n = hi - lo
                    rw, trw = raw.next()
                    if kind == "q":
                        fw.dma("sp", rw[:, :n], ZT[256 + j * 128:256 + (j + 1) * 128, lo:hi], writes=[trw])
                        gcol = 0
                        dst = qb[:, j, lo:hi]
                    else:
                        for h in range(2):
                            fw.dma("sp", rw[h * 64:(h + 1) * 64, :n], ZT[768 + j * 64:768 + (j + 1) * 64, lo:hi], writes=[trw])
                        gcol = 1
                        dst = kd[:, j, lo:hi]
                    sq, tsq = sqr.next()
                    fw.op("act", lambda e, sq=sq, rw=rw, n=n: e.activation(out=sq[:, :n], in_=rw[:, :n], func=AF.Square), reads=[trw], writes=[tsq])
                    ps, tps = psr.next()
                    fw.op("pe", lambda e, ps=ps, sq=sq, n=n: e.matmul(ps[:, :n], lhsT=bdb[:], rhs=sq[:, :n], start=True, stop=True),
                          reads=[tsq, tc], writes=[tps])
                    rs, trs = rsr.next()
                    rstd_from_ps(fw, rs, trs, ps, tps, n, 1.0, eps[:, 0:1], teps)
                    t1, tt1 = t1r.next()
                    fw.op("dve", lambda e, t1=t1, rw=rw, rs=rs, n=n, gcol=gcol: e.scalar_tensor_tensor(
                        out=t1[:, :n], in0=rw[:, :n], scalar=gq[:, gcol:gcol + 1], in1=rs[:, :n], op0=ALU.mult, op1=ALU.mult),
                        reads=[trw, trs, tg], writes=[tt1])
                    if ic:
                        fw.op("act", lambda e, dst=dst, t1=t1, n=n: e.copy(out=dst, in_=t1[:, :n]), reads=[tt1], writes=[tq])
                        continue
                    nb, tnb = nbr.next()
                    fw.op("act", lambda e, nb=nb, t1=t1, n=n: e.copy(out=nb[:, :n], in_=t1[:, :n]), reads=[tt1], writes=[tnb])
                    ps2, tps2 = psr2.next()
                    fw.op("pe", lambda e, ps2=ps2, nb=nb, n=n: e.matmul(ps2[:, :n], lhsT=permb[:], rhs=nb[:, :n], start=True, stop=True),
                          reads=[tnb, tc], writes=[tps2])
                    p0 = lo - NCTX
                    t2, tt2 = t2r.next()
                    fw.op("dve", lambda e, t2=t2, ps2=ps2, n=n, p0=p0: e.tensor_tensor(out=t2[:, :n], in0=ps2[:, :n], in1=sin[:, p0:p0 + n], op=ALU.mult),
                          reads=[tps2, ttab], writes=[tt2])
                    fw.op("pool", lambda e, t1=t1, nb=nb, n=n, p0=p0: e.tensor_tensor(out=t1[:, :n], in0=nb[:, :n], in1=cos[:, p0:p0 + n], op=ALU.mult),
                          reads=[tnb, ttab, tt1], writes=[tt1])
                    fw.op("pool", lambda e, dst=dst, t1=t1, t2=t2, n=n: e.tensor_tensor(out=dst, in0=t1[:, :n], in1=t2[:, :n], op=ALU.add),
                          reads=[tt1, tt2], writes=[tq])
            fw.barrier()
        va = P.sb([128, 18, 2, 128], BF16)
        tva = Tok()
        fw.op("pool", lambda e: e.memset(va[:], 1.0), writes=[tva])
        for g in range(2):
            fw.dma("pool", va[:, :, g, 0:64], VT.rearrange("(t p) c -> p t c", p=128)[:, :, g * 64:(g + 1) * 64], writes=[tva])
        att = P.sb([128, 4, T], F32)
        tatt = Tok()
        pss = Rot([P.ps() for _ in range(3)])
        pso = Rot([P.ps() for _ in range(2)])
        ptr = Rot([P.sb([128, 512], BF16) for _ in range(3)])
        rcr = Rot([P.sb([64, 512], F32) for _ in range(2)])
        jobs = [(0, 256, 0, 2)] + [(256 + 512 * i, 768 + 512 * i, 0, 18) for i in range(4)]
        for h in range(8):
            g = h // 4
            jt, r0 = h // 2, (h % 2) * 64
            for (qlo, qhi, k0, k1) in jobs:
                n = qhi - qlo
                po, tpo = pso.next()
                for kt in range(k0, k1):
                    ps, tps = pss.next()
                    fw.op("pe", lambda e, ps=ps, kt=kt, g=g, jt=jt, r0=r0, qlo=qlo, qhi=qhi, n=n: e.matmul(
                        ps[:, :n], lhsT=kd[r0:r0 + 64, g, kt * 128:(kt + 1) * 128], rhs=qb[r0:r0 + 64, jt, qlo:qhi],
                        start=True, stop=True), reads=[tq], writes=[tps])
                    pt, tpt = ptr.next()
                    fw.op("act", lambda e, pt=pt, ps=ps, n=n: e.activation(out=pt[:, :n], in_=ps[:, :n], func=AF.Exp, scale=0.125),
                          reads=[tps], writes=[tpt])
                    fw.op("pe", lambda e, po=po, pt=pt, kt=kt, g=g, n=n, k0=k0, k1=k1: e.matmul(
                        po[:, :n], lhsT=va[:, kt, g, :], rhs=pt[:, :n], start=(kt == k0), stop=(kt == k1 - 1)),
                        reads=[tpt, tva], writes=[tpo])
                rc, trc = rcr.next()
                fw.op("dve", lambda e, rc=rc, po=po, n=n: e.reciprocal(out=rc[:, :n], in_=po[64:128, :n]), reads=[tpo], writes=[trc])
                fw.op("dve", lambda e, rc=rc, po=po, n=n, jt=jt, r0=r0, qlo=qlo, qhi=qhi: e.tensor_tensor(
                    out=att[r0:r0 + 64, jt, qlo:qhi], in0=po[0:64, :n], in1=rc[:, :n], op=ALU.mult),
                    reads=[tpo, trc], writes=[tatt])
        ogs = P.sb([128, 4], F32)
        tog = Tok()
        fw.dma("sp", ogs[:], og.rearrange("(k p) -> p k", p=128), writes=[tog], allow_slow_non_contiguous=True)
        ones = P.sb([128, 128], BF16)
        fw.op("pool", lambda e: e.memset(ones[:], 1.0), writes=[tog])
        sq4 = P.sb([128, 4, 512], BF16)
        tsq4 = Tok()
        rs = P.sb([128, 512], F32)
        trs = Tok()
        stg = Rot([P.sb([128, 512], F32) for _ in range(3)])
        tout = Tok()
        for (lo, hi, ic) in SEGS:
            n = hi - lo
            fw.op("act", lambda e, lo=lo, hi=hi, n=n: e.activation(out=sq4[:, :, :n], in_=att[:, :, lo:hi], func=AF.Square), reads=[tatt], writes=[tsq4])
            ps, tps = pss.next()
            for k in range(4):
                fw.op("pe", lambda e, ps=ps, k=k, n=n: e.matmul(ps[:, :n], lhsT=ones[:], rhs=sq4[:, k, :n], start=(k == 0), stop=(k == 3)),
                      reads=[tsq4, tog], writes=[tps])
            rstd_from_ps(fw, rs, trs, ps, tps, n, 1.0 / 512, eps[:, 0:1], teps)
            for k in range(4):
                sg, tsg = stg.next()
                fw.op("dve", lambda e, sg=sg, k=k, lo=lo, hi=hi, n=n: e.scalar_tensor_tensor(
                    out=sg[:, :n], in0=att[:, k, lo:hi], scalar=ogs[:, k:k + 1], in1=rs[:, :n], op0=ALU.mult, op1=ALU.mult),
                    reads=[tatt, trs, tog], writes=[tsg])
                fw.dma("sp", ATT[k * 128:(k + 1) * 128, lo:hi], sg[:, :n], reads=[tsg], writes=[Tok()])
        fw.barrier()


def host_consts():
    pos = np.zeros((128, NLAT), np.float32)
    inv = np.zeros((128,), np.float32)
    tok = np.arange(NLAT)
    for p in range(128):
        d = p % 64
        pos[p] = (tok // 64) if d < 32 else (tok % 64)
        inv[p] = 10000.0 ** (-(d % 16) / 16.0)
    cst = np.zeros((128, 260), np.float32)
    cst[:, 0] = inv
    perm = np.zeros((128, 128), np.float32)
    for m in range(128):
        d = m % 32
        if d < 16:
            perm[m + 16, m] = -1.0
        else:
            perm[m - 16, m] = 1.0
    cst[:, 1:129] = perm
    bd = np.zeros((128, 128), np.float32)
    bd[:64, :64] = 1.0 / 64
    bd[64:, 64:] = 1.0 / 64
    cst[:, 129:257] = bd
    return pos, cst


def stage_p5a(fw, io):
    XT = io("XT", [DM, T], F32, "in")
    S5T = io("S5T", [256, T], F32, "in")
    ATT = io("ATT_T", [512, T], F32, "in")
    RWT = io("RWT", [256, T], F32, "in")
    MOD = io("MOD", [128, 48, 2], F32, "in")
    w_out = io("w_out", [DM, DM], F32, "in")
    XT1 = io("XT1", [DM, T], F32, "out")
    XTv = XT.rearrange("(k p) t -> p k t", p=128)
    with ExitStack() as es:
        P = Pool_(fw, es)
        MODs = P.sb([128, 48, 2], F32)
        tmod = Tok()
        fw.dma("sp", MODs[:], MOD, writes=[tmod])
        tw = Tok()
        wb = load_w_bf16(fw, P, w_out, DM, DM, tw, "wout", cast_engs=("dve", "act", "pool"))
        cat = P.sb([128, 8, T], BF16)
        tcat = Tok()
        cstg = Rot([P.sb([128, T], F32, "cstg") for _ in range(2)])
        for k in range(8):
            src = S5T[k * 128:(k + 1) * 128, :] if k < 2 else (ATT[(k - 2) * 128:(k - 1) * 128, :] if k < 6 else RWT[(k - 6) * 128:(k - 5) * 128, :])
            sg, tsg = cstg.next()
            fw.dma("sp", sg[:], src, writes=[tsg])
            if k % 2 == 0:
                fw.op("dve", lambda e, sg=sg, k=k: e.tensor_copy(out=cat[:, k, :], in_=sg[:]), reads=[tsg], writes=[tcat])
            else:
                fw.op("act", lambda e, sg=sg, k=k: e.copy(out=cat[:, k, :], in_=sg[:]), reads=[tsg], writes=[tcat])
        xr = Rot([P.sb([128, 8, 512], F32) for _ in range(2)])
        x1r = Rot([P.sb([128, 8, 512], F32) for _ in range(2)])
        psr = Rot([P.ps() for _ in range(4)])
        to1, to2 = Tok(), Tok()
        for (lo, hi, ic) in SEGS:
            n = hi - lo
            xs, txs = xr.next()
            fw.dma("sp", xs[:, :, :n], XTv[:, :, lo:hi], writes=[txs])
            x1, tx1 = x1r.next()
            for d in range(8):
                ps, tps = psr.next()
                for k in range(8):
                    fw.op("pe", lambda e, ps=ps, k=k, d=d, lo=lo, hi=hi, n=n: e.matmul(
                        ps[:, :n], lhsT=wb[:, k, d * 128:(d + 1) * 128], rhs=cat[:, k, lo:hi], start=(k == 0), stop=(k == 7)),
                        reads=[tw, tcat], writes=[tps])
                fw.op("dve", lambda e, ps=ps, d=d, n=n, ic=ic, x1=x1, xs=xs: e.scalar_tensor_tensor(
                    out=x1[:, d, :n], in0=ps[:, :n], scalar=MODs[:, 16 + d, ic:ic + 1], in1=xs[:, d, :n], op0=ALU.mult, op1=ALU.add),
                    reads=[tps, txs, tmod], writes=[tx1])
            for k in range(8):
                fw.dma("sp", XT1[k * 128:(k + 1) * 128, lo:hi], x1[:, k, :n], reads=[tx1], writes=[Tok()])
        fw.barrier()


def stage_p5b(fw, io):
    XT1 = io("XT1", [DM, T], F32, "in")
    n2g = io("norm2_g", [DM], F32, "in")
    MOD = io("MOD", [128, 48, 2], F32, "in")
    up = io("ffn_up", [44, 128, DM], F32, "in")
    cw = io("ffn_conv_w", [3, 5632], F32, "in")
    cb = io("ffn_conv_b", [5632], F32, "in")
    down = io("ffn_down", [2816, DM], F32, "in")
    XT2 = io("XT2", [DM, T], F32, "out")
    X1v = XT1.rearrange("(k p) t -> p k t", p=128)
    X2v = XT2.rearrange("(k p) t -> p k t", p=128)
    with ExitStack() as es:
        P = Pool_(fw, es)
        MODs = P.sb([128, 48, 2], F32)
        tmod = Tok()
        fw.dma("sp", MODs[:], MOD, writes=[tmod])
        cws = P.sb([128, 44, 3], F32)
        cbs = P.sb([128, 44], F32)
        tcw = Tok()
        for w in range(3):
            fw.dma("sp", cws[:, :, w], cw[w].rearrange("(j p) -> p j", p=128), writes=[tcw], allow_slow_non_contiguous=True)
        fw.dma("sp", cbs[:], cb.rearrange("(j p) -> p j", p=128), writes=[tcw], allow_slow_non_contiguous=True)
        h2 = P.sb([128, 8, T], BF16)
        th2 = Tok()
        AB, tab = make_AB(fw, P, MODs, tmod, n2g, 24, 32)
        with ExitStack() as es2:
            P2 = Pool_(fw, es2)
            st = norm_state(fw, P2)
            xr0 = Rot([P2.sb([128, 8, 512], F32) for _ in range(2)])
            for (lo, hi, ic) in SEGS:
                n = hi - lo
                xs, txs = xr0.next()
                fw.dma("sp", xs[:, :, :n], X1v[:, :, lo:hi], writes=[txs])
                norm_mod_seg(fw, P2, st, xs, txs, n, ic, AB, tab, lambda k, lo=lo, hi=hi: [h2[:, k, lo:hi]], [th2])
            fw.barrier()
        hid = P.sb([128, 11, T], BF16)
        thid = Tok()
        dwb = P.sb([128, 11, DM], BF16)
        tdw = Tok()
        urot = [Rot([P.sb([128, T], F32) for _ in range(2)]) for _ in range(2)]
        y = [P.sb([128, T], F32) for _ in range(2)]
        ty = [Tok(), Tok()]
        psr = Rot([P.ps() for _ in range(4)])
        tx2 = Tok()
        RANGES = [(0, NCTX), (NCTX, T)]
        GROUPS = [(0, 2), (2, 2), (4, 2), (6, 2), (8, 2), (10, 1)]
        for half in range(2):
            with ExitStack() as esu:
                Pu = Pool_(fw, esu)
                ustg = Rot([Pu.sb([128, 8, 256], F32, "ustg") for _ in range(2)])
                for jj in range(11):
                    r0 = (half * 11 + jj) * 128
                    sg, tsg = ustg.next()
                    sgv = sg[:].rearrange("p k n -> p (k n)")[:, 0:DM]
                    fw.dma("sp", sgv, down[r0:r0 + 128, :], writes=[tsg])
                    fw.op("pool", lambda e, sgv=sgv, jj=jj: e.tensor_copy(out=dwb[:, jj, :], in_=sgv), reads=[tsg], writes=[tdw])
                ubr = Rot([Pu.sb([128, 8, 2, 256], BF16, "ub") for _ in range(2)])

                def issue_load(g, half=half):
                    jj0, ng = GROUPS[g]
                    ub, tub = ubr.next()
                    for wh in range(2):
                        sg, tsg = ustg.next()
                        sgv = sg[:].rearrange("p k (j c) -> p (k j c)", j=2).rearrange("p (j k c) -> p j k c", j=2, k=8)
                        for jl in range(ng):
                            jt = wh * 22 + half * 11 + jj0 + jl
                            fw.dma("sp", sgv[:, jl].rearrange("p k c -> p (k c)"), up[jt], writes=[tsg])
                            fw.op("pool", lambda e, sgv=sgv, ub=ub, wh=wh, jl=jl: e.tensor_copy(out=ub[:, :, wh, jl * 128:(jl + 1) * 128], in_=sgv[:, jl]),
                                  reads=[tsg], writes=[tub])
                    return ub, tub
                loaded = issue_load(0)
                for g, (jj0, ng) in enumerate(GROUPS):
                    ub, tub = loaded
                    if g + 1 < len(GROUPS):
                        loaded = issue_load(g + 1)
                    for jl in range(ng):
                        jj = jj0 + jl
                        j = half * 11 + jj
                        for wh in range(2):
                            jc = wh * 22 + j
                            ucur, tucur = urot[wh].next()
                            for si, (lo, hi, ic) in enumerate(SEGS):
                                n = hi - lo
                                ps, tps = psr.next()
                                for k in range(8):
                                    fw.op("pe", lambda e, ps=ps, k=k, wh=wh, ub=ub, jl=jl, lo=lo, hi=hi, n=n: e.matmul(
                                        ps[:, :n], lhsT=ub[:, k, wh, jl * 128:(jl + 1) * 128], rhs=h2[:, k, lo:hi], start=(k == 0), stop=(k == 7)),
                                        reads=[tub, th2], writes=[tps])
                                fw.op("act", lambda e, ps=ps, ucur=ucur, lo=lo, hi=hi, n=n: e.copy(out=ucur[:, lo:hi], in_=ps[:, :n]),
                                      reads=[tps], writes=[tucur])
                            fw.op("act", lambda e, wh=wh, jc=jc, ucur=ucur: e.activation(out=y[wh][:], in_=ucur[:], func=AF.Identity,
                                                                                        scale=cws[:, jc, 1:2], bias=cbs[:, jc:jc + 1]),
                                  reads=[tucur, tcw], writes=[ty[wh]])
                            for (lo, hi) in RANGES:
                                fw.op("dve", lambda e, wh=wh, jc=jc, lo=lo, hi=hi, ucur=ucur: e.scalar_tensor_tensor(
                                    out=y[wh][:, lo + 1:hi], in0=ucur[:, lo:hi - 1], scalar=cws[:, jc, 0:1], in1=y[wh][:, lo + 1:hi],
                                    op0=ALU.mult, op1=ALU.add), reads=[tucur, tcw, ty[wh]], writes=[ty[wh]])
                                fw.op("dve", lambda e, wh=wh, jc=jc, lo=lo, hi=hi, ucur=ucur: e.scalar_tensor_tensor(
                                    out=y[wh][:, lo:hi - 1], in0=ucur[:, lo + 1:hi], scalar=cws[:, jc, 2:3], in1=y[wh][:, lo:hi - 1],
                                    op0=ALU.mult, op1=ALU.add), reads=[tucur, tcw, ty[wh]], writes=[ty[wh]])
                        fw.op("act", lambda e: e.activation(out=y[0][:], in_=y[0][:], func=AF.Silu), reads=[ty[0]], writes=[ty[0]])
                        fw.op("dve", lambda e, jj=jj: e.tensor_tensor(out=hid[:, jj, :], in0=y[0][:], in1=y[1][:], op=ALU.mult),
                              reads=[ty[0], ty[1]], writes=[thid])
                fw.barrier()
            with ExitStack() as esd:
                Pd = Pool_(fw, esd)
                xr = Rot([Pd.sb([128, 8, 512], F32) for _ in range(2)])
                for (lo, hi, ic) in SEGS:
                    n = hi - lo
                    xs, txs = xr.next()
                    src = X1v if half == 0 else X2v
                    fw.dma("sp", xs[:, :, :n], src[:, :, lo:hi], reads=([tx2] if half else []), writes=[txs])
                    for d in range(8):
                        ps, tps = psr.next()
                        for jj in range(11):
                            fw.op("pe", lambda e, ps=ps, jj=jj, d=d, lo=lo, hi=hi, n=n: e.matmul(
                                ps[:, :n], lhsT=dwb[:, jj, d * 128:(d + 1) * 128], rhs=hid[:, jj, lo:hi], start=(jj == 0), stop=(jj == 10)),
                                reads=[tdw, thid], writes=[tps])
                        fw.op("dve", lambda e, ps=ps, d=d, n=n, ic=ic, xs=xs: e.scalar_tensor_tensor(
                            out=xs[:, d, :n], in0=ps[:, :n], scalar=MODs[:, 40 + d, ic:ic + 1], in1=xs[:, d, :n], op0=ALU.mult, op1=ALU.add),
                            reads=[tps, tmod, txs], writes=[txs])
                    for k in range(8):
                        fw.dma("sp", XT2[k * 128:(k + 1) * 128, lo:hi], xs[:, k, :n], reads=[txs], writes=[tx2])
                fw.barrier()


def stage_p6(fw, io):
    XT = io("XT", [DM, T], F32, "in")
    fg = io("final_g", [DM], F32, "in")
    ident = io("ident", [128, 128], F32, "in")
    OUT = io("OUT", [NLAT, DM], F32, "out")
    XTv = XT.rearrange("(k p) t -> p k t", p=128)
    with ExitStack() as es:
        P = Pool_(fw, es)
        idt = P.sb([128, 128], F32)
        tid = Tok()
        fw.dma("sp", idt[:], ident, writes=[tid])
        g = P.sb([128, 8], F32)
        fw.dma("sp", g[:], fg.rearrange("(k p) -> p k", p=128), writes=[tid], allow_slow_non_contiguous=True)
        st = norm_state(fw, P)
        xr = Rot([P.sb([128, 8, 512], F32) for _ in range(2)])
        yr = Rot([P.sb([128, 8, 512], F32) for _ in range(2)])
        psr = Rot([P.ps() for _ in range(4)])
        orr = Rot([P.sb([128, DM], F32) for _ in range(3)])
        tout = Tok()
        for (lo, hi, ic) in SEGS[1:]:
            n = hi - lo
            xs, txs = xr.next()
            fw.dma("sp", xs[:, :, :n], XTv[:, :, lo:hi], writes=[txs])
            sq, ones = st["sq"], st["ones"]
            fw.op("act", lambda e, xs=xs: e.activation(out=sq[:], in_=xs[:], func=AF.Square), reads=[txs], writes=[st["tsq"]])
            ps, tps = st["psr"].next()
            for k in range(8):
                fw.op("pe", lambda e, ps=ps, k=k: e.matmul(ps[:], lhsT=ones[:], rhs=sq[:, k, :], start=(k == 0), stop=(k == 7)),
                      reads=[st["tsq"], st["tones"]], writes=[tps])
            rstd_from_ps(fw, st["rs"], st["trs"], ps, tps, n, 1.0 / DM, st["eps"][:, 0:1], st["teps"])
            ys, tys = yr.next()
            for k in range(8):
                fw.op("dve", lambda e, ys=ys, xs=xs, k=k: e.scalar_tensor_tensor(
                    out=ys[:, k, :], in0=xs[:, k, :], scalar=g[:, k:k + 1], in1=st["rs"][:], op0=ALU.mult, op1=ALU.mult),
                    reads=[txs, st["trs"], tid], writes=[tys])
            for blk in range(4):
                ot, tot = orr.next()
                for half in range(2):
                    ps2, tps2 = psr.next()
                    for k in range(4):
                        kk = half * 4 + k
                        fw.op("pe", lambda e, ps2=ps2, k=k, kk=kk, ys=ys, blk=blk: e.transpose(
                            ps2[:, k * 128:(k + 1) * 128], ys[:, kk, blk * 128:(blk + 1) * 128], idt[:]),
                            reads=[tys, tid], writes=[tps2])
                    if half == 0:
                        fw.op("dve", lambda e, ot=ot, ps2=ps2: e.tensor_copy(out=ot[:, 0:512], in_=ps2[:]), reads=[tps2], writes=[tot])
                    else:
                        fw.op("act", lambda e, ot=ot, ps2=ps2: e.copy(out=ot[:, 512:1024], in_=ps2[:]), reads=[tps2], writes=[tot])
                r0 = lo - NCTX + blk * 128
                fw.dma("sp", OUT[r0:r0 + 128, :], ot[:], reads=[tot], writes=[Tok()])
        fw.barrier()


def sin_reduced(fw, P, out, src, tsrc, shape, off, tout):
    ti = P.sb(shape, I32)
    tf = P.sb(shape, F32)
    tt = Tok()
    fw.op("dve", lambda e: e.tensor_scalar(out=ti[:], in0=src, scalar1=off, scalar2=1.0 / (2 * math.pi), op0=ALU.add, op1=ALU.mult),
          reads=[tsrc], writes=[tt])
    fw.op("dve", lambda e: e.tensor_copy(out=tf[:], in_=ti[:]), reads=[tt], writes=[tt])
    fw.op("dve", lambda e: e.scalar_tensor_tensor(out=tf[:], in0=tf[:], scalar=-2 * math.pi, in1=src, op0=ALU.mult, op1=ALU.add),
          reads=[tt, tsrc], writes=[tt])
    fw.op("dve", lambda e: e.tensor_scalar(out=tf[:], in0=tf[:], scalar1=off, scalar2=math.pi, op0=ALU.add, op1=ALU.min), reads=[tt], writes=[tt])
    fw.op("dve", lambda e: e.tensor_scalar(out=tf[:], in0=tf[:], scalar1=-math.pi, scalar2=None, op0=ALU.max), reads=[tt], writes=[tt])
    fw.op("act", lambda e: e.activation(out=out, in_=tf[:], func=AF.Sin), reads=[tt], writes=[tout])


def stage_p2(fw, io):
    ZT = io("ZT", [2048, TE], F32, "in")
    a_re = io("s5_a_re", [2, 16, 64], F32, "in")
    a_im = io("s5_a_im", [2, 16, 64], F32, "in")
    lstep = io("s5_log_step", [2, 16], F32, "in")
    b_re = io("s5_b_re", [2, 16, 64, 16], F32, "in")
    b_im = io("s5_b_im", [2, 16, 64, 16], F32, "in")
    c_re = io("s5_c_re", [2, 16, 16, 64], F32, "in")
    c_im = io("s5_c_im", [2, 16, 16, 64], F32, "in")
    dsk = io("s5_d", [256], F32, "in")
    glu_w = io("s5_glu_w", [256, 256], F32, "in")
    glu_b = io("s5_glu_b", [256], F32, "in")
    out_g = io("s5_out_g", [256], F32, "in")
    S5T = io("S5T", [256, T], F32, "out")
    N = T
    with ExitStack() as es:
        P = Pool_(fw, es)
        are = P.sb([128, 2, 8], F32)
        aim = P.sb([128, 2, 8], F32)
        lst = P.sb([128, 2, 8], F32)
        tpar = Tok()
        for di in range(2):
            fw.dma("sp", are[:, di, :], a_re[di].rearrange("(s g) p -> (g p) s", g=2), writes=[tpar], allow_slow_non_contiguous=True)
            fw.dma("sp", aim[:, di, :], a_im[di].rearrange("(s g) p -> (g p) s", g=2), writes=[tpar], allow_slow_non_contiguous=True)
            for g2 in range(2):
                fw.dma("sp", lst[g2 * 64:(g2 + 1) * 64, di:di + 1, :],
                       lstep[di].rearrange("(s g) -> g s", g=2)[g2:g2 + 1, :].partition_broadcast(64), writes=[tpar],
                       allow_slow_non_contiguous=True)
        sh = [128, 2, 8]
        step = P.sb(sh, F32)
        fw.op("act", lambda e: e.activation(out=step[:], in_=lst[:], func=AF.Exp), reads=[tpar], writes=[tpar])
        er = P.sb(sh, F32)
        th = P.sb(sh, F32)
        fw.op("dve", lambda e: e.tensor_tensor(out=er[:], in0=are[:], in1=step[:], op=ALU.mult), reads=[tpar], writes=[tpar])
        fw.op("act", lambda e: e.activation(out=er[:], in_=er[:], func=AF.Exp), reads=[tpar], writes=[tpar])
        fw.op("dve", lambda e: e.tensor_tensor(out=th[:], in0=aim[:], in1=step[:], op=ALU.mult), reads=[tpar], writes=[tpar])
        sn = P.sb(sh, F32)
        cs = P.sb(sh, F32)
        ttrig = Tok()
        sin_reduced(fw, P, sn[:], th[:], tpar, sh, 0.0, ttrig)
        sin_reduced(fw, P, cs[:], th[:], tpar, sh, math.pi / 2, ttrig)
        PW = P.sb([128, 9, 3, 2, 8], F32)
        tpw = Tok()
        fw.op("dve", lambda e: e.tensor_tensor(out=PW[:, 0, 0], in0=er[:], in1=cs[:], op=ALU.mult), reads=[tpar, ttrig], writes=[tpw])
        fw.op("dve", lambda e: e.tensor_tensor(out=PW[:, 0, 1], in0=er[:], in1=sn[:], op=ALU.mult), reads=[tpar, ttrig], writes=[tpw])
        t1 = P.sb(sh, F32)
        t2 = P.sb(sh, F32)
        for k in range(9):
            fw.op("dve", lambda e, k=k: e.tensor_scalar(out=PW[:, k, 2], in0=PW[:, k, 1], scalar1=-1.0, scalar2=None, op0=ALU.mult),
                  reads=[tpw], writes=[tpw])
            if k == 8:
                break
            fw.op("dve", lambda e, k=k: e.tensor_tensor(out=t1[:], in0=PW[:, k, 0], in1=PW[:, k, 0], op=ALU.mult), reads=[tpw], writes=[tpw])
            fw.op("dve", lambda e, k=k: e.tensor_tensor(out=t2[:], in0=PW[:, k, 1], in1=PW[:, k, 1], op=ALU.mult), reads=[tpw], writes=[tpw])
            fw.op("dve", lambda e, k=k: e.tensor_tensor(out=PW[:, k + 1, 0], in0=t1[:], in1=t2[:], op=ALU.subtract), reads=[tpw], writes=[tpw])
            fw.op("dve", lambda e, k=k: e.scalar_tensor_tensor(out=PW[:, k + 1, 1], in0=PW[:, k, 0], scalar=2.0, in1=PW[:, k, 1],
                                                               op0=ALU.mult, op1=ALU.mult), reads=[tpw], writes=[tpw])
        br = P.sb(sh, F32)
        bi = P.sb(sh, F32)
        nbi = P.sb(sh, F32)
        den = P.sb(sh, F32)
        nr = P.sb(sh, F32)
        tb = Tok()
        fw.op("dve", lambda e: e.tensor_tensor(out=den[:], in0=are[:], in1=are[:], op=ALU.mult), reads=[tpar], writes=[tb])
        fw.op("dve", lambda e: e.tensor_tensor(out=t1[:], in0=aim[:], in1=aim[:], op=ALU.mult), reads=[tpar, tpw], writes=[tpw])
        fw.op("dve", lambda e: e.tensor_tensor(out=den[:], in0=den[:], in1=t1[:], op=ALU.add), reads=[tb, tpw], writes=[tb])
        fw.op("dve", lambda e: e.reciprocal(out=den[:], in_=den[:]), reads=[tb], writes=[tb])
        fw.op("dve", lambda e: e.tensor_scalar(out=nr[:], in0=PW[:, 0, 0], scalar1=-1.0, scalar2=None, op0=ALU.add), reads=[tpw], writes=[tb])
        fw.op("dve", lambda e: e.tensor_tensor(out=t1[:], in0=nr[:], in1=are[:], op=ALU.mult), reads=[tb, tpar, tpw], writes=[tpw])
        fw.op("dve", lambda e: e.tensor_tensor(out=t2[:], in0=PW[:, 0, 1], in1=aim[:], op=ALU.mult), reads=[tpw, tpar], writes=[tpw])
        fw.op("dve", lambda e: e.tensor_tensor(out=t1[:], in0=t1[:], in1=t2[:], op=ALU.add), reads=[tpw], writes=[tpw])
        fw.op("dve", lambda e: e.tensor_tensor(out=br[:], in0=t1[:], in1=den[:], op=ALU.mult), reads=[tpw, tb], writes=[tb])
        fw.op("dve", lambda e: e.tensor_tensor(out=t1[:], in0=PW[:, 0, 1], in1=are[:], op=ALU.mult), reads=[tpw, tpar, tb], writes=[tpw])
        fw.op("dve", lambda e: e.tensor_tensor(out=t2[:], in0=nr[:], in1=aim[:], op=ALU.mult), reads=[tb, tpar, tpw], writes=[tpw])
        fw.op("dve", lambda e: e.tensor_tensor(out=t1[:], in0=t1[:], in1=t2[:], op=ALU.subtract), reads=[tpw], writes=[tpw])
        fw.op("dve", lambda e: e.tensor_tensor(out=bi[:], in0=t1[:], in1=den[:], op=ALU.mult), reads=[tpw, tb], writes=[tb])
        fw.op("dve", lambda e: e.tensor_scalar(out=nbi[:], in0=bi[:], scalar1=-1.0, scalar2=None, op0=ALU.mult), reads=[tb], writes=[tb])
        BTb = P.sb([128, 2, 2, 8, 128], BF16)
        CTb = P.sb([128, 2, 2, 8, 128], BF16)
        tBT = Tok()
        tCT = Tok()
        with ExitStack() as es2:
            P2 = Pool_(fw, es2)
            BTf = P2.sb([128, 2, 2, 8, 128], F32)
            CTf = P2.sb([128, 2, 2, 8, 128], F32)
            CT2 = P2.sb([128, 2, 2, 8, 128], F32)
            tf1, tf2 = Tok(), Tok()
            fw.op("pool", lambda e: e.memset(BTf[:], 0.0), writes=[tf1])
            fw.op("pool", lambda e: e.memset(CTf[:], 0.0), writes=[tf2])
            for di in range(2):
                for ri, (bsrc, csrc) in enumerate(((b_re, c_re), (b_im, c_im))):
                    for g in range(16):
                        s, g2 = g // 2, g % 2
                        r0 = (g % 8) * 16
                        fw.dma("sp", BTf[r0:r0 + 16, di, ri, s, g2 * 64:(g2 + 1) * 64], bsrc[di, g].rearrange("p h -> h p"),
                               writes=[tf1], allow_slow_non_contiguous=True)
                        fw.dma("sp", CTf[g2 * 64:(g2 + 1) * 64, di, ri, s, r0:r0 + 16], csrc[di, g].rearrange("h p -> p h"),
                               writes=[tf2], allow_slow_non_contiguous=True)
            fw.op("act", lambda e: e.copy(out=BTb[:], in_=BTf[:]), reads=[tf1], writes=[tBT])
            bsh = [128, 2, 8, 128]
            brb = br[:].unsqueeze(3).to_broadcast(bsh)
            nbib = nbi[:].unsqueeze(3).to_broadcast(bsh)
            tc2 = Tok()
            fw.op("dve", lambda e: e.tensor_tensor(out=CT2[:, :, 0], in0=CTf[:, :, 0], in1=brb, op=ALU.mult), reads=[tf2, tb], writes=[tc2])
            fw.op("pool", lambda e: e.tensor_tensor(out=CT2[:, :, 1], in0=CTf[:, :, 1], in1=nbib, op=ALU.mult), reads=[tf2, tb], writes=[tc2])
            fw.op("dve", lambda e: e.tensor_tensor(out=CT2[:, :, 0], in0=CT2[:, :, 0], in1=CT2[:, :, 1], op=ALU.add), reads=[tc2], writes=[tc2])
            fw.op("act", lambda e: e.copy(out=CTb[:, :, 0], in_=CT2[:, :, 0]), reads=[tc2], writes=[tCT])
            fw.op("dve", lambda e: e.tensor_tensor(out=CT2[:, :, 0], in0=CTf[:, :, 0], in1=nbib, op=ALU.mult), reads=[tf2, tb, tCT, tc2], writes=[tc2])
            fw.op("pool", lambda e: e.tensor_tensor(out=CT2[:, :, 1], in0=CTf[:, :, 1], in1=brb, op=ALU.mult), reads=[tf2, tb, tc2], writes=[tc2])
            fw.op("dve", lambda e: e.tensor_tensor(out=CT2[:, :, 0], in0=CT2[:, :, 0], in1=CT2[:, :, 1], op=ALU.subtract), reads=[tc2], writes=[tc2])
            fw.op("act", lambda e: e.copy(out=CTb[:, :, 1], in_=CT2[:, :, 0]), reads=[tc2], writes=[tCT])
            fw.barrier()
        ub = P.sb([128, 2, TE], BF16)
        tub = Tok()
        for ct in range(2):
            fw.dma("pool", ub[:, ct, :], ZT[ct * 128:(ct + 1) * 128, :], writes=[tub])
        dk = P.sb([128, 2], F32)
        tdk = Tok()
        fw.dma("sp", dk[:], dsk.rearrange("(c p) -> p c", p=128), writes=[tdk], allow_slow_non_contiguous=True)
        y = P.sb([128, 2, T], F32)
        ty = Tok()
        for ct in range(2):
            fw.dma("sp", y[:, ct, :], ZT[ct * 128:(ct + 1) * 128, 0:T], writes=[ty])
        for ct in range(2):
            fw.op("pool", lambda e, ct=ct: e.tensor_scalar(out=y[:, ct, :], in0=y[:, ct, :], scalar1=dk[:, ct:ct + 1], scalar2=None, op0=ALU.mult),
                  reads=[ty, tdk], writes=[ty])
        Xs = Rot([P.sb([128, 2, N], F32) for _ in range(4)])
        xbs = Rot([P.sb([128, 2, N], BF16) for _ in range(2)])
        psr = Rot([P.ps() for _ in range(4)])
        psy = Rot([P.ps() for _ in range(2)])
        tmpy = Rot([P.sb([128, 512], F32) for _ in range(2)])
        CH = [(0, 512), (512, 1024), (1024, 1536), (1536, 2048), (2048, 2304)]

        def prep(di, s):
            off = 0 if di == 0 else 256
            ct = s // 4
            X, tX = Xs.next()
            for ri in range(2):
                for ci, (c0, c1) in enumerate(CH):
                    n = c1 - c0
                    ps, tps = psr.next()
                    fw.op("pe", lambda e, ps=ps, ri=ri, c0=c0, c1=c1, n=n: e.matmul(
                        ps[:, :n], lhsT=BTb[:, di, ri, s, :], rhs=ub[:, ct, off + c0:off + c1], start=True, stop=True),
                        reads=[tBT, tub], writes=[tps])
                    fw.op("act", lambda e, ps=ps, ri=ri, c0=c0, c1=c1, n=n: e.copy(out=X[:, ri, c0:c1], in_=ps[:, :n]),
                          reads=[tps], writes=[tX])
            return X, tX

        def scan_gen(di, s, X, tX):
            def cstep(w_re, w_im, r_re, r_im, k):
                pr = PW[:, k, 0, di, s:s + 1]
                pi = PW[:, k, 1, di, s:s + 1]
                npi = PW[:, k, 2, di, s:s + 1]
                for (o, i0, sc) in ((w_re, r_re, pr), (w_re, r_im, npi), (w_im, r_im, pr), (w_im, r_re, pi)):
                    fw.op("dve", lambda e, o=o, i0=i0, sc=sc: e.scalar_tensor_tensor(out=o, in0=i0, scalar=sc, in1=o, op0=ALU.mult, op1=ALU.add),
                          reads=[tX, tpw], writes=[tX])
                    yield
            for k in range(8):
                st_ = 1 << k
                Xv = [X[:, ri, :].rearrange("p (m c) -> p m c", c=2 * st_) for ri in range(2)]
                if di == 0:
                    yield from cstep(Xv[0][:, :, 2 * st_ - 1], Xv[1][:, :, 2 * st_ - 1], Xv[0][:, :, st_ - 1], Xv[1][:, :, st_ - 1], k)
                else:
                    yield from cstep(Xv[0][:, :, 0], Xv[1][:, :, 0], Xv[0][:, :, st_], Xv[1][:, :, st_], k)
            for i in (range(1, 9) if di == 0 else range(7, -1, -1)):
                if di == 0:
                    w, r = 256 * i + 255, 256 * (i - 1) + 255
                else:
                    w, r = 256 * i, 256 * (i + 1)
                yield from cstep(X[:, 0, w:w + 1], X[:, 1, w:w + 1], X[:, 0, r:r + 1], X[:, 1, r:r + 1], 8)
            for k in range(7, -1, -1):
                st_ = 1 << k
                Xv = [X[:, ri, :].rearrange("p (m c) -> p m c", c=2 * st_) for ri in range(2)]
                if di == 0:
                    yield from cstep(Xv[0][:, 1:, st_ - 1], Xv[1][:, 1:, st_ - 1], Xv[0][:, :-1, 2 * st_ - 1], Xv[1][:, :-1, 2 * st_ - 1], k)
                else:
                    yield from cstep(Xv[0][:, :-1, st_], Xv[1][:, :-1, st_], Xv[0][:, 1:, 0], Xv[1][:, 1:, 0], k)

        def fin(di, s, X, tX):
            ct = s // 4
            xb, txb = xbs.next()
            fw.op("act", lambda e: e.copy(out=xb[:], in_=X[:]), reads=[tX], writes=[txb])
            for (c0, c1) in CH:
                n = c1 - c0
                if di == 0:
                    y0 = c0
                else:
                    y0 = c0 + 256 if c0 < 2048 else 0
                ps, tps = psy.next()
                for ri in range(2):
                    fw.op("pe", lambda e, ps=ps, ri=ri, c0=c0, c1=c1, n=n: e.matmul(
                        ps[:, :n], lhsT=CTb[:, di, ri, s, :], rhs=xb[:, ri, c0:c1], start=(ri == 0), stop=(ri == 1)),
                        reads=[tCT, txb], writes=[tps])
                tm, ttm = tmpy.next()
                fw.op("act", lambda e, tm=tm, ps=ps, n=n: e.copy(out=tm[:, :n], in_=ps[:, :n]), reads=[tps], writes=[ttm])
                fw.op("pool", lambda e, tm=tm, y0=y0, n=n: e.tensor_tensor(out=y[:, ct, y0:y0 + n], in0=y[:, ct, y0:y0 + n], in1=tm[:, :n], op=ALU.add),
                      reads=[ttm, ty], writes=[ty])

        pairs = [(di, s0, s0 + 1) for di in range(2) for s0 in range(0, 8, 2)]
        nxt = [prep(pairs[0][0], pairs[0][1]), prep(pairs[0][0], pairs[0][2])]
        for pi_, (di, s0, s1) in enumerate(pairs):
            cur = nxt
            if pi_ + 1 < len(pairs):
                d2, a0, a1 = pairs[pi_ + 1]
                nxt = [prep(d2, a0), prep(d2, a1)]
            gens = [scan_gen(di, s0, *cur[0]), scan_gen(di, s1, *cur[1])]
            alive = [True, True]
            while any(alive):
                for gi in range(2):
                    if alive[gi]:
                        try:
                            next(gens[gi])
                        except StopIteration:
                            alive[gi] = False
            fin(di, s0, *cur[0])
            fin(di, s1, *cur[1])
        tg = Tok()
        gw = load_w_bf16(fw, P, glu_w, 256, 256, tg, "gluw")
        gb = P.sb([128, 2], F32)
        og = P.sb([128, 2], F32)
        fw.dma("sp", gb[:], glu_b.rearrange("(c p) -> p c", p=128), writes=[tg], allow_slow_non_contiguous=True)
        fw.dma("sp", og[:], out_g.rearrange("(c p) -> p c", p=128), writes=[tg], allow_slow_non_contiguous=True)
        ones = P.sb([128, 128], BF16)
        eps = P.sb([128, 1], F32)
        fw.op("pool", lambda e: e.memset(ones[:], 1.0), writes=[tg])
        fw.op("pool", lambda e: e.memset(eps[:], RMS_EPS), writes=[tg])
        a32 = P.sb([128, 2, 512], F32)
        ab = P.sb([128, 2, 512], BF16)
        w1 = P.sb([128, 2, 512], F32)
        w2 = P.sb([128, 2, 512], F32)
        sqb = P.sb([128, 2, 512], BF16)
        rs = P.sb([128, 512], F32)
        ta, tw1, trs = Tok(), Tok(), Tok()
        stg = Rot([P.sb([128, 512], F32) for _ in range(2)])
        tout = Tok()
        C1 = math.sqrt(2.0 / math.pi)
        for (lo, hi, ic) in SEGS:
            n = hi - lo
            yv = y[:, :, lo:hi]
            fw.op("act", lambda e, yv=yv, n=n: e.activation(out=w1[:, :, :n], in_=yv, func=AF.Square), reads=[ty], writes=[tw1])
            fw.op("dve", lambda e, n=n: e.tensor_scalar(out=w1[:, :, :n], in0=w1[:, :, :n], scalar1=0.044715 * C1, scalar2=C1, op0=ALU.mult, op1=ALU.add),
                  reads=[tw1], writes=[tw1])
            fw.op("dve", lambda e, yv=yv, n=n: e.tensor_tensor(out=w1[:, :, :n], in0=w1[:, :, :n], in1=yv, op=ALU.mult), reads=[tw1, ty], writes=[tw1])
            fw.op("act", lambda e, n=n: e.activation(out=w1[:, :, :n], in_=w1[:, :, :n], func=AF.Tanh), reads=[tw1], writes=[tw1])
            fw.op("dve", lambda e, n=n: e.tensor_scalar(out=w1[:, :, :n], in0=w1[:, :, :n], scalar1=0.5, scalar2=0.5, op0=ALU.mult, op1=ALU.add),
                  reads=[tw1], writes=[tw1])
            fw.op("dve", lambda e, yv=yv, n=n: e.tensor_tensor(out=a32[:, :, :n], in0=w1[:, :, :n], in1=yv, op=ALU.mult), reads=[tw1, ty], writes=[ta])
            fw.op("act", lambda e, n=n: e.copy(out=ab[:, :, :n], in_=a32[:, :, :n]), reads=[ta], writes=[ta])
            for nt in range(2):
                ps, tps = psr.next()
                for k in range(2):
                    fw.op("pe", lambda e, ps=ps, k=k, nt=nt, n=n: e.matmul(ps[:, :n], lhsT=gw[:, k, nt * 128:(nt + 1) * 128], rhs=ab[:, k, :n],
                                                                      start=(k == 0), stop=(k == 1)), reads=[tg, ta], writes=[tps])
                fw.op("act", lambda e, ps=ps, nt=nt, n=n: e.activation(out=w2[:, nt, :n], in_=ps[:, :n], func=AF.Sigmoid, bias=gb[:, nt:nt + 1]),
                      reads=[tps, tg], writes=[tw1])
            fw.op("dve", lambda e, n=n: e.tensor_tensor(out=w2[:, :, :n], in0=w2[:, :, :n], in1=a32[:, :, :n], op=ALU.mult), reads=[tw1, ta], writes=[tw1])
            fw.op("act", lambda e, n=n: e.activation(out=sqb[:, :, :n], in_=w2[:, :, :n], func=AF.Square), reads=[tw1], writes=[tw1])
            ps, tps = psr.next()
            for k in range(2):
                fw.op("pe", lambda e, ps=ps, k=k, n=n: e.matmul(ps[:, :n], lhsT=ones[:], rhs=sqb[:, k, :n], start=(k == 0), stop=(k == 1)),
                      reads=[tw1, tg], writes=[tps])
            rstd_from_ps(fw, rs, trs, ps, tps, n, 1.0 / 256, eps[:, 0:1], tg)
            for k in range(2):
                sg, tsg = stg.next()
                fw.op("dve", lambda e, sg=sg, k=k, n=n: e.scalar_tensor_tensor(out=sg[:, :n], in0=w2[:, k, :n], scalar=og[:, k:k + 1], in1=rs[:, :n],
                                                                             op0=ALU.mult, op1=ALU.mult), reads=[tw1, trs, tg], writes=[tsg])
                fw.dma("sp", S5T[k * 128:(k + 1) * 128, lo:hi], sg[:, :n], reads=[tsg], writes=[Tok()])
        fw.barrier()


def gen_p2(fw, io):
    ZT = io("ZT", [2048, TE], F32, "in")
    a_re = io("s5_a_re", [2, 16, 64], F32, "in")
    a_im = io("s5_a_im", [2, 16, 64], F32, "in")
    lstep = io("s5_log_step", [2, 16], F32, "in")
    b_re = io("s5_b_re", [2, 16, 64, 16], F32, "in")
    b_im = io("s5_b_im", [2, 16, 64, 16], F32, "in")
    c_re = io("s5_c_re", [2, 16, 16, 64], F32, "in")
    c_im = io("s5_c_im", [2, 16, 16, 64], F32, "in")
    dsk = io("s5_d", [256], F32, "in")
    glu_w = io("s5_glu_w", [256, 256], F32, "in")
    glu_b = io("s5_glu_b", [256], F32, "in")
    out_g = io("s5_out_g", [256], F32, "in")
    S5T = io("S5T", [256, T], F32, "out")
    N = T
    with ExitStack() as es:
        P = Pool_(fw, es)
        are = P.sb([128, 2, 8], F32)
        aim = P.sb([128, 2, 8], F32)
        lst = P.sb([128, 2, 8], F32)
        tpar = Tok()
        for di in range(2):
            fw.dma("sp", are[:, di, :], a_re[di].rearrange("(s g) p -> (g p) s", g=2), writes=[tpar], allow_slow_non_contiguous=True)
            fw.dma("sp", aim[:, di, :], a_im[di].rearrange("(s g) p -> (g p) s", g=2), writes=[tpar], allow_slow_non_contiguous=True)
            for g2 in range(2):
                fw.dma("sp", lst[g2 * 64:(g2 + 1) * 64, di:di + 1, :],
                       lstep[di].rearrange("(s g) -> g s", g=2)[g2:g2 + 1, :].partition_broadcast(64), writes=[tpar],
                       allow_slow_non_contiguous=True)
        sh = [128, 2, 8]
        step = P.sb(sh, F32)
        fw.op("act", lambda e: e.activation(out=step[:], in_=lst[:], func=AF.Exp), reads=[tpar], writes=[tpar])
        er = P.sb(sh, F32)
        th = P.sb(sh, F32)
        fw.op("dve", lambda e: e.tensor_tensor(out=er[:], in0=are[:], in1=step[:], op=ALU.mult), reads=[tpar], writes=[tpar])
        fw.op("act", lambda e: e.activation(out=er[:], in_=er[:], func=AF.Exp), reads=[tpar], writes=[tpar])
        fw.op("dve", lambda e: e.tensor_tensor(out=th[:], in0=aim[:], in1=step[:], op=ALU.mult), reads=[tpar], writes=[tpar])
        sn = P.sb(sh, F32)
        cs = P.sb(sh, F32)
        ttrig = Tok()
        sin_reduced(fw, P, sn[:], th[:], tpar, sh, 0.0, ttrig)
        sin_reduced(fw, P, cs[:], th[:], tpar, sh, math.pi / 2, ttrig)
        PW = P.sb([128, 9, 3, 2, 8], F32)
        tpw = Tok()
        fw.op("dve", lambda e: e.tensor_tensor(out=PW[:, 0, 0], in0=er[:], in1=cs[:], op=ALU.mult), reads=[tpar, ttrig], writes=[tpw])
        fw.op("dve", lambda e: e.tensor_tensor(out=PW[:, 0, 1], in0=er[:], in1=sn[:], op=ALU.mult), reads=[tpar, ttrig], writes=[tpw])
        t1 = P.sb(sh, F32)
        t2 = P.sb(sh, F32)
        for k in range(9):
            fw.op("dve", lambda e, k=k: e.tensor_scalar(out=PW[:, k, 2], in0=PW[:, k, 1], scalar1=-1.0, scalar2=None, op0=ALU.mult),
                  reads=[tpw], writes=[tpw])
            if k == 8:
                break
            fw.op("dve", lambda e, k=k: e.tensor_tensor(out=t1[:], in0=PW[:, k, 0], in1=PW[:, k, 0], op=ALU.mult), reads=[tpw], writes=[tpw])
            fw.op("dve", lambda e, k=k: e.tensor_tensor(out=t2[:], in0=PW[:, k, 1], in1=PW[:, k, 1], op=ALU.mult), reads=[tpw], writes=[tpw])
            fw.op("dve", lambda e, k=k: e.tensor_tensor(out=PW[:, k + 1, 0], in0=t1[:], in1=t2[:], op=ALU.subtract), reads=[tpw], writes=[tpw])
            fw.op("dve", lambda e, k=k: e.scalar_tensor_tensor(out=PW[:, k + 1, 1], in0=PW[:, k, 0], scalar=2.0, in1=PW[:, k, 1],
                                                               op0=ALU.mult, op1=ALU.mult), reads=[tpw], writes=[tpw])
        br = P.sb(sh, F32)
        bi = P.sb(sh, F32)
        nbi = P.sb(sh, F32)
        den = P.sb(sh, F32)
        nr = P.sb(sh, F32)
        tb = Tok()
        fw.op("dve", lambda e: e.tensor_tensor(out=den[:], in0=are[:], in1=are[:], op=ALU.mult), reads=[tpar], writes=[tb])
        fw.op("dve", lambda e: e.tensor_tensor(out=t1[:], in0=aim[:], in1=aim[:], op=ALU.mult), reads=[tpar, tpw], writes=[tpw])
        fw.op("dve", lambda e: e.tensor_tensor(out=den[:], in0=den[:], in1=t1[:], op=ALU.add), reads=[tb, tpw], writes=[tb])
        fw.op("dve", lambda e: e.reciprocal(out=den[:], in_=den[:]), reads=[tb], writes=[tb])
        fw.op("dve", lambda e: e.tensor_scalar(out=nr[:], in0=PW[:, 0, 0], scalar1=-1.0, scalar2=None, op0=ALU.add), reads=[tpw], writes=[tb])
        fw.op("dve", lambda e: e.tensor_tensor(out=t1[:], in0=nr[:], in1=are[:], op=ALU.mult), reads=[tb, tpar, tpw], writes=[tpw])
        fw.op("dve", lambda e: e.tensor_tensor(out=t2[:], in0=PW[:, 0, 1], in1=aim[:], op=ALU.mult), reads=[tpw, tpar], writes=[tpw])
        fw.op("dve", lambda e: e.tensor_tensor(out=t1[:], in0=t1[:], in1=t2[:], op=ALU.add), reads=[tpw], writes=[tpw])
        fw.op("dve", lambda e: e.tensor_tensor(out=br[:], in0=t1[:], in1=den[:], op=ALU.mult), reads=[tpw, tb], writes=[tb])
        fw.op("dve", lambda e: e.tensor_tensor(out=t1[:], in0=PW[:, 0, 1], in1=are[:], op=ALU.mult), reads=[tpw, tpar, tb], writes=[tpw])
        fw.op("dve", lambda e: e.tensor_tensor(out=t2[:], in0=nr[:], in1=aim[:], op=ALU.mult), reads=[tb, tpar, tpw], writes=[tpw])
        fw.op("dve", lambda e: e.tensor_tensor(out=t1[:], in0=t1[:], in1=t2[:], op=ALU.subtract), reads=[tpw], writes=[tpw])
        fw.op("dve", lambda e: e.tensor_tensor(out=bi[:], in0=t1[:], in1=den[:], op=ALU.mult), reads=[tpw, tb], writes=[tb])
        fw.op("dve", lambda e: e.tensor_scalar(out=nbi[:], in0=bi[:], scalar1=-1.0, scalar2=None, op0=ALU.mult), reads=[tb], writes=[tb])
        BTb = P.sb([128, 2, 2, 8, 128], BF16)
        CTb = P.sb([128, 2, 2, 8, 128], BF16)
        tBT = Tok()
        tCT = Tok()
        with ExitStack() as es2:
            P2 = Pool_(fw, es2)
            BTf = P2.sb([128, 2, 2, 8, 128], F32)
            CTf = P2.sb([128, 2, 2, 8, 128], F32)
            CT2 = P2.sb([128, 2, 2, 8, 128], F32)
            tf1, tf2 = Tok(), Tok()
            fw.op("pool", lambda e: e.memset(BTf[:], 0.0), writes=[tf1])
            fw.op("pool", lambda e: e.memset(CTf[:], 0.0), writes=[tf2])
            for di in range(2):
                for ri, (bsrc, csrc) in enumerate(((b_re, c_re), (b_im, c_im))):
                    for g in range(16):
                        s, g2 = g // 2, g % 2
                        r0 = (g % 8) * 16
                        fw.dma("sp", BTf[r0:r0 + 16, di, ri, s, g2 * 64:(g2 + 1) * 64], bsrc[di, g].rearrange("p h -> h p"),
                               writes=[tf1], allow_slow_non_contiguous=True)
                        fw.dma("sp", CTf[g2 * 64:(g2 + 1) * 64, di, ri, s, r0:r0 + 16], csrc[di, g].rearrange("h p -> p h"),
                               writes=[tf2], allow_slow_non_contiguous=True)
            fw.op("act", lambda e: e.copy(out=BTb[:], in_=BTf[:]), reads=[tf1], writes=[tBT])
            bsh = [128, 2, 8, 128]
            brb = br[:].unsqueeze(3).to_broadcast(bsh)
            nbib = nbi[:].unsqueeze(3).to_broadcast(bsh)
            tc2 = Tok()
            fw.op("dve", lambda e: e.tensor_tensor(out=CT2[:, :, 0], in0=CTf[:, :, 0], in1=brb, op=ALU.mult), reads=[tf2, tb], writes=[tc2])
            fw.op("pool", lambda e: e.tensor_tensor(out=CT2[:, :, 1], in0=CTf[:, :, 1], in1=nbib, op=ALU.mult), reads=[tf2, tb], writes=[tc2])
            fw.op("dve", lambda e: e.tensor_tensor(out=CT2[:, :, 0], in0=CT2[:, :, 0], in1=CT2[:, :, 1], op=ALU.add), reads=[tc2], writes=[tc2])
            fw.op("act", lambda e: e.copy(out=CTb[:, :, 0], in_=CT2[:, :, 0]), reads=[tc2], writes=[tCT])
            fw.op("dve", lambda e: e.tensor_tensor(out=CT2[:, :, 0], in0=CTf[:, :, 0], in1=nbib, op=ALU.mult), reads=[tf2, tb, tCT, tc2], writes=[tc2])
            fw.op("pool", lambda e: e.tensor_tensor(out=CT2[:, :, 1], in0=CTf[:, :, 1], in1=brb, op=ALU.mult), reads=[tf2, tb, tc2], writes=[tc2])
            fw.op("dve", lambda e: e.tensor_tensor(out=CT2[:, :, 0], in0=CT2[:, :, 0], in1=CT2[:, :, 1], op=ALU.subtract), reads=[tc2], writes=[tc2])
            fw.op("act", lambda e: e.copy(out=CTb[:, :, 1], in_=CT2[:, :, 0]), reads=[tc2], writes=[tCT])
            fw.barrier()
        ub = P.sb([128, 2, TE], BF16)
        tub = Tok()
        for ct in range(2):
            fw.dma("pool", ub[:, ct, :], ZT[ct * 128:(ct + 1) * 128, :], writes=[tub])
        dk = P.sb([128, 2], F32)
        tdk = Tok()
        fw.dma("sp", dk[:], dsk.rearrange("(c p) -> p c", p=128), writes=[tdk], allow_slow_non_contiguous=True)
        y = P.sb([128, 2, T], F32)
        ty = Tok()
        for ct in range(2):
            fw.dma("sp", y[:, ct, :], ZT[ct * 128:(ct + 1) * 128, 0:T], writes=[ty])
        for ct in range(2):
            fw.op("pool", lambda e, ct=ct: e.tensor_scalar(out=y[:, ct, :], in0=y[:, ct, :], scalar1=dk[:, ct:ct + 1], scalar2=None, op0=ALU.mult),
                  reads=[ty, tdk], writes=[ty])
        Xs = Rot([P.sb([128, 2, N], F32) for _ in range(2)])
        xbs = Rot([P.sb([128, 2, N], BF16) for _ in range(2)])
        psr = Rot([P.ps() for _ in range(2)])
        psy = Rot([P.ps() for _ in range(1)])
        tmpy = Rot([P.sb([128, 512], F32) for _ in range(2)])
        CH = [(0, 512), (512, 1024), (1024, 1536), (1536, 2048), (2048, 2304)]

        def prep(di, s):
            off = 0 if di == 0 else 256
            ct = s // 4
            X, tX = Xs.next()
            for ri in range(2):
                for ci, (c0, c1) in enumerate(CH):
                    n = c1 - c0
                    ps, tps = psr.next()
                    fw.op("pe", lambda e, ps=ps, ri=ri, c0=c0, c1=c1, n=n: e.matmul(
                        ps[:, :n], lhsT=BTb[:, di, ri, s, :], rhs=ub[:, ct, off + c0:off + c1], start=True, stop=True),
                        reads=[tBT, tub], writes=[tps])
                    fw.op("act", lambda e, ps=ps, ri=ri, c0=c0, c1=c1, n=n: e.copy(out=X[:, ri, c0:c1], in_=ps[:, :n]),
                          reads=[tps], writes=[tX])
            return X, tX

        def scan_gen(di, s, X, tX):
            def cstep(w_re, w_im, r_re, r_im, k):
                pr = PW[:, k, 0, di, s:s + 1]
                pi = PW[:, k, 1, di, s:s + 1]
                npi = PW[:, k, 2, di, s:s + 1]
                for (o, i0, sc) in ((w_re, r_re, pr), (w_re, r_im, npi), (w_im, r_im, pr), (w_im, r_re, pi)):
                    fw.op("dve", lambda e, o=o, i0=i0, sc=sc: e.scalar_tensor_tensor(out=o, in0=i0, scalar=sc, in1=o, op0=ALU.mult, op1=ALU.add),
                          reads=[tX, tpw], writes=[tX])
                    yield
            for k in range(8):
                st_ = 1 << k
                Xv = [X[:, ri, :].rearrange("p (m c) -> p m c", c=2 * st_) for ri in range(2)]
                if di == 0:
                    yield from cstep(Xv[0][:, :, 2 * st_ - 1], Xv[1][:, :, 2 * st_ - 1], Xv[0][:, :, st_ - 1], Xv[1][:, :, st_ - 1], k)
                else:
                    yield from cstep(Xv[0][:, :, 0], Xv[1][:, :, 0], Xv[0][:, :, st_], Xv[1][:, :, st_], k)
            for i in (range(1, 9) if di == 0 else range(7, -1, -1)):
                if di == 0:
                    w, r = 256 * i + 255, 256 * (i - 1) + 255
                else:
                    w, r = 256 * i, 256 * (i + 1)
                yield from cstep(X[:, 0, w:w + 1], X[:, 1, w:w + 1], X[:, 0, r:r + 1], X[:, 1, r:r + 1], 8)
            for k in range(7, -1, -1):
                st_ = 1 << k
                Xv = [X[:, ri, :].rearrange("p (m c) -> p m c", c=2 * st_) for ri in range(2)]
                if di == 0:
                    yield from cstep(Xv[0][:, 1:, st_ - 1], Xv[1][:, 1:, st_ - 1], Xv[0][:, :-1, 2 * st_ - 1], Xv[1][:, :-1, 2 * st_ - 1], k)
                else:
                    yield from cstep(Xv[0][:, :-1, st_], Xv[1][:, :-1, st_], Xv[0][:, 1:, 0], Xv[1][:, 1:, 0], k)

        def fin(di, s, X, tX):
            ct = s // 4
            xb, txb = xbs.next()
            fw.op("act", lambda e: e.copy(out=xb[:], in_=X[:]), reads=[tX], writes=[txb])
            for (c0, c1) in CH:
                n = c1 - c0
                if di == 0:
                    y0 = c0
                else:
                    y0 = c0 + 256 if c0 < 2048 else 0
                ps, tps = psy.next()
                for ri in range(2):
                    fw.op("pe", lambda e, ps=ps, ri=ri, c0=c0, c1=c1, n=n: e.matmul(
                        ps[:, :n], lhsT=CTb[:, di, ri, s, :], rhs=xb[:, ri, c0:c1], start=(ri == 0), stop=(ri == 1)),
                        reads=[tCT, txb], writes=[tps])
                tm, ttm = tmpy.next()
                fw.op("act", lambda e, tm=tm, ps=ps, n=n: e.copy(out=tm[:, :n], in_=ps[:, :n]), reads=[tps], writes=[ttm])
                fw.op("pool", lambda e, tm=tm, y0=y0, n=n: e.tensor_tensor(out=y[:, ct, y0:y0 + n], in0=y[:, ct, y0:y0 + n], in1=tm[:, :n], op=ALU.add),
                      reads=[ttm, ty], writes=[ty])

        pairs = [(di, s0, s0 + 1) for di in range(2) for s0 in range(0, 8, 2)]
        yield "core"
        for pi_, (di, s0, s1) in enumerate(pairs):
            cur = [prep(di, s0), prep(di, s1)]
            yield
            gens = [scan_gen(di, s0, *cur[0]), scan_gen(di, s1, *cur[1])]
            alive = [True, True]
            while any(alive):
                for gi in range(2):
                    if alive[gi]:
                        try:
                            next(gens[gi])
                            yield
                        except StopIteration:
                            alive[gi] = False
            fin(di, s0, *cur[0])
            yield
            fin(di, s1, *cur[1])
            yield
        yield "fin"
        tg = Tok()
        gw = load_w_bf16(fw, P, glu_w, 256, 256, tg, "gluw")
        gb = P.sb([128, 2], F32)
        og = P.sb([128, 2], F32)
        fw.dma("sp", gb[:], glu_b.rearrange("(c p) -> p c", p=128), writes=[tg], allow_slow_non_contiguous=True)
        fw.dma("sp", og[:], out_g.rearrange("(c p) -> p c", p=128), writes=[tg], allow_slow_non_contiguous=True)
        ones = P.sb([128, 128], BF16)
        eps = P.sb([128, 1], F32)
        fw.op("pool", lambda e: e.memset(ones[:], 1.0), writes=[tg])
        fw.op("pool", lambda e: e.memset(eps[:], RMS_EPS), writes=[tg])
        a32 = P.sb([128, 2, 512], F32)
        ab = P.sb([128, 2, 512], BF16)
        w1 = P.sb([128, 2, 512], F32)
        w2 = P.sb([128, 2, 512], F32)
        sqb = P.sb([128, 2, 512], BF16)
        rs = P.sb([128, 512], F32)
        ta, tw1, trs = Tok(), Tok(), Tok()
        stg = Rot([P.sb([128, 512], F32) for _ in range(2)])
        tout = Tok()
        C1 = math.sqrt(2.0 / math.pi)
        for (lo, hi, ic) in SEGS:
            n = hi - lo
            yv = y[:, :, lo:hi]
            fw.op("act", lambda e, yv=yv, n=n: e.activation(out=w1[:, :, :n], in_=yv, func=AF.Square), reads=[ty], writes=[tw1])
            fw.op("dve", lambda e, n=n: e.tensor_scalar(out=w1[:, :, :n], in0=w1[:, :, :n], scalar1=0.044715 * C1, scalar2=C1, op0=ALU.mult, op1=ALU.add),
                  reads=[tw1], writes=[tw1])
            fw.op("dve", lambda e, yv=yv, n=n: e.tensor_tensor(out=w1[:, :, :n], in0=w1[:, :, :n], in1=yv, op=ALU.mult), reads=[tw1, ty], writes=[tw1])
            fw.op("act", lambda e, n=n: e.activation(out=w1[:, :, :n], in_=w1[:, :, :n], func=AF.Tanh), reads=[tw1], writes=[tw1])
            fw.op("dve", lambda e, n=n: e.tensor_scalar(out=w1[:, :, :n], in0=w1[:, :, :n], scalar1=0.5, scalar2=0.5, op0=ALU.mult, op1=ALU.add),
                  reads=[tw1], writes=[tw1])
            fw.op("dve", lambda e, yv=yv, n=n: e.tensor_tensor(out=a32[:, :, :n], in0=w1[:, :, :n], in1=yv, op=ALU.mult), reads=[tw1, ty], writes=[ta])
            fw.op("act", lambda e, n=n: e.copy(out=ab[:, :, :n], in_=a32[:, :, :n]), reads=[ta], writes=[ta])
            for nt in range(2):
                ps, tps = psr.next()
                for k in range(2):
                    fw.op("pe", lambda e, ps=ps, k=k, nt=nt, n=n: e.matmul(ps[:, :n], lhsT=gw[:, k, nt * 128:(nt + 1) * 128], rhs=ab[:, k, :n],
                                                                      start=(k == 0), stop=(k == 1)), reads=[tg, ta], writes=[tps])
                fw.op("act", lambda e, ps=ps, nt=nt, n=n: e.activation(out=w2[:, nt, :n], in_=ps[:, :n], func=AF.Sigmoid, bias=gb[:, nt:nt + 1]),
                      reads=[tps, tg], writes=[tw1])
            fw.op("dve", lambda e, n=n: e.tensor_tensor(out=w2[:, :, :n], in0=w2[:, :, :n], in1=a32[:, :, :n], op=ALU.mult), reads=[tw1, ta], writes=[tw1])
            fw.op("act", lambda e, n=n: e.activation(out=sqb[:, :, :n], in_=w2[:, :, :n], func=AF.Square), reads=[tw1], writes=[tw1])
            ps, tps = psr.next()
            for k in range(2):
                fw.op("pe", lambda e, ps=ps, k=k, n=n: e.matmul(ps[:, :n], lhsT=ones[:], rhs=sqb[:, k, :n], start=(k == 0), stop=(k == 1)),
                      reads=[tw1, tg], writes=[tps])
            rstd_from_ps(fw, rs, trs, ps, tps, n, 1.0 / 256, eps[:, 0:1], tg)
            for k in range(2):
                sg, tsg = stg.next()
                fw.op("dve", lambda e, sg=sg, k=k, n=n: e.scalar_tensor_tensor(out=sg[:, :n], in0=w2[:, k, :n], scalar=og[:, k:k + 1], in1=rs[:, :n],
                                                                             op0=ALU.mult, op1=ALU.mult), reads=[tw1, trs, tg], writes=[tsg])
                fw.dma("sp", S5T[k * 128:(k + 1) * 128, lo:hi], sg[:, :n], reads=[tsg], writes=[Tok()])
        fw.barrier()


def gen_p3(fw, io):
    ZT = io("ZT", [2048, TE], F32, "in")
    VT = io("VT", [T, 128], F32, "in")
    qn_g = io("att_qn_g", [64], F32, "in")
    kn_g = io("att_kn_g", [64], F32, "in")
    og = io("att_out_g", [512], F32, "in")
    POS = io("POS", [128, NLAT], F32, "in")
    CST = io("CST", [128, 260], F32, "in")
    ATT = io("ATT_T", [512, T], F32, "out")
    with ExitStack() as es:
        P = Pool_(fw, es)
        cst = P.sb([128, 260], F32)
        tc = Tok()
        fw.dma("sp", cst[:], CST, writes=[tc])
        permb = P.sb([128, 128], BF16)
        bdb = P.sb([128, 128], BF16)
        fw.op("dve", lambda e: e.tensor_copy(out=permb[:], in_=cst[:, 1:129]), reads=[tc], writes=[tc])
        fw.op("dve", lambda e: e.tensor_copy(out=bdb[:], in_=cst[:, 129:257]), reads=[tc], writes=[tc])
        eps = P.sb([128, 2], F32)
        teps = Tok()
        fw.op("pool", lambda e: e.memset(eps[:, 0:1], RMS_EPS), writes=[teps])
        fw.op("pool", lambda e: e.memset(eps[:, 1:2], 0.0), writes=[teps])
        gq = P.sb([128, 2], F32)
        tg = Tok()
        for h in range(2):
            fw.dma("sp", gq[h * 64:(h + 1) * 64, 0:1], qn_g.rearrange("(p o) -> p o", o=1), writes=[tg])
            fw.dma("sp", gq[h * 64:(h + 1) * 64, 1:2], kn_g.rearrange("(p o) -> p o", o=1), writes=[tg])
        qb = P.sb([128, 4, T], BF16)
        kd = P.sb([128, 2, T], BF16)
        tq = Tok()
        esr = ExitStack()
        Pr = Pool_(fw, esr)
        cos = Pr.sb([128, NLAT], F32)
        sin = Pr.sb([128, NLAT], F32)
        ttab = Tok()
        with ExitStack() as es2:
            P2 = Pool_(fw, es2)
            ang = P2.sb([128, NLAT], F32)
            tmpf = P2.sb([128, NLAT], F32)
            tmpi = P2.sb([128, NLAT], I32)
            ta = Tok()
            fw.dma("sp", ang[:], POS, writes=[ta])
            fw.op("dve", lambda e: e.tensor_scalar(out=ang[:], in0=ang[:], scalar1=cst[:, 0:1], scalar2=None, op0=ALU.mult),
                  reads=[ta, tc], writes=[ta])
            for (tab, off) in ((sin, 0.0), (cos, math.pi / 2)):
                fw.op("dve", lambda e, off=off: e.tensor_scalar(out=tmpi[:], in0=ang[:], scalar1=off, scalar2=1.0 / (2 * math.pi),
                                                                op0=ALU.add, op1=ALU.mult), reads=[ta], writes=[ta])
                fw.op("dve", lambda e: e.tensor_copy(out=tmpf[:], in_=tmpi[:]), reads=[ta], writes=[ta])
                fw.op("dve", lambda e: e.scalar_tensor_tensor(out=tmpf[:], in0=tmpf[:], scalar=-2 * math.pi, in1=ang[:],
                                                              op0=ALU.mult, op1=ALU.add), reads=[ta], writes=[ta])
                fw.op("dve", lambda e, off=off: e.tensor_scalar(out=tmpf[:], in0=tmpf[:], scalar1=off, scalar2=math.pi,
                                                                op0=ALU.add, op1=ALU.min), reads=[ta], writes=[ta])
                fw.op("dve", lambda e: e.tensor_scalar(out=tmpf[:], in0=tmpf[:], scalar1=-math.pi, scalar2=None, op0=ALU.max),
                      reads=[ta], writes=[ta])
                fw.op("act", lambda e, tab=tab: e.activation(out=tab[:], in_=tmpf[:], func=AF.Sin), reads=[ta], writes=[ttab])
            fw.barrier()
        with ExitStack() as es2:
            P2 = Pool_(fw, es2)
            raw = Rot([P2.sb([128, 512], F32) for _ in range(2)])
            sqr = Rot([P2.sb([128, 512], BF16) for _ in range(2)])
            psr = Rot([P2.ps() for _ in range(2)])
            psr2 = Rot([P2.ps() for _ in range(2)])
            rsr = Rot([P2.sb([128, 512], F32) for _ in range(2)])
            nbr = Rot([P2.sb([128, 512], BF16) for _ in range(2)])
            t1r = Rot([P2.sb([128, 512], F32) for _ in range(2)])
            t2r = Rot([P2.sb([128, 512], F32) for _ in range(2)])
            items = [("q", j) for j in range(4)] + [("k", g) for g in range(2)]
            for (kind, j) in items:
                for (lo, hi, ic) in SEGS:
                    n = hi - lo
                    rw, trw = raw.next()
                    if kind == "q":
                        fw.dma("sp", rw[:, :n], ZT[256 + j * 128:256 + (j + 1) * 128, lo:hi], writes=[trw])
                        gcol = 0
                        dst = qb[:, j, lo:hi]
                    else:
                        for h in range(2):
                            fw.dma("sp", rw[h * 64:(h + 1) * 64, :n], ZT[768 + j * 64:768 + (j + 1) * 64, lo:hi], writes=[trw])
                        gcol = 1
                        dst = kd[:, j, lo:hi]
                    sq, tsq = sqr.next()
                    fw.op("act", lambda e, sq=sq, rw=rw, n=n: e.activation(out=sq[:, :n], in_=rw[:, :n], func=AF.Square), reads=[trw], writes=[tsq])
                    ps, tps = psr.next()
                    fw.op("pe", lambda e, ps=ps, sq=sq, n=n: e.matmul(ps[:, :n], lhsT=bdb[:], rhs=sq[:, :n], start=True, stop=True),
                          reads=[tsq, tc], writes=[tps])
                    rs, trs = rsr.next()
                    rstd_from_ps(fw, rs, trs, ps, tps, n, 1.0, eps[:, 0:1], teps)
                    t1, tt1 = t1r.next()
                    fw.op("dve", lambda e, t1=t1, rw=rw, rs=rs, n=n, gcol=gcol: e.scalar_tensor_tensor(
                        out=t1[:, :n], in0=rw[:, :n], scalar=gq[:, gcol:gcol + 1], in1=rs[:, :n], op0=ALU.mult, op1=ALU.mult),
                        reads=[trw, trs, tg], writes=[tt1])
                    if ic:
                        fw.op("act", lambda e, dst=dst, t1=t1, n=n: e.copy(out=dst, in_=t1[:, :n]), reads=[tt1], writes=[tq])
                        continue
                    nb, tnb = nbr.next()
                    fw.op("act", lambda e, nb=nb, t1=t1, n=n: e.copy(out=nb[:, :n], in_=t1[:, :n]), reads=[tt1], writes=[tnb])
                    ps2, tps2 = psr2.next()
                    fw.op("pe", lambda e, ps2=ps2, nb=nb, n=n: e.matmul(ps2[:, :n], lhsT=permb[:], rhs=nb[:, :n], start=True, stop=True),
                          reads=[tnb, tc], writes=[tps2])
                    p0 = lo - NCTX
                    t2, tt2 = t2r.next()
                    fw.op("dve", lambda e, t2=t2, ps2=ps2, n=n, p0=p0: e.tensor_tensor(out=t2[:, :n], in0=ps2[:, :n], in1=sin[:, p0:p0 + n], op=ALU.mult),
                          reads=[tps2, ttab], writes=[tt2])
                    fw.op("pool", lambda e, t1=t1, nb=nb, n=n, p0=p0: e.tensor_tensor(out=t1[:, :n], in0=nb[:, :n], in1=cos[:, p0:p0 + n], op=ALU.mult),
                          reads=[tnb, ttab, tt1], writes=[tt1])
                    fw.op("pool", lambda e, dst=dst, t1=t1, t2=t2, n=n: e.tensor_tensor(out=dst, in0=t1[:, :n], in1=t2[:, :n], op=ALU.add),
                          reads=[tt1, tt2], writes=[tq])
            fw.barrier()
        esr.close()
        va = P.sb([128, 18, 2, 128], BF16)
        tva = Tok()
        fw.op("pool", lambda e: e.memset(va[:], 1.0), writes=[tva])
        for g in range(2):
            fw.dma("pool", va[:, :, g, 0:64], VT.rearrange("(t p) c -> p t c", p=128)[:, :, g * 64:(g + 1) * 64], writes=[tva])
        att = P.sb([128, 4, T], BF16)
        tatt = Tok()
        pss = Rot([P.ps() for _ in range(2)])
        pso = Rot([P.ps() for _ in range(3)])
        pend = []
        yield "core"
        ptr = Rot([P.sb([128, 512], BF16) for _ in range(3)])
        rcr = Rot([P.sb([64, 512], F32) for _ in range(2)])
        jobs = [(0, 256, 0, 2)] + [(256 + 512 * i, 768 + 512 * i, 0, 18) for i in range(4)]
        for h in range(8):
            g = h // 4
            jt, r0 = h // 2, (h % 2) * 64
            for (qlo, qhi, k0, k1) in jobs:
                n = qhi - qlo
                po, tpo = pso.next()
                for kt in range(k0, k1):
                    ps, tps = pss.next()
                    fw.op("pe", lambda e, ps=ps, kt=kt, g=g, jt=jt, r0=r0, qlo=qlo, qhi=qhi, n=n: e.matmul(
                        ps[:, :n], lhsT=kd[r0:r0 + 64, g, kt * 128:(kt + 1) * 128], rhs=qb[r0:r0 + 64, jt, qlo:qhi],
                        start=True, stop=True), reads=[tq], writes=[tps])
                    pt, tpt = ptr.next()
                    fw.op("act", lambda e, pt=pt, ps=ps, n=n: e.activation(out=pt[:, :n], in_=ps[:, :n], func=AF.Exp, scale=0.125),
                          reads=[tps], writes=[tpt])
                    fw.op("pe", lambda e, po=po, pt=pt, kt=kt, g=g, n=n, k0=k0, k1=k1: e.matmul(
                        po[:, :n], lhsT=va[:, kt, g, :], rhs=pt[:, :n], start=(kt == k0), stop=(kt == k1 - 1)),
                        reads=[tpt, tva], writes=[tpo])
                def norm_job(po=po, tpo=tpo, n=n, jt=jt, r0=r0, qlo=qlo, qhi=qhi):
                    rc, trc = rcr.next()
                    fw.op("dve", lambda e: e.reciprocal(out=rc[:, :n], in_=po[64:128, :n]), reads=[tpo], writes=[trc])
                    fw.op("dve", lambda e: e.tensor_tensor(out=att[r0:r0 + 64, jt, qlo:qhi], in0=po[0:64, :n], in1=rc[:, :n], op=ALU.mult),
                          reads=[tpo, trc], writes=[tatt])
                pend.append(norm_job)
                if len(pend) > 2:
                    pend.pop(0)()
                yield
        for nj in pend:
            nj()
        yield "fin"
        ogs = P.sb([128, 4], F32)
        tog = Tok()
        fw.dma("sp", ogs[:], og.rearrange("(k p) -> p k", p=128), writes=[tog], allow_slow_non_contiguous=True)
        ones = P.sb([128, 128], BF16)
        fw.op("pool", lambda e: e.memset(ones[:], 1.0), writes=[tog])
        sq4 = P.sb([128, 4, 512], BF16)
        tsq4 = Tok()
        rs = P.sb([128, 512], F32)
        trs = Tok()
        stg = Rot([P.sb([128, 512], F32) for _ in range(3)])
        tout = Tok()
        for (lo, hi, ic) in SEGS:
            n = hi - lo
            fw.op("act", lambda e, lo=lo, hi=hi, n=n: e.activation(out=sq4[:, :, :n], in_=att[:, :, lo:hi], func=AF.Square), reads=[tatt], writes=[tsq4])
            ps, tps = pss.next()
            for k in range(4):
                fw.op("pe", lambda e, ps=ps, k=k, n=n: e.matmul(ps[:, :n], lhsT=ones[:], rhs=sq4[:, k, :n], start=(k == 0), stop=(k == 3)),
                      reads=[tsq4, tog], writes=[tps])
            rstd_from_ps(fw, rs, trs, ps, tps, n, 1.0 / 512, eps[:, 0:1], teps)
            for k in range(4):
                sg, tsg = stg.next()
                fw.op("dve", lambda e, sg=sg, k=k, lo=lo, hi=hi, n=n: e.scalar_tensor_tensor(
                    out=sg[:, :n], in0=att[:, k, lo:hi], scalar=ogs[:, k:k + 1], in1=rs[:, :n], op0=ALU.mult, op1=ALU.mult),
                    reads=[tatt, trs, tog], writes=[tsg])
                fw.dma("sp", ATT[k * 128:(k + 1) * 128, lo:hi], sg[:, :n], reads=[tsg], writes=[Tok()])
        fw.barrier()


def stage_p23(fw, io):
    g2 = gen_p2(fw, io)
    g3 = gen_p3(fw, io)

    def until(g, tag):
        for v in g:
            if v == tag:
                return True
        return False
    until(g2, "core")
    until(g3, "core")
    a2 = a3 = True
    R = 36
    while a2 or a3:
        if a2:
            for _ in range(R):
                if next(g2) == "fin":
                    a2 = False
                    break
        if a3:
            if next(g3) == "fin":
                a3 = False
    for _ in g3:
        pass
    for _ in g2:
        pass


RW_BASE = 1024
CHK = 64
NCH = T // CHK
LN_EPS_RW = 64e-5


def rw_consts():
    idx = np.arange(64)
    m = np.zeros((64, 4, 64), np.float32)
    m[:, 0, :] = (idx[:, None] < idx[None, :])
    m[:, 1, :] = (idx[:, None] > idx[None, :])
    m[:, 2, :] = (idx[:, None] <= idx[None, :])
    m[:, 3, :] = (idx[:, None] >= idx[None, :])
    bd = np.zeros((128, 128), np.float32)
    bd[:64, :64] = 1.0
    bd[64:, 64:] = 1.0
    return m, bd


def stage_p4(fw, io):
    ZT = io("ZT", [2048, TE], F32, "in")
    mu = io("rw_mu", [960], F32, "in")
    w0 = io("rw_w0", [2, 256], F32, "in")
    w2 = io("rw_w2", [2, 32, 256], F32, "in")
    a0 = io("rw_a0", [2, 256], F32, "in")
    a2 = io("rw_a2", [2, 32, 256], F32, "in")
    g2 = io("rw_g2", [64, 256], F32, "in")
    k_k = io("rw_k_k", [256], F32, "in")
    k_a = io("rw_k_a", [256], F32, "in")
    r_k = io("rw_r_k", [256], F32, "in")
    ln_g = io("rw_ln_g", [256], F32, "in")
    ln_b = io("rw_ln_b", [256], F32, "in")
    ident = io("ident", [128, 128], F32, "in")
    MASKS = io("RWMASK", [64, 4, 64], F32, "in")
    BD = io("RWBD", [128, 128], F32, "in")
    RWT = io("RWT", [256, T], F32, "out")
    N = T
    CH5 = [(0, 512), (512, 1024), (1024, 1536), (1536, 2048), (2048, 2560)]
    with ExitStack() as es:
        P = Pool_(fw, es)
        tc = Tok()
        idt = P.sb([128, 128], F32)
        idb = P.sb([128, 128], BF16)
        msk = P.sb([64, 4, 64], F32)
        bd1 = P.sb([128, 128], F32)
        fw.dma("sp", idt[:], ident, writes=[tc])
        fw.dma("sp", msk[:], MASKS, writes=[tc])
        fw.dma("sp", bd1[:], BD, writes=[tc])
        fw.op("dve", lambda e: e.tensor_copy(out=idb[:], in_=idt[:]), reads=[tc], writes=[tc])
        mrep = P.sb([64, 4, 4, 64], F32)
        for rep in range(4):
            fw.op("dve", lambda e, rep=rep: e.tensor_copy(out=mrep[:, :, rep, :], in_=msk[:]), reads=[tc], writes=[tc])
        pp = P.sb([128, 12, 2], F32)
        tpp = Tok()
        srcs = [w0[0], w0[1], a0[0], a0[1], k_k, k_a, k_a, r_k, ln_g, ln_b]
        for i, sap in enumerate(srcs):
            fw.dma("sp", pp[:, i, :], sap.rearrange("(c p) -> p c", p=128), writes=[tpp], allow_slow_non_contiguous=True)
        fw.op("dve", lambda e: e.tensor_scalar(out=pp[:, 6, :], in0=pp[:, 6, :], scalar1=-1.0, scalar2=1.0, op0=ALU.mult, op1=ALU.add),
              reads=[tpp], writes=[tpp])
        epsl = P.sb([128, 2], F32)
        fw.op("pool", lambda e: e.memset(epsl[:, 0:1], LN_EPS_RW), writes=[tpp])
        fw.op("pool", lambda e: e.memset(epsl[:, 1:2], 1e-24), writes=[tpp])
        wA = P.sb([128, 256], BF16)
        wB = P.sb([128, 256], BF16)
        tlw = Tok()
        fw.dma("pool", wA[0:32, :], w2[0], writes=[tlw])
        fw.dma("pool", wA[32:64, :], w2[1], writes=[tlw])
        fw.dma("pool", wA[64:96, :], a2[0], writes=[tlw])
        fw.dma("pool", wB[0:32, :], a2[1], writes=[tlw])
        fw.dma("pool", wB[64:128, :], g2, writes=[tlw])
        smask = P.sb([128, N], BF16)
        tsm = Tok()
        fw.op("pool", lambda e: e.memset(smask[:], 1.0), writes=[tsm])
        fw.op("pool", lambda e: e.memset(smask[:].rearrange("p (c j) -> p c j", j=CHK)[:, :, 0], 0.0), writes=[tsm])

        def shift_tmp(P2):
            return (Rot([P2.sb([128, TE], F32) for _ in range(2)]), Rot([P2.sb([128, TE], F32) for _ in range(2)]),
                    Rot([P2.sb([128, 2], F32) for _ in range(2)]))

        def load_shift(dst, tdst, row0, rows, tmp, post=None, pb=0):
            zt_, tz = tmp[0].next()
            nbt_, tnb = tmp[1].next()
            mtt_, tm = tmp[2].next()
            ps_ = slice(pb, pb + rows)
            z = zt_[ps_, :]
            fw.dma("sp", z, ZT[RW_BASE + row0:RW_BASE + row0 + rows, :], writes=[tz])
            fw.dma("sp", mtt_[ps_, 0:1], mu[row0:row0 + rows].rearrange("(p o) -> p o", o=1), writes=[tm])
            fw.op("dve", lambda e: e.tensor_scalar(out=mtt_[ps_, 1:2], in0=mtt_[ps_, 0:1], scalar1=0.5, scalar2=None, op0=ALU.mult), reads=[tm], writes=[tm])
            fw.op("dve", lambda e: e.tensor_scalar(out=mtt_[ps_, 0:1], in0=mtt_[ps_, 0:1], scalar1=-1.0, scalar2=1.0, op0=ALU.mult, op1=ALU.add),
                  reads=[tm], writes=[tm])
            fw.op("pool", lambda e: e.memset(nbt_[ps_, 0:1], 0.0), writes=[tnb])
            fw.op("pool", lambda e: e.tensor_copy(out=nbt_[ps_, 1:TE], in_=zt_[ps_, 0:TE - 1]), reads=[tz], writes=[tnb])
            fw.op("pool", lambda e: e.tensor_tensor(out=nbt_[ps_, 0:TE - 1], in0=nbt_[ps_, 0:TE - 1], in1=zt_[ps_, 1:TE], op=ALU.add),
                  reads=[tz, tnb], writes=[tnb])
            for cb in (256, 2304):
                fw.op("pool", lambda e, cb=cb: e.tensor_tensor(out=nbt_[ps_, cb:cb + 1], in0=nbt_[ps_, cb:cb + 1], in1=zt_[ps_, cb - 1:cb], op=ALU.subtract),
                      reads=[tz, tnb], writes=[tnb])
                fw.op("pool", lambda e, cb=cb: e.tensor_tensor(out=nbt_[ps_, cb - 1:cb], in0=nbt_[ps_, cb - 1:cb], in1=zt_[ps_, cb:cb + 1], op=ALU.subtract),
                      reads=[tz, tnb], writes=[tnb])
            fw.op("act", lambda e: e.activation(out=z, in_=z, func=AF.Identity, scale=mtt_[ps_, 0:1]), reads=[tz, tm], writes=[tz])
            if post is None:
                fw.op("dve", lambda e: e.scalar_tensor_tensor(out=dst, in0=nbt_[ps_, :], scalar=mtt_[ps_, 1:2], in1=z, op0=ALU.mult, op1=ALU.add),
                      reads=[tz, tnb, tm], writes=[tdst])
            else:
                fw.op("dve", lambda e: e.scalar_tensor_tensor(out=z, in0=nbt_[ps_, :], scalar=mtt_[ps_, 1:2], in1=z, op0=ALU.mult, op1=ALU.add),
                      reads=[tz, tnb, tm], writes=[tz])
                fw.op("act", lambda e: e.activation(out=dst, in_=z, func=post), reads=[tz], writes=[tdst])

        lorA = P.sb([128, TE], BF16)
        lorB = P.sb([128, TE], BF16)
        tlor = Tok()
        with ExitStack() as es2:
            tmp = shift_tmp(Pool_(fw, es2))
            for i in range(4):
                dstt = lorA[32 * i:32 * i + 32, :] if i < 3 else lorB[0:32, :]
                load_shift(dstt, tlor, 768 + 32 * i, 32, tmp, post=(AF.Tanh if i < 2 else AF.Copy), pb=(32 * i if i < 3 else 0))
            load_shift(lorB[64:128, :], tlor, 896, 64, tmp, post=AF.Sigmoid, pb=64)
            fw.barrier()

        for c in range(2):
            with ExitStack() as esc:
                Pc = Pool_(fw, esc)
                rr = Pc.sb([128, TE], F32)
                kx = Pc.sb([128, TE], F32)
                vv = Pc.sb([128, TE], F32)
                kk = Pc.sb([128, TE], F32)
                trr, tkx, tvv, tkk = Tok(), Tok(), Tok(), Tok()
                vtok = Pc.sb([64, TE // CHK, 128], BF16)
                tvt = Tok()
                yacc = Pc.sb([128, T], F32)
                bon = Pc.sb([128, T], F32)
                tya, tbon = Tok(), Tok()
                fw.op("pool", lambda e: e.memset(yacc[:], 0.0), writes=[tya])
                fw.op("pool", lambda e: e.memset(bon[:], 0.0), writes=[tbon])
                with ExitStack() as es2:
                    tmp = shift_tmp(Pool_(fw, es2))
                    for (dst, tdst, r0) in ((rr, trr, 0), (kx, tkx, 256), (vv, tvv, 512)):
                        load_shift(dst[:], tdst, r0 + c * 128, 128, tmp)
                    fw.barrier()
                with ExitStack() as es2:
                    P2 = Pool_(fw, es2)
                    sq = P2.sb([128, 512], F32)
                    rn = P2.sb([128, 512], F32)
                    tsq, trn = Tok(), Tok()
                    ps1 = P2.ps()
                    tps1 = Tok()
                    fw.op("dve", lambda e: e.tensor_scalar(out=kk[:], in0=kx[:], scalar1=pp[:, 4, c:c + 1], scalar2=None, op0=ALU.mult),
                          reads=[tkx, tpp], writes=[tkk])
                    for (c0, c1) in CH5:
                        fw.op("act", lambda e, c0=c0, c1=c1: e.activation(out=sq[:], in_=kk[:, c0:c1], func=AF.Square), reads=[tkk], writes=[tsq])
                        fw.op("pe", lambda e: e.matmul(ps1[:], lhsT=bd1[:], rhs=sq[:], start=True, stop=True), reads=[tsq, tc], writes=[tps1])
                        rstd_from_ps(fw, rn, trn, ps1, tps1, 512, 1.0, epsl[:, 1:2], tpp)
                        fw.op("dve", lambda e, c0=c0, c1=c1: e.tensor_tensor(out=kk[:, c0:c1], in0=kk[:, c0:c1], in1=rn[:], op=ALU.mult),
                              reads=[tkk, trn], writes=[tkk])
                    vb = P2.sb([128, TE], BF16)
                    tvb = Tok()
                    fw.op("act", lambda e: e.copy(out=vb[:], in_=vv[:]), reads=[tvv], writes=[tvb])
                    pst = P2.ps([128, 1024], BF16)
                    tpst = Tok()
                    for q in range(TE // CHK // 4):
                        for j in range(4):
                            ch = q * 4 + j
                            fw.op("pe", lambda e, j=j, ch=ch: e.transpose(pst[0:64, j * 128:(j + 1) * 128], vb[:, ch * CHK:(ch + 1) * CHK], idb[:]),
                                  reads=[tvb, tc], writes=[tpst])
                        fw.op("dve", lambda e, q=q: e.tensor_copy(out=vtok[:, q * 4:(q + 1) * 4, :], in_=pst[0:64, 0:512].rearrange("p (j c) -> p j c", j=4)),
                              reads=[tpst], writes=[tvt])
                    fw.barrier()

                for di in range(2):
                    off = 0 if di == 0 else 256
                    with ExitStack() as esd:
                        Pd = Pool_(fw, esd)
                        aT = Pd.sb([128, N], BF16)
                        bT = Pd.sb([128, N], BF16)
                        kT = Pd.sb([128, N], BF16)
                        rT = Pd.sb([128, N], BF16)
                        btok = Pd.sb([64, NCH, 128], BF16)
                        ktok = Pd.sb([64, NCH, 128], BF16)
                        pC = Pd.sb([128, NCH], F32)
                        tops = Tok()
                        ttok = Tok()
                        with ExitStack() as es2:
                            P2 = Pool_(fw, es2)
                            ld = P2.sb([128, TE], F32)
                            kd = P2.sb([128, TE], F32)
                            bb = P2.sb([128, TE], F32)
                            tld, tkd, tbb = Tok(), Tok(), Tok()
                            psr = Rot([P2.ps() for _ in range(3)])
                            tm5 = Rot([P2.sb([128, 512], F32) for _ in range(2)])
                            for (c0, c1) in CH5:
                                ps, tps = psr.next()
                                fw.op("pe", lambda e, ps=ps, c0=c0, c1=c1: e.matmul(ps[:], lhsT=wA[32 * di:32 * di + 32, c * 128:(c + 1) * 128], rhs=lorA[32 * di:32 * di + 32, c0:c1],
                                                                                  start=True, stop=True), reads=[tlw, tlor], writes=[tps])
                                fw.op("act", lambda e, ps=ps, c0=c0, c1=c1: e.activation(out=ld[:, c0:c1], in_=ps[:], func=AF.Sigmoid, bias=pp[:, di, c:c + 1]),
                                      reads=[tps, tpp], writes=[tld])
                                ps, tps = psr.next()
                                fw.op("pe", lambda e, ps=ps, c0=c0, c1=c1: e.matmul(ps[:], lhsT=(wA[64:96, c * 128:(c + 1) * 128] if di == 0 else wB[0:32, c * 128:(c + 1) * 128]),
                                                                                  rhs=(lorA[64:96, c0:c1] if di == 0 else lorB[0:32, c0:c1]),
                                                                                  start=True, stop=True), reads=[tlw, tlor], writes=[tps])
                                fw.op("act", lambda e, ps=ps, c0=c0, c1=c1: e.activation(out=bb[:, c0:c1], in_=ps[:], func=AF.Sigmoid, bias=pp[:, 2 + di, c:c + 1]),
                                      reads=[tps, tpp], writes=[tbb])
                            fw.op("pool", lambda e: e.tensor_scalar(out=ld[:], in0=ld[:], scalar1=-math.exp(-0.5), scalar2=None, op0=ALU.mult), reads=[tld], writes=[tld])
                            fw.op("act", lambda e: e.activation(out=kd[:], in_=bb[:], func=AF.Identity, scale=pp[:, 5, c:c + 1], bias=pp[:, 6, c:c + 1]),
                                  reads=[tbb, tpp], writes=[tkd])
                            fw.op("dve", lambda e: e.tensor_tensor(out=kd[:], in0=kd[:], in1=kx[:], op=ALU.mult), reads=[tkd, tkx], writes=[tkd])
                            fw.op("pool", lambda e: e.tensor_tensor(out=bb[:], in0=bb[:], in1=kk[:], op=ALU.mult), reads=[tbb, tkk], writes=[tbb])
                            for (c0, c1) in CH5:
                                c1 = min(c1, T)
                                n = c1 - c0
                                tm, ttm = tm5.next()
                                fw.op("dve", lambda e, tm=tm, c0=c0, c1=c1, n=n: e.scalar_tensor_tensor(out=tm[:, :n], in0=rr[:, c0:c1], scalar=pp[:, 7, c:c + 1], in1=kd[:, c0:c1],
                                                                                                 op0=ALU.mult, op1=ALU.mult), reads=[trr, tkd, tpp], writes=[ttm])
                                ps, tps = psr.next()
                                fw.op("pe", lambda e, ps=ps, tm=tm, n=n: e.matmul(ps[:, :n], lhsT=bd1[:], rhs=tm[:, :n], start=True, stop=True), reads=[ttm, tc], writes=[tps])
                                tm2, ttm2 = tm5.next()
                                fw.op("dve", lambda e, tm2=tm2, ps=ps, c0=c0, c1=c1, n=n: e.tensor_tensor(out=tm2[:, :n], in0=ps[:, :n], in1=vv[:, c0:c1], op=ALU.mult),
                                      reads=[tps, tvv], writes=[ttm2])
                                fw.op("pool", lambda e, tm2=tm2, c0=c0, c1=c1, n=n: e.tensor_tensor(out=bon[:, c0:c1], in0=bon[:, c0:c1], in1=tm2[:, :n], op=ALU.add),
                                      reads=[ttm2, tbon], writes=[tbon])
                            cs = P2.sb([128, N], F32)
                            ex = P2.sb([128, N], F32)
                            tcs, tex = Tok(), Tok()
                            ldw = ld[:, off:off + N]
                            fw.op("dve", lambda e: e.tensor_tensor_scan(out=cs[:], data0=smask[:], data1=ldw, initial=0.0, op0=ALU.mult, op1=ALU.add),
                                  reads=[tsm, tld], writes=[tcs])
                            csv = cs[:].rearrange("p (c j) -> p c j", j=CHK)
                            tot = P2.sb([128, NCH, 1], F32)
                            ttot = Tok()
                            fw.op("dve", lambda e: e.tensor_copy(out=tot[:], in_=csv[:, :, CHK - 1:CHK]), reads=[tcs], writes=[ttot])
                            totb = tot[:].to_broadcast([128, NCH, CHK])
                            if di == 1:
                                fw.op("dve", lambda e: e.tensor_tensor(out=csv, in0=totb, in1=csv, op=ALU.subtract), reads=[ttot, tcs], writes=[tcs])
                                fw.op("dve", lambda e: e.tensor_tensor(out=cs[:], in0=cs[:], in1=ldw, op=ALU.add), reads=[tcs, tld], writes=[tcs])
                            fw.op("act", lambda e: e.activation(out=pC[:], in_=tot[:, :, 0], func=AF.Exp), reads=[ttot], writes=[tops])
                            kdw, bbw = kd[:, off:off + N], bb[:, off:off + N]
                            rrw, kkw = rr[:, off:off + N], kk[:, off:off + N]
                            fw.op("act", lambda e: e.activation(out=ex[:], in_=cs[:], func=AF.Exp), reads=[tcs], writes=[tex])
                            fw.op("dve", lambda e: e.tensor_tensor(out=rT[:], in0=rrw, in1=ex[:], op=ALU.mult), reads=[trr, tex], writes=[tops])
                            fw.op("act", lambda e: e.activation(out=ex[:], in_=cs[:], func=AF.Exp, scale=-1.0), reads=[tcs, tops], writes=[tex])
                            fw.op("dve", lambda e: e.tensor_tensor(out=bT[:], in0=bbw, in1=ex[:], op=ALU.mult), reads=[tbb, tex], writes=[tops])
                            fw.op("pool", lambda e: e.tensor_tensor(out=kT[:], in0=kdw, in1=ex[:], op=ALU.mult), reads=[tkd, tex], writes=[tops])
                            e3 = ex
                            te3 = tex
                            fw.op("dve", lambda e: e.tensor_tensor(out=e3[:], in0=cs[:], in1=ldw, op=ALU.subtract), reads=[tcs, tld], writes=[te3])
                            fw.op("act", lambda e: e.activation(out=e3[:], in_=e3[:], func=AF.Exp), reads=[te3], writes=[te3])
                            fw.op("dve", lambda e: e.scalar_tensor_tensor(out=aT[:], in0=kkw, scalar=-1.0, in1=e3[:], op0=ALU.mult, op1=ALU.mult),
                                  reads=[tkk, te3], writes=[tops])
                            e3v = e3[:].rearrange("p (c j) -> p c j", j=CHK)
                            fw.op("dve", lambda e: e.tensor_tensor(out=e3v, in0=totb, in1=csv, op=ALU.subtract), reads=[ttot, tcs, tops, te3], writes=[te3])
                            fw.op("act", lambda e: e.activation(out=e3[:], in_=e3[:], func=AF.Exp), reads=[te3], writes=[te3])
                            bh = P2.sb([128, N], BF16)
                            kh = P2.sb([128, N], BF16)
                            tbh = Tok()
                            fw.op("dve", lambda e: e.tensor_tensor(out=bh[:], in0=bbw, in1=e3[:], op=ALU.mult), reads=[tbb, te3], writes=[tbh])
                            fw.op("pool", lambda e: e.tensor_tensor(out=kh[:], in0=kdw, in1=e3[:], op=ALU.mult), reads=[tkd, te3], writes=[tbh])
                            pst = P2.ps([128, 1024], BF16)
                            tpst = Tok()
                            for (src, dstt) in ((bh, btok), (kh, ktok)):
                                for q in range(NCH // 4):
                                    for j in range(4):
                                        ch = q * 4 + j
                                        fw.op("pe", lambda e, j=j, ch=ch, src=src: e.transpose(pst[0:64, j * 128:(j + 1) * 128], src[:, ch * CHK:(ch + 1) * CHK], idb[:]),
                                              reads=[tbh, tc], writes=[tpst])
                                    fw.op("act", lambda e, q=q, dstt=dstt: e.copy(out=dstt[:, q * 4:(q + 1) * 4, :], in_=pst[0:64, 0:512].rearrange("p (j c) -> p j c", j=4)),
                                          reads=[tpst], writes=[ttok])
                            fw.barrier()
                        with ExitStack() as es3:
                            P3 = Pool_(fw, es3)
                            Hs = P3.sb([128, 64], F32)
                            Hb = P3.sb([128, 64], BF16)
                            tH = Tok()
                            fw.op("pool", lambda e: e.memset(Hs[:], 0.0), writes=[tH])
                            fw.op("pool", lambda e: e.memset(Hb[:], 0.0), writes=[tH])
                            psA = Rot([P3.ps([64, 512]) for _ in range(1)])
                            psB = Rot([P3.ps([64, 512]) for _ in range(1)])
                            psI = Rot([P3.ps([64, 512]) for _ in range(1)])
                            psT = Rot([P3.ps([64, 512]) for _ in range(1)])
                            psG = Rot([P3.ps([64, 512]) for _ in range(1)])
                            psH = Rot([P3.ps([128, 512]) for _ in range(1)])
                            psY = Rot([P3.ps([128, 512]) for _ in range(1)])
                            g1b = Rot([P3.sb([64, 2, 64], BF16) for _ in range(2)])
                            nl = Rot([P3.sb([64, 2, 2, 64], F32) for _ in range(2)])
                            nl2 = Rot([P3.sb([64, 2, 2, 64], F32) for _ in range(2)])
                            g2b_ = Rot([P3.sb([64, 4, 64], BF16) for _ in range(2)])
                            Pm = Rot([P3.sb([64, 2, 64], F32) for _ in range(2)])
                            Gs = Rot([P3.sb([64, 128], F32) for _ in range(2)])
                            Ub = Rot([P3.sb([64, 128], BF16) for _ in range(2)])
                            Ys = Rot([P3.sb([64, 128], F32) for _ in range(2)])
                            ms, ml, mi = (0, 1, 2) if di == 0 else (1, 0, 3)
                            order = list(range(NCH)) if di == 0 else list(range(NCH - 1, -1, -1))
                            res = {}

                            def par_gen(i):
                                cc0, cc1 = i * CHK, (i + 1) * CHK
                                pa, tpa = psA.next()
                                pb, tpb = psB.next()
                                for h in range(2):
                                    hp = slice(h * 64, (h + 1) * 64)
                                    for (dst, lt, rt_) in ((pa[:, h * 64:(h + 1) * 64], kT, aT), (pa[:, 128 + h * 64:128 + (h + 1) * 64], bT, aT),
                                                           (pa[:, 256 + h * 64:256 + (h + 1) * 64], aT, bT)):
                                        fw.op("pe", lambda e, dst=dst, lt=lt, rt_=rt_, hp=hp: e.matmul(dst, lhsT=lt[hp, cc0:cc1], rhs=rt_[hp, cc0:cc1], start=True, stop=True),
                                              reads=[tops], writes=[tpa])
                                        yield
                                    for (dst, lt, rt_) in ((pb[:, h * 64:(h + 1) * 64], bT, rT), (pb[:, 128 + h * 64:128 + (h + 1) * 64], kT, rT)):
                                        fw.op("pe", lambda e, dst=dst, lt=lt, rt_=rt_, hp=hp: e.matmul(dst, lhsT=lt[hp, cc0:cc1], rhs=rt_[hp, cc0:cc1], start=True, stop=True),
                                              reads=[tops], writes=[tpb])
                                        yield
                                a1, ta1 = g1b.next()
                                nlt, tnl = nl.next()
                                a45, ta45 = g2b_.next()
                                fw.op("dve", lambda e: e.tensor_tensor(out=a1[:], in0=pa[:, 0:128].rearrange("p (h t) -> p h t", h=2), in1=mrep[:, ms, 0:2, :], op=ALU.mult),
                                      reads=[tpa, tc], writes=[ta1])
                                yield
                                fw.op("dve", lambda e: e.tensor_tensor(out=nlt[:, 0], in0=pa[:, 128:256].rearrange("p (h t) -> p h t", h=2), in1=mrep[:, ms, 0:2, :], op=ALU.mult),
                                      reads=[tpa, tc], writes=[tnl])
                                yield
                                fw.op("dve", lambda e: e.tensor_tensor(out=nlt[:, 1], in0=pa[:, 256:384].rearrange("p (h t) -> p h t", h=2), in1=mrep[:, ml, 0:2, :], op=ALU.mult),
                                      reads=[tpa, tc], writes=[tnl])
                                yield
                                fw.op("dve", lambda e: e.tensor_tensor(out=a45[:], in0=pb[:, 0:256].rearrange("p (h t) -> p h t", h=4), in1=mrep[:, mi, :, :], op=ALU.mult),
                                      reads=[tpb, tc], writes=[ta45])
                                yield
                                pm, tpm = Pm.next()
                                fw.op("dve", lambda e: e.tensor_tensor(out=pm[:], in0=nlt[:, 0], in1=idt[0:64, 0:64].unsqueeze(1).to_broadcast([64, 2, 64]), op=ALU.add),
                                      reads=[tnl, tc], writes=[tpm])
                                yield
                                cur, tcur = nlt, tnl
                                for lev in range(5):
                                    pi_, tpi = psI.next()
                                    for h in range(2):
                                        fw.op("pe", lambda e, pi_=pi_, h=h, cur=cur: e.matmul(pi_[:, h * 64:(h + 1) * 64], lhsT=cur[:, 0, h, :], rhs=cur[:, 1, h, :], start=True, stop=True),
                                              reads=[tcur], writes=[tpi])
                                        yield
                                    nxt, tnxt = (nl2.next() if lev % 2 == 0 else nl.next())
                                    fw.op("act", lambda e, nxt=nxt, pi_=pi_: e.copy(out=nxt[:, 1], in_=pi_[:, 0:128].rearrange("p (h t) -> p h t", h=2)), reads=[tpi], writes=[tnxt])
                                    yield
                                    for h in range(2):
                                        fw.op("pe", lambda e, pi_=pi_, h=h, nxt=nxt: e.matmul(pi_[:, 256 + h * 64:256 + (h + 1) * 64], lhsT=nxt[:, 1, h, :], rhs=pm[:, h, :], start=True, stop=True),
                                              reads=[tnxt, tpm], writes=[tpi])
                                        yield
                                    if lev < 4:
                                        pt_, tpt = psT.next()
                                        for h in range(2):
                                            fw.op("pe", lambda e, pt_=pt_, h=h, nxt=nxt: e.transpose(pt_[:, h * 64:(h + 1) * 64], nxt[:, 1, h, :], idt[0:64, 0:64]),
                                                  reads=[tnxt, tc], writes=[tpt])
                                            yield
                                    fw.op("dve", lambda e, pi_=pi_: e.tensor_tensor(out=pm[:], in0=pm[:], in1=pi_[:, 256:384].rearrange("p (h t) -> p h t", h=2), op=ALU.add),
                                          reads=[tpi, tpm], writes=[tpm])
                                    yield
                                    if lev < 4:
                                        fw.op("act", lambda e, nxt=nxt, pt_=pt_: e.copy(out=nxt[:, 0], in_=pt_[:, 0:128].rearrange("p (h t) -> p h t", h=2)), reads=[tpt], writes=[tnxt])
                                        yield
                                    cur, tcur = nxt, tnxt
                                res[i] = (a1, ta1, a45, ta45, pm, tpm)

                            def chain_gen(i):
                                cc0, cc1 = i * CHK, (i + 1) * CHK
                                gch = i + off // CHK
                                a1, ta1, a45, ta45, pm, tpm = res.pop(i)
                                pg, tpg = psG.next()
                                for h in range(2):
                                    hp = slice(h * 64, (h + 1) * 64)
                                    fw.op("pe", lambda e, h=h, hp=hp: e.matmul(pg[:, h * 64:(h + 1) * 64], lhsT=aT[hp, cc0:cc1], rhs=Hb[hp, :], start=True, stop=False),
                                          reads=[tops, tH], writes=[tpg])
                                    yield
                                    fw.op("pe", lambda e, h=h, hp=hp: e.matmul(pg[:, h * 64:(h + 1) * 64], lhsT=a1[:, h, :], rhs=vtok[:, gch, hp], start=False, stop=True),
                                          reads=[ta1, tvt], writes=[tpg])
                                    yield
                                gs, tgs = Gs.next()
                                fw.op("act", lambda e: e.copy(out=gs[:], in_=pg[:, 0:128]), reads=[tpg], writes=[tgs])
                                yield
                                for h in range(2):
                                    fw.op("pe", lambda e, h=h: e.matmul(pg[:, 128 + h * 64:128 + (h + 1) * 64], lhsT=pm[:, h, :], rhs=gs[:, h * 64:(h + 1) * 64], start=True, stop=True),
                                          reads=[tpm, tgs], writes=[tpg])
                                    yield
                                ub, tub = Ub.next()
                                fw.op("dve", lambda e: e.tensor_copy(out=ub[:], in_=pg[:, 128:256]), reads=[tpg], writes=[tub])
                                yield
                                ph, tph = psH.next()
                                py, tpy = psY.next()
                                for h in range(2):
                                    hp = slice(h * 64, (h + 1) * 64)
                                    fw.op("pe", lambda e, h=h, hp=hp: e.matmul(ph[hp, 0:64], lhsT=btok[:, i, hp], rhs=ub[:, hp], start=True, stop=False),
                                          reads=[ttok, tub], writes=[tph])
                                    yield
                                    fw.op("pe", lambda e, h=h, hp=hp: e.matmul(ph[hp, 0:64], lhsT=ktok[:, i, hp], rhs=vtok[:, gch, hp], start=False, stop=True),
                                          reads=[ttok, tvt], writes=[tph])
                                    yield
                                for h in range(2):
                                    hp = slice(h * 64, (h + 1) * 64)
                                    fw.op("pe", lambda e, h=h, hp=hp: e.matmul(py[0:64, hp], lhsT=rT[hp, cc0:cc1], rhs=Hb[hp, :], start=True, stop=False),
                                          reads=[tops, tH], writes=[tpy])
                                    yield
                                    fw.op("pe", lambda e, h=h, hp=hp: e.matmul(py[0:64, hp], lhsT=a45[:, h, :], rhs=ub[:, hp], start=False, stop=False),
                                          reads=[ta45, tub], writes=[tpy])
                                    yield
                                    fw.op("pe", lambda e, h=h, hp=hp: e.matmul(py[0:64, hp], lhsT=a45[:, 2 + h, :], rhs=vtok[:, gch, hp], start=False, stop=True),
                                          reads=[ta45, tvt], writes=[tpy])
                                    yield
                                fw.op("dve", lambda e: e.scalar_tensor_tensor(out=Hs[:], in0=Hs[:], scalar=pC[:, i:i + 1], in1=ph[:, 0:64], op0=ALU.mult, op1=ALU.add),
                                      reads=[tph, tH, tops, tpy], writes=[tH])
                                yield
                                fw.op("act", lambda e: e.copy(out=Hb[:], in_=Hs[:]), reads=[tH, tpy, tpg], writes=[tH])
                                yield
                                ys, tys = Ys.next()
                                fw.op("act", lambda e: e.copy(out=ys[:], in_=py[0:64, 0:128]), reads=[tpy], writes=[tys])
                                yield
                                fw.op("pe", lambda e: e.transpose(py[:, 256:320], ys[:], idt[0:64, 0:64]), reads=[tys, tc], writes=[tpy])
                                yield
                                y0 = off + cc0
                                if y0 >= T:
                                    y0 -= T
                                fw.op("dve", lambda e: e.tensor_tensor(out=yacc[:, y0:y0 + CHK], in0=yacc[:, y0:y0 + CHK], in1=py[:, 256:320], op=ALU.add),
                                      reads=[tpy, tya], writes=[tya])
                                yield

                            for _ in par_gen(order[0]):
                                pass
                            for idx, i in enumerate(order):
                                gp = par_gen(order[idx + 1]) if idx + 1 < len(order) else iter(())
                                gc = chain_gen(i)
                                done_p = done_c = False
                                while not (done_p and done_c):
                                    for _ in range(2):
                                        if not done_p:
                                            try:
                                                next(gp)
                                            except StopIteration:
                                                done_p = True
                                    if not done_c:
                                        try:
                                            next(gc)
                                        except StopIteration:
                                            done_c = True
                            fw.barrier()
                with ExitStack() as es4:
                    P4 = Pool_(fw, es4)
                    psr = Rot([P4.ps() for _ in range(3)])
                    xc = P4.sb([128, 512], F32)
                    sq = P4.sb([128, 512], F32)
                    rs = P4.sb([128, 512], F32)
                    txc, tsq, trs = Tok(), Tok(), Tok()
                    stg = Rot([P4.sb([128, 512], F32) for _ in range(2)])
                    tout = Tok()
                    for (lo, hi, ic) in SEGS:
                        n = hi - lo
                        ps, tps = psr.next()
                        fw.op("pe", lambda e, ps=ps, lo=lo, hi=hi, n=n: e.matmul(ps[:, :n], lhsT=bd1[:], rhs=yacc[:, lo:hi], start=True, stop=True), reads=[tya, tc], writes=[tps])
                        fw.op("dve", lambda e, ps=ps, lo=lo, hi=hi, n=n: e.scalar_tensor_tensor(out=xc[:, :n], in0=ps[:, :n], scalar=-1.0 / 64, in1=yacc[:, lo:hi], op0=ALU.mult, op1=ALU.add),
                              reads=[tps, tya], writes=[txc])
                        fw.op("act", lambda e, n=n: e.activation(out=sq[:, :n], in_=xc[:, :n], func=AF.Square), reads=[txc], writes=[tsq])
                        ps2, tps2 = psr.next()
                        fw.op("pe", lambda e, ps2=ps2, n=n: e.matmul(ps2[:, :n], lhsT=bd1[:], rhs=sq[:, :n], start=True, stop=True), reads=[tsq, tc], writes=[tps2])
                        rstd_from_ps(fw, rs, trs, ps2, tps2, n, 1.0 / 64, epsl[:, 0:1], tpp)
                        fw.op("dve", lambda e, n=n: e.tensor_tensor(out=xc[:, :n], in0=xc[:, :n], in1=rs[:, :n], op=ALU.mult), reads=[txc, trs], writes=[txc])
                        fw.op("act", lambda e, n=n: e.activation(out=xc[:, :n], in_=xc[:, :n], func=AF.Identity, scale=pp[:, 8, c:c + 1], bias=pp[:, 9, c:c + 1]),
                              reads=[txc, tpp], writes=[txc])
                        fw.op("pool", lambda e, lo=lo, hi=hi, n=n: e.tensor_tensor(out=xc[:, :n], in0=xc[:, :n], in1=bon[:, lo:hi], op=ALU.add), reads=[txc, tbon], writes=[txc])
                        ps3, tps3 = psr.next()
                        fw.op("pe", lambda e, ps3=ps3, lo=lo, hi=hi, n=n: e.matmul(ps3[:, :n], lhsT=wB[64:128, c * 128:(c + 1) * 128], rhs=lorB[64:128, lo:hi], start=True, stop=True),
                              reads=[tlw, tlor], writes=[tps3])
                        sg, tsg = stg.next()
                        fw.op("dve", lambda e, sg=sg, ps3=ps3, n=n: e.tensor_tensor(out=sg[:, :n], in0=ps3[:, :n], in1=xc[:, :n], op=ALU.mult), reads=[tps3, txc], writes=[tsg])
                        fw.dma("sp", RWT[c * 128:(c + 1) * 128, lo:hi], sg[:, :n], reads=[tsg], writes=[Tok()])
                    fw.barrier()
        fw.barrier()
NCORES = 8
DEPTH = 4
S5_KEYS = ["s5_a_re", "s5_a_im", "s5_log_step", "s5_b_re", "s5_b_im", "s5_c_re", "s5_c_im", "s5_d", "s5_glu_w", "s5_glu_b", "s5_out_g"]
RW_KEYS = ["rw_mu", "rw_w0", "rw_w2", "rw_a0", "rw_a2", "rw_g2", "rw_k_k", "rw_k_a", "rw_r_k", "rw_ln_g", "rw_ln_b"]
LAYER_KEYS = (["norm1_g", "norm2_g", "mod_w", "mod_b", "w_in", "w_out", "att_qn_g", "att_kn_g", "att_out_g",
               "ffn_up", "ffn_conv_w", "ffn_conv_b", "ffn_down"] + S5_KEYS + RW_KEYS)
SHARED_KEYS = ["c_ctx", "final_g", "ident", "POS", "CST", "RWMASK", "RWBD"]
PERCORE_KEYS = ["x_b", "ctx_b", "c_b"]
SCRATCH = {"ZT": [2048, TE], "VT": [T, 128], "MOD": [128, 48, 2], "S5T": [256, T], "ATT_T": [512, T], "RWT": [256, T],
           "XT1": [DM, T], "XA": [DM, T], "XB": [DM, T]}


def build_fused(depth=DEPTH):
    nc = bass.Bass("TRN2", target_bir_lowering=False)
    decl = {}

    def ext(name, shape, dt, kind):
        if name not in decl:
            decl[name] = nc.dram_tensor(name, list(shape), dt, kind=kind).ap()
        return decl[name]

    def make_io(l):
        xin = "XA" if l % 2 == 0 else "XB"
        xout = "XB" if l % 2 == 0 else "XA"

        def io(name, shape, dt, role):
            if name in LAYER_KEYS:
                full = ext(name, [DEPTH] + list(shape), dt, "ExternalInput")
                return full[l]
            if name in SHARED_KEYS or name in PERCORE_KEYS:
                return ext(name, shape, dt, "ExternalInput")
            if name == "OUT":
                return ext(name, shape, dt, "ExternalOutput")
            if name == "XT":
                name = xin
            elif name == "XT2":
                name = xout
            return ext(name, SCRATCH[name], dt, "Internal")
        return io
    with ExitStack() as es:
        fw = FW(nc, es)
        stage_p0(fw, make_io(0))
        for l in range(depth):
            io = make_io(l)
            for st in (stage_p1, stage_p23, stage_p4, stage_p5a, stage_p5b):
                st(fw, io)
        stage_p6(fw, make_io(depth))
        fw.barrier()
    return nc, fw


def tile_up(up):
    lead = up.shape[:-2]
    v = up.reshape(lead + (8, 128, 44, 128))
    nd = len(lead)
    v = np.transpose(v, tuple(range(nd)) + (nd + 2, nd + 1, nd + 0, nd + 3))
    return np.ascontiguousarray(v).reshape(lead + (44, 128, 1024))


_FUSED = {}


def kernel(**inp):
    inp = {k: np.ascontiguousarray(np.asarray(v)) for k, v in inp.items()}
    if "nc" not in _FUSED:
        _FUSED["nc"], _FUSED["fw"] = build_fused()
    nc = _FUSED["nc"]
    pos, cst = host_consts()
    rwm, rwbd = rw_consts()
    shared = {k: inp[k] for k in LAYER_KEYS if k != "rw_r_k"}
    shared["rw_r_k"] = inp["rw_r_k"].reshape(DEPTH, 256)
    shared["ffn_up"] = tile_up(inp["ffn_up"])
    shared.update(c_ctx=inp["c_ctx"], final_g=inp["final_g"], ident=np.eye(128, dtype=np.float32),
                  POS=pos, CST=cst, RWMASK=rwm, RWBD=rwbd)
    in_maps = [dict(shared, x_b=inp["x"][b], ctx_b=inp["ctx"][b], c_b=inp["c"][b]) for b in range(NCORES)]
    res = run_bass_kernel_spmd(nc, in_maps, core_ids=list(range(NCORES)))
    return np.stack([res.results[b]["OUT"] for b in range(NCORES)], 0).astype(np.float32)
```

```python
import math
import numpy as np
from contextlib import ExitStack
import concourse.bass as bass
import concourse.mybir as mybir
from concourse.bass_utils import run_bass_kernel_spmd

F32 = mybir.dt.float32
F32R = mybir.dt.float32r
BF16 = mybir.dt.bfloat16
I32 = mybir.dt.int32
ALU = mybir.AluOpType
AF = mybir.ActivationFunctionType
AX = mybir.AxisListType

T = 2304
TE = 2560
NCTX = 256
NLAT = 2048
DM = 1024
SEGS = [(0, 256, 1), (256, 768, 0), (768, 1280, 0), (1280, 1792, 0), (1792, 2304, 0)]
RMS_EPS = 1e-6


class Tok:
    __slots__ = ("w", "r")

    def __init__(self):
        self.w = None
        self.r = {}


class FW:
    ENG = ("pe", "dve", "act", "pool", "sp")
    NDMA = 8

    def __init__(self, nc, es):
        self.nc = nc
        self.es = es
        self.eng = {"pe": nc.tensor, "dve": nc.vector, "act": nc.scalar,
                    "pool": nc.gpsimd, "sp": nc.sync}
        self.sem = {}
        self.cnt = {}
        for e in self.ENG:
            self.sem[e] = es.enter_context(nc.semaphore("s_" + e))
            self.cnt[e] = 0
        self.dq = {}
        for q in ("sp", "pool", "act"):
            ring = []
            for i in range(self.NDMA):
                k = "d_%s_%d" % (q, i)
                self.sem[k] = es.enter_context(nc.semaphore(k))
                self.cnt[k] = 0
                ring.append(k)
            self.dq[q] = [ring, 0]
        self.seen = {e: {} for e in self.ENG}
        self.attach = True
        self.ninst = 0
        self.uid = 0

    def name(self, p):
        self.uid += 1
        return "%s_%d" % (p, self.uid)

    def _deps(self, reads, writes):
        deps = {}
        for t in reads:
            if t.w is not None and deps.get(t.w[0], 0) < t.w[1]:
                deps[t.w[0]] = t.w[1]
        for t in writes:
            if t.w is not None and deps.get(t.w[0], 0) < t.w[1]:
                deps[t.w[0]] = t.w[1]
            for k, v in t.r.items():
                if deps.get(k, 0) < v:
                    deps[k] = v
        return deps

    def _wait(self, e, deps):
        seen = self.seen[e]
        for k, v in deps.items():
            if seen.get(k, 0) < v:
                self.eng[e].wait_ge(self.sem[k], v)
                seen[k] = v

    def op(self, e, fn, reads=(), writes=()):
        deps = self._deps(reads, writes)
        seen = self.seen[e]
        need = [(k, v) for k, v in deps.items() if seen.get(k, 0) < v]
        att = None
        if need and self.attach:
            att = need.pop()
        for k, v in need:
            self.eng[e].wait_ge(self.sem[k], v)
            seen[k] = v
        inst = fn(self.eng[e])
        if att is not None:
            inst._wait_ge(self.sem[att[0]], att[1])
            seen[att[0]] = att[1]
        self.cnt[e] += 1
        inst.then_inc(self.sem[e], 1)
        v = self.cnt[e]
        for t in reads:
            t.r[e] = v
        for t in writes:
            t.w = (e, v)
            t.r = {}
        self.ninst += 1
        return inst

    def dma(self, q, out, in_, reads=(), writes=(), **kw):
        ring, idx = self.dq[q]
        k = ring[idx % len(ring)]
        self.dq[q][1] = idx + 1
        deps = self._deps(reads, writes)
        if self.cnt[k] > 0:
            deps[k] = max(deps.get(k, 0), self.cnt[k])
        self._wait(q, deps)
        inst = self.eng[q].dma_start(out=out, in_=in_, **kw)
        self.cnt[k] += 16
        inst.then_inc(self.sem[k], 16)
        v = self.cnt[k]
        for t in reads:
            t.r[k] = v
        for t in writes:
            t.w = (k, v)
            t.r = {}
        self.ninst += 1
        return inst

    def barrier(self, engines=None):
        allv = {k: v for k, v in self.cnt.items() if v > 0}
        for e in (engines or self.ENG):
            self._wait(e, allv)


class Pool_:
    def __init__(self, fw, es):
        self.fw = fw
        self.es = es
        self.nc = fw.nc

    def sb(self, shape, dt, name="t"):
        return self.es.enter_context(self.nc.sbuf_tensor(self.fw.name(name), list(shape), dt))

    def ps(self, shape=(128, 512), dt=F32, name="ps"):
        return self.es.enter_context(self.nc.psum_tensor(self.fw.name(name), list(shape), dt))


class Rot:
    def __init__(self, bufs):
        self.bufs = bufs
        self.toks = [Tok() for _ in bufs]
        self.i = 0

    def next(self):
        j = self.i % len(self.bufs)
        self.i += 1
        return self.bufs[j], self.toks[j]


def load_w_bf16(fw, P, W, rows, cols, tok, name="w", q="pool", chunk=2048, stage=None, cast_engs=("pool",)):
    kt = rows // 128
    wb = P.sb([128, kt, cols], BF16, name)
    chunk = min(chunk, cols)
    st = stage or Rot([P.sb([128, chunk], F32, "wstg") for _ in range(3)])
    ci = 0
    for k in range(kt):
        for c0 in range(0, cols, chunk):
            n = min(chunk, cols - c0)
            sg, tsg = st.next()
            fw.dma("sp", sg[:, :n], W[k * 128:(k + 1) * 128, c0:c0 + n], writes=[tsg])
            ce = cast_engs[ci % len(cast_engs)]
            ci += 1
            if ce == "act":
                fw.op("act", lambda e, sg=sg, k=k, c0=c0, n=n: e.copy(out=wb[:, k, c0:c0 + n], in_=sg[:, :n]), reads=[tsg], writes=[tok])
            else:
                fw.op(ce, lambda e, sg=sg, k=k, c0=c0, n=n: e.tensor_copy(out=wb[:, k, c0:c0 + n], in_=sg[:, :n]), reads=[tsg], writes=[tok])
    return wb


def stage_p0(fw, io):
    nc = fw.nc
    xb = io("x_b", [NLAT, DM], F32, "in")
    cb = io("ctx_b", [NCTX, DM], F32, "in")
    ident = io("ident", [128, 128], F32, "in")
    XT = io("XT", [DM, T], F32, "out")
    with ExitStack() as es:
        P = Pool_(fw, es)
        idt = P.sb([128, 128], F32)
        tid = Tok()
        fw.dma("sp", idt[:], ident, writes=[tid])
        xt = P.sb([128, 8, T], F32)
        txt = Tok()
        xin = Rot([P.sb([128, DM], F32) for _ in range(3)])
        pss = Rot([P.ps() for _ in range(4)])
        for tt in range(18):
            src = cb[tt * 128:(tt + 1) * 128, :] if tt < 2 else xb[(tt - 2) * 128:(tt - 1) * 128, :]
            xi, txi = xin.next()
            fw.dma("sp", xi[:], src, writes=[txi])
            for half in range(2):
                ps, tps = pss.next()
                for k in range(4):
                    kk = half * 4 + k
                    fw.op("pe", lambda e, ps=ps, k=k, kk=kk, xi=xi: e.transpose(
                        ps[:, k * 128:(k + 1) * 128], xi[:, kk * 128:(kk + 1) * 128], idt[:]),
                        reads=[txi, tid], writes=[tps])
                eng = "dve" if half == 0 else "act"
                outap = xt[:, half * 4:half * 4 + 4, tt * 128:(tt + 1) * 128]
                inap = ps[:].rearrange("p (k t) -> p k t", k=4)
                if eng == "dve":
                    fw.op("dve", lambda e, o=outap, i=inap: e.tensor_copy(out=o, in_=i), reads=[tps], writes=[txt])
                else:
                    fw.op("act", lambda e, o=outap, i=inap: e.copy(out=o, in_=i), reads=[tps], writes=[txt])
        tout = Tok()
        for k in range(8):
            fw.dma("sp", XT[k * 128:(k + 1) * 128, :], xt[:, k, :], reads=[txt], writes=[Tok()])
        fw.barrier()


def make_AB(fw, P, MODs, tmod, g_ap, sh_base, sc_base):
    g = P.sb([128, 8], F32)
    tg = Tok()
    fw.dma("sp", g[:], g_ap.rearrange("(k p) -> p k", p=128), writes=[tg], allow_slow_non_contiguous=True)
    AB = P.sb([128, 2, 2, 8], F32)
    tab = Tok()
    for ic in range(2):
        fw.op("dve", lambda e, ic=ic: e.tensor_scalar(out=AB[:, ic, 0, :], in0=MODs[:, sc_base:sc_base + 8, ic],
                                                      scalar1=1.0, scalar2=None, op0=ALU.add),
              reads=[tmod], writes=[tab])
        fw.op("dve", lambda e, ic=ic: e.tensor_tensor(out=AB[:, ic, 0, :], in0=AB[:, ic, 0, :], in1=g[:], op=ALU.mult),
              reads=[tg, tab], writes=[tab])
        fw.op("dve", lambda e, ic=ic: e.tensor_copy(out=AB[:, ic, 1, :], in_=MODs[:, sh_base:sh_base + 8, ic]),
              reads=[tmod], writes=[tab])
    return AB, tab


def norm_mod_seg(fw, P, st, xs, txs, n, ic, AB, tab, outs, touts):
    sq, ones, tones, psr, rs, tmpr = st["sq"], st["ones"], st["tones"], st["psr"], st["rs"], st["tmpr"]
    tsq, trs = st["tsq"], st["trs"]
    fw.op("act", lambda e: e.activation(out=sq[:, :, :n], in_=xs[:, :, :n], func=AF.Square), reads=[txs], writes=[tsq])
    ps, tps = psr.next()
    for k in range(8):
        fw.op("pe", lambda e, k=k: e.matmul(ps[:, :n], lhsT=ones[:], rhs=sq[:, k, :n], start=(k == 0), stop=(k == 7)),
              reads=[tsq, tones], writes=[tps])
    fw.op("act", lambda e: e.activation(out=rs[:, :n], in_=ps[:, :n], func=AF.Ln, scale=1.0 / DM, bias=st["eps"][:, 0:1]),
          reads=[tps, st["teps"]], writes=[trs])
    fw.op("act", lambda e: e.activation(out=rs[:, :n], in_=rs[:, :n], func=AF.Exp, scale=-0.5), reads=[trs], writes=[trs])
    for k in range(8):
        tmp, ttmp = tmpr.next()
        fw.op("dve", lambda e, k=k, tmp=tmp: e.tensor_tensor(out=tmp[:, :n], in0=xs[:, k, :n], in1=rs[:, :n], op=ALU.mult),
              reads=[txs, trs], writes=[ttmp])
        for o in outs(k):
            fw.op("act", lambda e, k=k, tmp=tmp, o=o: e.activation(out=o, in_=tmp[:, :n], func=AF.Identity,
                                                                   scale=AB[:, ic, 0, k:k + 1], bias=AB[:, ic, 1, k:k + 1]),
                  reads=[ttmp, tab], writes=touts)


def norm_state(fw, P):
    st = {}
    st["sq"] = P.sb([128, 8, 512], BF16)
    st["tsq"] = Tok()
    st["ones"] = P.sb([128, 128], BF16)
    st["tones"] = Tok()
    fw.op("pool", lambda e: e.memset(st["ones"][:], 1.0), writes=[st["tones"]])
    st["eps"] = P.sb([128, 1], F32)
    st["teps"] = Tok()
    fw.op("pool", lambda e: e.memset(st["eps"][:], RMS_EPS), writes=[st["teps"]])
    st["psr"] = Rot([P.ps() for _ in range(2)])
    st["rs"] = P.sb([128, 512], F32)
    st["trs"] = Tok()
    st["tmpr"] = Rot([P.sb([128, 512], F32) for _ in range(2)])
    return st


def stage_p1(fw, io):
    XT = io("XT", [DM, T], F32, "in")
    c_b = io("c_b", [DM], F32, "in")
    c_ctx = io("c_ctx", [DM], F32, "in")
    mod_w = io("mod_w", [DM, 6 * DM], F32, "in")
    mod_b = io("mod_b", [6 * DM], F32, "in")
    n1g = io("norm1_g", [DM], F32, "in")
    w_in = io("w_in", [DM, 1984], F32, "in")
    MOD = io("MOD", [128, 48, 2], F32, "out")
    ZT = io("ZT", [2048, TE], F32, "out")
    VT = io("VT", [T, 128], F32, "out")
    XTv = XT.rearrange("(k p) t -> p k t", p=128)
    with ExitStack() as es:
        P = Pool_(fw, es)
        wstage = Rot([P.sb([128, 2048], F32, "wstg") for _ in range(3)])
        tmw = Tok()
        cc = P.sb([128, 8, 2], F32)
        tcc = Tok()
        fw.dma("sp", cc[:, :, 0], c_b.rearrange("(k p) -> p k", p=128), writes=[tcc], allow_slow_non_contiguous=True)
        fw.dma("sp", cc[:, :, 1], c_ctx.rearrange("(k p) -> p k", p=128), writes=[tcc], allow_slow_non_contiguous=True)
        scb = P.sb([128, 8, 2], BF16)
        tscb = Tok()
        fw.op("act", lambda e: e.activation(out=scb[:], in_=cc[:], func=AF.Silu), reads=[tcc], writes=[tscb])
        mb = P.sb([128, 48], F32)
        tmb = Tok()
        fw.dma("sp", mb[:], mod_b.rearrange("(j p) -> p j", p=128), writes=[tmb], allow_slow_non_contiguous=True)
        MODs = P.sb([128, 48, 2], F32)
        tmod = Tok()
        with ExitStack() as es2:
            P2 = Pool_(fw, es2)
            mwb = load_w_bf16(fw, P2, mod_w, DM, 6 * DM, tmw, "modw", stage=wstage, cast_engs=("dve", "act", "pool"))
            psm = P2.ps([128, 512])
            tpsm = Tok()
            for j in range(48):
                for k in range(8):
                    fw.op("pe", lambda e, j=j, k=k: e.matmul(psm[:, 2 * j:2 * j + 2], lhsT=mwb[:, k, j * 128:(j + 1) * 128],
                                                             rhs=scb[:, k, :], start=(k == 0), stop=(k == 7)),
                          reads=[tmw, tscb], writes=[tpsm])
            for ic in range(2):
                fw.op("dve", lambda e, ic=ic: e.tensor_tensor(
                    out=MODs[:, :, ic], in0=psm[:, 0:96].rearrange("p (j c) -> p j c", c=2)[:, :, ic], in1=mb[:], op=ALU.add),
                    reads=[tpsm, tmb], writes=[tmod])
            fw.barrier()
        tmo = Tok()
        fw.dma("sp", MOD, MODs[:], reads=[tmod], writes=[tmo])
        AB, tab = make_AB(fw, P, MODs, tmod, n1g, 0, 8)
        tw = Tok()
        wb = load_w_bf16(fw, P, w_in, DM, 1984, tw, "win", stage=wstage, cast_engs=("dve", "act", "pool"))
        hT = P.sb([128, 8, TE], BF16)
        thT = Tok()
        st = norm_state(fw, P)
        xr = Rot([P.sb([128, 8, 512], F32) for _ in range(2)])
        for (lo, hi, ic) in SEGS:
            n = hi - lo
            xs, txs = xr.next()
            fw.dma("sp", xs[:, :, :n], XTv[:, :, lo:hi], writes=[txs])

            def outs(k, lo=lo, hi=hi, ic=ic):
                o = [hT[:, k, lo:hi]]
                if ic:
                    o.append(hT[:, k, T + lo:T + hi])
                return o
            norm_mod_seg(fw, P, st, xs, txs, n, ic, AB, tab, outs, [thT])
        psr = Rot([P.ps() for _ in range(4)])
        stg = Rot([P.sb([128, 512], F32) for _ in range(4)])
        tz = Tok()
        cnt = 0
        for nt in range(16):
            if nt == 7:
                continue
            M = 64 if nt == 15 else 128
            for cc_ in range(5):
                c0 = cc_ * 512
                ps, tps = psr.next()
                for k in range(8):
                    fw.op("pe", lambda e, ps=ps, k=k, nt=nt, M=M, c0=c0: e.matmul(
                        ps[0:M, :], lhsT=wb[:, k, nt * 128:nt * 128 + M], rhs=hT[:, k, c0:c0 + 512],
                        start=(k == 0), stop=(k == 7)), reads=[tw, thT], writes=[tps])
                sg, tsg = stg.next()
                if cnt % 2 == 0:
                    fw.op("dve", lambda e, sg=sg, ps=ps, M=M: e.tensor_copy(out=sg[0:M, :], in_=ps[0:M, :]), reads=[tps], writes=[tsg])
                else:
                    fw.op("act", lambda e, sg=sg, ps=ps, M=M: e.copy(out=sg[0:M, :], in_=ps[0:M, :]), reads=[tps], writes=[tsg])
                cnt += 1
                fw.dma("sp", ZT[nt * 128:nt * 128 + M, c0:c0 + 512], sg[0:M, :], reads=[tsg], writes=[Tok()])
        for tt in range(18):
            ps, tps = psr.next()
            for k in range(8):
                fw.op("pe", lambda e, ps=ps, k=k, tt=tt: e.matmul(
                    ps[:, 0:128], lhsT=hT[:, k, tt * 128:(tt + 1) * 128], rhs=wb[:, k, 896:1024],
                    start=(k == 0), stop=(k == 7)), reads=[tw, thT], writes=[tps])
            sg, tsg = stg.next()
            fw.op("dve", lambda e, sg=sg, ps=ps: e.tensor_copy(out=sg[:, 0:128], in_=ps[:, 0:128]), reads=[tps], writes=[tsg])
            fw.dma("sp", VT[tt * 128:(tt + 1) * 128, :], sg[:, 0:128], reads=[tsg], writes=[Tok()])
        fw.barrier()


def build_program(stage_fns):
    nc = bass.Bass("TRN2", target_bir_lowering=False)
    decl = {}

    def io(name, shape, dt, role):
        if name in decl:
            return decl[name][0]
        kind = "ExternalInput" if role == "in" else "ExternalOutput"
        ap = nc.dram_tensor(name, list(shape), dt, kind=kind).ap()
        decl[name] = (ap, role, shape)
        return ap
    with ExitStack() as es:
        fw = FW(nc, es)
        for fn in stage_fns:
            fn(fw, io)
        fw.barrier()
    return nc, decl, fw


_PROG_CACHE = {}


def run_stage(key, stage_fns, in_maps, ncores):
    if key not in _PROG_CACHE:
        _PROG_CACHE[key] = build_program(stage_fns)
    nc, decl, fw = _PROG_CACHE[key]
    res = run_bass_kernel_spmd(nc, in_maps, core_ids=list(range(ncores)))
    return res.results


def rstd_from_ps(fw, rs, trs, ps, tps, n, scale, epsap, teps, rows=128):
    fw.op("act", lambda e: e.activation(out=rs[0:rows, :n], in_=ps[0:rows, :n], func=AF.Ln, scale=scale, bias=epsap),
          reads=[tps, teps], writes=[trs])
    fw.op("act", lambda e: e.activation(out=rs[0:rows, :n], in_=rs[0:rows, :n], func=AF.Exp, scale=-0.5), reads=[trs], writes=[trs])


def stage_p3(fw, io):
    ZT = io("ZT", [2048, TE], F32, "in")
    VT = io("VT", [T, 128], F32, "in")
    qn_g = io("att_qn_g", [64], F32, "in")
    kn_g = io("att_kn_g", [64], F32, "in")
    og = io("att_out_g", [512], F32, "in")
    POS = io("POS", [128, NLAT], F32, "in")
    CST = io("CST", [128, 260], F32, "in")
    ATT = io("ATT_T", [512, T], F32, "out")
    with ExitStack() as es:
        P = Pool_(fw, es)
        cst = P.sb([128, 260], F32)
        tc = Tok()
        fw.dma("sp", cst[:], CST, writes=[tc])
        permb = P.sb([128, 128], BF16)
        bdb = P.sb([128, 128], BF16)
        fw.op("dve", lambda e: e.tensor_copy(out=permb[:], in_=cst[:, 1:129]), reads=[tc], writes=[tc])
        fw.op("dve", lambda e: e.tensor_copy(out=bdb[:], in_=cst[:, 129:257]), reads=[tc], writes=[tc])
        eps = P.sb([128, 2], F32)
        teps = Tok()
        fw.op("pool", lambda e: e.memset(eps[:, 0:1], RMS_EPS), writes=[teps])
        fw.op("pool", lambda e: e.memset(eps[:, 1:2], 0.0), writes=[teps])
        gq = P.sb([128, 2], F32)
        tg = Tok()
        for h in range(2):
            fw.dma("sp", gq[h * 64:(h + 1) * 64, 0:1], qn_g.rearrange("(p o) -> p o", o=1), writes=[tg])
            fw.dma("sp", gq[h * 64:(h + 1) * 64, 1:2], kn_g.rearrange("(p o) -> p o", o=1), writes=[tg])
        cos = P.sb([128, NLAT], F32)
        sin = P.sb([128, NLAT], F32)
        ttab = Tok()
        with ExitStack() as es2:
            P2 = Pool_(fw, es2)
            ang = P2.sb([128, NLAT], F32)
            tmpf = P2.sb([128, NLAT], F32)
            tmpi = P2.sb([128, NLAT], I32)
            ta = Tok()
            fw.dma("sp", ang[:], POS, writes=[ta])
            fw.op("dve", lambda e: e.tensor_scalar(out=ang[:], in0=ang[:], scalar1=cst[:, 0:1], scalar2=None, op0=ALU.mult),
                  reads=[ta, tc], writes=[ta])
            for (tab, off) in ((sin, 0.0), (cos, math.pi / 2)):
                fw.op("dve", lambda e, off=off: e.tensor_scalar(out=tmpi[:], in0=ang[:], scalar1=off, scalar2=1.0 / (2 * math.pi),
                                                                op0=ALU.add, op1=ALU.mult), reads=[ta], writes=[ta])
                fw.op("dve", lambda e: e.tensor_copy(out=tmpf[:], in_=tmpi[:]), reads=[ta], writes=[ta])
                fw.op("dve", lambda e: e.scalar_tensor_tensor(out=tmpf[:], in0=tmpf[:], scalar=-2 * math.pi, in1=ang[:],
                                                              op0=ALU.mult, op1=ALU.add), reads=[ta], writes=[ta])
                fw.op("dve", lambda e, off=off: e.tensor_scalar(out=tmpf[:], in0=tmpf[:], scalar1=off, scalar2=math.pi,
                                                                op0=ALU.add, op1=ALU.min), reads=[ta], writes=[ta])
                fw.op("dve", lambda e: e.tensor_scalar(out=tmpf[:], in0=tmpf[:], scalar1=-math.pi, scalar2=None, op0=ALU.max),
                      reads=[ta], writes=[ta])
                fw.op("act", lambda e, tab=tab: e.activation(out=tab[:], in_=tmpf[:], func=AF.Sin), reads=[ta], writes=[ttab])
            fw.barrier()
        qb = P.sb([128, 4, T], BF16)
        kd = P.sb([128, 2, T], BF16)
        tq = Tok()
        with ExitStack() as es2:
            P2 = Pool_(fw, es2)
            raw = Rot([P2.sb([128, 512], F32) for _ in range(2)])
            sqr = Rot([P2.sb([128, 512], BF16) for _ in range(2)])
            psr = Rot([P2.ps() for _ in range(2)])
            psr2 = Rot([P2.ps() for _ in range(2)])
            rsr = Rot([P2.sb([128, 512], F32) for _ in range(2)])
            nbr = Rot([P2.sb([128, 512], BF16) for _ in range(2)])
            t1r = Rot([P2.sb([128, 512], F32) for _ in range(2)])
            t2r = Rot([P2.sb([128, 512], F32) for _ in range(2)])
            items = [("q", j) for j in range(4)] + [("k", g) for g in range(2)]
            for (kind, j) in items:
                for (lo, hi, ic) in SEGS:
                    n = hi - lo
                    rw, trw = raw.next()
                    if kind == "q":
                        fw.dma("sp", rw[:, :n], ZT[256 + j * 128:256 + (j + 1) * 128, lo:hi], writes=[trw])
                        gcol = 0
                        dst = qb[:, j, lo:hi]
                    else:
                        for h in range(2):
                            fw.dma("sp", rw[h * 64:(h + 1) * 64, :n], ZT[768 + j * 64:768 + (j + 1) * 64, lo:hi], writes=[trw])
                        gcol = 1
                        dst = kd[:, j, lo:hi]
                    sq, tsq = sqr.next()
                    fw.op("act", lambda e, sq=sq, rw=rw, n=n: e.activation(out=sq[:, :n], in_=rw[:, :n], func=AF.Square), reads=[trw], writes=[tsq])
                    ps, tps = psr.next()
                    fw.op("pe", lambda e, ps=ps, sq=sq, n=n: e.matmul(ps[:, :n], lhsT=bdb[:], rhs=sq[:, :n], start=True, stop=True),
                          reads=[tsq, tc], writes=[tps])
                    rs, trs = rsr.next()
                    rstd_from_ps(fw, rs, trs, ps, tps, n, 1.0, eps[:, 0:1], teps)
                    t1, tt1 = t1r.next()
                    fw.op("dve", lambda e, t1=t1, rw=rw, rs=rs, n=n, gcol=gcol: e.scalar_tensor_tensor(
                        out=t1[:, :n], in0=rw[:, :n], scalar=gq[:, gcol:gcol + 1], in1=rs[:, :n], op0=ALU.mult, op1=ALU.mult),
                        reads=[trw, trs, tg], writes=[tt1])
                    if ic:
                        fw.op("act", lambda e, dst=dst, t1=t1, n=n: e.copy(out=dst, in_=t1[:, :n]), reads=[tt1], writes=[tq])
                        continue
                    nb, tnb = nbr.next()
                    fw.op("act", lambda e, nb=nb, t1=t1, n=n: e.copy(out=nb[:, :n], in_=t1[:, :n]), reads=[tt1], writes=[tnb])
                    ps2, tps2 = psr2.next()
                    fw.op("pe", lambda e, ps2=ps2, nb=nb, n=n: e.matmul(ps2[:, :n], lhsT=permb[:], rhs=nb[:, :n], start=True, stop=True),
                          reads=[tnb, tc], writes=[tps2])
                    p0 = lo - NCTX
                    t2, tt2 = t2r.next()
                    fw.op("dve", lambda e, t2=t2, ps2=ps2, n=n, p0=p0: e.tensor_tensor(out=t2[:, :n], in0=ps2[:, :n], in1=sin[:, p0:p0 + n], op=ALU.mult),
                          reads=[tps2, ttab], writes=[tt2])
                    fw.op("pool", lambda e, t1=t1, nb=nb, n=n, p0=p0: e.tensor_tensor(out=t1[:, :n], in0=nb[:, :n], in1=cos[:, p0:p0 + n], op=ALU.mult),
                          reads=[tnb, ttab, tt1], writes=[tt1])
                    fw.op("pool", lambda e, dst=dst, t1=t1, t2=t2, n=n: e.tensor_tensor(out=dst, in0=t1[:, :n], in1=t2[:, :n], op=ALU.add),
                          reads=[tt1, tt2], writes=[tq])
            fw.barrier()
        va = P.sb([128, 18, 2, 128], BF16)
        tva = Tok()
        fw.op("pool", lambda e: e.memset(va[:], 1.0), writes=[tva])
        for g in range(2):
            fw.dma("pool", va[:, :, g, 0:64], VT.rearrange("(t p) c -> p t c", p=128)[:, :, g * 64:(g + 1) * 64], writes=[tva])
        att = P.sb([128, 4, T], F32)
        tatt = Tok()
        pss = Rot([P.ps() for _ in range(3)])
        pso = Rot([P.ps() for _ in range(2)])
        ptr = Rot([P.sb([128, 512], BF16) for _ in range(3)])
        rcr = Rot([P.sb([64, 512], F32) for _ in range(2)])
        jobs = [(0, 256, 0, 2)] + [(256 + 512 * i, 768 + 512 * i, 0, 18) for i in range(4)]
        for h in range(8):
            g = h // 4
            jt, r0 = h // 2, (h % 2) * 64
            for (qlo, qhi, k0, k1) in jobs:
                n = qhi - qlo
                po, tpo = pso.next()
                for kt in range(k0, k1):
                    ps, tps = pss.next()
                    fw.op("pe", lambda e, ps=ps, kt=kt, g=g, jt=jt, r0=r0, qlo=qlo, qhi=qhi, n=n: e.matmul(
                        ps[:, :n], lhsT=kd[r0:r0 + 64, g, kt * 128:(kt + 1) * 128], rhs=qb[r0:r0 + 64, jt, qlo:qhi],
                        start=True, stop=True), reads=[tq], writes=[tps])
                    pt, tpt = ptr.next()
                    fw.op("act", lambda e, pt=pt, ps=ps, n=n: e.activation(out=pt[:, :n], in_=ps[:, :n], func=AF.Exp, scale=0.125),
                          reads=[tps], writes=[tpt])
                    fw.op("pe", lambda e, po=po, pt=pt, kt=kt, g=g, n=n, k0=k0, k1=k1: e.matmul(
                        po[:, :n], lhsT=va[:, kt, g, :], rhs=pt[:, :n], start=(kt == k0), stop=(kt == k1 - 1)),
                        reads=[tpt, tva], writes=[tpo])
                rc, trc = rcr.next()
                fw.op("dve", lambda e, rc=rc, po=po, n=n: e.reciprocal(out=rc[:, :n], in_=po[64:128, :n]), reads=[tpo], writes=[trc])
                fw.op("dve", lambda e, rc=rc, po=po, n=n, jt=jt, r0=r0, qlo=qlo, qhi=qhi: e.tensor_tensor(
                    out=att[r0:r0 + 64, jt, qlo:qhi], in0=po[0:64, :n], in1=rc[:, :n], op=ALU.mult),
                    reads=[tpo, trc], writes=[tatt])
        ogs = P.sb([128, 4], F32)
        tog = Tok()
        fw.dma("sp", ogs[:], og.rearrange("(k p) -> p k", p=128), writes=[tog], allow_slow_non_contiguous=True)
        ones = P.sb([128, 128], BF16)
        fw.op("pool", lambda e: e.memset(ones[:], 1.0), writes=[tog])
        sq4 = P.sb([128, 4, 512], BF16)
        tsq4 = Tok()
        rs = P.sb([128, 512], F32)
        trs = Tok()
        stg = Rot([P.sb([128, 512], F32) for _ in range(3)])
        tout = Tok()
        for (lo, hi, ic) in SEGS:
            n = hi - lo
            fw.op("act", lambda e, lo=lo, hi=hi, n=n: e.activation(out=sq4[:, :, :n], in_=att[:, :, lo:hi], func=AF.Square), reads=[tatt], writes=[tsq4])
            ps, tps = pss.next()
            for k in range(4):
                fw.op("pe", lambda e, ps=ps, k=k, n=n: e.matmul(ps[:, :n], lhsT=ones[:], rhs=sq4[:, k, :n], start=(k == 0), stop=(k == 3)),
                      reads=[tsq4, tog], writes=[tps])
            rstd_from_ps(fw, rs, trs, ps, tps, n, 1.0 / 512, eps[:, 0:1], teps)
            for k in range(4):
                sg, tsg = stg.next()
                fw.op("dve", lambda e, sg=sg, k=k, lo=lo, hi=hi, n=n: e.scalar_tensor_tensor(
                    out=sg[:, :n], in0=att[:, k, lo:hi], scalar=ogs[:, k:k + 1], in1=rs[:, :n], op0=ALU.mult, op1=ALU.mult),
                    reads=[tatt, trs, tog], writes=[tsg])
                fw.dma("sp", ATT[k * 128:(k + 1) * 128, lo:hi], sg[:, :n], reads=[tsg], writes=[Tok()])
        fw.barrier()


def host_consts():
    pos = np.zeros((128, NLAT), np.float32)
    inv = np.zeros((128,), np.float32)
    tok = np.arange(NLAT)
    for p in range(128):
        d = p % 64
        pos[p] = (tok // 64) if d < 32 else (tok % 64)
        inv[p] = 10000.0 ** (-(d % 16) / 16.0)
    cst = np.zeros((128, 260), np.float32)
    cst[:, 0] = inv
    perm = np.zeros((128, 128), np.float32)
    for m in range(128):
        d = m % 32
        if d < 16:
            perm[m + 16, m] = -1.0
        else:
            perm[m - 16, m] = 1.0
    cst[:, 1:129] = perm
    bd = np.zeros((128, 128), np.float32)
    bd[:64, :64] = 1.0 / 64
    bd[64:, 64:] = 1.0 / 64
    cst[:, 129:257] = bd
    return pos, cst


def stage_p5a(fw, io):
    XT = io("XT", [DM, T], F32, "in")
    S5T = io("S5T", [256, T], F32, "in")
    ATT = io("ATT_T", [512, T], F32, "in")
    RWT = io("RWT", [256, T], F32, "in")
    MOD = io("MOD", [128, 48, 2], F32, "in")
    w_out = io("w_out", [DM, DM], F32, "in")
    XT1 = io("XT1", [DM, T], F32, "out")
    XTv = XT.rearrange("(k p) t -> p k t", p=128)
    with ExitStack() as es:
        P = Pool_(fw, es)
        MODs = P.sb([128, 48, 2], F32)
        tmod = Tok()
        fw.dma("sp", MODs[:], MOD, writes=[tmod])
        tw = Tok()
        wb = load_w_bf16(fw, P, w_out, DM, DM, tw, "wout", cast_engs=("dve", "act", "pool"))
        cat = P.sb([128, 8, T], BF16)
        tcat = Tok()
        cstg = Rot([P.sb([128, T], F32, "cstg") for _ in range(2)])
        for k in range(8):
            src = S5T[k * 128:(k + 1) * 128, :] if k < 2 else (ATT[(k - 2) * 128:(k - 1) * 128, :] if k < 6 else RWT[(k - 6) * 128:(k - 5) * 128, :])
            sg, tsg = cstg.next()
            fw.dma("sp", sg[:], src, writes=[tsg])
            if k % 2 == 0:
                fw.op("dve", lambda e, sg=sg, k=k: e.tensor_copy(out=cat[:, k, :], in_=sg[:]), reads=[tsg], writes=[tcat])
            else:
                fw.op("act", lambda e, sg=sg, k=k: e.copy(out=cat[:, k, :], in_=sg[:]), reads=[tsg], writes=[tcat])
        xr = Rot([P.sb([128, 8, 512], F32) for _ in range(2)])
        x1r = Rot([P.sb([128, 8, 512], F32) for _ in range(2)])
        psr = Rot([P.ps() for _ in range(4)])
        to1, to2 = Tok(), Tok()
        for (lo, hi, ic) in SEGS:
            n = hi - lo
            xs, txs = xr.next()
            fw.dma("sp", xs[:, :, :n], XTv[:, :, lo:hi], writes=[txs])
            x1, tx1 = x1r.next()
            for d in range(8):
                ps, tps = psr.next()
                for k in range(8):
                    fw.op("pe", lambda e, ps=ps, k=k, d=d, lo=lo, hi=hi, n=n: e.matmul(
                        ps[:, :n], lhsT=wb[:, k, d * 128:(d + 1) * 128], rhs=cat[:, k, lo:hi], start=(k == 0), stop=(k == 7)),
                        reads=[tw, tcat], writes=[tps])
                fw.op("dve", lambda e, ps=ps, d=d, n=n, ic=ic, x1=x1, xs=xs: e.scalar_tensor_tensor(
                    out=x1[:, d, :n], in0=ps[:, :n], scalar=MODs[:, 16 + d, ic:ic + 1], in1=xs[:, d, :n], op0=ALU.mult, op1=ALU.add),
                    reads=[tps, txs, tmod], writes=[tx1])
            for k in range(8):
                fw.dma("sp", XT1[k * 128:(k + 1) * 128, lo:hi], x1[:, k, :n], reads=[tx1], writes=[Tok()])
        fw.barrier()


def stage_p5b(fw, io):
    XT1 = io("XT1", [DM, T], F32, "in")
    n2g = io("norm2_g", [DM], F32, "in")
    MOD = io("MOD", [128, 48, 2], F32, "in")
    up = io("ffn_up", [44, 128, DM], F32, "in")
    cw = io("ffn_conv_w", [3, 5632], F32, "in")
    cb = io("ffn_conv_b", [5632], F32, "in")
    down = io("ffn_down", [2816, DM], F32, "in")
    XT2 = io("XT2", [DM, T], F32, "out")
    X1v = XT1.rearrange("(k p) t -> p k t", p=128)
    X2v = XT2.rearrange("(k p) t -> p k t", p=128)
    with ExitStack() as es:
        P = Pool_(fw, es)
        MODs = P.sb([128, 48, 2], F32)
        tmod = Tok()
        fw.dma("sp", MODs[:], MOD, writes=[tmod])
        cws = P.sb([128, 44, 3], F32)
        cbs = P.sb([128, 44], F32)
        tcw = Tok()
        for w in range(3):
            fw.dma("sp", cws[:, :, w], cw[w].rearrange("(j p) -> p j", p=128), writes=[tcw], allow_slow_non_contiguous=True)
        fw.dma("sp", cbs[:], cb.rearrange("(j p) -> p j", p=128), writes=[tcw], allow_slow_non_contiguous=True)
        h2 = P.sb([128, 8, T], BF16)
        th2 = Tok()
        AB, tab = make_AB(fw, P, MODs, tmod, n2g, 24, 32)
        with ExitStack() as es2:
            P2 = Pool_(fw, es2)
            st = norm_state(fw, P2)
            xr0 = Rot([P2.sb([128, 8, 512], F32) for _ in range(2)])
            for (lo, hi, ic) in SEGS:
                n = hi - lo
                xs, txs = xr0.next()
                fw.dma("sp", xs[:, :, :n], X1v[:, :, lo:hi], writes=[txs])
                norm_mod_seg(fw, P2, st, xs, txs, n, ic, AB, tab, lambda k, lo=lo, hi=hi: [h2[:, k, lo:hi]], [th2])
            fw.barrier()
        hid = P.sb([128, 11, T], BF16)
        thid = Tok()
        dwb = P.sb([128, 11, DM], BF16)
        tdw = Tok()
        urot = [Rot([P.sb([128, T], F32) for _ in range(2)]) for _ in range(2)]
        y = [P.sb([128, T], F32) for _ in range(2)]
        ty = [Tok(), Tok()]
        psr = Rot([P.ps() for _ in range(4)])
        tx2 = Tok()
        RANGES = [(0, NCTX), (NCTX, T)]
        GROUPS = [(0, 2), (2, 2), (4, 2), (6, 2), (8, 2), (10, 1)]
        for half in range(2):
            with ExitStack() as esu:
                Pu = Pool_(fw, esu)
                ustg = Rot([Pu.sb([128, 8, 256], F32, "ustg") for _ in range(2)])
                for jj in range(11):
                    r0 = (half * 11 + jj) * 128
                    sg, tsg = ustg.next()
                    sgv = sg[:].rearrange("p k n -> p (k n)")[:, 0:DM]
                    fw.dma("sp", sgv, down[r0:r0 + 128, :], writes=[tsg])
                    fw.op("pool", lambda e, sgv=sgv, jj=jj: e.tensor_copy(out=dwb[:, jj, :], in_=sgv), reads=[tsg], writes=[tdw])
                ubr = Rot([Pu.sb([128, 8, 2, 256], BF16, "ub") for _ in range(2)])

                def issue_load(g, half=half):
                    jj0, ng = GROUPS[g]
                    ub, tub = ubr.next()
                    for wh in range(2):
                        sg, tsg = ustg.next()
                        sgv = sg[:].rearrange("p k (j c) -> p (k j c)", j=2).rearrange("p (j k c) -> p j k c", j=2, k=8)
                        for jl in range(ng):
                            jt = wh * 22 + half * 11 + jj0 + jl
                            fw.dma("sp", sgv[:, jl].rearrange("p k c -> p (k c)"), up[jt], writes=[tsg])
                            fw.op("pool", lambda e, sgv=sgv, ub=ub, wh=wh, jl=jl: e.tensor_copy(out=ub[:, :, wh, jl * 128:(jl + 1) * 128], in_=sgv[:, jl]),
                                  reads=[tsg], writes=[tub])
                    return ub, tub
                loaded = issue_load(0)
                for g, (jj0, ng) in enumerate(GROUPS):
                    ub, tub = loaded
                    if g + 1 < len(GROUPS):
                        loaded = issue_load(g + 1)
                    for jl in range(ng):
                        jj = jj0 + jl
                        j = half * 11 + jj
                        for wh in range(2):
                            jc = wh * 22 + j
                            ucur, tucur = urot[wh].next()
                            for si, (lo, hi, ic) in enumerate(SEGS):
                                n = hi - lo
                                ps, tps = psr.next()
                                for k in range(8):
                                    fw.op("pe", lambda e, ps=ps, k=k, wh=wh, ub=ub, jl=jl, lo=lo, hi=hi, n=n: e.matmul(
                                        ps[:, :n], lhsT=ub[:, k, wh, jl * 128:(jl + 1) * 128], rhs=h2[:, k, lo:hi], start=(k == 0), stop=(k == 7)),
                                        reads=[tub, th2], writes=[tps])
                                fw.op("act", lambda e, ps=ps, ucur=ucur, lo=lo, hi=hi, n=n: e.copy(out=ucur[:, lo:hi], in_=ps[:, :n]),
                                      reads=[tps], writes=[tucur])
                            fw.op("act", lambda e, wh=wh, jc=jc, ucur=ucur: e.activation(out=y[wh][:], in_=ucur[:], func=AF.Identity,
                                                                                        scale=cws[:, jc, 1:2], bias=cbs[:, jc:jc + 1]),
                                  reads=[tucur, tcw], writes=[ty[wh]])
                            for (lo, hi) in RANGES:
                                fw.op("dve", lambda e, wh=wh, jc=jc, lo=lo, hi=hi, ucur=ucur: e.scalar_tensor_tensor(
                                    out=y[wh][:, lo + 1:hi], in0=ucur[:, lo:hi - 1], scalar=cws[:, jc, 0:1], in1=y[wh][:, lo + 1:hi],
                                    op0=ALU.mult, op1=ALU.add), reads=[tucur, tcw, ty[wh]], writes=[ty[wh]])
                                fw.op("dve", lambda e, wh=wh, jc=jc, lo=lo, hi=hi, ucur=ucur: e.scalar_tensor_tensor(
                                    out=y[wh][:, lo:hi - 1], in0=ucur[:, lo + 1:hi], scalar=cws[:, jc, 2:3], in1=y[wh][:, lo:hi - 1],
                                    op0=ALU.mult, op1=ALU.add), reads=[tucur, tcw, ty[wh]], writes=[ty[wh]])
                        fw.op("act", lambda e: e.activation(out=y[0][:], in_=y[0][:], func=AF.Silu), reads=[ty[0]], writes=[ty[0]])
                        fw.op("dve", lambda e, jj=jj: e.tensor_tensor(out=hid[:, jj, :], in0=y[0][:], in1=y[1][:], op=ALU.mult),
                              reads=[ty[0], ty[1]], writes=[thid])
                fw.barrier()
            with ExitStack() as esd:
                Pd = Pool_(fw, esd)
                xr = Rot([Pd.sb([128, 8, 512], F32) for _ in range(2)])
                for (lo, hi, ic) in SEGS:
                    n = hi - lo
                    xs, txs = xr.next()
                    src = X1v if half == 0 else X2v
                    fw.dma("sp", xs[:, :, :n], src[:, :, lo:hi], reads=([tx2] if half else []), writes=[txs])
                    for d in range(8):
                        ps, tps = psr.next()
                        for jj in range(11):
                            fw.op("pe", lambda e, ps=ps, jj=jj, d=d, lo=lo, hi=hi, n=n: e.matmul(
                                ps[:, :n], lhsT=dwb[:, jj, d * 128:(d + 1) * 128], rhs=hid[:, jj, lo:hi], start=(jj == 0), stop=(jj == 10)),
                                reads=[tdw, thid], writes=[tps])
                        fw.op("dve", lambda e, ps=ps, d=d, n=n, ic=ic, xs=xs: e.scalar_tensor_tensor(
                            out=xs[:, d, :n], in0=ps[:, :n], scalar=MODs[:, 40 + d, ic:ic + 1], in1=xs[:, d, :n], op0=ALU.mult, op1=ALU.add),
                            reads=[tps, tmod, txs], writes=[txs])
                    for k in range(8):
                        fw.dma("sp", XT2[k * 128:(k + 1) * 128, lo:hi], xs[:, k, :n], reads=[txs], writes=[tx2])
                fw.barrier()


def stage_p6(fw, io):
    XT = io("XT", [DM, T], F32, "in")
    fg = io("final_g", [DM], F32, "in")
    ident = io("ident", [128, 128], F32, "in")
    OUT = io("OUT", [NLAT, DM], F32, "out")
    XTv = XT.rearrange("(k p) t -> p k t", p=128)
    with ExitStack() as es:
        P = Pool_(fw, es)
        idt = P.sb([128, 128], F32)
        tid = Tok()
        fw.dma("sp", idt[:], ident, writes=[tid])
        g = P.sb([128, 8], F32)
        fw.dma("sp", g[:], fg.rearrange("(k p) -> p k", p=128), writes=[tid], allow_slow_non_contiguous=True)
        st = norm_state(fw, P)
        xr = Rot([P.sb([128, 8, 512], F32) for _ in range(2)])
        yr = Rot([P.sb([128, 8, 512], F32) for _ in range(2)])
        psr = Rot([P.ps() for _ in range(4)])
        orr = Rot([P.sb([128, DM], F32) for _ in range(3)])
        tout = Tok()
        for (lo, hi, ic) in SEGS[1:]:
            n = hi - lo
            xs, txs = xr.next()
            fw.dma("sp", xs[:, :, :n], XTv[:, :, lo:hi], writes=[txs])
            sq, ones = st["sq"], st["ones"]
            fw.op("act", lambda e, xs=xs: e.activation(out=sq[:], in_=xs[:], func=AF.Square), reads=[txs], writes=[st["tsq"]])
            ps, tps = st["psr"].next()
            for k in range(8):
                fw.op("pe", lambda e, ps=ps, k=k: e.matmul(ps[:], lhsT=ones[:], rhs=sq[:, k, :], start=(k == 0), stop=(k == 7)),
                      reads=[st["tsq"], st["tones"]], writes=[tps])
            rstd_from_ps(fw, st["rs"], st["trs"], ps, tps, n, 1.0 / DM, st["eps"][:, 0:1], st["teps"])
            ys, tys = yr.next()
            for k in range(8):
                fw.op("dve", lambda e, ys=ys, xs=xs, k=k: e.scalar_tensor_tensor(
                    out=ys[:, k, :], in0=xs[:, k, :], scalar=g[:, k:k + 1], in1=st["rs"][:], op0=ALU.mult, op1=ALU.mult),
                    reads=[txs, st["trs"], tid], writes=[tys])
            for blk in range(4):
                ot, tot = orr.next()
                for half in range(2):
                    ps2, tps2 = psr.next()
                    for k in range(4):
                        kk = half * 4 + k
                        fw.op("pe", lambda e, ps2=ps2, k=k, kk=kk, ys=ys, blk=blk: e.transpose(
                            ps2[:, k * 128:(k + 1) * 128], ys[:, kk, blk * 128:(blk + 1) * 128], idt[:]),
                            reads=[tys, tid], writes=[tps2])
                    if half == 0:
                        fw.op("dve", lambda e, ot=ot, ps2=ps2: e.tensor_copy(out=ot[:, 0:512], in_=ps2[:]), reads=[tps2], writes=[tot])
                    else:
                        fw.op("act", lambda e, ot=ot, ps2=ps2: e.copy(out=ot[:, 512:1024], in_=ps2[:]), reads=[tps2], writes=[tot])
                r0 = lo - NCTX + blk * 128
                fw.dma("sp", OUT[r0:r0 + 128, :], ot[:], reads=[tot], writes=[Tok()])
        fw.barrier()


def sin_reduced(fw, P, out, src, tsrc, shape, off, tout):
    ti = P.sb(shape, I32)
    tf = P.sb(shape, F32)
    tt = Tok()
    fw.op("dve", lambda e: e.tensor_scalar(out=ti[:], in0=src, scalar1=off, scalar2=1.0 / (2 * math.pi), op0=ALU.add, op1=ALU.mult),
          reads=[tsrc], writes=[tt])
    fw.op("dve", lambda e: e.tensor_copy(out=tf[:], in_=ti[:]), reads=[tt], writes=[tt])
    fw.op("dve", lambda e: e.scalar_tensor_tensor(out=tf[:], in0=tf[:], scalar=-2 * math.pi, in1=src, op0=ALU.mult, op1=ALU.add),
          reads=[tt, tsrc], writes=[tt])
    fw.op("dve", lambda e: e.tensor_scalar(out=tf[:], in0=tf[:], scalar1=off, scalar2=math.pi, op0=ALU.add, op1=ALU.min), reads=[tt], writes=[tt])
    fw.op("dve", lambda e: e.tensor_scalar(out=tf[:], in0=tf[:], scalar1=-math.pi, scalar2=None, op0=ALU.max), reads=[tt], writes=[tt])
    fw.op("act", lambda e: e.activation(out=out, in_=tf[:], func=AF.Sin), reads=[tt], writes=[tout])


def stage_p2(fw, io):
    ZT = io("ZT", [2048, TE], F32, "in")
    a_re = io("s5_a_re", [2, 16, 64], F32, "in")
    a_im = io("s5_a_im", [2, 16, 64], F32, "in")
    lstep = io("s5_log_step", [2, 16], F32, "in")
    b_re = io("s5_b_re", [2, 16, 64, 16], F32, "in")
    b_im = io("s5_b_im", [2, 16, 64, 16], F32, "in")
    c_re = io("s5_c_re", [2, 16, 16, 64], F32, "in")
    c_im = io("s5_c_im", [2, 16, 16, 64], F32, "in")
    dsk = io("s5_d", [256], F32, "in")
    glu_w = io("s5_glu_w", [256, 256], F32, "in")
    glu_b = io("s5_glu_b", [256], F32, "in")
    out_g = io("s5_out_g", [256], F32, "in")
    S5T = io("S5T", [256, T], F32, "out")
    N = T
    with ExitStack() as es:
        P = Pool_(fw, es)
        are = P.sb([128, 2, 8], F32)
        aim = P.sb([128, 2, 8], F32)
        lst = P.sb([128, 2, 8], F32)
        tpar = Tok()
        for di in range(2):
            fw.dma("sp", are[:, di, :], a_re[di].rearrange("(s g) p -> (g p) s", g=2), writes=[tpar], allow_slow_non_contiguous=True)
            fw.dma("sp", aim[:, di, :], a_im[di].rearrange("(s g) p -> (g p) s", g=2), writes=[tpar], allow_slow_non_contiguous=True)
            for g2 in range(2):
                fw.dma("sp", lst[g2 * 64:(g2 + 1) * 64, di:di + 1, :],
                       lstep[di].rearrange("(s g) -> g s", g=2)[g2:g2 + 1, :].partition_broadcast(64), writes=[tpar],
                       allow_slow_non_contiguous=True)
        sh = [128, 2, 8]
        step = P.sb(sh, F32)
        fw.op("act", lambda e: e.activation(out=step[:], in_=lst[:], func=AF.Exp), reads=[tpar], writes=[tpar])
        er = P.sb(sh, F32)
        th = P.sb(sh, F32)
        fw.op("dve", lambda e: e.tensor_tensor(out=er[:], in0=are[:], in1=step[:], op=ALU.mult), reads=[tpar], writes=[tpar])
        fw.op("act", lambda e: e.activation(out=er[:], in_=er[:], func=AF.Exp), reads=[tpar], writes=[tpar])
        fw.op("dve", lambda e: e.tensor_tensor(out=th[:], in0=aim[:], in1=step[:], op=ALU.mult), reads=[tpar], writes=[tpar])
        sn = P.sb(sh, F32)
        cs = P.sb(sh, F32)
        ttrig = Tok()
        sin_reduced(fw, P, sn[:], th[:], tpar, sh, 0.0, ttrig)
        sin_reduced(fw, P, cs[:], th[:], tpar, sh, math.pi / 2, ttrig)
        PW = P.sb([128, 9, 3, 2, 8], F32)
        tpw = Tok()
        fw.op("dve", lambda e: e.tensor_tensor(out=PW[:, 0, 0], in0=er[:], in1=cs[:], op=ALU.mult), reads=[tpar, ttrig], writes=[tpw])
        fw.op("dve", lambda e: e.tensor_tensor(out=PW[:, 0, 1], in0=er[:], in1=sn[:], op=ALU.mult), reads=[tpar, ttrig], writes=[tpw])
        t1 = P.sb(sh, F32)
        t2 = P.sb(sh, F32)
        for k in range(9):
            fw.op("dve", lambda e, k=k: e.tensor_scalar(out=PW[:, k, 2], in0=PW[:, k, 1], scalar1=-1.0, scalar2=None, op0=ALU.mult),
                  reads=[tpw], writes=[tpw])
            if k == 8:
                break
            fw.op("dve", lambda e, k=k: e.tensor_tensor(out=t1[:], in0=PW[:, k, 0], in1=PW[:, k, 0], op=ALU.mult), reads=[tpw], writes=[tpw])
            fw.op("dve", lambda e, k=k: e.tensor_tensor(out=t2[:], in0=PW[:, k, 1], in1=PW[:, k, 1], op=ALU.mult), reads=[tpw], writes=[tpw])
            fw.op("dve", lambda e, k=k: e.tensor_tensor(out=PW[:, k + 1, 0], in0=t1[:], in1=t2[:], op=ALU.subtract), reads=[tpw], writes=[tpw])
            fw.op("dve", lambda e, k=k: e.scalar_tensor_tensor(out=PW[:, k + 1, 1], in0=PW[:, k, 0], scalar=2.0, in1=PW[:, k, 1],
                                                               op0=ALU.mult, op1=ALU.mult), reads=[tpw], writes=[tpw])
        br = P.sb(sh, F32)
        bi = P.sb(sh, F32)
        nbi = P.sb(sh, F32)
        den = P.sb(sh, F32)
        nr = P.sb(sh, F32)
        tb = Tok()
        fw.op("dve", lambda e: e.tensor_tensor(out=den[:], in0=are[:], in1=are[:], op=ALU.mult), reads=[tpar], writes=[tb])
        fw.op("dve", lambda e: e.tensor_tensor(out=t1[:], in0=aim[:], in1=aim[:], op=ALU.mult), reads=[tpar, tpw], writes=[tpw])
        fw.op("dve", lambda e: e.tensor_tensor(out=den[:], in0=den[:], in1=t1[:], op=ALU.add), reads=[tb, tpw], writes=[tb])
        fw.op("dve", lambda e: e.reciprocal(out=den[:], in_=den[:]), reads=[tb], writes=[tb])
        fw.op("dve", lambda e: e.tensor_scalar(out=nr[:], in0=PW[:, 0, 0], scalar1=-1.0, scalar2=None, op0=ALU.add), reads=[tpw], writes=[tb])
        fw.op("dve", lambda e: e.tensor_tensor(out=t1[:], in0=nr[:], in1=are[:], op=ALU.mult), reads=[tb, tpar, tpw], writes=[tpw])
        fw.op("dve", lambda e: e.tensor_tensor(out=t2[:], in0=PW[:, 0, 1], in1=aim[:], op=ALU.mult), reads=[tpw, tpar], writes=[tpw])
        fw.op("dve", lambda e: e.tensor_tensor(out=t1[:], in0=t1[:], in1=t2[:], op=ALU.add), reads=[tpw], writes=[tpw])
        fw.op("dve", lambda e: e.tensor_tensor(out=br[:], in0=t1[:], in1=den[:], op=ALU.mult), reads=[tpw, tb], writes=[tb])
        fw.op("dve", lambda e: e.tensor_tensor(out=t1[:], in0=PW[:, 0, 1], in1=are[:], op=ALU.mult), reads=[tpw, tpar, tb], writes=[tpw])
        fw.op("dve", lambda e: e.tensor_tensor(out=t2[:], in0=nr[:], in1=aim[:], op=ALU.mult), reads=[tb, tpar, tpw], writes=[tpw])
        fw.op("dve", lambda e: e.tensor_tensor(out=t1[:], in0=t1[:], in1=t2[:], op=ALU.subtract), reads=[tpw], writes=[tpw])
        fw.op("dve", lambda e: e.tensor_tensor(out=bi[:], in0=t1[:], in1=den[:], op=ALU.mult), reads=[tpw, tb], writes=[tb])
        fw.op("dve", lambda e: e.tensor_scalar(out=nbi[:], in0=bi[:], scalar1=-1.0, scalar2=None, op0=ALU.mult), reads=[tb], writes=[tb])
        BTb = P.sb([128, 2, 2, 8, 128], BF16)
        CTb = P.sb([128, 2, 2, 8, 128], BF16)
        tBT = Tok()
        tCT = Tok()
        with ExitStack() as es2:
            P2 = Pool_(fw, es2)
            BTf = P2.sb([128, 2, 2, 8, 128], F32)
            CTf = P2.sb([128, 2, 2, 8, 128], F32)
            CT2 = P2.sb([128, 2, 2, 8, 128], F32)
            tf1, tf2 = Tok(), Tok()
            fw.op("pool", lambda e: e.memset(BTf[:], 0.0), writes=[tf1])
            fw.op("pool", lambda e: e.memset(CTf[:], 0.0), writes=[tf2])
            for di in range(2):
                for ri, (bsrc, csrc) in enumerate(((b_re, c_re), (b_im, c_im))):
                    for g in range(16):
                        s, g2 = g // 2, g % 2
                        r0 = (g % 8) * 16
                        fw.dma("sp", BTf[r0:r0 + 16, di, ri, s, g2 * 64:(g2 + 1) * 64], bsrc[di, g].rearrange("p h -> h p"),
                               writes=[tf1], allow_slow_non_contiguous=True)
                        fw.dma("sp", CTf[g2 * 64:(g2 + 1) * 64, di, ri, s, r0:r0 + 16], csrc[di, g].rearrange("h p -> p h"),
                               writes=[tf2], allow_slow_non_contiguous=True)
            fw.op("act", lambda e: e.copy(out=BTb[:], in_=BTf[:]), reads=[tf1], writes=[tBT])
            bsh = [128, 2, 8, 128]
            brb = br[:].unsqueeze(3).to_broadcast(bsh)
            nbib = nbi[:].unsqueeze(3).to_broadcast(bsh)
            tc2 = Tok()
            fw.op("dve", lambda e: e.tensor_tensor(out=CT2[:, :, 0], in0=CTf[:, :, 0], in1=brb, op=ALU.mult), reads=[tf2, tb], writes=[tc2])
            fw.op("pool", lambda e: e.tensor_tensor(out=CT2[:, :, 1], in0=CTf[:, :, 1], in1=nbib, op=ALU.mult), reads=[tf2, tb], writes=[tc2])
            fw.op("dve", lambda e: e.tensor_tensor(out=CT2[:, :, 0], in0=CT2[:, :, 0], in1=CT2[:, :, 1], op=ALU.add), reads=[tc2], writes=[tc2])
            fw.op("act", lambda e: e.copy(out=CTb[:, :, 0], in_=CT2[:, :, 0]), reads=[tc2], writes=[tCT])
            fw.op("dve", lambda e: e.tensor_tensor(out=CT2[:, :, 0], in0=CTf[:, :, 0], in1=nbib, op=ALU.mult), reads=[tf2, tb, tCT, tc2], writes=[tc2])
            fw.op("pool", lambda e: e.tensor_tensor(out=CT2[:, :, 1], in0=CTf[:, :, 1], in1=brb, op=ALU.mult), reads=[tf2, tb, tc2], writes=[tc2])
            fw.op("dve", lambda e: e.tensor_tensor(out=CT2[:, :, 0], in0=CT2[:, :, 0], in1=CT2[:, :, 1], op=ALU.subtract), reads=[tc2], writes=[tc2])
            fw.op("act", lambda e: e.copy(out=CTb[:, :, 1], in_=CT2[:, :, 0]), reads=[tc2], writes=[tCT])
            fw.barrier()
        ub = P.sb([128, 2, TE], BF16)
        tub = Tok()
        for ct in range(2):
            fw.dma("pool", ub[:, ct, :], ZT[ct * 128:(ct + 1) * 128, :], writes=[tub])
        dk = P.sb([128, 2], F32)
        tdk = Tok()
        fw.dma("sp", dk[:], dsk.rearrange("(c p) -> p c", p=128), writes=[tdk], allow_slow_non_contiguous=True)
        y = P.sb([128, 2, T], F32)
        ty = Tok()
        for ct in range(2):
            fw.dma("sp", y[:, ct, :], ZT[ct * 128:(ct + 1) * 128, 0:T], writes=[ty])
        for ct in range(2):
            fw.op("pool", lambda e, ct=ct: e.tensor_scalar(out=y[:, ct, :], in0=y[:, ct, :], scalar1=dk[:, ct:ct + 1], scalar2=None, op0=ALU.mult),
                  reads=[ty, tdk], writes=[ty])
        Xs = Rot([P.sb([128, 2, N], F32) for _ in range(4)])
        xbs = Rot([P.sb([128, 2, N], BF16) for _ in range(2)])
        psr = Rot([P.ps() for _ in range(4)])
        psy = Rot([P.ps() for _ in range(2)])
        tmpy = Rot([P.sb([128, 512], F32) for _ in range(2)])
        CH = [(0, 512), (512, 1024), (1024, 1536), (1536, 2048), (2048, 2304)]

        def prep(di, s):
            off = 0 if di == 0 else 256
            ct = s // 4
            X, tX = Xs.next()
            for ri in range(2):
                for ci, (c0, c1) in enumerate(CH):
                    n = c1 - c0
                    ps, tps = psr.next()
                    fw.op("pe", lambda e, ps=ps, ri=ri, c0=c0, c1=c1, n=n: e.matmul(
                        ps[:, :n], lhsT=BTb[:, di, ri, s, :], rhs=ub[:, ct, off + c0:off + c1], start=True, stop=True),
                        reads=[tBT, tub], writes=[tps])
                    fw.op("act", lambda e, ps=ps, ri=ri, c0=c0, c1=c1, n=n: e.copy(out=X[:, ri, c0:c1], in_=ps[:, :n]),
                          reads=[tps], writes=[tX])
            return X, tX

        def scan_gen(di, s, X, tX):
            def cstep(w_re, w_im, r_re, r_im, k):
                pr = PW[:, k, 0, di, s:s + 1]
                pi = PW[:, k, 1, di, s:s + 1]
                npi = PW[:, k, 2, di, s:s + 1]
                for (o, i0, sc) in ((w_re, r_re, pr), (w_re, r_im, npi), (w_im, r_im, pr), (w_im, r_re, pi)):
                    fw.op("dve", lambda e, o=o, i0=i0, sc=sc: e.scalar_tensor_tensor(out=o, in0=i0, scalar=sc, in1=o, op0=ALU.mult, op1=ALU.add),
                          reads=[tX, tpw], writes=[tX])
                    yield
            for k in range(8):
                st_ = 1 << k
                Xv = [X[:, ri, :].rearrange("p (m c) -> p m c", c=2 * st_) for ri in range(2)]
                if di == 0:
                    yield from cstep(Xv[0][:, :, 2 * st_ - 1], Xv[1][:, :, 2 * st_ - 1], Xv[0][:, :, st_ - 1], Xv[1][:, :, st_ - 1], k)
                else:
                    yield from cstep(Xv[0][:, :, 0], Xv[1][:, :, 0], Xv[0][:, :, st_], Xv[1][:, :, st_], k)
            for i in (range(1, 9) if di == 0 else range(7, -1, -1)):
                if di == 0:
                    w, r = 256 * i + 255, 256 * (i - 1) + 255
                else:
                    w, r = 256 * i, 256 * (i + 1)
                yield from cstep(X[:, 0, w:w + 1], X[:, 1, w:w + 1], X[:, 0, r:r + 1], X[:, 1, r:r + 1], 8)
            for k in range(7, -1, -1):
                st_ = 1 << k
                Xv = [X[:, ri, :].rearrange("p (m c) -> p m c", c=2 * st_) for ri in range(2)]
                if di == 0:
                    yield from cstep(Xv[0][:, 1:, st_ - 1], Xv[1][:, 1:, st_ - 1], Xv[0][:, :-1, 2 * st_ - 1], Xv[1][:, :-1, 2 * st_ - 1], k)
                else:
                    yield from cstep(Xv[0][:, :-1, st_], Xv[1][:, :-1, st_], Xv[0][:, 1:, 0], Xv[1][:, 1:, 0], k)

        def fin(di, s, X, tX):
            ct = s // 4
            xb, txb = xbs.next()
            fw.op("act", lambda e: e.copy(out=xb[:], in_=X[:]), reads=[tX], writes=[txb])
            for (c0, c1) in CH:
                n = c1 - c0
                if di == 0:
                    y0 = c0
                else:
                    y0 = c0 + 256 if c0 < 2048 else 0
                ps, tps = psy.next()
                for ri in range(2):
                    fw.op("pe", lambda e, ps=ps, ri=ri, c0=c0, c1=c1, n=n: e.matmul(
                        ps[:, :n], lhsT=CTb[:, di, ri, s, :], rhs=xb[:, ri, c0:c1], start=(ri == 0), stop=(ri == 1)),
                        reads=[tCT, txb], writes=[tps])
                tm, ttm = tmpy.next()
                fw.op("act", lambda e, tm=tm, ps=ps, n=n: e.copy(out=tm[:, :n], in_=ps[:, :n]), reads=[tps], writes=[ttm])
                fw.op("pool", lambda e, tm=tm, y0=y0, n=n: e.tensor_tensor(out=y[:, ct, y0:y0 + n], in0=y[:, ct, y0:y0 + n], in1=tm[:, :n], op=ALU.add),
                      reads=[ttm, ty], writes=[ty])

        pairs = [(di, s0, s0 + 1) for di in range(2) for s0 in range(0, 8, 2)]
        nxt = [prep(pairs[0][0], pairs[0][1]), prep(pairs[0][0], pairs[0][2])]
        for pi_, (di, s0, s1) in enumerate(pairs):
            cur = nxt
            if pi_ + 1 < len(pairs):
                d2, a0, a1 = pairs[pi_ + 1]
                nxt = [prep(d2, a0), prep(d2, a1)]
            gens = [scan_gen(di, s0, *cur[0]), scan_gen(di, s1, *cur[1])]
            alive = [True, True]
            while any(alive):
                for gi in range(2):
                    if alive[gi]:
                        try:
                            next(gens[gi])
                        except StopIteration:
                            alive[gi] = False
            fin(di, s0, *cur[0])
            fin(di, s1, *cur[1])
        tg = Tok()
        gw = load_w_bf16(fw, P, glu_w, 256, 256, tg, "gluw")
        gb = P.sb([128, 2], F32)
        og = P.sb([128, 2], F32)
        fw.dma("sp", gb[:], glu_b.rearrange("(c p) -> p c", p=128), writes=[tg], allow_slow_non_contiguous=True)
        fw.dma("sp", og[:], out_g.rearrange("(c p) -> p c", p=128), writes=[tg], allow_slow_non_contiguous=True)
        ones = P.sb([128, 128], BF16)
        eps = P.sb([128, 1], F32)
        fw.op("pool", lambda e: e.memset(ones[:], 1.0), writes=[tg])
        fw.op("pool", lambda e: e.memset(eps[:], RMS_EPS), writes=[tg])
        a32 = P.sb([128, 2, 512], F32)
        ab = P.sb([128, 2, 512], BF16)
        w1 = P.sb([128, 2, 512], F32)
        w2 = P.sb([128, 2, 512], F32)
        sqb = P.sb([128, 2, 512], BF16)
        rs = P.sb([128, 512], F32)
        ta, tw1, trs = Tok(), Tok(), Tok()
        stg = Rot([P.sb([128, 512], F32) for _ in range(2)])
        tout = Tok()
        C1 = math.sqrt(2.0 / math.pi)
        for (lo, hi, ic) in SEGS:
            n = hi - lo
            yv = y[:, :, lo:hi]
            fw.op("act", lambda e, yv=yv, n=n: e.activation(out=w1[:, :, :n], in_=yv, func=AF.Square), reads=[ty], writes=[tw1])
            fw.op("dve", lambda e, n=n: e.tensor_scalar(out=w1[:, :, :n], in0=w1[:, :, :n], scalar1=0.044715 * C1, scalar2=C1, op0=ALU.mult, op1=ALU.add),
                  reads=[tw1], writes=[tw1])
            fw.op("dve", lambda e, yv=yv, n=n: e.tensor_tensor(out=w1[:, :, :n], in0=w1[:, :, :n], in1=yv, op=ALU.mult), reads=[tw1, ty], writes=[tw1])
            fw.op("act", lambda e, n=n: e.activation(out=w1[:, :, :n], in_=w1[:, :, :n], func=AF.Tanh), reads=[tw1], writes=[tw1])
            fw.op("dve", lambda e, n=n: e.tensor_scalar(out=w1[:, :, :n], in0=w1[:, :, :n], scalar1=0.5, scalar2=0.5, op0=ALU.mult, op1=ALU.add),
                  reads=[tw1], writes=[tw1])
            fw.op("dve", lambda e, yv=yv, n=n: e.tensor_tensor(out=a32[:, :, :n], in0=w1[:, :, :n], in1=yv, op=ALU.mult), reads=[tw1, ty], writes=[ta])
            fw.op("act", lambda e, n=n: e.copy(out=ab[:, :, :n], in_=a32[:, :, :n]), reads=[ta], writes=[ta])
            for nt in range(2):
                ps, tps = psr.next()
                for k in range(2):
                    fw.op("pe", lambda e, ps=ps, k=k, nt=nt, n=n: e.matmul(ps[:, :n], lhsT=gw[:, k, nt * 128:(nt + 1) * 128], rhs=ab[:, k, :n],
                                                                      start=(k == 0), stop=(k == 1)), reads=[tg, ta], writes=[tps])
                fw.op("act", lambda e, ps=ps, nt=nt, n=n: e.activation(out=w2[:, nt, :n], in_=ps[:, :n], func=AF.Sigmoid, bias=gb[:, nt:nt + 1]),
                      reads=[tps, tg], writes=[tw1])
            fw.op("dve", lambda e, n=n: e.tensor_tensor(out=w2[:, :, :n], in0=w2[:, :, :n], in1=a32[:, :, :n], op=ALU.mult), reads=[tw1, ta], writes=[tw1])
            fw.op("act", lambda e, n=n: e.activation(out=sqb[:, :, :n], in_=w2[:, :, :n], func=AF.Square), reads=[tw1], writes=[tw1])
            ps, tps = psr.next()
            for k in range(2):
                fw.op("pe", lambda e, ps=ps, k=k, n=n: e.matmul(ps[:, :n], lhsT=ones[:], rhs=sqb[:, k, :n], start=(k == 0), stop=(k == 1)),
                      reads=[tw1, tg], writes=[tps])
            rstd_from_ps(fw, rs, trs, ps, tps, n, 1.0 / 256, eps[:, 0:1], tg)
            for k in range(2):
                sg, tsg = stg.next()
                fw.op("dve", lambda e, sg=sg, k=k, n=n: e.scalar_tensor_tensor(out=sg[:, :n], in0=w2[:, k, :n], scalar=og[:, k:k + 1], in1=rs[:, :n],
                                                                             op0=ALU.mult, op1=ALU.mult), reads=[tw1, trs, tg], writes=[tsg])
                fw.dma("sp", S5T[k * 128:(k + 1) * 128, lo:hi], sg[:, :n], reads=[tsg], writes=[Tok()])
        fw.barrier()


def gen_p2(fw, io):
    ZT = io("ZT", [2048, TE], F32, "in")
    a_re = io("s5_a_re", [2, 16, 64], F32, "in")
    a_im = io("s5_a_im", [2, 16, 64], F32, "in")
    lstep = io("s5_log_step", [2, 16], F32, "in")
    b_re = io("s5_b_re", [2, 16, 64, 16], F32, "in")
    b_im = io("s5_b_im", [2, 16, 64, 16], F32, "in")
    c_re = io("s5_c_re", [2, 16, 16, 64], F32, "in")
    c_im = io("s5_c_im", [2, 16, 16, 64], F32, "in")
    dsk = io("s5_d", [256], F32, "in")
    glu_w = io("s5_glu_w", [256, 256], F32, "in")
    glu_b = io("s5_glu_b", [256], F32, "in")
    out_g = io("s5_out_g", [256], F32, "in")
    S5T = io("S5T", [256, T], F32, "out")
    N = T
    with ExitStack() as es:
        P = Pool_(fw, es)
        are = P.sb([128, 2, 8], F32)
        aim = P.sb([128, 2, 8], F32)
        lst = P.sb([128, 2, 8], F32)
        tpar = Tok()
        for di in range(2):
            fw.dma("sp", are[:, di, :], a_re[di].rearrange("(s g) p -> (g p) s", g=2), writes=[tpar], allow_slow_non_contiguous=True)
            fw.dma("sp", aim[:, di, :], a_im[di].rearrange("(s g) p -> (g p) s", g=2), writes=[tpar], allow_slow_non_contiguous=True)
            for g2 in range(2):
                fw.dma("sp", lst[g2 * 64:(g2 + 1) * 64, di:di + 1, :],
                       lstep[di].rearrange("(s g) -> g s", g=2)[g2:g2 + 1, :].partition_broadcast(64), writes=[tpar],
                       allow_slow_non_contiguous=True)
        sh = [128, 2, 8]
        step = P.sb(sh, F32)
        fw.op("act", lambda e: e.activation(out=step[:], in_=lst[:], func=AF.Exp), reads=[tpar], writes=[tpar])
        er = P.sb(sh, F32)
        th = P.sb(sh, F32)
        fw.op("dve", lambda e: e.tensor_tensor(out=er[:], in0=are[:], in1=step[:], op=ALU.mult), reads=[tpar], writes=[tpar])
        fw.op("act", lambda e: e.activation(out=er[:], in_=er[:], func=AF.Exp), reads=[tpar], writes=[tpar])
        fw.op("dve", lambda e: e.tensor_tensor(out=th[:], in0=aim[:], in1=step[:], op=ALU.mult), reads=[tpar], writes=[tpar])
        sn = P.sb(sh, F32)
        cs = P.sb(sh, F32)
        ttrig = Tok()
        sin_reduced(fw, P, sn[:], th[:], tpar, sh, 0.0, ttrig)
        sin_reduced(fw, P, cs[:], th[:], tpar, sh, math.pi / 2, ttrig)
        PW = P.sb([128, 9, 3, 2, 8], F32)
        tpw = Tok()
        fw.op("dve", lambda e: e.tensor_tensor(out=PW[:, 0, 0], in0=er[:], in1=cs[:], op=ALU.mult), reads=[tpar, ttrig], writes=[tpw])
        fw.op("dve", lambda e: e.tensor_tensor(out=PW[:, 0, 1], in0=er[:], in1=sn[:], op=ALU.mult), reads=[tpar, ttrig], writes=[tpw])
        t1 = P.sb(sh, F32)
        t2 = P.sb(sh, F32)
        for k in range(9):
            fw.op("dve", lambda e, k=k: e.tensor_scalar(out=PW[:, k, 2], in0=PW[:, k, 1], scalar1=-1.0, scalar2=None, op0=ALU.mult),
                  reads=[tpw], writes=[tpw])
            if k == 8:
                break
            fw.op("dve", lambda e, k=k: e.tensor_tensor(out=t1[:], in0=PW[:, k, 0], in1=PW[:, k, 0], op=ALU.mult), reads=[tpw], writes=[tpw])
            fw.op("dve", lambda e, k=k: e.tensor_tensor(out=t2[:], in0=PW[:, k, 1], in1=PW[:, k, 1], op=ALU.mult), reads=[tpw], writes=[tpw])
            fw.op("dve", lambda e, k=k: e.tensor_tensor(out=PW[:, k + 1, 0], in0=t1[:], in1=t2[:], op=ALU.subtract), reads=[tpw], writes=[tpw])
            fw.op("dve", lambda e, k=k: e.scalar_tensor_tensor(out=PW[:, k + 1, 1], in0=PW[:, k, 0], scalar=2.0, in1=PW[:, k, 1],
                                                               op0=ALU.mult, op1=ALU.mult), reads=[tpw], writes=[tpw])
        br = P.sb(sh, F32)
        bi = P.sb(sh, F32)
        nbi = P.sb(sh, F32)
        den = P.sb(sh, F32)
        nr = P.sb(sh, F32)
        tb = Tok()
        fw.op("dve", lambda e: e.tensor_tensor(out=den[:], in0=are[:], in1=are[:], op=ALU.mult), reads=[tpar], writes=[tb])
        fw.op("dve", lambda e: e.tensor_tensor(out=t1[:], in0=aim[:], in1=aim[:], op=ALU.mult), reads=[tpar, tpw], writes=[tpw])
        fw.op("dve", lambda e: e.tensor_tensor(out=den[:], in0=den[:], in1=t1[:], op=ALU.add), reads=[tb, tpw], writes=[tb])
        fw.op("dve", lambda e: e.reciprocal(out=den[:], in_=den[:]), reads=[tb], writes=[tb])
        fw.op("dve", lambda e: e.tensor_scalar(out=nr[:], in0=PW[:, 0, 0], scalar1=-1.0, scalar2=None, op0=ALU.add), reads=[tpw], writes=[tb])
        fw.op("dve", lambda e: e.tensor_tensor(out=t1[:], in0=nr[:], in1=are[:], op=ALU.mult), reads=[tb, tpar, tpw], writes=[tpw])
        fw.op("dve", lambda e: e.tensor_tensor(out=t2[:], in0=PW[:, 0, 1], in1=aim[:], op=ALU.mult), reads=[tpw, tpar], writes=[tpw])
        fw.op("dve", lambda e: e.tensor_tensor(out=t1[:], in0=t1[:], in1=t2[:], op=ALU.add), reads=[tpw], writes=[tpw])
        fw.op("dve", lambda e: e.tensor_tensor(out=br[:], in0=t1[:], in1=den[:], op=ALU.mult), reads=[tpw, tb], writes=[tb])
        fw.op("dve", lambda e: e.tensor_tensor(out=t1[:], in0=PW[:, 0, 1], in1=are[:], op=ALU.mult), reads=[tpw, tpar, tb], writes=[tpw])
        fw.op("dve", lambda e: e.tensor_tensor(out=t2[:], in0=nr[:], in1=aim[:], op=ALU.mult), reads=[tb, tpar, tpw], writes=[tpw])
        fw.op("dve", lambda e: e.tensor_tensor(out=t1[:], in0=t1[:], in1=t2[:], op=ALU.subtract), reads=[tpw], writes=[tpw])
        fw.op("dve", lambda e: e.tensor_tensor(out=bi[:], in0=t1[:], in1=den[:], op=ALU.mult), reads=[tpw, tb], writes=[tb])
        fw.op("dve", lambda e: e.tensor_scalar(out=nbi[:], in0=bi[:], scalar1=-1.0, scalar2=None, op0=ALU.mult), reads=[tb], writes=[tb])
        BTb = P.sb([128, 2, 2, 8, 128], BF16)
        CTb = P.sb([128, 2, 2, 8, 128], BF16)
        tBT = Tok()
        tCT = Tok()
        with ExitStack() as es2:
            P2 = Pool_(fw, es2)
            BTf = P2.sb([128, 2, 2, 8, 128], F32)
            CTf = P2.sb([128, 2, 2, 8, 128], F32)
            CT2 = P2.sb([128, 2, 2, 8, 128], F32)
            tf1, tf2 = Tok(), Tok()
            fw.op("pool", lambda e: e.memset(BTf[:], 0.0), writes=[tf1])
            fw.op("pool", lambda e: e.memset(CTf[:], 0.0), writes=[tf2])
            for di in range(2):
                for ri, (bsrc, csrc) in enumerate(((b_re, c_re), (b_im, c_im))):
                    for g in range(16):
                        s, g2 = g // 2, g % 2
                        r0 = (g % 8) * 16
                        fw.dma("sp", BTf[r0:r0 + 16, di, ri, s, g2 * 64:(g2 + 1) * 64], bsrc[di, g].rearrange("p h -> h p"),
                               writes=[tf1], allow_slow_non_contiguous=True)
                        fw.dma("sp", CTf[g2 * 64:(g2 + 1) * 64, di, ri, s, r0:r0 + 16], csrc[di, g].rearrange("h p -> p h"),
                               writes=[tf2], allow_slow_non_contiguous=True)
            fw.op("act", lambda e: e.copy(out=BTb[:], in_=BTf[:]), reads=[tf1], writes=[tBT])
            bsh = [128, 2, 8, 128]
            brb = br[:].unsqueeze(3).to_broadcast(bsh)
            nbib = nbi[:].unsqueeze(3).to_broadcast(bsh)
            tc2 = Tok()
            fw.op("dve", lambda e: e.tensor_tensor(out=CT2[:, :, 0], in0=CTf[:, :, 0], in1=brb, op=ALU.mult), reads=[tf2, tb], writes=[tc2])
            fw.op("pool", lambda e: e.tensor_tensor(out=CT2[:, :, 1], in0=CTf[:, :, 1], in1=nbib, op=ALU.mult), reads=[tf2, tb], writes=[tc2])
            fw.op("dve", lambda e: e.tensor_tensor(out=CT2[:, :, 0], in0=CT2[:, :, 0], in1=CT2[:, :, 1], op=ALU.add), reads=[tc2], writes=[tc2])
            fw.op("act", lambda e: e.copy(out=CTb[:, :, 0], in_=CT2[:, :, 0]), reads=[tc2], writes=[tCT])
            fw.op("dve", lambda e: e.tensor_tensor(out=CT2[:, :, 0], in0=CTf[:, :, 0], in1=nbib, op=ALU.mult), reads=[tf2, tb, tCT, tc2], writes=[tc2])
            fw.op("pool", lambda e: e.tensor_tensor(out=CT2[:, :, 1], in0=CTf[:, :, 1], in1=brb, op=ALU.mult), reads=[tf2, tb, tc2], writes=[tc2])
            fw.op("dve", lambda e: e.tensor_tensor(out=CT2[:, :, 0], in0=CT2[:, :, 0], in1=CT2[:, :, 1], op=ALU.subtract), reads=[tc2], writes=[tc2])
            fw.op("act", lambda e: e.copy(out=CTb[:, :, 1], in_=CT2[:, :, 0]), reads=[tc2], writes=[tCT])
            fw.barrier()
        ub = P.sb([128, 2, TE], BF16)
        tub = Tok()
        for ct in range(2):
            fw.dma("pool", ub[:, ct, :], ZT[ct * 128:(ct + 1) * 128, :], writes=[tub])
        dk = P.sb([128, 2], F32)
        tdk = Tok()
        fw.dma("sp", dk[:], dsk.rearrange("(c p) -> p c", p=128), writes=[tdk], allow_slow_non_contiguous=True)
        y = P.sb([128, 2, T], F32)
        ty = Tok()
        for ct in range(2):
            fw.dma("sp", y[:, ct, :], ZT[ct * 128:(ct + 1) * 128, 0:T], writes=[ty])
        for ct in range(2):
            fw.op("pool", lambda e, ct=ct: e.tensor_scalar(out=y[:, ct, :], in0=y[:, ct, :], scalar1=dk[:, ct:ct + 1], scalar2=None, op0=ALU.mult),
                  reads=[ty, tdk], writes=[ty])
        Xs = Rot([P.sb([128, 2, N], F32) for _ in range(2)])
        xbs = Rot([P.sb([128, 2, N], BF16) for _ in range(2)])
        psr = Rot([P.ps() for _ in range(2)])
        psy = Rot([P.ps() for _ in range(1)])
        tmpy = Rot([P.sb([128, 512], F32) for _ in range(2)])
        CH = [(0, 512), (512, 1024), (1024, 1536), (1536, 2048), (2048, 2304)]

        def prep(di, s):
            off = 0 if di == 0 else 256
            ct = s // 4
            X, tX = Xs.next()
            for ri in range(2):
                for ci, (c0, c1) in enumerate(CH):
                    n = c1 - c0
                    ps, tps = psr.next()
                    fw.op("pe", lambda e, ps=ps, ri=ri, c0=c0, c1=c1, n=n: e.matmul(
                        ps[:, :n], lhsT=BTb[:, di, ri, s, :], rhs=ub[:, ct, off + c0:off + c1], start=True, stop=True),
                        reads=[tBT, tub], writes=[tps])
                    fw.op("act", lambda e, ps=ps, ri=ri, c0=c0, c1=c1, n=n: e.copy(out=X[:, ri, c0:c1], in_=ps[:, :n]),
                          reads=[tps], writes=[tX])
            return X, tX

        def scan_gen(di, s, X, tX):
            def cstep(w_re, w_im, r_re, r_im, k):
                pr = PW[:, k, 0, di, s:s + 1]
                pi = PW[:, k, 1, di, s:s + 1]
                npi = PW[:, k, 2, di, s:s + 1]
                for (o, i0, sc) in ((w_re, r_re, pr), (w_re, r_im, npi), (w_im, r_im, pr), (w_im, r_re, pi)):
                    fw.op("dve", lambda e, o=o, i0=i0, sc=sc: e.scalar_tensor_tensor(out=o, in0=i0, scalar=sc, in1=o, op0=ALU.mult, op1=ALU.add),
                          reads=[tX, tpw], writes=[tX])
                    yield
            for k in range(8):
                st_ = 1 << k
                Xv = [X[:, ri, :].rearrange("p (m c) -> p m c", c=2 * st_) for ri in range(2)]
                if di == 0:
                    yield from cstep(Xv[0][:, :, 2 * st_ - 1], Xv[1][:, :, 2 * st_ - 1], Xv[0][:, :, st_ - 1], Xv[1][:, :, st_ - 1], k)
                else:
                    yield from cstep(Xv[0][:, :, 0], Xv[1][:, :, 0], Xv[0][:, :, st_], Xv[1][:, :, st_], k)
            for i in (range(1, 9) if di == 0 else range(7, -1, -1)):
                if di == 0:
                    w, r = 256 * i + 255, 256 * (i - 1) + 255
                else:
                    w, r = 256 * i, 256 * (i + 1)
                yield from cstep(X[:, 0, w:w + 1], X[:, 1, w:w + 1], X[:, 0, r:r + 1], X[:, 1, r:r + 1], 8)
            for k in range(7, -1, -1):
                st_ = 1 << k
                Xv = [X[:, ri, :].rearrange("p (m c) -> p m c", c=2 * st_) for ri in range(2)]
                if di == 0:
                    yield from cstep(Xv[0][:, 1:, st_ - 1], Xv[1][:, 1:, st_ - 1], Xv[0][:, :-1, 2 * st_ - 1], Xv[1][:, :-1, 2 * st_ - 1], k)
                else:
                    yield from cstep(Xv[0][:, :-1, st_], Xv[1][:, :-1, st_], Xv[0][:, 1:, 0], Xv[1][:, 1:, 0], k)

        def fin(di, s, X, tX):
            ct = s // 4
            xb, txb = xbs.next()
            fw.op("act", lambda e: e.copy(out=xb[:], in_=X[:]), reads=[tX], writes=[txb])
            for (c0, c1) in CH:
                n = c1 - c0
                if di == 0:
                    y0 = c0
                else:
                    y0 = c0 + 256 if c0 < 2048 else 0
                ps, tps = psy.next()
                for ri in range(2):
                    fw.op("pe", lambda e, ps=ps, ri=ri, c0=c0, c1=c1, n=n: e.matmul(
                        ps[:, :n], lhsT=CTb[:, di, ri, s, :], rhs=xb[:, ri, c0:c1], start=(ri == 0), stop=(ri == 1)),
                        reads=[tCT, txb], writes=[tps])
                tm, ttm = tmpy.next()
                fw.op("act", lambda e, tm=tm, ps=ps, n=n: e.copy(out=tm[:, :n], in_=ps[:, :n]), reads=[tps], writes=[ttm])
                fw.op("pool", lambda e, tm=tm, y0=y0, n=n: e.tensor_tensor(out=y[:, ct, y0:y0 + n], in0=y[:, ct, y0:y0 + n], in1=tm[:, :n], op=ALU.add),
                      reads=[ttm, ty], writes=[ty])

        pairs = [(di, s0, s0 + 1) for di in range(2) for s0 in range(0, 8, 2)]
        yield "core"
        for pi_, (di, s0, s1) in enumerate(pairs):
            cur = [prep(di, s0), prep(di, s1)]
            yield
            gens = [scan_gen(di, s0, *cur[0]), scan_gen(di, s1, *cur[1])]
            alive = [True, True]
            while any(alive):
                for gi in range(2):
                    if alive[gi]:
                        try:
                            next(gens[gi])
                            yield
                        except StopIteration:
                            alive[gi] = False
            fin(di, s0, *cur[0])
            yield
            fin(di, s1, *cur[1])
            yield
        yield "fin"
        tg = Tok()
        gw = load_w_bf16(fw, P, glu_w, 256, 256, tg, "gluw")
        gb = P.sb([128, 2], F32)
        og = P.sb([128, 2], F32)
        fw.dma("sp", gb[:], glu_b.rearrange("(c p) -> p c", p=128), writes=[tg], allow_slow_non_contiguous=True)
        fw.dma("sp", og[:], out_g.rearrange("(c p) -> p c", p=128), writes=[tg], allow_slow_non_contiguous=True)
        ones = P.sb([128, 128], BF16)
        eps = P.sb([128, 1], F32)
        fw.op("pool", lambda e: e.memset(ones[:], 1.0), writes=[tg])
        fw.op("pool", lambda e: e.memset(eps[:], RMS_EPS), writes=[tg])
        a32 = P.sb([128, 2, 512], F32)
        ab = P.sb([128, 2, 512], BF16)
        w1 = P.sb([128, 2, 512], F32)
        w2 = P.sb([128, 2, 512], F32)
        sqb = P.sb([128, 2, 512], BF16)
        rs = P.sb([128, 512], F32)
        ta, tw1, trs = Tok(), Tok(), Tok()
        stg = Rot([P.sb([128, 512], F32) for _ in range(2)])
        tout = Tok()
        C1 = math.sqrt(2.0 / math.pi)
        for (lo, hi, ic) in SEGS:
            n = hi - lo
            yv = y[:, :, lo:hi]
            fw.op("act", lambda e, yv=yv, n=n: e.activation(out=w1[:, :, :n], in_=yv, func=AF.Square), reads=[ty], writes=[tw1])
            fw.op("dve", lambda e, n=n: e.tensor_scalar(out=w1[:, :, :n], in0=w1[:, :, :n], scalar1=0.044715 * C1, scalar2=C1, op0=ALU.mult, op1=ALU.add),
                  reads=[tw1], writes=[tw1])
            fw.op("dve", lambda e, yv=yv, n=n: e.tensor_tensor(out=w1[:, :, :n], in0=w1[:, :, :n], in1=yv, op=ALU.mult), reads=[tw1, ty], writes=[tw1])
            fw.op("act", lambda e, n=n: e.activation(out=w1[:, :, :n], in_=w1[:, :, :n], func=AF.Tanh), reads=[tw1], writes=[tw1])
            fw.op("dve", lambda e, n=n: e.tensor_scalar(out=w1[:, :, :n], in0=w1[:, :, :n], scalar1=0.5, scalar2=0.5, op0=ALU.mult, op1=ALU.add),
                  reads=[tw1], writes=[tw1])
            fw.op("dve", lambda e, yv=yv, n=n: e.tensor_tensor(out=a32[:, :, :n], in0=w1[:, :, :n], in1=yv, op=ALU.mult), reads=[tw1, ty], writes=[ta])
            fw.op("act", lambda e, n=n: e.copy(out=ab[:, :, :n], in_=a32[:, :, :n]), reads=[ta], writes=[ta])
            for nt in range(2):
                ps, tps = psr.next()
                for k in range(2):
                    fw.op("pe", lambda e, ps=ps, k=k, nt=nt, n=n: e.matmul(ps[:, :n], lhsT=gw[:, k, nt * 128:(nt + 1) * 128], rhs=ab[:, k, :n],
                                                                      start=(k == 0), stop=(k == 1)), reads=[tg, ta], writes=[tps])
                fw.op("act", lambda e, ps=ps, nt=nt, n=n: e.activation(out=w2[:, nt, :n], in_=ps[:, :n], func=AF.Sigmoid, bias=gb[:, nt:nt + 1]),
                      reads=[tps, tg], writes=[tw1])
            fw.op("dve", lambda e, n=n: e.tensor_tensor(out=w2[:, :, :n], in0=w2[:, :, :n], in1=a32[:, :, :n], op=ALU.mult), reads=[tw1, ta], writes=[tw1])
            fw.op("act", lambda e, n=n: e.activation(out=sqb[:, :, :n], in_=w2[:, :, :n], func=AF.Square), reads=[tw1], writes=[tw1])
            ps, tps = psr.next()
            for k in range(2):
                fw.op("pe", lambda e, ps=ps, k=k, n=n: e.matmul(ps[:, :n], lhsT=ones[:], rhs=sqb[:, k, :n], start=(k == 0), stop=(k == 1)),
                      reads=[tw1, tg], writes=[tps])
            rstd_from_ps(fw, rs, trs, ps, tps, n, 1.0 / 256, eps[:, 0:1], tg)
            for k in range(2):
                sg, tsg = stg.next()
                fw.op("dve", lambda e, sg=sg, k=k, n=n: e.scalar_tensor_tensor(out=sg[:, :n], in0=w2[:, k, :n], scalar=og[:, k:k + 1], in1=rs[:, :n],
                                                                             op0=ALU.mult, op1=ALU.mult), reads=[tw1, trs, tg], writes=[tsg])
                fw.dma("sp", S5T[k * 128:(k + 1) * 128, lo:hi], sg[:, :n], reads=[tsg], writes=[Tok()])
        fw.barrier()


def gen_p3(fw, io):
    ZT = io("ZT", [2048, TE], F32, "in")
    VT = io("VT", [T, 128], F32, "in")
    qn_g = io("att_qn_g", [64], F32, "in")
    kn_g = io("att_kn_g", [64], F32, "in")
    og = io("att_out_g", [512], F32, "in")
    POS = io("POS", [128, NLAT], F32, "in")
    CST = io("CST", [128, 260], F32, "in")
    ATT = io("ATT_T", [512, T], F32, "out")
    with ExitStack() as es:
        P = Pool_(fw, es)
        cst = P.sb([128, 260], F32)
        tc = Tok()
        fw.dma("sp", cst[:], CST, writes=[tc])
        permb = P.sb([128, 128], BF16)
        bdb = P.sb([128, 128], BF16)
        fw.op("dve", lambda e: e.tensor_copy(out=permb[:], in_=cst[:, 1:129]), reads=[tc], writes=[tc])
        fw.op("dve", lambda e: e.tensor_copy(out=bdb[:], in_=cst[:, 129:257]), reads=[tc], writes=[tc])
        eps = P.sb([128, 2], F32)
        teps = Tok()
        fw.op("pool", lambda e: e.memset(eps[:, 0:1], RMS_EPS), writes=[teps])
        fw.op("pool", lambda e: e.memset(eps[:, 1:2], 0.0), writes=[teps])
        gq = P.sb([128, 2], F32)
        tg = Tok()
        for h in range(2):
            fw.dma("sp", gq[h * 64:(h + 1) * 64, 0:1], qn_g.rearrange("(p o) -> p o", o=1), writes=[tg])
            fw.dma("sp", gq[h * 64:(h + 1) * 64, 1:2], kn_g.rearrange("(p o) -> p o", o=1), writes=[tg])
        qb = P.sb([128, 4, T], BF16)
        kd = P.sb([128, 2, T], BF16)
        tq = Tok()
        esr = ExitStack()
        Pr = Pool_(fw, esr)
        cos = Pr.sb([128, NLAT], F32)
        sin = Pr.sb([128, NLAT], F32)
        ttab = Tok()
        with ExitStack() as es2:
            P2 = Pool_(fw, es2)
            ang = P2.sb([128, NLAT], F32)
            tmpf = P2.sb([128, NLAT], F32)
            tmpi = P2.sb([128, NLAT], I32)
            ta = Tok()
            fw.dma("sp", ang[:], POS, writes=[ta])
            fw.op("dve", lambda e: e.tensor_scalar(out=ang[:], in0=ang[:], scalar1=cst[:, 0:1], scalar2=None, op0=ALU.mult),
                  reads=[ta, tc], writes=[ta])
            for (tab, off) in ((sin, 0.0), (cos, math.pi / 2)):
                fw.op("dve", lambda e, off=off: e.tensor_scalar(out=tmpi[:], in0=ang[:], scalar1=off, scalar2=1.0 / (2 * math.pi),
                                                                op0=ALU.add, op1=ALU.mult), reads=[ta], writes=[ta])
                fw.op("dve", lambda e: e.tensor_copy(out=tmpf[:], in_=tmpi[:]), reads=[ta], writes=[ta])
                fw.op("dve", lambda e: e.scalar_tensor_tensor(out=tmpf[:], in0=tmpf[:], scalar=-2 * math.pi, in1=ang[:],
                                                              op0=ALU.mult, op1=ALU.add), reads=[ta], writes=[ta])
                fw.op("dve", lambda e, off=off: e.tensor_scalar(out=tmpf[:], in0=tmpf[:], scalar1=off, scalar2=math.pi,
                                                                op0=ALU.add, op1=ALU.min), reads=[ta], writes=[ta])
                fw.op("dve", lambda e: e.tensor_scalar(out=tmpf[:], in0=tmpf[:], scalar1=-math.pi, scalar2=None, op0=ALU.max),
                      reads=[ta], writes=[ta])
                fw.op("act", lambda e, tab=tab: e.activation(out=tab[:], in_=tmpf[:], func=AF.Sin), reads=[ta], writes=[ttab])
            fw.barrier()
        with ExitStack() as es2:
            P2 = Pool_(fw, es2)
            raw = Rot([P2.sb([128, 512], F32) for _ in range(2)])
            sqr = Rot([P2.sb([128, 512], BF16) for _ in range(2)])
            psr = Rot([P2.ps() for _ in range(2)])
            psr2 = Rot([P2.ps() for _ in range(2)])
            rsr = Rot([P2.sb([128, 512], F32) for _ in range(2)])
            nbr = Rot([P2.sb([128, 512], BF16) for _ in range(2)])
            t1r = Rot([P2.sb([128, 512], F32) for _ in range(2)])
            t2r = Rot([P2.sb([128, 512], F32) for _ in range(2)])
            items = [("q", j) for j in range(4)] + [("k", g) for g in range(2)]
            for (kind, j) in items:
                for (lo, hi, ic) in SEGS:
                    n = hi - lo
                    rw, trw = raw.next()
                    if kind == "q":
                        fw.dma("sp", rw[:, :n], ZT[256 + j * 128:256 + (j + 1) * 128, lo:hi], writes=[trw])
                        gcol = 0
                        dst = qb[:, j, lo:hi]
                    else:
                        for h in range(2):
                            fw.dma("sp", rw[h * 64:(h + 1) * 64, :n], ZT[768 + j * 64:768 + (j + 1) * 64, lo:hi], writes=[trw])
                        gcol = 1
                        dst = kd[:, j, lo:hi]
                    sq, tsq = sqr.next()
                    fw.op("act", lambda e, sq=sq, rw=rw, n=n: e.activation(out=sq[:, :n], in_=rw[:, :n], func=AF.Square), reads=[trw], writes=[tsq])
                    ps, tps = psr.next()
                    fw.op("pe", lambda e, ps=ps, sq=sq, n=n: e.matmul(ps[:, :n], lhsT=bdb[:], rhs=sq[:, :n], start=True, stop=True),
                          reads=[tsq, tc], writes=[tps])
                    rs, trs = rsr.next()
                    rstd_from_ps(fw, rs, trs, ps, tps, n, 1.0, eps[:, 0:1], teps)
                    t1, tt1 = t1r.next()
                    fw.op("dve", lambda e, t1=t1, rw=rw, rs=rs, n=n, gcol=gcol: e.scalar_tensor_tensor(
                        out=t1[:, :n], in0=rw[:, :n], scalar=gq[:, gcol:gcol + 1], in1=rs[:, :n], op0=ALU.mult, op1=ALU.mult),
                        reads=[trw, trs, tg], writes=[tt1])
                    if ic:
                        fw.op("act", lambda e, dst=dst, t1=t1, n=n: e.copy(out=dst, in_=t1[:, :n]), reads=[tt1], writes=[tq])
                        continue
                    nb, tnb = nbr.next()
                    fw.op("act", lambda e, nb=nb, t1=t1, n=n: e.copy(out=nb[:, :n], in_=t1[:, :n]), reads=[tt1], writes=[tnb])
                    ps2, tps2 = psr2.next()
                    fw.op("pe", lambda e, ps2=ps2, nb=nb, n=n: e.matmul(ps2[:, :n], lhsT=permb[:], rhs=nb[:, :n], start=True, stop=True),
                          reads=[tnb, tc], writes=[tps2])
                    p0 = lo - NCTX
                    t2, tt2 = t2r.next()
                    fw.op("dve", lambda e, t2=t2, ps2=ps2, n=n, p0=p0: e.tensor_tensor(out=t2[:, :n], in0=ps2[:, :n], in1=sin[:, p0:p0 + n], op=ALU.mult),
                          reads=[tps2, ttab], writes=[tt2])
                    fw.op("pool", lambda e, t1=t1, nb=nb, n=n, p0=p0: e.tensor_tensor(out=t1[:, :n], in0=nb[:, :n], in1=cos[:, p0:p0 + n], op=ALU.mult),
                          reads=[tnb, ttab, tt1], writes=[tt1])
                    fw.op("pool", lambda e, dst=dst, t1=t1, t2=t2, n=n: e.tensor_tensor(out=dst, in0=t1[:, :n], in1=t2[:, :n], op=ALU.add),
                          reads=[tt1, tt2], writes=[tq])
            fw.barrier()
        esr.close()
        va = P.sb([128, 18, 2, 128], BF16)
        tva = Tok()
        fw.op("pool", lambda e: e.memset(va[:], 1.0), writes=[tva])
        for g in range(2):
            fw.dma("pool", va[:, :, g, 0:64], VT.rearrange("(t p) c -> p t c", p=128)[:, :, g * 64:(g + 1) * 64], writes=[tva])
        att = P.sb([128, 4, T], BF16)
        tatt = Tok()
        pss = Rot([P.ps() for _ in range(2)])
        pso = Rot([P.ps() for _ in range(3)])
        pend = []
        yield "core"
        ptr = Rot([P.sb([128, 512], BF16) for _ in range(3)])
        rcr = Rot([P.sb([64, 512], F32) for _ in range(2)])
        jobs = [(0, 256, 0, 2)] + [(256 + 512 * i, 768 + 512 * i, 0, 18) for i in range(4)]
        for h in range(8):
            g = h // 4
            jt, r0 = h // 2, (h % 2) * 64
            for (qlo, qhi, k0, k1) in jobs:
                n = qhi - qlo
                po, tpo = pso.next()
                for kt in range(k0, k1):
                    ps, tps = pss.next()
                    fw.op("pe", lambda e, ps=ps, kt=kt, g=g, jt=jt, r0=r0, qlo=qlo, qhi=qhi, n=n: e.matmul(
                        ps[:, :n], lhsT=kd[r0:r0 + 64, g, kt * 128:(kt + 1) * 128], rhs=qb[r0:r0 + 64, jt, qlo:qhi],
                        start=True, stop=True), reads=[tq], writes=[tps])
                    pt, tpt = ptr.next()
                    fw.op("act", lambda e, pt=pt, ps=ps, n=n: e.activation(out=pt[:, :n], in_=ps[:, :n], func=AF.Exp, scale=0.125),
                          reads=[tps], writes=[tpt])
                    fw.op("pe", lambda e, po=po, pt=pt, kt=kt, g=g, n=n, k0=k0, k1=k1: e.matmul(
                        po[:, :n], lhsT=va[:, kt, g, :], rhs=pt[:, :n], start=(kt == k0), stop=(kt == k1 - 1)),
                        reads=[tpt, tva], writes=[tpo])
                def norm_job(po=po, tpo=tpo, n=n, jt=jt, r0=r0, qlo=qlo, qhi=qhi):
                    rc, trc = rcr.next()
                    fw.op("dve", lambda e: e.reciprocal(out=rc[:, :n], in_=po[64:128, :n]), reads=[tpo], writes=[trc])
                    fw.op("dve", lambda e: e.tensor_tensor(out=att[r0:r0 + 64, jt, qlo:qhi], in0=po[0:64, :n], in1=rc[:, :n], op=ALU.mult),
                          reads=[tpo, trc], writes=[tatt])
                pend.append(norm_job)
                if len(pend) > 2:
                    pend.pop(0)()
                yield
        for nj in pend:
            nj()
        yield "fin"
        ogs = P.sb([128, 4], F32)
        tog = Tok()
        fw.dma("sp", ogs[:], og.rearrange("(k p) -> p k", p=128), writes=[tog], allow_slow_non_contiguous=True)
        ones = P.sb([128, 128], BF16)
        fw.op("pool", lambda e: e.memset(ones[:], 1.0), writes=[tog])
        sq4 = P.sb([128, 4, 512], BF16)
        tsq4 = Tok()
        rs = P.sb([128, 512], F32)
        trs = Tok()
        stg = Rot([P.sb([128, 512], F32) for _ in range(3)])
        tout = Tok()
        for (lo, hi, ic) in SEGS:
            n = hi - lo
            fw.op("act", lambda e, lo=lo, hi=hi, n=n: e.activation(out=sq4[:, :, :n], in_=att[:, :, lo:hi], func=AF.Square), reads=[tatt], writes=[tsq4])
            ps, tps = pss.next()
            for k in range(4):
                fw.op("pe", lambda e, ps=ps, k=k, n=n: e.matmul(ps[:, :n], lhsT=ones[:], rhs=sq4[:, k, :n], start=(k == 0), stop=(k == 3)),
                      reads=[tsq4, tog], writes=[tps])
            rstd_from_ps(fw, rs, trs, ps, tps, n, 1.0 / 512, eps[:, 0:1], teps)
            for k in range(4):
                sg, tsg = stg.next()
                fw.op("dve", lambda e, sg=sg, k=k, lo=lo, hi=hi, n=n: e.scalar_tensor_tensor(
                    out=sg[:, :n], in0=att[:, k, lo:hi], scalar=ogs[:, k:k + 1], in1=rs[:, :n], op0=ALU.mult, op1=ALU.mult),
                    reads=[tatt, trs, tog], writes=[tsg])
                fw.dma("sp", ATT[k * 128:(k + 1) * 128, lo:hi], sg[:, :n], reads=[tsg], writes=[Tok()])
        fw.barrier()


def stage_p23(fw, io):
    g2 = gen_p2(fw, io)
    g3 = gen_p3(fw, io)

    def until(g, tag):
        for v in g:
            if v == tag:
                return True
        return False
    until(g2, "core")
    until(g3, "core")
    a2 = a3 = True
    R = 36
    while a2 or a3:
        if a2:
            for _ in range(R):
                if next(g2) == "fin":
                    a2 = False
                    break
        if a3:
            if next(g3) == "fin":
                a3 = False
    for _ in g3:
        pass
    for _ in g2:
        pass


RW_BASE = 1024
CHK = 64
NCH = T // CHK
LN_EPS_RW = 64e-5


def rw_consts():
    idx = np.arange(64)
    m = np.zeros((64, 4, 64), np.float32)
    m[:, 0, :] = (idx[:, None] < idx[None, :])
    m[:, 1, :] = (idx[:, None] > idx[None, :])
    m[:, 2, :] = (idx[:, None] <= idx[None, :])
    m[:, 3, :] = (idx[:, None] >= idx[None, :])
    bd = np.zeros((128, 128), np.float32)
    bd[:64, :64] = 1.0
    bd[64:, 64:] = 1.0
    return m, bd


def stage_p4(fw, io):
    ZT = io("ZT", [2048, TE], F32, "in")
    mu = io("rw_mu", [960], F32, "in")
    w0 = io("rw_w0", [2, 256], F32, "in")
    w2 = io("rw_w2", [2, 32, 256], F32, "in")
    a0 = io("rw_a0", [2, 256], F32, "in")
    a2 = io("rw_a2", [2, 32, 256], F32, "in")
    g2 = io("rw_g2", [64, 256], F32, "in")
    k_k = io("rw_k_k", [256], F32, "in")
    k_a = io("rw_k_a", [256], F32, "in")
    r_k = io("rw_r_k", [256], F32, "in")
    ln_g = io("rw_ln_g", [256], F32, "in")
    ln_b = io("rw_ln_b", [256], F32, "in")
    ident = io("ident", [128, 128], F32, "in")
    MASKS = io("RWMASK", [64, 4, 64], F32, "in")
    BD = io("RWBD", [128, 128], F32, "in")
    RWT = io("RWT", [256, T], F32, "out")
    N = T
    CH5 = [(0, 512), (512, 1024), (1024, 1536), (1536, 2048), (2048, 2560)]
    with ExitStack() as es:
        P = Pool_(fw, es)
        tc = Tok()
        idt = P.sb([128, 128], F32)
        idb = P.sb([128, 128], BF16)
        msk = P.sb([64, 4, 64], F32)
        bd1 = P.sb([128, 128], F32)
        fw.dma("sp", idt[:], ident, writes=[tc])
        fw.dma("sp", msk[:], MASKS, writes=[tc])
        fw.dma("sp", bd1[:], BD, writes=[tc])
        fw.op("dve", lambda e: e.tensor_copy(out=idb[:], in_=idt[:]), reads=[tc], writes=[tc])
        mrep = P.sb([64, 4, 4, 64], F32)
        for rep in range(4):
            fw.op("dve", lambda e, rep=rep: e.tensor_copy(out=mrep[:, :, rep, :], in_=msk[:]), reads=[tc], writes=[tc])
        pp = P.sb([128, 12, 2], F32)
        tpp = Tok()
        srcs = [w0[0], w0[1], a0[0], a0[1], k_k, k_a, k_a, r_k, ln_g, ln_b]
        for i, sap in enumerate(srcs):
            fw.dma("sp", pp[:, i, :], sap.rearrange("(c p) -> p c", p=128), writes=[tpp], allow_slow_non_contiguous=True)
        fw.op("dve", lambda e: e.tensor_scalar(out=pp[:, 6, :], in0=pp[:, 6, :], scalar1=-1.0, scalar2=1.0, op0=ALU.mult, op1=ALU.add),
              reads=[tpp], writes=[tpp])
        epsl = P.sb([128, 2], F32)
        fw.op("pool", lambda e: e.memset(epsl[:, 0:1], LN_EPS_RW), writes=[tpp])
        fw.op("pool", lambda e: e.memset(epsl[:, 1:2], 1e-24), writes=[tpp])
        wA = P.sb([128, 256], BF16)
        wB = P.sb([128, 256], BF16)
        tlw = Tok()
        fw.dma("pool", wA[0:32, :], w2[0], writes=[tlw])
        fw.dma("pool", wA[32:64, :], w2[1], writes=[tlw])
        fw.dma("pool", wA[64:96, :], a2[0], writes=[tlw])
        fw.dma("pool", wB[0:32, :], a2[1], writes=[tlw])
        fw.dma("pool", wB[64:128, :], g2, writes=[tlw])
        smask = P.sb([128, N], BF16)
        tsm = Tok()
        fw.op("pool", lambda e: e.memset(smask[:], 1.0), writes=[tsm])
        fw.op("pool", lambda e: e.memset(smask[:].rearrange("p (c j) -> p c j", j=CHK)[:, :, 0], 0.0), writes=[tsm])

        def shift_tmp(P2):
            return (Rot([P2.sb([128, TE], F32) for _ in range(2)]), Rot([P2.sb([128, TE], F32) for _ in range(2)]),
                    Rot([P2.sb([128, 2], F32) for _ in range(2)]))

        def load_shift(dst, tdst, row0, rows, tmp, post=None, pb=0):
            zt_, tz = tmp[0].next()
            nbt_, tnb = tmp[1].next()
            mtt_, tm = tmp[2].next()
            ps_ = slice(pb, pb + rows)
            z = zt_[ps_, :]
            fw.dma("sp", z, ZT[RW_BASE + row0:RW_BASE + row0 + rows, :], writes=[tz])
            fw.dma("sp", mtt_[ps_, 0:1], mu[row0:row0 + rows].rearrange("(p o) -> p o", o=1), writes=[tm])
            fw.op("dve", lambda e: e.tensor_scalar(out=mtt_[ps_, 1:2], in0=mtt_[ps_, 0:1], scalar1=0.5, scalar2=None, op0=ALU.mult), reads=[tm], writes=[tm])
            fw.op("dve", lambda e: e.tensor_scalar(out=mtt_[ps_, 0:1], in0=mtt_[ps_, 0:1], scalar1=-1.0, scalar2=1.0, op0=ALU.mult, op1=ALU.add),
                  reads=[tm], writes=[tm])
            fw.op("pool", lambda e: e.memset(nbt_[ps_, 0:1], 0.0), writes=[tnb])
            fw.op("pool", lambda e: e.tensor_copy(out=nbt_[ps_, 1:TE], in_=zt_[ps_, 0:TE - 1]), reads=[tz], writes=[tnb])
            fw.op("pool", lambda e: e.tensor_tensor(out=nbt_[ps_, 0:TE - 1], in0=nbt_[ps_, 0:TE - 1], in1=zt_[ps_, 1:TE], op=ALU.add),
                  reads=[tz, tnb], writes=[tnb])
            for cb in (256, 2304):
                fw.op("pool", lambda e, cb=cb: e.tensor_tensor(out=nbt_[ps_, cb:cb + 1], in0=nbt_[ps_, cb:cb + 1], in1=zt_[ps_, cb - 1:cb], op=ALU.subtract),
                      reads=[tz, tnb], writes=[tnb])
                fw.op("pool", lambda e, cb=cb: e.tensor_tensor(out=nbt_[ps_, cb - 1:cb], in0=nbt_[ps_, cb - 1:cb], in1=zt_[ps_, cb:cb + 1], op=ALU.subtract),
                      reads=[tz, tnb], writes=[tnb])
            fw.op("act", lambda e: e.activation(out=z, in_=z, func=AF.Identity, scale=mtt_[ps_, 0:1]), reads=[tz, tm], writes=[tz])
            if post is None:
                fw.op("dve", lambda e: e.scalar_tensor_tensor(out=dst, in0=nbt_[ps_, :], scalar=mtt_[ps_, 1:2], in1=z, op0=ALU.mult, op1=ALU.add),
                      reads=[tz, tnb, tm], writes=[tdst])
            else:
                fw.op("dve", lambda e: e.scalar_tensor_tensor(out=z, in0=nbt_[ps_, :], scalar=mtt_[ps_, 1:2], in1=z, op0=ALU.mult, op1=ALU.add),
                      reads=[tz, tnb, tm], writes=[tz])
                fw.op("act", lambda e: e.activation(out=dst, in_=z, func=post), reads=[tz], writes=[tdst])

        lorA = P.sb([128, TE], BF16)
        lorB = P.sb([128, TE], BF16)
        tlor = Tok()
        with ExitStack() as es2:
            tmp = shift_tmp(Pool_(fw, es2))
            for i in range(4):
                dstt = lorA[32 * i:32 * i + 32, :] if i < 3 else lorB[0:32, :]
                load_shift(dstt, tlor, 768 + 32 * i, 32, tmp, post=(AF.Tanh if i < 2 else AF.Copy), pb=(32 * i if i < 3 else 0))
            load_shift(lorB[64:128, :], tlor, 896, 64, tmp, post=AF.Sigmoid, pb=64)
            fw.barrier()

        for c in range(2):
            with ExitStack() as esc:
                Pc = Pool_(fw, esc)
                rr = Pc.sb([128, TE], F32)
                kx = Pc.sb([128, TE], F32)
                vv = Pc.sb([128, TE], F32)
                kk = Pc.sb([128, TE], F32)
                trr, tkx, tvv, tkk = Tok(), Tok(), Tok(), Tok()
                vtok = Pc.sb([64, TE // CHK, 128], BF16)
                tvt = Tok()
                yacc = Pc.sb([128, T], F32)
                bon = Pc.sb([128, T], F32)
                tya, tbon = Tok(), Tok()
                fw.op("pool", lambda e: e.memset(yacc[:], 0.0), writes=[tya])
                fw.op("pool", lambda e: e.memset(bon[:], 0.0), writes=[tbon])
                with ExitStack() as es2:
                    tmp = shift_tmp(Pool_(fw, es2))
                    for (dst, tdst, r0) in ((rr, trr, 0), (kx, tkx, 256), (vv, tvv, 512)):
                        load_shift(dst[:], tdst, r0 + c * 128, 128, tmp)
                    fw.barrier()
                with ExitStack() as es2:
                    P2 = Pool_(fw, es2)
                    sq = P2.sb([128, 512], F32)
                    rn = P2.sb([128, 512], F32)
                    tsq, trn = Tok(), Tok()
                    ps1 = P2.ps()
                    tps1 = Tok()
                    fw.op("dve", lambda e: e.tensor_scalar(out=kk[:], in0=kx[:], scalar1=pp[:, 4, c:c + 1], scalar2=None, op0=ALU.mult),
                          reads=[tkx, tpp], writes=[tkk])
                    for (c0, c1) in CH5:
                        fw.op("act", lambda e, c0=c0, c1=c1: e.activation(out=sq[:], in_=kk[:, c0:c1], func=AF.Square), reads=[tkk], writes=[tsq])
                        fw.op("pe", lambda e: e.matmul(ps1[:], lhsT=bd1[:], rhs=sq[:], start=True, stop=True), reads=[tsq, tc], writes=[tps1])
                        rstd_from_ps(fw, rn, trn, ps1, tps1, 512, 1.0, epsl[:, 1:2], tpp)
                        fw.op("dve", lambda e, c0=c0, c1=c1: e.tensor_tensor(out=kk[:, c0:c1], in0=kk[:, c0:c1], in1=rn[:], op=ALU.mult),
                              reads=[tkk, trn], writes=[tkk])
                    vb = P2.sb([128, TE], BF16)
                    tvb = Tok()
                    fw.op("act", lambda e: e.copy(out=vb[:], in_=vv[:]), reads=[tvv], writes=[tvb])
                    pst = P2.ps([128, 1024], BF16)
                    tpst = Tok()
                    for q in range(TE // CHK // 4):
                        for j in range(4):
                            ch = q * 4 + j
                            fw.op("pe", lambda e, j=j, ch=ch: e.transpose(pst[0:64, j * 128:(j + 1) * 128], vb[:, ch * CHK:(ch + 1) * CHK], idb[:]),
                                  reads=[tvb, tc], writes=[tpst])
                        fw.op("dve", lambda e, q=q: e.tensor_copy(out=vtok[:, q * 4:(q + 1) * 4, :], in_=pst[0:64, 0:512].rearrange("p (j c) -> p j c", j=4)),
                              reads=[tpst], writes=[tvt])
                    fw.barrier()

                for di in range(2):
                    off = 0 if di == 0 else 256
                    with ExitStack() as esd:
                        Pd = Pool_(fw, esd)
                        aT = Pd.sb([128, N], BF16)
                        bT = Pd.sb([128, N], BF16)
                        kT = Pd.sb([128, N], BF16)
                        rT = Pd.sb([128, N], BF16)
                        btok = Pd.sb([64, NCH, 128], BF16)
                        ktok = Pd.sb([64, NCH, 128], BF16)
                        pC = Pd.sb([128, NCH], F32)
                        tops = Tok()
                        ttok = Tok()
                        with ExitStack() as es2:
                            P2 = Pool_(fw, es2)
                            ld = P2.sb([128, TE], F32)
                            kd = P2.sb([128, TE], F32)
                            bb = P2.sb([128, TE], F32)
                            tld, tkd, tbb = Tok(), Tok(), Tok()
                            psr = Rot([P2.ps() for _ in range(3)])
                            tm5 = Rot([P2.sb([128, 512], F32) for _ in range(2)])
                            for (c0, c1) in CH5:
                                ps, tps = psr.next()
                                fw.op("pe", lambda e, ps=ps, c0=c0, c1=c1: e.matmul(ps[:], lhsT=wA[32 * di:32 * di + 32, c * 128:(c + 1) * 128], rhs=lorA[32 * di:32 * di + 32, c0:c1],
                                                                                  start=True, stop=True), reads=[tlw, tlor], writes=[tps])
                                fw.op("act", lambda e, ps=ps, c0=c0, c1=c1: e.activation(out=ld[:, c0:c1], in_=ps[:], func=AF.Sigmoid, bias=pp[:, di, c:c + 1]),
                                      reads=[tps, tpp], writes=[tld])
                                ps, tps = psr.next()
                                fw.op("pe", lambda e, ps=ps, c0=c0, c1=c1: e.matmul(ps[:], lhsT=(wA[64:96, c * 128:(c + 1) * 128] if di == 0 else wB[0:32, c * 128:(c + 1) * 128]),
                                                                                  rhs=(lorA[64:96, c0:c1] if di == 0 else lorB[0:32, c0:c1]),
                                                                                  start=True, stop=True), reads=[tlw, tlor], writes=[tps])
                                fw.op("act", lambda e, ps=ps, c0=c0, c1=c1: e.activation(out=bb[:, c0:c1], in_=ps[:], func=AF.Sigmoid, bias=pp[:, 2 + di, c:c + 1]),
                                      reads=[tps, tpp], writes=[tbb])
                            fw.op("pool", lambda e: e.tensor_scalar(out=ld[:], in0=ld[:], scalar1=-math.exp(-0.5), scalar2=None, op0=ALU.mult), reads=[tld], writes=[tld])
                            fw.op("act", lambda e: e.activation(out=kd[:], in_=bb[:], func=AF.Identity, scale=pp[:, 5, c:c + 1], bias=pp[:, 6, c:c + 1]),
                                  reads=[tbb, tpp], writes=[tkd])
                            fw.op("dve", lambda e: e.tensor_tensor(out=kd[:], in0=kd[:], in1=kx[:], op=ALU.mult), reads=[tkd, tkx], writes=[tkd])
                            fw.op("pool", lambda e: e.tensor_tensor(out=bb[:], in0=bb[:], in1=kk[:], op=ALU.mult), reads=[tbb, tkk], writes=[tbb])
                            for (c0, c1) in CH5:
                                c1 = min(c1, T)
                                n = c1 - c0
                                tm, ttm = tm5.next()
                                fw.op("dve", lambda e, tm=tm, c0=c0, c1=c1, n=n: e.scalar_tensor_tensor(out=tm[:, :n], in0=rr[:, c0:c1], scalar=pp[:, 7, c:c + 1], in1=kd[:, c0:c1],
                                                                                                 op0=ALU.mult, op1=ALU.mult), reads=[trr, tkd, tpp], writes=[ttm])
                                ps, tps = psr.next()
                                fw.op("pe", lambda e, ps=ps, tm=tm, n=n: e.matmul(ps[:, :n], lhsT=bd1[:], rhs=tm[:, :n], start=True, stop=True), reads=[ttm, tc], writes=[tps])
                                tm2, ttm2 = tm5.next()
                                fw.op("dve", lambda e, tm2=tm2, ps=ps, c0=c0, c1=c1, n=n: e.tensor_tensor(out=tm2[:, :n], in0=ps[:, :n], in1=vv[:, c0:c1], op=ALU.mult),
                                      reads=[tps, tvv], writes=[ttm2])
                                fw.op("pool", lambda e, tm2=tm2, c0=c0, c1=c1, n=n: e.tensor_tensor(out=bon[:, c0:c1], in0=bon[:, c0:c1], in1=tm2[:, :n], op=ALU.add),
                                      reads=[ttm2, tbon], writes=[tbon])
                            cs = P2.sb([128, N], F32)
                            ex = P2.sb([128, N], F32)
                            tcs, tex = Tok(), Tok()
                            ldw = ld[:, off:off + N]
                            fw.op("dve", lambda e: e.tensor_tensor_scan(out=cs[:], data0=smask[:], data1=ldw, initial=0.0, op0=ALU.mult, op1=ALU.add),
                                  reads=[tsm, tld], writes=[tcs])
                            csv = cs[:].rearrange("p (c j) -> p c j", j=CHK)
                            tot = P2.sb([128, NCH, 1], F32)
                            ttot = Tok()
                            fw.op("dve", lambda e: e.tensor_copy(out=tot[:], in_=csv[:, :, CHK - 1:CHK]), reads=[tcs], writes=[ttot])
                            totb = tot[:].to_broadcast([128, NCH, CHK])
                            if di == 1:
                                fw.op("dve", lambda e: e.tensor_tensor(out=csv, in0=totb, in1=csv, op=ALU.subtract), reads=[ttot, tcs], writes=[tcs])
                                fw.op("dve", lambda e: e.tensor_tensor(out=cs[:], in0=cs[:], in1=ldw, op=ALU.add), reads=[tcs, tld], writes=[tcs])
                            fw.op("act", lambda e: e.activation(out=pC[:], in_=tot[:, :, 0], func=AF.Exp), reads=[ttot], writes=[tops])
                            kdw, bbw = kd[:, off:off + N], bb[:, off:off + N]
                            rrw, kkw = rr[:, off:off + N], kk[:, off:off + N]
                            fw.op("act", lambda e: e.activation(out=ex[:], in_=cs[:], func=AF.Exp), reads=[tcs], writes=[tex])
                            fw.op("dve", lambda e: e.tensor_tensor(out=rT[:], in0=rrw, in1=ex[:], op=ALU.mult), reads=[trr, tex], writes=[tops])
                            fw.op("act", lambda e: e.activation(out=ex[:], in_=cs[:], func=AF.Exp, scale=-1.0), reads=[tcs, tops], writes=[tex])
                            fw.op("dve", lambda e: e.tensor_tensor(out=bT[:], in0=bbw, in1=ex[:], op=ALU.mult), reads=[tbb, tex], writes=[tops])
                            fw.op("pool", lambda e: e.tensor_tensor(out=kT[:], in0=kdw, in1=ex[:], op=ALU.mult), reads=[tkd, tex], writes=[tops])
                            e3 = ex
                            te3 = tex
                            fw.op("dve", lambda e: e.tensor_tensor(out=e3[:], in0=cs[:], in1=ldw, op=ALU.subtract), reads=[tcs, tld], writes=[te3])
                            fw.op("act", lambda e: e.activation(out=e3[:], in_=e3[:], func=AF.Exp), reads=[te3], writes=[te3])
                            fw.op("dve", lambda e: e.scalar_tensor_tensor(out=aT[:], in0=kkw, scalar=-1.0, in1=e3[:], op0=ALU.mult, op1=ALU.mult),
                                  reads=[tkk, te3], writes=[tops])
                            e3v = e3[:].rearrange("p (c j) -> p c j", j=CHK)
                            fw.op("dve", lambda e: e.tensor_tensor(out=e3v, in0=totb, in1=csv, op=ALU.subtract), reads=[ttot, tcs, tops, te3], writes=[te3])
                            fw.op("act", lambda e: e.activation(out=e3[:], in_=e3[:], func=AF.Exp), reads=[te3], writes=[te3])
                            bh = P2.sb([128, N], BF16)
                            kh = P2.sb([128, N], BF16)
                            tbh = Tok()
                            fw.op("dve", lambda e: e.tensor_tensor(out=bh[:], in0=bbw, in1=e3[:], op=ALU.mult), reads=[tbb, te3], writes=[tbh])
                            fw.op("pool", lambda e: e.tensor_tensor(out=kh[:], in0=kdw, in1=e3[:], op=ALU.mult), reads=[tkd, te3], writes=[tbh])
                            pst = P2.ps([128, 1024], BF16)
                            tpst = Tok()
                            for (src, dstt) in ((bh, btok), (kh, ktok)):
                                for q in range(NCH // 4):
                                    for j in range(4):
                                        ch = q * 4 + j
                                        fw.op("pe", lambda e, j=j, ch=ch, src=src: e.transpose(pst[0:64, j * 128:(j + 1) * 128], src[:, ch * CHK:(ch + 1) * CHK], idb[:]),
                                              reads=[tbh, tc], writes=[tpst])
                                    fw.op("act", lambda e, q=q, dstt=dstt: e.copy(out=dstt[:, q * 4:(q + 1) * 4, :], in_=pst[0:64, 0:512].rearrange("p (j c) -> p j c", j=4)),
                                          reads=[tpst], writes=[ttok])
                            fw.barrier()
                        with ExitStack() as es3:
                            P3 = Pool_(fw, es3)
                            Hs = P3.sb([128, 64], F32)
                            Hb = P3.sb([128, 64], BF16)
                            tH = Tok()
                            fw.op("pool", lambda e: e.memset(Hs[:], 0.0), writes=[tH])
                            fw.op("pool", lambda e: e.memset(Hb[:], 0.0), writes=[tH])
                            psA = Rot([P3.ps([64, 512]) for _ in range(1)])
                            psB = Rot([P3.ps([64, 512]) for _ in range(1)])
                            psI = Rot([P3.ps([64, 512]) for _ in range(1)])
                            psT = Rot([P3.ps([64, 512]) for _ in range(1)])
                            psG = Rot([P3.ps([64, 512]) for _ in range(1)])
                            psH = Rot([P3.ps([128, 512]) for _ in range(1)])
                            psY = Rot([P3.ps([128, 512]) for _ in range(1)])
                            g1b = Rot([P3.sb([64, 2, 64], BF16) for _ in range(2)])
                            nl = Rot([P3.sb([64, 2, 2, 64], F32) for _ in range(2)])
                            nl2 = Rot([P3.sb([64, 2, 2, 64], F32) for _ in range(2)])
                            g2b_ = Rot([P3.sb([64, 4, 64], BF16) for _ in range(2)])
                            Pm = Rot([P3.sb([64, 2, 64], F32) for _ in range(2)])
                            Gs = Rot([P3.sb([64, 128], F32) for _ in range(2)])
                            Ub = Rot([P3.sb([64, 128], BF16) for _ in range(2)])
                            Ys = Rot([P3.sb([64, 128], F32) for _ in range(2)])
                            ms, ml, mi = (0, 1, 2) if di == 0 else (1, 0, 3)
                            order = list(range(NCH)) if di == 0 else list(range(NCH - 1, -1, -1))
                            res = {}

                            def par_gen(i):
                                cc0, cc1 = i * CHK, (i + 1) * CHK
                                pa, tpa = psA.next()
                                pb, tpb = psB.next()
                                for h in range(2):
                                    hp = slice(h * 64, (h + 1) * 64)
                                    for (dst, lt, rt_) in ((pa[:, h * 64:(h + 1) * 64], kT, aT), (pa[:, 128 + h * 64:128 + (h + 1) * 64], bT, aT),
                                                           (pa[:, 256 + h * 64:256 + (h + 1) * 64], aT, bT)):
                                        fw.op("pe", lambda e, dst=dst, lt=lt, rt_=rt_, hp=hp: e.matmul(dst, lhsT=lt[hp, cc0:cc1], rhs=rt_[hp, cc0:cc1], start=True, stop=True),
                                              reads=[tops], writes=[tpa])
                                        yield
                                    for (dst, lt, rt_) in ((pb[:, h * 64:(h + 1) * 64], bT, rT), (pb[:, 128 + h * 64:128 + (h + 1) * 64], kT, rT)):
                                        fw.op("pe", lambda e, dst=dst, lt=lt, rt_=rt_, hp=hp: e.matmul(dst, lhsT=lt[hp, cc0:cc1], rhs=rt_[hp, cc0:cc1], start=True, stop=True),
                                              reads=[tops], writes=[tpb])
                                        yield
                                a1, ta1 = g1b.next()
                                nlt, tnl = nl.next()
                                a45, ta45 = g2b_.next()
                                fw.op("dve", lambda e: e.tensor_tensor(out=a1[:], in0=pa[:, 0:128].rearrange("p (h t) -> p h t", h=2), in1=mrep[:, ms, 0:2, :], op=ALU.mult),
                                      reads=[tpa, tc], writes=[ta1])
                                yield
                                fw.op("dve", lambda e: e.tensor_tensor(out=nlt[:, 0], in0=pa[:, 128:256].rearrange("p (h t) -> p h t", h=2), in1=mrep[:, ms, 0:2, :], op=ALU.mult),
                                      reads=[tpa, tc], writes=[tnl])
                                yield
                                fw.op("dve", lambda e: e.tensor_tensor(out=nlt[:, 1], in0=pa[:, 256:384].rearrange("p (h t) -> p h t", h=2), in1=mrep[:, ml, 0:2, :], op=ALU.mult),
                                      reads=[tpa, tc], writes=[tnl])
                                yield
                                fw.op("dve", lambda e: e.tensor_tensor(out=a45[:], in0=pb[:, 0:256].rearrange("p (h t) -> p h t", h=4), in1=mrep[:, mi, :, :], op=ALU.mult),
                                      reads=[tpb, tc], writes=[ta45])
                                yield
                                pm, tpm = Pm.next()
                                fw.op("dve", lambda e: e.tensor_tensor(out=pm[:], in0=nlt[:, 0], in1=idt[0:64, 0:64].unsqueeze(1).to_broadcast([64, 2, 64]), op=ALU.add),
                                      reads=[tnl, tc], writes=[tpm])
                                yield
                                cur, tcur = nlt, tnl
                                for lev in range(5):
                                    pi_, tpi = psI.next()
                                    for h in range(2):
                                        fw.op("pe", lambda e, pi_=pi_, h=h, cur=cur: e.matmul(pi_[:, h * 64:(h + 1) * 64], lhsT=cur[:, 0, h, :], rhs=cur[:, 1, h, :], start=True, stop=True),
                                              reads=[tcur], writes=[tpi])
                                        yield
                                    nxt, tnxt = (nl2.next() if lev % 2 == 0 else nl.next())
                                    fw.op("act", lambda e, nxt=nxt, pi_=pi_: e.copy(out=nxt[:, 1], in_=pi_[:, 0:128].rearrange("p (h t) -> p h t", h=2)), reads=[tpi], writes=[tnxt])
                                    yield
                                    for h in range(2):
                                        fw.op("pe", lambda e, pi_=pi_, h=h, nxt=nxt: e.matmul(pi_[:, 256 + h * 64:256 + (h + 1) * 64], lhsT=nxt[:, 1, h, :], rhs=pm[:, h, :], start=True, stop=True),
                                              reads=[tnxt, tpm], writes=[tpi])
                                        yield
                                    if lev < 4:
                                        pt_, tpt = psT.next()
                                        for h in range(2):
                                            fw.op("pe", lambda e, pt_=pt_, h=h, nxt=nxt: e.transpose(pt_[:, h * 64:(h + 1) * 64], nxt[:, 1, h, :], idt[0:64, 0:64]),
                                                  reads=[tnxt, tc], writes=[tpt])
                                            yield
                                    fw.op("dve", lambda e, pi_=pi_: e.tensor_tensor(out=pm[:], in0=pm[:], in1=pi_[:, 256:384].rearrange("p (h t) -> p h t", h=2), op=ALU.add),
                                          reads=[tpi, tpm], writes=[tpm])
                                    yield
                                    if lev < 4:
                                        fw.op("act", lambda e, nxt=nxt, pt_=pt_: e.copy(out=nxt[:, 0], in_=pt_[:, 0:128].rearrange("p (h t) -> p h t", h=2)), reads=[tpt], writes=[tnxt])
                                        yield
                                    cur, tcur = nxt, tnxt
                                res[i] = (a1, ta1, a45, ta45, pm, tpm)

                            def chain_gen(i):
                                cc0, cc1 = i * CHK, (i + 1) * CHK
                                gch = i + off // CHK
                                a1, ta1, a45, ta45, pm, tpm = res.pop(i)
                                pg, tpg = psG.next()
                                for h in range(2):
                                    hp = slice(h * 64, (h + 1) * 64)
                                    fw.op("pe", lambda e, h=h, hp=hp: e.matmul(pg[:, h * 64:(h + 1) * 64], lhsT=aT[hp, cc0:cc1], rhs=Hb[hp, :], start=True, stop=False),
                                          reads=[tops, tH], writes=[tpg])
                                    yield
                                    fw.op("pe", lambda e, h=h, hp=hp: e.matmul(pg[:, h * 64:(h + 1) * 64], lhsT=a1[:, h, :], rhs=vtok[:, gch, hp], start=False, stop=True),
                                          reads=[ta1, tvt], writes=[tpg])
                                    yield
                                gs, tgs = Gs.next()
                                fw.op("act", lambda e: e.copy(out=gs[:], in_=pg[:, 0:128]), reads=[tpg], writes=[tgs])
                                yield
                                for h in range(2):
                                    fw.op("pe", lambda e, h=h: e.matmul(pg[:, 128 + h * 64:128 + (h + 1) * 64], lhsT=pm[:, h, :], rhs=gs[:, h * 64:(h + 1) * 64], start=True, stop=True),
                                          reads=[tpm, tgs], writes=[tpg])
                                    yield
                                ub, tub = Ub.next()
                                fw.op("dve", lambda e: e.tensor_copy(out=ub[:], in_=pg[:, 128:256]), reads=[tpg], writes=[tub])
                                yield
                                ph, tph = psH.next()
                                py, tpy = psY.next()
                                for h in range(2):
                                    hp = slice(h * 64, (h + 1) * 64)
                                    fw.op("pe", lambda e, h=h, hp=hp: e.matmul(ph[hp, 0:64], lhsT=btok[:, i, hp], rhs=ub[:, hp], start=True, stop=False),
                                          reads=[ttok, tub], writes=[tph])
                                    yield
                                    fw.op("pe", lambda e, h=h, hp=hp: e.matmul(ph[hp, 0:64], lhsT=ktok[:, i, hp], rhs=vtok[:, gch, hp], start=False, stop=True),
                                          reads=[ttok, tvt], writes=[tph])
                                    yield
                                for h in range(2):
                                    hp = slice(h * 64, (h + 1) * 64)
                                    fw.op("pe", lambda e, h=h, hp=hp: e.matmul(py[hp, 0:64], lhsT=Hb[hp, :], rhs=rT[hp, cc0:cc1], start=True, stop=False),
                                          reads=[tops, tH], writes=[tpy])
                                    yield
                                    fw.op("pe", lambda e, h=h, hp=hp: e.matmul(py[hp, 0:64], lhsT=ub[:, hp], rhs=a45[:, h, :], start=False, stop=False),
                                          reads=[ta45, tub], writes=[tpy])
                                    yield
                                    fw.op("pe", lambda e, h=h, hp=hp: e.matmul(py[hp, 0:64], lhsT=vtok[:, gch, hp], rhs=a45[:, 2 + h, :], start=False, stop=True),
                                          reads=[ta45, tvt], writes=[tpy])
                                    yield
                                fw.op("dve", lambda e: e.scalar_tensor_tensor(out=Hs[:], in0=Hs[:], scalar=pC[:, i:i + 1], in1=ph[:, 0:64], op0=ALU.mult, op1=ALU.add),
                                      reads=[tph, tH, tops, tpy], writes=[tH])
                                yield
                                fw.op("act", lambda e: e.copy(out=Hb[:], in_=Hs[:]), reads=[tH, tpy, tpg], writes=[tH])
                                yield
                                y0 = off + cc0
                                if y0 >= T:
                                    y0 -= T
                                fw.op("dve", lambda e: e.tensor_tensor(out=yacc[:, y0:y0 + CHK], in0=yacc[:, y0:y0 + CHK], in1=py[:, 0:64], op=ALU.add),
                                      reads=[tpy, tya], writes=[tya])
                                yield

                            for _ in par_gen(order[0]):
                                pass
                            for idx, i in enumerate(order):
                                gp = par_gen(order[idx + 1]) if idx + 1 < len(order) else iter(())
                                gc = chain_gen(i)
                                done_p = done_c = False
                                while not (done_p and done_c):
                                    for _ in range(2):
                                        if not done_p:
                                            try:
                                                next(gp)
                                            except StopIteration:
                                                done_p = True
                                    if not done_c:
                                        try:
                                            next(gc)
                                        except StopIteration:
                                            done_c = True
                            fw.barrier()
                with ExitStack() as es4:
                    P4 = Pool_(fw, es4)
                    psr = Rot([P4.ps() for _ in range(3)])
                    xc = P4.sb([128, 512], F32)
                    sq = P4.sb([128, 512], F32)
                    rs = P4.sb([128, 512], F32)
                    txc, tsq, trs = Tok(), Tok(), Tok()
                    stg = Rot([P4.sb([128, 512], F32) for _ in range(2)])
                    tout = Tok()
                    for (lo, hi, ic) in SEGS:
                        n = hi - lo
                        ps, tps = psr.next()
                        fw.op("pe", lambda e, ps=ps, lo=lo, hi=hi, n=n: e.matmul(ps[:, :n], lhsT=bd1[:], rhs=yacc[:, lo:hi], start=True, stop=True), reads=[tya, tc], writes=[tps])
                        fw.op("dve", lambda e, ps=ps, lo=lo, hi=hi, n=n: e.scalar_tensor_tensor(out=xc[:, :n], in0=ps[:, :n], scalar=-1.0 / 64, in1=yacc[:, lo:hi], op0=ALU.mult, op1=ALU.add),
                              reads=[tps, tya], writes=[txc])
                        fw.op("act", lambda e, n=n: e.activation(out=sq[:, :n], in_=xc[:, :n], func=AF.Square), reads=[txc], writes=[tsq])
                        ps2, tps2 = psr.next()
                        fw.op("pe", lambda e, ps2=ps2, n=n: e.matmul(ps2[:, :n], lhsT=bd1[:], rhs=sq[:, :n], start=True, stop=True), reads=[tsq, tc], writes=[tps2])
                        rstd_from_ps(fw, rs, trs, ps2, tps2, n, 1.0 / 64, epsl[:, 0:1], tpp)
                        fw.op("dve", lambda e, n=n: e.tensor_tensor(out=xc[:, :n], in0=xc[:, :n], in1=rs[:, :n], op=ALU.mult), reads=[txc, trs], writes=[txc])
                        fw.op("act", lambda e, n=n: e.activation(out=xc[:, :n], in_=xc[:, :n], func=AF.Identity, scale=pp[:, 8, c:c + 1], bias=pp[:, 9, c:c + 1]),
                              reads=[txc, tpp], writes=[txc])
                        fw.op("pool", lambda e, lo=lo, hi=hi, n=n: e.tensor_tensor(out=xc[:, :n], in0=xc[:, :n], in1=bon[:, lo:hi], op=ALU.add), reads=[txc, tbon], writes=[txc])
                        ps3, tps3 = psr.next()
                        fw.op("pe", lambda e, ps3=ps3, lo=lo, hi=hi, n=n: e.matmul(ps3[:, :n], lhsT=wB[64:128, c * 128:(c + 1) * 128], rhs=lorB[64:128, lo:hi], start=True, stop=True),
                              reads=[tlw, tlor], writes=[tps3])
                        sg, tsg = stg.next()
                        fw.op("dve", lambda e, sg=sg, ps3=ps3, n=n: e.tensor_tensor(out=sg[:, :n], in0=ps3[:, :n], in1=xc[:, :n], op=ALU.mult), reads=[tps3, txc], writes=[tsg])
                        fw.dma("sp", RWT[c * 128:(c + 1) * 128, lo:hi], sg[:, :n], reads=[tsg], writes=[Tok()])
                    fw.barrier()
        fw.barrier()
NCORES = 8
DEPTH = 4
S5_KEYS = ["s5_a_re", "s5_a_im", "s5_log_step", "s5_b_re", "s5_b_im", "s5_c_re", "s5_c_im", "s5_d", "s5_glu_w", "s5_glu_b", "s5_out_g"]
RW_KEYS = ["rw_mu", "rw_w0", "rw_w2", "rw_a0", "rw_a2", "rw_g2", "rw_k_k", "rw_k_a", "rw_r_k", "rw_ln_g", "rw_ln_b"]
LAYER_KEYS = (["norm1_g", "norm2_g", "mod_w", "mod_b", "w_in", "w_out", "att_qn_g", "att_kn_g", "att_out_g",
               "ffn_up", "ffn_conv_w", "ffn_conv_b", "ffn_down"] + S5_KEYS + RW_KEYS)
SHARED_KEYS = ["c_ctx", "final_g", "ident", "POS", "CST", "RWMASK", "RWBD"]
PERCORE_KEYS = ["x_b", "ctx_b", "c_b"]
SCRATCH = {"ZT": [2048, TE], "VT": [T, 128], "MOD": [128, 48, 2], "S5T": [256, T], "ATT_T": [512, T], "RWT": [256, T],
           "XT1": [DM, T], "XA": [DM, T], "XB": [DM, T]}


def build_fused(depth=DEPTH):
    nc = bass.Bass("TRN2", target_bir_lowering=False)
    decl = {}

    def ext(name, shape, dt, kind):
        if name not in decl:
            decl[name] = nc.dram_tensor(name, list(shape), dt, kind=kind).ap()
        return decl[name]

    def make_io(l):
        xin = "XA" if l % 2 == 0 else "XB"
        xout = "XB" if l % 2 == 0 else "XA"

        def io(name, shape, dt, role):
            if name in LAYER_KEYS:
                full = ext(name, [DEPTH] + list(shape), dt, "ExternalInput")
                return full[l]
            if name in SHARED_KEYS or name in PERCORE_KEYS:
                return ext(name, shape, dt, "ExternalInput")
            if name == "OUT":
                return ext(name, shape, dt, "ExternalOutput")
            if name == "XT":
                name = xin
            elif name == "XT2":
                name = xout
            return ext(name, SCRATCH[name], dt, "Internal")
        return io
    with ExitStack() as es:
        fw = FW(nc, es)
        stage_p0(fw, make_io(0))
        for l in range(depth):
            io = make_io(l)
            for st in (stage_p1, stage_p23, stage_p4, stage_p5a, stage_p5b):
                st(fw, io)
        stage_p6(fw, make_io(depth))
        fw.barrier()
    return nc, fw


def tile_up(up):
    lead = up.shape[:-2]
    v = up.reshape(lead + (8, 128, 44, 128))
    nd = len(lead)
    v = np.transpose(v, tuple(range(nd)) + (nd + 2, nd + 1, nd + 0, nd + 3))
    return np.ascontiguousarray(v).reshape(lead + (44, 128, 1024))


_FUSED = {}


def kernel(**inp):
    inp = {k: np.ascontiguousarray(np.asarray(v)) for k, v in inp.items()}
    if "nc" not in _FUSED:
        _FUSED["nc"], _FUSED["fw"] = build_fused()
    nc = _FUSED["nc"]
    pos, cst = host_consts()
    rwm, rwbd = rw_consts()
    shared = {k: inp[k] for k in LAYER_KEYS if k != "rw_r_k"}
    shared["rw_r_k"] = inp["rw_r_k"].reshape(DEPTH, 256)
    shared["ffn_up"] = tile_up(inp["ffn_up"])
    shared.update(c_ctx=inp["c_ctx"], final_g=inp["final_g"], ident=np.eye(128, dtype=np.float32),
                  POS=pos, CST=cst, RWMASK=rwm, RWBD=rwbd)
    in_maps = [dict(shared, x_b=inp["x"][b], ctx_b=inp["ctx"][b], c_b=inp["c"][b]) for b in range(NCORES)]
    res = run_bass_kernel_spmd(nc, in_maps, core_ids=list(range(NCORES)))
    return np.stack([res.results[b]["OUT"] for b in range(NCORES)], 0).astype(np.float32)
```
